# Optimizing a Trainium2 kernel written in Bass

```python
import math
import jax, jax.numpy as jnp
from jax import lax
import numpy as np

D_MODEL = 2048
BATCH = 4
SEQ = 2048
DEPTH = 2

D_MIX = D_MODEL
GLA_W = D_MIX // 4
GDN_W = D_MIX // 4
DIFF_W = D_MIX // 2
GLA_HEADS = 4
GLA_DV = GLA_W // GLA_HEADS
GLA_DK = GLA_DV // 2
GLA_RANK = 16
GLA_TAU = 16.0
GDN_HEADS = 4
GDN_D = GDN_W // GDN_HEADS
CONV_K = 4
DIFF_HEADS = 4
DIFF_DV = DIFF_W // DIFF_HEADS
DIFF_D = DIFF_DV // 2
CHUNK = 64
Q_BLOCK = 128
ROPE_THETA = 10000.0
EPS = 1e-6

IN_SPLITS = (
    GLA_HEADS * GLA_DK, GLA_HEADS * GLA_DK, GLA_W, GLA_RANK, GLA_W,
    GDN_W, GDN_W, GDN_W, GDN_HEADS, GDN_HEADS, GDN_W,
    DIFF_HEADS * 2 * DIFF_D, DIFF_HEADS * 2 * DIFF_D, DIFF_W, DIFF_W,
)
D_IN = sum(IN_SPLITS)

kernel_name = "hybrid_gla_gdn_diffattn_parallel_heads"


def rms_norm(x, w):
    xf = x.astype(jnp.float32)
    y = xf * lax.rsqrt(jnp.mean(xf * xf, axis=-1, keepdims=True) + EPS)
    return (y * w.astype(jnp.float32)).astype(x.dtype)


def l2_norm(x):
    xf = x.astype(jnp.float32)
    return xf * lax.rsqrt(jnp.sum(xf * xf, axis=-1, keepdims=True) + EPS)


def rope_tables(positions, dim):
    half = dim // 2
    inv_freq = ROPE_THETA ** (-jnp.arange(half, dtype=jnp.float32) / half)
    ang = positions.astype(jnp.float32)[..., None] * inv_freq
    return jnp.cos(ang), jnp.sin(ang)


def apply_rope(x, cos, sin):
    xf = x.astype(jnp.float32)
    x1, x2 = jnp.split(xf, 2, axis=-1)
    return jnp.concatenate([x1 * cos - x2 * sin, x2 * cos + x1 * sin], axis=-1).astype(x.dtype)


def causal_depthwise_conv(x, w):
    C = x.shape[-1]
    return lax.conv_general_dilated(x, w[:, None, :].astype(x.dtype), window_strides=(1,),
                                    padding=[(CONV_K - 1, 0)], dimension_numbers=('NWC', 'WIO', 'NWC'),
                                    feature_group_count=C)


def _to_chunks(t, B, N, H):
    t = t.astype(jnp.float32)
    return jnp.moveaxis(t.reshape((B, N, CHUNK, H) + t.shape[3:]), 3, 1)


def gla_chunked(q, k, v, log_a):
    B, S, H, DK = q.shape
    DV = v.shape[-1]
    N = S // CHUNK
    q = _to_chunks(q, B, N, H) * DK ** -0.5
    k = _to_chunks(k, B, N, H)
    v = _to_chunks(v, B, N, H)
    b = jnp.cumsum(_to_chunks(log_a, B, N, H), axis=3)
    q_e = q * jnp.exp(b)
    k_e = k * jnp.exp(-b)
    causal = jnp.tril(jnp.ones((CHUNK, CHUNK), dtype=bool))
    att = jnp.where(causal, jnp.einsum('bhncd,bhnsd->bhncs', q_e, k_e), 0.0)
    o_intra = jnp.einsum('bhncs,bhnsv->bhncv', att, v)
    b_last = b[:, :, :, -1:, :]
    d_state = jnp.einsum('bhncd,bhncv->bhndv', k * jnp.exp(b_last - b), v)
    decay = jnp.exp(b_last[:, :, :, 0, :])

    def step(state, inp):
        ds_n, dec_n = inp
        return dec_n[..., None] * state + ds_n, state

    s0 = jnp.zeros((B, H, DK, DV), jnp.float32)
    _, s_prev = lax.scan(step, s0, (jnp.moveaxis(d_state, 2, 0), jnp.moveaxis(decay, 2, 0)))
    s_prev = jnp.moveaxis(s_prev, 0, 2)
    o = o_intra + jnp.einsum('bhncd,bhndv->bhncv', q_e, s_prev)
    return jnp.moveaxis(o, 1, 3).reshape(B, S, H, DV)


def gated_delta_chunked(q, k, v, g, beta):
    B, S, H, DK = q.shape
    DV = v.shape[-1]
    N = S // CHUNK
    q = _to_chunks(q, B, N, H) * DK ** -0.5
    k = _to_chunks(k, B, N, H)
    v = _to_chunks(v, B, N, H)
    gc = jnp.cumsum(_to_chunks(g, B, N, H), axis=-1)
    beta = _to_chunks(beta, B, N, H)
    incl = jnp.tril(jnp.ones((CHUNK, CHUNK), dtype=bool))
    strict = jnp.tril(jnp.ones((CHUNK, CHUNK), dtype=bool), k=-1)
    decay = jnp.exp(jnp.where(incl, gc[..., :, None] - gc[..., None, :], -jnp.inf))
    k_beta = k * beta[..., None]
    v_beta = v * beta[..., None]
    lower = jnp.where(strict, jnp.einsum('bhncd,bhnsd->bhncs', k_beta, k) * decay, 0.0)
    eye = jnp.eye(CHUNK, dtype=jnp.float32)
    rhs = jnp.concatenate([v_beta, k_beta * jnp.exp(gc)[..., None]], axis=-1)
    sol = lax.linalg.triangular_solve(eye + lower, rhs, left_side=True, lower=True, unit_diagonal=True)
    u, w = sol[..., :DV], sol[..., DV:]
    qk = jnp.einsum('bhncd,bhnsd->bhncs', q, k) * decay
    q_e = q * jnp.exp(gc)[..., None]
    k_tail = k * jnp.exp(gc[..., -1:] - gc)[..., None]
    chunk_decay = jnp.exp(gc[..., -1])

    def step(state, inp):
        u_n, w_n, qe_n, qk_n, kt_n, cd_n = inp
        v_new = u_n - jnp.einsum('bhck,bhkv->bhcv', w_n, state)
        o_n = jnp.einsum('bhck,bhkv->bhcv', qe_n, state) + jnp.einsum('bhcs,bhsv->bhcv', qk_n, v_new)
        state = state * cd_n[..., None, None] + jnp.einsum('bhck,bhcv->bhkv', kt_n, v_new)
        return state, o_n

    xs = tuple(jnp.moveaxis(t, 2, 0) for t in (u, w, q_e, qk, k_tail, chunk_decay))
    s0 = jnp.zeros((B, H, DK, DV), jnp.float32)
    _, o = lax.scan(step, s0, xs)
    o = jnp.moveaxis(o, 0, 2)
    return jnp.moveaxis(o, 1, 3).reshape(B, S, H, DV)


def diff_attention(q, k, v, lam):
    B, H, _, S, D = q.shape
    nb = S // Q_BLOCK
    scale = D ** -0.5
    q_blocks = jnp.moveaxis(q.reshape(B, H, 2, nb, Q_BLOCK, D), 3, 0)
    key_idx = jnp.arange(S)

    def block(args):
        qb, i = args
        s = jnp.einsum('bhmqd,bhmkd->bhmqk', qb, k).astype(jnp.float32) * scale
        q_idx = i * Q_BLOCK + jnp.arange(Q_BLOCK)
        s = jnp.where(key_idx[None, :] <= q_idx[:, None], s, -jnp.inf)
        p = jax.nn.softmax(s, axis=-1)
        a = p[:, :, 0] - lam * p[:, :, 1]
        return jnp.einsum('bhqk,bhkv->bhqv', a.astype(v.dtype), v)

    out = lax.map(block, (q_blocks, jnp.arange(nb)))
    return jnp.moveaxis(out, 0, 2).reshape(B, H, S, v.shape[-1])


def setup_inputs(seed: int = 0) -> dict:
    key = jax.random.key(seed)
    ks = jax.random.split(key, 20)
    f32 = jnp.float32
    x = jax.random.normal(ks[0], (BATCH, SEQ, D_MODEL), f32)
    c = jax.random.normal(ks[1], (BATCH, D_MODEL), f32)
    offsets = jax.random.randint(ks[2], (BATCH, 1), 0, 4096, dtype=jnp.int32)
    positions = offsets + jnp.arange(SEQ, dtype=jnp.int32)[None, :]
    norm_w = 1.0 + 0.02 * jax.random.normal(ks[3], (DEPTH, D_MODEL), f32)
    w_ada = jax.random.normal(ks[4], (DEPTH, D_MODEL, 3 * D_MODEL), f32) * D_MODEL ** -0.5
    b_ada = 0.01 * jax.random.normal(ks[5], (DEPTH, 3 * D_MODEL), f32)
    w_in = jax.random.normal(ks[6], (DEPTH, D_MODEL, D_IN), f32) * D_MODEL ** -0.5
    gla_w_lr = jax.random.normal(ks[7], (DEPTH, GLA_RANK, GLA_HEADS * GLA_DK), f32) * GLA_RANK ** -0.5
    gla_b_lr = 0.01 * jax.random.normal(ks[8], (DEPTH, GLA_HEADS * GLA_DK), f32)
    gla_norm_w = 1.0 + 0.02 * jax.random.normal(ks[9], (DEPTH, GLA_DV), f32)
    gdn_conv_w = jax.random.normal(ks[10], (DEPTH, CONV_K, 3 * GDN_W), f32) * CONV_K ** -0.5
    gdn_a_log = jnp.log(jax.random.uniform(ks[11], (DEPTH, GDN_HEADS), f32, 1.0, 16.0))
    dt = jnp.exp(jax.random.uniform(ks[12], (DEPTH, GDN_HEADS), f32, math.log(1e-3), math.log(1e-1)))
    gdn_dt_bias = dt + jnp.log(-jnp.expm1(-dt))
    gdn_norm_w = 1.0 + 0.02 * jax.random.normal(ks[13], (DEPTH, GDN_D), f32)
    diff_q_norm_w = 1.0 + 0.02 * jax.random.normal(ks[14], (DEPTH, DIFF_D), f32)
    diff_k_norm_w = 1.0 + 0.02 * jax.random.normal(ks[15], (DEPTH, DIFF_D), f32)
    diff_lambda = 0.1 * jax.random.normal(ks[16], (DEPTH, 4, DIFF_D), f32)
    diff_norm_w = 1.0 + 0.02 * jax.random.normal(ks[17], (DEPTH, DIFF_DV), f32)
    w_out = jax.random.normal(ks[18], (DEPTH, D_MIX, D_MODEL), f32) * D_MIX ** -0.5
    return {"x": x, "c": c, "positions": positions, "norm_w": norm_w, "w_ada": w_ada, "b_ada": b_ada,
            "w_in": w_in, "gla_w_lr": gla_w_lr, "gla_b_lr": gla_b_lr, "gla_norm_w": gla_norm_w,
            "gdn_conv_w": gdn_conv_w, "gdn_a_log": gdn_a_log, "gdn_dt_bias": gdn_dt_bias, "gdn_norm_w": gdn_norm_w,
            "diff_q_norm_w": diff_q_norm_w, "diff_k_norm_w": diff_k_norm_w, "diff_lambda": diff_lambda,
            "diff_norm_w": diff_norm_w, "w_out": w_out}


def reference(x, c, positions, norm_w, w_ada, b_ada, w_in, gla_w_lr, gla_b_lr, gla_norm_w,
              gdn_conv_w, gdn_a_log, gdn_dt_bias, gdn_norm_w, diff_q_norm_w, diff_k_norm_w, diff_lambda,
              diff_norm_w, w_out):
    B, S, _ = x.shape
    cos, sin = rope_tables(positions, DIFF_D)
    cos_b, sin_b = cos[:, None, None], sin[:, None, None]
    bounds = np.cumsum(IN_SPLITS)[:-1].tolist()
    c_act = jax.nn.silu(c)

    for l in range(DEPTH):
        shift, scale, gate = jnp.split(c_act @ w_ada[l] + b_ada[l], 3, axis=-1)
        h = rms_norm(x, norm_w[l]) * (1.0 + scale[:, None, :]) + shift[:, None, :]
        (gq, gk, gv, glr, gz, dq, dk, dv, da, db, dz, aq, ak, av, az) = jnp.split(h @ w_in[l], bounds, axis=-1)

        log_a = jax.nn.log_sigmoid((glr @ gla_w_lr[l] + gla_b_lr[l]).astype(jnp.float32)) / GLA_TAU
        o_gla = gla_chunked(gq.reshape(B, S, GLA_HEADS, GLA_DK), gk.reshape(B, S, GLA_HEADS, GLA_DK),
                            gv.reshape(B, S, GLA_HEADS, GLA_DV), log_a.reshape(B, S, GLA_HEADS, GLA_DK))
        o_gla = (rms_norm(o_gla, gla_norm_w[l]).reshape(B, S, GLA_W) * jax.nn.silu(gz)).astype(x.dtype)

        qkv = jax.nn.silu(causal_depthwise_conv(jnp.concatenate([dq, dk, dv], axis=-1), gdn_conv_w[l]))
        cq, ck, cv = jnp.split(qkv, 3, axis=-1)
        g = -jnp.exp(gdn_a_log[l].astype(jnp.float32)) * jax.nn.softplus(
            (da + gdn_dt_bias[l]).astype(jnp.float32))
        beta = jax.nn.sigmoid(db.astype(jnp.float32))
        o_gdn = gated_delta_chunked(l2_norm(cq.reshape(B, S, GDN_HEADS, GDN_D)),
                                    l2_norm(ck.reshape(B, S, GDN_HEADS, GDN_D)),
                                    cv.reshape(B, S, GDN_HEADS, GDN_D), g, beta)
        o_gdn = (rms_norm(o_gdn, gdn_norm_w[l]).reshape(B, S, GDN_W) * jax.nn.silu(dz)).astype(x.dtype)

        q_d = rms_norm(aq.reshape(B, S, DIFF_HEADS, 2, DIFF_D), diff_q_norm_w[l]).transpose(0, 2, 3, 1, 4)
        k_d = rms_norm(ak.reshape(B, S, DIFF_HEADS, 2, DIFF_D), diff_k_norm_w[l]).transpose(0, 2, 3, 1, 4)
        q_d = apply_rope(q_d, cos_b, sin_b)
        k_d = apply_rope(k_d, cos_b, sin_b)
        v_d = av.reshape(B, S, DIFF_HEADS, DIFF_DV).transpose(0, 2, 1, 3)
        lam_init = 0.8 - 0.6 * math.exp(-0.3 * l)
        lv = diff_lambda[l].astype(jnp.float32)
        lam = jnp.exp(jnp.sum(lv[0] * lv[1])) - jnp.exp(jnp.sum(lv[2] * lv[3])) + lam_init
        o_diff = diff_attention(q_d, k_d, v_d, lam)
        o_diff = rms_norm(o_diff, diff_norm_w[l]) * (1.0 - lam_init)
        o_diff = (o_diff.transpose(0, 2, 1, 3).reshape(B, S, DIFF_W) * jax.nn.silu(az)).astype(x.dtype)

        y = jnp.concatenate([o_gla, o_gdn, o_diff], axis=-1) @ w_out[l]
        x = x + gate[:, None, :] * y
    return x
```

```python
import math
import numpy as np
import concourse.bass as bass
import concourse.mybir as mybir
from concourse.bass_utils import run_bass_kernel_spmd

F32 = mybir.dt.float32
BF16 = mybir.dt.bfloat16
I32 = mybir.dt.int32
AF = mybir.ActivationFunctionType
ALU = mybir.AluOpType
AX = mybir.AxisListType

S = 2048
D = 2048
NT = 16
EPS = 1e-6
SLW = 880
SBASE = 32
NS = SBASE + 2 * SLW
CI, CO, CP, CTU, CNSU, CNSL, CB32, CO32, CO64 = range(9)
NCM = 9


class Reg:
    __slots__ = ("name", "lw", "rd", "dsem", "local", "psum")

    def __init__(self, name, local=False, psum=False):
        self.name = name
        self.psum = psum
        self.lw = None
        self.rd = {}
        self.dsem = None
        self.local = local


class Prog:
    def __init__(self, nc):
        self.nc = nc
        self.eng = {"pe": nc.tensor, "act": nc.scalar, "dve": nc.vector, "pool": nc.gpsimd, "sp": nc.sync}
        self.sem, self.cnt, self.semobj = {}, {}, {}
        self.seen = {e: {} for e in self.eng}
        for e in self.eng:
            s = nc.alloc_semaphore(name="s_" + e)
            self.sem[e] = s
            self.semobj[e] = s
            self.cnt[e] = 0
        self.dcnt = {}
        self.local_keys = []
        self.local_next = 0
        self.ninstr = 0

    def _wait(self, e, key, val):
        if self.seen[e].get(key, 0) >= val:
            return
        self.seen[e][key] = val
        self.eng[e].wait_ge(self.semobj[key], val)

    def _deps(self, e, reads, writes, pe_accum=False, nowaw=False):
        for r in reads:
            if r.lw is not None:
                self._wait(e, *r.lw)
            if r.psum:
                for k, v in r.rd.items():
                    if k != e:
                        self._wait(e, k, v)
        for w in writes:
            if w.lw is not None and not nowaw and not (pe_accum and w.lw[0] == "pe"):
                self._wait(e, *w.lw)
            for k, v in w.rd.items():
                self._wait(e, k, v)

    def _record(self, ev, reads, writes, nowaw=False):
        for r in reads:
            if r.rd.get(ev[0], 0) < ev[1]:
                r.rd[ev[0]] = ev[1]
        for w in writes:
            w.lw = ev
            if not nowaw:
                w.rd = {}

    def op(self, e, fn, reads=(), writes=(), pe_accum=False):
        self._deps(e, reads, writes, pe_accum)
        ins = fn(self.eng[e])
        self.cnt[e] += 1
        ins.then_inc(self.sem[e], 1)
        self._record((e, self.cnt[e]), reads, writes)
        self.ninstr += 1

    def _newkey(self):
        key = ("d", len(self.dcnt))
        self.semobj[key] = self.nc.alloc_semaphore(name="d_%d" % len(self.dcnt))
        self.dcnt[key] = 0
        return key

    def _dkey(self, w):
        if w.dsem is None:
            if w.local:
                if self.local_next == len(self.local_keys):
                    self.local_keys.append(self._newkey())
                w.dsem = self.local_keys[self.local_next]
                self.local_next += 1
            else:
                w.dsem = self._newkey()
        return w.dsem

    def dma(self, q, out_ap, in_ap, reads=(), writes=(), nowaw=False, **kw):
        w = writes[0]
        self._deps(q, reads, writes, nowaw=nowaw)
        key = self._dkey(w)
        ins = self.eng[q].dma_start(out=out_ap, in_=in_ap, **kw)
        self.dcnt[key] += 16
        ins.then_inc(self.semobj[key], 16)
        self._record((key, self.dcnt[key]), reads, writes, nowaw=nowaw)
        self.ninstr += 1

    def barrier(self, reset=True):
        evs = [(e, self.cnt[e]) for e in self.eng if self.cnt[e] > 0]
        evs += [(k, v) for k, v in self.dcnt.items() if v > 0]
        for e in self.eng:
            for k, v in evs:
                if k != e:
                    self._wait(e, k, v)
        if reset:
            self.local_next = 0


class Arena:
    def __init__(self, nc, nwords):
        self.t = nc.alloc_sbuf_tensor("arena", [128, nwords], F32)
        self.n = nwords
        self.off = 0
        self.k = 0

    def reset(self):
        self.off = 0

    def alloc(self, shape, dt, name=None):
        free = int(np.prod(shape[1:]))
        words = free if dt in (F32, I32) else (free + 1) // 2
        words = (words + 7) // 8 * 8
        assert self.off + words <= self.n, ("arena overflow", name, self.off, words, self.n)
        v = self.t[:, self.off:self.off + words]
        self.off += words
        if dt == BF16:
            v = v.bitcast(BF16)[:, 0:free]
        elif dt == I32:
            v = v.bitcast(I32)[:, 0:free]
        else:
            v = v[:, 0:free]
        if len(shape) == 3:
            v = v.rearrange("p (a b) -> p a b", b=shape[2])
        self.k += 1
        if shape[0] < 128:
            v = v[0:shape[0]]
        return v, Reg(name or "a%d" % self.k, local=True)


def build(nlayers=2, debug=False, stop=None):
    nc = bass.Bass("TRN2", target_bir_lowering=False)
    P = Prog(nc)
    L = nlayers

    def din(name, shape, dt=F32):
        return nc.dram_tensor(name, shape, dt, kind="ExternalInput").ap()

    x_d = din("x", [S, D])
    small_d = din("small", [128, NS])
    cmat_d = din("cmat", [128, NCM * 128])
    selm_d = din("selm", [8, 8 * 128])
    rmask_d = din("rmask", [128, S])
    pos_d = din("pos", [1, S], I32)
    wada_d = din("wada", [2, 48, 128, 16 * 128])
    bgate_d = din("bgate", [2, 1, D])
    winfm_d = din("winfm", [2, 50, 128, 16 * 128])
    wintm_d = din("wintm", [2, 6, 128, 16 * 256])
    wout_d = din("wout", [2, 128, 16 * D])
    out_d = nc.dram_tensor("out", [S, D], F32, kind="ExternalOutput").ap()
    x1_d = nc.dram_tensor("x1s", [S, D], F32, kind="Internal").ap()
    oT_d = nc.dram_tensor("oTs", [16, 128, S], BF16, kind="ExternalOutput" if debug else "Internal").ap()
    R_out, R_x1, R_oT = Reg("out"), Reg("x1"), Reg("oT")

    def sb(name, shape, dt):
        return nc.alloc_sbuf_tensor("sb_" + name, shape, dt), Reg(name)

    big, R_big = sb("big", [128, 16, S], BF16)
    cosT, R_cos = sb("cosT", [128, S], BF16)
    sinT, R_sin = sb("sinT", [128, S], BF16)
    gate_bc, R_gate = sb("gate_bc", [128, D], F32)
    rmask, R_rmask = sb("rmask", [128, S], BF16)
    cm_f, R_cmf = sb("cm_f", [128, NCM, 128], F32)
    cm_b, R_cmb = sb("cm_b", [128, NCM, 128], BF16)
    selm, R_selm = sb("selm", [8, 8, 128], F32)
    small, R_small = sb("small", [128, NS], F32)
    modc, R_modc = sb("modc", [128, 48], F32)
    misc, R_misc = sb("misc", [128, 64], F32)
    ar = Arena(nc, (nc.sbuf_bytes_remaining - 2048) // 4)

    pss = [nc.alloc_psum_tensor("ps%d" % i, [128, 512], F32) for i in range(8)]
    R_ps = [Reg("ps%d" % i, psum=True) for i in range(8)]

    state = {"alt": 0}

    def alt():
        state["alt"] ^= 1
        return "act" if state["alt"] else "dve"

    def copy_on(e, out, in_, reads, writes):
        if e == "act":
            P.op("act", lambda g: g.activation(out=out, in_=in_, func=AF.Copy), reads=reads, writes=writes)
        else:
            P.op(e, lambda g: g.tensor_copy(out, in_), reads=reads, writes=writes)

    def mm(out, lhsT, rhs, start, stop, reads, w):
        P.op("pe", lambda g: g.matmul(out, lhsT=lhsT, rhs=rhs, start=start, stop=stop),
             reads=reads, writes=[w], pe_accum=not start)

    def new_phase():
        P.barrier()
        ar.reset()

    ident_b = cm_b[:, CI, :]
    ones_b = cm_b[:, CO, :]
    ident_f = cm_f[:, CI, :]

    def rstd_part(srcs, n, denom, tmps, psi):
        (sq, R_sq), (t1, R_t1), (rinv, R_rinv) = tmps
        for i, (sap, sreg) in enumerate(srcs):
            P.op("act", lambda g: g.activation(out=sq[:, i, 0:n], in_=sap, func=AF.Square), reads=[sreg], writes=[R_sq])
        for i in range(len(srcs)):
            mm(pss[psi][:, 0:n], ones_b, sq[:, i, 0:n], i == 0, i == len(srcs) - 1, [R_sq, R_cmb], R_ps[psi])
        P.op("act", lambda g: g.activation(out=t1[:, 0:n], in_=pss[psi][:, 0:n], func=AF.Sqrt, scale=1.0 / denom, bias=eps_ap),
             reads=[R_ps[psi], R_misc], writes=[R_t1])
        P.op("dve", lambda g: g.reciprocal(rinv[:, 0:n], t1[:, 0:n]), reads=[R_t1], writes=[R_rinv])
        return rinv, R_rinv

    P.dma("sp", small[:], small_d, writes=[R_small])
    P.dma("sp", cm_f[:], cmat_d.rearrange("p (a b) -> p a b", b=128), writes=[R_cmf])
    P.dma("sp", selm[:], selm_d.rearrange("p (a b) -> p a b", b=128), writes=[R_selm])
    P.dma("pool", rmask[:], rmask_d, writes=[R_rmask])
    P.op("dve", lambda g: g.tensor_copy(cm_b[:], cm_f[:]), reads=[R_cmf], writes=[R_cmb])
    eps_ap = misc[:, 0:1]
    P.op("dve", lambda g: g.memset(misc[:], 0.0), writes=[R_misc])
    P.op("dve", lambda g: g.memset(misc[:, 0:1], EPS), reads=[], writes=[R_misc])
    P.op("dve", lambda g: g.memset(misc[:, 1:2], 1.0), reads=[], writes=[R_misc])
    one_ap = misc[:, 1:2]

    posi, R_posi = ar.alloc([128, S], I32, "posi")
    y, R_y = ar.alloc([128, S], F32, "y")
    yi, R_yi = ar.alloc([128, S], I32, "yi")
    yf, R_yf = ar.alloc([128, S], F32, "yf")
    fr, R_fr = ar.alloc([128, S], F32, "fr")
    m1, R_m1 = ar.alloc([128, S], F32, "m1")
    P.dma("sp", posi, pos_d.to_broadcast([128, S]), writes=[R_posi])
    P.op("dve", lambda g: g.tensor_copy(y, posi), reads=[R_posi], writes=[R_y])
    P.op("dve", lambda g: g.tensor_scalar(y, y, small[:, 16:17], None, ALU.mult), reads=[R_y, R_small], writes=[R_y])
    P.op("dve", lambda g: g.tensor_copy(yi, y), reads=[R_y], writes=[R_yi])
    P.op("dve", lambda g: g.tensor_copy(yf, yi), reads=[R_yi], writes=[R_yf])
    P.op("dve", lambda g: g.tensor_tensor(fr, y, yf, ALU.subtract), reads=[R_y, R_yf], writes=[R_fr])
    for which, dst, R_dst in ((0, sinT, R_sin), (1, cosT, R_cos)):
        src = fr
        if which == 1:
            P.op("dve", lambda g: g.tensor_scalar(y, fr, 0.25, None, ALU.add), reads=[R_fr], writes=[R_y])
            src = y
        R_src = R_fr if which == 0 else R_y
        P.op("dve", lambda g: g.tensor_scalar(m1, src, 0.5, None, ALU.is_gt), reads=[R_src], writes=[R_m1])
        P.op("dve", lambda g: g.tensor_tensor(yf, src, m1, ALU.subtract), reads=[R_src, R_m1], writes=[R_yf])
        P.op("dve", lambda g: g.tensor_scalar(m1, yf, -0.5, None, ALU.is_lt), reads=[R_yf], writes=[R_m1])
        P.op("dve", lambda g: g.tensor_tensor(yf, yf, m1, ALU.add), reads=[R_yf, R_m1], writes=[R_yf])
        P.op("act", lambda g: g.activation(out=dst[:], in_=yf, func=AF.Sin, scale=2.0 * math.pi), reads=[R_yf], writes=[R_dst])

    if stop == "p0":
        P.barrier(); return nc
    def proj_fm(l, gi, M, wbufs, consume, bankset):
        wb, R_wb = wbufs[state.setdefault("wfm_i", 0) % len(wbufs)]
        state["wfm_i"] += 1
        P.dma("pool", wb, winfm_d[l, gi].rearrange("p (a b) -> p a b", b=128), writes=[R_wb])
        banks = [bankset * 4 + i for i in range(4)]
        for kc in range(16):
            for tb in range(4):
                mm(pss[banks[tb]][0:M, :], wb[:, kc, 0:M], big[:, kc, tb * 512:(tb + 1) * 512], kc == 0, kc == 15,
                   [R_wb, R_big], R_ps[banks[tb]])
        for tb in range(4):
            consume(tb, pss[banks[tb]][0:M, :], R_ps[banks[tb]])

    def proj_tm(l, gi, wtb, consume, banks):
        wb, R_wb = wtb
        P.dma("pool", wb, wintm_d[l, gi].rearrange("p (a b) -> p a b", b=256), writes=[R_wb])
        for t in range(NT):
            bk = banks[t % len(banks)]
            for kc in range(16):
                mm(pss[bk][:, 0:256], big[:, kc, t * 128:(t + 1) * 128], wb[:, kc, :], kc == 0, kc == 15,
                   [R_wb, R_big], R_ps[bk])
            consume(t, pss[bk][:, 0:256], R_ps[bk])

    def post_norm_store(l, o_acc, R_oacc, nsub, wcols, szs, R_sz, fc0, tmps, denom, psi):
        (sq, R_sq), (t1, R_t1), (rinv, R_rinv), (u, R_u), (ost, R_ost) = tmps
        for blk in range(4):
            sl = slice(blk * 512, (blk + 1) * 512)
            rstd_part([(o_acc[:, j, sl], R_oacc) for j in range(nsub)], 512, denom, tmps[0:3], psi)
            for j in range(nsub):
                P.op("dve", lambda g: g.scalar_tensor_tensor(out=u[:, 0:512], in0=o_acc[:, j, sl], scalar=wcols[j], in1=rinv[:, 0:512],
                                                             op0=ALU.mult, op1=ALU.mult),
                     reads=[R_oacc, R_rinv, R_small, R_misc], writes=[R_u])
                P.op("pool", lambda g: g.tensor_tensor(ost[:, 0:512], u[:, 0:512], szs[:, j, sl], ALU.mult),
                     reads=[R_u, R_sz], writes=[R_ost])
                P.dma("sp", oT_d[fc0 + j, :, sl], ost[:, 0:512], reads=[R_ost], writes=[R_oT], nowaw=True)

    for l in range(L):
        sb_l = SBASE + l * SLW
        lam_init = 0.8 - 0.6 * math.exp(-0.3 * l)

        def sc(off, n=1):
            return small[:, sb_l + off: sb_l + off + n]

        new_phase()
        cact, R_cact = ar.alloc([128, 16], F32, "cact")
        c2, R_c2 = ar.alloc([128, 16, 2], F32, "c2")
        crep, R_crep = ar.alloc([128, 16, 128], F32, "crep")
        bg, R_bg = ar.alloc([128, D], F32, "bg")
        wab = [ar.alloc([128, 16, 128], F32, "wa%d" % i) for i in range(3)]
        P.op("act", lambda g: g.activation(out=cact, in_=small[:, 0:16], func=AF.Silu), reads=[R_small], writes=[R_cact])
        P.op("dve", lambda g: g.tensor_copy(c2, cact.unsqueeze(2).to_broadcast([128, 16, 2])), reads=[R_cact], writes=[R_c2])
        P.op("dve", lambda g: g.tensor_copy(crep, cact.unsqueeze(2).to_broadcast([128, 16, 128])), reads=[R_cact], writes=[R_crep])
        P.dma("sp", bg, bgate_d[l].to_broadcast([128, D]), writes=[R_bg])
        for g_ in range(48):
            wa, R_wa = wab[g_ % 3]
            P.dma("sp", wa, wada_d[l, g_].rearrange("p (a b) -> p a b", b=128), writes=[R_wa])
            if g_ < 32:
                for kc in range(16):
                    mm(pss[0][:, 2 * g_:2 * g_ + 2], wa[:, kc, :], c2[:, kc, :], kc == 0, kc == 15, [R_wa, R_c2], R_ps[0])
                if g_ == 31:
                    P.op("dve", lambda g: g.tensor_tensor(modc[:, 0:32], pss[0][:, 0:64:2], sc(16, 32), ALU.add),
                         reads=[R_ps[0], R_small], writes=[R_modc])
                    P.op("dve", lambda g: g.scalar_tensor_tensor(out=modc[:, 32:48], in0=modc[:, 16:32], scalar=1.0, in1=sc(0, 16),
                                                                 op0=ALU.add, op1=ALU.mult),
                         reads=[R_modc, R_small], writes=[R_modc])
            else:
                gg = g_ - 32
                bk = 1 + (gg // 4) % 2
                c0 = (gg % 4) * 128
                for kc in range(16):
                    mm(pss[bk][:, c0:c0 + 128], crep[:, kc, :], wa[:, kc, :], kc == 0, kc == 15, [R_wa, R_crep], R_ps[bk])
                if gg % 4 == 3:
                    sl = slice((gg // 4) * 512, (gg // 4 + 1) * 512)
                    P.op("dve", lambda g: g.tensor_tensor(gate_bc[:, sl], pss[bk][:], bg[:, sl], ALU.add),
                         reads=[R_ps[bk], R_bg], writes=[R_gate])

        if stop == "pA":
            P.barrier(); return nc
        new_phase()
        xsrc = x_d if l == 0 else x1_d
        xts = [ar.alloc([128, D], F32, "xt%d" % i) for i in range(2)]
        xn, R_xn = ar.alloc([128, 4, D], BF16, "xn")
        junk, R_junk = ar.alloc([128, D], BF16, "junk")
        ssq, R_ssq = ar.alloc([128, 8], F32, "ssq")
        hT = big
        for tb in range(4):
            for i in range(4):
                t = tb * 4 + i
                xt, R_xt = xts[t % 2]
                P.dma("sp", xt, xsrc[t * 128:(t + 1) * 128, :], reads=[R_x1] if l > 0 else [], writes=[R_xt])
                P.op("act", lambda g: g.activation(out=junk, in_=xt, func=AF.Square, accum_out=ssq[:, 0:1]),
                     reads=[R_xt], writes=[R_junk, R_ssq])
                P.op("act", lambda g: g.activation(out=ssq[:, 1:2], in_=ssq[:, 0:1], func=AF.Sqrt, scale=1.0 / D, bias=eps_ap),
                     reads=[R_ssq, R_misc], writes=[R_ssq])
                P.op("dve", lambda g: g.reciprocal(ssq[:, 2:3], ssq[:, 1:2]), reads=[R_ssq], writes=[R_ssq])
                P.op("dve", lambda g: g.tensor_scalar(xn[:, i, :], xt, ssq[:, 2:3], None, ALU.mult),
                     reads=[R_xt, R_ssq], writes=[R_xn])
            for fc in range(16):
                bk = fc % 4
                pT = pss[bk][:].bitcast(BF16)
                for i in range(4):
                    P.op("pe", lambda g: g.transpose(pT[:, i * 128:(i + 1) * 128], xn[:, i, fc * 128:(fc + 1) * 128], ident_b),
                         reads=[R_xn, R_cmb], writes=[R_ps[bk]], pe_accum=i > 0)
                dst = hT[:, fc, tb * 512:(tb + 1) * 512]
                if alt() == "act":
                    P.op("act", lambda g: g.activation(out=dst, in_=pT[:, 0:512], func=AF.Identity,
                                                       scale=modc[:, 32 + fc:33 + fc], bias=modc[:, fc:fc + 1]),
                         reads=[R_ps[bk], R_modc], writes=[R_big])
                else:
                    P.op("dve", lambda g: g.tensor_scalar(dst, pT[:, 0:512], modc[:, 32 + fc:33 + fc], modc[:, fc:fc + 1],
                                                          ALU.mult, ALU.add),
                         reads=[R_ps[bk], R_modc], writes=[R_big])

        if stop == "pB":
            P.barrier(); return nc
        for pr in range(2):
            new_phase()
            wfm = [ar.alloc([128, 16, 128], BF16, "wfm%d" % i) for i in range(3)]
            wtm = ar.alloc([128, 16, 256], BF16, "wtm")
            glrT, R_glrT = ar.alloc([16, S], BF16, "glrT")
            wlr, R_wlr = ar.alloc([16, 256], BF16, "wlr")
            bcs, R_bcs = ar.alloc([128, S], F32, "bcs")
            eb, R_eb = ar.alloc([128, S], F32, "eb")
            q_eT, R_qe = ar.alloc([128, S], BF16, "q_eT")
            k_eT, R_ke = ar.alloc([128, S], BF16, "k_eT")
            v_g, R_vg = ar.alloc([128, NT, 256], BF16, "v_g")
            sz, R_sz = ar.alloc([128, 2, S], BF16, "sz")
            ke_tok, R_ket = ar.alloc([128, NT, 128], BF16, "ke_tok")
            o_acc, R_oacc = ar.alloc([128, 2, S], F32, "o_acc")
            e1, R_e1 = ar.alloc([128, 512], F32, "e1")
            dec, R_dec = ar.alloc([128, 16], F32, "dec")
            nb, R_nb = ar.alloc([128, 2], F32, "nb")
            Sp, R_Sp = ar.alloc([128, 256], F32, "Sp")
            Stmp, R_Stmp = ar.alloc([128, 256], F32, "Stmp")
            Sbf, R_Sbf = ar.alloc([128, 256], BF16, "Sbf")
            attm, R_attm = ar.alloc([128, 2, 128], BF16, "attm")
            tmps = [ar.alloc([128, 2, 512], BF16, "sq"), ar.alloc([128, 512], F32, "t1"), ar.alloc([128, 512], F32, "rinv"),
                    ar.alloc([128, 512], F32, "u"), ar.alloc([128, 512], BF16, "ost")]

            def c_glr(tb, ps, R):
                copy_on("act", glrT[0:16, tb * 512:(tb + 1) * 512], ps, [R], [R_glrT])
            proj_fm(l, 4, 16, wfm, c_glr, 0)
            P.op("dve", lambda g: g.tensor_copy(wlr, sc(624, 256)[0:16, :]), reads=[R_small], writes=[R_wlr])
            P.op("dve", lambda g: g.tensor_scalar(nb, sc(48, 2), -1.0, None, ALU.mult), reads=[R_small], writes=[R_nb])
            for blk in range(4):
                sl = slice(blk * 512, (blk + 1) * 512)
                bk = 4 + blk % 2
                mm(pss[bk][:], wlr[0:16, pr * 128:(pr + 1) * 128], glrT[0:16, sl], True, True, [R_wlr, R_glrT], R_ps[bk])
                P.op("act", lambda g: g.activation(out=e1, in_=pss[bk][:], func=AF.Exp, scale=-1.0, bias=nb[:, pr:pr + 1]),
                     reads=[R_ps[bk], R_nb], writes=[R_e1])
                P.op("act", lambda g: g.activation(out=bcs[:, sl], in_=e1, func=AF.Ln, bias=one_ap), reads=[R_e1, R_misc], writes=[R_bcs])
            P.op("dve", lambda g: g.tensor_tensor_scan(out=bcs, data0=rmask[:], data1=bcs, initial=0.0, op0=ALU.mult, op1=ALU.add),
                 reads=[R_rmask, R_bcs], writes=[R_bcs])
            P.op("act", lambda g: g.activation(out=eb, in_=bcs, func=AF.Exp, scale=-1.0 / 16.0), reads=[R_bcs], writes=[R_eb])
            P.op("dve", lambda g: g.tensor_copy(dec, eb[:, 127:S:128]), reads=[R_eb], writes=[R_dec])
            P.op("act", lambda g: g.activation(out=bcs, in_=bcs, func=AF.Exp, scale=1.0 / 16.0), reads=[R_bcs], writes=[R_bcs])
            enb = bcs

            def c_q(tb, ps, R):
                sl = slice(tb * 512, (tb + 1) * 512)
                P.op("dve", lambda g: g.scalar_tensor_tensor(out=q_eT[:, sl], in0=ps, scalar=0.125, in1=eb[:, sl], op0=ALU.mult, op1=ALU.mult),
                     reads=[R, R_eb], writes=[R_qe])
            proj_fm(l, pr, 128, wfm, c_q, 1)

            def c_k(tb, ps, R):
                sl = slice(tb * 512, (tb + 1) * 512)
                P.op("dve", lambda g: g.tensor_tensor(k_eT[:, sl], ps, enb[:, sl], ALU.mult), reads=[R, R_bcs], writes=[R_ke])
            proj_fm(l, 2 + pr, 128, wfm, c_k, 0)

            def c_v(t, ps, R):
                copy_on("act", v_g[:, t, :], ps, [R], [R_vg])
            proj_tm(l, pr, wtm, c_v, [4, 5])
            for hh in range(2):
                def c_z(tb, ps, R):
                    P.op("act", lambda g: g.activation(out=sz[:, hh, tb * 512:(tb + 1) * 512], in_=ps, func=AF.Silu), reads=[R], writes=[R_sz])
                proj_fm(l, 5 + 2 * pr + hh, 128, wfm, c_z, hh)
            for t4 in range(4):
                bk = 6 + t4 % 2
                pT = pss[bk][:].bitcast(BF16)
                for i in range(4):
                    t = t4 * 4 + i
                    P.op("pe", lambda g: g.transpose(pT[:, i * 128:(i + 1) * 128], k_eT[:, t * 128:(t + 1) * 128], ident_b),
                         reads=[R_ke, R_cmb], writes=[R_ps[bk]], pe_accum=i > 0)
                P.op("dve", lambda g: g.tensor_copy(ke_tok[:, t4 * 4:(t4 + 1) * 4, :], pT[:, 0:512].rearrange("p (a b) -> p a b", b=128)),
                     reads=[R_ps[bk]], writes=[R_ket])
            P.op("dve", lambda g: g.memset(Sp, 0.0), writes=[R_Sp])
            for n in range(NT):
                ch = slice(n * 128, (n + 1) * 128)
                ba, bo, bkv = n % 2, 2 + n % 2, 4 + n % 2
                for hh in range(2):
                    hp = slice(64 * hh, 64 * hh + 64)
                    mm(pss[ba][:, hh * 128:(hh + 1) * 128], k_eT[hp, ch], q_eT[hp, ch], True, True, [R_ke, R_qe], R_ps[ba])
                P.op("dve", lambda g: g.tensor_tensor(attm, pss[ba][:, 0:256].rearrange("p (a b) -> p a b", b=128),
                                                      cm_f[:, CTU:CTU + 1, :].to_broadcast([128, 2, 128]), ALU.mult),
                     reads=[R_ps[ba], R_cmf], writes=[R_attm])
                for hh in range(2):
                    hp = slice(64 * hh, 64 * hh + 64)
                    vs = slice(hh * 128, (hh + 1) * 128)
                    mm(pss[bo][:, vs], v_g[:, n, vs], attm[:, hh, :], True, n == 0, [R_vg, R_attm], R_ps[bo])
                    if n > 0:
                        mm(pss[bo][:, vs], Sbf[hp, vs], q_eT[hp, ch], False, True, [R_Sbf, R_qe], R_ps[bo])
                P.op("act", lambda g: g.activation(out=o_acc[:, :, ch], in_=pss[bo][:, 0:256].rearrange("p (a b) -> p a b", b=128), func=AF.Copy),
                     reads=[R_ps[bo]], writes=[R_oacc])
                if n < NT - 1:
                    mm(pss[bkv][:, 0:256], ke_tok[:, n, :], v_g[:, n, :], True, True, [R_ket, R_vg], R_ps[bkv])
                    P.op("dve", lambda g: g.tensor_tensor(Stmp, Sp, pss[bkv][:, 0:256], ALU.add), reads=[R_Sp, R_ps[bkv]], writes=[R_Stmp])
                    P.op("dve", lambda g: g.tensor_scalar(Sp, Stmp, dec[:, n:n + 1], None, ALU.mult), reads=[R_Stmp, R_dec], writes=[R_Sp])
                    P.op("pool", lambda g: g.tensor_scalar(Sbf, Stmp, dec[:, n:n + 1], None, ALU.mult), reads=[R_Stmp, R_dec], writes=[R_Sbf])
            for hh in range(2):
                post_norm_store(l, o_acc[:, hh:hh + 1, :], R_oacc, 1, [sc(50)], sz[:, hh:hh + 1, :], R_sz, 2 * pr + hh, tmps, 128.0, 6)

        if stop == "pC1":
            P.barrier(); return nc
        for h in range(4):
            new_phase()
            wfm = [ar.alloc([128, 16, 128], BF16, "wfm%d" % i) for i in range(2)]
            xbf, R_xbf = ar.alloc([128, S + 8], BF16, "xbf")
            diag, R_diag = ar.alloc([128, 4, 128], BF16, "diag")
            cs, R_cs = ar.alloc([128, S], F32, "cs")
            knT, R_kn = ar.alloc([128, S], BF16, "knT")
            qnT, R_qn = ar.alloc([128, S], BF16, "qnT")
            cvT, R_cv = ar.alloc([128, S], BF16, "cvT")
            kbT, R_kb = ar.alloc([128, S], BF16, "kbT")
            q_eT, R_qe = ar.alloc([128, S], BF16, "q_eT")
            vb_tok, R_vb = ar.alloc([128, NT, 128], BF16, "vb_tok")
            kbg_tok, R_kbg = ar.alloc([128, NT, 128], BF16, "kbg_tok")
            kt_tok, R_kt = ar.alloc([128, NT, 128], BF16, "kt_tok")
            gc, R_gc = ar.alloc([128, S], F32, "gc")
            beta, R_beta = ar.alloc([128, S], BF16, "beta")
            eg, R_eg = ar.alloc([128, S], F32, "eg")
            tl, R_tl = ar.alloc([128, S], F32, "tl")
            dabT, R_dab = tl[0:8, :], R_tl
            cols, R_cols = ar.alloc([128, 6, 16], F32, "cols")
            nA, R_nA = ar.alloc([128, 4], F32, "nA")
            Sp, R_Sp = ar.alloc([128, 128], F32, "Sp")
            Sbf, R_Sbf = ar.alloc([128, 128], BF16, "Sbf")
            dm = [ar.alloc([128, 128], F32, "dm%d" % i) for i in range(4)]
            mb = [ar.alloc([128, 128], BF16, "mb%d" % i) for i in range(10)]
            u_sb, R_u = ar.alloc([128, 128], F32, "u_sb")
            tmps = [ar.alloc([128, 2, 512], BF16, "sq"), ar.alloc([128, 512], F32, "t1"), ar.alloc([128, 512], F32, "rinv"),
                    ar.alloc([128, 512], F32, "u"), ar.alloc([128, 512], BF16, "ost")]
            e1, R_e1 = tmps[3]
            o_acc, R_oacc = cs.rearrange("p (a b) -> p a b", a=1), R_cs

            def c_dab(tb, ps, R):
                copy_on("act", dabT[0:8, tb * 512:(tb + 1) * 512], ps, [R], [R_dab])
            proj_fm(l, 21, 8, wfm, c_dab, 0)
            P.op("dve", lambda g: g.memset(xbf[:, 0:3], 0.0), writes=[R_xbf])
            for which in range(3):
                ti = which * 4 + h
                for j in range(4):
                    P.op("dve", lambda g: g.tensor_scalar(diag[:, j, :], ident_f, sc(64 + ti * 4 + j), None, ALU.mult),
                         reads=[R_cmf, R_small], writes=[R_diag])

                def c_x(tb, ps, R):
                    copy_on(alt(), xbf[:, 3 + tb * 512:3 + (tb + 1) * 512], ps, [R], [R_xbf])
                proj_fm(l, 9 + 4 * which + h, 128, wfm, c_x, 1)
                for blk in range(4):
                    sl = slice(blk * 512, (blk + 1) * 512)
                    bk = blk % 2
                    for j in range(4):
                        mm(pss[bk][:], diag[:, j, :], xbf[:, blk * 512 + j: blk * 512 + j + 512], j == 0, j == 3, [R_diag, R_xbf], R_ps[bk])
                    if which == 2:
                        P.op("act", lambda g: g.activation(out=cvT[:, sl], in_=pss[bk][:], func=AF.Silu), reads=[R_ps[bk]], writes=[R_cv])
                    else:
                        P.op("act", lambda g: g.activation(out=cs[:, sl], in_=pss[bk][:], func=AF.Silu), reads=[R_ps[bk]], writes=[R_cs])
                if which < 2:
                    for blk in range(4):
                        sl = slice(blk * 512, (blk + 1) * 512)
                        rinv, R_rinv = rstd_part([(cs[:, sl], R_cs)], 512, 1.0, tmps[0:3], 2 + blk % 2)
                        if which == 0:
                            P.op("dve", lambda g: g.scalar_tensor_tensor(out=qnT[:, sl], in0=cs[:, sl], scalar=128.0 ** -0.5, in1=rinv[:, 0:512],
                                                                         op0=ALU.mult, op1=ALU.mult), reads=[R_cs, R_rinv], writes=[R_qn])
                        else:
                            P.op("dve", lambda g: g.tensor_tensor(knT[:, sl], cs[:, sl], rinv[:, 0:512], ALU.mult), reads=[R_cs, R_rinv], writes=[R_kn])
            if stop == "g1":
                P.barrier(); return nc
            P.op("act", lambda g: g.activation(out=nA, in_=sc(56, 4), func=AF.Exp), reads=[R_small], writes=[R_nA])
            P.op("dve", lambda g: g.tensor_scalar(nA, nA, -1.0, None, ALU.mult), reads=[R_nA], writes=[R_nA])
            for blk in range(4):
                sl = slice(blk * 512, (blk + 1) * 512)
                bk = 4 + blk % 2
                mm(pss[bk][:], selm[0:8, h, :], dabT[0:8, sl], True, True, [R_selm, R_dab], R_ps[bk])
                P.op("act", lambda g: g.activation(out=e1, in_=pss[bk][:], func=AF.Exp, bias=sc(60 + h)), reads=[R_ps[bk], R_small], writes=[R_e1])
                P.op("act", lambda g: g.activation(out=gc[:, sl], in_=e1, func=AF.Ln, bias=one_ap), reads=[R_e1, R_misc], writes=[R_gc])
                (sqv, R_s0), (s1, R_s1), (s2, R_s2) = tmps[0], tmps[1], tmps[2]
                s0 = sqv.rearrange("p a b -> p (a b)").bitcast(F32)
                P.op("dve", lambda g: g.tensor_scalar(s1, e1, 2.0, None, ALU.add), reads=[R_e1], writes=[R_s1])
                P.op("dve", lambda g: g.reciprocal(s2, s1), reads=[R_s1], writes=[R_s2])
                P.op("dve", lambda g: g.tensor_tensor(s1, e1, s2, ALU.mult), reads=[R_e1, R_s2], writes=[R_s1])
                P.op("dve", lambda g: g.tensor_tensor(s2, s1, s1, ALU.mult), reads=[R_s1], writes=[R_s2])
                P.op("dve", lambda g: g.tensor_scalar(s0, s2, 1.0 / 7.0, 0.2, ALU.mult, ALU.add), reads=[R_s2], writes=[R_s0])
                P.op("dve", lambda g: g.tensor_tensor(s0, s0, s2, ALU.mult), reads=[R_s0, R_s2], writes=[R_s0])
                P.op("dve", lambda g: g.tensor_scalar(s0, s0, 1.0 / 3.0, None, ALU.add), reads=[R_s0], writes=[R_s0])
                P.op("dve", lambda g: g.tensor_tensor(s0, s0, s2, ALU.mult), reads=[R_s0, R_s2], writes=[R_s0])
                P.op("dve", lambda g: g.tensor_scalar(s0, s0, 1.0, None, ALU.add), reads=[R_s0], writes=[R_s0])
                P.op("dve", lambda g: g.tensor_tensor(s0, s0, s1, ALU.mult), reads=[R_s0, R_s1], writes=[R_s0])
                P.op("dve", lambda g: g.tensor_scalar(s2, e1, 0.5, None, ALU.is_lt), reads=[R_e1], writes=[R_s2])
                P.op("dve", lambda g: g.scalar_tensor_tensor(out=s0, in0=s0, scalar=2.0, in1=gc[:, sl], op0=ALU.mult, op1=ALU.subtract),
                     reads=[R_s0, R_gc], writes=[R_s0])
                P.op("dve", lambda g: g.tensor_tensor(s0, s0, s2, ALU.mult), reads=[R_s0, R_s2], writes=[R_s0])
                P.op("dve", lambda g: g.tensor_tensor(gc[:, sl], gc[:, sl], s0, ALU.add), reads=[R_s0, R_gc], writes=[R_gc])
                bk2 = 6 + blk % 2
                mm(pss[bk2][:], selm[0:8, 4 + h, :], dabT[0:8, sl], True, True, [R_selm, R_dab], R_ps[bk2])
                P.op("act", lambda g: g.activation(out=beta[:, sl], in_=pss[bk2][:], func=AF.Sigmoid), reads=[R_ps[bk2]], writes=[R_beta])
            P.op("dve", lambda g: g.tensor_tensor_scan(out=gc, data0=rmask[:], data1=gc, initial=0.0, op0=ALU.mult, op1=ALU.add),
                 reads=[R_rmask, R_gc], writes=[R_gc])
            P.op("dve", lambda g: g.tensor_scalar(gc, gc, nA[:, h:h + 1], None, ALU.mult), reads=[R_gc, R_nA], writes=[R_gc])
            gcl = cols[:, 0, :]
            cd = cols[:, 1, :]
            P.op("dve", lambda g: g.tensor_copy(gcl, gc[:, 127:S:128]), reads=[R_gc], writes=[R_cols])
            P.op("act", lambda g: g.activation(out=cd, in_=gcl, func=AF.Exp), reads=[R_cols], writes=[R_cols])
            P.op("act", lambda g: g.activation(out=eg, in_=gc, func=AF.Exp), reads=[R_gc], writes=[R_eg])
            P.op("dve", lambda g: g.tensor_tensor(q_eT, qnT, eg, ALU.mult), reads=[R_qn, R_eg], writes=[R_qe])
            P.op("dve", lambda g: g.tensor_tensor(kbT, knT, beta, ALU.mult), reads=[R_kn, R_beta], writes=[R_kb])
            P.op("pool", lambda g: g.tensor_tensor(eg, eg, beta, ALU.mult), reads=[R_eg, R_beta], writes=[R_eg])
            for n in range(NT):
                ch = slice(n * 128, (n + 1) * 128)
                P.op("act", lambda g: g.activation(out=tl[:, ch], in_=gc[:, ch], func=AF.Exp, scale=-1.0, bias=gcl[:, n:n + 1]),
                     reads=[R_gc, R_cols], writes=[R_tl])
            if stop == "g2":
                P.barrier(); return nc
            for qi, (src, R_src) in enumerate(((gc, R_gc), (beta, R_beta), (eg, R_eg), (tl, R_tl))):
                oh = cm_b[:, CI, 0:2] if src is beta else cm_f[:, CI, 0:2]
                for n in range(NT):
                    c0 = (qi * NT + n) * 2
                    mm(pss[3][:, c0:c0 + 2], src[:, n * 128:(n + 1) * 128], oh, True, True, [R_src, R_cmf, R_cmb], R_ps[3])
            P.op("dve", lambda g: g.tensor_copy(cols[:, 2:6, :], pss[3][:, 0:128:2].rearrange("p (a b) -> p a b", b=NT)),
                 reads=[R_ps[3]], writes=[R_cols])
            gc_col, beta_col, bexp_col, tail_col = (cols[:, i, :] for i in (2, 3, 4, 5))
            if stop == "g2b":
                P.barrier(); return nc
            for t in range(NT):
                bk = t % 2
                pT = pss[bk][:].bitcast(BF16)
                ts = slice(t * 128, (t + 1) * 128)
                P.op("pe", lambda g: g.transpose(pT[:, 0:128], knT[:, ts], ident_b), reads=[R_kn, R_cmb], writes=[R_ps[bk]])
                P.op("pe", lambda g: g.transpose(pT[:, 128:256], cvT[:, ts], ident_b), reads=[R_cv, R_cmb], writes=[R_ps[bk]], pe_accum=True)
                P.op("dve", lambda g: g.tensor_scalar(kbg_tok[:, t, :], pT[:, 0:128], bexp_col[:, t:t + 1], None, ALU.mult),
                     reads=[R_ps[bk], R_cols], writes=[R_kbg])
                P.op("dve", lambda g: g.tensor_scalar(kt_tok[:, t, :], pT[:, 0:128], tail_col[:, t:t + 1], None, ALU.mult),
                     reads=[R_ps[bk], R_cols], writes=[R_kt])
                P.op("dve", lambda g: g.tensor_scalar(vb_tok[:, t, :], pT[:, 128:256], beta_col[:, t:t + 1], None, ALU.mult),
                     reads=[R_ps[bk], R_cols], writes=[R_vb])

            if stop == "g3":
                P.barrier(); return nc
            P.barrier(reset=False)
            sz, R_sz = tl.bitcast(BF16)[:, 0:S].rearrange("p (a b) -> p a b", a=1), Reg("sz")
            mb2 = [(xbf[:, i * 128:(i + 1) * 128], Reg("mb2_%d" % i)) for i in range(16)]

            def c_z(tb, ps, R):
                P.op("act", lambda g: g.activation(out=sz[:, 0, tb * 512:(tb + 1) * 512], in_=ps, func=AF.Silu), reads=[R], writes=[R_sz])
            proj_fm(l, 22 + h, 128, wfm, c_z, 1)
            if stop == "g4":
                P.barrier(); return nc
            P.op("dve", lambda g: g.memset(Sp, 0.0), writes=[R_Sp])
            P.op("dve", lambda g: g.memset(Sbf, 0.0), writes=[R_Sbf])
            (dA, R_dA), (dB, R_dB), (dC, R_dC), (dD, R_dD) = dm
            for n in range(NT):
                ch = slice(n * 128, (n + 1) * 128)
                gcc = gc_col[:, n:n + 1]
                P.op("dve", lambda g: g.tensor_scalar(dA, gc[:, ch], gcc, 0.0, ALU.subtract, ALU.max), reads=[R_gc, R_cols], writes=[R_dA])
                P.op("act", lambda g: g.activation(out=dA, in_=dA, func=AF.Exp, scale=-1.0), reads=[R_dA], writes=[R_dA])
                P.op("dve", lambda g: g.tensor_scalar(dB, gc[:, ch], gcc, 0.0, ALU.subtract, ALU.min), reads=[R_gc, R_cols], writes=[R_dB])
                P.op("act", lambda g: g.activation(out=dB, in_=dB, func=AF.Exp), reads=[R_dB], writes=[R_dB])
                P.op("pool", lambda g: g.tensor_tensor(dC, dA, cm_f[:, CNSL, :], ALU.mult), reads=[R_dA, R_cmf], writes=[R_dC])
                P.op("pool", lambda g: g.tensor_tensor(dD, dB, cm_f[:, CNSU, :], ALU.mult), reads=[R_dB, R_cmf], writes=[R_dD])
                P.op("pool", lambda g: g.tensor_tensor(dB, dB, cm_f[:, CTU, :], ALU.mult), reads=[R_dB, R_cmf], writes=[R_dB])
                if stop == "c1":
                    P.barrier(); return nc
                (Pm, R_P), (PT, R_PT), (qkT, R_qk), (wT, R_wT), (vnew, R_vn) = mb[0:5]
                (Pd, R_Pd), (PTd, R_PTd), (Po32, R_Po32), (PTo32, R_PTo32), (Po64, R_Po64) = mb[5:10]
                b0, b1, b2 = 0, 1, 2
                mm(pss[b0][:, 0:128], kbT[:, ch], knT[:, ch], True, True, [R_kb, R_kn], R_ps[b0])
                mm(pss[b1][:, 0:128], knT[:, ch], kbT[:, ch], True, True, [R_kb, R_kn], R_ps[b1])
                mm(pss[b2][:, 0:128], knT[:, ch], qnT[:, ch], True, True, [R_kn, R_qn], R_ps[b2])
                P.op("dve", lambda g: g.tensor_tensor(Pm, pss[b0][:, 0:128], dC, ALU.mult), reads=[R_ps[b0], R_dC], writes=[R_P])
                P.op("dve", lambda g: g.tensor_tensor(PT, pss[b1][:, 0:128], dD, ALU.mult), reads=[R_ps[b1], R_dD], writes=[R_PT])
                P.op("dve", lambda g: g.tensor_tensor(qkT, pss[b2][:, 0:128], dB, ALU.mult), reads=[R_ps[b2], R_dB], writes=[R_qk])
                P.op("pool", lambda g: g.tensor_tensor(Pd, Pm, cm_b[:, CB32, :], ALU.mult), reads=[R_P, R_cmb], writes=[R_Pd])
                P.op("pool", lambda g: g.tensor_tensor(PTd, PT, cm_b[:, CB32, :], ALU.mult), reads=[R_PT, R_cmb], writes=[R_PTd])
                P.op("pool", lambda g: g.tensor_tensor(Po32, Pm, cm_b[:, CO32, :], ALU.mult), reads=[R_P, R_cmb], writes=[R_Po32])
                P.op("pool", lambda g: g.tensor_tensor(PTo32, PT, cm_b[:, CO32, :], ALU.mult), reads=[R_PT, R_cmb], writes=[R_PTo32])
                P.op("pool", lambda g: g.tensor_tensor(Po64, Pm, cm_b[:, CO64, :], ALU.mult), reads=[R_P, R_cmb], writes=[R_Po64])
                Acur, ATcur = mb2[0], mb2[1]
                Anxt, ATnxt = mb2[2], mb2[3]
                P.op("pool", lambda g: g.tensor_tensor(Acur[0], Pd, ident_b, ALU.add), reads=[R_Pd, R_cmb], writes=[Acur[1]])
                P.op("pool", lambda g: g.tensor_tensor(ATcur[0], PTd, ident_b, ALU.add), reads=[R_PTd, R_cmb], writes=[ATcur[1]])
                cur = ((Pd, R_Pd), (PTd, R_PTd))
                sqb = [(mb2[4], mb2[5]), (mb2[6], mb2[7])]
                for lev in range(4):
                    (cP, R_cP), (cPT, R_cPT) = cur
                    (nP, R_nP), (nPT, R_nPT) = sqb[lev % 2]
                    mm(pss[3][:, 0:128], cPT, cP, True, True, [R_cP, R_cPT], R_ps[3])
                    mm(pss[4][:, 0:128], cP, cPT, True, True, [R_cP, R_cPT], R_ps[4])
                    copy_on("act", nP, pss[3][:, 0:128], [R_ps[3]], [R_nP])
                    copy_on("dve", nPT, pss[4][:, 0:128], [R_ps[4]], [R_nPT])
                    mm(pss[5][:, 0:128], ident_b, Acur[0], True, False, [R_cmb, Acur[1]], R_ps[5])
                    mm(pss[5][:, 0:128], nPT, Acur[0], False, True, [R_nPT, Acur[1]], R_ps[5])
                    mm(pss[6][:, 0:128], ident_b, ATcur[0], True, False, [R_cmb, ATcur[1]], R_ps[6])
                    mm(pss[6][:, 0:128], nP, ATcur[0], False, True, [R_nP, ATcur[1]], R_ps[6])
                    copy_on("dve", Anxt[0], pss[5][:, 0:128], [R_ps[5]], [Anxt[1]])
                    copy_on("act", ATnxt[0], pss[6][:, 0:128], [R_ps[6]], [ATnxt[1]])
                    Acur, Anxt = Anxt, Acur
                    ATcur, ATnxt = ATnxt, ATcur
                    cur = ((nP, R_nP), (nPT, R_nPT))
                (U1, R_U1), (T1, R_T1) = mb2[8], mb2[9]
                mm(pss[3][:, 0:128], PTo32, Acur[0], True, True, [R_PTo32, Acur[1]], R_ps[3])
                mm(pss[4][:, 0:128], Po32, ATcur[0], True, True, [R_Po32, ATcur[1]], R_ps[4])
                copy_on("act", U1, pss[3][:, 0:128], [R_ps[3]], [R_U1])
                copy_on("dve", T1, pss[4][:, 0:128], [R_ps[4]], [R_T1])
                mm(pss[5][:, 0:128], ident_b, Acur[0], True, False, [R_cmb, Acur[1]], R_ps[5])
                mm(pss[5][:, 0:128], ATcur[0], U1, False, True, [ATcur[1], R_U1], R_ps[5])
                mm(pss[6][:, 0:128], ident_b, ATcur[0], True, False, [R_cmb, ATcur[1]], R_ps[6])
                mm(pss[6][:, 0:128], Acur[0], T1, False, True, [Acur[1], R_T1], R_ps[6])
                copy_on("dve", Anxt[0], pss[5][:, 0:128], [R_ps[5]], [Anxt[1]])
                copy_on("act", ATnxt[0], pss[6][:, 0:128], [R_ps[6]], [ATnxt[1]])
                Acur, Anxt = Anxt, Acur
                ATcur, ATnxt = ATnxt, ATcur
                mm(pss[4][:, 0:128], Po64, ATcur[0], True, True, [R_Po64, ATcur[1]], R_ps[4])
                copy_on("dve", T1, pss[4][:, 0:128], [R_ps[4]], [R_T1])
                mm(pss[6][:, 0:128], ident_b, ATcur[0], True, False, [R_cmb, ATcur[1]], R_ps[6])
                mm(pss[6][:, 0:128], Acur[0], T1, False, True, [Acur[1], R_T1], R_ps[6])
                copy_on("act", ATnxt[0], pss[6][:, 0:128], [R_ps[6]], [ATnxt[1]])
                ATc = ATnxt
                if stop == "c3":
                    P.barrier(); return nc
                AT, R_AT = ATc
                mm(pss[6][:, 0:128], AT, vb_tok[:, n, :], True, True, [R_AT, R_vb], R_ps[6])
                copy_on("act", u_sb, pss[6][:, 0:128], [R_ps[6]], [R_u])
                if stop == "c3a":
                    P.barrier(); return nc
                mm(pss[5][:, 0:128], kbg_tok[:, n, :], AT, True, True, [R_AT, R_kbg], R_ps[5])
                if stop == "c3b":
                    P.barrier(); return nc
                copy_on("dve", wT, pss[5][:, 0:128], [R_ps[5]], [R_wT])
                if stop == "c4":
                    P.barrier(); return nc
                if n > 0:
                    mm(pss[0][:, 0:128], wT, Sbf, True, True, [R_wT, R_Sbf], R_ps[0])
                    P.op("dve", lambda g: g.tensor_tensor(vnew, u_sb, pss[0][:, 0:128], ALU.subtract), reads=[R_u, R_ps[0]], writes=[R_vn])
                else:
                    copy_on("dve", vnew, u_sb, [R_u], [R_vn])
                if n > 0:
                    mm(pss[7][:, 0:128], Sbf, q_eT[:, ch], True, False, [R_Sbf, R_qe], R_ps[7])
                mm(pss[7][:, 0:128], vnew, qkT, n == 0, True, [R_vn, R_qk], R_ps[7])
                copy_on("act", o_acc[:, 0, ch], pss[7][:, 0:128], [R_ps[7]], [R_oacc])
                if n < NT - 1:
                    mm(pss[1][:, 0:128], kt_tok[:, n, :], vnew, True, True, [R_kt, R_vn], R_ps[1])
                    P.op("dve", lambda g: g.scalar_tensor_tensor(out=Sp, in0=Sp, scalar=cd[:, n:n + 1], in1=pss[1][:, 0:128],
                                                                 op0=ALU.mult, op1=ALU.add), reads=[R_Sp, R_cols, R_ps[1]], writes=[R_Sp])
                    copy_on("act", Sbf, Sp, [R_Sp], [R_Sbf])
            post_norm_store(l, o_acc, R_oacc, 1, [sc(51)], sz, R_sz, 4 + h, tmps, 128.0, 6)

        if stop == "pC2":
            P.barrier(); return nc
        for h in range(4):
            new_phase()
            wfm = [ar.alloc([128, 16, 128], BF16, "wfm%d" % i) for i in range(3)]
            wtm = ar.alloc([128, 16, 256], BF16, "wtm")
            qT, R_q = ar.alloc([128, 2, S], BF16, "qT")
            kT, R_k = ar.alloc([128, 2, S], BF16, "kT")
            v_sb, R_v = ar.alloc([128, NT, 256], BF16, "v_sb")
            sz, R_sz = ar.alloc([128, 2, S], BF16, "sz")
            ebuf = [ar.alloc([128, 512], BF16, "e%d" % i) for i in range(3)]
            qn, R_qnb = ar.alloc([128, 512], BF16, "qn")
            ta, R_ta = ar.alloc([128, 512], F32, "ta")
            tb_, R_tb = ar.alloc([128, 512], F32, "tb")
            tO, R_tO = ar.alloc([128, 2, 512], F32, "tO")
            rs, R_rs = ar.alloc([128, 512], F32, "rs")
            lamt, R_lam = ar.alloc([128, 8], F32, "lam")
            lprod, R_lprod = ar.alloc([128, 256], F32, "lprod")
            wn2, R_wn2 = ar.alloc([128, 2], F32, "wn2")
            tmps = [ar.alloc([128, 2, 512], BF16, "sq"), ar.alloc([128, 512], F32, "t1"), ar.alloc([128, 512], F32, "rinv"),
                    ar.alloc([128, 512], F32, "u"), ar.alloc([128, 512], BF16, "ost")]
            P.op("dve", lambda g: g.tensor_tensor(lprod[:, 0:128], sc(112, 128), sc(240, 128), ALU.mult), reads=[R_small], writes=[R_lprod])
            P.op("dve", lambda g: g.tensor_tensor(lprod[:, 128:256], sc(368, 128), sc(496, 128), ALU.mult), reads=[R_small], writes=[R_lprod])
            P.op("dve", lambda g: g.tensor_reduce(lamt[:, 0:2], lprod.rearrange("p (a b) -> p a b", b=128), AX.X, ALU.add),
                 reads=[R_lprod], writes=[R_lam])
            P.op("act", lambda g: g.activation(out=lamt[:, 2:4], in_=lamt[:, 0:2], func=AF.Exp), reads=[R_lam], writes=[R_lam])
            P.op("dve", lambda g: g.scalar_tensor_tensor(out=lamt[:, 4:5], in0=lamt[:, 3:4], scalar=-lam_init, in1=lamt[:, 2:3],
                                                         op0=ALU.add, op1=ALU.subtract), reads=[R_lam], writes=[R_lam])
            nlam = lamt[:, 4:5]
            P.op("dve", lambda g: g.tensor_scalar(wn2, sc(54, 2), 1.0 - lam_init, None, ALU.mult), reads=[R_small], writes=[R_wn2])
            for which, (dstT, R_dst, g0, wcol) in enumerate(((qT, R_q, 26, 52), (kT, R_k, 34, 53))):
                for m in range(2):
                    def c_qk(tb, ps, R):
                        sl = slice(tb * 512, (tb + 1) * 512)
                        rinv, R_rinv = rstd_part([(ps, R)], 512, 128.0, tmps[0:3], 2 + tb % 2)
                        P.op("dve", lambda g: g.scalar_tensor_tensor(out=qn, in0=ps, scalar=sc(wcol), in1=rinv[:, 0:512], op0=ALU.mult, op1=ALU.mult),
                             reads=[R, R_rinv, R_small], writes=[R_qnb])
                        bkr = 2 + (tb + 1) % 2
                        mm(pss[bkr][:], cm_b[:, CP, :], qn, True, True, [R_cmb, R_qnb], R_ps[bkr])
                        P.op("pool", lambda g: g.tensor_tensor(ta, qn, cosT[:, sl], ALU.mult), reads=[R_qnb, R_cos], writes=[R_ta])
                        P.op("dve", lambda g: g.tensor_tensor(tb_, pss[bkr][:], sinT[:, sl], ALU.mult), reads=[R_ps[bkr], R_sin], writes=[R_tb])
                        P.op("pool", lambda g: g.tensor_tensor(dstT[:, m, sl], ta, tb_, ALU.add), reads=[R_ta, R_tb], writes=[R_dst])
                    proj_fm(l, g0 + 2 * h + m, 128, wfm, c_qk, 1)

            def c_v(t, ps, R):
                copy_on(alt(), v_sb[:, t, :], ps, [R], [R_v])
            proj_tm(l, 2 + h, wtm, c_v, [0, 1])
            for j in range(2):
                def c_z(tb, ps, R):
                    P.op("act", lambda g: g.activation(out=sz[:, j, tb * 512:(tb + 1) * 512], in_=ps, func=AF.Silu), reads=[R], writes=[R_sz])
                proj_fm(l, 42 + 2 * h + j, 128, wfm, c_z, j)
            scale = 128.0 ** -0.5
            ei = 0
            for qb in range(4):
                qs = slice(qb * 512, (qb + 1) * 512)
                nk = 4 * (qb + 1)
                for m in range(2):
                    bo0, bo1, bs = (2, 3, 4) if m == 0 else (5, 6, 7)
                    for kc in range(nk):
                        ks = slice(kc * 128, (kc + 1) * 128)
                        bsc = kc % 2
                        c = kc - 4 * qb
                        col0 = max(c, 0) * 128
                        mm(pss[bsc][:, col0:512], kT[:, m, ks], qT[:, m, qb * 512 + col0:(qb + 1) * 512], True, True, [R_k, R_q], R_ps[bsc])
                        e, R_e = ebuf[ei % 3]
                        ei += 1
                        P.op("act", lambda g: g.activation(out=e[:, col0:512], in_=pss[bsc][:, col0:512], func=AF.Exp, scale=scale),
                             reads=[R_ps[bsc]], writes=[R_e])
                        if c >= 0:
                            P.op("pool", lambda g: g.tensor_tensor(e[:, col0:col0 + 128], e[:, col0:col0 + 128], cm_b[:, CTU, :], ALU.mult),
                                 reads=[R_e, R_cmb], writes=[R_e])
                        first, last = kc == 0, kc == nk - 1
                        mm(pss[bo0][:, col0:512], v_sb[:, kc, 0:128], e[:, col0:512], first, last, [R_v, R_e], R_ps[bo0])
                        mm(pss[bo1][:, col0:512], v_sb[:, kc, 128:256], e[:, col0:512], first, last, [R_v, R_e], R_ps[bo1])
                        mm(pss[bs][:, col0:512], ones_b, e[:, col0:512], first, last, [R_cmb, R_e], R_ps[bs])
                    P.op("dve", lambda g: g.reciprocal(rs, pss[bs][:]), reads=[R_ps[bs]], writes=[R_rs])
                    for j, bo in enumerate((bo0, bo1)):
                        if m == 0:
                            P.op("dve", lambda g: g.tensor_tensor(tO[:, j, :], pss[bo][:], rs, ALU.mult), reads=[R_ps[bo], R_rs], writes=[R_tO])
                        else:
                            P.op("dve", lambda g: g.scalar_tensor_tensor(out=ta, in0=pss[bo][:], scalar=nlam, in1=rs, op0=ALU.mult, op1=ALU.mult),
                                 reads=[R_ps[bo], R_rs, R_lam], writes=[R_ta])
                            P.op("pool", lambda g: g.tensor_tensor(tO[:, j, :], tO[:, j, :], ta, ALU.add), reads=[R_tO, R_ta], writes=[R_tO])
                (sq, R_sq), (t1, R_t1), (rinv, R_rinv), (u, R_u), (ost, R_ost) = tmps
                rstd_part([(tO[:, j, :], R_tO) for j in range(2)], 512, 256.0, tmps[0:3], 0)
                for j in range(2):
                    P.op("dve", lambda g: g.scalar_tensor_tensor(out=u, in0=tO[:, j, :], scalar=wn2[:, j:j + 1], in1=rinv, op0=ALU.mult, op1=ALU.mult),
                         reads=[R_tO, R_rinv, R_wn2], writes=[R_u])
                    P.op("pool", lambda g: g.tensor_tensor(ost, u, sz[:, j, qs], ALU.mult), reads=[R_u, R_sz], writes=[R_ost])
                    P.dma("sp", oT_d[8 + 2 * h + j, :, qs], ost, reads=[R_ost], writes=[R_oT], nowaw=True)

        if stop == "pC3":
            P.barrier(); return nc
        new_phase()
        wo = big
        for fc in range(16):
            P.dma("pool", wo[:, fc, :], wout_d[l][:, fc * D:(fc + 1) * D], writes=[R_big], nowaw=fc > 0)
        xts = [ar.alloc([128, D], F32, "xt%d" % i) for i in range(2)]
        obs = [ar.alloc([128, 16, 512], BF16, "ob%d" % i) for i in range(2)]
        xos = [ar.alloc([128, D], F32, "xo%d" % i) for i in range(2)]
        ytmp, R_yt = ar.alloc([128, 512], F32, "ytmp")
        xsrc = x_d if l == 0 else x1_d
        xdst = out_d if l == L - 1 else x1_d
        R_dst = R_out if l == L - 1 else R_x1
        for tb in range(4):
            ob, R_ob = obs[tb % 2]
            P.dma("sp", ob, oT_d[:, :, tb * 512:(tb + 1) * 512].rearrange("c p t -> p c t"), reads=[R_oT], writes=[R_ob])
            for i in range(4):
                t = tb * 4 + i
                xt, R_xt = xts[t % 2]
                xo, R_xo = xos[t % 2]
                P.dma("sp", xt, xsrc[t * 128:(t + 1) * 128, :], reads=[R_x1] if l > 0 else [], writes=[R_xt])
                for ng in range(4):
                    ns = slice(ng * 512, (ng + 1) * 512)
                    bk = (t * 4 + ng) % 4
                    for fc in range(16):
                        mm(pss[bk][:], ob[:, fc, i * 128:(i + 1) * 128], wo[:, fc, ns], fc == 0, fc == 15, [R_ob, R_big], R_ps[bk])
                    P.op("dve", lambda g: g.tensor_tensor(ytmp, pss[bk][:], gate_bc[:, ns], ALU.mult), reads=[R_ps[bk], R_gate], writes=[R_yt])
                    P.op("pool", lambda g: g.tensor_tensor(xo[:, ns], ytmp, xt[:, ns], ALU.add), reads=[R_yt, R_xt], writes=[R_xo])
                P.dma("sp", xdst[t * 128:(t + 1) * 128, :], xo, reads=[R_xo], writes=[R_dst], nowaw=True)

    P.barrier()
    return nc


def _col(v):
    return np.ascontiguousarray(np.asarray(v, np.float32).reshape(-1, 128).T)


def _consts():
    p = np.arange(128)[:, None]
    j = np.arange(128)[None, :]
    cm = np.zeros((128, NCM, 128), np.float32)
    cm[:, CI] = (p == j)
    cm[:, CO] = 1.0
    prot = np.zeros((128, 128), np.float32)
    prot[(j[0, :64] + 64), j[0, :64]] = -1.0
    prot[(j[0, 64:] - 64), j[0, 64:]] = 1.0
    cm[:, CP] = prot
    cm[:, CTU] = (p <= j)
    cm[:, CNSU] = -1.0 * (p < j)
    cm[:, CNSL] = -1.0 * (p > j)
    bd32 = (p // 32 == j // 32)
    bd64 = (p // 64 == j // 64)
    cm[:, CB32] = bd32
    cm[:, CO32] = bd64 & ~bd32
    cm[:, CO64] = ~bd64
    selm = np.zeros((8, 8, 128), np.float32)
    for k in range(8):
        selm[k, k, :] = 1.0
    rmask = np.ones((128, S), np.float32)
    rmask[:, 0::128] = 0.0
    half = 64
    inv_freq = (10000.0 ** (-(np.arange(half, dtype=np.float32) / np.float32(half)))).astype(np.float32)
    invf = np.concatenate([inv_freq, inv_freq]).astype(np.float64) / (2.0 * math.pi)
    return cm.reshape(128, NCM * 128), selm.reshape(8, 8 * 128), rmask, invf.astype(np.float32)


FM_GROUPS = ([(0, 128), (128, 128), (256, 128), (384, 128), (1024, 16)] + [(1040 + 128 * i, 128) for i in range(4)]
             + [(1552 + 128 * i, 128) for i in range(12)] + [(3088, 8)] + [(3096 + 128 * i, 128) for i in range(4)]
             + [(3608 + 128 * i, 128) for i in range(8)] + [(4632 + 128 * i, 128) for i in range(8)]
             + [(6680 + 128 * i, 128) for i in range(8)])
TM_GROUPS = [(512, 256), (768, 256)] + [(5656 + 256 * i, 256) for i in range(4)]


def _prep_shared(inp):
    f = lambda k: np.asarray(inp[k], np.float32)
    cm, selm, rmask, invf = _consts()
    w_in = f("w_in")
    winfm = np.zeros((2, 50, 128, 16, 128), np.float32)
    wintm = np.zeros((2, 6, 128, 16, 256), np.float32)
    for l in range(2):
        wl = w_in[l].reshape(16, 128, -1)
        for gi, (c0, n) in enumerate(FM_GROUPS):
            winfm[l, gi, :, :, :n] = wl[:, :, c0:c0 + n].transpose(1, 0, 2)
        for gi, (c0, n) in enumerate(TM_GROUPS):
            wintm[l, gi] = wl[:, :, c0:c0 + n].transpose(1, 0, 2)
    wada = np.ascontiguousarray(f("w_ada").reshape(2, 16, 128, 48, 128).transpose(0, 3, 2, 1, 4)).reshape(2, 48, 128, 16 * 128)
    wout = np.ascontiguousarray(f("w_out").reshape(2, 16, 128, D).transpose(0, 2, 1, 3)).reshape(2, 128, 16 * D)
    bgate = np.ascontiguousarray(f("b_ada")[:, 2 * D:].reshape(2, 1, D))
    sm = np.zeros((128, NS), np.float32)
    sm[:, 16] = invf
    for l in range(2):
        b = SBASE + l * SLW
        sm[:, b:b + 16] = _col(f("norm_w")[l])
        sm[:, b + 16:b + 48] = _col(f("b_ada")[l, :2 * D])
        sm[:, b + 48:b + 50] = _col(f("gla_b_lr")[l])
        sm[:, b + 50] = f("gla_norm_w")[l]
        sm[:, b + 51] = f("gdn_norm_w")[l]
        sm[:, b + 52] = f("diff_q_norm_w")[l]
        sm[:, b + 53] = f("diff_k_norm_w")[l]
        sm[:, b + 54:b + 56] = _col(f("diff_norm_w")[l])
        sm[:, b + 56:b + 60] = f("gdn_a_log")[l][None, :]
        sm[:, b + 60:b + 64] = f("gdn_dt_bias")[l][None, :]
        cw = f("gdn_conv_w")[l]
        sm[:, b + 64:b + 112] = cw.reshape(4, 12, 128).transpose(2, 1, 0).reshape(128, 48)
        sm[:, b + 112:b + 624] = f("diff_lambda")[l].reshape(1, 512)
        sm[0:16, b + 624:b + 880] = f("gla_w_lr")[l]
    shared = {"cmat": cm, "selm": selm, "rmask": rmask, "wada": wada, "bgate": bgate,
              "winfm": winfm.reshape(2, 50, 128, 16 * 128), "wintm": wintm.reshape(2, 6, 128, 16 * 256), "wout": wout}
    return shared, sm


def make_in_maps(inp, cores):
    shared, sm = _prep_shared(inp)
    x = np.asarray(inp["x"], np.float32)
    c = np.asarray(inp["c"], np.float32)
    pos = np.asarray(inp["positions"], np.int32)
    maps = []
    for b in cores:
        s = sm.copy()
        s[:, 0:16] = _col(c[b])
        m = dict(shared)
        m["x"] = np.ascontiguousarray(x[b])
        m["small"] = s
        m["pos"] = np.ascontiguousarray(pos[b:b + 1])
        maps.append(m)
    return maps


_NC_CACHE = {}


def kernel(**inputs):
    if "nc" not in _NC_CACHE:
        _NC_CACHE["nc"] = build(2)
    nc = _NC_CACHE["nc"]
    cores = [i // 2 for i in range(8)]
    maps = make_in_maps(inputs, cores)
    res = run_bass_kernel_spmd(nc, maps, core_ids=list(range(8)))
    out = np.stack([np.asarray(res.results[2 * b]["out"], np.float32) for b in range(4)], axis=0)
    return out
```

```python
import math
import numpy as np
import concourse.bass as bass
import concourse.mybir as mybir
from concourse.bass_utils import run_bass_kernel_spmd

F32 = mybir.dt.float32
BF16 = mybir.dt.bfloat16
I32 = mybir.dt.int32
AF = mybir.ActivationFunctionType
ALU = mybir.AluOpType
AX = mybir.AxisListType

S = 2048
D = 2048
NT = 16
EPS = 1e-6
SLW = 880
SBASE = 32
NS = SBASE + 2 * SLW
CI, CO, CP, CTU, CNSU, CNSL, CB32, CO32, CO64 = range(9)
NCM = 9


class Reg:
    __slots__ = ("name", "lw", "rd", "dsem", "local", "psum")

    def __init__(self, name, local=False, psum=False):
        self.name = name
        self.psum = psum
        self.lw = None
        self.rd = {}
        self.dsem = None
        self.local = local


class Prog:
    def __init__(self, nc):
        self.nc = nc
        self.eng = {"pe": nc.tensor, "act": nc.scalar, "dve": nc.vector, "pool": nc.gpsimd, "sp": nc.sync}
        self.sem, self.cnt, self.semobj = {}, {}, {}
        self.seen = {e: {} for e in self.eng}
        for e in self.eng:
            s = nc.alloc_semaphore(name="s_" + e)
            self.sem[e] = s
            self.semobj[e] = s
            self.cnt[e] = 0
        self.vc = {}
        self.dcnt = {}
        self.local_keys = []
        self.local_next = 0
        self.ninstr = 0
        self.nwait = 0

    def _wait(self, e, key, val):
        if self.seen[e].get(key, 0) >= val:
            return
        self.eng[e].wait_ge(self.semobj[key], val)
        self.nwait += 1
        se = self.seen[e]
        se[key] = val
        clk = self.vc.get((key, val))
        if clk:
            for k, v in clk.items():
                if se.get(k, 0) < v:
                    se[k] = v

    def _deps(self, e, reads, writes, pe_accum=False, nowaw=False):
        need = {}

        def add(k, v):
            if need.get(k, 0) < v:
                need[k] = v
        for r in reads:
            if r.lw is not None:
                add(*r.lw)
            if r.psum:
                for k, v in r.rd.items():
                    if k != e:
                        add(k, v)
        for w in writes:
            if w.lw is not None and not nowaw and not (pe_accum and w.lw[0] == "pe"):
                add(*w.lw)
            for k, v in w.rd.items():
                add(k, v)
        for k, v in sorted(need.items(), key=lambda kv: -kv[1] if isinstance(kv[0], str) else 0):
            self._wait(e, k, v)

    def _record(self, ev, reads, writes, nowaw=False):
        for r in reads:
            if r.rd.get(ev[0], 0) < ev[1]:
                r.rd[ev[0]] = ev[1]
        for w in writes:
            w.lw = ev
            if not nowaw:
                w.rd = {}

    def op(self, e, fn, reads=(), writes=(), pe_accum=False):
        self._deps(e, reads, writes, pe_accum)
        ins = fn(self.eng[e])
        self.cnt[e] += 1
        ins.then_inc(self.sem[e], 1)
        ev = (e, self.cnt[e])
        self.vc[ev] = dict(self.seen[e])
        self._record(ev, reads, writes)
        self.ninstr += 1

    def _newkey(self):
        key = ("d", len(self.dcnt))
        self.semobj[key] = self.nc.alloc_semaphore(name="d_%d" % len(self.dcnt))
        self.dcnt[key] = 0
        return key

    def _dkey(self, w):
        if w.dsem is None:
            if w.local:
                if self.local_next == len(self.local_keys):
                    self.local_keys.append(self._newkey())
                w.dsem = self.local_keys[self.local_next]
                self.local_next += 1
            else:
                w.dsem = self._newkey()
        return w.dsem

    def dma(self, q, out_ap, in_ap, reads=(), writes=(), nowaw=False, **kw):
        w = writes[0]
        self._deps(q, reads, writes, nowaw=nowaw)
        key = self._dkey(w)
        ins = self.eng[q].dma_start(out=out_ap, in_=in_ap, **kw)
        self.dcnt[key] += 16
        ins.then_inc(self.semobj[key], 16)
        ev = (key, self.dcnt[key])
        self.vc[ev] = dict(self.seen[q])
        self._record(ev, reads, writes, nowaw=nowaw)
        self.ninstr += 1

    def barrier(self, reset=True):
        evs = [(e, self.cnt[e]) for e in self.eng if self.cnt[e] > 0]
        evs += [(k, v) for k, v in self.dcnt.items() if v > 0]
        for e in self.eng:
            for k, v in evs:
                if k != e:
                    self._wait(e, k, v)
        if reset:
            self.local_next = 0


class Arena:
    def __init__(self, nc, nwords):
        self.t = nc.alloc_sbuf_tensor("arena", [128, nwords], F32)
        self.n = nwords
        self.off = 0
        self.k = 0

    def reset(self):
        self.off = 0

    def alloc(self, shape, dt, name=None):
        free = int(np.prod(shape[1:]))
        words = free if dt in (F32, I32) else (free + 1) // 2
        words = (words + 7) // 8 * 8
        assert self.off + words <= self.n, ("arena overflow", name, self.off, words, self.n)
        v = self.t[:, self.off:self.off + words]
        self.off += words
        if dt == BF16:
            v = v.bitcast(BF16)[:, 0:free]
        elif dt == I32:
            v = v.bitcast(I32)[:, 0:free]
        else:
            v = v[:, 0:free]
        if len(shape) == 3:
            v = v.rearrange("p (a b) -> p a b", b=shape[2])
        self.k += 1
        if shape[0] < 128:
            v = v[0:shape[0]]
        return v, Reg(name or "a%d" % self.k, local=True)


def build(nlayers=2, debug=False, stop=None, scopes=False):
    nc = bass.Bass("TRN2", target_bir_lowering=False)
    P = Prog(nc)
    L = nlayers

    def din(name, shape, dt=F32):
        return nc.dram_tensor(name, shape, dt, kind="ExternalInput").ap()

    x_d = din("x", [S, D])
    small_d = din("small", [128, NS])
    cmat_d = din("cmat", [128, NCM * 128])
    selm_d = din("selm", [8, 8 * 128])
    rmask_d = din("rmask", [128, S])
    pos_d = din("pos", [1, S], I32)
    wada_d = din("wada", [2, 48, 128, 16 * 128])
    bgate_d = din("bgate", [2, 1, D])
    winfm_d = din("winfm", [2, 50, 128, 16 * 128])
    wintm_d = din("wintm", [2, 6, 128, 16 * 256])
    wout_d = din("wout", [2, 128, 16 * D])
    out_d = nc.dram_tensor("out", [S, D], F32, kind="ExternalOutput").ap()
    x1_d = nc.dram_tensor("x1s", [S, D], F32, kind="Internal").ap()
    oT_d = nc.dram_tensor("oTs", [16, 128, S], BF16, kind="ExternalOutput" if debug else "Internal").ap()
    R_out, R_x1, R_oT = Reg("out"), Reg("x1"), Reg("oT")

    def sb(name, shape, dt):
        return nc.alloc_sbuf_tensor("sb_" + name, shape, dt), Reg(name)

    big, R_big = sb("big", [128, 16, S], BF16)
    cosT, R_cos = sb("cosT", [128, S], BF16)
    sinT, R_sin = sb("sinT", [128, S], BF16)
    gate_bc, R_gate = sb("gate_bc", [128, D], F32)
    rmask, R_rmask = sb("rmask", [128, S], BF16)
    cm_f, R_cmf = sb("cm_f", [128, NCM, 128], F32)
    cm_b, R_cmb = sb("cm_b", [128, NCM, 128], BF16)
    selm, R_selm = sb("selm", [8, 8, 128], F32)
    small, R_small = sb("small", [128, NS], F32)
    modc, R_modc = sb("modc", [128, 48], F32)
    misc, R_misc = sb("misc", [128, 64], F32)
    ar = Arena(nc, (nc.sbuf_bytes_remaining - 2048) // 4)

    pss = [nc.alloc_psum_tensor("ps%d" % i, [128, 512], F32) for i in range(8)]
    R_ps = [Reg("ps%d" % i, psum=True) for i in range(8)]

    state = {"alt": 0}

    def alt():
        state["alt"] ^= 1
        return "act" if state["alt"] else "dve"

    def copy_on(e, out, in_, reads, writes):
        if e == "act":
            P.op("act", lambda g: g.activation(out=out, in_=in_, func=AF.Copy), reads=reads, writes=writes)
        else:
            P.op(e, lambda g: g.tensor_copy(out, in_), reads=reads, writes=writes)

    def mm(out, lhsT, rhs, start, stop, reads, w):
        P.op("pe", lambda g: g.matmul(out, lhsT=lhsT, rhs=rhs, start=start, stop=stop),
             reads=reads, writes=[w], pe_accum=not start)

    def new_phase(name="ph"):
        P.barrier()
        ar.reset()
        if scopes:
            if state.get("scope") is not None:
                nc.leave_named_scope(state["scope"][0], state["scope"][1], False)
            nm = "%s_%d" % (name, state.setdefault("nscope", 0))
            state["nscope"] += 1
            sid, _ = nc.enter_named_scope(nm, False)
            state["scope"] = (nm, sid)

    ident_b = cm_b[:, CI, :]
    ones_b = cm_b[:, CO, :]
    ident_f = cm_f[:, CI, :]

    def rstd_part(srcs, n, denom, tmps, psi):
        (sq, R_sq), (t1, R_t1), (rinv, R_rinv) = tmps
        for i, (sap, sreg) in enumerate(srcs):
            P.op("act", lambda g: g.activation(out=sq[:, i, 0:n], in_=sap, func=AF.Square), reads=[sreg], writes=[R_sq])
        for i in range(len(srcs)):
            mm(pss[psi][:, 0:n], ones_b, sq[:, i, 0:n], i == 0, i == len(srcs) - 1, [R_sq, R_cmb], R_ps[psi])
        P.op("act", lambda g: g.activation(out=t1[:, 0:n], in_=pss[psi][:, 0:n], func=AF.Sqrt, scale=1.0 / denom, bias=eps_ap),
             reads=[R_ps[psi], R_misc], writes=[R_t1])
        P.op("dve", lambda g: g.reciprocal(rinv[:, 0:n], t1[:, 0:n]), reads=[R_t1], writes=[R_rinv])
        return rinv, R_rinv

    P.dma("sp", small[:], small_d, writes=[R_small])
    P.dma("sp", cm_f[:], cmat_d.rearrange("p (a b) -> p a b", b=128), writes=[R_cmf])
    P.dma("sp", selm[:], selm_d.rearrange("p (a b) -> p a b", b=128), writes=[R_selm])
    P.dma("pool", rmask[:], rmask_d, writes=[R_rmask])
    P.op("dve", lambda g: g.tensor_copy(cm_b[:], cm_f[:]), reads=[R_cmf], writes=[R_cmb])
    eps_ap = misc[:, 0:1]
    P.op("dve", lambda g: g.memset(misc[:], 0.0), writes=[R_misc])
    P.op("dve", lambda g: g.memset(misc[:, 0:1], EPS), reads=[], writes=[R_misc])
    P.op("dve", lambda g: g.memset(misc[:, 1:2], 1.0), reads=[], writes=[R_misc])
    one_ap = misc[:, 1:2]

    posi, R_posi = ar.alloc([128, S], I32, "posi")
    y, R_y = ar.alloc([128, S], F32, "y")
    yi, R_yi = ar.alloc([128, S], I32, "yi")
    yf, R_yf = ar.alloc([128, S], F32, "yf")
    fr, R_fr = ar.alloc([128, S], F32, "fr")
    m1, R_m1 = ar.alloc([128, S], F32, "m1")
    P.dma("sp", posi, pos_d.to_broadcast([128, S]), writes=[R_posi])
    P.op("dve", lambda g: g.tensor_copy(y, posi), reads=[R_posi], writes=[R_y])
    P.op("dve", lambda g: g.tensor_scalar(y, y, small[:, 16:17], None, ALU.mult), reads=[R_y, R_small], writes=[R_y])
    P.op("dve", lambda g: g.tensor_copy(yi, y), reads=[R_y], writes=[R_yi])
    P.op("dve", lambda g: g.tensor_copy(yf, yi), reads=[R_yi], writes=[R_yf])
    P.op("dve", lambda g: g.tensor_tensor(fr, y, yf, ALU.subtract), reads=[R_y, R_yf], writes=[R_fr])
    for which, dst, R_dst in ((0, sinT, R_sin), (1, cosT, R_cos)):
        src = fr
        if which == 1:
            P.op("dve", lambda g: g.tensor_scalar(y, fr, 0.25, None, ALU.add), reads=[R_fr], writes=[R_y])
            src = y
        R_src = R_fr if which == 0 else R_y
        P.op("dve", lambda g: g.tensor_scalar(m1, src, 0.5, None, ALU.is_gt), reads=[R_src], writes=[R_m1])
        P.op("dve", lambda g: g.tensor_tensor(yf, src, m1, ALU.subtract), reads=[R_src, R_m1], writes=[R_yf])
        P.op("dve", lambda g: g.tensor_scalar(m1, yf, -0.5, None, ALU.is_lt), reads=[R_yf], writes=[R_m1])
        P.op("dve", lambda g: g.tensor_tensor(yf, yf, m1, ALU.add), reads=[R_yf, R_m1], writes=[R_yf])
        P.op("act", lambda g: g.activation(out=dst[:], in_=yf, func=AF.Sin, scale=2.0 * math.pi), reads=[R_yf], writes=[R_dst])

    if stop == "p0":
        P.barrier(); return nc
    def proj_fm(l, gi, M, wbufs, consume, bankset):
        wb, R_wb = wbufs[state.setdefault("wfm_i", 0) % len(wbufs)]
        state["wfm_i"] += 1
        P.dma("pool", wb, winfm_d[l, gi].rearrange("p (a b) -> p a b", b=128), writes=[R_wb])
        banks = [bankset * 4 + i for i in range(4)]
        for kc in range(16):
            for tb in range(4):
                mm(pss[banks[tb]][0:M, :], wb[:, kc, 0:M], big[:, kc, tb * 512:(tb + 1) * 512], kc == 0, kc == 15,
                   [R_wb, R_big], R_ps[banks[tb]])
        for tb in range(4):
            consume(tb, pss[banks[tb]][0:M, :], R_ps[banks[tb]])

    def proj_tm(l, gi, wtb, consume, banks):
        wb, R_wb = wtb
        P.dma("pool", wb, wintm_d[l, gi].rearrange("p (a b) -> p a b", b=256), writes=[R_wb])
        for t in range(NT):
            bk = banks[t % len(banks)]
            for kc in range(16):
                mm(pss[bk][:, 0:256], big[:, kc, t * 128:(t + 1) * 128], wb[:, kc, :], kc == 0, kc == 15,
                   [R_wb, R_big], R_ps[bk])
            consume(t, pss[bk][:, 0:256], R_ps[bk])

    def post_norm_store(l, o_acc, R_oacc, nsub, wcols, szs, R_sz, fc0, tmps, denom, psi):
        (sq, R_sq), (t1, R_t1), (rinv, R_rinv), (u, R_u), (ost, R_ost) = tmps
        for blk in range(4):
            sl = slice(blk * 512, (blk + 1) * 512)
            rstd_part([(o_acc[:, j, sl], R_oacc) for j in range(nsub)], 512, denom, tmps[0:3], psi)
            for j in range(nsub):
                P.op("dve", lambda g: g.scalar_tensor_tensor(out=u[:, 0:512], in0=o_acc[:, j, sl], scalar=wcols[j], in1=rinv[:, 0:512],
                                                             op0=ALU.mult, op1=ALU.mult),
                     reads=[R_oacc, R_rinv, R_small, R_misc], writes=[R_u])
                P.op("pool", lambda g: g.tensor_tensor(ost[:, 0:512], u[:, 0:512], szs[:, j, sl], ALU.mult),
                     reads=[R_u, R_sz], writes=[R_ost])
                P.dma("sp", oT_d[fc0 + j, :, sl], ost[:, 0:512], reads=[R_ost], writes=[R_oT], nowaw=True)

    for l in range(L):
        sb_l = SBASE + l * SLW
        lam_init = 0.8 - 0.6 * math.exp(-0.3 * l)

        def sc(off, n=1):
            return small[:, sb_l + off: sb_l + off + n]

        new_phase("A")
        cact, R_cact = ar.alloc([128, 16], F32, "cact")
        c2, R_c2 = ar.alloc([128, 16, 2], F32, "c2")
        crep, R_crep = ar.alloc([128, 16, 128], F32, "crep")
        bg, R_bg = ar.alloc([128, D], F32, "bg")
        wab = [ar.alloc([128, 16, 128], F32, "wa%d" % i) for i in range(3)]
        P.op("act", lambda g: g.activation(out=cact, in_=small[:, 0:16], func=AF.Silu), reads=[R_small], writes=[R_cact])
        P.op("dve", lambda g: g.tensor_copy(c2, cact.unsqueeze(2).to_broadcast([128, 16, 2])), reads=[R_cact], writes=[R_c2])
        P.op("dve", lambda g: g.tensor_copy(crep, cact.unsqueeze(2).to_broadcast([128, 16, 128])), reads=[R_cact], writes=[R_crep])
        P.dma("sp", bg, bgate_d[l].to_broadcast([128, D]), writes=[R_bg])
        for g_ in range(48):
            wa, R_wa = wab[g_ % 3]
            P.dma("sp", wa, wada_d[l, g_].rearrange("p (a b) -> p a b", b=128), writes=[R_wa])
            if g_ < 32:
                for kc in range(16):
                    mm(pss[0][:, 2 * g_:2 * g_ + 2], wa[:, kc, :], c2[:, kc, :], kc == 0, kc == 15, [R_wa, R_c2], R_ps[0])
                if g_ == 31:
                    P.op("dve", lambda g: g.tensor_tensor(modc[:, 0:32], pss[0][:, 0:64:2], sc(16, 32), ALU.add),
                         reads=[R_ps[0], R_small], writes=[R_modc])
                    P.op("dve", lambda g: g.scalar_tensor_tensor(out=modc[:, 32:48], in0=modc[:, 16:32], scalar=1.0, in1=sc(0, 16),
                                                                 op0=ALU.add, op1=ALU.mult),
                         reads=[R_modc, R_small], writes=[R_modc])
            else:
                gg = g_ - 32
                bk = 1 + (gg // 4) % 2
                c0 = (gg % 4) * 128
                for kc in range(16):
                    mm(pss[bk][:, c0:c0 + 128], crep[:, kc, :], wa[:, kc, :], kc == 0, kc == 15, [R_wa, R_crep], R_ps[bk])
                if gg % 4 == 3:
                    sl = slice((gg // 4) * 512, (gg // 4 + 1) * 512)
                    P.op("dve", lambda g: g.tensor_tensor(gate_bc[:, sl], pss[bk][:], bg[:, sl], ALU.add),
                         reads=[R_ps[bk], R_bg], writes=[R_gate])

        if stop == "pA":
            P.barrier(); return nc
        new_phase("B")
        xsrc = x_d if l == 0 else x1_d
        xts = [ar.alloc([128, D], F32, "xt%d" % i) for i in range(2)]
        xn, R_xn = ar.alloc([128, 4, D], BF16, "xn")
        junk, R_junk = ar.alloc([128, D], BF16, "junk")
        ssq, R_ssq = ar.alloc([128, 8], F32, "ssq")
        hT = big
        for tb in range(4):
            for i in range(4):
                t = tb * 4 + i
                xt, R_xt = xts[t % 2]
                P.dma("sp", xt, xsrc[t * 128:(t + 1) * 128, :], reads=[R_x1] if l > 0 else [], writes=[R_xt])
                P.op("act", lambda g: g.activation(out=junk, in_=xt, func=AF.Square, accum_out=ssq[:, 0:1]),
                     reads=[R_xt], writes=[R_junk, R_ssq])
                P.op("act", lambda g: g.activation(out=ssq[:, 1:2], in_=ssq[:, 0:1], func=AF.Sqrt, scale=1.0 / D, bias=eps_ap),
                     reads=[R_ssq, R_misc], writes=[R_ssq])
                P.op("dve", lambda g: g.reciprocal(ssq[:, 2:3], ssq[:, 1:2]), reads=[R_ssq], writes=[R_ssq])
                P.op("dve", lambda g: g.tensor_scalar(xn[:, i, :], xt, ssq[:, 2:3], None, ALU.mult),
                     reads=[R_xt, R_ssq], writes=[R_xn])
            for fc in range(16):
                bk = fc % 4
                pT = pss[bk][:].bitcast(BF16)
                for i in range(4):
                    P.op("pe", lambda g: g.transpose(pT[:, i * 128:(i + 1) * 128], xn[:, i, fc * 128:(fc + 1) * 128], ident_b),
                         reads=[R_xn, R_cmb], writes=[R_ps[bk]], pe_accum=i > 0)
                dst = hT[:, fc, tb * 512:(tb + 1) * 512]
                if alt() == "act":
                    P.op("act", lambda g: g.activation(out=dst, in_=pT[:, 0:512], func=AF.Identity,
                                                       scale=modc[:, 32 + fc:33 + fc], bias=modc[:, fc:fc + 1]),
                         reads=[R_ps[bk], R_modc], writes=[R_big])
                else:
                    P.op("dve", lambda g: g.tensor_scalar(dst, pT[:, 0:512], modc[:, 32 + fc:33 + fc], modc[:, fc:fc + 1],
                                                          ALU.mult, ALU.add),
                         reads=[R_ps[bk], R_modc], writes=[R_big])

        if stop == "pB":
            P.barrier(); return nc
        for pr in range(2):
            new_phase("gla")
            wfm = [ar.alloc([128, 16, 128], BF16, "wfm%d" % i) for i in range(3)]
            wtm = ar.alloc([128, 16, 256], BF16, "wtm")
            glrT, R_glrT = ar.alloc([16, S], BF16, "glrT")
            wlr, R_wlr = ar.alloc([16, 256], BF16, "wlr")
            bcs, R_bcs = ar.alloc([128, S], F32, "bcs")
            eb, R_eb = ar.alloc([128, S], F32, "eb")
            q_eT, R_qe = ar.alloc([128, S], BF16, "q_eT")
            k_eT, R_ke = ar.alloc([128, S], BF16, "k_eT")
            v_g, R_vg = ar.alloc([128, NT, 256], BF16, "v_g")
            sz, R_sz = ar.alloc([128, 2, S], BF16, "sz")
            ke_tok, R_ket = ar.alloc([128, NT, 128], BF16, "ke_tok")
            o_acc, R_oacc = ar.alloc([128, 2, S], F32, "o_acc")
            e1, R_e1 = ar.alloc([128, 512], F32, "e1")
            dec, R_dec = ar.alloc([128, 16], F32, "dec")
            nb, R_nb = ar.alloc([128, 2], F32, "nb")
            Sp, R_Sp = ar.alloc([128, 256], F32, "Sp")
            Stmp, R_Stmp = ar.alloc([128, 256], F32, "Stmp")
            Sbf, R_Sbf = ar.alloc([128, 256], BF16, "Sbf")
            attm, R_attm = ar.alloc([128, 2, 128], BF16, "attm")
            tmps = [ar.alloc([128, 2, 512], BF16, "sq"), ar.alloc([128, 512], F32, "t1"), ar.alloc([128, 512], F32, "rinv"),
                    ar.alloc([128, 512], F32, "u"), ar.alloc([128, 512], BF16, "ost")]

            def c_glr(tb, ps, R):
                copy_on("act", glrT[0:16, tb * 512:(tb + 1) * 512], ps, [R], [R_glrT])
            proj_fm(l, 4, 16, wfm, c_glr, 0)
            P.op("dve", lambda g: g.tensor_copy(wlr, sc(624, 256)[0:16, :]), reads=[R_small], writes=[R_wlr])
            P.op("dve", lambda g: g.tensor_scalar(nb, sc(48, 2), -1.0, None, ALU.mult), reads=[R_small], writes=[R_nb])
            for blk in range(4):
                sl = slice(blk * 512, (blk + 1) * 512)
                bk = 4 + blk % 2
                mm(pss[bk][:], wlr[0:16, pr * 128:(pr + 1) * 128], glrT[0:16, sl], True, True, [R_wlr, R_glrT], R_ps[bk])
                P.op("act", lambda g: g.activation(out=e1, in_=pss[bk][:], func=AF.Exp, scale=-1.0, bias=nb[:, pr:pr + 1]),
                     reads=[R_ps[bk], R_nb], writes=[R_e1])
                P.op("act", lambda g: g.activation(out=bcs[:, sl], in_=e1, func=AF.Ln, bias=one_ap), reads=[R_e1, R_misc], writes=[R_bcs])
            P.op("dve", lambda g: g.tensor_tensor_scan(out=bcs, data0=rmask[:], data1=bcs, initial=0.0, op0=ALU.mult, op1=ALU.add),
                 reads=[R_rmask, R_bcs], writes=[R_bcs])
            P.op("act", lambda g: g.activation(out=eb, in_=bcs, func=AF.Exp, scale=-1.0 / 16.0), reads=[R_bcs], writes=[R_eb])
            P.op("dve", lambda g: g.tensor_copy(dec, eb[:, 127:S:128]), reads=[R_eb], writes=[R_dec])
            P.op("act", lambda g: g.activation(out=bcs, in_=bcs, func=AF.Exp, scale=1.0 / 16.0), reads=[R_bcs], writes=[R_bcs])
            enb = bcs

            def c_q(tb, ps, R):
                sl = slice(tb * 512, (tb + 1) * 512)
                P.op("dve", lambda g: g.scalar_tensor_tensor(out=q_eT[:, sl], in0=ps, scalar=0.125, in1=eb[:, sl], op0=ALU.mult, op1=ALU.mult),
                     reads=[R, R_eb], writes=[R_qe])
            proj_fm(l, pr, 128, wfm, c_q, 1)

            def c_k(tb, ps, R):
                sl = slice(tb * 512, (tb + 1) * 512)
                P.op("dve", lambda g: g.tensor_tensor(k_eT[:, sl], ps, enb[:, sl], ALU.mult), reads=[R, R_bcs], writes=[R_ke])
            proj_fm(l, 2 + pr, 128, wfm, c_k, 0)

            def c_v(t, ps, R):
                copy_on("act", v_g[:, t, :], ps, [R], [R_vg])
            proj_tm(l, pr, wtm, c_v, [4, 5])
            for hh in range(2):
                def c_z(tb, ps, R):
                    P.op("act", lambda g: g.activation(out=sz[:, hh, tb * 512:(tb + 1) * 512], in_=ps, func=AF.Silu), reads=[R], writes=[R_sz])
                proj_fm(l, 5 + 2 * pr + hh, 128, wfm, c_z, hh)
            for t4 in range(4):
                bk = 6 + t4 % 2
                pT = pss[bk][:].bitcast(BF16)
                for i in range(4):
                    t = t4 * 4 + i
                    P.op("pe", lambda g: g.transpose(pT[:, i * 128:(i + 1) * 128], k_eT[:, t * 128:(t + 1) * 128], ident_b),
                         reads=[R_ke, R_cmb], writes=[R_ps[bk]], pe_accum=i > 0)
                P.op("dve", lambda g: g.tensor_copy(ke_tok[:, t4 * 4:(t4 + 1) * 4, :], pT[:, 0:512].rearrange("p (a b) -> p a b", b=128)),
                     reads=[R_ps[bk]], writes=[R_ket])
            P.op("dve", lambda g: g.memset(Sp, 0.0), writes=[R_Sp])
            for n in range(NT):
                ch = slice(n * 128, (n + 1) * 128)
                ba, bo, bkv = n % 2, 2 + n % 2, 4 + n % 2
                for hh in range(2):
                    hp = slice(64 * hh, 64 * hh + 64)
                    mm(pss[ba][:, hh * 128:(hh + 1) * 128], k_eT[hp, ch], q_eT[hp, ch], True, True, [R_ke, R_qe], R_ps[ba])
                P.op("dve", lambda g: g.tensor_tensor(attm, pss[ba][:, 0:256].rearrange("p (a b) -> p a b", b=128),
                                                      cm_f[:, CTU:CTU + 1, :].to_broadcast([128, 2, 128]), ALU.mult),
                     reads=[R_ps[ba], R_cmf], writes=[R_attm])
                for hh in range(2):
                    hp = slice(64 * hh, 64 * hh + 64)
                    vs = slice(hh * 128, (hh + 1) * 128)
                    mm(pss[bo][:, vs], v_g[:, n, vs], attm[:, hh, :], True, n == 0, [R_vg, R_attm], R_ps[bo])
                    if n > 0:
                        mm(pss[bo][:, vs], Sbf[hp, vs], q_eT[hp, ch], False, True, [R_Sbf, R_qe], R_ps[bo])
                P.op("act", lambda g: g.activation(out=o_acc[:, :, ch], in_=pss[bo][:, 0:256].rearrange("p (a b) -> p a b", b=128), func=AF.Copy),
                     reads=[R_ps[bo]], writes=[R_oacc])
                if n < NT - 1:
                    mm(pss[bkv][:, 0:256], ke_tok[:, n, :], v_g[:, n, :], True, True, [R_ket, R_vg], R_ps[bkv])
                    P.op("dve", lambda g: g.tensor_tensor(Stmp, Sp, pss[bkv][:, 0:256], ALU.add), reads=[R_Sp, R_ps[bkv]], writes=[R_Stmp])
                    P.op("dve", lambda g: g.tensor_scalar(Sp, Stmp, dec[:, n:n + 1], None, ALU.mult), reads=[R_Stmp, R_dec], writes=[R_Sp])
                    P.op("pool", lambda g: g.tensor_scalar(Sbf, Stmp, dec[:, n:n + 1], None, ALU.mult), reads=[R_Stmp, R_dec], writes=[R_Sbf])
            for hh in range(2):
                post_norm_store(l, o_acc[:, hh:hh + 1, :], R_oacc, 1, [sc(50)], sz[:, hh:hh + 1, :], R_sz, 2 * pr + hh, tmps, 128.0, 6)

        if stop == "pC1":
            P.barrier(); return nc
        for h in range(4):
            new_phase("gdn")
            wfm = [ar.alloc([128, 16, 128], BF16, "wfm%d" % i) for i in range(2)]
            xbf, R_xbf = ar.alloc([128, S + 8], BF16, "xbf")
            diag, R_diag = ar.alloc([128, 4, 128], BF16, "diag")
            cs, R_cs = ar.alloc([128, S], F32, "cs")
            knT, R_kn = ar.alloc([128, S], BF16, "knT")
            qnT, R_qn = ar.alloc([128, S], BF16, "qnT")
            cvT, R_cv = ar.alloc([128, S], BF16, "cvT")
            kbT, R_kb = ar.alloc([128, S], BF16, "kbT")
            q_eT, R_qe = ar.alloc([128, S], BF16, "q_eT")
            vb_tok, R_vb = ar.alloc([128, NT, 128], BF16, "vb_tok")
            kbg_tok, R_kbg = ar.alloc([128, NT, 128], BF16, "kbg_tok")
            kt_tok, R_kt = ar.alloc([128, NT, 128], BF16, "kt_tok")
            gc, R_gc = ar.alloc([128, S], F32, "gc")
            beta, R_beta = ar.alloc([128, S], BF16, "beta")
            eg, R_eg = ar.alloc([128, S], F32, "eg")
            tl, R_tl = ar.alloc([128, S], F32, "tl")
            dabT, R_dab = tl[0:8, :], R_tl
            cols, R_cols = ar.alloc([128, 6, 16], F32, "cols")
            nA, R_nA = ar.alloc([128, 4], F32, "nA")
            Sp, R_Sp = ar.alloc([128, 128], F32, "Sp")
            Sbf, R_Sbf = ar.alloc([128, 128], BF16, "Sbf")
            dm = [ar.alloc([128, 128], F32, "dm%d" % i) for i in range(4)]
            mb = [ar.alloc([128, 128], BF16, "mb%d" % i) for i in range(10)]
            u_sb, R_u = ar.alloc([128, 128], F32, "u_sb")
            tmps = [ar.alloc([128, 2, 512], BF16, "sq"), ar.alloc([128, 512], F32, "t1"), ar.alloc([128, 512], F32, "rinv"),
                    ar.alloc([128, 512], F32, "u"), ar.alloc([128, 512], BF16, "ost")]
            e1, R_e1 = tmps[3]
            o_acc, R_oacc = cs.rearrange("p (a b) -> p a b", a=1), R_cs

            def c_dab(tb, ps, R):
                copy_on("act", dabT[0:8, tb * 512:(tb + 1) * 512], ps, [R], [R_dab])
            proj_fm(l, 21, 8, wfm, c_dab, 0)
            P.op("dve", lambda g: g.memset(xbf[:, 0:3], 0.0), writes=[R_xbf])
            for which in range(3):
                ti = which * 4 + h
                for j in range(4):
                    P.op("dve", lambda g: g.tensor_scalar(diag[:, j, :], ident_f, sc(64 + ti * 4 + j), None, ALU.mult),
                         reads=[R_cmf, R_small], writes=[R_diag])

                def c_x(tb, ps, R):
                    copy_on(alt(), xbf[:, 3 + tb * 512:3 + (tb + 1) * 512], ps, [R], [R_xbf])
                proj_fm(l, 9 + 4 * which + h, 128, wfm, c_x, 1)
                for blk in range(4):
                    sl = slice(blk * 512, (blk + 1) * 512)
                    bk = blk % 2
                    for j in range(4):
                        mm(pss[bk][:], diag[:, j, :], xbf[:, blk * 512 + j: blk * 512 + j + 512], j == 0, j == 3, [R_diag, R_xbf], R_ps[bk])
                    if which == 2:
                        P.op("act", lambda g: g.activation(out=cvT[:, sl], in_=pss[bk][:], func=AF.Silu), reads=[R_ps[bk]], writes=[R_cv])
                    else:
                        P.op("act", lambda g: g.activation(out=cs[:, sl], in_=pss[bk][:], func=AF.Silu), reads=[R_ps[bk]], writes=[R_cs])
                if which < 2:
                    for blk in range(4):
                        sl = slice(blk * 512, (blk + 1) * 512)
                        rinv, R_rinv = rstd_part([(cs[:, sl], R_cs)], 512, 1.0, tmps[0:3], 2 + blk % 2)
                        if which == 0:
                            P.op("dve", lambda g: g.scalar_tensor_tensor(out=qnT[:, sl], in0=cs[:, sl], scalar=128.0 ** -0.5, in1=rinv[:, 0:512],
                                                                         op0=ALU.mult, op1=ALU.mult), reads=[R_cs, R_rinv], writes=[R_qn])
                        else:
                            P.op("dve", lambda g: g.tensor_tensor(knT[:, sl], cs[:, sl], rinv[:, 0:512], ALU.mult), reads=[R_cs, R_rinv], writes=[R_kn])
            if stop == "g1":
                P.barrier(); return nc
            P.op("act", lambda g: g.activation(out=nA, in_=sc(56, 4), func=AF.Exp), reads=[R_small], writes=[R_nA])
            P.op("dve", lambda g: g.tensor_scalar(nA, nA, -1.0, None, ALU.mult), reads=[R_nA], writes=[R_nA])
            for blk in range(4):
                sl = slice(blk * 512, (blk + 1) * 512)
                bk = 4 + blk % 2
                mm(pss[bk][:], selm[0:8, h, :], dabT[0:8, sl], True, True, [R_selm, R_dab], R_ps[bk])
                P.op("act", lambda g: g.activation(out=e1, in_=pss[bk][:], func=AF.Exp, bias=sc(60 + h)), reads=[R_ps[bk], R_small], writes=[R_e1])
                P.op("act", lambda g: g.activation(out=gc[:, sl], in_=e1, func=AF.Ln, bias=one_ap), reads=[R_e1, R_misc], writes=[R_gc])
                (sqv, R_s0), (s1, R_s1), (s2, R_s2) = tmps[0], tmps[1], tmps[2]
                s0 = sqv.rearrange("p a b -> p (a b)").bitcast(F32)
                P.op("dve", lambda g: g.tensor_scalar(s1, e1, 2.0, None, ALU.add), reads=[R_e1], writes=[R_s1])
                P.op("dve", lambda g: g.reciprocal(s2, s1), reads=[R_s1], writes=[R_s2])
                P.op("dve", lambda g: g.tensor_tensor(s1, e1, s2, ALU.mult), reads=[R_e1, R_s2], writes=[R_s1])
                P.op("dve", lambda g: g.tensor_tensor(s2, s1, s1, ALU.mult), reads=[R_s1], writes=[R_s2])
                P.op("dve", lambda g: g.tensor_scalar(s0, s2, 1.0 / 7.0, 0.2, ALU.mult, ALU.add), reads=[R_s2], writes=[R_s0])
                P.op("dve", lambda g: g.tensor_tensor(s0, s0, s2, ALU.mult), reads=[R_s0, R_s2], writes=[R_s0])
                P.op("dve", lambda g: g.tensor_scalar(s0, s0, 1.0 / 3.0, None, ALU.add), reads=[R_s0], writes=[R_s0])
                P.op("dve", lambda g: g.tensor_tensor(s0, s0, s2, ALU.mult), reads=[R_s0, R_s2], writes=[R_s0])
                P.op("dve", lambda g: g.tensor_scalar(s0, s0, 1.0, None, ALU.add), reads=[R_s0], writes=[R_s0])
                P.op("dve", lambda g: g.tensor_tensor(s0, s0, s1, ALU.mult), reads=[R_s0, R_s1], writes=[R_s0])
                P.op("dve", lambda g: g.tensor_scalar(s2, e1, 0.5, None, ALU.is_lt), reads=[R_e1], writes=[R_s2])
                P.op("dve", lambda g: g.scalar_tensor_tensor(out=s0, in0=s0, scalar=2.0, in1=gc[:, sl], op0=ALU.mult, op1=ALU.subtract),
                     reads=[R_s0, R_gc], writes=[R_s0])
                P.op("dve", lambda g: g.tensor_tensor(s0, s0, s2, ALU.mult), reads=[R_s0, R_s2], writes=[R_s0])
                P.op("dve", lambda g: g.tensor_tensor(gc[:, sl], gc[:, sl], s0, ALU.add), reads=[R_s0, R_gc], writes=[R_gc])
                bk2 = 6 + blk % 2
                mm(pss[bk2][:], selm[0:8, 4 + h, :], dabT[0:8, sl], True, True, [R_selm, R_dab], R_ps[bk2])
                P.op("act", lambda g: g.activation(out=beta[:, sl], in_=pss[bk2][:], func=AF.Sigmoid), reads=[R_ps[bk2]], writes=[R_beta])
            P.op("dve", lambda g: g.tensor_tensor_scan(out=gc, data0=rmask[:], data1=gc, initial=0.0, op0=ALU.mult, op1=ALU.add),
                 reads=[R_rmask, R_gc], writes=[R_gc])
            P.op("dve", lambda g: g.tensor_scalar(gc, gc, nA[:, h:h + 1], None, ALU.mult), reads=[R_gc, R_nA], writes=[R_gc])
            gcl = cols[:, 0, :]
            cd = cols[:, 1, :]
            P.op("dve", lambda g: g.tensor_copy(gcl, gc[:, 127:S:128]), reads=[R_gc], writes=[R_cols])
            P.op("act", lambda g: g.activation(out=cd, in_=gcl, func=AF.Exp), reads=[R_cols], writes=[R_cols])
            P.op("act", lambda g: g.activation(out=eg, in_=gc, func=AF.Exp), reads=[R_gc], writes=[R_eg])
            P.op("dve", lambda g: g.tensor_tensor(q_eT, qnT, eg, ALU.mult), reads=[R_qn, R_eg], writes=[R_qe])
            P.op("dve", lambda g: g.tensor_tensor(kbT, knT, beta, ALU.mult), reads=[R_kn, R_beta], writes=[R_kb])
            P.op("pool", lambda g: g.tensor_tensor(eg, eg, beta, ALU.mult), reads=[R_eg, R_beta], writes=[R_eg])
            for n in range(NT):
                ch = slice(n * 128, (n + 1) * 128)
                P.op("act", lambda g: g.activation(out=tl[:, ch], in_=gc[:, ch], func=AF.Exp, scale=-1.0, bias=gcl[:, n:n + 1]),
                     reads=[R_gc, R_cols], writes=[R_tl])
            if stop == "g2":
                P.barrier(); return nc
            for qi, (src, R_src) in enumerate(((gc, R_gc), (beta, R_beta), (eg, R_eg), (tl, R_tl))):
                oh = cm_b[:, CI, 0:2] if src is beta else cm_f[:, CI, 0:2]
                for n in range(NT):
                    c0 = (qi * NT + n) * 2
                    mm(pss[3][:, c0:c0 + 2], src[:, n * 128:(n + 1) * 128], oh, True, True, [R_src, R_cmf, R_cmb], R_ps[3])
            P.op("dve", lambda g: g.tensor_copy(cols[:, 2:6, :], pss[3][:, 0:128:2].rearrange("p (a b) -> p a b", b=NT)),
                 reads=[R_ps[3]], writes=[R_cols])
            gc_col, beta_col, bexp_col, tail_col = (cols[:, i, :] for i in (2, 3, 4, 5))
            if stop == "g2b":
                P.barrier(); return nc
            for t in range(NT):
                bk = t % 2
                pT = pss[bk][:].bitcast(BF16)
                ts = slice(t * 128, (t + 1) * 128)
                P.op("pe", lambda g: g.transpose(pT[:, 0:128], knT[:, ts], ident_b), reads=[R_kn, R_cmb], writes=[R_ps[bk]])
                P.op("pe", lambda g: g.transpose(pT[:, 128:256], cvT[:, ts], ident_b), reads=[R_cv, R_cmb], writes=[R_ps[bk]], pe_accum=True)
                P.op("dve", lambda g: g.tensor_scalar(kbg_tok[:, t, :], pT[:, 0:128], bexp_col[:, t:t + 1], None, ALU.mult),
                     reads=[R_ps[bk], R_cols], writes=[R_kbg])
                P.op("dve", lambda g: g.tensor_scalar(kt_tok[:, t, :], pT[:, 0:128], tail_col[:, t:t + 1], None, ALU.mult),
                     reads=[R_ps[bk], R_cols], writes=[R_kt])
                P.op("dve", lambda g: g.tensor_scalar(vb_tok[:, t, :], pT[:, 128:256], beta_col[:, t:t + 1], None, ALU.mult),
                     reads=[R_ps[bk], R_cols], writes=[R_vb])

            if stop == "g3":
                P.barrier(); return nc
            P.barrier(reset=False)
            sz, R_sz = tl.bitcast(BF16)[:, 0:S].rearrange("p (a b) -> p a b", a=1), Reg("sz")
            mb2 = [(xbf[:, i * 128:(i + 1) * 128], Reg("mb2_%d" % i)) for i in range(16)]

            def c_z(tb, ps, R):
                P.op("act", lambda g: g.activation(out=sz[:, 0, tb * 512:(tb + 1) * 512], in_=ps, func=AF.Silu), reads=[R], writes=[R_sz])
            proj_fm(l, 22 + h, 128, wfm, c_z, 1)
            if stop == "g4":
                P.barrier(); return nc
            P.op("dve", lambda g: g.memset(Sp, 0.0), writes=[R_Sp])
            P.op("dve", lambda g: g.memset(Sbf, 0.0), writes=[R_Sbf])
            (dA, R_dA), (dB, R_dB), (dC, R_dC), (dD, R_dD) = dm
            for n in range(NT):
                ch = slice(n * 128, (n + 1) * 128)
                gcc = gc_col[:, n:n + 1]
                P.op("dve", lambda g: g.tensor_scalar(dA, gc[:, ch], gcc, 0.0, ALU.subtract, ALU.max), reads=[R_gc, R_cols], writes=[R_dA])
                P.op("act", lambda g: g.activation(out=dA, in_=dA, func=AF.Exp, scale=-1.0), reads=[R_dA], writes=[R_dA])
                P.op("dve", lambda g: g.tensor_scalar(dB, gc[:, ch], gcc, 0.0, ALU.subtract, ALU.min), reads=[R_gc, R_cols], writes=[R_dB])
                P.op("act", lambda g: g.activation(out=dB, in_=dB, func=AF.Exp), reads=[R_dB], writes=[R_dB])
                P.op("pool", lambda g: g.tensor_tensor(dC, dA, cm_f[:, CNSL, :], ALU.mult), reads=[R_dA, R_cmf], writes=[R_dC])
                P.op("pool", lambda g: g.tensor_tensor(dD, dB, cm_f[:, CNSU, :], ALU.mult), reads=[R_dB, R_cmf], writes=[R_dD])
                P.op("pool", lambda g: g.tensor_tensor(dB, dB, cm_f[:, CTU, :], ALU.mult), reads=[R_dB, R_cmf], writes=[R_dB])
                if stop == "c1":
                    P.barrier(); return nc
                (Pm, R_P), (PT, R_PT), (qkT, R_qk), (wT, R_wT), (vnew, R_vn) = mb[0:5]
                (Pd, R_Pd), (PTd, R_PTd), (Po32, R_Po32), (PTo32, R_PTo32), (Po64, R_Po64) = mb[5:10]
                b0, b1, b2 = 0, 1, 2
                mm(pss[b0][:, 0:128], kbT[:, ch], knT[:, ch], True, True, [R_kb, R_kn], R_ps[b0])
                mm(pss[b1][:, 0:128], knT[:, ch], kbT[:, ch], True, True, [R_kb, R_kn], R_ps[b1])
                mm(pss[b2][:, 0:128], knT[:, ch], qnT[:, ch], True, True, [R_kn, R_qn], R_ps[b2])
                P.op("dve", lambda g: g.tensor_tensor(Pm, pss[b0][:, 0:128], dC, ALU.mult), reads=[R_ps[b0], R_dC], writes=[R_P])
                P.op("dve", lambda g: g.tensor_tensor(PT, pss[b1][:, 0:128], dD, ALU.mult), reads=[R_ps[b1], R_dD], writes=[R_PT])
                P.op("dve", lambda g: g.tensor_tensor(qkT, pss[b2][:, 0:128], dB, ALU.mult), reads=[R_ps[b2], R_dB], writes=[R_qk])
                P.op("pool", lambda g: g.tensor_tensor(Pd, Pm, cm_b[:, CB32, :], ALU.mult), reads=[R_P, R_cmb], writes=[R_Pd])
                P.op("pool", lambda g: g.tensor_tensor(PTd, PT, cm_b[:, CB32, :], ALU.mult), reads=[R_PT, R_cmb], writes=[R_PTd])
                P.op("pool", lambda g: g.tensor_tensor(Po32, Pm, cm_b[:, CO32, :], ALU.mult), reads=[R_P, R_cmb], writes=[R_Po32])
                P.op("pool", lambda g: g.tensor_tensor(PTo32, PT, cm_b[:, CO32, :], ALU.mult), reads=[R_PT, R_cmb], writes=[R_PTo32])
                P.op("pool", lambda g: g.tensor_tensor(Po64, Pm, cm_b[:, CO64, :], ALU.mult), reads=[R_P, R_cmb], writes=[R_Po64])
                Acur, ATcur = mb2[0], mb2[1]
                Anxt, ATnxt = mb2[2], mb2[3]
                P.op("pool", lambda g: g.tensor_tensor(Acur[0], Pd, ident_b, ALU.add), reads=[R_Pd, R_cmb], writes=[Acur[1]])
                P.op("pool", lambda g: g.tensor_tensor(ATcur[0], PTd, ident_b, ALU.add), reads=[R_PTd, R_cmb], writes=[ATcur[1]])
                cur = ((Pd, R_Pd), (PTd, R_PTd))
                sqb = [(mb2[4], mb2[5]), (mb2[6], mb2[7])]
                for lev in range(4):
                    (cP, R_cP), (cPT, R_cPT) = cur
                    (nP, R_nP), (nPT, R_nPT) = sqb[lev % 2]
                    mm(pss[3][:, 0:128], cPT, cP, True, True, [R_cP, R_cPT], R_ps[3])
                    mm(pss[4][:, 0:128], cP, cPT, True, True, [R_cP, R_cPT], R_ps[4])
                    copy_on("act", nP, pss[3][:, 0:128], [R_ps[3]], [R_nP])
                    copy_on("dve", nPT, pss[4][:, 0:128], [R_ps[4]], [R_nPT])
                    mm(pss[5][:, 0:128], ident_b, Acur[0], True, False, [R_cmb, Acur[1]], R_ps[5])
                    mm(pss[5][:, 0:128], nPT, Acur[0], False, True, [R_nPT, Acur[1]], R_ps[5])
                    mm(pss[6][:, 0:128], ident_b, ATcur[0], True, False, [R_cmb, ATcur[1]], R_ps[6])
                    mm(pss[6][:, 0:128], nP, ATcur[0], False, True, [R_nP, ATcur[1]], R_ps[6])
                    copy_on("dve", Anxt[0], pss[5][:, 0:128], [R_ps[5]], [Anxt[1]])
                    copy_on("act", ATnxt[0], pss[6][:, 0:128], [R_ps[6]], [ATnxt[1]])
                    Acur, Anxt = Anxt, Acur
                    ATcur, ATnxt = ATnxt, ATcur
                    cur = ((nP, R_nP), (nPT, R_nPT))
                (U1, R_U1), (T1, R_T1) = mb2[8], mb2[9]
                mm(pss[3][:, 0:128], PTo32, Acur[0], True, True, [R_PTo32, Acur[1]], R_ps[3])
                mm(pss[4][:, 0:128], Po32, ATcur[0], True, True, [R_Po32, ATcur[1]], R_ps[4])
                copy_on("act", U1, pss[3][:, 0:128], [R_ps[3]], [R_U1])
                copy_on("dve", T1, pss[4][:, 0:128], [R_ps[4]], [R_T1])
                mm(pss[5][:, 0:128], ident_b, Acur[0], True, False, [R_cmb, Acur[1]], R_ps[5])
                mm(pss[5][:, 0:128], ATcur[0], U1, False, True, [ATcur[1], R_U1], R_ps[5])
                mm(pss[6][:, 0:128], ident_b, ATcur[0], True, False, [R_cmb, ATcur[1]], R_ps[6])
                mm(pss[6][:, 0:128], Acur[0], T1, False, True, [Acur[1], R_T1], R_ps[6])
                copy_on("dve", Anxt[0], pss[5][:, 0:128], [R_ps[5]], [Anxt[1]])
                copy_on("act", ATnxt[0], pss[6][:, 0:128], [R_ps[6]], [ATnxt[1]])
                Acur, Anxt = Anxt, Acur
                ATcur, ATnxt = ATnxt, ATcur
                mm(pss[4][:, 0:128], Po64, ATcur[0], True, True, [R_Po64, ATcur[1]], R_ps[4])
                copy_on("dve", T1, pss[4][:, 0:128], [R_ps[4]], [R_T1])
                mm(pss[6][:, 0:128], ident_b, ATcur[0], True, False, [R_cmb, ATcur[1]], R_ps[6])
                mm(pss[6][:, 0:128], Acur[0], T1, False, True, [Acur[1], R_T1], R_ps[6])
                copy_on("act", ATnxt[0], pss[6][:, 0:128], [R_ps[6]], [ATnxt[1]])
                ATc = ATnxt
                if stop == "c3":
                    P.barrier(); return nc
                AT, R_AT = ATc
                mm(pss[6][:, 0:128], AT, vb_tok[:, n, :], True, True, [R_AT, R_vb], R_ps[6])
                copy_on("act", u_sb, pss[6][:, 0:128], [R_ps[6]], [R_u])
                if stop == "c3a":
                    P.barrier(); return nc
                mm(pss[5][:, 0:128], kbg_tok[:, n, :], AT, True, True, [R_AT, R_kbg], R_ps[5])
                if stop == "c3b":
                    P.barrier(); return nc
                copy_on("dve", wT, pss[5][:, 0:128], [R_ps[5]], [R_wT])
                if stop == "c4":
                    P.barrier(); return nc
                if n > 0:
                    mm(pss[0][:, 0:128], wT, Sbf, True, True, [R_wT, R_Sbf], R_ps[0])
                    P.op("dve", lambda g: g.tensor_tensor(vnew, u_sb, pss[0][:, 0:128], ALU.subtract), reads=[R_u, R_ps[0]], writes=[R_vn])
                else:
                    copy_on("dve", vnew, u_sb, [R_u], [R_vn])
                if n > 0:
                    mm(pss[7][:, 0:128], Sbf, q_eT[:, ch], True, False, [R_Sbf, R_qe], R_ps[7])
                mm(pss[7][:, 0:128], vnew, qkT, n == 0, True, [R_vn, R_qk], R_ps[7])
                copy_on("act", o_acc[:, 0, ch], pss[7][:, 0:128], [R_ps[7]], [R_oacc])
                if n < NT - 1:
                    mm(pss[1][:, 0:128], kt_tok[:, n, :], vnew, True, True, [R_kt, R_vn], R_ps[1])
                    P.op("dve", lambda g: g.scalar_tensor_tensor(out=Sp, in0=Sp, scalar=cd[:, n:n + 1], in1=pss[1][:, 0:128],
                                                                 op0=ALU.mult, op1=ALU.add), reads=[R_Sp, R_cols, R_ps[1]], writes=[R_Sp])
                    copy_on("act", Sbf, Sp, [R_Sp], [R_Sbf])
            post_norm_store(l, o_acc, R_oacc, 1, [sc(51)], sz, R_sz, 4 + h, tmps, 128.0, 6)

        if stop == "pC2":
            P.barrier(); return nc
        for h in range(4):
            new_phase("diff")
            wfm = [ar.alloc([128, 16, 128], BF16, "wfm%d" % i) for i in range(3)]
            wtm = ar.alloc([128, 16, 256], BF16, "wtm")
            qT, R_q = ar.alloc([128, 2, S], BF16, "qT")
            kT, R_k = ar.alloc([128, 2, S], BF16, "kT")
            v_sb, R_v = ar.alloc([128, NT, 256], BF16, "v_sb")
            sz, R_sz = ar.alloc([128, 2, S], BF16, "sz")
            ebuf = [ar.alloc([128, 512], BF16, "e%d" % i) for i in range(3)]
            qn, R_qnb = ar.alloc([128, 512], BF16, "qn")
            ta, R_ta = ar.alloc([128, 512], F32, "ta")
            tb_, R_tb = ar.alloc([128, 512], F32, "tb")
            tO, R_tO = ar.alloc([128, 2, 512], F32, "tO")
            rs, R_rs = ar.alloc([128, 512], F32, "rs")
            lamt, R_lam = ar.alloc([128, 8], F32, "lam")
            lprod, R_lprod = ar.alloc([128, 256], F32, "lprod")
            wn2, R_wn2 = ar.alloc([128, 2], F32, "wn2")
            tmps = [ar.alloc([128, 2, 512], BF16, "sq"), ar.alloc([128, 512], F32, "t1"), ar.alloc([128, 512], F32, "rinv"),
                    ar.alloc([128, 512], F32, "u"), ar.alloc([128, 512], BF16, "ost")]
            P.op("dve", lambda g: g.tensor_tensor(lprod[:, 0:128], sc(112, 128), sc(240, 128), ALU.mult), reads=[R_small], writes=[R_lprod])
            P.op("dve", lambda g: g.tensor_tensor(lprod[:, 128:256], sc(368, 128), sc(496, 128), ALU.mult), reads=[R_small], writes=[R_lprod])
            P.op("dve", lambda g: g.tensor_reduce(lamt[:, 0:2], lprod.rearrange("p (a b) -> p a b", b=128), AX.X, ALU.add),
                 reads=[R_lprod], writes=[R_lam])
            P.op("act", lambda g: g.activation(out=lamt[:, 2:4], in_=lamt[:, 0:2], func=AF.Exp), reads=[R_lam], writes=[R_lam])
            P.op("dve", lambda g: g.scalar_tensor_tensor(out=lamt[:, 4:5], in0=lamt[:, 3:4], scalar=-lam_init, in1=lamt[:, 2:3],
                                                         op0=ALU.add, op1=ALU.subtract), reads=[R_lam], writes=[R_lam])
            nlam = lamt[:, 4:5]
            P.op("dve", lambda g: g.tensor_scalar(wn2, sc(54, 2), 1.0 - lam_init, None, ALU.mult), reads=[R_small], writes=[R_wn2])
            for which, (dstT, R_dst, g0, wcol) in enumerate(((qT, R_q, 26, 52), (kT, R_k, 34, 53))):
                for m in range(2):
                    def c_qk(tb, ps, R):
                        sl = slice(tb * 512, (tb + 1) * 512)
                        rinv, R_rinv = rstd_part([(ps, R)], 512, 128.0, tmps[0:3], 2 + tb % 2)
                        P.op("dve", lambda g: g.scalar_tensor_tensor(out=qn, in0=ps, scalar=sc(wcol), in1=rinv[:, 0:512], op0=ALU.mult, op1=ALU.mult),
                             reads=[R, R_rinv, R_small], writes=[R_qnb])
                        bkr = 2 + (tb + 1) % 2
                        mm(pss[bkr][:], cm_b[:, CP, :], qn, True, True, [R_cmb, R_qnb], R_ps[bkr])
                        P.op("pool", lambda g: g.tensor_tensor(ta, qn, cosT[:, sl], ALU.mult), reads=[R_qnb, R_cos], writes=[R_ta])
                        P.op("dve", lambda g: g.tensor_tensor(tb_, pss[bkr][:], sinT[:, sl], ALU.mult), reads=[R_ps[bkr], R_sin], writes=[R_tb])
                        P.op("pool", lambda g: g.tensor_tensor(dstT[:, m, sl], ta, tb_, ALU.add), reads=[R_ta, R_tb], writes=[R_dst])
                    proj_fm(l, g0 + 2 * h + m, 128, wfm, c_qk, 1)

            def c_v(t, ps, R):
                copy_on(alt(), v_sb[:, t, :], ps, [R], [R_v])
            proj_tm(l, 2 + h, wtm, c_v, [0, 1])
            for j in range(2):
                def c_z(tb, ps, R):
                    P.op("act", lambda g: g.activation(out=sz[:, j, tb * 512:(tb + 1) * 512], in_=ps, func=AF.Silu), reads=[R], writes=[R_sz])
                proj_fm(l, 42 + 2 * h + j, 128, wfm, c_z, j)
            scale = 128.0 ** -0.5
            ei = 0
            for qb in range(4):
                qs = slice(qb * 512, (qb + 1) * 512)
                nk = 4 * (qb + 1)
                for m in range(2):
                    bo0, bo1, bs = (2, 3, 4) if m == 0 else (5, 6, 7)
                    for kc in range(nk):
                        ks = slice(kc * 128, (kc + 1) * 128)
                        bsc = kc % 2
                        c = kc - 4 * qb
                        col0 = max(c, 0) * 128
                        mm(pss[bsc][:, col0:512], kT[:, m, ks], qT[:, m, qb * 512 + col0:(qb + 1) * 512], True, True, [R_k, R_q], R_ps[bsc])
                        e, R_e = ebuf[ei % 3]
                        ei += 1
                        P.op("act", lambda g: g.activation(out=e[:, col0:512], in_=pss[bsc][:, col0:512], func=AF.Exp, scale=scale),
                             reads=[R_ps[bsc]], writes=[R_e])
                        if c >= 0:
                            P.op("pool", lambda g: g.tensor_tensor(e[:, col0:col0 + 128], e[:, col0:col0 + 128], cm_b[:, CTU, :], ALU.mult),
                                 reads=[R_e, R_cmb], writes=[R_e])
                        first, last = kc == 0, kc == nk - 1
                        mm(pss[bo0][:, col0:512], v_sb[:, kc, 0:128], e[:, col0:512], first, last, [R_v, R_e], R_ps[bo0])
                        mm(pss[bo1][:, col0:512], v_sb[:, kc, 128:256], e[:, col0:512], first, last, [R_v, R_e], R_ps[bo1])
                        mm(pss[bs][:, col0:512], ones_b, e[:, col0:512], first, last, [R_cmb, R_e], R_ps[bs])
                    P.op("dve", lambda g: g.reciprocal(rs, pss[bs][:]), reads=[R_ps[bs]], writes=[R_rs])
                    for j, bo in enumerate((bo0, bo1)):
                        if m == 0:
                            P.op("dve", lambda g: g.tensor_tensor(tO[:, j, :], pss[bo][:], rs, ALU.mult), reads=[R_ps[bo], R_rs], writes=[R_tO])
                        else:
                            P.op("dve", lambda g: g.scalar_tensor_tensor(out=ta, in0=pss[bo][:], scalar=nlam, in1=rs, op0=ALU.mult, op1=ALU.mult),
                                 reads=[R_ps[bo], R_rs, R_lam], writes=[R_ta])
                            P.op("pool", lambda g: g.tensor_tensor(tO[:, j, :], tO[:, j, :], ta, ALU.add), reads=[R_tO, R_ta], writes=[R_tO])
                (sq, R_sq), (t1, R_t1), (rinv, R_rinv), (u, R_u), (ost, R_ost) = tmps
                rstd_part([(tO[:, j, :], R_tO) for j in range(2)], 512, 256.0, tmps[0:3], 0)
                for j in range(2):
                    P.op("dve", lambda g: g.scalar_tensor_tensor(out=u, in0=tO[:, j, :], scalar=wn2[:, j:j + 1], in1=rinv, op0=ALU.mult, op1=ALU.mult),
                         reads=[R_tO, R_rinv, R_wn2], writes=[R_u])
                    P.op("pool", lambda g: g.tensor_tensor(ost, u, sz[:, j, qs], ALU.mult), reads=[R_u, R_sz], writes=[R_ost])
                    P.dma("sp", oT_d[8 + 2 * h + j, :, qs], ost, reads=[R_ost], writes=[R_oT], nowaw=True)

        if stop == "pC3":
            P.barrier(); return nc
        new_phase("D")
        wo = big
        for fc in range(16):
            P.dma("pool", wo[:, fc, :], wout_d[l][:, fc * D:(fc + 1) * D], writes=[R_big], nowaw=fc > 0)
        xts = [ar.alloc([128, D], F32, "xt%d" % i) for i in range(2)]
        obs = [ar.alloc([128, 16, 512], BF16, "ob%d" % i) for i in range(2)]
        xos = [ar.alloc([128, D], F32, "xo%d" % i) for i in range(2)]
        ytmp, R_yt = ar.alloc([128, 512], F32, "ytmp")
        xsrc = x_d if l == 0 else x1_d
        xdst = out_d if l == L - 1 else x1_d
        R_dst = R_out if l == L - 1 else R_x1
        for tb in range(4):
            ob, R_ob = obs[tb % 2]
            P.dma("sp", ob, oT_d[:, :, tb * 512:(tb + 1) * 512].rearrange("c p t -> p c t"), reads=[R_oT], writes=[R_ob])
            for i in range(4):
                t = tb * 4 + i
                xt, R_xt = xts[t % 2]
                xo, R_xo = xos[t % 2]
                P.dma("sp", xt, xsrc[t * 128:(t + 1) * 128, :], reads=[R_x1] if l > 0 else [], writes=[R_xt])
                for ng in range(4):
                    ns = slice(ng * 512, (ng + 1) * 512)
                    bk = (t * 4 + ng) % 4
                    for fc in range(16):
                        mm(pss[bk][:], ob[:, fc, i * 128:(i + 1) * 128], wo[:, fc, ns], fc == 0, fc == 15, [R_ob, R_big], R_ps[bk])
                    P.op("dve", lambda g: g.tensor_tensor(ytmp, pss[bk][:], gate_bc[:, ns], ALU.mult), reads=[R_ps[bk], R_gate], writes=[R_yt])
                    P.op("pool", lambda g: g.tensor_tensor(xo[:, ns], ytmp, xt[:, ns], ALU.add), reads=[R_yt, R_xt], writes=[R_xo])
                P.dma("sp", xdst[t * 128:(t + 1) * 128, :], xo, reads=[R_xo], writes=[R_dst], nowaw=True)

    P.barrier()
    if scopes and state.get("scope") is not None:
        nc.leave_named_scope(state["scope"][0], state["scope"][1], False)
    return nc


def _col(v):
    return np.ascontiguousarray(np.asarray(v, np.float32).reshape(-1, 128).T)


def _consts():
    p = np.arange(128)[:, None]
    j = np.arange(128)[None, :]
    cm = np.zeros((128, NCM, 128), np.float32)
    cm[:, CI] = (p == j)
    cm[:, CO] = 1.0
    prot = np.zeros((128, 128), np.float32)
    prot[(j[0, :64] + 64), j[0, :64]] = -1.0
    prot[(j[0, 64:] - 64), j[0, 64:]] = 1.0
    cm[:, CP] = prot
    cm[:, CTU] = (p <= j)
    cm[:, CNSU] = -1.0 * (p < j)
    cm[:, CNSL] = -1.0 * (p > j)
    bd32 = (p // 32 == j // 32)
    bd64 = (p // 64 == j // 64)
    cm[:, CB32] = bd32
    cm[:, CO32] = bd64 & ~bd32
    cm[:, CO64] = ~bd64
    selm = np.zeros((8, 8, 128), np.float32)
    for k in range(8):
        selm[k, k, :] = 1.0
    rmask = np.ones((128, S), np.float32)
    rmask[:, 0::128] = 0.0
    half = 64
    inv_freq = (10000.0 ** (-(np.arange(half, dtype=np.float32) / np.float32(half)))).astype(np.float32)
    invf = np.concatenate([inv_freq, inv_freq]).astype(np.float64) / (2.0 * math.pi)
    return cm.reshape(128, NCM * 128), selm.reshape(8, 8 * 128), rmask, invf.astype(np.float32)


FM_GROUPS = ([(0, 128), (128, 128), (256, 128), (384, 128), (1024, 16)] + [(1040 + 128 * i, 128) for i in range(4)]
             + [(1552 + 128 * i, 128) for i in range(12)] + [(3088, 8)] + [(3096 + 128 * i, 128) for i in range(4)]
             + [(3608 + 128 * i, 128) for i in range(8)] + [(4632 + 128 * i, 128) for i in range(8)]
             + [(6680 + 128 * i, 128) for i in range(8)])
TM_GROUPS = [(512, 256), (768, 256)] + [(5656 + 256 * i, 256) for i in range(4)]


def _prep_shared(inp):
    f = lambda k: np.asarray(inp[k], np.float32)
    cm, selm, rmask, invf = _consts()
    w_in = f("w_in")
    winfm = np.zeros((2, 50, 128, 16, 128), np.float32)
    wintm = np.zeros((2, 6, 128, 16, 256), np.float32)
    for l in range(2):
        wl = w_in[l].reshape(16, 128, -1)
        for gi, (c0, n) in enumerate(FM_GROUPS):
            winfm[l, gi, :, :, :n] = wl[:, :, c0:c0 + n].transpose(1, 0, 2)
        for gi, (c0, n) in enumerate(TM_GROUPS):
            wintm[l, gi] = wl[:, :, c0:c0 + n].transpose(1, 0, 2)
    wada = np.ascontiguousarray(f("w_ada").reshape(2, 16, 128, 48, 128).transpose(0, 3, 2, 1, 4)).reshape(2, 48, 128, 16 * 128)
    wout = np.ascontiguousarray(f("w_out").reshape(2, 16, 128, D).transpose(0, 2, 1, 3)).reshape(2, 128, 16 * D)
    bgate = np.ascontiguousarray(f("b_ada")[:, 2 * D:].reshape(2, 1, D))
    sm = np.zeros((128, NS), np.float32)
    sm[:, 16] = invf
    for l in range(2):
        b = SBASE + l * SLW
        sm[:, b:b + 16] = _col(f("norm_w")[l])
        sm[:, b + 16:b + 48] = _col(f("b_ada")[l, :2 * D])
        sm[:, b + 48:b + 50] = _col(f("gla_b_lr")[l])
        sm[:, b + 50] = f("gla_norm_w")[l]
        sm[:, b + 51] = f("gdn_norm_w")[l]
        sm[:, b + 52] = f("diff_q_norm_w")[l]
        sm[:, b + 53] = f("diff_k_norm_w")[l]
        sm[:, b + 54:b + 56] = _col(f("diff_norm_w")[l])
        sm[:, b + 56:b + 60] = f("gdn_a_log")[l][None, :]
        sm[:, b + 60:b + 64] = f("gdn_dt_bias")[l][None, :]
        cw = f("gdn_conv_w")[l]
        sm[:, b + 64:b + 112] = cw.reshape(4, 12, 128).transpose(2, 1, 0).reshape(128, 48)
        sm[:, b + 112:b + 624] = f("diff_lambda")[l].reshape(1, 512)
        sm[0:16, b + 624:b + 880] = f("gla_w_lr")[l]
    shared = {"cmat": cm, "selm": selm, "rmask": rmask, "wada": wada, "bgate": bgate,
              "winfm": winfm.reshape(2, 50, 128, 16 * 128), "wintm": wintm.reshape(2, 6, 128, 16 * 256), "wout": wout}
    return shared, sm


def make_in_maps(inp, cores):
    shared, sm = _prep_shared(inp)
    x = np.asarray(inp["x"], np.float32)
    c = np.asarray(inp["c"], np.float32)
    pos = np.asarray(inp["positions"], np.int32)
    maps = []
    for b in cores:
        s = sm.copy()
        s[:, 0:16] = _col(c[b])
        m = dict(shared)
        m["x"] = np.ascontiguousarray(x[b])
        m["small"] = s
        m["pos"] = np.ascontiguousarray(pos[b:b + 1])
        maps.append(m)
    return maps


_NC_CACHE = {}


def kernel(**inputs):
    if "nc" not in _NC_CACHE:
        _NC_CACHE["nc"] = build(2)
    nc = _NC_CACHE["nc"]
    cores = [i // 2 for i in range(8)]
    maps = make_in_maps(inputs, cores)
    res = run_bass_kernel_spmd(nc, maps, core_ids=list(range(8)))
    out = np.stack([np.asarray(res.results[2 * b]["out"], np.float32) for b in range(4)], axis=0)
    return out
```

```python
import math
import numpy as np
import concourse.bass as bass
import concourse.mybir as mybir
from concourse.bass_utils import run_bass_kernel_spmd

F32 = mybir.dt.float32
BF16 = mybir.dt.bfloat16
I32 = mybir.dt.int32
AF = mybir.ActivationFunctionType
ALU = mybir.AluOpType
AX = mybir.AxisListType

S = 2048
D = 2048
NT = 16
EPS = 1e-6
SLW = 880
SBASE = 32
NS = SBASE + 2 * SLW
CI, CO, CP, CTU, CNSU, CNSL, CB32, CO32, CO64 = range(9)
NCM = 9


class Reg:
    __slots__ = ("name", "lw", "rd", "dsem", "local", "psum")

    def __init__(self, name, local=False, psum=False):
        self.name = name
        self.psum = psum
        self.lw = None
        self.rd = {}
        self.dsem = None
        self.local = local


class Prog:
    def __init__(self, nc):
        self.nc = nc
        self.eng = {"pe": nc.tensor, "act": nc.scalar, "dve": nc.vector, "pool": nc.gpsimd, "sp": nc.sync}
        self.sem, self.cnt, self.semobj = {}, {}, {}
        self.seen = {e: {} for e in self.eng}
        for e in self.eng:
            s = nc.alloc_semaphore(name="s_" + e)
            self.sem[e] = s
            self.semobj[e] = s
            self.cnt[e] = 0
        self.vc = {}
        self.dcnt = {}
        self.local_keys = []
        self.local_next = 0
        self.ninstr = 0
        self.nwait = 0

    def _wait(self, e, key, val):
        if self.seen[e].get(key, 0) >= val:
            return
        self.eng[e].wait_ge(self.semobj[key], val)
        self.nwait += 1
        se = self.seen[e]
        se[key] = val
        clk = self.vc.get((key, val))
        if clk:
            for k, v in clk.items():
                if se.get(k, 0) < v:
                    se[k] = v

    def _deps(self, e, reads, writes, pe_accum=False, nowaw=False):
        need = {}

        def add(k, v):
            if need.get(k, 0) < v:
                need[k] = v
        for r in reads:
            if r.lw is not None:
                add(*r.lw)
            if r.psum:
                for k, v in r.rd.items():
                    if k != e:
                        add(k, v)
        for w in writes:
            if w.lw is not None and not nowaw and not (pe_accum and w.lw[0] == "pe"):
                add(*w.lw)
            for k, v in w.rd.items():
                add(k, v)
        for k, v in sorted(need.items(), key=lambda kv: -kv[1] if isinstance(kv[0], str) else 0):
            self._wait(e, k, v)

    def _record(self, ev, reads, writes, nowaw=False):
        for r in reads:
            if r.rd.get(ev[0], 0) < ev[1]:
                r.rd[ev[0]] = ev[1]
        for w in writes:
            w.lw = ev
            if not nowaw:
                w.rd = {}

    def op(self, e, fn, reads=(), writes=(), pe_accum=False):
        self._deps(e, reads, writes, pe_accum)
        ins = fn(self.eng[e])
        self.cnt[e] += 1
        ins.then_inc(self.sem[e], 1)
        ev = (e, self.cnt[e])
        self.vc[ev] = dict(self.seen[e])
        self._record(ev, reads, writes)
        self.ninstr += 1

    def _newkey(self):
        key = ("d", len(self.dcnt))
        self.semobj[key] = self.nc.alloc_semaphore(name="d_%d" % len(self.dcnt))
        self.dcnt[key] = 0
        return key

    def _dkey(self, w):
        if w.dsem is None:
            if w.local:
                if self.local_next == len(self.local_keys):
                    self.local_keys.append(self._newkey())
                w.dsem = self.local_keys[self.local_next]
                self.local_next += 1
            else:
                w.dsem = self._newkey()
        return w.dsem

    def dma(self, q, out_ap, in_ap, reads=(), writes=(), nowaw=False, **kw):
        w = writes[0]
        self._deps(q, reads, writes, nowaw=nowaw)
        key = self._dkey(w)
        ins = self.eng[q].dma_start(out=out_ap, in_=in_ap, **kw)
        self.dcnt[key] += 16
        ins.then_inc(self.semobj[key], 16)
        ev = (key, self.dcnt[key])
        self.vc[ev] = dict(self.seen[q])
        self._record(ev, reads, writes, nowaw=nowaw)
        self.ninstr += 1

    def barrier(self, reset=True):
        evs = [(e, self.cnt[e]) for e in self.eng if self.cnt[e] > 0]
        evs += [(k, v) for k, v in self.dcnt.items() if v > 0]
        for e in self.eng:
            for k, v in evs:
                if k != e:
                    self._wait(e, k, v)
        if reset:
            self.local_next = 0


class Arena:
    def __init__(self, nc, nwords):
        self.t = nc.alloc_sbuf_tensor("arena", [128, nwords], F32)
        self.n = nwords
        self.off = 0
        self.k = 0

    def reset(self):
        self.off = 0

    def alloc(self, shape, dt, name=None):
        free = int(np.prod(shape[1:]))
        words = free if dt in (F32, I32) else (free + 1) // 2
        words = (words + 7) // 8 * 8
        assert self.off + words <= self.n, ("arena overflow", name, self.off, words, self.n)
        v = self.t[:, self.off:self.off + words]
        self.off += words
        if dt == BF16:
            v = v.bitcast(BF16)[:, 0:free]
        elif dt == I32:
            v = v.bitcast(I32)[:, 0:free]
        else:
            v = v[:, 0:free]
        if len(shape) == 3:
            v = v.rearrange("p (a b) -> p a b", b=shape[2])
        self.k += 1
        if shape[0] < 128:
            v = v[0:shape[0]]
        return v, Reg(name or "a%d" % self.k, local=True)


def build(nlayers=2, debug=False, stop=None, scopes=False):
    nc = bass.Bass("TRN2", target_bir_lowering=False)
    P = Prog(nc)
    L = nlayers

    def din(name, shape, dt=F32):
        return nc.dram_tensor(name, shape, dt, kind="ExternalInput").ap()

    x_d = din("x", [S, D])
    small_d = din("small", [128, NS])
    cmat_d = din("cmat", [128, NCM * 128])
    selm_d = din("selm", [8, 8 * 128])
    rmask_d = din("rmask", [128, S])
    pos_d = din("pos", [1, S], I32)
    wada_d = din("wada", [2, 48, 128, 16 * 128])
    bgate_d = din("bgate", [2, 1, D])
    winfm_d = din("winfm", [2, 50, 128, 16 * 128])
    wintm_d = din("wintm", [2, 6, 128, 16 * 256])
    wout_d = din("wout", [2, 128, 16 * D])
    out_d = nc.dram_tensor("out", [S, D], F32, kind="ExternalOutput").ap()
    x1_d = nc.dram_tensor("x1s", [S, D], F32, kind="Internal").ap()
    oT_d = nc.dram_tensor("oTs", [16, 128, S], BF16, kind="ExternalOutput" if debug else "Internal").ap()
    R_out, R_x1, R_oT = Reg("out"), Reg("x1"), Reg("oT")

    def sb(name, shape, dt):
        return nc.alloc_sbuf_tensor("sb_" + name, shape, dt), Reg(name)

    big, R_big = sb("big", [128, 16, S], BF16)
    cosT, R_cos = sb("cosT", [128, S], BF16)
    sinT, R_sin = sb("sinT", [128, S], BF16)
    gate_bc, R_gate = sb("gate_bc", [128, D], F32)
    rmask, R_rmask = sb("rmask", [128, S], BF16)
    cm_f, R_cmf = sb("cm_f", [128, NCM, 128], F32)
    cm_b, R_cmb = sb("cm_b", [128, NCM, 128], BF16)
    selm, R_selm = sb("selm", [8, 8, 128], F32)
    small, R_small = sb("small", [128, NS], F32)
    modc, R_modc = sb("modc", [128, 48], F32)
    misc, R_misc = sb("misc", [128, 64], F32)
    ar = Arena(nc, (nc.sbuf_bytes_remaining - 2048) // 4)

    pss = [nc.alloc_psum_tensor("ps%d" % i, [128, 512], F32) for i in range(8)]
    R_ps = [Reg("ps%d" % i, psum=True) for i in range(8)]

    state = {"alt": 0}

    def alt():
        state["alt"] ^= 1
        return "act" if state["alt"] else "dve"

    def copy_on(e, out, in_, reads, writes):
        if e == "act":
            P.op("act", lambda g: g.activation(out=out, in_=in_, func=AF.Copy), reads=reads, writes=writes)
        else:
            P.op(e, lambda g: g.tensor_copy(out, in_), reads=reads, writes=writes)

    def mm(out, lhsT, rhs, start, stop, reads, w):
        P.op("pe", lambda g: g.matmul(out, lhsT=lhsT, rhs=rhs, start=start, stop=stop),
             reads=reads, writes=[w], pe_accum=not start)

    def new_phase(name="ph"):
        P.barrier()
        ar.reset()
        if scopes:
            if state.get("scope") is not None:
                nc.leave_named_scope(state["scope"][0], state["scope"][1], False)
            nm = "%s_%d" % (name, state.setdefault("nscope", 0))
            state["nscope"] += 1
            sid, _ = nc.enter_named_scope(nm, False)
            state["scope"] = (nm, sid)

    ident_b = cm_b[:, CI, :]
    ones_b = cm_b[:, CO, :]
    ident_f = cm_f[:, CI, :]

    def rstd_part(srcs, n, denom, tmps, psi):
        (sq, R_sq), (t1, R_t1), (rinv, R_rinv) = tmps
        for i, (sap, sreg) in enumerate(srcs):
            P.op("act", lambda g: g.activation(out=sq[:, i, 0:n], in_=sap, func=AF.Square), reads=[sreg], writes=[R_sq])
        for i in range(len(srcs)):
            mm(pss[psi][:, 0:n], ones_b, sq[:, i, 0:n], i == 0, i == len(srcs) - 1, [R_sq, R_cmb], R_ps[psi])
        P.op("act", lambda g: g.activation(out=t1[:, 0:n], in_=pss[psi][:, 0:n], func=AF.Sqrt, scale=1.0 / denom, bias=eps_ap),
             reads=[R_ps[psi], R_misc], writes=[R_t1])
        P.op("dve", lambda g: g.reciprocal(rinv[:, 0:n], t1[:, 0:n]), reads=[R_t1], writes=[R_rinv])
        return rinv, R_rinv

    P.dma("sp", small[:], small_d, writes=[R_small])
    P.dma("sp", cm_f[:], cmat_d.rearrange("p (a b) -> p a b", b=128), writes=[R_cmf])
    P.dma("sp", selm[:], selm_d.rearrange("p (a b) -> p a b", b=128), writes=[R_selm])
    P.dma("pool", rmask[:], rmask_d, writes=[R_rmask])
    P.op("dve", lambda g: g.tensor_copy(cm_b[:], cm_f[:]), reads=[R_cmf], writes=[R_cmb])
    eps_ap = misc[:, 0:1]
    P.op("dve", lambda g: g.memset(misc[:], 0.0), writes=[R_misc])
    P.op("dve", lambda g: g.memset(misc[:, 0:1], EPS), reads=[], writes=[R_misc])
    P.op("dve", lambda g: g.memset(misc[:, 1:2], 1.0), reads=[], writes=[R_misc])
    one_ap = misc[:, 1:2]

    posi, R_posi = ar.alloc([128, S], I32, "posi")
    y, R_y = ar.alloc([128, S], F32, "y")
    yi, R_yi = ar.alloc([128, S], I32, "yi")
    yf, R_yf = ar.alloc([128, S], F32, "yf")
    fr, R_fr = ar.alloc([128, S], F32, "fr")
    m1, R_m1 = ar.alloc([128, S], F32, "m1")
    P.dma("sp", posi, pos_d.to_broadcast([128, S]), writes=[R_posi])
    P.op("dve", lambda g: g.tensor_copy(y, posi), reads=[R_posi], writes=[R_y])
    P.op("dve", lambda g: g.tensor_scalar(y, y, small[:, 16:17], None, ALU.mult), reads=[R_y, R_small], writes=[R_y])
    P.op("dve", lambda g: g.tensor_copy(yi, y), reads=[R_y], writes=[R_yi])
    P.op("dve", lambda g: g.tensor_copy(yf, yi), reads=[R_yi], writes=[R_yf])
    P.op("dve", lambda g: g.tensor_tensor(fr, y, yf, ALU.subtract), reads=[R_y, R_yf], writes=[R_fr])
    for which, dst, R_dst in ((0, sinT, R_sin), (1, cosT, R_cos)):
        src = fr
        if which == 1:
            P.op("dve", lambda g: g.tensor_scalar(y, fr, 0.25, None, ALU.add), reads=[R_fr], writes=[R_y])
            src = y
        R_src = R_fr if which == 0 else R_y
        P.op("dve", lambda g: g.tensor_scalar(m1, src, 0.5, None, ALU.is_gt), reads=[R_src], writes=[R_m1])
        P.op("dve", lambda g: g.tensor_tensor(yf, src, m1, ALU.subtract), reads=[R_src, R_m1], writes=[R_yf])
        P.op("dve", lambda g: g.tensor_scalar(m1, yf, -0.5, None, ALU.is_lt), reads=[R_yf], writes=[R_m1])
        P.op("dve", lambda g: g.tensor_tensor(yf, yf, m1, ALU.add), reads=[R_yf, R_m1], writes=[R_yf])
        P.op("act", lambda g: g.activation(out=dst[:], in_=yf, func=AF.Sin, scale=2.0 * math.pi), reads=[R_yf], writes=[R_dst])

    if stop == "p0":
        P.barrier(); return nc
    def proj_fm(l, gi, M, wbufs, consume, bankset):
        wb, R_wb = wbufs[state.setdefault("wfm_i", 0) % len(wbufs)]
        state["wfm_i"] += 1
        P.dma("pool", wb, winfm_d[l, gi].rearrange("p (a b) -> p a b", b=128), writes=[R_wb])
        banks = [bankset * 4 + i for i in range(4)]
        for kc in range(16):
            for tb in range(4):
                mm(pss[banks[tb]][0:M, :], wb[:, kc, 0:M], big[:, kc, tb * 512:(tb + 1) * 512], kc == 0, kc == 15,
                   [R_wb, R_big], R_ps[banks[tb]])
        for tb in range(4):
            consume(tb, pss[banks[tb]][0:M, :], R_ps[banks[tb]])

    def proj_tm(l, gi, wtb, consume, banks):
        wb, R_wb = wtb
        P.dma("pool", wb, wintm_d[l, gi].rearrange("p (a b) -> p a b", b=256), writes=[R_wb])
        for t in range(NT):
            bk = banks[t % len(banks)]
            for kc in range(16):
                mm(pss[bk][:, 0:256], big[:, kc, t * 128:(t + 1) * 128], wb[:, kc, :], kc == 0, kc == 15,
                   [R_wb, R_big], R_ps[bk])
            consume(t, pss[bk][:, 0:256], R_ps[bk])

    def post_norm_store(l, o_acc, R_oacc, nsub, wcols, szs, R_sz, fc0, tmps, denom, psi):
        (sq, R_sq), (t1, R_t1), (rinv, R_rinv), (u, R_u) = tmps[0:4]
        for blk in range(4):
            sl = slice(blk * 512, (blk + 1) * 512)
            rstd_part([(o_acc[:, j, sl], R_oacc) for j in range(nsub)], 512, denom, tmps[0:3], psi)
            for j in range(nsub):
                P.op("dve", lambda g: g.scalar_tensor_tensor(out=u[:, 0:512], in0=o_acc[:, j, sl], scalar=wcols[j], in1=rinv[:, 0:512],
                                                             op0=ALU.mult, op1=ALU.mult),
                     reads=[R_oacc, R_rinv, R_small, R_misc], writes=[R_u])
                ost, R_ost = tmps[4 + state.setdefault("ost_i", 0) % 2]
                state["ost_i"] += 1
                P.op("pool", lambda g: g.tensor_tensor(ost[:, 0:512], u[:, 0:512], szs[:, j, sl], ALU.mult),
                     reads=[R_u, R_sz], writes=[R_ost])
                P.dma("sp", oT_d[fc0 + j, :, sl], ost[:, 0:512], reads=[R_ost], writes=[R_oT], nowaw=True)

    for l in range(L):
        sb_l = SBASE + l * SLW
        lam_init = 0.8 - 0.6 * math.exp(-0.3 * l)

        def sc(off, n=1):
            return small[:, sb_l + off: sb_l + off + n]

        new_phase("A")
        cact, R_cact = ar.alloc([128, 16], F32, "cact")
        c2, R_c2 = ar.alloc([128, 16, 2], F32, "c2")
        crep, R_crep = ar.alloc([128, 16, 128], F32, "crep")
        bg, R_bg = ar.alloc([128, D], F32, "bg")
        wab = [ar.alloc([128, 16, 128], F32, "wa%d" % i) for i in range(3)]
        P.op("act", lambda g: g.activation(out=cact, in_=small[:, 0:16], func=AF.Silu), reads=[R_small], writes=[R_cact])
        P.op("dve", lambda g: g.tensor_copy(c2, cact.unsqueeze(2).to_broadcast([128, 16, 2])), reads=[R_cact], writes=[R_c2])
        P.op("dve", lambda g: g.tensor_copy(crep, cact.unsqueeze(2).to_broadcast([128, 16, 128])), reads=[R_cact], writes=[R_crep])
        P.dma("sp", bg, bgate_d[l].to_broadcast([128, D]), writes=[R_bg])
        for g_ in range(48):
            wa, R_wa = wab[g_ % 3]
            P.dma("sp", wa, wada_d[l, g_].rearrange("p (a b) -> p a b", b=128), writes=[R_wa])
            if g_ < 32:
                for kc in range(16):
                    mm(pss[0][:, 2 * g_:2 * g_ + 2], wa[:, kc, :], c2[:, kc, :], kc == 0, kc == 15, [R_wa, R_c2], R_ps[0])
                if g_ == 31:
                    P.op("dve", lambda g: g.tensor_tensor(modc[:, 0:32], pss[0][:, 0:64:2], sc(16, 32), ALU.add),
                         reads=[R_ps[0], R_small], writes=[R_modc])
                    P.op("dve", lambda g: g.scalar_tensor_tensor(out=modc[:, 32:48], in0=modc[:, 16:32], scalar=1.0, in1=sc(0, 16),
                                                                 op0=ALU.add, op1=ALU.mult),
                         reads=[R_modc, R_small], writes=[R_modc])
            else:
                gg = g_ - 32
                bk = 1 + (gg // 4) % 2
                c0 = (gg % 4) * 128
                for kc in range(16):
                    mm(pss[bk][:, c0:c0 + 128], crep[:, kc, :], wa[:, kc, :], kc == 0, kc == 15, [R_wa, R_crep], R_ps[bk])
                if gg % 4 == 3:
                    sl = slice((gg // 4) * 512, (gg // 4 + 1) * 512)
                    P.op("dve", lambda g: g.tensor_tensor(gate_bc[:, sl], pss[bk][:], bg[:, sl], ALU.add),
                         reads=[R_ps[bk], R_bg], writes=[R_gate])

        if stop == "pA":
            P.barrier(); return nc
        new_phase("B")
        xsrc = x_d if l == 0 else x1_d
        xts = [ar.alloc([128, D], F32, "xt%d" % i) for i in range(2)]
        xn, R_xn = ar.alloc([128, 4, D], BF16, "xn")
        junk, R_junk = ar.alloc([128, D], BF16, "junk")
        ssq, R_ssq = ar.alloc([128, 8], F32, "ssq")
        hT = big
        for tb in range(4):
            for i in range(4):
                t = tb * 4 + i
                xt, R_xt = xts[t % 2]
                P.dma("sp", xt, xsrc[t * 128:(t + 1) * 128, :], reads=[R_x1] if l > 0 else [], writes=[R_xt])
                P.op("act", lambda g: g.activation(out=junk, in_=xt, func=AF.Square, accum_out=ssq[:, 0:1]),
                     reads=[R_xt], writes=[R_junk, R_ssq])
                P.op("act", lambda g: g.activation(out=ssq[:, 1:2], in_=ssq[:, 0:1], func=AF.Sqrt, scale=1.0 / D, bias=eps_ap),
                     reads=[R_ssq, R_misc], writes=[R_ssq])
                P.op("dve", lambda g: g.reciprocal(ssq[:, 2:3], ssq[:, 1:2]), reads=[R_ssq], writes=[R_ssq])
                P.op("dve", lambda g: g.tensor_scalar(xn[:, i, :], xt, ssq[:, 2:3], None, ALU.mult),
                     reads=[R_xt, R_ssq], writes=[R_xn])
            for fc in range(16):
                bk = fc % 4
                pT = pss[bk][:].bitcast(BF16)
                for i in range(4):
                    P.op("pe", lambda g: g.transpose(pT[:, i * 128:(i + 1) * 128], xn[:, i, fc * 128:(fc + 1) * 128], ident_b),
                         reads=[R_xn, R_cmb], writes=[R_ps[bk]], pe_accum=i > 0)
                dst = hT[:, fc, tb * 512:(tb + 1) * 512]
                if alt() == "act":
                    P.op("act", lambda g: g.activation(out=dst, in_=pT[:, 0:512], func=AF.Identity,
                                                       scale=modc[:, 32 + fc:33 + fc], bias=modc[:, fc:fc + 1]),
                         reads=[R_ps[bk], R_modc], writes=[R_big])
                else:
                    P.op("dve", lambda g: g.tensor_scalar(dst, pT[:, 0:512], modc[:, 32 + fc:33 + fc], modc[:, fc:fc + 1],
                                                          ALU.mult, ALU.add),
                         reads=[R_ps[bk], R_modc], writes=[R_big])

        if stop == "pB":
            P.barrier(); return nc
        for pr in range(2):
            new_phase("gla")
            wfm = [ar.alloc([128, 16, 128], BF16, "wfm%d" % i) for i in range(3)]
            wtm = ar.alloc([128, 16, 256], BF16, "wtm")
            glrT, R_glrT = ar.alloc([16, S], BF16, "glrT")
            wlr, R_wlr = ar.alloc([16, 256], BF16, "wlr")
            bcs, R_bcs = ar.alloc([128, S], F32, "bcs")
            eb, R_eb = ar.alloc([128, S], F32, "eb")
            q_eT, R_qe = ar.alloc([128, S], BF16, "q_eT")
            k_eT, R_ke = ar.alloc([128, S], BF16, "k_eT")
            v_g, R_vg = ar.alloc([128, NT, 256], BF16, "v_g")
            sz, R_sz = ar.alloc([128, 2, S], BF16, "sz")
            ke_tok, R_ket = ar.alloc([128, NT, 128], BF16, "ke_tok")
            o_acc, R_oacc = ar.alloc([128, 2, S], F32, "o_acc")
            e1, R_e1 = ar.alloc([128, 512], F32, "e1")
            dec, R_dec = ar.alloc([128, 16], F32, "dec")
            nb, R_nb = ar.alloc([128, 2], F32, "nb")
            Sp, R_Sp = ar.alloc([128, 256], F32, "Sp")
            Stmp, R_Stmp = ar.alloc([128, 256], F32, "Stmp")
            Sbf, R_Sbf = ar.alloc([128, 256], BF16, "Sbf")
            attm, R_attm = ar.alloc([128, 2, 128], BF16, "attm")
            tmps = [ar.alloc([128, 2, 512], BF16, "sq"), ar.alloc([128, 512], F32, "t1"), ar.alloc([128, 512], F32, "rinv"),
                    ar.alloc([128, 512], F32, "u"), ar.alloc([128, 512], BF16, "ost"), ar.alloc([128, 512], BF16, "ost2")]

            def c_glr(tb, ps, R):
                copy_on("act", glrT[0:16, tb * 512:(tb + 1) * 512], ps, [R], [R_glrT])
            proj_fm(l, 4, 16, wfm, c_glr, 0)
            P.op("dve", lambda g: g.tensor_copy(wlr, sc(624, 256)[0:16, :]), reads=[R_small], writes=[R_wlr])
            P.op("dve", lambda g: g.tensor_scalar(nb, sc(48, 2), -1.0, None, ALU.mult), reads=[R_small], writes=[R_nb])
            for blk in range(4):
                sl = slice(blk * 512, (blk + 1) * 512)
                bk = 4 + blk % 2
                mm(pss[bk][:], wlr[0:16, pr * 128:(pr + 1) * 128], glrT[0:16, sl], True, True, [R_wlr, R_glrT], R_ps[bk])
                P.op("act", lambda g: g.activation(out=e1, in_=pss[bk][:], func=AF.Exp, scale=-1.0, bias=nb[:, pr:pr + 1]),
                     reads=[R_ps[bk], R_nb], writes=[R_e1])
                P.op("act", lambda g: g.activation(out=bcs[:, sl], in_=e1, func=AF.Ln, bias=one_ap), reads=[R_e1, R_misc], writes=[R_bcs])
            P.op("dve", lambda g: g.tensor_tensor_scan(out=bcs, data0=rmask[:], data1=bcs, initial=0.0, op0=ALU.mult, op1=ALU.add),
                 reads=[R_rmask, R_bcs], writes=[R_bcs])
            P.op("act", lambda g: g.activation(out=eb, in_=bcs, func=AF.Exp, scale=-1.0 / 16.0), reads=[R_bcs], writes=[R_eb])
            P.op("dve", lambda g: g.tensor_copy(dec, eb[:, 127:S:128]), reads=[R_eb], writes=[R_dec])
            P.op("act", lambda g: g.activation(out=bcs, in_=bcs, func=AF.Exp, scale=1.0 / 16.0), reads=[R_bcs], writes=[R_bcs])
            enb = bcs

            def c_q(tb, ps, R):
                sl = slice(tb * 512, (tb + 1) * 512)
                P.op("dve", lambda g: g.scalar_tensor_tensor(out=q_eT[:, sl], in0=ps, scalar=0.125, in1=eb[:, sl], op0=ALU.mult, op1=ALU.mult),
                     reads=[R, R_eb], writes=[R_qe])
            proj_fm(l, pr, 128, wfm, c_q, 1)

            def c_k(tb, ps, R):
                sl = slice(tb * 512, (tb + 1) * 512)
                P.op("dve", lambda g: g.tensor_tensor(k_eT[:, sl], ps, enb[:, sl], ALU.mult), reads=[R, R_bcs], writes=[R_ke])
            proj_fm(l, 2 + pr, 128, wfm, c_k, 0)

            def c_v(t, ps, R):
                copy_on("act", v_g[:, t, :], ps, [R], [R_vg])
            proj_tm(l, pr, wtm, c_v, [4, 5])
            for hh in range(2):
                def c_z(tb, ps, R):
                    P.op("act", lambda g: g.activation(out=sz[:, hh, tb * 512:(tb + 1) * 512], in_=ps, func=AF.Silu), reads=[R], writes=[R_sz])
                proj_fm(l, 5 + 2 * pr + hh, 128, wfm, c_z, hh)
            for t4 in range(4):
                bk = 6 + t4 % 2
                pT = pss[bk][:].bitcast(BF16)
                for i in range(4):
                    t = t4 * 4 + i
                    P.op("pe", lambda g: g.transpose(pT[:, i * 128:(i + 1) * 128], k_eT[:, t * 128:(t + 1) * 128], ident_b),
                         reads=[R_ke, R_cmb], writes=[R_ps[bk]], pe_accum=i > 0)
                P.op("dve", lambda g: g.tensor_copy(ke_tok[:, t4 * 4:(t4 + 1) * 4, :], pT[:, 0:512].rearrange("p (a b) -> p a b", b=128)),
                     reads=[R_ps[bk]], writes=[R_ket])
            P.op("dve", lambda g: g.memset(Sp, 0.0), writes=[R_Sp])
            for n in range(NT):
                ch = slice(n * 128, (n + 1) * 128)
                ba, bo, bkv = n % 2, 2 + n % 2, 4 + n % 2
                for hh in range(2):
                    hp = slice(64 * hh, 64 * hh + 64)
                    mm(pss[ba][:, hh * 128:(hh + 1) * 128], k_eT[hp, ch], q_eT[hp, ch], True, True, [R_ke, R_qe], R_ps[ba])
                P.op("dve", lambda g: g.tensor_tensor(attm, pss[ba][:, 0:256].rearrange("p (a b) -> p a b", b=128),
                                                      cm_f[:, CTU:CTU + 1, :].to_broadcast([128, 2, 128]), ALU.mult),
                     reads=[R_ps[ba], R_cmf], writes=[R_attm])
                for hh in range(2):
                    hp = slice(64 * hh, 64 * hh + 64)
                    vs = slice(hh * 128, (hh + 1) * 128)
                    mm(pss[bo][:, vs], v_g[:, n, vs], attm[:, hh, :], True, n == 0, [R_vg, R_attm], R_ps[bo])
                    if n > 0:
                        mm(pss[bo][:, vs], Sbf[hp, vs], q_eT[hp, ch], False, True, [R_Sbf, R_qe], R_ps[bo])
                P.op("act", lambda g: g.activation(out=o_acc[:, :, ch], in_=pss[bo][:, 0:256].rearrange("p (a b) -> p a b", b=128), func=AF.Copy),
                     reads=[R_ps[bo]], writes=[R_oacc])
                if n < NT - 1:
                    mm(pss[bkv][:, 0:256], ke_tok[:, n, :], v_g[:, n, :], True, True, [R_ket, R_vg], R_ps[bkv])
                    P.op("dve", lambda g: g.tensor_tensor(Stmp, Sp, pss[bkv][:, 0:256], ALU.add), reads=[R_Sp, R_ps[bkv]], writes=[R_Stmp])
                    P.op("dve", lambda g: g.tensor_scalar(Sp, Stmp, dec[:, n:n + 1], None, ALU.mult), reads=[R_Stmp, R_dec], writes=[R_Sp])
                    P.op("pool", lambda g: g.tensor_scalar(Sbf, Stmp, dec[:, n:n + 1], None, ALU.mult), reads=[R_Stmp, R_dec], writes=[R_Sbf])
            for hh in range(2):
                post_norm_store(l, o_acc[:, hh:hh + 1, :], R_oacc, 1, [sc(50)], sz[:, hh:hh + 1, :], R_sz, 2 * pr + hh, tmps, 128.0, 6)

        if stop == "pC1":
            P.barrier(); return nc
        for h in range(4):
            new_phase("gdn")
            wfm = [ar.alloc([128, 16, 128], BF16, "wfm%d" % i) for i in range(2)]
            xbf, R_xbf = ar.alloc([128, S + 8], BF16, "xbf")
            diag, R_diag = ar.alloc([128, 4, 128], BF16, "diag")
            cs, R_cs = ar.alloc([128, S], F32, "cs")
            knT, R_kn = ar.alloc([128, S], BF16, "knT")
            qnT, R_qn = ar.alloc([128, S], BF16, "qnT")
            cvT, R_cv = ar.alloc([128, S], BF16, "cvT")
            kbT, R_kb = ar.alloc([128, S], BF16, "kbT")
            q_eT, R_qe = ar.alloc([128, S], BF16, "q_eT")
            vb_tok, R_vb = ar.alloc([128, NT, 128], BF16, "vb_tok")
            kbg_tok, R_kbg = ar.alloc([128, NT, 128], BF16, "kbg_tok")
            kt_tok, R_kt = ar.alloc([128, NT, 128], BF16, "kt_tok")
            gc, R_gc = ar.alloc([128, S], F32, "gc")
            beta, R_beta = ar.alloc([128, S], BF16, "beta")
            eg, R_eg = ar.alloc([128, S], F32, "eg")
            tl, R_tl = ar.alloc([128, S], F32, "tl")
            dabT, R_dab = tl[0:8, :], R_tl
            cols, R_cols = ar.alloc([128, 6, 16], F32, "cols")
            nA, R_nA = ar.alloc([128, 4], F32, "nA")
            Sp, R_Sp = ar.alloc([128, 128], F32, "Sp")
            Sbf, R_Sbf = ar.alloc([128, 128], BF16, "Sbf")
            dm = [ar.alloc([128, 128], F32, "dm%d" % i) for i in range(4)]
            mb = [ar.alloc([128, 128], BF16, "mb%d" % i) for i in range(10)]
            u_sb, R_u = ar.alloc([128, 128], F32, "u_sb")
            tmps = [ar.alloc([128, 2, 512], BF16, "sq"), ar.alloc([128, 512], F32, "t1"), ar.alloc([128, 512], F32, "rinv"),
                    ar.alloc([128, 512], F32, "u"), ar.alloc([128, 512], BF16, "ost"), ar.alloc([128, 512], BF16, "ost2")]
            e1, R_e1 = tmps[3]
            o_acc, R_oacc = cs.rearrange("p (a b) -> p a b", a=1), R_cs

            def c_dab(tb, ps, R):
                copy_on("act", dabT[0:8, tb * 512:(tb + 1) * 512], ps, [R], [R_dab])
            proj_fm(l, 21, 8, wfm, c_dab, 0)
            P.op("dve", lambda g: g.memset(xbf[:, 0:3], 0.0), writes=[R_xbf])
            for which in range(3):
                ti = which * 4 + h
                for j in range(4):
                    P.op("dve", lambda g: g.tensor_scalar(diag[:, j, :], ident_f, sc(64 + ti * 4 + j), None, ALU.mult),
                         reads=[R_cmf, R_small], writes=[R_diag])

                def c_x(tb, ps, R):
                    copy_on(alt(), xbf[:, 3 + tb * 512:3 + (tb + 1) * 512], ps, [R], [R_xbf])
                proj_fm(l, 9 + 4 * which + h, 128, wfm, c_x, 1)
                for blk in range(4):
                    sl = slice(blk * 512, (blk + 1) * 512)
                    bk = blk % 2
                    for j in range(4):
                        mm(pss[bk][:], diag[:, j, :], xbf[:, blk * 512 + j: blk * 512 + j + 512], j == 0, j == 3, [R_diag, R_xbf], R_ps[bk])
                    if which == 2:
                        P.op("act", lambda g: g.activation(out=cvT[:, sl], in_=pss[bk][:], func=AF.Silu), reads=[R_ps[bk]], writes=[R_cv])
                    else:
                        P.op("act", lambda g: g.activation(out=cs[:, sl], in_=pss[bk][:], func=AF.Silu), reads=[R_ps[bk]], writes=[R_cs])
                if which < 2:
                    for blk in range(4):
                        sl = slice(blk * 512, (blk + 1) * 512)
                        rinv, R_rinv = rstd_part([(cs[:, sl], R_cs)], 512, 1.0, tmps[0:3], 2 + blk % 2)
                        if which == 0:
                            P.op("dve", lambda g: g.scalar_tensor_tensor(out=qnT[:, sl], in0=cs[:, sl], scalar=128.0 ** -0.5, in1=rinv[:, 0:512],
                                                                         op0=ALU.mult, op1=ALU.mult), reads=[R_cs, R_rinv], writes=[R_qn])
                        else:
                            P.op("dve", lambda g: g.tensor_tensor(knT[:, sl], cs[:, sl], rinv[:, 0:512], ALU.mult), reads=[R_cs, R_rinv], writes=[R_kn])
            if stop == "g1":
                P.barrier(); return nc
            P.op("act", lambda g: g.activation(out=nA, in_=sc(56, 4), func=AF.Exp), reads=[R_small], writes=[R_nA])
            P.op("dve", lambda g: g.tensor_scalar(nA, nA, -1.0, None, ALU.mult), reads=[R_nA], writes=[R_nA])
            for blk in range(4):
                sl = slice(blk * 512, (blk + 1) * 512)
                bk = 4 + blk % 2
                mm(pss[bk][:], selm[0:8, h, :], dabT[0:8, sl], True, True, [R_selm, R_dab], R_ps[bk])
                P.op("act", lambda g: g.activation(out=e1, in_=pss[bk][:], func=AF.Exp, bias=sc(60 + h)), reads=[R_ps[bk], R_small], writes=[R_e1])
                P.op("act", lambda g: g.activation(out=gc[:, sl], in_=e1, func=AF.Ln, bias=one_ap), reads=[R_e1, R_misc], writes=[R_gc])
                bk2 = 6 + blk % 2
                mm(pss[bk2][:], selm[0:8, 4 + h, :], dabT[0:8, sl], True, True, [R_selm, R_dab], R_ps[bk2])
                P.op("act", lambda g: g.activation(out=beta[:, sl], in_=pss[bk2][:], func=AF.Sigmoid), reads=[R_ps[bk2]], writes=[R_beta])
            P.op("dve", lambda g: g.tensor_tensor_scan(out=gc, data0=rmask[:], data1=gc, initial=0.0, op0=ALU.mult, op1=ALU.add),
                 reads=[R_rmask, R_gc], writes=[R_gc])
            P.op("dve", lambda g: g.tensor_scalar(gc, gc, nA[:, h:h + 1], None, ALU.mult), reads=[R_gc, R_nA], writes=[R_gc])
            gcl = cols[:, 0, :]
            cd = cols[:, 1, :]
            P.op("dve", lambda g: g.tensor_copy(gcl, gc[:, 127:S:128]), reads=[R_gc], writes=[R_cols])
            P.op("act", lambda g: g.activation(out=cd, in_=gcl, func=AF.Exp), reads=[R_cols], writes=[R_cols])
            P.op("act", lambda g: g.activation(out=eg, in_=gc, func=AF.Exp), reads=[R_gc], writes=[R_eg])
            P.op("dve", lambda g: g.tensor_tensor(q_eT, qnT, eg, ALU.mult), reads=[R_qn, R_eg], writes=[R_qe])
            P.op("dve", lambda g: g.tensor_tensor(kbT, knT, beta, ALU.mult), reads=[R_kn, R_beta], writes=[R_kb])
            P.op("pool", lambda g: g.tensor_tensor(eg, eg, beta, ALU.mult), reads=[R_eg, R_beta], writes=[R_eg])
            for n in range(NT):
                ch = slice(n * 128, (n + 1) * 128)
                P.op("act", lambda g: g.activation(out=tl[:, ch], in_=gc[:, ch], func=AF.Exp, scale=-1.0, bias=gcl[:, n:n + 1]),
                     reads=[R_gc, R_cols], writes=[R_tl])
            if stop == "g2":
                P.barrier(); return nc
            for qi, (src, R_src) in enumerate(((gc, R_gc), (beta, R_beta), (eg, R_eg), (tl, R_tl))):
                oh = cm_b[:, CI, 0:2] if src is beta else cm_f[:, CI, 0:2]
                for n in range(NT):
                    c0 = (qi * NT + n) * 2
                    mm(pss[3][:, c0:c0 + 2], src[:, n * 128:(n + 1) * 128], oh, True, True, [R_src, R_cmf, R_cmb], R_ps[3])
            P.op("dve", lambda g: g.tensor_copy(cols[:, 2:6, :], pss[3][:, 0:128:2].rearrange("p (a b) -> p a b", b=NT)),
                 reads=[R_ps[3]], writes=[R_cols])
            gc_col, beta_col, bexp_col, tail_col = (cols[:, i, :] for i in (2, 3, 4, 5))
            if stop == "g2b":
                P.barrier(); return nc
            for t in range(NT):
                bk = t % 2
                pT = pss[bk][:].bitcast(BF16)
                ts = slice(t * 128, (t + 1) * 128)
                P.op("pe", lambda g: g.transpose(pT[:, 0:128], knT[:, ts], ident_b), reads=[R_kn, R_cmb], writes=[R_ps[bk]])
                P.op("pe", lambda g: g.transpose(pT[:, 128:256], cvT[:, ts], ident_b), reads=[R_cv, R_cmb], writes=[R_ps[bk]], pe_accum=True)
                P.op("dve", lambda g: g.tensor_scalar(kbg_tok[:, t, :], pT[:, 0:128], bexp_col[:, t:t + 1], None, ALU.mult),
                     reads=[R_ps[bk], R_cols], writes=[R_kbg])
                P.op("dve", lambda g: g.tensor_scalar(kt_tok[:, t, :], pT[:, 0:128], tail_col[:, t:t + 1], None, ALU.mult),
                     reads=[R_ps[bk], R_cols], writes=[R_kt])
                P.op("dve", lambda g: g.tensor_scalar(vb_tok[:, t, :], pT[:, 128:256], beta_col[:, t:t + 1], None, ALU.mult),
                     reads=[R_ps[bk], R_cols], writes=[R_vb])

            if stop == "g3":
                P.barrier(); return nc
            P.barrier(reset=False)
            sz, R_sz = tl.bitcast(BF16)[:, 0:S].rearrange("p (a b) -> p a b", a=1), Reg("sz")
            mb2 = [(xbf[:, i * 128:(i + 1) * 128], Reg("mb2_%d" % i)) for i in range(16)]

            def c_z(tb, ps, R):
                P.op("act", lambda g: g.activation(out=sz[:, 0, tb * 512:(tb + 1) * 512], in_=ps, func=AF.Silu), reads=[R], writes=[R_sz])
            proj_fm(l, 22 + h, 128, wfm, c_z, 1)
            if stop == "g4":
                P.barrier(); return nc
            P.op("dve", lambda g: g.memset(Sp, 0.0), writes=[R_Sp])
            P.op("dve", lambda g: g.memset(Sbf, 0.0), writes=[R_Sbf])
            (dA, R_dA), (dB, R_dB), (dC, R_dC), (dD, R_dD) = dm
            for n in range(NT):
                ch = slice(n * 128, (n + 1) * 128)
                gcc = gc_col[:, n:n + 1]
                P.op("dve", lambda g: g.tensor_scalar(dA, gc[:, ch], gcc, 0.0, ALU.subtract, ALU.max), reads=[R_gc, R_cols], writes=[R_dA])
                P.op("act", lambda g: g.activation(out=dA, in_=dA, func=AF.Exp, scale=-1.0), reads=[R_dA], writes=[R_dA])
                P.op("dve", lambda g: g.tensor_scalar(dB, gc[:, ch], gcc, 0.0, ALU.subtract, ALU.min), reads=[R_gc, R_cols], writes=[R_dB])
                P.op("act", lambda g: g.activation(out=dB, in_=dB, func=AF.Exp), reads=[R_dB], writes=[R_dB])
                P.op("pool", lambda g: g.tensor_tensor(dC, dA, cm_f[:, CNSL, :], ALU.mult), reads=[R_dA, R_cmf], writes=[R_dC])
                P.op("pool", lambda g: g.tensor_tensor(dD, dB, cm_f[:, CNSU, :], ALU.mult), reads=[R_dB, R_cmf], writes=[R_dD])
                P.op("pool", lambda g: g.tensor_tensor(dB, dB, cm_f[:, CTU, :], ALU.mult), reads=[R_dB, R_cmf], writes=[R_dB])
                if stop == "c1":
                    P.barrier(); return nc
                (Pm, R_P), (PT, R_PT), (qkT, R_qk), (wT, R_wT), (vnew, R_vn) = mb[0:5]
                (Pd, R_Pd), (PTd, R_PTd), (Po32, R_Po32), (PTo32, R_PTo32), (Po64, R_Po64) = mb[5:10]
                b0, b1, b2 = 0, 1, 2
                mm(pss[b0][:, 0:128], kbT[:, ch], knT[:, ch], True, True, [R_kb, R_kn], R_ps[b0])
                mm(pss[b1][:, 0:128], knT[:, ch], kbT[:, ch], True, True, [R_kb, R_kn], R_ps[b1])
                mm(pss[b2][:, 0:128], knT[:, ch], qnT[:, ch], True, True, [R_kn, R_qn], R_ps[b2])
                P.op("dve", lambda g: g.tensor_tensor(Pm, pss[b0][:, 0:128], dC, ALU.mult), reads=[R_ps[b0], R_dC], writes=[R_P])
                P.op("dve", lambda g: g.tensor_tensor(PT, pss[b1][:, 0:128], dD, ALU.mult), reads=[R_ps[b1], R_dD], writes=[R_PT])
                P.op("dve", lambda g: g.tensor_tensor(qkT, pss[b2][:, 0:128], dB, ALU.mult), reads=[R_ps[b2], R_dB], writes=[R_qk])
                P.op("pool", lambda g: g.tensor_tensor(Pd, Pm, cm_b[:, CB32, :], ALU.mult), reads=[R_P, R_cmb], writes=[R_Pd])
                P.op("pool", lambda g: g.tensor_tensor(PTd, PT, cm_b[:, CB32, :], ALU.mult), reads=[R_PT, R_cmb], writes=[R_PTd])
                P.op("pool", lambda g: g.tensor_tensor(Po32, Pm, cm_b[:, CO32, :], ALU.mult), reads=[R_P, R_cmb], writes=[R_Po32])
                P.op("pool", lambda g: g.tensor_tensor(PTo32, PT, cm_b[:, CO32, :], ALU.mult), reads=[R_PT, R_cmb], writes=[R_PTo32])
                P.op("pool", lambda g: g.tensor_tensor(Po64, Pm, cm_b[:, CO64, :], ALU.mult), reads=[R_P, R_cmb], writes=[R_Po64])
                Acur, ATcur = mb2[0], mb2[1]
                Anxt, ATnxt = mb2[2], mb2[3]
                P.op("pool", lambda g: g.tensor_tensor(Acur[0], Pd, ident_b, ALU.add), reads=[R_Pd, R_cmb], writes=[Acur[1]])
                P.op("pool", lambda g: g.tensor_tensor(ATcur[0], PTd, ident_b, ALU.add), reads=[R_PTd, R_cmb], writes=[ATcur[1]])
                cur = ((Pd, R_Pd), (PTd, R_PTd))
                sqb = [(mb2[4], mb2[5]), (mb2[6], mb2[7])]
                for lev in range(4):
                    (cP, R_cP), (cPT, R_cPT) = cur
                    (nP, R_nP), (nPT, R_nPT) = sqb[lev % 2]
                    mm(pss[3][:, 0:128], cPT, cP, True, True, [R_cP, R_cPT], R_ps[3])
                    mm(pss[4][:, 0:128], cP, cPT, True, True, [R_cP, R_cPT], R_ps[4])
                    copy_on("act", nP, pss[3][:, 0:128], [R_ps[3]], [R_nP])
                    copy_on("dve", nPT, pss[4][:, 0:128], [R_ps[4]], [R_nPT])
                    mm(pss[5][:, 0:128], ident_b, Acur[0], True, False, [R_cmb, Acur[1]], R_ps[5])
                    mm(pss[5][:, 0:128], nPT, Acur[0], False, True, [R_nPT, Acur[1]], R_ps[5])
                    mm(pss[6][:, 0:128], ident_b, ATcur[0], True, False, [R_cmb, ATcur[1]], R_ps[6])
                    mm(pss[6][:, 0:128], nP, ATcur[0], False, True, [R_nP, ATcur[1]], R_ps[6])
                    copy_on("dve", Anxt[0], pss[5][:, 0:128], [R_ps[5]], [Anxt[1]])
                    copy_on("act", ATnxt[0], pss[6][:, 0:128], [R_ps[6]], [ATnxt[1]])
                    Acur, Anxt = Anxt, Acur
                    ATcur, ATnxt = ATnxt, ATcur
                    cur = ((nP, R_nP), (nPT, R_nPT))
                (U1, R_U1), (T1, R_T1) = mb2[8], mb2[9]
                mm(pss[3][:, 0:128], PTo32, Acur[0], True, True, [R_PTo32, Acur[1]], R_ps[3])
                mm(pss[4][:, 0:128], Po32, ATcur[0], True, True, [R_Po32, ATcur[1]], R_ps[4])
                copy_on("act", U1, pss[3][:, 0:128], [R_ps[3]], [R_U1])
                copy_on("dve", T1, pss[4][:, 0:128], [R_ps[4]], [R_T1])
                mm(pss[5][:, 0:128], ident_b, Acur[0], True, False, [R_cmb, Acur[1]], R_ps[5])
                mm(pss[5][:, 0:128], ATcur[0], U1, False, True, [ATcur[1], R_U1], R_ps[5])
                mm(pss[6][:, 0:128], ident_b, ATcur[0], True, False, [R_cmb, ATcur[1]], R_ps[6])
                mm(pss[6][:, 0:128], Acur[0], T1, False, True, [Acur[1], R_T1], R_ps[6])
                copy_on("dve", Anxt[0], pss[5][:, 0:128], [R_ps[5]], [Anxt[1]])
                copy_on("act", ATnxt[0], pss[6][:, 0:128], [R_ps[6]], [ATnxt[1]])
                Acur, Anxt = Anxt, Acur
                ATcur, ATnxt = ATnxt, ATcur
                mm(pss[4][:, 0:128], Po64, ATcur[0], True, True, [R_Po64, ATcur[1]], R_ps[4])
                copy_on("dve", T1, pss[4][:, 0:128], [R_ps[4]], [R_T1])
                mm(pss[6][:, 0:128], ident_b, ATcur[0], True, False, [R_cmb, ATcur[1]], R_ps[6])
                mm(pss[6][:, 0:128], Acur[0], T1, False, True, [Acur[1], R_T1], R_ps[6])
                copy_on("act", ATnxt[0], pss[6][:, 0:128], [R_ps[6]], [ATnxt[1]])
                ATc = ATnxt
                if stop == "c3":
                    P.barrier(); return nc
                AT, R_AT = ATc
                mm(pss[6][:, 0:128], AT, vb_tok[:, n, :], True, True, [R_AT, R_vb], R_ps[6])
                copy_on("act", u_sb, pss[6][:, 0:128], [R_ps[6]], [R_u])
                if stop == "c3a":
                    P.barrier(); return nc
                mm(pss[5][:, 0:128], kbg_tok[:, n, :], AT, True, True, [R_AT, R_kbg], R_ps[5])
                if stop == "c3b":
                    P.barrier(); return nc
                copy_on("dve", wT, pss[5][:, 0:128], [R_ps[5]], [R_wT])
                if stop == "c4":
                    P.barrier(); return nc
                if n > 0:
                    mm(pss[0][:, 0:128], wT, Sbf, True, True, [R_wT, R_Sbf], R_ps[0])
                    P.op("dve", lambda g: g.tensor_tensor(vnew, u_sb, pss[0][:, 0:128], ALU.subtract), reads=[R_u, R_ps[0]], writes=[R_vn])
                else:
                    copy_on("dve", vnew, u_sb, [R_u], [R_vn])
                if n > 0:
                    mm(pss[7][:, 0:128], Sbf, q_eT[:, ch], True, False, [R_Sbf, R_qe], R_ps[7])
                mm(pss[7][:, 0:128], vnew, qkT, n == 0, True, [R_vn, R_qk], R_ps[7])
                copy_on("act", o_acc[:, 0, ch], pss[7][:, 0:128], [R_ps[7]], [R_oacc])
                if n < NT - 1:
                    mm(pss[1][:, 0:128], kt_tok[:, n, :], vnew, True, True, [R_kt, R_vn], R_ps[1])
                    P.op("dve", lambda g: g.scalar_tensor_tensor(out=Sp, in0=Sp, scalar=cd[:, n:n + 1], in1=pss[1][:, 0:128],
                                                                 op0=ALU.mult, op1=ALU.add), reads=[R_Sp, R_cols, R_ps[1]], writes=[R_Sp])
                    copy_on("act", Sbf, Sp, [R_Sp], [R_Sbf])
            post_norm_store(l, o_acc, R_oacc, 1, [sc(51)], sz, R_sz, 4 + h, tmps, 128.0, 6)

        if stop == "pC2":
            P.barrier(); return nc
        for h in range(4):
            new_phase("diff")
            wfm = [ar.alloc([128, 16, 128], BF16, "wfm%d" % i) for i in range(3)]
            wtm = ar.alloc([128, 16, 256], BF16, "wtm")
            qT, R_q = ar.alloc([128, 2, S], BF16, "qT")
            kT, R_k = ar.alloc([128, 2, S], BF16, "kT")
            v_sb, R_v = ar.alloc([128, NT, 256], BF16, "v_sb")
            sz, R_sz = ar.alloc([128, 2, S], BF16, "sz")
            ebuf = [ar.alloc([128, 512], BF16, "e%d" % i) for i in range(3)]
            qn, R_qnb = ar.alloc([128, 512], BF16, "qn")
            ta, R_ta = ar.alloc([128, 512], F32, "ta")
            tb_, R_tb = ar.alloc([128, 512], F32, "tb")
            tO, R_tO = ar.alloc([128, 2, 512], F32, "tO")
            rs, R_rs = ar.alloc([128, 512], F32, "rs")
            lamt, R_lam = ar.alloc([128, 8], F32, "lam")
            lprod, R_lprod = ar.alloc([128, 256], F32, "lprod")
            wn2, R_wn2 = ar.alloc([128, 2], F32, "wn2")
            tmps = [ar.alloc([128, 2, 512], BF16, "sq"), ar.alloc([128, 512], F32, "t1"), ar.alloc([128, 512], F32, "rinv"),
                    ar.alloc([128, 512], F32, "u"), ar.alloc([128, 512], BF16, "ost"), ar.alloc([128, 512], BF16, "ost2")]
            P.op("dve", lambda g: g.tensor_tensor(lprod[:, 0:128], sc(112, 128), sc(240, 128), ALU.mult), reads=[R_small], writes=[R_lprod])
            P.op("dve", lambda g: g.tensor_tensor(lprod[:, 128:256], sc(368, 128), sc(496, 128), ALU.mult), reads=[R_small], writes=[R_lprod])
            P.op("dve", lambda g: g.tensor_reduce(lamt[:, 0:2], lprod.rearrange("p (a b) -> p a b", b=128), AX.X, ALU.add),
                 reads=[R_lprod], writes=[R_lam])
            P.op("act", lambda g: g.activation(out=lamt[:, 2:4], in_=lamt[:, 0:2], func=AF.Exp), reads=[R_lam], writes=[R_lam])
            P.op("dve", lambda g: g.scalar_tensor_tensor(out=lamt[:, 4:5], in0=lamt[:, 3:4], scalar=-lam_init, in1=lamt[:, 2:3],
                                                         op0=ALU.add, op1=ALU.subtract), reads=[R_lam], writes=[R_lam])
            nlam = lamt[:, 4:5]
            P.op("dve", lambda g: g.tensor_scalar(wn2, sc(54, 2), 1.0 - lam_init, None, ALU.mult), reads=[R_small], writes=[R_wn2])
            units = [(which, m, half) for which in range(2) for m in range(2) for half in range(2)]
            qk_meta = ((qT, R_q, 26, 52), (kT, R_k, 34, 53))
            ustate = {}

            def qk_issue(ui):
                which, m, half = units[ui]
                g0 = qk_meta[which][2]
                if half == 0:
                    wb, R_wb = wfm[state.setdefault("wfm_i", 0) % len(wfm)]
                    state["wfm_i"] += 1
                    P.dma("pool", wb, winfm_d[l, g0 + 2 * h + m].rearrange("p (a b) -> p a b", b=128), writes=[R_wb])
                    ustate["w"] = (wb, R_wb)
                wb, R_wb = ustate["w"]
                banks = [4 + 2 * (ui % 2), 5 + 2 * (ui % 2)]
                for kc in range(16):
                    for i in range(2):
                        tb = half * 2 + i
                        mm(pss[banks[i]][:], wb[:, kc, :], big[:, kc, tb * 512:(tb + 1) * 512], kc == 0, kc == 15,
                           [R_wb, R_big], R_ps[banks[i]])

            def qk_consume(ui):
                which, m, half = units[ui]
                dstT, R_dst, g0, wcol = qk_meta[which]
                banks = [4 + 2 * (ui % 2), 5 + 2 * (ui % 2)]
                for i in range(2):
                    tb = half * 2 + i
                    ps, R = pss[banks[i]][:], R_ps[banks[i]]
                    sl = slice(tb * 512, (tb + 1) * 512)
                    rinv, R_rinv = rstd_part([(ps, R)], 512, 128.0, tmps[0:3], 2 + tb % 2)
                    P.op("dve", lambda g: g.scalar_tensor_tensor(out=qn, in0=ps, scalar=sc(wcol), in1=rinv[:, 0:512], op0=ALU.mult, op1=ALU.mult),
                         reads=[R, R_rinv, R_small], writes=[R_qnb])
                    bkr = 2 + (tb + 1) % 2
                    mm(pss[bkr][:], cm_b[:, CP, :], qn, True, True, [R_cmb, R_qnb], R_ps[bkr])
                    P.op("pool", lambda g: g.tensor_tensor(ta, qn, cosT[:, sl], ALU.mult), reads=[R_qnb, R_cos], writes=[R_ta])
                    P.op("dve", lambda g: g.tensor_tensor(tb_, pss[bkr][:], sinT[:, sl], ALU.mult), reads=[R_ps[bkr], R_sin], writes=[R_tb])
                    P.op("pool", lambda g: g.tensor_tensor(dstT[:, m, sl], ta, tb_, ALU.add), reads=[R_ta, R_tb], writes=[R_dst])
            qk_issue(0)
            for ui in range(len(units)):
                if ui + 1 < len(units):
                    qk_issue(ui + 1)
                qk_consume(ui)

            def c_v(t, ps, R):
                copy_on(alt(), v_sb[:, t, :], ps, [R], [R_v])
            proj_tm(l, 2 + h, wtm, c_v, [0, 1])
            for j in range(2):
                def c_z(tb, ps, R):
                    P.op("act", lambda g: g.activation(out=sz[:, j, tb * 512:(tb + 1) * 512], in_=ps, func=AF.Silu), reads=[R], writes=[R_sz])
                proj_fm(l, 42 + 2 * h + j, 128, wfm, c_z, j)
            if h == 3:
                for fc in range(16):
                    P.dma("pool", big[:, fc, :], wout_d[l][:, fc * D:(fc + 1) * D], writes=[R_big], nowaw=fc > 0)
            scale = 128.0 ** -0.5
            steps = [(qb, m, kc) for qb in range(4) for m in range(2) for kc in range(4 * (qb + 1))]

            def qk_mm(si):
                qb, m, kc = steps[si]
                col0 = max(kc - 4 * qb, 0) * 128
                bsc = si % 2
                mm(pss[bsc][:, col0:512], kT[:, m, kc * 128:(kc + 1) * 128], qT[:, m, qb * 512 + col0:(qb + 1) * 512], True, True,
                   [R_k, R_q], R_ps[bsc])
            qk_mm(0)
            for si, (qb, m, kc) in enumerate(steps):
                qs = slice(qb * 512, (qb + 1) * 512)
                nk = 4 * (qb + 1)
                bo0, bo1, bs = (2, 3, 4) if m == 0 else (5, 6, 7)
                if si + 1 < len(steps):
                    qk_mm(si + 1)
                bsc = si % 2
                c = kc - 4 * qb
                col0 = max(c, 0) * 128
                e, R_e = ebuf[si % 3]
                P.op("act", lambda g: g.activation(out=e[:, col0:512], in_=pss[bsc][:, col0:512], func=AF.Exp, scale=scale),
                     reads=[R_ps[bsc]], writes=[R_e])
                if c >= 0:
                    P.op("pool", lambda g: g.tensor_tensor(e[:, col0:col0 + 128], e[:, col0:col0 + 128], cm_b[:, CTU, :], ALU.mult),
                         reads=[R_e, R_cmb], writes=[R_e])
                first, last = kc == 0, kc == nk - 1
                mm(pss[bo0][:, col0:512], v_sb[:, kc, 0:128], e[:, col0:512], first, last, [R_v, R_e], R_ps[bo0])
                mm(pss[bo1][:, col0:512], v_sb[:, kc, 128:256], e[:, col0:512], first, last, [R_v, R_e], R_ps[bo1])
                mm(pss[bs][:, col0:512], ones_b, e[:, col0:512], first, last, [R_cmb, R_e], R_ps[bs])
                if not last:
                    continue
                P.op("dve", lambda g: g.reciprocal(rs, pss[bs][:]), reads=[R_ps[bs]], writes=[R_rs])
                for j, bo in enumerate((bo0, bo1)):
                    if m == 0:
                        P.op("dve", lambda g: g.tensor_tensor(tO[:, j, :], pss[bo][:], rs, ALU.mult), reads=[R_ps[bo], R_rs], writes=[R_tO])
                    else:
                        P.op("dve", lambda g: g.scalar_tensor_tensor(out=ta, in0=pss[bo][:], scalar=nlam, in1=rs, op0=ALU.mult, op1=ALU.mult),
                             reads=[R_ps[bo], R_rs, R_lam], writes=[R_ta])
                        P.op("pool", lambda g: g.tensor_tensor(tO[:, j, :], tO[:, j, :], ta, ALU.add), reads=[R_tO, R_ta], writes=[R_tO])
                if m == 0:
                    continue
                (sq, R_sq), (t1, R_t1), (rinv, R_rinv), (u, R_u) = tmps[0:4]
                rstd_part([(tO[:, j, :], R_tO) for j in range(2)], 512, 256.0, tmps[0:3], 4)
                for j in range(2):
                    ost, R_ost = tmps[4 + j]
                    P.op("dve", lambda g: g.scalar_tensor_tensor(out=u, in0=tO[:, j, :], scalar=wn2[:, j:j + 1], in1=rinv, op0=ALU.mult, op1=ALU.mult),
                         reads=[R_tO, R_rinv, R_wn2], writes=[R_u])
                    P.op("pool", lambda g: g.tensor_tensor(ost, u, sz[:, j, qs], ALU.mult), reads=[R_u, R_sz], writes=[R_ost])
                    P.dma("sp", oT_d[8 + 2 * h + j, :, qs], ost, reads=[R_ost], writes=[R_oT], nowaw=True)

        if stop == "pC3":
            P.barrier(); return nc
        new_phase("D")
        wo = big
        xts = [ar.alloc([128, D], F32, "xt%d" % i) for i in range(2)]
        obs = [ar.alloc([128, 16, 512], BF16, "ob%d" % i) for i in range(2)]
        xos = [ar.alloc([128, D], F32, "xo%d" % i) for i in range(2)]
        ytmp, R_yt = ar.alloc([128, 512], F32, "ytmp")
        xsrc = x_d if l == 0 else x1_d
        xdst = out_d if l == L - 1 else x1_d
        R_dst = R_out if l == L - 1 else R_x1
        for tb in range(4):
            ob, R_ob = obs[tb % 2]
            P.dma("sp", ob, oT_d[:, :, tb * 512:(tb + 1) * 512].rearrange("c p t -> p c t"), reads=[R_oT], writes=[R_ob])
            for i in range(4):
                t = tb * 4 + i
                xt, R_xt = xts[t % 2]
                xo, R_xo = xos[t % 2]
                P.dma("sp", xt, xsrc[t * 128:(t + 1) * 128, :], reads=[R_x1] if l > 0 else [], writes=[R_xt])
                for ng in range(4):
                    ns = slice(ng * 512, (ng + 1) * 512)
                    bk = (t * 4 + ng) % 4
                    for fc in range(16):
                        mm(pss[bk][:], ob[:, fc, i * 128:(i + 1) * 128], wo[:, fc, ns], fc == 0, fc == 15, [R_ob, R_big], R_ps[bk])
                    P.op("dve", lambda g: g.tensor_tensor(ytmp, pss[bk][:], gate_bc[:, ns], ALU.mult), reads=[R_ps[bk], R_gate], writes=[R_yt])
                    P.op("pool", lambda g: g.tensor_tensor(xo[:, ns], ytmp, xt[:, ns], ALU.add), reads=[R_yt, R_xt], writes=[R_xo])
                P.dma("sp", xdst[t * 128:(t + 1) * 128, :], xo, reads=[R_xo], writes=[R_dst], nowaw=True)

    P.barrier()
    if scopes and state.get("scope") is not None:
        nc.leave_named_scope(state["scope"][0], state["scope"][1], False)
    return nc


def _col(v):
    return np.ascontiguousarray(np.asarray(v, np.float32).reshape(-1, 128).T)


def _consts():
    p = np.arange(128)[:, None]
    j = np.arange(128)[None, :]
    cm = np.zeros((128, NCM, 128), np.float32)
    cm[:, CI] = (p == j)
    cm[:, CO] = 1.0
    prot = np.zeros((128, 128), np.float32)
    prot[(j[0, :64] + 64), j[0, :64]] = -1.0
    prot[(j[0, 64:] - 64), j[0, 64:]] = 1.0
    cm[:, CP] = prot
    cm[:, CTU] = (p <= j)
    cm[:, CNSU] = -1.0 * (p < j)
    cm[:, CNSL] = -1.0 * (p > j)
    bd32 = (p // 32 == j // 32)
    bd64 = (p // 64 == j // 64)
    cm[:, CB32] = bd32
    cm[:, CO32] = bd64 & ~bd32
    cm[:, CO64] = ~bd64
    selm = np.zeros((8, 8, 128), np.float32)
    for k in range(8):
        selm[k, k, :] = 1.0
    rmask = np.ones((128, S), np.float32)
    rmask[:, 0::128] = 0.0
    half = 64
    inv_freq = (10000.0 ** (-(np.arange(half, dtype=np.float32) / np.float32(half)))).astype(np.float32)
    invf = np.concatenate([inv_freq, inv_freq]).astype(np.float64) / (2.0 * math.pi)
    return cm.reshape(128, NCM * 128), selm.reshape(8, 8 * 128), rmask, invf.astype(np.float32)


FM_GROUPS = ([(0, 128), (128, 128), (256, 128), (384, 128), (1024, 16)] + [(1040 + 128 * i, 128) for i in range(4)]
             + [(1552 + 128 * i, 128) for i in range(12)] + [(3088, 8)] + [(3096 + 128 * i, 128) for i in range(4)]
             + [(3608 + 128 * i, 128) for i in range(8)] + [(4632 + 128 * i, 128) for i in range(8)]
             + [(6680 + 128 * i, 128) for i in range(8)])
TM_GROUPS = [(512, 256), (768, 256)] + [(5656 + 256 * i, 256) for i in range(4)]


def _prep_shared(inp):
    f = lambda k: np.asarray(inp[k], np.float32)
    cm, selm, rmask, invf = _consts()
    w_in = f("w_in")
    winfm = np.zeros((2, 50, 128, 16, 128), np.float32)
    wintm = np.zeros((2, 6, 128, 16, 256), np.float32)
    for l in range(2):
        wl = w_in[l].reshape(16, 128, -1)
        for gi, (c0, n) in enumerate(FM_GROUPS):
            winfm[l, gi, :, :, :n] = wl[:, :, c0:c0 + n].transpose(1, 0, 2)
        for gi, (c0, n) in enumerate(TM_GROUPS):
            wintm[l, gi] = wl[:, :, c0:c0 + n].transpose(1, 0, 2)
    wada = np.ascontiguousarray(f("w_ada").reshape(2, 16, 128, 48, 128).transpose(0, 3, 2, 1, 4)).reshape(2, 48, 128, 16 * 128)
    wout = np.ascontiguousarray(f("w_out").reshape(2, 16, 128, D).transpose(0, 2, 1, 3)).reshape(2, 128, 16 * D)
    bgate = np.ascontiguousarray(f("b_ada")[:, 2 * D:].reshape(2, 1, D))
    sm = np.zeros((128, NS), np.float32)
    sm[:, 16] = invf
    for l in range(2):
        b = SBASE + l * SLW
        sm[:, b:b + 16] = _col(f("norm_w")[l])
        sm[:, b + 16:b + 48] = _col(f("b_ada")[l, :2 * D])
        sm[:, b + 48:b + 50] = _col(f("gla_b_lr")[l])
        sm[:, b + 50] = f("gla_norm_w")[l]
        sm[:, b + 51] = f("gdn_norm_w")[l]
        sm[:, b + 52] = f("diff_q_norm_w")[l]
        sm[:, b + 53] = f("diff_k_norm_w")[l]
        sm[:, b + 54:b + 56] = _col(f("diff_norm_w")[l])
        sm[:, b + 56:b + 60] = f("gdn_a_log")[l][None, :]
        sm[:, b + 60:b + 64] = f("gdn_dt_bias")[l][None, :]
        cw = f("gdn_conv_w")[l]
        sm[:, b + 64:b + 112] = cw.reshape(4, 12, 128).transpose(2, 1, 0).reshape(128, 48)
        sm[:, b + 112:b + 624] = f("diff_lambda")[l].reshape(1, 512)
        sm[0:16, b + 624:b + 880] = f("gla_w_lr")[l]
    shared = {"cmat": cm, "selm": selm, "rmask": rmask, "wada": wada, "bgate": bgate,
              "winfm": winfm.reshape(2, 50, 128, 16 * 128), "wintm": wintm.reshape(2, 6, 128, 16 * 256), "wout": wout}
    return shared, sm


def make_in_maps(inp, cores):
    shared, sm = _prep_shared(inp)
    x = np.asarray(inp["x"], np.float32)
    c = np.asarray(inp["c"], np.float32)
    pos = np.asarray(inp["positions"], np.int32)
    maps = []
    for b in cores:
        s = sm.copy()
        s[:, 0:16] = _col(c[b])
        m = dict(shared)
        m["x"] = np.ascontiguousarray(x[b])
        m["small"] = s
        m["pos"] = np.ascontiguousarray(pos[b:b + 1])
        maps.append(m)
    return maps


_NC_CACHE = {}


def kernel(**inputs):
    if "nc" not in _NC_CACHE:
        _NC_CACHE["nc"] = build(2)
    nc = _NC_CACHE["nc"]
    cores = [i // 2 for i in range(8)]
    maps = make_in_maps(inputs, cores)
    res = run_bass_kernel_spmd(nc, maps, core_ids=list(range(8)))
    out = np.stack([np.asarray(res.results[2 * b]["out"], np.float32) for b in range(4)], axis=0)
    return out
```

```python
import math
import numpy as np
import concourse.bass as bass
import concourse.mybir as mybir
from concourse.bass_utils import run_bass_kernel_spmd

F32 = mybir.dt.float32
BF16 = mybir.dt.bfloat16
I32 = mybir.dt.int32
AF = mybir.ActivationFunctionType
ALU = mybir.AluOpType
AX = mybir.AxisListType

S = 2048
D = 2048
NT = 16
EPS = 1e-6
SLW = 880
SBASE = 32
NS = SBASE + 2 * SLW
CI, CO, CP, CTU, CNSU, CNSL, CB32, CO32, CO64 = range(9)
NCM = 9


class Reg:
    __slots__ = ("name", "lw", "rd", "dsem", "local", "psum")

    def __init__(self, name, local=False, psum=False):
        self.name = name
        self.psum = psum
        self.lw = None
        self.rd = {}
        self.dsem = None
        self.local = local


class Prog:
    def __init__(self, nc):
        self.nc = nc
        self.eng = {"pe": nc.tensor, "act": nc.scalar, "dve": nc.vector, "pool": nc.gpsimd, "sp": nc.sync}
        self.sem, self.cnt, self.semobj = {}, {}, {}
        self.seen = {e: {} for e in self.eng}
        for e in self.eng:
            s = nc.alloc_semaphore(name="s_" + e)
            self.sem[e] = s
            self.semobj[e] = s
            self.cnt[e] = 0
        self.vc = {}
        self.dcnt = {}
        self.local_keys = []
        self.local_next = 0
        self.ninstr = 0
        self.nwait = 0

    def _wait(self, e, key, val):
        if self.seen[e].get(key, 0) >= val:
            return
        self.eng[e].wait_ge(self.semobj[key], val)
        self.nwait += 1
        se = self.seen[e]
        se[key] = val
        clk = self.vc.get((key, val))
        if clk:
            for k, v in clk.items():
                if se.get(k, 0) < v:
                    se[k] = v

    def _deps(self, e, reads, writes, pe_accum=False, nowaw=False):
        need = {}

        def add(k, v):
            if need.get(k, 0) < v:
                need[k] = v
        for r in reads:
            if r.lw is not None:
                add(*r.lw)
            if r.psum:
                for k, v in r.rd.items():
                    if k != e:
                        add(k, v)
        for w in writes:
            if w.lw is not None and not nowaw and not (pe_accum and w.lw[0] == "pe"):
                add(*w.lw)
            for k, v in w.rd.items():
                add(k, v)
        for k, v in sorted(need.items(), key=lambda kv: -kv[1] if isinstance(kv[0], str) else 0):
            self._wait(e, k, v)

    def _record(self, ev, reads, writes, nowaw=False):
        for r in reads:
            if r.rd.get(ev[0], 0) < ev[1]:
                r.rd[ev[0]] = ev[1]
        for w in writes:
            w.lw = ev
            if not nowaw:
                w.rd = {}

    def op(self, e, fn, reads=(), writes=(), pe_accum=False):
        self._deps(e, reads, writes, pe_accum)
        ins = fn(self.eng[e])
        self.cnt[e] += 1
        ins.then_inc(self.sem[e], 1)
        ev = (e, self.cnt[e])
        self.vc[ev] = dict(self.seen[e])
        self._record(ev, reads, writes)
        self.ninstr += 1

    def _newkey(self):
        key = ("d", len(self.dcnt))
        self.semobj[key] = self.nc.alloc_semaphore(name="d_%d" % len(self.dcnt))
        self.dcnt[key] = 0
        return key

    def _dkey(self, w):
        if w.dsem is None:
            if w.local:
                if self.local_next == len(self.local_keys):
                    self.local_keys.append(self._newkey())
                w.dsem = self.local_keys[self.local_next]
                self.local_next += 1
            else:
                w.dsem = self._newkey()
        return w.dsem

    def dma(self, q, out_ap, in_ap, reads=(), writes=(), nowaw=False, **kw):
        w = writes[0]
        self._deps(q, reads, writes, nowaw=nowaw)
        key = self._dkey(w)
        ins = self.eng[q].dma_start(out=out_ap, in_=in_ap, **kw)
        self.dcnt[key] += 16
        ins.then_inc(self.semobj[key], 16)
        ev = (key, self.dcnt[key])
        self.vc[ev] = dict(self.seen[q])
        self._record(ev, reads, writes, nowaw=nowaw)
        self.ninstr += 1

    def barrier(self, reset=True):
        evs = [(e, self.cnt[e]) for e in self.eng if self.cnt[e] > 0]
        evs += [(k, v) for k, v in self.dcnt.items() if v > 0]
        for e in self.eng:
            for k, v in evs:
                if k != e:
                    self._wait(e, k, v)
        if reset:
            self.local_next = 0


class Arena:
    def __init__(self, nc, nwords):
        self.t = nc.alloc_sbuf_tensor("arena", [128, nwords], F32)
        self.n = nwords
        self.off = 0
        self.k = 0

    def reset(self):
        self.off = 0

    def alloc(self, shape, dt, name=None):
        free = int(np.prod(shape[1:]))
        words = free if dt in (F32, I32) else (free + 1) // 2
        words = (words + 7) // 8 * 8
        assert self.off + words <= self.n, ("arena overflow", name, self.off, words, self.n)
        v = self.t[:, self.off:self.off + words]
        self.off += words
        if dt == BF16:
            v = v.bitcast(BF16)[:, 0:free]
        elif dt == I32:
            v = v.bitcast(I32)[:, 0:free]
        else:
            v = v[:, 0:free]
        if len(shape) == 3:
            v = v.rearrange("p (a b) -> p a b", b=shape[2])
        self.k += 1
        if shape[0] < 128:
            v = v[0:shape[0]]
        return v, Reg(name or "a%d" % self.k, local=True)


def build(nlayers=2, debug=False, stop=None, scopes=False):
    nc = bass.Bass("TRN2", target_bir_lowering=False)
    P = Prog(nc)
    L = nlayers

    def din(name, shape, dt=F32):
        return nc.dram_tensor(name, shape, dt, kind="ExternalInput").ap()

    x_d = din("x", [S, D])
    small_d = din("small", [128, NS])
    cmat_d = din("cmat", [128, NCM * 128])
    selm_d = din("selm", [8, 8 * 128])
    rmask_d = din("rmask", [128, S])
    pos_d = din("pos", [1, S], I32)
    wada_d = din("wada", [2, 48, 128, 16 * 128])
    bgate_d = din("bgate", [2, 1, D])
    winfm_d = din("winfm", [2, 50, 128, 16 * 128])
    wintm_d = din("wintm", [2, 6, 128, 16 * 256])
    wout_d = din("wout", [2, 128, 16 * D])
    out_d = nc.dram_tensor("out", [S, D], F32, kind="ExternalOutput").ap()
    x1_d = nc.dram_tensor("x1s", [S, D], F32, kind="Internal").ap()
    oT_d = nc.dram_tensor("oTs", [16, 128, S], BF16, kind="ExternalOutput" if debug else "Internal").ap()
    R_out, R_x1, R_oT = Reg("out"), Reg("x1"), Reg("oT")

    def sb(name, shape, dt):
        return nc.alloc_sbuf_tensor("sb_" + name, shape, dt), Reg(name)

    big, R_big = sb("big", [128, 16, S], BF16)
    cosT, R_cos = sb("cosT", [128, S], BF16)
    sinT, R_sin = sb("sinT", [128, S], BF16)
    gate_bc, R_gate = sb("gate_bc", [128, D], F32)
    rmask, R_rmask = sb("rmask", [128, S], BF16)
    cm_f, R_cmf = sb("cm_f", [128, NCM, 128], F32)
    cm_b, R_cmb = sb("cm_b", [128, NCM, 128], BF16)
    selm, R_selm = sb("selm", [8, 8, 128], F32)
    small, R_small = sb("small", [128, NS], F32)
    modc, R_modc = sb("modc", [128, 48], F32)
    misc, R_misc = sb("misc", [128, 64], F32)
    ar = Arena(nc, (nc.sbuf_bytes_remaining - 2048) // 4)

    pss = [nc.alloc_psum_tensor("ps%d" % i, [128, 512], F32) for i in range(8)]
    R_ps = [Reg("ps%d" % i, psum=True) for i in range(8)]

    state = {"alt": 0}

    def alt():
        state["alt"] ^= 1
        return "act" if state["alt"] else "dve"

    def copy_on(e, out, in_, reads, writes):
        if e == "act":
            P.op("act", lambda g: g.activation(out=out, in_=in_, func=AF.Copy), reads=reads, writes=writes)
        else:
            P.op(e, lambda g: g.tensor_copy(out, in_), reads=reads, writes=writes)

    def mm(out, lhsT, rhs, start, stop, reads, w):
        P.op("pe", lambda g: g.matmul(out, lhsT=lhsT, rhs=rhs, start=start, stop=stop),
             reads=reads, writes=[w], pe_accum=not start)

    def new_phase(name="ph"):
        P.barrier()
        ar.reset()
        if scopes:
            if state.get("scope") is not None:
                nc.leave_named_scope(state["scope"][0], state["scope"][1], False)
            nm = "%s_%d" % (name, state.setdefault("nscope", 0))
            state["nscope"] += 1
            sid, _ = nc.enter_named_scope(nm, False)
            state["scope"] = (nm, sid)

    ident_b = cm_b[:, CI, :]
    ones_b = cm_b[:, CO, :]
    ident_f = cm_f[:, CI, :]

    def rstd_part(srcs, n, denom, tmps, psi):
        (sq, R_sq), (t1, R_t1), (rinv, R_rinv) = tmps
        for i, (sap, sreg) in enumerate(srcs):
            P.op("act", lambda g: g.activation(out=sq[:, i, 0:n], in_=sap, func=AF.Square), reads=[sreg], writes=[R_sq])
        for i in range(len(srcs)):
            mm(pss[psi][:, 0:n], ones_b, sq[:, i, 0:n], i == 0, i == len(srcs) - 1, [R_sq, R_cmb], R_ps[psi])
        P.op("act", lambda g: g.activation(out=t1[:, 0:n], in_=pss[psi][:, 0:n], func=AF.Sqrt, scale=1.0 / denom, bias=eps_ap),
             reads=[R_ps[psi], R_misc], writes=[R_t1])
        P.op("dve", lambda g: g.reciprocal(rinv[:, 0:n], t1[:, 0:n]), reads=[R_t1], writes=[R_rinv])
        return rinv, R_rinv

    P.dma("sp", small[:], small_d, writes=[R_small])
    P.dma("sp", cm_f[:], cmat_d.rearrange("p (a b) -> p a b", b=128), writes=[R_cmf])
    P.dma("sp", selm[:], selm_d.rearrange("p (a b) -> p a b", b=128), writes=[R_selm])
    P.dma("pool", rmask[:], rmask_d, writes=[R_rmask])
    P.op("dve", lambda g: g.tensor_copy(cm_b[:], cm_f[:]), reads=[R_cmf], writes=[R_cmb])
    eps_ap = misc[:, 0:1]
    P.op("dve", lambda g: g.memset(misc[:], 0.0), writes=[R_misc])
    P.op("dve", lambda g: g.memset(misc[:, 0:1], EPS), reads=[], writes=[R_misc])
    P.op("dve", lambda g: g.memset(misc[:, 1:2], 1.0), reads=[], writes=[R_misc])
    one_ap = misc[:, 1:2]

    posi, R_posi = ar.alloc([128, S], I32, "posi")
    y, R_y = ar.alloc([128, S], F32, "y")
    yi, R_yi = ar.alloc([128, S], I32, "yi")
    yf, R_yf = ar.alloc([128, S], F32, "yf")
    fr, R_fr = ar.alloc([128, S], F32, "fr")
    m1, R_m1 = ar.alloc([128, S], F32, "m1")
    P.dma("sp", posi, pos_d.to_broadcast([128, S]), writes=[R_posi])
    P.op("dve", lambda g: g.tensor_copy(y, posi), reads=[R_posi], writes=[R_y])
    P.op("dve", lambda g: g.tensor_scalar(y, y, small[:, 16:17], None, ALU.mult), reads=[R_y, R_small], writes=[R_y])
    P.op("dve", lambda g: g.tensor_copy(yi, y), reads=[R_y], writes=[R_yi])
    P.op("dve", lambda g: g.tensor_copy(yf, yi), reads=[R_yi], writes=[R_yf])
    P.op("dve", lambda g: g.tensor_tensor(fr, y, yf, ALU.subtract), reads=[R_y, R_yf], writes=[R_fr])
    for which, dst, R_dst in ((0, sinT, R_sin), (1, cosT, R_cos)):
        src = fr
        if which == 1:
            P.op("dve", lambda g: g.tensor_scalar(y, fr, 0.25, None, ALU.add), reads=[R_fr], writes=[R_y])
            src = y
        R_src = R_fr if which == 0 else R_y
        P.op("dve", lambda g: g.tensor_scalar(m1, src, 0.5, None, ALU.is_gt), reads=[R_src], writes=[R_m1])
        P.op("dve", lambda g: g.tensor_tensor(yf, src, m1, ALU.subtract), reads=[R_src, R_m1], writes=[R_yf])
        P.op("dve", lambda g: g.tensor_scalar(m1, yf, -0.5, None, ALU.is_lt), reads=[R_yf], writes=[R_m1])
        P.op("dve", lambda g: g.tensor_tensor(yf, yf, m1, ALU.add), reads=[R_yf, R_m1], writes=[R_yf])
        P.op("act", lambda g: g.activation(out=dst[:], in_=yf, func=AF.Sin, scale=2.0 * math.pi), reads=[R_yf], writes=[R_dst])

    if stop == "p0":
        P.barrier(); return nc
    def proj_fm(l, gi, M, wbufs, consume, bankset):
        wb, R_wb = wbufs[state.setdefault("wfm_i", 0) % len(wbufs)]
        state["wfm_i"] += 1
        P.dma("pool", wb, winfm_d[l, gi].rearrange("p (a b) -> p a b", b=128), writes=[R_wb])
        banks = [bankset * 4 + i for i in range(4)]
        for kc in range(16):
            for tb in range(4):
                mm(pss[banks[tb]][0:M, :], wb[:, kc, 0:M], big[:, kc, tb * 512:(tb + 1) * 512], kc == 0, kc == 15,
                   [R_wb, R_big], R_ps[banks[tb]])
        for tb in range(4):
            consume(tb, pss[banks[tb]][0:M, :], R_ps[banks[tb]])

    def proj_tm(l, gi, wtb, consume, banks):
        wb, R_wb = wtb
        P.dma("pool", wb, wintm_d[l, gi].rearrange("p (a b) -> p a b", b=256), writes=[R_wb])
        for t in range(NT):
            bk = banks[t % len(banks)]
            for kc in range(16):
                mm(pss[bk][:, 0:256], big[:, kc, t * 128:(t + 1) * 128], wb[:, kc, :], kc == 0, kc == 15,
                   [R_wb, R_big], R_ps[bk])
            consume(t, pss[bk][:, 0:256], R_ps[bk])

    def post_norm_store(l, o_acc, R_oacc, nsub, wcols, szs, R_sz, fc0, tmps, denom, psi):
        (sq, R_sq), (t1, R_t1), (rinv, R_rinv), (u, R_u) = tmps[0:4]
        for blk in range(4):
            sl = slice(blk * 512, (blk + 1) * 512)
            rstd_part([(o_acc[:, j, sl], R_oacc) for j in range(nsub)], 512, denom, tmps[0:3], psi)
            for j in range(nsub):
                P.op("dve", lambda g: g.scalar_tensor_tensor(out=u[:, 0:512], in0=o_acc[:, j, sl], scalar=wcols[j], in1=rinv[:, 0:512],
                                                             op0=ALU.mult, op1=ALU.mult),
                     reads=[R_oacc, R_rinv, R_small, R_misc], writes=[R_u])
                ost, R_ost = tmps[4 + state.setdefault("ost_i", 0) % 2]
                state["ost_i"] += 1
                P.op("pool", lambda g: g.tensor_tensor(ost[:, 0:512], u[:, 0:512], szs[:, j, sl], ALU.mult),
                     reads=[R_u, R_sz], writes=[R_ost])
                P.dma("sp", oT_d[fc0 + j, :, sl], ost[:, 0:512], reads=[R_ost], writes=[R_oT], nowaw=True)

    for l in range(L):
        sb_l = SBASE + l * SLW
        lam_init = 0.8 - 0.6 * math.exp(-0.3 * l)

        def sc(off, n=1):
            return small[:, sb_l + off: sb_l + off + n]

        new_phase("A")
        cact, R_cact = ar.alloc([128, 16], F32, "cact")
        c2, R_c2 = ar.alloc([128, 16, 2], F32, "c2")
        crep, R_crep = ar.alloc([128, 16, 128], F32, "crep")
        bg, R_bg = ar.alloc([128, D], F32, "bg")
        wab = [ar.alloc([128, 16, 128], F32, "wa%d" % i) for i in range(3)]
        P.op("act", lambda g: g.activation(out=cact, in_=small[:, 0:16], func=AF.Silu), reads=[R_small], writes=[R_cact])
        P.op("dve", lambda g: g.tensor_copy(c2, cact.unsqueeze(2).to_broadcast([128, 16, 2])), reads=[R_cact], writes=[R_c2])
        P.op("dve", lambda g: g.tensor_copy(crep, cact.unsqueeze(2).to_broadcast([128, 16, 128])), reads=[R_cact], writes=[R_crep])
        P.dma("sp", bg, bgate_d[l].to_broadcast([128, D]), writes=[R_bg])
        for g_ in range(48):
            wa, R_wa = wab[g_ % 3]
            P.dma("sp", wa, wada_d[l, g_].rearrange("p (a b) -> p a b", b=128), writes=[R_wa])
            if g_ < 32:
                for kc in range(16):
                    mm(pss[0][:, 2 * g_:2 * g_ + 2], wa[:, kc, :], c2[:, kc, :], kc == 0, kc == 15, [R_wa, R_c2], R_ps[0])
                if g_ == 31:
                    P.op("dve", lambda g: g.tensor_tensor(modc[:, 0:32], pss[0][:, 0:64:2], sc(16, 32), ALU.add),
                         reads=[R_ps[0], R_small], writes=[R_modc])
                    P.op("dve", lambda g: g.scalar_tensor_tensor(out=modc[:, 32:48], in0=modc[:, 16:32], scalar=1.0, in1=sc(0, 16),
                                                                 op0=ALU.add, op1=ALU.mult),
                         reads=[R_modc, R_small], writes=[R_modc])
            else:
                gg = g_ - 32
                bk = 1 + (gg // 4) % 2
                c0 = (gg % 4) * 128
                for kc in range(16):
                    mm(pss[bk][:, c0:c0 + 128], crep[:, kc, :], wa[:, kc, :], kc == 0, kc == 15, [R_wa, R_crep], R_ps[bk])
                if gg % 4 == 3:
                    sl = slice((gg // 4) * 512, (gg // 4 + 1) * 512)
                    P.op("dve", lambda g: g.tensor_tensor(gate_bc[:, sl], pss[bk][:], bg[:, sl], ALU.add),
                         reads=[R_ps[bk], R_bg], writes=[R_gate])

        if stop == "pA":
            P.barrier(); return nc
        new_phase("B")
        xsrc = x_d if l == 0 else x1_d
        xts = [ar.alloc([128, D], F32, "xt%d" % i) for i in range(2)]
        xn, R_xn = ar.alloc([128, 4, D], BF16, "xn")
        junk, R_junk = ar.alloc([128, D], BF16, "junk")
        ssq, R_ssq = ar.alloc([128, 8], F32, "ssq")
        hT = big
        for tb in range(4):
            for i in range(4):
                t = tb * 4 + i
                xt, R_xt = xts[t % 2]
                P.dma("sp", xt, xsrc[t * 128:(t + 1) * 128, :], reads=[R_x1] if l > 0 else [], writes=[R_xt])
                P.op("act", lambda g: g.activation(out=junk, in_=xt, func=AF.Square, accum_out=ssq[:, 0:1]),
                     reads=[R_xt], writes=[R_junk, R_ssq])
                P.op("act", lambda g: g.activation(out=ssq[:, 1:2], in_=ssq[:, 0:1], func=AF.Sqrt, scale=1.0 / D, bias=eps_ap),
                     reads=[R_ssq, R_misc], writes=[R_ssq])
                P.op("dve", lambda g: g.reciprocal(ssq[:, 2:3], ssq[:, 1:2]), reads=[R_ssq], writes=[R_ssq])
                P.op("dve", lambda g: g.tensor_scalar(xn[:, i, :], xt, ssq[:, 2:3], None, ALU.mult),
                     reads=[R_xt, R_ssq], writes=[R_xn])
            for fc in range(16):
                bk = fc % 4
                pT = pss[bk][:].bitcast(BF16)
                for i in range(4):
                    P.op("pe", lambda g: g.transpose(pT[:, i * 128:(i + 1) * 128], xn[:, i, fc * 128:(fc + 1) * 128], ident_b),
                         reads=[R_xn, R_cmb], writes=[R_ps[bk]], pe_accum=i > 0)
                dst = hT[:, fc, tb * 512:(tb + 1) * 512]
                if alt() == "act":
                    P.op("act", lambda g: g.activation(out=dst, in_=pT[:, 0:512], func=AF.Identity,
                                                       scale=modc[:, 32 + fc:33 + fc], bias=modc[:, fc:fc + 1]),
                         reads=[R_ps[bk], R_modc], writes=[R_big])
                else:
                    P.op("dve", lambda g: g.tensor_scalar(dst, pT[:, 0:512], modc[:, 32 + fc:33 + fc], modc[:, fc:fc + 1],
                                                          ALU.mult, ALU.add),
                         reads=[R_ps[bk], R_modc], writes=[R_big])

        if stop == "pB":
            P.barrier(); return nc
        for pr in range(2):
            new_phase("gla")
            wfm = [ar.alloc([128, 16, 128], BF16, "wfm%d" % i) for i in range(3)]
            wtm = ar.alloc([128, 16, 256], BF16, "wtm")
            glrT, R_glrT = ar.alloc([16, S], BF16, "glrT")
            wlr, R_wlr = ar.alloc([16, 256], BF16, "wlr")
            bcs, R_bcs = ar.alloc([128, S], F32, "bcs")
            eb, R_eb = ar.alloc([128, S], F32, "eb")
            q_eT, R_qe = ar.alloc([128, S], BF16, "q_eT")
            k_eT, R_ke = ar.alloc([128, S], BF16, "k_eT")
            v_g, R_vg = ar.alloc([128, NT, 256], BF16, "v_g")
            sz, R_sz = ar.alloc([128, 2, S], BF16, "sz")
            ke_tok, R_ket = ar.alloc([128, NT, 128], BF16, "ke_tok")
            o_acc, R_oacc = ar.alloc([128, 2, S], F32, "o_acc")
            e1, R_e1 = ar.alloc([128, 512], F32, "e1")
            dec, R_dec = ar.alloc([128, 16], F32, "dec")
            nb, R_nb = ar.alloc([128, 2], F32, "nb")
            Sp, R_Sp = ar.alloc([128, 256], F32, "Sp")
            Stmp, R_Stmp = ar.alloc([128, 256], F32, "Stmp")
            Sbf, R_Sbf = ar.alloc([128, 256], BF16, "Sbf")
            attm, R_attm = ar.alloc([128, 2, 128], BF16, "attm")
            tmps = [ar.alloc([128, 2, 512], BF16, "sq"), ar.alloc([128, 512], F32, "t1"), ar.alloc([128, 512], F32, "rinv"),
                    ar.alloc([128, 512], F32, "u"), ar.alloc([128, 512], BF16, "ost"), ar.alloc([128, 512], BF16, "ost2")]

            def c_glr(tb, ps, R):
                copy_on("act", glrT[0:16, tb * 512:(tb + 1) * 512], ps, [R], [R_glrT])
            proj_fm(l, 4, 16, wfm, c_glr, 0)
            P.op("dve", lambda g: g.tensor_copy(wlr, sc(624, 256)[0:16, :]), reads=[R_small], writes=[R_wlr])
            P.op("dve", lambda g: g.tensor_scalar(nb, sc(48, 2), -1.0, None, ALU.mult), reads=[R_small], writes=[R_nb])
            for blk in range(4):
                sl = slice(blk * 512, (blk + 1) * 512)
                bk = 4 + blk % 2
                mm(pss[bk][:], wlr[0:16, pr * 128:(pr + 1) * 128], glrT[0:16, sl], True, True, [R_wlr, R_glrT], R_ps[bk])
                P.op("act", lambda g: g.activation(out=e1, in_=pss[bk][:], func=AF.Exp, scale=-1.0, bias=nb[:, pr:pr + 1]),
                     reads=[R_ps[bk], R_nb], writes=[R_e1])
                P.op("act", lambda g: g.activation(out=bcs[:, sl], in_=e1, func=AF.Ln, bias=one_ap), reads=[R_e1, R_misc], writes=[R_bcs])
            P.op("dve", lambda g: g.tensor_tensor_scan(out=bcs, data0=rmask[:], data1=bcs, initial=0.0, op0=ALU.mult, op1=ALU.add),
                 reads=[R_rmask, R_bcs], writes=[R_bcs])
            P.op("act", lambda g: g.activation(out=eb, in_=bcs, func=AF.Exp, scale=-1.0 / 16.0), reads=[R_bcs], writes=[R_eb])
            P.op("dve", lambda g: g.tensor_copy(dec, eb[:, 127:S:128]), reads=[R_eb], writes=[R_dec])
            P.op("act", lambda g: g.activation(out=bcs, in_=bcs, func=AF.Exp, scale=1.0 / 16.0), reads=[R_bcs], writes=[R_bcs])
            enb = bcs

            def c_q(tb, ps, R):
                sl = slice(tb * 512, (tb + 1) * 512)
                P.op("dve", lambda g: g.scalar_tensor_tensor(out=q_eT[:, sl], in0=ps, scalar=0.125, in1=eb[:, sl], op0=ALU.mult, op1=ALU.mult),
                     reads=[R, R_eb], writes=[R_qe])
            proj_fm(l, pr, 128, wfm, c_q, 1)

            def c_k(tb, ps, R):
                sl = slice(tb * 512, (tb + 1) * 512)
                P.op("dve", lambda g: g.tensor_tensor(k_eT[:, sl], ps, enb[:, sl], ALU.mult), reads=[R, R_bcs], writes=[R_ke])
            proj_fm(l, 2 + pr, 128, wfm, c_k, 0)

            def c_v(t, ps, R):
                copy_on("act", v_g[:, t, :], ps, [R], [R_vg])
            proj_tm(l, pr, wtm, c_v, [4, 5])
            for hh in range(2):
                def c_z(tb, ps, R):
                    P.op("act", lambda g: g.activation(out=sz[:, hh, tb * 512:(tb + 1) * 512], in_=ps, func=AF.Silu), reads=[R], writes=[R_sz])
                proj_fm(l, 5 + 2 * pr + hh, 128, wfm, c_z, hh)
            for t4 in range(4):
                bk = 6 + t4 % 2
                pT = pss[bk][:].bitcast(BF16)
                for i in range(4):
                    t = t4 * 4 + i
                    P.op("pe", lambda g: g.transpose(pT[:, i * 128:(i + 1) * 128], k_eT[:, t * 128:(t + 1) * 128], ident_b),
                         reads=[R_ke, R_cmb], writes=[R_ps[bk]], pe_accum=i > 0)
                P.op("dve", lambda g: g.tensor_copy(ke_tok[:, t4 * 4:(t4 + 1) * 4, :], pT[:, 0:512].rearrange("p (a b) -> p a b", b=128)),
                     reads=[R_ps[bk]], writes=[R_ket])
            P.op("dve", lambda g: g.memset(Sp, 0.0), writes=[R_Sp])
            for n in range(NT):
                ch = slice(n * 128, (n + 1) * 128)
                ba, bo, bkv = n % 2, 2 + n % 2, 4 + n % 2
                for hh in range(2):
                    hp = slice(64 * hh, 64 * hh + 64)
                    mm(pss[ba][:, hh * 128:(hh + 1) * 128], k_eT[hp, ch], q_eT[hp, ch], True, True, [R_ke, R_qe], R_ps[ba])
                P.op("dve", lambda g: g.tensor_tensor(attm, pss[ba][:, 0:256].rearrange("p (a b) -> p a b", b=128),
                                                      cm_f[:, CTU:CTU + 1, :].to_broadcast([128, 2, 128]), ALU.mult),
                     reads=[R_ps[ba], R_cmf], writes=[R_attm])
                for hh in range(2):
                    hp = slice(64 * hh, 64 * hh + 64)
                    vs = slice(hh * 128, (hh + 1) * 128)
                    mm(pss[bo][:, vs], v_g[:, n, vs], attm[:, hh, :], True, n == 0, [R_vg, R_attm], R_ps[bo])
                    if n > 0:
                        mm(pss[bo][:, vs], Sbf[hp, vs], q_eT[hp, ch], False, True, [R_Sbf, R_qe], R_ps[bo])
                P.op("act", lambda g: g.activation(out=o_acc[:, :, ch], in_=pss[bo][:, 0:256].rearrange("p (a b) -> p a b", b=128), func=AF.Copy),
                     reads=[R_ps[bo]], writes=[R_oacc])
                if n < NT - 1:
                    mm(pss[bkv][:, 0:256], ke_tok[:, n, :], v_g[:, n, :], True, True, [R_ket, R_vg], R_ps[bkv])
                    P.op("dve", lambda g: g.tensor_tensor(Stmp, Sp, pss[bkv][:, 0:256], ALU.add), reads=[R_Sp, R_ps[bkv]], writes=[R_Stmp])
                    P.op("dve", lambda g: g.tensor_scalar(Sp, Stmp, dec[:, n:n + 1], None, ALU.mult), reads=[R_Stmp, R_dec], writes=[R_Sp])
                    P.op("pool", lambda g: g.tensor_scalar(Sbf, Stmp, dec[:, n:n + 1], None, ALU.mult), reads=[R_Stmp, R_dec], writes=[R_Sbf])
            for hh in range(2):
                post_norm_store(l, o_acc[:, hh:hh + 1, :], R_oacc, 1, [sc(50)], sz[:, hh:hh + 1, :], R_sz, 2 * pr + hh, tmps, 128.0, 6)

        if stop == "pC1":
            P.barrier(); return nc
        for h in range(4):
            new_phase("gdn")
            wfm = [ar.alloc([128, 16, 128], BF16, "wfm%d" % i) for i in range(2)]
            xbf, R_xbf = ar.alloc([128, S + 8], BF16, "xbf")
            diag, R_diag = ar.alloc([128, 4, 128], BF16, "diag")
            cs, R_cs = ar.alloc([128, S], F32, "cs")
            knT, R_kn = ar.alloc([128, S], BF16, "knT")
            qnT, R_qn = ar.alloc([128, S], BF16, "qnT")
            cvT, R_cv = ar.alloc([128, S], BF16, "cvT")
            kbT, R_kb = ar.alloc([128, S], BF16, "kbT")
            q_eT, R_qe = ar.alloc([128, S], BF16, "q_eT")
            vb_tok, R_vb = ar.alloc([128, NT, 128], BF16, "vb_tok")
            kbg_tok, R_kbg = ar.alloc([128, NT, 128], BF16, "kbg_tok")
            kt_tok, R_kt = ar.alloc([128, NT, 128], BF16, "kt_tok")
            gc, R_gc = ar.alloc([128, S], F32, "gc")
            beta, R_beta = ar.alloc([128, S], BF16, "beta")
            eg, R_eg = ar.alloc([128, S], F32, "eg")
            tl, R_tl = ar.alloc([128, S], F32, "tl")
            dabT, R_dab = tl[0:8, :], R_tl
            cols, R_cols = ar.alloc([128, 6, 16], F32, "cols")
            nA, R_nA = ar.alloc([128, 4], F32, "nA")
            Sp, R_Sp = ar.alloc([128, 128], F32, "Sp")
            Sbf, R_Sbf = ar.alloc([128, 128], BF16, "Sbf")
            dm = [ar.alloc([128, 128], F32, "dm%d" % i) for i in range(4)]
            mb = [ar.alloc([128, 128], BF16, "mb%d" % i) for i in range(10)]
            u_sb, R_u = ar.alloc([128, 128], F32, "u_sb")
            tmps = [ar.alloc([128, 2, 512], BF16, "sq"), ar.alloc([128, 512], F32, "t1"), ar.alloc([128, 512], F32, "rinv"),
                    ar.alloc([128, 512], F32, "u"), ar.alloc([128, 512], BF16, "ost"), ar.alloc([128, 512], BF16, "ost2")]
            e1, R_e1 = tmps[3]
            o_acc, R_oacc = cs.rearrange("p (a b) -> p a b", a=1), R_cs

            def c_dab(tb, ps, R):
                copy_on("act", dabT[0:8, tb * 512:(tb + 1) * 512], ps, [R], [R_dab])
            proj_fm(l, 21, 8, wfm, c_dab, 0)
            P.op("dve", lambda g: g.memset(xbf[:, 0:3], 0.0), writes=[R_xbf])
            for which in range(3):
                ti = which * 4 + h
                for j in range(4):
                    P.op("dve", lambda g: g.tensor_scalar(diag[:, j, :], ident_f, sc(64 + ti * 4 + j), None, ALU.mult),
                         reads=[R_cmf, R_small], writes=[R_diag])

                def c_x(tb, ps, R):
                    copy_on(alt(), xbf[:, 3 + tb * 512:3 + (tb + 1) * 512], ps, [R], [R_xbf])
                proj_fm(l, 9 + 4 * which + h, 128, wfm, c_x, 1)
                for blk in range(4):
                    sl = slice(blk * 512, (blk + 1) * 512)
                    bk = blk % 2
                    for j in range(4):
                        mm(pss[bk][:], diag[:, j, :], xbf[:, blk * 512 + j: blk * 512 + j + 512], j == 0, j == 3, [R_diag, R_xbf], R_ps[bk])
                    if which == 2:
                        P.op("act", lambda g: g.activation(out=cvT[:, sl], in_=pss[bk][:], func=AF.Silu), reads=[R_ps[bk]], writes=[R_cv])
                    else:
                        P.op("act", lambda g: g.activation(out=cs[:, sl], in_=pss[bk][:], func=AF.Silu), reads=[R_ps[bk]], writes=[R_cs])
                if which < 2:
                    for blk in range(4):
                        sl = slice(blk * 512, (blk + 1) * 512)
                        rinv, R_rinv = rstd_part([(cs[:, sl], R_cs)], 512, 1.0, tmps[0:3], 2 + blk % 2)
                        if which == 0:
                            P.op("dve", lambda g: g.scalar_tensor_tensor(out=qnT[:, sl], in0=cs[:, sl], scalar=128.0 ** -0.5, in1=rinv[:, 0:512],
                                                                         op0=ALU.mult, op1=ALU.mult), reads=[R_cs, R_rinv], writes=[R_qn])
                        else:
                            P.op("dve", lambda g: g.tensor_tensor(knT[:, sl], cs[:, sl], rinv[:, 0:512], ALU.mult), reads=[R_cs, R_rinv], writes=[R_kn])
            if stop == "g1":
                P.barrier(); return nc
            P.op("act", lambda g: g.activation(out=nA, in_=sc(56, 4), func=AF.Exp), reads=[R_small], writes=[R_nA])
            P.op("dve", lambda g: g.tensor_scalar(nA, nA, -1.0, None, ALU.mult), reads=[R_nA], writes=[R_nA])
            for blk in range(4):
                sl = slice(blk * 512, (blk + 1) * 512)
                bk = 4 + blk % 2
                mm(pss[bk][:], selm[0:8, h, :], dabT[0:8, sl], True, True, [R_selm, R_dab], R_ps[bk])
                P.op("act", lambda g: g.activation(out=e1, in_=pss[bk][:], func=AF.Exp, bias=sc(60 + h)), reads=[R_ps[bk], R_small], writes=[R_e1])
                P.op("act", lambda g: g.activation(out=gc[:, sl], in_=e1, func=AF.Ln, bias=one_ap), reads=[R_e1, R_misc], writes=[R_gc])
                bk2 = 6 + blk % 2
                mm(pss[bk2][:], selm[0:8, 4 + h, :], dabT[0:8, sl], True, True, [R_selm, R_dab], R_ps[bk2])
                P.op("act", lambda g: g.activation(out=beta[:, sl], in_=pss[bk2][:], func=AF.Sigmoid), reads=[R_ps[bk2]], writes=[R_beta])
            P.op("dve", lambda g: g.tensor_tensor_scan(out=gc, data0=rmask[:], data1=gc, initial=0.0, op0=ALU.mult, op1=ALU.add),
                 reads=[R_rmask, R_gc], writes=[R_gc])
            P.op("dve", lambda g: g.tensor_scalar(gc, gc, nA[:, h:h + 1], None, ALU.mult), reads=[R_gc, R_nA], writes=[R_gc])
            gcl = cols[:, 0, :]
            cd = cols[:, 1, :]
            P.op("dve", lambda g: g.tensor_copy(gcl, gc[:, 127:S:128]), reads=[R_gc], writes=[R_cols])
            P.op("act", lambda g: g.activation(out=cd, in_=gcl, func=AF.Exp), reads=[R_cols], writes=[R_cols])
            P.op("act", lambda g: g.activation(out=eg, in_=gc, func=AF.Exp), reads=[R_gc], writes=[R_eg])
            P.op("dve", lambda g: g.tensor_tensor(q_eT, qnT, eg, ALU.mult), reads=[R_qn, R_eg], writes=[R_qe])
            P.op("dve", lambda g: g.tensor_tensor(kbT, knT, beta, ALU.mult), reads=[R_kn, R_beta], writes=[R_kb])
            P.op("pool", lambda g: g.tensor_tensor(eg, eg, beta, ALU.mult), reads=[R_eg, R_beta], writes=[R_eg])
            for n in range(NT):
                ch = slice(n * 128, (n + 1) * 128)
                P.op("act", lambda g: g.activation(out=tl[:, ch], in_=gc[:, ch], func=AF.Exp, scale=-1.0, bias=gcl[:, n:n + 1]),
                     reads=[R_gc, R_cols], writes=[R_tl])
            if stop == "g2":
                P.barrier(); return nc
            for qi, (src, R_src) in enumerate(((gc, R_gc), (beta, R_beta), (eg, R_eg), (tl, R_tl))):
                oh = cm_b[:, CI, 0:2] if src is beta else cm_f[:, CI, 0:2]
                for n in range(NT):
                    c0 = (qi * NT + n) * 2
                    mm(pss[3][:, c0:c0 + 2], src[:, n * 128:(n + 1) * 128], oh, True, True, [R_src, R_cmf, R_cmb], R_ps[3])
            P.op("dve", lambda g: g.tensor_copy(cols[:, 2:6, :], pss[3][:, 0:128:2].rearrange("p (a b) -> p a b", b=NT)),
                 reads=[R_ps[3]], writes=[R_cols])
            gc_col, beta_col, bexp_col, tail_col = (cols[:, i, :] for i in (2, 3, 4, 5))
            if stop == "g2b":
                P.barrier(); return nc
            for t in range(NT):
                bk = t % 2
                pT = pss[bk][:].bitcast(BF16)
                ts = slice(t * 128, (t + 1) * 128)
                P.op("pe", lambda g: g.transpose(pT[:, 0:128], knT[:, ts], ident_b), reads=[R_kn, R_cmb], writes=[R_ps[bk]])
                P.op("pe", lambda g: g.transpose(pT[:, 128:256], cvT[:, ts], ident_b), reads=[R_cv, R_cmb], writes=[R_ps[bk]], pe_accum=True)
                P.op("dve", lambda g: g.tensor_scalar(kbg_tok[:, t, :], pT[:, 0:128], bexp_col[:, t:t + 1], None, ALU.mult),
                     reads=[R_ps[bk], R_cols], writes=[R_kbg])
                P.op("dve", lambda g: g.tensor_scalar(kt_tok[:, t, :], pT[:, 0:128], tail_col[:, t:t + 1], None, ALU.mult),
                     reads=[R_ps[bk], R_cols], writes=[R_kt])
                P.op("dve", lambda g: g.tensor_scalar(vb_tok[:, t, :], pT[:, 128:256], beta_col[:, t:t + 1], None, ALU.mult),
                     reads=[R_ps[bk], R_cols], writes=[R_vb])

            if stop == "g3":
                P.barrier(); return nc
            P.barrier(reset=False)
            sz, R_sz = tl.bitcast(BF16)[:, 0:S].rearrange("p (a b) -> p a b", a=1), Reg("sz")
            mb2 = [(xbf[:, i * 128:(i + 1) * 128], Reg("mb2_%d" % i)) for i in range(16)]

            def c_z(tb, ps, R):
                P.op("act", lambda g: g.activation(out=sz[:, 0, tb * 512:(tb + 1) * 512], in_=ps, func=AF.Silu), reads=[R], writes=[R_sz])
            proj_fm(l, 22 + h, 128, wfm, c_z, 1)
            if stop == "g4":
                P.barrier(); return nc
            P.barrier(reset=False)
            G = 4
            eg4 = eg.rearrange("p (t g c) -> p t g c", g=G, c=128)
            (dA4, R_dA), (dB4, R_dB), (dD4, R_dD), (u4, R_u4) = [(eg4[:, i], Reg("f4_%d" % i)) for i in range(4)]
            pool16 = []
            for src in (beta, cvT, wfm[0][0].rearrange("p a b -> p (a b)"), wfm[1][0].rearrange("p a b -> p (a b)"),
                        tl.bitcast(BF16)[:, S:2 * S]):
                v4 = src.rearrange("p (t g c) -> p t g c", g=G, c=128)
                pool16 += [(v4[:, i], Reg("b4_%d" % len(pool16))) for i in range(4)]
            ((Pm4, R_P), (PT4, R_PT), (qk4, R_qk), (Pd4, R_Pd), (PTd4, R_PTd), (Po32, R_Po32), (PTo32, R_PTo32), (Po64, R_Po64),
             Abuf0, ATbuf0, Abuf1, ATbuf1, sq0, sqT0, sq1, sqT1, (U1, R_U1), (T1, R_T1), (wT4, R_wT)) = pool16[0:19]
            vnew, R_vn = mb[0]

            def bc4(blk, f32=True):
                src = cm_f if f32 else cm_b
                return src[:, blk:blk + 1, :].to_broadcast([128, G, 128])

            def flat(t4):
                return t4.rearrange("p g c -> p (g c)")

            def p4(bank):
                return pss[bank][:].rearrange("p (g c) -> p g c", c=128)

            P.op("dve", lambda g: g.memset(Sp, 0.0), writes=[R_Sp])
            P.op("dve", lambda g: g.memset(Sbf, 0.0), writes=[R_Sbf])
            for bb in range(NT // G):
                ns = [bb * G + g_ for g_ in range(G)]
                chs = [slice(n * 128, (n + 1) * 128) for n in ns]
                for g_, n in enumerate(ns):
                    gcc = gc_col[:, n:n + 1]
                    P.op("dve", lambda g: g.tensor_scalar(dA4[:, g_, :], gc[:, chs[g_]], gcc, 0.0, ALU.subtract, ALU.max),
                         reads=[R_gc, R_cols], writes=[R_dA])
                    P.op("dve", lambda g: g.tensor_scalar(dB4[:, g_, :], gc[:, chs[g_]], gcc, 0.0, ALU.subtract, ALU.min),
                         reads=[R_gc, R_cols], writes=[R_dB])
                P.op("act", lambda g: g.activation(out=flat(dA4), in_=flat(dA4), func=AF.Exp, scale=-1.0), reads=[R_dA], writes=[R_dA])
                P.op("act", lambda g: g.activation(out=flat(dB4), in_=flat(dB4), func=AF.Exp), reads=[R_dB], writes=[R_dB])
                P.op("dve", lambda g: g.tensor_tensor(dA4, dA4, bc4(CNSL), ALU.mult), reads=[R_dA, R_cmf], writes=[R_dA])
                P.op("dve", lambda g: g.tensor_tensor(dD4, dB4, bc4(CNSU), ALU.mult), reads=[R_dB, R_cmf], writes=[R_dD])
                P.op("dve", lambda g: g.tensor_tensor(dB4, dB4, bc4(CTU), ALU.mult), reads=[R_dB, R_cmf], writes=[R_dB])
                for g_ in range(G):
                    cs_ = slice(g_ * 128, (g_ + 1) * 128)
                    mm(pss[0][:, cs_], kbT[:, chs[g_]], knT[:, chs[g_]], True, True, [R_kb, R_kn], R_ps[0])
                for g_ in range(G):
                    cs_ = slice(g_ * 128, (g_ + 1) * 128)
                    mm(pss[1][:, cs_], knT[:, chs[g_]], kbT[:, chs[g_]], True, True, [R_kb, R_kn], R_ps[1])
                for g_ in range(G):
                    cs_ = slice(g_ * 128, (g_ + 1) * 128)
                    mm(pss[2][:, cs_], knT[:, chs[g_]], qnT[:, chs[g_]], True, True, [R_kn, R_qn], R_ps[2])
                P.op("dve", lambda g: g.tensor_tensor(Pm4, p4(0), dA4, ALU.mult), reads=[R_ps[0], R_dA], writes=[R_P])
                P.op("dve", lambda g: g.tensor_tensor(PT4, p4(1), dD4, ALU.mult), reads=[R_ps[1], R_dD], writes=[R_PT])
                P.op("dve", lambda g: g.tensor_tensor(qk4, p4(2), dB4, ALU.mult), reads=[R_ps[2], R_dB], writes=[R_qk])
                for dst, R_d, src, R_s, mk in ((Pd4, R_Pd, Pm4, R_P, CB32), (PTd4, R_PTd, PT4, R_PT, CB32), (Po32, R_Po32, Pm4, R_P, CO32),
                                               (PTo32, R_PTo32, PT4, R_PT, CO32), (Po64, R_Po64, Pm4, R_P, CO64)):
                    P.op("dve", lambda g: g.tensor_tensor(dst, src, bc4(mk, False), ALU.mult), reads=[R_s, R_cmb], writes=[R_d])
                Acur, ATcur, Anxt, ATnxt = Abuf0, ATbuf0, Abuf1, ATbuf1
                P.op("dve", lambda g: g.tensor_tensor(Acur[0], Pd4, bc4(CI, False), ALU.add), reads=[R_Pd, R_cmb], writes=[Acur[1]])
                P.op("dve", lambda g: g.tensor_tensor(ATcur[0], PTd4, bc4(CI, False), ALU.add), reads=[R_PTd, R_cmb], writes=[ATcur[1]])

                def mm4(bank, lhs4, R_l, rhs4, R_r, start=True):
                    for g_ in range(G):
                        cs_ = slice(g_ * 128, (g_ + 1) * 128)
                        mm(pss[bank][:, cs_], lhs4[:, g_, :], rhs4[:, g_, :], start, start or g_ == G - 1, [R_l, R_r], R_ps[bank])

                def add_mm4(bank, base, lhs4, R_l, rhs4, R_r):
                    mm(pss[bank][:], ident_b, flat(base[0]), True, False, [R_cmb, base[1]], R_ps[bank])
                    mm4(bank, lhs4, R_l, rhs4, R_r, start=False)

                cur = ((Pd4, R_Pd), (PTd4, R_PTd))
                sqb = [(sq0, sqT0), (sq1, sqT1)]
                for lev in range(4):
                    (cP, R_cP), (cPT, R_cPT) = cur
                    (nP, R_nP), (nPT, R_nPT) = sqb[lev % 2]
                    mm4(3, cPT, R_cPT, cP, R_cP)
                    mm4(4, cP, R_cP, cPT, R_cPT)
                    copy_on("act", flat(nP), pss[3][:], [R_ps[3]], [R_nP])
                    copy_on("dve", flat(nPT), pss[4][:], [R_ps[4]], [R_nPT])
                    add_mm4(5, Acur, nPT, R_nPT, Acur[0], Acur[1])
                    add_mm4(6, ATcur, nP, R_nP, ATcur[0], ATcur[1])
                    copy_on("dve", flat(Anxt[0]), pss[5][:], [R_ps[5]], [Anxt[1]])
                    copy_on("act", flat(ATnxt[0]), pss[6][:], [R_ps[6]], [ATnxt[1]])
                    Acur, Anxt = Anxt, Acur
                    ATcur, ATnxt = ATnxt, ATcur
                    cur = ((nP, R_nP), (nPT, R_nPT))
                mm4(3, PTo32, R_PTo32, Acur[0], Acur[1])
                mm4(4, Po32, R_Po32, ATcur[0], ATcur[1])
                copy_on("act", flat(U1), pss[3][:], [R_ps[3]], [R_U1])
                copy_on("dve", flat(T1), pss[4][:], [R_ps[4]], [R_T1])
                add_mm4(5, Acur, ATcur[0], ATcur[1], U1, R_U1)
                add_mm4(6, ATcur, Acur[0], Acur[1], T1, R_T1)
                copy_on("dve", flat(Anxt[0]), pss[5][:], [R_ps[5]], [Anxt[1]])
                copy_on("act", flat(ATnxt[0]), pss[6][:], [R_ps[6]], [ATnxt[1]])
                Acur, Anxt = Anxt, Acur
                ATcur, ATnxt = ATnxt, ATcur
                mm4(4, Po64, R_Po64, ATcur[0], ATcur[1])
                copy_on("dve", flat(T1), pss[4][:], [R_ps[4]], [R_T1])
                add_mm4(6, ATcur, Acur[0], Acur[1], T1, R_T1)
                copy_on("act", flat(ATnxt[0]), pss[6][:], [R_ps[6]], [ATnxt[1]])
                AT4, R_AT = ATnxt
                for g_, n in enumerate(ns):
                    cs_ = slice(g_ * 128, (g_ + 1) * 128)
                    mm(pss[3][:, cs_], AT4[:, g_, :], vb_tok[:, n, :], True, True, [R_AT, R_vb], R_ps[3])
                for g_, n in enumerate(ns):
                    cs_ = slice(g_ * 128, (g_ + 1) * 128)
                    mm(pss[4][:, cs_], kbg_tok[:, n, :], AT4[:, g_, :], True, True, [R_AT, R_kbg], R_ps[4])
                copy_on("act", flat(u4), pss[3][:], [R_ps[3]], [R_u4])
                copy_on("dve", flat(wT4), pss[4][:], [R_ps[4]], [R_wT])
                for g_, n in enumerate(ns):
                    ch = chs[g_]
                    if n > 0:
                        mm(pss[0][:, 0:128], wT4[:, g_, :], Sbf, True, True, [R_wT, R_Sbf], R_ps[0])
                        P.op("dve", lambda g: g.tensor_tensor(vnew, u4[:, g_, :], pss[0][:, 0:128], ALU.subtract),
                             reads=[R_u4, R_ps[0]], writes=[R_vn])
                    else:
                        copy_on("dve", vnew, u4[:, g_, :], [R_u4], [R_vn])
                    if n > 0:
                        mm(pss[7][:, 0:128], Sbf, q_eT[:, ch], True, False, [R_Sbf, R_qe], R_ps[7])
                    mm(pss[7][:, 0:128], vnew, qk4[:, g_, :], n == 0, True, [R_vn, R_qk], R_ps[7])
                    copy_on("act", o_acc[:, 0, ch], pss[7][:, 0:128], [R_ps[7]], [R_oacc])
                    if n < NT - 1:
                        mm(pss[1][:, 0:128], kt_tok[:, n, :], vnew, True, True, [R_kt, R_vn], R_ps[1])
                        P.op("dve", lambda g: g.scalar_tensor_tensor(out=Sp, in0=Sp, scalar=cd[:, n:n + 1], in1=pss[1][:, 0:128],
                                                                     op0=ALU.mult, op1=ALU.add), reads=[R_Sp, R_cols, R_ps[1]], writes=[R_Sp])
                        copy_on("act", Sbf, Sp, [R_Sp], [R_Sbf])
            post_norm_store(l, o_acc, R_oacc, 1, [sc(51)], sz, R_sz, 4 + h, tmps, 128.0, 6)

        if stop == "pC2":
            P.barrier(); return nc
        for h in range(4):
            new_phase("diff")
            wfm = [ar.alloc([128, 16, 128], BF16, "wfm%d" % i) for i in range(3)]
            wtm = ar.alloc([128, 16, 256], BF16, "wtm")
            qT, R_q = ar.alloc([128, 2, S], BF16, "qT")
            kT, R_k = ar.alloc([128, 2, S], BF16, "kT")
            v_sb, R_v = ar.alloc([128, NT, 256], BF16, "v_sb")
            sz, R_sz = ar.alloc([128, 2, S], BF16, "sz")
            ebuf = [ar.alloc([128, 512], BF16, "e%d" % i) for i in range(3)]
            qn, R_qnb = ar.alloc([128, 512], BF16, "qn")
            ta, R_ta = ar.alloc([128, 512], F32, "ta")
            tb_, R_tb = ar.alloc([128, 512], F32, "tb")
            tO, R_tO = ar.alloc([128, 2, 512], F32, "tO")
            rs, R_rs = ar.alloc([128, 512], F32, "rs")
            lamt, R_lam = ar.alloc([128, 8], F32, "lam")
            lprod, R_lprod = ar.alloc([128, 256], F32, "lprod")
            wn2, R_wn2 = ar.alloc([128, 2], F32, "wn2")
            tmps = [ar.alloc([128, 2, 512], BF16, "sq"), ar.alloc([128, 512], F32, "t1"), ar.alloc([128, 512], F32, "rinv"),
                    ar.alloc([128, 512], F32, "u"), ar.alloc([128, 512], BF16, "ost"), ar.alloc([128, 512], BF16, "ost2")]
            P.op("dve", lambda g: g.tensor_tensor(lprod[:, 0:128], sc(112, 128), sc(240, 128), ALU.mult), reads=[R_small], writes=[R_lprod])
            P.op("dve", lambda g: g.tensor_tensor(lprod[:, 128:256], sc(368, 128), sc(496, 128), ALU.mult), reads=[R_small], writes=[R_lprod])
            P.op("dve", lambda g: g.tensor_reduce(lamt[:, 0:2], lprod.rearrange("p (a b) -> p a b", b=128), AX.X, ALU.add),
                 reads=[R_lprod], writes=[R_lam])
            P.op("act", lambda g: g.activation(out=lamt[:, 2:4], in_=lamt[:, 0:2], func=AF.Exp), reads=[R_lam], writes=[R_lam])
            P.op("dve", lambda g: g.scalar_tensor_tensor(out=lamt[:, 4:5], in0=lamt[:, 3:4], scalar=-lam_init, in1=lamt[:, 2:3],
                                                         op0=ALU.add, op1=ALU.subtract), reads=[R_lam], writes=[R_lam])
            nlam = lamt[:, 4:5]
            P.op("dve", lambda g: g.tensor_scalar(wn2, sc(54, 2), 1.0 - lam_init, None, ALU.mult), reads=[R_small], writes=[R_wn2])
            units = [(which, m, half) for which in range(2) for m in range(2) for half in range(2)]
            qk_meta = ((qT, R_q, 26, 52), (kT, R_k, 34, 53))
            ustate = {}

            def qk_issue(ui):
                which, m, half = units[ui]
                g0 = qk_meta[which][2]
                if half == 0:
                    wb, R_wb = wfm[state.setdefault("wfm_i", 0) % len(wfm)]
                    state["wfm_i"] += 1
                    P.dma("pool", wb, winfm_d[l, g0 + 2 * h + m].rearrange("p (a b) -> p a b", b=128), writes=[R_wb])
                    ustate["w"] = (wb, R_wb)
                wb, R_wb = ustate["w"]
                banks = [4 + 2 * (ui % 2), 5 + 2 * (ui % 2)]
                for kc in range(16):
                    for i in range(2):
                        tb = half * 2 + i
                        mm(pss[banks[i]][:], wb[:, kc, :], big[:, kc, tb * 512:(tb + 1) * 512], kc == 0, kc == 15,
                           [R_wb, R_big], R_ps[banks[i]])

            def qk_consume(ui):
                which, m, half = units[ui]
                dstT, R_dst, g0, wcol = qk_meta[which]
                banks = [4 + 2 * (ui % 2), 5 + 2 * (ui % 2)]
                for i in range(2):
                    tb = half * 2 + i
                    ps, R = pss[banks[i]][:], R_ps[banks[i]]
                    sl = slice(tb * 512, (tb + 1) * 512)
                    rinv, R_rinv = rstd_part([(ps, R)], 512, 128.0, tmps[0:3], 2 + tb % 2)
                    P.op("dve", lambda g: g.scalar_tensor_tensor(out=qn, in0=ps, scalar=sc(wcol), in1=rinv[:, 0:512], op0=ALU.mult, op1=ALU.mult),
                         reads=[R, R_rinv, R_small], writes=[R_qnb])
                    bkr = 2 + (tb + 1) % 2
                    mm(pss[bkr][:], cm_b[:, CP, :], qn, True, True, [R_cmb, R_qnb], R_ps[bkr])
                    P.op("pool", lambda g: g.tensor_tensor(ta, qn, cosT[:, sl], ALU.mult), reads=[R_qnb, R_cos], writes=[R_ta])
                    P.op("dve", lambda g: g.tensor_tensor(tb_, pss[bkr][:], sinT[:, sl], ALU.mult), reads=[R_ps[bkr], R_sin], writes=[R_tb])
                    P.op("pool", lambda g: g.tensor_tensor(dstT[:, m, sl], ta, tb_, ALU.add), reads=[R_ta, R_tb], writes=[R_dst])
            qk_issue(0)
            for ui in range(len(units)):
                if ui + 1 < len(units):
                    qk_issue(ui + 1)
                qk_consume(ui)

            def c_v(t, ps, R):
                copy_on(alt(), v_sb[:, t, :], ps, [R], [R_v])
            proj_tm(l, 2 + h, wtm, c_v, [0, 1])
            for j in range(2):
                def c_z(tb, ps, R):
                    P.op("act", lambda g: g.activation(out=sz[:, j, tb * 512:(tb + 1) * 512], in_=ps, func=AF.Silu), reads=[R], writes=[R_sz])
                proj_fm(l, 42 + 2 * h + j, 128, wfm, c_z, j)
            if h == 3:
                for fc in range(16):
                    P.dma("pool", big[:, fc, :], wout_d[l][:, fc * D:(fc + 1) * D], writes=[R_big], nowaw=fc > 0)
            scale = 128.0 ** -0.5
            steps = [(qb, m, kc) for qb in range(4) for m in range(2) for kc in range(4 * (qb + 1))]

            def qk_mm(si):
                qb, m, kc = steps[si]
                col0 = max(kc - 4 * qb, 0) * 128
                bsc = si % 2
                mm(pss[bsc][:, col0:512], kT[:, m, kc * 128:(kc + 1) * 128], qT[:, m, qb * 512 + col0:(qb + 1) * 512], True, True,
                   [R_k, R_q], R_ps[bsc])
            qk_mm(0)
            for si, (qb, m, kc) in enumerate(steps):
                qs = slice(qb * 512, (qb + 1) * 512)
                nk = 4 * (qb + 1)
                bo0, bo1, bs = (2, 3, 4) if m == 0 else (5, 6, 7)
                if si + 1 < len(steps):
                    qk_mm(si + 1)
                bsc = si % 2
                c = kc - 4 * qb
                col0 = max(c, 0) * 128
                e, R_e = ebuf[si % 3]
                P.op("act", lambda g: g.activation(out=e[:, col0:512], in_=pss[bsc][:, col0:512], func=AF.Exp, scale=scale),
                     reads=[R_ps[bsc]], writes=[R_e])
                if c >= 0:
                    P.op("pool", lambda g: g.tensor_tensor(e[:, col0:col0 + 128], e[:, col0:col0 + 128], cm_b[:, CTU, :], ALU.mult),
                         reads=[R_e, R_cmb], writes=[R_e])
                first, last = kc == 0, kc == nk - 1
                mm(pss[bo0][:, col0:512], v_sb[:, kc, 0:128], e[:, col0:512], first, last, [R_v, R_e], R_ps[bo0])
                mm(pss[bo1][:, col0:512], v_sb[:, kc, 128:256], e[:, col0:512], first, last, [R_v, R_e], R_ps[bo1])
                mm(pss[bs][:, col0:512], ones_b, e[:, col0:512], first, last, [R_cmb, R_e], R_ps[bs])
                if not last:
                    continue
                P.op("dve", lambda g: g.reciprocal(rs, pss[bs][:]), reads=[R_ps[bs]], writes=[R_rs])
                for j, bo in enumerate((bo0, bo1)):
                    if m == 0:
                        P.op("dve", lambda g: g.tensor_tensor(tO[:, j, :], pss[bo][:], rs, ALU.mult), reads=[R_ps[bo], R_rs], writes=[R_tO])
                    else:
                        P.op("dve", lambda g: g.scalar_tensor_tensor(out=ta, in0=pss[bo][:], scalar=nlam, in1=rs, op0=ALU.mult, op1=ALU.mult),
                             reads=[R_ps[bo], R_rs, R_lam], writes=[R_ta])
                        P.op("pool", lambda g: g.tensor_tensor(tO[:, j, :], tO[:, j, :], ta, ALU.add), reads=[R_tO, R_ta], writes=[R_tO])
                if m == 0:
                    continue
                (sq, R_sq), (t1, R_t1), (rinv, R_rinv), (u, R_u) = tmps[0:4]
                rstd_part([(tO[:, j, :], R_tO) for j in range(2)], 512, 256.0, tmps[0:3], 4)
                for j in range(2):
                    ost, R_ost = tmps[4 + j]
                    P.op("dve", lambda g: g.scalar_tensor_tensor(out=u, in0=tO[:, j, :], scalar=wn2[:, j:j + 1], in1=rinv, op0=ALU.mult, op1=ALU.mult),
                         reads=[R_tO, R_rinv, R_wn2], writes=[R_u])
                    P.op("pool", lambda g: g.tensor_tensor(ost, u, sz[:, j, qs], ALU.mult), reads=[R_u, R_sz], writes=[R_ost])
                    P.dma("sp", oT_d[8 + 2 * h + j, :, qs], ost, reads=[R_ost], writes=[R_oT], nowaw=True)

        if stop == "pC3":
            P.barrier(); return nc
        new_phase("D")
        wo = big
        xts = [ar.alloc([128, D], F32, "xt%d" % i) for i in range(2)]
        obs = [ar.alloc([128, 16, 512], BF16, "ob%d" % i) for i in range(2)]
        xos = [ar.alloc([128, D], F32, "xo%d" % i) for i in range(2)]
        ytmp, R_yt = ar.alloc([128, 512], F32, "ytmp")
        xsrc = x_d if l == 0 else x1_d
        xdst = out_d if l == L - 1 else x1_d
        R_dst = R_out if l == L - 1 else R_x1
        for tb in range(4):
            ob, R_ob = obs[tb % 2]
            P.dma("sp", ob, oT_d[:, :, tb * 512:(tb + 1) * 512].rearrange("c p t -> p c t"), reads=[R_oT], writes=[R_ob])
            for i in range(4):
                t = tb * 4 + i
                xt, R_xt = xts[t % 2]
                xo, R_xo = xos[t % 2]
                P.dma("sp", xt, xsrc[t * 128:(t + 1) * 128, :], reads=[R_x1] if l > 0 else [], writes=[R_xt])
                for ng in range(4):
                    ns = slice(ng * 512, (ng + 1) * 512)
                    bk = (t * 4 + ng) % 4
                    for fc in range(16):
                        mm(pss[bk][:], ob[:, fc, i * 128:(i + 1) * 128], wo[:, fc, ns], fc == 0, fc == 15, [R_ob, R_big], R_ps[bk])
                    P.op("dve", lambda g: g.tensor_tensor(ytmp, pss[bk][:], gate_bc[:, ns], ALU.mult), reads=[R_ps[bk], R_gate], writes=[R_yt])
                    P.op("pool", lambda g: g.tensor_tensor(xo[:, ns], ytmp, xt[:, ns], ALU.add), reads=[R_yt, R_xt], writes=[R_xo])
                P.dma("sp", xdst[t * 128:(t + 1) * 128, :], xo, reads=[R_xo], writes=[R_dst], nowaw=True)

    P.barrier()
    if scopes and state.get("scope") is not None:
        nc.leave_named_scope(state["scope"][0], state["scope"][1], False)
    return nc


def _col(v):
    return np.ascontiguousarray(np.asarray(v, np.float32).reshape(-1, 128).T)


def _consts():
    p = np.arange(128)[:, None]
    j = np.arange(128)[None, :]
    cm = np.zeros((128, NCM, 128), np.float32)
    cm[:, CI] = (p == j)
    cm[:, CO] = 1.0
    prot = np.zeros((128, 128), np.float32)
    prot[(j[0, :64] + 64), j[0, :64]] = -1.0
    prot[(j[0, 64:] - 64), j[0, 64:]] = 1.0
    cm[:, CP] = prot
    cm[:, CTU] = (p <= j)
    cm[:, CNSU] = -1.0 * (p < j)
    cm[:, CNSL] = -1.0 * (p > j)
    bd32 = (p // 32 == j // 32)
    bd64 = (p // 64 == j // 64)
    cm[:, CB32] = bd32
    cm[:, CO32] = bd64 & ~bd32
    cm[:, CO64] = ~bd64
    selm = np.zeros((8, 8, 128), np.float32)
    for k in range(8):
        selm[k, k, :] = 1.0
    rmask = np.ones((128, S), np.float32)
    rmask[:, 0::128] = 0.0
    half = 64
    inv_freq = (10000.0 ** (-(np.arange(half, dtype=np.float32) / np.float32(half)))).astype(np.float32)
    invf = np.concatenate([inv_freq, inv_freq]).astype(np.float64) / (2.0 * math.pi)
    return cm.reshape(128, NCM * 128), selm.reshape(8, 8 * 128), rmask, invf.astype(np.float32)


FM_GROUPS = ([(0, 128), (128, 128), (256, 128), (384, 128), (1024, 16)] + [(1040 + 128 * i, 128) for i in range(4)]
             + [(1552 + 128 * i, 128) for i in range(12)] + [(3088, 8)] + [(3096 + 128 * i, 128) for i in range(4)]
             + [(3608 + 128 * i, 128) for i in range(8)] + [(4632 + 128 * i, 128) for i in range(8)]
             + [(6680 + 128 * i, 128) for i in range(8)])
TM_GROUPS = [(512, 256), (768, 256)] + [(5656 + 256 * i, 256) for i in range(4)]


def _prep_shared(inp):
    f = lambda k: np.asarray(inp[k], np.float32)
    cm, selm, rmask, invf = _consts()
    w_in = f("w_in")
    winfm = np.zeros((2, 50, 128, 16, 128), np.float32)
    wintm = np.zeros((2, 6, 128, 16, 256), np.float32)
    for l in range(2):
        wl = w_in[l].reshape(16, 128, -1)
        for gi, (c0, n) in enumerate(FM_GROUPS):
            winfm[l, gi, :, :, :n] = wl[:, :, c0:c0 + n].transpose(1, 0, 2)
        for gi, (c0, n) in enumerate(TM_GROUPS):
            wintm[l, gi] = wl[:, :, c0:c0 + n].transpose(1, 0, 2)
    wada = np.ascontiguousarray(f("w_ada").reshape(2, 16, 128, 48, 128).transpose(0, 3, 2, 1, 4)).reshape(2, 48, 128, 16 * 128)
    wout = np.ascontiguousarray(f("w_out").reshape(2, 16, 128, D).transpose(0, 2, 1, 3)).reshape(2, 128, 16 * D)
    bgate = np.ascontiguousarray(f("b_ada")[:, 2 * D:].reshape(2, 1, D))
    sm = np.zeros((128, NS), np.float32)
    sm[:, 16] = invf
    for l in range(2):
        b = SBASE + l * SLW
        sm[:, b:b + 16] = _col(f("norm_w")[l])
        sm[:, b + 16:b + 48] = _col(f("b_ada")[l, :2 * D])
        sm[:, b + 48:b + 50] = _col(f("gla_b_lr")[l])
        sm[:, b + 50] = f("gla_norm_w")[l]
        sm[:, b + 51] = f("gdn_norm_w")[l]
        sm[:, b + 52] = f("diff_q_norm_w")[l]
        sm[:, b + 53] = f("diff_k_norm_w")[l]
        sm[:, b + 54:b + 56] = _col(f("diff_norm_w")[l])
        sm[:, b + 56:b + 60] = f("gdn_a_log")[l][None, :]
        sm[:, b + 60:b + 64] = f("gdn_dt_bias")[l][None, :]
        cw = f("gdn_conv_w")[l]
        sm[:, b + 64:b + 112] = cw.reshape(4, 12, 128).transpose(2, 1, 0).reshape(128, 48)
        sm[:, b + 112:b + 624] = f("diff_lambda")[l].reshape(1, 512)
        sm[0:16, b + 624:b + 880] = f("gla_w_lr")[l]
    shared = {"cmat": cm, "selm": selm, "rmask": rmask, "wada": wada, "bgate": bgate,
              "winfm": winfm.reshape(2, 50, 128, 16 * 128), "wintm": wintm.reshape(2, 6, 128, 16 * 256), "wout": wout}
    return shared, sm


def make_in_maps(inp, cores):
    shared, sm = _prep_shared(inp)
    x = np.asarray(inp["x"], np.float32)
    c = np.asarray(inp["c"], np.float32)
    pos = np.asarray(inp["positions"], np.int32)
    maps = []
    for b in cores:
        s = sm.copy()
        s[:, 0:16] = _col(c[b])
        m = dict(shared)
        m["x"] = np.ascontiguousarray(x[b])
        m["small"] = s
        m["pos"] = np.ascontiguousarray(pos[b:b + 1])
        maps.append(m)
    return maps


_NC_CACHE = {}


def kernel(**inputs):
    if "nc" not in _NC_CACHE:
        _NC_CACHE["nc"] = build(2)
    nc = _NC_CACHE["nc"]
    cores = [i // 2 for i in range(8)]
    maps = make_in_maps(inputs, cores)
    res = run_bass_kernel_spmd(nc, maps, core_ids=list(range(8)))
    out = np.stack([np.asarray(res.results[2 * b]["out"], np.float32) for b in range(4)], axis=0)
    return out
```

```python
import math
import numpy as np
import concourse.bass as bass
import concourse.mybir as mybir
from concourse.bass_utils import run_bass_kernel_spmd

F32 = mybir.dt.float32
BF16 = mybir.dt.bfloat16
I32 = mybir.dt.int32
AF = mybir.ActivationFunctionType
ALU = mybir.AluOpType
AX = mybir.AxisListType

S = 2048
D = 2048
NT = 16
EPS = 1e-6
SLW = 880
SBASE = 32
NS = SBASE + 2 * SLW
CI, CO, CP, CTU, CNSU, CNSL, CB32, CO32, CO64 = range(9)
NCM = 9


class Reg:
    __slots__ = ("name", "lw", "rd", "dsem", "local", "psum")

    def __init__(self, name, local=False, psum=False):
        self.name = name
        self.psum = psum
        self.lw = None
        self.rd = {}
        self.dsem = None
        self.local = local


class Prog:
    def __init__(self, nc):
        self.nc = nc
        self.eng = {"pe": nc.tensor, "act": nc.scalar, "dve": nc.vector, "pool": nc.gpsimd, "sp": nc.sync}
        self.sem, self.cnt, self.semobj = {}, {}, {}
        self.seen = {e: {} for e in self.eng}
        for e in self.eng:
            s = nc.alloc_semaphore(name="s_" + e)
            self.sem[e] = s
            self.semobj[e] = s
            self.cnt[e] = 0
        self.vc = {}
        self.dcnt = {}
        self.local_keys = []
        self.local_next = 0
        self.ninstr = 0
        self.nwait = 0

    def _wait(self, e, key, val):
        if self.seen[e].get(key, 0) >= val:
            return
        self.eng[e].wait_ge(self.semobj[key], val)
        self.nwait += 1
        se = self.seen[e]
        se[key] = val
        clk = self.vc.get((key, val))
        if clk:
            for k, v in clk.items():
                if se.get(k, 0) < v:
                    se[k] = v

    def _deps(self, e, reads, writes, pe_accum=False, nowaw=False):
        need = {}

        def add(k, v):
            if need.get(k, 0) < v:
                need[k] = v
        for r in reads:
            if r.lw is not None:
                add(*r.lw)
            if r.psum:
                for k, v in r.rd.items():
                    if k != e:
                        add(k, v)
        for w in writes:
            if w.lw is not None and not nowaw and not (pe_accum and w.lw[0] == "pe"):
                add(*w.lw)
            for k, v in w.rd.items():
                add(k, v)
        for k, v in sorted(need.items(), key=lambda kv: -kv[1] if isinstance(kv[0], str) else 0):
            self._wait(e, k, v)

    def _record(self, ev, reads, writes, nowaw=False):
        for r in reads:
            if r.rd.get(ev[0], 0) < ev[1]:
                r.rd[ev[0]] = ev[1]
        for w in writes:
            w.lw = ev
            if not nowaw:
                w.rd = {}

    def op(self, e, fn, reads=(), writes=(), pe_accum=False):
        self._deps(e, reads, writes, pe_accum)
        ins = fn(self.eng[e])
        self.cnt[e] += 1
        ins.then_inc(self.sem[e], 1)
        ev = (e, self.cnt[e])
        self.vc[ev] = dict(self.seen[e])
        self._record(ev, reads, writes)
        self.ninstr += 1

    def _newkey(self):
        key = ("d", len(self.dcnt))
        self.semobj[key] = self.nc.alloc_semaphore(name="d_%d" % len(self.dcnt))
        self.dcnt[key] = 0
        return key

    def _dkey(self, w):
        if w.dsem is None:
            if w.local:
                if self.local_next == len(self.local_keys):
                    self.local_keys.append(self._newkey())
                w.dsem = self.local_keys[self.local_next]
                self.local_next += 1
            else:
                w.dsem = self._newkey()
        return w.dsem

    def dma(self, q, out_ap, in_ap, reads=(), writes=(), nowaw=False, **kw):
        w = writes[0]
        self._deps(q, reads, writes, nowaw=nowaw)
        key = self._dkey(w)
        ins = self.eng[q].dma_start(out=out_ap, in_=in_ap, **kw)
        self.dcnt[key] += 16
        ins.then_inc(self.semobj[key], 16)
        ev = (key, self.dcnt[key])
        self.vc[ev] = dict(self.seen[q])
        self._record(ev, reads, writes, nowaw=nowaw)
        self.ninstr += 1

    def barrier(self, reset=True):
        evs = [(e, self.cnt[e]) for e in self.eng if self.cnt[e] > 0]
        evs += [(k, v) for k, v in self.dcnt.items() if v > 0]
        for e in self.eng:
            for k, v in evs:
                if k != e:
                    self._wait(e, k, v)
        if reset:
            self.local_next = 0


class Arena:
    def __init__(self, nc, nwords):
        self.t = nc.alloc_sbuf_tensor("arena", [128, nwords], F32)
        self.n = nwords
        self.off = 0
        self.k = 0

    def reset(self):
        self.off = 0

    def alloc(self, shape, dt, name=None):
        free = int(np.prod(shape[1:]))
        words = free if dt in (F32, I32) else (free + 1) // 2
        words = (words + 7) // 8 * 8
        assert self.off + words <= self.n, ("arena overflow", name, self.off, words, self.n)
        v = self.t[:, self.off:self.off + words]
        self.off += words
        if dt == BF16:
            v = v.bitcast(BF16)[:, 0:free]
        elif dt == I32:
            v = v.bitcast(I32)[:, 0:free]
        else:
            v = v[:, 0:free]
        if len(shape) == 3:
            v = v.rearrange("p (a b) -> p a b", b=shape[2])
        self.k += 1
        if shape[0] < 128:
            v = v[0:shape[0]]
        return v, Reg(name or "a%d" % self.k, local=True)


def build(nlayers=2, debug=False, stop=None, scopes=False):
    nc = bass.Bass("TRN2", target_bir_lowering=False)
    P = Prog(nc)
    L = nlayers

    def din(name, shape, dt=F32):
        return nc.dram_tensor(name, shape, dt, kind="ExternalInput").ap()

    x_d = din("x", [S, D])
    small_d = din("small", [128, NS])
    cmat_d = din("cmat", [128, NCM * 128])
    selm_d = din("selm", [8, 8 * 128])
    rmask_d = din("rmask", [128, S])
    pos_d = din("pos", [1, S], I32)
    wada_d = din("wada", [2, 48, 128, 16 * 128])
    bgate_d = din("bgate", [2, 1, D])
    winfm_d = din("winfm", [2, 50, 128, 16 * 128])
    wintm_d = din("wintm", [2, 6, 128, 16 * 256])
    wout_d = din("wout", [2, 128, 16 * D])
    out_d = nc.dram_tensor("out", [S, D], F32, kind="ExternalOutput").ap()
    x1_d = nc.dram_tensor("x1s", [S, D], F32, kind="Internal").ap()
    oT_d = nc.dram_tensor("oTs", [16, 128, S], BF16, kind="ExternalOutput" if debug else "Internal").ap()
    R_out, R_x1, R_oT = Reg("out"), Reg("x1"), Reg("oT")

    def sb(name, shape, dt):
        return nc.alloc_sbuf_tensor("sb_" + name, shape, dt), Reg(name)

    big, R_big = sb("big", [128, 16, S], BF16)
    cosT, R_cos = sb("cosT", [128, S], BF16)
    sinT, R_sin = sb("sinT", [128, S], BF16)
    gate_bc, R_gate = sb("gate_bc", [128, D], F32)
    rmask, R_rmask = sb("rmask", [128, S], BF16)
    cm_f, R_cmf = sb("cm_f", [128, NCM, 128], F32)
    cm_b, R_cmb = sb("cm_b", [128, NCM, 128], BF16)
    selm, R_selm = sb("selm", [8, 8, 128], F32)
    small, R_small = sb("small", [128, NS], F32)
    modc, R_modc = sb("modc", [128, 48], F32)
    misc, R_misc = sb("misc", [128, 64], F32)
    ar = Arena(nc, (nc.sbuf_bytes_remaining - 2048) // 4)

    psall = nc.alloc_psum_tensor("psall", [128, 8 * 512], F32)
    pss = [psall[:, i * 512:(i + 1) * 512] for i in range(8)]
    R_ps = [Reg("ps%d" % i, psum=True) for i in range(8)]

    state = {"alt": 0}

    def alt():
        state["alt"] ^= 1
        return "act" if state["alt"] else "dve"

    def copy_on(e, out, in_, reads, writes):
        if e == "act":
            P.op("act", lambda g: g.activation(out=out, in_=in_, func=AF.Copy), reads=reads, writes=writes)
        else:
            P.op(e, lambda g: g.tensor_copy(out, in_), reads=reads, writes=writes)

    def mm(out, lhsT, rhs, start, stop, reads, w):
        P.op("pe", lambda g: g.matmul(out, lhsT=lhsT, rhs=rhs, start=start, stop=stop),
             reads=reads, writes=[w], pe_accum=not start)

    def new_phase(name="ph"):
        P.barrier()
        ar.reset()
        if scopes:
            if state.get("scope") is not None:
                nc.leave_named_scope(state["scope"][0], state["scope"][1], False)
            nm = "%s_%d" % (name, state.setdefault("nscope", 0))
            state["nscope"] += 1
            sid, _ = nc.enter_named_scope(nm, False)
            state["scope"] = (nm, sid)

    ident_b = cm_b[:, CI, :]
    ones_b = cm_b[:, CO, :]
    ident_f = cm_f[:, CI, :]

    def rstd_part(srcs, n, denom, tmps, psi):
        (sq, R_sq), (t1, R_t1), (rinv, R_rinv) = tmps
        for i, (sap, sreg) in enumerate(srcs):
            P.op("act", lambda g: g.activation(out=sq[:, i, 0:n], in_=sap, func=AF.Square), reads=[sreg], writes=[R_sq])
        for i in range(len(srcs)):
            mm(pss[psi][:, 0:n], ones_b, sq[:, i, 0:n], i == 0, i == len(srcs) - 1, [R_sq, R_cmb], R_ps[psi])
        P.op("act", lambda g: g.activation(out=t1[:, 0:n], in_=pss[psi][:, 0:n], func=AF.Sqrt, scale=1.0 / denom, bias=eps_ap),
             reads=[R_ps[psi], R_misc], writes=[R_t1])
        P.op("dve", lambda g: g.reciprocal(rinv[:, 0:n], t1[:, 0:n]), reads=[R_t1], writes=[R_rinv])
        return rinv, R_rinv

    P.dma("sp", small[:], small_d, writes=[R_small])
    P.dma("sp", cm_f[:], cmat_d.rearrange("p (a b) -> p a b", b=128), writes=[R_cmf])
    P.dma("sp", selm[:], selm_d.rearrange("p (a b) -> p a b", b=128), writes=[R_selm])
    P.dma("pool", rmask[:], rmask_d, writes=[R_rmask])
    P.op("dve", lambda g: g.tensor_copy(cm_b[:], cm_f[:]), reads=[R_cmf], writes=[R_cmb])
    eps_ap = misc[:, 0:1]
    P.op("dve", lambda g: g.memset(misc[:], 0.0), writes=[R_misc])
    P.op("dve", lambda g: g.memset(misc[:, 0:1], EPS), reads=[], writes=[R_misc])
    P.op("dve", lambda g: g.memset(misc[:, 1:2], 1.0), reads=[], writes=[R_misc])
    one_ap = misc[:, 1:2]

    posi, R_posi = ar.alloc([128, S], I32, "posi")
    y, R_y = ar.alloc([128, S], F32, "y")
    yi, R_yi = ar.alloc([128, S], I32, "yi")
    yf, R_yf = ar.alloc([128, S], F32, "yf")
    fr, R_fr = ar.alloc([128, S], F32, "fr")
    m1, R_m1 = ar.alloc([128, S], F32, "m1")
    P.dma("sp", posi, pos_d.to_broadcast([128, S]), writes=[R_posi])
    P.op("dve", lambda g: g.tensor_copy(y, posi), reads=[R_posi], writes=[R_y])
    P.op("dve", lambda g: g.tensor_scalar(y, y, small[:, 16:17], None, ALU.mult), reads=[R_y, R_small], writes=[R_y])
    P.op("dve", lambda g: g.tensor_copy(yi, y), reads=[R_y], writes=[R_yi])
    P.op("dve", lambda g: g.tensor_copy(yf, yi), reads=[R_yi], writes=[R_yf])
    P.op("dve", lambda g: g.tensor_tensor(fr, y, yf, ALU.subtract), reads=[R_y, R_yf], writes=[R_fr])
    for which, dst, R_dst in ((0, sinT, R_sin), (1, cosT, R_cos)):
        src = fr
        if which == 1:
            P.op("dve", lambda g: g.tensor_scalar(y, fr, 0.25, None, ALU.add), reads=[R_fr], writes=[R_y])
            src = y
        R_src = R_fr if which == 0 else R_y
        P.op("dve", lambda g: g.tensor_scalar(m1, src, 0.5, None, ALU.is_gt), reads=[R_src], writes=[R_m1])
        P.op("dve", lambda g: g.tensor_tensor(yf, src, m1, ALU.subtract), reads=[R_src, R_m1], writes=[R_yf])
        P.op("dve", lambda g: g.tensor_scalar(m1, yf, -0.5, None, ALU.is_lt), reads=[R_yf], writes=[R_m1])
        P.op("dve", lambda g: g.tensor_tensor(yf, yf, m1, ALU.add), reads=[R_yf, R_m1], writes=[R_yf])
        P.op("act", lambda g: g.activation(out=dst[:], in_=yf, func=AF.Sin, scale=2.0 * math.pi), reads=[R_yf], writes=[R_dst])

    if stop == "p0":
        P.barrier(); return nc
    def proj_fm(l, gi, M, wbufs, consume, bankset):
        wb, R_wb = wbufs[state.setdefault("wfm_i", 0) % len(wbufs)]
        state["wfm_i"] += 1
        P.dma("pool", wb, winfm_d[l, gi].rearrange("p (a b) -> p a b", b=128), writes=[R_wb])
        banks = [bankset * 4 + i for i in range(4)]
        for kc in range(16):
            for tb in range(4):
                mm(pss[banks[tb]][0:M, :], wb[:, kc, 0:M], big[:, kc, tb * 512:(tb + 1) * 512], kc == 0, kc == 15,
                   [R_wb, R_big], R_ps[banks[tb]])
        for tb in range(4):
            consume(tb, pss[banks[tb]][0:M, :], R_ps[banks[tb]])

    def proj_tm(l, gi, wtb, consume, banks):
        wb, R_wb = wtb
        P.dma("pool", wb, wintm_d[l, gi].rearrange("p (a b) -> p a b", b=256), writes=[R_wb])
        for t in range(NT):
            bk = banks[t % len(banks)]
            for kc in range(16):
                mm(pss[bk][:, 0:256], big[:, kc, t * 128:(t + 1) * 128], wb[:, kc, :], kc == 0, kc == 15,
                   [R_wb, R_big], R_ps[bk])
            consume(t, pss[bk][:, 0:256], R_ps[bk])

    def post_norm_store(l, o_acc, R_oacc, nsub, wcols, szs, R_sz, fc0, tmps, denom, psi):
        (sq, R_sq), (t1, R_t1), (rinv, R_rinv), (u, R_u) = tmps[0:4]
        for blk in range(4):
            sl = slice(blk * 512, (blk + 1) * 512)
            rstd_part([(o_acc[:, j, sl], R_oacc) for j in range(nsub)], 512, denom, tmps[0:3], psi)
            for j in range(nsub):
                P.op("dve", lambda g: g.scalar_tensor_tensor(out=u[:, 0:512], in0=o_acc[:, j, sl], scalar=wcols[j], in1=rinv[:, 0:512],
                                                             op0=ALU.mult, op1=ALU.mult),
                     reads=[R_oacc, R_rinv, R_small, R_misc], writes=[R_u])
                ost, R_ost = tmps[4 + state.setdefault("ost_i", 0) % 2]
                state["ost_i"] += 1
                P.op("pool", lambda g: g.tensor_tensor(ost[:, 0:512], u[:, 0:512], szs[:, j, sl], ALU.mult),
                     reads=[R_u, R_sz], writes=[R_ost])
                P.dma("sp", oT_d[fc0 + j, :, sl], ost[:, 0:512], reads=[R_ost], writes=[R_oT], nowaw=True)

    for l in range(L):
        sb_l = SBASE + l * SLW
        lam_init = 0.8 - 0.6 * math.exp(-0.3 * l)

        def sc(off, n=1):
            return small[:, sb_l + off: sb_l + off + n]

        new_phase("A")
        cact, R_cact = ar.alloc([128, 16], F32, "cact")
        c2, R_c2 = ar.alloc([128, 16, 2], F32, "c2")
        crep, R_crep = ar.alloc([128, 16, 128], F32, "crep")
        bg, R_bg = ar.alloc([128, D], F32, "bg")
        wab = [ar.alloc([128, 16, 128], F32, "wa%d" % i) for i in range(3)]
        P.op("act", lambda g: g.activation(out=cact, in_=small[:, 0:16], func=AF.Silu), reads=[R_small], writes=[R_cact])
        P.op("dve", lambda g: g.tensor_copy(c2, cact.unsqueeze(2).to_broadcast([128, 16, 2])), reads=[R_cact], writes=[R_c2])
        P.op("dve", lambda g: g.tensor_copy(crep, cact.unsqueeze(2).to_broadcast([128, 16, 128])), reads=[R_cact], writes=[R_crep])
        P.dma("sp", bg, bgate_d[l].to_broadcast([128, D]), writes=[R_bg])
        for g_ in range(48):
            wa, R_wa = wab[g_ % 3]
            P.dma("sp", wa, wada_d[l, g_].rearrange("p (a b) -> p a b", b=128), writes=[R_wa])
            if g_ < 32:
                for kc in range(16):
                    mm(pss[0][:, 2 * g_:2 * g_ + 2], wa[:, kc, :], c2[:, kc, :], kc == 0, kc == 15, [R_wa, R_c2], R_ps[0])
                if g_ == 31:
                    P.op("dve", lambda g: g.tensor_tensor(modc[:, 0:32], pss[0][:, 0:64:2], sc(16, 32), ALU.add),
                         reads=[R_ps[0], R_small], writes=[R_modc])
                    P.op("dve", lambda g: g.scalar_tensor_tensor(out=modc[:, 32:48], in0=modc[:, 16:32], scalar=1.0, in1=sc(0, 16),
                                                                 op0=ALU.add, op1=ALU.mult),
                         reads=[R_modc, R_small], writes=[R_modc])
            else:
                gg = g_ - 32
                bk = 1 + (gg // 4) % 2
                c0 = (gg % 4) * 128
                for kc in range(16):
                    mm(pss[bk][:, c0:c0 + 128], crep[:, kc, :], wa[:, kc, :], kc == 0, kc == 15, [R_wa, R_crep], R_ps[bk])
                if gg % 4 == 3:
                    sl = slice((gg // 4) * 512, (gg // 4 + 1) * 512)
                    P.op("dve", lambda g: g.tensor_tensor(gate_bc[:, sl], pss[bk][:], bg[:, sl], ALU.add),
                         reads=[R_ps[bk], R_bg], writes=[R_gate])

        if stop == "pA":
            P.barrier(); return nc
        new_phase("B")
        xsrc = x_d if l == 0 else x1_d
        xts = [ar.alloc([128, D], F32, "xt%d" % i) for i in range(2)]
        xn, R_xn = ar.alloc([128, 4, D], BF16, "xn")
        junk, R_junk = ar.alloc([128, D], BF16, "junk")
        ssq, R_ssq = ar.alloc([128, 8], F32, "ssq")
        hT = big
        for tb in range(4):
            for i in range(4):
                t = tb * 4 + i
                xt, R_xt = xts[t % 2]
                P.dma("sp", xt, xsrc[t * 128:(t + 1) * 128, :], reads=[R_x1] if l > 0 else [], writes=[R_xt])
                P.op("act", lambda g: g.activation(out=junk, in_=xt, func=AF.Square, accum_out=ssq[:, 0:1]),
                     reads=[R_xt], writes=[R_junk, R_ssq])
                P.op("act", lambda g: g.activation(out=ssq[:, 1:2], in_=ssq[:, 0:1], func=AF.Sqrt, scale=1.0 / D, bias=eps_ap),
                     reads=[R_ssq, R_misc], writes=[R_ssq])
                P.op("dve", lambda g: g.reciprocal(ssq[:, 2:3], ssq[:, 1:2]), reads=[R_ssq], writes=[R_ssq])
                P.op("dve", lambda g: g.tensor_scalar(xn[:, i, :], xt, ssq[:, 2:3], None, ALU.mult),
                     reads=[R_xt, R_ssq], writes=[R_xn])
            for fc in range(16):
                bk = fc % 4
                pT = pss[bk][:].bitcast(BF16)
                for i in range(4):
                    P.op("pe", lambda g: g.transpose(pT[:, i * 128:(i + 1) * 128], xn[:, i, fc * 128:(fc + 1) * 128], ident_b),
                         reads=[R_xn, R_cmb], writes=[R_ps[bk]], pe_accum=i > 0)
                dst = hT[:, fc, tb * 512:(tb + 1) * 512]
                if alt() == "act":
                    P.op("act", lambda g: g.activation(out=dst, in_=pT[:, 0:512], func=AF.Identity,
                                                       scale=modc[:, 32 + fc:33 + fc], bias=modc[:, fc:fc + 1]),
                         reads=[R_ps[bk], R_modc], writes=[R_big])
                else:
                    P.op("dve", lambda g: g.tensor_scalar(dst, pT[:, 0:512], modc[:, 32 + fc:33 + fc], modc[:, fc:fc + 1],
                                                          ALU.mult, ALU.add),
                         reads=[R_ps[bk], R_modc], writes=[R_big])

        if stop == "pB":
            P.barrier(); return nc
        for pr in range(2):
            new_phase("gla")
            wfm = [ar.alloc([128, 16, 128], BF16, "wfm%d" % i) for i in range(3)]
            wtm = ar.alloc([128, 16, 256], BF16, "wtm")
            glrT, R_glrT = ar.alloc([16, S], BF16, "glrT")
            wlr, R_wlr = ar.alloc([16, 256], BF16, "wlr")
            bcs, R_bcs = ar.alloc([128, S], F32, "bcs")
            eb, R_eb = ar.alloc([128, S], F32, "eb")
            q_eT, R_qe = ar.alloc([128, S], BF16, "q_eT")
            k_eT, R_ke = ar.alloc([128, S], BF16, "k_eT")
            v_g, R_vg = ar.alloc([128, NT, 256], BF16, "v_g")
            sz, R_sz = ar.alloc([128, 2, S], BF16, "sz")
            ke_tok, R_ket = ar.alloc([128, NT, 128], BF16, "ke_tok")
            o_acc, R_oacc = ar.alloc([128, 2, S], F32, "o_acc")
            e1, R_e1 = ar.alloc([128, 512], F32, "e1")
            dec, R_dec = ar.alloc([128, 16], F32, "dec")
            nb, R_nb = ar.alloc([128, 2], F32, "nb")
            Sp, R_Sp = ar.alloc([128, 256], F32, "Sp")
            Stmp, R_Stmp = ar.alloc([128, 256], F32, "Stmp")
            Sbf, R_Sbf = ar.alloc([128, 256], BF16, "Sbf")
            attm, R_attm = ar.alloc([128, 2, 128], BF16, "attm")
            tmps = [ar.alloc([128, 2, 512], BF16, "sq"), ar.alloc([128, 512], F32, "t1"), ar.alloc([128, 512], F32, "rinv"),
                    ar.alloc([128, 512], F32, "u"), ar.alloc([128, 512], BF16, "ost"), ar.alloc([128, 512], BF16, "ost2")]

            def c_glr(tb, ps, R):
                copy_on("act", glrT[0:16, tb * 512:(tb + 1) * 512], ps, [R], [R_glrT])
            proj_fm(l, 4, 16, wfm, c_glr, 0)
            P.op("dve", lambda g: g.tensor_copy(wlr, sc(624, 256)[0:16, :]), reads=[R_small], writes=[R_wlr])
            P.op("dve", lambda g: g.tensor_scalar(nb, sc(48, 2), -1.0, None, ALU.mult), reads=[R_small], writes=[R_nb])
            for blk in range(4):
                sl = slice(blk * 512, (blk + 1) * 512)
                bk = 4 + blk % 2
                mm(pss[bk][:], wlr[0:16, pr * 128:(pr + 1) * 128], glrT[0:16, sl], True, True, [R_wlr, R_glrT], R_ps[bk])
                P.op("act", lambda g: g.activation(out=e1, in_=pss[bk][:], func=AF.Exp, scale=-1.0, bias=nb[:, pr:pr + 1]),
                     reads=[R_ps[bk], R_nb], writes=[R_e1])
                P.op("act", lambda g: g.activation(out=bcs[:, sl], in_=e1, func=AF.Ln, bias=one_ap), reads=[R_e1, R_misc], writes=[R_bcs])
            P.op("dve", lambda g: g.tensor_tensor_scan(out=bcs, data0=rmask[:], data1=bcs, initial=0.0, op0=ALU.mult, op1=ALU.add),
                 reads=[R_rmask, R_bcs], writes=[R_bcs])
            P.op("act", lambda g: g.activation(out=eb, in_=bcs, func=AF.Exp, scale=-1.0 / 16.0), reads=[R_bcs], writes=[R_eb])
            P.op("dve", lambda g: g.tensor_copy(dec, eb[:, 127:S:128]), reads=[R_eb], writes=[R_dec])
            P.op("act", lambda g: g.activation(out=bcs, in_=bcs, func=AF.Exp, scale=1.0 / 16.0), reads=[R_bcs], writes=[R_bcs])
            enb = bcs

            def c_q(tb, ps, R):
                sl = slice(tb * 512, (tb + 1) * 512)
                P.op("dve", lambda g: g.scalar_tensor_tensor(out=q_eT[:, sl], in0=ps, scalar=0.125, in1=eb[:, sl], op0=ALU.mult, op1=ALU.mult),
                     reads=[R, R_eb], writes=[R_qe])
            proj_fm(l, pr, 128, wfm, c_q, 1)

            def c_k(tb, ps, R):
                sl = slice(tb * 512, (tb + 1) * 512)
                P.op("dve", lambda g: g.tensor_tensor(k_eT[:, sl], ps, enb[:, sl], ALU.mult), reads=[R, R_bcs], writes=[R_ke])
            proj_fm(l, 2 + pr, 128, wfm, c_k, 0)

            def c_v(t, ps, R):
                copy_on("act", v_g[:, t, :], ps, [R], [R_vg])
            proj_tm(l, pr, wtm, c_v, [4, 5])
            for hh in range(2):
                def c_z(tb, ps, R):
                    P.op("act", lambda g: g.activation(out=sz[:, hh, tb * 512:(tb + 1) * 512], in_=ps, func=AF.Silu), reads=[R], writes=[R_sz])
                proj_fm(l, 5 + 2 * pr + hh, 128, wfm, c_z, hh)
            for t4 in range(4):
                bk = 6 + t4 % 2
                pT = pss[bk][:].bitcast(BF16)
                for i in range(4):
                    t = t4 * 4 + i
                    P.op("pe", lambda g: g.transpose(pT[:, i * 128:(i + 1) * 128], k_eT[:, t * 128:(t + 1) * 128], ident_b),
                         reads=[R_ke, R_cmb], writes=[R_ps[bk]], pe_accum=i > 0)
                P.op("dve", lambda g: g.tensor_copy(ke_tok[:, t4 * 4:(t4 + 1) * 4, :], pT[:, 0:512].rearrange("p (a b) -> p a b", b=128)),
                     reads=[R_ps[bk]], writes=[R_ket])
            P.op("dve", lambda g: g.memset(Sp, 0.0), writes=[R_Sp])
            for n in range(NT):
                ch = slice(n * 128, (n + 1) * 128)
                ba, bo, bkv = n % 2, 2 + n % 2, 4 + n % 2
                for hh in range(2):
                    hp = slice(64 * hh, 64 * hh + 64)
                    mm(pss[ba][:, hh * 128:(hh + 1) * 128], k_eT[hp, ch], q_eT[hp, ch], True, True, [R_ke, R_qe], R_ps[ba])
                P.op("dve", lambda g: g.tensor_tensor(attm, pss[ba][:, 0:256].rearrange("p (a b) -> p a b", b=128),
                                                      cm_f[:, CTU:CTU + 1, :].to_broadcast([128, 2, 128]), ALU.mult),
                     reads=[R_ps[ba], R_cmf], writes=[R_attm])
                for hh in range(2):
                    hp = slice(64 * hh, 64 * hh + 64)
                    vs = slice(hh * 128, (hh + 1) * 128)
                    mm(pss[bo][:, vs], v_g[:, n, vs], attm[:, hh, :], True, n == 0, [R_vg, R_attm], R_ps[bo])
                    if n > 0:
                        mm(pss[bo][:, vs], Sbf[hp, vs], q_eT[hp, ch], False, True, [R_Sbf, R_qe], R_ps[bo])
                P.op("act", lambda g: g.activation(out=o_acc[:, :, ch], in_=pss[bo][:, 0:256].rearrange("p (a b) -> p a b", b=128), func=AF.Copy),
                     reads=[R_ps[bo]], writes=[R_oacc])
                if n < NT - 1:
                    mm(pss[bkv][:, 0:256], ke_tok[:, n, :], v_g[:, n, :], True, True, [R_ket, R_vg], R_ps[bkv])
                    P.op("dve", lambda g: g.tensor_tensor(Stmp, Sp, pss[bkv][:, 0:256], ALU.add), reads=[R_Sp, R_ps[bkv]], writes=[R_Stmp])
                    P.op("dve", lambda g: g.tensor_scalar(Sp, Stmp, dec[:, n:n + 1], None, ALU.mult), reads=[R_Stmp, R_dec], writes=[R_Sp])
                    P.op("pool", lambda g: g.tensor_scalar(Sbf, Stmp, dec[:, n:n + 1], None, ALU.mult), reads=[R_Stmp, R_dec], writes=[R_Sbf])
            for hh in range(2):
                post_norm_store(l, o_acc[:, hh:hh + 1, :], R_oacc, 1, [sc(50)], sz[:, hh:hh + 1, :], R_sz, 2 * pr + hh, tmps, 128.0, 6)

        if stop == "pC1":
            P.barrier(); return nc
        for h in range(4):
            new_phase("gdn")
            wfm = [ar.alloc([128, 16, 128], BF16, "wfm%d" % i) for i in range(2)]
            xbf, R_xbf = ar.alloc([128, S + 8], BF16, "xbf")
            diag, R_diag = ar.alloc([128, 4, 128], BF16, "diag")
            cs, R_cs = ar.alloc([128, S], F32, "cs")
            knT, R_kn = ar.alloc([128, S], BF16, "knT")
            qnT, R_qn = ar.alloc([128, S], BF16, "qnT")
            cvT, R_cv = ar.alloc([128, S], BF16, "cvT")
            kbT, R_kb = ar.alloc([128, S], BF16, "kbT")
            q_eT, R_qe = ar.alloc([128, S], BF16, "q_eT")
            vb_tok, R_vb = ar.alloc([128, NT, 128], BF16, "vb_tok")
            kbg_tok, R_kbg = ar.alloc([128, NT, 128], BF16, "kbg_tok")
            kt_tok, R_kt = ar.alloc([128, NT, 128], BF16, "kt_tok")
            gc, R_gc = ar.alloc([128, S], F32, "gc")
            beta, R_beta = ar.alloc([128, S], BF16, "beta")
            eg, R_eg = ar.alloc([128, S], F32, "eg")
            tl, R_tl = ar.alloc([128, S], F32, "tl")
            dabT, R_dab = tl[0:8, :], R_tl
            cols, R_cols = ar.alloc([128, 6, 16], F32, "cols")
            nA, R_nA = ar.alloc([128, 4], F32, "nA")
            Sp, R_Sp = ar.alloc([128, 128], F32, "Sp")
            Sbf, R_Sbf = ar.alloc([128, 128], BF16, "Sbf")
            dm = [ar.alloc([128, 128], F32, "dm%d" % i) for i in range(4)]
            mb = [ar.alloc([128, 128], BF16, "mb%d" % i) for i in range(10)]
            u_sb, R_u = ar.alloc([128, 128], F32, "u_sb")
            tmps = [ar.alloc([128, 2, 512], BF16, "sq"), ar.alloc([128, 512], F32, "t1"), ar.alloc([128, 512], F32, "rinv"),
                    ar.alloc([128, 512], F32, "u"), ar.alloc([128, 512], BF16, "ost"), ar.alloc([128, 512], BF16, "ost2")]
            e1, R_e1 = tmps[3]
            o_acc, R_oacc = cs.rearrange("p (a b) -> p a b", a=1), R_cs

            def c_dab(tb, ps, R):
                copy_on("act", dabT[0:8, tb * 512:(tb + 1) * 512], ps, [R], [R_dab])
            proj_fm(l, 21, 8, wfm, c_dab, 0)
            P.op("dve", lambda g: g.memset(xbf[:, 0:3], 0.0), writes=[R_xbf])
            for which in range(3):
                ti = which * 4 + h
                for j in range(4):
                    P.op("dve", lambda g: g.tensor_scalar(diag[:, j, :], ident_f, sc(64 + ti * 4 + j), None, ALU.mult),
                         reads=[R_cmf, R_small], writes=[R_diag])

                def c_x(tb, ps, R):
                    copy_on(alt(), xbf[:, 3 + tb * 512:3 + (tb + 1) * 512], ps, [R], [R_xbf])
                proj_fm(l, 9 + 4 * which + h, 128, wfm, c_x, 1)
                for blk in range(4):
                    sl = slice(blk * 512, (blk + 1) * 512)
                    bk = blk % 2
                    for j in range(4):
                        mm(pss[bk][:], diag[:, j, :], xbf[:, blk * 512 + j: blk * 512 + j + 512], j == 0, j == 3, [R_diag, R_xbf], R_ps[bk])
                    if which == 2:
                        P.op("act", lambda g: g.activation(out=cvT[:, sl], in_=pss[bk][:], func=AF.Silu), reads=[R_ps[bk]], writes=[R_cv])
                    else:
                        P.op("act", lambda g: g.activation(out=cs[:, sl], in_=pss[bk][:], func=AF.Silu), reads=[R_ps[bk]], writes=[R_cs])
                if which < 2:
                    for blk in range(4):
                        sl = slice(blk * 512, (blk + 1) * 512)
                        rinv, R_rinv = rstd_part([(cs[:, sl], R_cs)], 512, 1.0, tmps[0:3], 2 + blk % 2)
                        if which == 0:
                            P.op("dve", lambda g: g.scalar_tensor_tensor(out=qnT[:, sl], in0=cs[:, sl], scalar=128.0 ** -0.5, in1=rinv[:, 0:512],
                                                                         op0=ALU.mult, op1=ALU.mult), reads=[R_cs, R_rinv], writes=[R_qn])
                        else:
                            P.op("dve", lambda g: g.tensor_tensor(knT[:, sl], cs[:, sl], rinv[:, 0:512], ALU.mult), reads=[R_cs, R_rinv], writes=[R_kn])
            if stop == "g1":
                P.barrier(); return nc
            P.op("act", lambda g: g.activation(out=nA, in_=sc(56, 4), func=AF.Exp), reads=[R_small], writes=[R_nA])
            P.op("dve", lambda g: g.tensor_scalar(nA, nA, -1.0, None, ALU.mult), reads=[R_nA], writes=[R_nA])
            for blk in range(4):
                sl = slice(blk * 512, (blk + 1) * 512)
                bk = 4 + blk % 2
                mm(pss[bk][:], selm[0:8, h, :], dabT[0:8, sl], True, True, [R_selm, R_dab], R_ps[bk])
                P.op("act", lambda g: g.activation(out=e1, in_=pss[bk][:], func=AF.Exp, bias=sc(60 + h)), reads=[R_ps[bk], R_small], writes=[R_e1])
                P.op("act", lambda g: g.activation(out=gc[:, sl], in_=e1, func=AF.Ln, bias=one_ap), reads=[R_e1, R_misc], writes=[R_gc])
                bk2 = 6 + blk % 2
                mm(pss[bk2][:], selm[0:8, 4 + h, :], dabT[0:8, sl], True, True, [R_selm, R_dab], R_ps[bk2])
                P.op("act", lambda g: g.activation(out=beta[:, sl], in_=pss[bk2][:], func=AF.Sigmoid), reads=[R_ps[bk2]], writes=[R_beta])
            P.op("dve", lambda g: g.tensor_tensor_scan(out=gc, data0=rmask[:], data1=gc, initial=0.0, op0=ALU.mult, op1=ALU.add),
                 reads=[R_rmask, R_gc], writes=[R_gc])
            P.op("dve", lambda g: g.tensor_scalar(gc, gc, nA[:, h:h + 1], None, ALU.mult), reads=[R_gc, R_nA], writes=[R_gc])
            gcl = cols[:, 0, :]
            cd = cols[:, 1, :]
            P.op("dve", lambda g: g.tensor_copy(gcl, gc[:, 127:S:128]), reads=[R_gc], writes=[R_cols])
            P.op("act", lambda g: g.activation(out=cd, in_=gcl, func=AF.Exp), reads=[R_cols], writes=[R_cols])
            P.op("act", lambda g: g.activation(out=eg, in_=gc, func=AF.Exp), reads=[R_gc], writes=[R_eg])
            P.op("dve", lambda g: g.tensor_tensor(q_eT, qnT, eg, ALU.mult), reads=[R_qn, R_eg], writes=[R_qe])
            P.op("dve", lambda g: g.tensor_tensor(kbT, knT, beta, ALU.mult), reads=[R_kn, R_beta], writes=[R_kb])
            P.op("pool", lambda g: g.tensor_tensor(eg, eg, beta, ALU.mult), reads=[R_eg, R_beta], writes=[R_eg])
            for n in range(NT):
                ch = slice(n * 128, (n + 1) * 128)
                P.op("act", lambda g: g.activation(out=tl[:, ch], in_=gc[:, ch], func=AF.Exp, scale=-1.0, bias=gcl[:, n:n + 1]),
                     reads=[R_gc, R_cols], writes=[R_tl])
            if stop == "g2":
                P.barrier(); return nc
            for qi, (src, R_src) in enumerate(((gc, R_gc), (beta, R_beta), (eg, R_eg), (tl, R_tl))):
                oh = cm_b[:, CI, 0:2] if src is beta else cm_f[:, CI, 0:2]
                for n in range(NT):
                    c0 = (qi * NT + n) * 2
                    mm(pss[3][:, c0:c0 + 2], src[:, n * 128:(n + 1) * 128], oh, True, True, [R_src, R_cmf, R_cmb], R_ps[3])
            P.op("dve", lambda g: g.tensor_copy(cols[:, 2:6, :], pss[3][:, 0:128:2].rearrange("p (a b) -> p a b", b=NT)),
                 reads=[R_ps[3]], writes=[R_cols])
            gc_col, beta_col, bexp_col, tail_col = (cols[:, i, :] for i in (2, 3, 4, 5))
            if stop == "g2b":
                P.barrier(); return nc
            for t in range(NT):
                bk = t % 2
                pT = pss[bk][:].bitcast(BF16)
                ts = slice(t * 128, (t + 1) * 128)
                P.op("pe", lambda g: g.transpose(pT[:, 0:128], knT[:, ts], ident_b), reads=[R_kn, R_cmb], writes=[R_ps[bk]])
                P.op("pe", lambda g: g.transpose(pT[:, 128:256], cvT[:, ts], ident_b), reads=[R_cv, R_cmb], writes=[R_ps[bk]], pe_accum=True)
                P.op("dve", lambda g: g.tensor_scalar(kbg_tok[:, t, :], pT[:, 0:128], bexp_col[:, t:t + 1], None, ALU.mult),
                     reads=[R_ps[bk], R_cols], writes=[R_kbg])
                P.op("dve", lambda g: g.tensor_scalar(kt_tok[:, t, :], pT[:, 0:128], tail_col[:, t:t + 1], None, ALU.mult),
                     reads=[R_ps[bk], R_cols], writes=[R_kt])
                P.op("dve", lambda g: g.tensor_scalar(vb_tok[:, t, :], pT[:, 128:256], beta_col[:, t:t + 1], None, ALU.mult),
                     reads=[R_ps[bk], R_cols], writes=[R_vb])

            if stop == "g3":
                P.barrier(); return nc
            P.barrier(reset=False)
            sz, R_sz = tl.bitcast(BF16)[:, 0:S].rearrange("p (a b) -> p a b", a=1), Reg("sz")
            mb2 = [(xbf[:, i * 128:(i + 1) * 128], Reg("mb2_%d" % i)) for i in range(16)]

            def c_z(tb, ps, R):
                P.op("act", lambda g: g.activation(out=sz[:, 0, tb * 512:(tb + 1) * 512], in_=ps, func=AF.Silu), reads=[R], writes=[R_sz])
            proj_fm(l, 22 + h, 128, wfm, c_z, 1)
            if stop == "g4":
                P.barrier(); return nc
            P.barrier(reset=False)
            G = 4
            eg4 = eg.rearrange("p (t g c) -> p t g c", g=G, c=128)
            (dA4, R_dA), (dB4, R_dB), (dD4, R_dD), (u4, R_u4) = [(eg4[:, i], Reg("f4_%d" % i)) for i in range(4)]
            pool16 = []
            for src in (beta, cvT, wfm[0][0].rearrange("p a b -> p (a b)"), wfm[1][0].rearrange("p a b -> p (a b)"),
                        tl.bitcast(BF16)[:, S:2 * S]):
                v4 = src.rearrange("p (t g c) -> p t g c", g=G, c=128)
                pool16 += [(v4[:, i], Reg("b4_%d" % len(pool16))) for i in range(4)]
            ((Pm4, R_P), (PT4, R_PT), (qk4, R_qk), (Pd4, R_Pd), (PTd4, R_PTd), (Po32, R_Po32), (PTo32, R_PTo32), (Po64, R_Po64),
             Abuf0, ATbuf0, Abuf1, ATbuf1, sq0, sqT0, sq1, sqT1, (U1, R_U1), (T1, R_T1), (wT4, R_wT)) = pool16[0:19]
            vnew, R_vn = mb[0]

            def bc4(blk, f32=True):
                src = cm_f if f32 else cm_b
                return src[:, blk:blk + 1, :].to_broadcast([128, G, 128])

            def flat(t4):
                return t4.rearrange("p g c -> p (g c)")

            def p4(bank):
                return pss[bank][:].rearrange("p (g c) -> p g c", c=128)

            P.op("dve", lambda g: g.memset(Sp, 0.0), writes=[R_Sp])
            P.op("dve", lambda g: g.memset(Sbf, 0.0), writes=[R_Sbf])
            for bb in range(NT // G):
                ns = [bb * G + g_ for g_ in range(G)]
                chs = [slice(n * 128, (n + 1) * 128) for n in ns]
                for g_, n in enumerate(ns):
                    gcc = gc_col[:, n:n + 1]
                    P.op("dve", lambda g: g.tensor_scalar(dA4[:, g_, :], gc[:, chs[g_]], gcc, 0.0, ALU.subtract, ALU.max),
                         reads=[R_gc, R_cols], writes=[R_dA])
                    P.op("dve", lambda g: g.tensor_scalar(dB4[:, g_, :], gc[:, chs[g_]], gcc, 0.0, ALU.subtract, ALU.min),
                         reads=[R_gc, R_cols], writes=[R_dB])
                P.op("act", lambda g: g.activation(out=flat(dA4), in_=flat(dA4), func=AF.Exp, scale=-1.0), reads=[R_dA], writes=[R_dA])
                P.op("act", lambda g: g.activation(out=flat(dB4), in_=flat(dB4), func=AF.Exp), reads=[R_dB], writes=[R_dB])
                P.op("dve", lambda g: g.tensor_tensor(dA4, dA4, bc4(CNSL), ALU.mult), reads=[R_dA, R_cmf], writes=[R_dA])
                P.op("dve", lambda g: g.tensor_tensor(dD4, dB4, bc4(CNSU), ALU.mult), reads=[R_dB, R_cmf], writes=[R_dD])
                P.op("dve", lambda g: g.tensor_tensor(dB4, dB4, bc4(CTU), ALU.mult), reads=[R_dB, R_cmf], writes=[R_dB])
                for g_ in range(G):
                    cs_ = slice(g_ * 128, (g_ + 1) * 128)
                    mm(pss[0][:, cs_], kbT[:, chs[g_]], knT[:, chs[g_]], True, True, [R_kb, R_kn], R_ps[0])
                for g_ in range(G):
                    cs_ = slice(g_ * 128, (g_ + 1) * 128)
                    mm(pss[1][:, cs_], knT[:, chs[g_]], kbT[:, chs[g_]], True, True, [R_kb, R_kn], R_ps[1])
                for g_ in range(G):
                    cs_ = slice(g_ * 128, (g_ + 1) * 128)
                    mm(pss[2][:, cs_], knT[:, chs[g_]], qnT[:, chs[g_]], True, True, [R_kn, R_qn], R_ps[2])
                P.op("dve", lambda g: g.tensor_tensor(Pm4, p4(0), dA4, ALU.mult), reads=[R_ps[0], R_dA], writes=[R_P])
                P.op("dve", lambda g: g.tensor_tensor(PT4, p4(1), dD4, ALU.mult), reads=[R_ps[1], R_dD], writes=[R_PT])
                P.op("dve", lambda g: g.tensor_tensor(qk4, p4(2), dB4, ALU.mult), reads=[R_ps[2], R_dB], writes=[R_qk])
                for dst, R_d, src, R_s, mk in ((Pd4, R_Pd, Pm4, R_P, CB32), (PTd4, R_PTd, PT4, R_PT, CB32), (Po32, R_Po32, Pm4, R_P, CO32),
                                               (PTo32, R_PTo32, PT4, R_PT, CO32), (Po64, R_Po64, Pm4, R_P, CO64)):
                    P.op("dve", lambda g: g.tensor_tensor(dst, src, bc4(mk, False), ALU.mult), reads=[R_s, R_cmb], writes=[R_d])
                Acur, ATcur, Anxt, ATnxt = Abuf0, ATbuf0, Abuf1, ATbuf1
                P.op("dve", lambda g: g.tensor_tensor(Acur[0], Pd4, bc4(CI, False), ALU.add), reads=[R_Pd, R_cmb], writes=[Acur[1]])
                P.op("dve", lambda g: g.tensor_tensor(ATcur[0], PTd4, bc4(CI, False), ALU.add), reads=[R_PTd, R_cmb], writes=[ATcur[1]])

                def mm4(bank, lhs4, R_l, rhs4, R_r, start=True):
                    for g_ in range(G):
                        cs_ = slice(g_ * 128, (g_ + 1) * 128)
                        mm(pss[bank][:, cs_], lhs4[:, g_, :], rhs4[:, g_, :], start, start or g_ == G - 1, [R_l, R_r], R_ps[bank])

                def add_mm4(bank, base, lhs4, R_l, rhs4, R_r):
                    mm(pss[bank][:], ident_b, flat(base[0]), True, False, [R_cmb, base[1]], R_ps[bank])
                    mm4(bank, lhs4, R_l, rhs4, R_r, start=False)

                cur = ((Pd4, R_Pd), (PTd4, R_PTd))
                sqb = [(sq0, sqT0), (sq1, sqT1)]
                for lev in range(4):
                    (cP, R_cP), (cPT, R_cPT) = cur
                    (nP, R_nP), (nPT, R_nPT) = sqb[lev % 2]
                    mm4(3, cPT, R_cPT, cP, R_cP)
                    mm4(4, cP, R_cP, cPT, R_cPT)
                    copy_on("act", flat(nP), pss[3][:], [R_ps[3]], [R_nP])
                    copy_on("dve", flat(nPT), pss[4][:], [R_ps[4]], [R_nPT])
                    add_mm4(5, Acur, nPT, R_nPT, Acur[0], Acur[1])
                    add_mm4(6, ATcur, nP, R_nP, ATcur[0], ATcur[1])
                    copy_on("dve", flat(Anxt[0]), pss[5][:], [R_ps[5]], [Anxt[1]])
                    copy_on("act", flat(ATnxt[0]), pss[6][:], [R_ps[6]], [ATnxt[1]])
                    Acur, Anxt = Anxt, Acur
                    ATcur, ATnxt = ATnxt, ATcur
                    cur = ((nP, R_nP), (nPT, R_nPT))
                mm4(3, PTo32, R_PTo32, Acur[0], Acur[1])
                mm4(4, Po32, R_Po32, ATcur[0], ATcur[1])
                copy_on("act", flat(U1), pss[3][:], [R_ps[3]], [R_U1])
                copy_on("dve", flat(T1), pss[4][:], [R_ps[4]], [R_T1])
                add_mm4(5, Acur, ATcur[0], ATcur[1], U1, R_U1)
                add_mm4(6, ATcur, Acur[0], Acur[1], T1, R_T1)
                copy_on("dve", flat(Anxt[0]), pss[5][:], [R_ps[5]], [Anxt[1]])
                copy_on("act", flat(ATnxt[0]), pss[6][:], [R_ps[6]], [ATnxt[1]])
                Acur, Anxt = Anxt, Acur
                ATcur, ATnxt = ATnxt, ATcur
                mm4(4, Po64, R_Po64, ATcur[0], ATcur[1])
                copy_on("dve", flat(T1), pss[4][:], [R_ps[4]], [R_T1])
                add_mm4(6, ATcur, Acur[0], Acur[1], T1, R_T1)
                copy_on("act", flat(ATnxt[0]), pss[6][:], [R_ps[6]], [ATnxt[1]])
                AT4, R_AT = ATnxt
                for g_, n in enumerate(ns):
                    cs_ = slice(g_ * 128, (g_ + 1) * 128)
                    mm(pss[3][:, cs_], AT4[:, g_, :], vb_tok[:, n, :], True, True, [R_AT, R_vb], R_ps[3])
                for g_, n in enumerate(ns):
                    cs_ = slice(g_ * 128, (g_ + 1) * 128)
                    mm(pss[4][:, cs_], kbg_tok[:, n, :], AT4[:, g_, :], True, True, [R_AT, R_kbg], R_ps[4])
                copy_on("act", flat(u4), pss[3][:], [R_ps[3]], [R_u4])
                copy_on("dve", flat(wT4), pss[4][:], [R_ps[4]], [R_wT])
                for g_, n in enumerate(ns):
                    ch = chs[g_]
                    if n > 0:
                        mm(pss[0][:, 0:128], wT4[:, g_, :], Sbf, True, True, [R_wT, R_Sbf], R_ps[0])
                        P.op("dve", lambda g: g.tensor_tensor(vnew, u4[:, g_, :], pss[0][:, 0:128], ALU.subtract),
                             reads=[R_u4, R_ps[0]], writes=[R_vn])
                    else:
                        copy_on("dve", vnew, u4[:, g_, :], [R_u4], [R_vn])
                    if n > 0:
                        mm(pss[7][:, 0:128], Sbf, q_eT[:, ch], True, False, [R_Sbf, R_qe], R_ps[7])
                    mm(pss[7][:, 0:128], vnew, qk4[:, g_, :], n == 0, True, [R_vn, R_qk], R_ps[7])
                    copy_on("act", o_acc[:, 0, ch], pss[7][:, 0:128], [R_ps[7]], [R_oacc])
                    if n < NT - 1:
                        mm(pss[1][:, 0:128], kt_tok[:, n, :], vnew, True, True, [R_kt, R_vn], R_ps[1])
                        P.op("dve", lambda g: g.scalar_tensor_tensor(out=Sp, in0=Sp, scalar=cd[:, n:n + 1], in1=pss[1][:, 0:128],
                                                                     op0=ALU.mult, op1=ALU.add), reads=[R_Sp, R_cols, R_ps[1]], writes=[R_Sp])
                        copy_on("act", Sbf, Sp, [R_Sp], [R_Sbf])
            post_norm_store(l, o_acc, R_oacc, 1, [sc(51)], sz, R_sz, 4 + h, tmps, 128.0, 6)

        if stop == "pC2":
            P.barrier(); return nc
        for h in range(4):
            new_phase("diff")
            wfm = [ar.alloc([128, 16, 128], BF16, "wfm%d" % i) for i in range(3)]
            wtm = ar.alloc([128, 16, 256], BF16, "wtm")
            qT, R_q = ar.alloc([128, 2, S], BF16, "qT")
            kT, R_k = ar.alloc([128, 2, S], BF16, "kT")
            v_sb, R_v = ar.alloc([128, NT, 256], BF16, "v_sb")
            sz, R_sz = ar.alloc([128, 2, S], BF16, "sz")
            ebuf = [ar.alloc([128, 512], BF16, "e%d" % i) for i in range(3)]
            qn, R_qnb = ar.alloc([128, 512], BF16, "qn")
            ta, R_ta = ar.alloc([128, 512], F32, "ta")
            tb_, R_tb = ar.alloc([128, 512], F32, "tb")
            tO, R_tO = ar.alloc([128, 2, 512], F32, "tO")
            rs, R_rs = ar.alloc([128, 512], F32, "rs")
            sq2, R_sq2 = ar.alloc([128, 1024], BF16, "sq2")
            t1b, R_t1b = ar.alloc([128, 1024], F32, "t1b")
            rinvb, R_rinvb = ar.alloc([128, 1024], F32, "rinvb")
            qnb, R_qnb2 = ar.alloc([128, 1024], BF16, "qnb")
            tab, R_tab = ar.alloc([128, 1024], F32, "tab")
            tbb, R_tbb = ar.alloc([128, 1024], F32, "tbb")
            lamt, R_lam = ar.alloc([128, 8], F32, "lam")
            lprod, R_lprod = ar.alloc([128, 256], F32, "lprod")
            wn2, R_wn2 = ar.alloc([128, 2], F32, "wn2")
            tmps = [ar.alloc([128, 2, 512], BF16, "sq"), ar.alloc([128, 512], F32, "t1"), ar.alloc([128, 512], F32, "rinv"),
                    ar.alloc([128, 512], F32, "u"), ar.alloc([128, 512], BF16, "ost"), ar.alloc([128, 512], BF16, "ost2")]
            P.op("dve", lambda g: g.tensor_tensor(lprod[:, 0:128], sc(112, 128), sc(240, 128), ALU.mult), reads=[R_small], writes=[R_lprod])
            P.op("dve", lambda g: g.tensor_tensor(lprod[:, 128:256], sc(368, 128), sc(496, 128), ALU.mult), reads=[R_small], writes=[R_lprod])
            P.op("dve", lambda g: g.tensor_reduce(lamt[:, 0:2], lprod.rearrange("p (a b) -> p a b", b=128), AX.X, ALU.add),
                 reads=[R_lprod], writes=[R_lam])
            P.op("act", lambda g: g.activation(out=lamt[:, 2:4], in_=lamt[:, 0:2], func=AF.Exp), reads=[R_lam], writes=[R_lam])
            P.op("dve", lambda g: g.scalar_tensor_tensor(out=lamt[:, 4:5], in0=lamt[:, 3:4], scalar=-lam_init, in1=lamt[:, 2:3],
                                                         op0=ALU.add, op1=ALU.subtract), reads=[R_lam], writes=[R_lam])
            nlam = lamt[:, 4:5]
            P.op("dve", lambda g: g.tensor_scalar(wn2, sc(54, 2), 1.0 - lam_init, None, ALU.mult), reads=[R_small], writes=[R_wn2])
            units = [(which, m, half) for which in range(2) for m in range(2) for half in range(2)]
            qk_meta = ((qT, R_q, 26, 52), (kT, R_k, 34, 53))
            ustate = {}

            def qk_issue(ui):
                which, m, half = units[ui]
                g0 = qk_meta[which][2]
                if half == 0:
                    wb, R_wb = wfm[state.setdefault("wfm_i", 0) % len(wfm)]
                    state["wfm_i"] += 1
                    P.dma("pool", wb, winfm_d[l, g0 + 2 * h + m].rearrange("p (a b) -> p a b", b=128), writes=[R_wb])
                    ustate["w"] = (wb, R_wb)
                wb, R_wb = ustate["w"]
                banks = [4 + 2 * (ui % 2), 5 + 2 * (ui % 2)]
                for kc in range(16):
                    for i in range(2):
                        tb = half * 2 + i
                        mm(pss[banks[i]][:], wb[:, kc, :], big[:, kc, tb * 512:(tb + 1) * 512], kc == 0, kc == 15,
                           [R_wb, R_big], R_ps[banks[i]])

            def qk_consume(ui):
                which, m, half = units[ui]
                dstT, R_dst, g0, wcol = qk_meta[which]
                b0_ = 4 + 2 * (ui % 2)
                ps2 = psall[:, b0_ * 512:(b0_ + 2) * 512]
                Rb = [R_ps[b0_], R_ps[b0_ + 1]]
                sl2 = slice(half * 1024, (half + 1) * 1024)
                P.op("act", lambda g: g.activation(out=sq2, in_=ps2, func=AF.Square), reads=Rb, writes=[R_sq2])
                for i in range(2):
                    mm(pss[2 + i][:], ones_b, sq2[:, i * 512:(i + 1) * 512], True, True, [R_sq2, R_cmb], R_ps[2 + i])
                P.op("act", lambda g: g.activation(out=t1b, in_=psall[:, 2 * 512:4 * 512], func=AF.Sqrt, scale=1.0 / 128.0, bias=eps_ap),
                     reads=[R_ps[2], R_ps[3], R_misc], writes=[R_t1b])
                P.op("dve", lambda g: g.reciprocal(rinvb, t1b), reads=[R_t1b], writes=[R_rinvb])
                P.op("dve", lambda g: g.scalar_tensor_tensor(out=qnb, in0=ps2, scalar=sc(wcol), in1=rinvb, op0=ALU.mult, op1=ALU.mult),
                     reads=Rb + [R_rinvb, R_small], writes=[R_qnb2])
                for i in range(2):
                    mm(pss[i][:], cm_b[:, CP, :], qnb[:, i * 512:(i + 1) * 512], True, True, [R_cmb, R_qnb2], R_ps[i])
                P.op("pool", lambda g: g.tensor_tensor(tab, qnb, cosT[:, sl2], ALU.mult), reads=[R_qnb2, R_cos], writes=[R_tab])
                P.op("dve", lambda g: g.tensor_tensor(tbb, psall[:, 0:1024], sinT[:, sl2], ALU.mult), reads=[R_ps[0], R_ps[1], R_sin], writes=[R_tbb])
                P.op("pool", lambda g: g.tensor_tensor(dstT[:, m, sl2], tab, tbb, ALU.add), reads=[R_tab, R_tbb], writes=[R_dst])
            qk_issue(0)
            for ui in range(len(units)):
                if ui + 1 < len(units):
                    qk_issue(ui + 1)
                qk_consume(ui)

            def c_v(t, ps, R):
                copy_on(alt(), v_sb[:, t, :], ps, [R], [R_v])
            proj_tm(l, 2 + h, wtm, c_v, [0, 1])
            for j in range(2):
                def c_z(tb, ps, R):
                    P.op("act", lambda g: g.activation(out=sz[:, j, tb * 512:(tb + 1) * 512], in_=ps, func=AF.Silu), reads=[R], writes=[R_sz])
                proj_fm(l, 42 + 2 * h + j, 128, wfm, c_z, j)
            if h == 3:
                for fc in range(16):
                    P.dma("pool", big[:, fc, :], wout_d[l][:, fc * D:(fc + 1) * D], writes=[R_big], nowaw=fc > 0)
            scale = 128.0 ** -0.5
            steps = [(qb, m, kc) for qb in range(4) for m in range(2) for kc in range(4 * (qb + 1))]

            def qk_mm(si):
                qb, m, kc = steps[si]
                col0 = max(kc - 4 * qb, 0) * 128
                bsc = si % 2
                mm(pss[bsc][:, col0:512], kT[:, m, kc * 128:(kc + 1) * 128], qT[:, m, qb * 512 + col0:(qb + 1) * 512], True, True,
                   [R_k, R_q], R_ps[bsc])
            qk_mm(0)
            for si, (qb, m, kc) in enumerate(steps):
                qs = slice(qb * 512, (qb + 1) * 512)
                nk = 4 * (qb + 1)
                bo0, bo1, bs = (2, 3, 4) if m == 0 else (5, 6, 7)
                if si + 1 < len(steps):
                    qk_mm(si + 1)
                bsc = si % 2
                c = kc - 4 * qb
                col0 = max(c, 0) * 128
                e, R_e = ebuf[si % 3]
                P.op("act", lambda g: g.activation(out=e[:, col0:512], in_=pss[bsc][:, col0:512], func=AF.Exp, scale=scale),
                     reads=[R_ps[bsc]], writes=[R_e])
                if c >= 0:
                    P.op("pool", lambda g: g.tensor_tensor(e[:, col0:col0 + 128], e[:, col0:col0 + 128], cm_b[:, CTU, :], ALU.mult),
                         reads=[R_e, R_cmb], writes=[R_e])
                first, last = kc == 0, kc == nk - 1
                mm(pss[bo0][:, col0:512], v_sb[:, kc, 0:128], e[:, col0:512], first, last, [R_v, R_e], R_ps[bo0])
                mm(pss[bo1][:, col0:512], v_sb[:, kc, 128:256], e[:, col0:512], first, last, [R_v, R_e], R_ps[bo1])
                mm(pss[bs][:, col0:512], ones_b, e[:, col0:512], first, last, [R_cmb, R_e], R_ps[bs])
                if not last:
                    continue
                P.op("dve", lambda g: g.reciprocal(rs, pss[bs][:]), reads=[R_ps[bs]], writes=[R_rs])
                for j, bo in enumerate((bo0, bo1)):
                    if m == 0:
                        P.op("dve", lambda g: g.tensor_tensor(tO[:, j, :], pss[bo][:], rs, ALU.mult), reads=[R_ps[bo], R_rs], writes=[R_tO])
                    else:
                        P.op("dve", lambda g: g.scalar_tensor_tensor(out=ta, in0=pss[bo][:], scalar=nlam, in1=rs, op0=ALU.mult, op1=ALU.mult),
                             reads=[R_ps[bo], R_rs, R_lam], writes=[R_ta])
                        P.op("pool", lambda g: g.tensor_tensor(tO[:, j, :], tO[:, j, :], ta, ALU.add), reads=[R_tO, R_ta], writes=[R_tO])
                if m == 0:
                    continue
                (sq, R_sq), (t1, R_t1), (rinv, R_rinv), (u, R_u) = tmps[0:4]
                rstd_part([(tO[:, j, :], R_tO) for j in range(2)], 512, 256.0, tmps[0:3], 4)
                for j in range(2):
                    ost, R_ost = tmps[4 + j]
                    P.op("dve", lambda g: g.scalar_tensor_tensor(out=u, in0=tO[:, j, :], scalar=wn2[:, j:j + 1], in1=rinv, op0=ALU.mult, op1=ALU.mult),
                         reads=[R_tO, R_rinv, R_wn2], writes=[R_u])
                    P.op("pool", lambda g: g.tensor_tensor(ost, u, sz[:, j, qs], ALU.mult), reads=[R_u, R_sz], writes=[R_ost])
                    P.dma("sp", oT_d[8 + 2 * h + j, :, qs], ost, reads=[R_ost], writes=[R_oT], nowaw=True)

        if stop == "pC3":
            P.barrier(); return nc
        new_phase("D")
        wo = big
        xts = [ar.alloc([128, D], F32, "xt%d" % i) for i in range(2)]
        obs = [ar.alloc([128, 16, 512], BF16, "ob%d" % i) for i in range(2)]
        xos = [ar.alloc([128, D], F32, "xo%d" % i) for i in range(2)]
        ytmp, R_yt = ar.alloc([128, 512], F32, "ytmp")
        xsrc = x_d if l == 0 else x1_d
        xdst = out_d if l == L - 1 else x1_d
        R_dst = R_out if l == L - 1 else R_x1
        for tb in range(4):
            ob, R_ob = obs[tb % 2]
            P.dma("sp", ob, oT_d[:, :, tb * 512:(tb + 1) * 512].rearrange("c p t -> p c t"), reads=[R_oT], writes=[R_ob])
            for i in range(4):
                t = tb * 4 + i
                xt, R_xt = xts[t % 2]
                xo, R_xo = xos[t % 2]
                P.dma("sp", xt, xsrc[t * 128:(t + 1) * 128, :], reads=[R_x1] if l > 0 else [], writes=[R_xt])
                for ng in range(4):
                    ns = slice(ng * 512, (ng + 1) * 512)
                    bk = (t * 4 + ng) % 4
                    for fc in range(16):
                        mm(pss[bk][:], ob[:, fc, i * 128:(i + 1) * 128], wo[:, fc, ns], fc == 0, fc == 15, [R_ob, R_big], R_ps[bk])
                    P.op("dve", lambda g: g.tensor_tensor(ytmp, pss[bk][:], gate_bc[:, ns], ALU.mult), reads=[R_ps[bk], R_gate], writes=[R_yt])
                    P.op("pool", lambda g: g.tensor_tensor(xo[:, ns], ytmp, xt[:, ns], ALU.add), reads=[R_yt, R_xt], writes=[R_xo])
                P.dma("sp", xdst[t * 128:(t + 1) * 128, :], xo, reads=[R_xo], writes=[R_dst], nowaw=True)

    P.barrier()
    if scopes and state.get("scope") is not None:
        nc.leave_named_scope(state["scope"][0], state["scope"][1], False)
    return nc


def _col(v):
    return np.ascontiguousarray(np.asarray(v, np.float32).reshape(-1, 128).T)


def _consts():
    p = np.arange(128)[:, None]
    j = np.arange(128)[None, :]
    cm = np.zeros((128, NCM, 128), np.float32)
    cm[:, CI] = (p == j)
    cm[:, CO] = 1.0
    prot = np.zeros((128, 128), np.float32)
    prot[(j[0, :64] + 64), j[0, :64]] = -1.0
    prot[(j[0, 64:] - 64), j[0, 64:]] = 1.0
    cm[:, CP] = prot
    cm[:, CTU] = (p <= j)
    cm[:, CNSU] = -1.0 * (p < j)
    cm[:, CNSL] = -1.0 * (p > j)
    bd32 = (p // 32 == j // 32)
    bd64 = (p // 64 == j // 64)
    cm[:, CB32] = bd32
    cm[:, CO32] = bd64 & ~bd32
    cm[:, CO64] = ~bd64
    selm = np.zeros((8, 8, 128), np.float32)
    for k in range(8):
        selm[k, k, :] = 1.0
    rmask = np.ones((128, S), np.float32)
    rmask[:, 0::128] = 0.0
    half = 64
    inv_freq = (10000.0 ** (-(np.arange(half, dtype=np.float32) / np.float32(half)))).astype(np.float32)
    invf = np.concatenate([inv_freq, inv_freq]).astype(np.float64) / (2.0 * math.pi)
    return cm.reshape(128, NCM * 128), selm.reshape(8, 8 * 128), rmask, invf.astype(np.float32)


FM_GROUPS = ([(0, 128), (128, 128), (256, 128), (384, 128), (1024, 16)] + [(1040 + 128 * i, 128) for i in range(4)]
             + [(1552 + 128 * i, 128) for i in range(12)] + [(3088, 8)] + [(3096 + 128 * i, 128) for i in range(4)]
             + [(3608 + 128 * i, 128) for i in range(8)] + [(4632 + 128 * i, 128) for i in range(8)]
             + [(6680 + 128 * i, 128) for i in range(8)])
TM_GROUPS = [(512, 256), (768, 256)] + [(5656 + 256 * i, 256) for i in range(4)]


def _prep_shared(inp):
    f = lambda k: np.asarray(inp[k], np.float32)
    cm, selm, rmask, invf = _consts()
    w_in = f("w_in")
    winfm = np.zeros((2, 50, 128, 16, 128), np.float32)
    wintm = np.zeros((2, 6, 128, 16, 256), np.float32)
    for l in range(2):
        wl = w_in[l].reshape(16, 128, -1)
        for gi, (c0, n) in enumerate(FM_GROUPS):
            winfm[l, gi, :, :, :n] = wl[:, :, c0:c0 + n].transpose(1, 0, 2)
        for gi, (c0, n) in enumerate(TM_GROUPS):
            wintm[l, gi] = wl[:, :, c0:c0 + n].transpose(1, 0, 2)
    wada = np.ascontiguousarray(f("w_ada").reshape(2, 16, 128, 48, 128).transpose(0, 3, 2, 1, 4)).reshape(2, 48, 128, 16 * 128)
    wout = np.ascontiguousarray(f("w_out").reshape(2, 16, 128, D).transpose(0, 2, 1, 3)).reshape(2, 128, 16 * D)
    bgate = np.ascontiguousarray(f("b_ada")[:, 2 * D:].reshape(2, 1, D))
    sm = np.zeros((128, NS), np.float32)
    sm[:, 16] = invf
    for l in range(2):
        b = SBASE + l * SLW
        sm[:, b:b + 16] = _col(f("norm_w")[l])
        sm[:, b + 16:b + 48] = _col(f("b_ada")[l, :2 * D])
        sm[:, b + 48:b + 50] = _col(f("gla_b_lr")[l])
        sm[:, b + 50] = f("gla_norm_w")[l]
        sm[:, b + 51] = f("gdn_norm_w")[l]
        sm[:, b + 52] = f("diff_q_norm_w")[l]
        sm[:, b + 53] = f("diff_k_norm_w")[l]
        sm[:, b + 54:b + 56] = _col(f("diff_norm_w")[l])
        sm[:, b + 56:b + 60] = f("gdn_a_log")[l][None, :]
        sm[:, b + 60:b + 64] = f("gdn_dt_bias")[l][None, :]
        cw = f("gdn_conv_w")[l]
        sm[:, b + 64:b + 112] = cw.reshape(4, 12, 128).transpose(2, 1, 0).reshape(128, 48)
        sm[:, b + 112:b + 624] = f("diff_lambda")[l].reshape(1, 512)
        sm[0:16, b + 624:b + 880] = f("gla_w_lr")[l]
    shared = {"cmat": cm, "selm": selm, "rmask": rmask, "wada": wada, "bgate": bgate,
              "winfm": winfm.reshape(2, 50, 128, 16 * 128), "wintm": wintm.reshape(2, 6, 128, 16 * 256), "wout": wout}
    return shared, sm


def make_in_maps(inp, cores):
    shared, sm = _prep_shared(inp)
    x = np.asarray(inp["x"], np.float32)
    c = np.asarray(inp["c"], np.float32)
    pos = np.asarray(inp["positions"], np.int32)
    maps = []
    for b in cores:
        s = sm.copy()
        s[:, 0:16] = _col(c[b])
        m = dict(shared)
        m["x"] = np.ascontiguousarray(x[b])
        m["small"] = s
        m["pos"] = np.ascontiguousarray(pos[b:b + 1])
        maps.append(m)
    return maps


_NC_CACHE = {}


def kernel(**inputs):
    if "nc" not in _NC_CACHE:
        _NC_CACHE["nc"] = build(2)
    nc = _NC_CACHE["nc"]
    cores = [i // 2 for i in range(8)]
    maps = make_in_maps(inputs, cores)
    res = run_bass_kernel_spmd(nc, maps, core_ids=list(range(8)))
    out = np.stack([np.asarray(res.results[2 * b]["out"], np.float32) for b in range(4)], axis=0)
    return out
```

```python
import math
import numpy as np
import concourse.bass as bass
import concourse.mybir as mybir
from concourse.bass_utils import run_bass_kernel_spmd

F32 = mybir.dt.float32
BF16 = mybir.dt.bfloat16
I32 = mybir.dt.int32
AF = mybir.ActivationFunctionType
ALU = mybir.AluOpType
AX = mybir.AxisListType

S = 2048
D = 2048
NT = 16
EPS = 1e-6
SLW = 880
SBASE = 32
NS = SBASE + 2 * SLW
CI, CO, CP, CTU, CNSU, CNSL, CB32, CO32, CO64 = range(9)
NCM = 9


class Reg:
    __slots__ = ("name", "lw", "rd", "dsem", "local", "psum")

    def __init__(self, name, local=False, psum=False):
        self.name = name
        self.psum = psum
        self.lw = None
        self.rd = {}
        self.dsem = None
        self.local = local


class Prog:
    def __init__(self, nc):
        self.nc = nc
        self.eng = {"pe": nc.tensor, "act": nc.scalar, "dve": nc.vector, "pool": nc.gpsimd, "sp": nc.sync}
        self.sem, self.cnt, self.semobj = {}, {}, {}
        self.seen = {e: {} for e in self.eng}
        for e in self.eng:
            s = nc.alloc_semaphore(name="s_" + e)
            self.sem[e] = s
            self.semobj[e] = s
            self.cnt[e] = 0
        self.vc = {}
        self.dcnt = {}
        self.local_keys = []
        self.local_next = 0
        self.ninstr = 0
        self.nwait = 0

    def _wait(self, e, key, val):
        if self.seen[e].get(key, 0) >= val:
            return
        self.eng[e].wait_ge(self.semobj[key], val)
        self.nwait += 1
        se = self.seen[e]
        se[key] = val
        clk = self.vc.get((key, val))
        if clk:
            for k, v in clk.items():
                if se.get(k, 0) < v:
                    se[k] = v

    def _deps(self, e, reads, writes, pe_accum=False, nowaw=False):
        need = {}

        def add(k, v):
            if need.get(k, 0) < v:
                need[k] = v
        for r in reads:
            if r.lw is not None:
                add(*r.lw)
            if r.psum:
                for k, v in r.rd.items():
                    if k != e:
                        add(k, v)
        for w in writes:
            if w.lw is not None and not nowaw and not (pe_accum and w.lw[0] == "pe"):
                add(*w.lw)
            for k, v in w.rd.items():
                add(k, v)
        for k, v in sorted(need.items(), key=lambda kv: -kv[1] if isinstance(kv[0], str) else 0):
            self._wait(e, k, v)

    def _record(self, ev, reads, writes, nowaw=False):
        for r in reads:
            if r.rd.get(ev[0], 0) < ev[1]:
                r.rd[ev[0]] = ev[1]
        for w in writes:
            w.lw = ev
            if not nowaw:
                w.rd = {}

    def op(self, e, fn, reads=(), writes=(), pe_accum=False):
        self._deps(e, reads, writes, pe_accum)
        ins = fn(self.eng[e])
        self.cnt[e] += 1
        ins.then_inc(self.sem[e], 1)
        ev = (e, self.cnt[e])
        self.vc[ev] = dict(self.seen[e])
        self._record(ev, reads, writes)
        self.ninstr += 1

    def _newkey(self):
        key = ("d", len(self.dcnt))
        self.semobj[key] = self.nc.alloc_semaphore(name="d_%d" % len(self.dcnt))
        self.dcnt[key] = 0
        return key

    def _dkey(self, w):
        if w.dsem is None:
            if w.local:
                if self.local_next == len(self.local_keys):
                    self.local_keys.append(self._newkey())
                w.dsem = self.local_keys[self.local_next]
                self.local_next += 1
            else:
                w.dsem = self._newkey()
        return w.dsem

    def dma(self, q, out_ap, in_ap, reads=(), writes=(), nowaw=False, **kw):
        w = writes[0]
        self._deps(q, reads, writes, nowaw=nowaw)
        key = self._dkey(w)
        ins = self.eng[q].dma_start(out=out_ap, in_=in_ap, **kw)
        self.dcnt[key] += 16
        ins.then_inc(self.semobj[key], 16)
        ev = (key, self.dcnt[key])
        self.vc[ev] = dict(self.seen[q])
        self._record(ev, reads, writes, nowaw=nowaw)
        self.ninstr += 1

    def barrier(self, reset=True):
        evs = [(e, self.cnt[e]) for e in self.eng if self.cnt[e] > 0]
        evs += [(k, v) for k, v in self.dcnt.items() if v > 0]
        for e in self.eng:
            for k, v in evs:
                if k != e:
                    self._wait(e, k, v)
        if reset:
            self.local_next = 0


class Arena:
    def __init__(self, nc, nwords):
        self.t = nc.alloc_sbuf_tensor("arena", [128, nwords], F32)
        self.n = nwords
        self.off = 0
        self.k = 0

    def reset(self):
        self.off = 0

    def alloc(self, shape, dt, name=None):
        free = int(np.prod(shape[1:]))
        words = free if dt in (F32, I32) else (free + 1) // 2
        words = (words + 7) // 8 * 8
        assert self.off + words <= self.n, ("arena overflow", name, self.off, words, self.n)
        v = self.t[:, self.off:self.off + words]
        self.off += words
        if dt == BF16:
            v = v.bitcast(BF16)[:, 0:free]
        elif dt == I32:
            v = v.bitcast(I32)[:, 0:free]
        else:
            v = v[:, 0:free]
        if len(shape) == 3:
            v = v.rearrange("p (a b) -> p a b", b=shape[2])
        self.k += 1
        if shape[0] < 128:
            v = v[0:shape[0]]
        return v, Reg(name or "a%d" % self.k, local=True)


def build(nlayers=2, debug=False, stop=None, scopes=False):
    nc = bass.Bass("TRN2", target_bir_lowering=False)
    P = Prog(nc)
    L = nlayers

    def din(name, shape, dt=F32):
        return nc.dram_tensor(name, shape, dt, kind="ExternalInput").ap()

    x_d = din("x", [S, D])
    small_d = din("small", [128, NS])
    cmat_d = din("cmat", [128, NCM * 128])
    selm_d = din("selm", [8, 8 * 128])
    rmask_d = din("rmask", [128, S])
    pos_d = din("pos", [1, S], I32)
    wada_d = din("wada", [2, 48, 128, 16 * 128])
    bgate_d = din("bgate", [2, 1, D])
    winfm_d = din("winfm", [2, 50, 128, 16 * 128])
    wintm_d = din("wintm", [2, 6, 128, 16 * 256])
    wout_d = din("wout", [2, 128, 16 * D])
    out_d = nc.dram_tensor("out", [S, D], F32, kind="ExternalOutput").ap()
    x1_d = nc.dram_tensor("x1s", [S, D], F32, kind="Internal").ap()
    oT_d = nc.dram_tensor("oTs", [16, 128, S], BF16, kind="ExternalOutput" if debug else "Internal").ap()
    R_out, R_x1, R_oT = Reg("out"), Reg("x1"), Reg("oT")

    def sb(name, shape, dt):
        return nc.alloc_sbuf_tensor("sb_" + name, shape, dt), Reg(name)

    big, R_big = sb("big", [128, 16, S], BF16)
    cosT, R_cos = sb("cosT", [128, S], BF16)
    sinT, R_sin = sb("sinT", [128, S], BF16)
    gate_bc, R_gate = sb("gate_bc", [128, D], F32)
    rmask, R_rmask = sb("rmask", [128, S], BF16)
    cm_f, R_cmf = sb("cm_f", [128, NCM, 128], F32)
    cm_b, R_cmb = sb("cm_b", [128, NCM, 128], BF16)
    selm, R_selm = sb("selm", [8, 8, 128], F32)
    small, R_small = sb("small", [128, NS], F32)
    modc, R_modc = sb("modc", [128, 48], F32)
    misc, R_misc = sb("misc", [128, 64], F32)
    ar = Arena(nc, (nc.sbuf_bytes_remaining - 2048) // 4)

    psall = nc.alloc_psum_tensor("psall", [128, 8 * 512], F32)
    pss = [psall[:, i * 512:(i + 1) * 512] for i in range(8)]
    R_ps = [Reg("ps%d" % i, psum=True) for i in range(8)]

    state = {"alt": 0}

    def alt():
        state["alt"] ^= 1
        return "act" if state["alt"] else "dve"

    def copy_on(e, out, in_, reads, writes):
        if e == "act":
            P.op("act", lambda g: g.activation(out=out, in_=in_, func=AF.Copy), reads=reads, writes=writes)
        else:
            P.op(e, lambda g: g.tensor_copy(out, in_), reads=reads, writes=writes)

    def mm(out, lhsT, rhs, start, stop, reads, w):
        P.op("pe", lambda g: g.matmul(out, lhsT=lhsT, rhs=rhs, start=start, stop=stop),
             reads=reads, writes=[w], pe_accum=not start)

    def new_phase(name="ph"):
        P.barrier()
        ar.reset()
        if scopes:
            if state.get("scope") is not None:
                nc.leave_named_scope(state["scope"][0], state["scope"][1], False)
            nm = "%s_%d" % (name, state.setdefault("nscope", 0))
            state["nscope"] += 1
            sid, _ = nc.enter_named_scope(nm, False)
            state["scope"] = (nm, sid)

    ident_b = cm_b[:, CI, :]
    ones_b = cm_b[:, CO, :]
    ident_f = cm_f[:, CI, :]

    def rstd_part(srcs, n, denom, tmps, psi):
        (sq, R_sq), (t1, R_t1), (rinv, R_rinv) = tmps
        for i, (sap, sreg) in enumerate(srcs):
            P.op("act", lambda g: g.activation(out=sq[:, i, 0:n], in_=sap, func=AF.Square), reads=[sreg], writes=[R_sq])
        for i in range(len(srcs)):
            mm(pss[psi][:, 0:n], ones_b, sq[:, i, 0:n], i == 0, i == len(srcs) - 1, [R_sq, R_cmb], R_ps[psi])
        P.op("act", lambda g: g.activation(out=t1[:, 0:n], in_=pss[psi][:, 0:n], func=AF.Sqrt, scale=1.0 / denom, bias=eps_ap),
             reads=[R_ps[psi], R_misc], writes=[R_t1])
        P.op("dve", lambda g: g.reciprocal(rinv[:, 0:n], t1[:, 0:n]), reads=[R_t1], writes=[R_rinv])
        return rinv, R_rinv

    P.dma("sp", small[:], small_d, writes=[R_small])
    P.dma("sp", cm_f[:], cmat_d.rearrange("p (a b) -> p a b", b=128), writes=[R_cmf])
    P.dma("sp", selm[:], selm_d.rearrange("p (a b) -> p a b", b=128), writes=[R_selm])
    P.dma("pool", rmask[:], rmask_d, writes=[R_rmask])
    P.op("dve", lambda g: g.tensor_copy(cm_b[:], cm_f[:]), reads=[R_cmf], writes=[R_cmb])
    eps_ap = misc[:, 0:1]
    P.op("dve", lambda g: g.memset(misc[:], 0.0), writes=[R_misc])
    P.op("dve", lambda g: g.memset(misc[:, 0:1], EPS), reads=[], writes=[R_misc])
    P.op("dve", lambda g: g.memset(misc[:, 1:2], 1.0), reads=[], writes=[R_misc])
    one_ap = misc[:, 1:2]

    posi, R_posi = ar.alloc([128, S], I32, "posi")
    y, R_y = ar.alloc([128, S], F32, "y")
    yi, R_yi = ar.alloc([128, S], I32, "yi")
    yf, R_yf = ar.alloc([128, S], F32, "yf")
    fr, R_fr = ar.alloc([128, S], F32, "fr")
    m1, R_m1 = ar.alloc([128, S], F32, "m1")
    P.dma("sp", posi, pos_d.to_broadcast([128, S]), writes=[R_posi])
    P.op("dve", lambda g: g.tensor_copy(y, posi), reads=[R_posi], writes=[R_y])
    P.op("dve", lambda g: g.tensor_scalar(y, y, small[:, 16:17], None, ALU.mult), reads=[R_y, R_small], writes=[R_y])
    P.op("dve", lambda g: g.tensor_copy(yi, y), reads=[R_y], writes=[R_yi])
    P.op("dve", lambda g: g.tensor_copy(yf, yi), reads=[R_yi], writes=[R_yf])
    P.op("dve", lambda g: g.tensor_tensor(fr, y, yf, ALU.subtract), reads=[R_y, R_yf], writes=[R_fr])
    for which, dst, R_dst in ((0, sinT, R_sin), (1, cosT, R_cos)):
        src = fr
        if which == 1:
            P.op("dve", lambda g: g.tensor_scalar(y, fr, 0.25, None, ALU.add), reads=[R_fr], writes=[R_y])
            src = y
        R_src = R_fr if which == 0 else R_y
        P.op("dve", lambda g: g.tensor_scalar(m1, src, 0.5, None, ALU.is_gt), reads=[R_src], writes=[R_m1])
        P.op("dve", lambda g: g.tensor_tensor(yf, src, m1, ALU.subtract), reads=[R_src, R_m1], writes=[R_yf])
        P.op("dve", lambda g: g.tensor_scalar(m1, yf, -0.5, None, ALU.is_lt), reads=[R_yf], writes=[R_m1])
        P.op("dve", lambda g: g.tensor_tensor(yf, yf, m1, ALU.add), reads=[R_yf, R_m1], writes=[R_yf])
        P.op("act", lambda g: g.activation(out=dst[:], in_=yf, func=AF.Sin, scale=2.0 * math.pi), reads=[R_yf], writes=[R_dst])

    if stop == "p0":
        P.barrier(); return nc
    def proj_fm(l, gi, M, wbufs, consume, bankset):
        wb, R_wb = wbufs[state.setdefault("wfm_i", 0) % len(wbufs)]
        state["wfm_i"] += 1
        P.dma("pool", wb, winfm_d[l, gi].rearrange("p (a b) -> p a b", b=128), writes=[R_wb])
        banks = [bankset * 4 + i for i in range(4)]
        for kc in range(16):
            for tb in range(4):
                mm(pss[banks[tb]][0:M, :], wb[:, kc, 0:M], big[:, kc, tb * 512:(tb + 1) * 512], kc == 0, kc == 15,
                   [R_wb, R_big], R_ps[banks[tb]])
        for tb in range(4):
            consume(tb, pss[banks[tb]][0:M, :], R_ps[banks[tb]])

    def proj_tm(l, gi, wtb, consume, banks):
        wb, R_wb = wtb
        P.dma("pool", wb, wintm_d[l, gi].rearrange("p (a b) -> p a b", b=256), writes=[R_wb])
        for t in range(NT):
            bk = banks[t % len(banks)]
            for kc in range(16):
                mm(pss[bk][:, 0:256], big[:, kc, t * 128:(t + 1) * 128], wb[:, kc, :], kc == 0, kc == 15,
                   [R_wb, R_big], R_ps[bk])
            consume(t, pss[bk][:, 0:256], R_ps[bk])

    def post_norm_store(l, o_acc, R_oacc, nsub, wcols, szs, R_sz, fc0, tmps, denom, psi):
        (sq, R_sq), (t1, R_t1), (rinv, R_rinv), (u, R_u) = tmps[0:4]
        for blk in range(4):
            sl = slice(blk * 512, (blk + 1) * 512)
            rstd_part([(o_acc[:, j, sl], R_oacc) for j in range(nsub)], 512, denom, tmps[0:3], psi)
            for j in range(nsub):
                P.op("dve", lambda g: g.scalar_tensor_tensor(out=u[:, 0:512], in0=o_acc[:, j, sl], scalar=wcols[j], in1=rinv[:, 0:512],
                                                             op0=ALU.mult, op1=ALU.mult),
                     reads=[R_oacc, R_rinv, R_small, R_misc], writes=[R_u])
                ost, R_ost = tmps[4 + state.setdefault("ost_i", 0) % 2]
                state["ost_i"] += 1
                P.op("pool", lambda g: g.tensor_tensor(ost[:, 0:512], u[:, 0:512], szs[:, j, sl], ALU.mult),
                     reads=[R_u, R_sz], writes=[R_ost])
                P.dma("sp", oT_d[fc0 + j, :, sl], ost[:, 0:512], reads=[R_ost], writes=[R_oT], nowaw=True)

    for l in range(L):
        sb_l = SBASE + l * SLW
        lam_init = 0.8 - 0.6 * math.exp(-0.3 * l)

        def sc(off, n=1):
            return small[:, sb_l + off: sb_l + off + n]

        new_phase("A")
        cact, R_cact = ar.alloc([128, 16], F32, "cact")
        c2, R_c2 = ar.alloc([128, 16, 2], F32, "c2")
        crep, R_crep = ar.alloc([128, 16, 128], F32, "crep")
        bg, R_bg = ar.alloc([128, D], F32, "bg")
        wab = [ar.alloc([128, 16, 128], F32, "wa%d" % i) for i in range(3)]
        P.op("act", lambda g: g.activation(out=cact, in_=small[:, 0:16], func=AF.Silu), reads=[R_small], writes=[R_cact])
        P.op("dve", lambda g: g.tensor_copy(c2, cact.unsqueeze(2).to_broadcast([128, 16, 2])), reads=[R_cact], writes=[R_c2])
        P.op("dve", lambda g: g.tensor_copy(crep, cact.unsqueeze(2).to_broadcast([128, 16, 128])), reads=[R_cact], writes=[R_crep])
        P.dma("sp", bg, bgate_d[l].to_broadcast([128, D]), writes=[R_bg])
        for g_ in range(48):
            wa, R_wa = wab[g_ % 3]
            P.dma("sp", wa, wada_d[l, g_].rearrange("p (a b) -> p a b", b=128), writes=[R_wa])
            if g_ < 32:
                for kc in range(16):
                    mm(pss[0][:, 2 * g_:2 * g_ + 2], wa[:, kc, :], c2[:, kc, :], kc == 0, kc == 15, [R_wa, R_c2], R_ps[0])
                if g_ == 31:
                    P.op("dve", lambda g: g.tensor_tensor(modc[:, 0:32], pss[0][:, 0:64:2], sc(16, 32), ALU.add),
                         reads=[R_ps[0], R_small], writes=[R_modc])
                    P.op("dve", lambda g: g.scalar_tensor_tensor(out=modc[:, 32:48], in0=modc[:, 16:32], scalar=1.0, in1=sc(0, 16),
                                                                 op0=ALU.add, op1=ALU.mult),
                         reads=[R_modc, R_small], writes=[R_modc])
            else:
                gg = g_ - 32
                bk = 1 + (gg // 4) % 2
                c0 = (gg % 4) * 128
                for kc in range(16):
                    mm(pss[bk][:, c0:c0 + 128], crep[:, kc, :], wa[:, kc, :], kc == 0, kc == 15, [R_wa, R_crep], R_ps[bk])
                if gg % 4 == 3:
                    sl = slice((gg // 4) * 512, (gg // 4 + 1) * 512)
                    P.op("dve", lambda g: g.tensor_tensor(gate_bc[:, sl], pss[bk][:], bg[:, sl], ALU.add),
                         reads=[R_ps[bk], R_bg], writes=[R_gate])

        if stop == "pA":
            P.barrier(); return nc
        new_phase("B")
        xsrc = x_d if l == 0 else x1_d
        xts = [ar.alloc([128, D], F32, "xt%d" % i) for i in range(2)]
        xn, R_xn = ar.alloc([128, 4, D], BF16, "xn")
        junk, R_junk = ar.alloc([128, D], BF16, "junk")
        ssq, R_ssq = ar.alloc([128, 8], F32, "ssq")
        hT = big
        for tb in range(4):
            for i in range(4):
                t = tb * 4 + i
                xt, R_xt = xts[t % 2]
                P.dma("sp", xt, xsrc[t * 128:(t + 1) * 128, :], reads=[R_x1] if l > 0 else [], writes=[R_xt])
                P.op("act", lambda g: g.activation(out=junk, in_=xt, func=AF.Square, accum_out=ssq[:, 0:1]),
                     reads=[R_xt], writes=[R_junk, R_ssq])
                P.op("act", lambda g: g.activation(out=ssq[:, 1:2], in_=ssq[:, 0:1], func=AF.Sqrt, scale=1.0 / D, bias=eps_ap),
                     reads=[R_ssq, R_misc], writes=[R_ssq])
                P.op("dve", lambda g: g.reciprocal(ssq[:, 2:3], ssq[:, 1:2]), reads=[R_ssq], writes=[R_ssq])
                P.op("dve", lambda g: g.tensor_scalar(xn[:, i, :], xt, ssq[:, 2:3], None, ALU.mult),
                     reads=[R_xt, R_ssq], writes=[R_xn])
            for fc in range(16):
                bk = fc % 4
                pT = pss[bk][:].bitcast(BF16)
                for i in range(4):
                    P.op("pe", lambda g: g.transpose(pT[:, i * 128:(i + 1) * 128], xn[:, i, fc * 128:(fc + 1) * 128], ident_b),
                         reads=[R_xn, R_cmb], writes=[R_ps[bk]], pe_accum=i > 0)
                dst = hT[:, fc, tb * 512:(tb + 1) * 512]
                if alt() == "act":
                    P.op("act", lambda g: g.activation(out=dst, in_=pT[:, 0:512], func=AF.Identity,
                                                       scale=modc[:, 32 + fc:33 + fc], bias=modc[:, fc:fc + 1]),
                         reads=[R_ps[bk], R_modc], writes=[R_big])
                else:
                    P.op("dve", lambda g: g.tensor_scalar(dst, pT[:, 0:512], modc[:, 32 + fc:33 + fc], modc[:, fc:fc + 1],
                                                          ALU.mult, ALU.add),
                         reads=[R_ps[bk], R_modc], writes=[R_big])

        if stop == "pB":
            P.barrier(); return nc
        for pr in range(2):
            new_phase("gla")
            wfm = [ar.alloc([128, 16, 128], BF16, "wfm%d" % i) for i in range(3)]
            wtm = ar.alloc([128, 16, 256], BF16, "wtm")
            glrT, R_glrT = ar.alloc([16, S], BF16, "glrT")
            wlr, R_wlr = ar.alloc([16, 256], BF16, "wlr")
            bcs, R_bcs = ar.alloc([128, S], F32, "bcs")
            eb, R_eb = ar.alloc([128, S], F32, "eb")
            q_eT, R_qe = ar.alloc([128, S], BF16, "q_eT")
            k_eT, R_ke = ar.alloc([128, S], BF16, "k_eT")
            v_g, R_vg = ar.alloc([128, NT, 256], BF16, "v_g")
            sz, R_sz = ar.alloc([128, 2, S], BF16, "sz")
            ke_tok, R_ket = ar.alloc([128, NT, 128], BF16, "ke_tok")
            o_acc, R_oacc = ar.alloc([128, 2, S], F32, "o_acc")
            e1, R_e1 = ar.alloc([128, 512], F32, "e1")
            dec, R_dec = ar.alloc([128, 16], F32, "dec")
            nb, R_nb = ar.alloc([128, 2], F32, "nb")
            Sp, R_Sp = ar.alloc([128, 256], F32, "Sp")
            Stmp, R_Stmp = ar.alloc([128, 256], F32, "Stmp")
            Sbf, R_Sbf = ar.alloc([128, 256], BF16, "Sbf")
            attm, R_attm = ar.alloc([128, 2, 128], BF16, "attm")
            tmps = [ar.alloc([128, 2, 512], BF16, "sq"), ar.alloc([128, 512], F32, "t1"), ar.alloc([128, 512], F32, "rinv"),
                    ar.alloc([128, 512], F32, "u"), ar.alloc([128, 512], BF16, "ost"), ar.alloc([128, 512], BF16, "ost2")]

            def c_glr(tb, ps, R):
                copy_on("act", glrT[0:16, tb * 512:(tb + 1) * 512], ps, [R], [R_glrT])
            proj_fm(l, 4, 16, wfm, c_glr, 0)
            P.op("dve", lambda g: g.tensor_copy(wlr, sc(624, 256)[0:16, :]), reads=[R_small], writes=[R_wlr])
            P.op("dve", lambda g: g.tensor_scalar(nb, sc(48, 2), -1.0, None, ALU.mult), reads=[R_small], writes=[R_nb])
            for blk in range(4):
                sl = slice(blk * 512, (blk + 1) * 512)
                bk = 4 + blk % 2
                mm(pss[bk][:], wlr[0:16, pr * 128:(pr + 1) * 128], glrT[0:16, sl], True, True, [R_wlr, R_glrT], R_ps[bk])
                P.op("act", lambda g: g.activation(out=e1, in_=pss[bk][:], func=AF.Exp, scale=-1.0, bias=nb[:, pr:pr + 1]),
                     reads=[R_ps[bk], R_nb], writes=[R_e1])
                P.op("act", lambda g: g.activation(out=bcs[:, sl], in_=e1, func=AF.Ln, bias=one_ap), reads=[R_e1, R_misc], writes=[R_bcs])
            P.op("dve", lambda g: g.tensor_tensor_scan(out=bcs, data0=rmask[:], data1=bcs, initial=0.0, op0=ALU.mult, op1=ALU.add),
                 reads=[R_rmask, R_bcs], writes=[R_bcs])
            P.op("act", lambda g: g.activation(out=eb, in_=bcs, func=AF.Exp, scale=-1.0 / 16.0), reads=[R_bcs], writes=[R_eb])
            P.op("dve", lambda g: g.tensor_copy(dec, eb[:, 127:S:128]), reads=[R_eb], writes=[R_dec])
            P.op("act", lambda g: g.activation(out=bcs, in_=bcs, func=AF.Exp, scale=1.0 / 16.0), reads=[R_bcs], writes=[R_bcs])
            enb = bcs

            def c_q(tb, ps, R):
                sl = slice(tb * 512, (tb + 1) * 512)
                P.op("dve", lambda g: g.scalar_tensor_tensor(out=q_eT[:, sl], in0=ps, scalar=0.125, in1=eb[:, sl], op0=ALU.mult, op1=ALU.mult),
                     reads=[R, R_eb], writes=[R_qe])
            proj_fm(l, pr, 128, wfm, c_q, 1)

            def c_k(tb, ps, R):
                sl = slice(tb * 512, (tb + 1) * 512)
                P.op("dve", lambda g: g.tensor_tensor(k_eT[:, sl], ps, enb[:, sl], ALU.mult), reads=[R, R_bcs], writes=[R_ke])
            proj_fm(l, 2 + pr, 128, wfm, c_k, 0)

            def c_v(t, ps, R):
                copy_on("act", v_g[:, t, :], ps, [R], [R_vg])
            proj_tm(l, pr, wtm, c_v, [4, 5])
            for hh in range(2):
                def c_z(tb, ps, R):
                    P.op("act", lambda g: g.activation(out=sz[:, hh, tb * 512:(tb + 1) * 512], in_=ps, func=AF.Silu), reads=[R], writes=[R_sz])
                proj_fm(l, 5 + 2 * pr + hh, 128, wfm, c_z, hh)
            for t4 in range(4):
                bk = 6 + t4 % 2
                pT = pss[bk][:].bitcast(BF16)
                for i in range(4):
                    t = t4 * 4 + i
                    P.op("pe", lambda g: g.transpose(pT[:, i * 128:(i + 1) * 128], k_eT[:, t * 128:(t + 1) * 128], ident_b),
                         reads=[R_ke, R_cmb], writes=[R_ps[bk]], pe_accum=i > 0)
                P.op("dve", lambda g: g.tensor_copy(ke_tok[:, t4 * 4:(t4 + 1) * 4, :], pT[:, 0:512].rearrange("p (a b) -> p a b", b=128)),
                     reads=[R_ps[bk]], writes=[R_ket])
            P.op("dve", lambda g: g.memset(Sp, 0.0), writes=[R_Sp])
            for n in range(NT):
                ch = slice(n * 128, (n + 1) * 128)
                ba, bo, bkv = n % 2, 2 + n % 2, 4 + n % 2
                for hh in range(2):
                    hp = slice(64 * hh, 64 * hh + 64)
                    mm(pss[ba][:, hh * 128:(hh + 1) * 128], k_eT[hp, ch], q_eT[hp, ch], True, True, [R_ke, R_qe], R_ps[ba])
                P.op("dve", lambda g: g.tensor_tensor(attm, pss[ba][:, 0:256].rearrange("p (a b) -> p a b", b=128),
                                                      cm_f[:, CTU:CTU + 1, :].to_broadcast([128, 2, 128]), ALU.mult),
                     reads=[R_ps[ba], R_cmf], writes=[R_attm])
                for hh in range(2):
                    hp = slice(64 * hh, 64 * hh + 64)
                    vs = slice(hh * 128, (hh + 1) * 128)
                    mm(pss[bo][:, vs], v_g[:, n, vs], attm[:, hh, :], True, n == 0, [R_vg, R_attm], R_ps[bo])
                    if n > 0:
                        mm(pss[bo][:, vs], Sbf[hp, vs], q_eT[hp, ch], False, True, [R_Sbf, R_qe], R_ps[bo])
                P.op("act", lambda g: g.activation(out=o_acc[:, :, ch], in_=pss[bo][:, 0:256].rearrange("p (a b) -> p a b", b=128), func=AF.Copy),
                     reads=[R_ps[bo]], writes=[R_oacc])
                if n < NT - 1:
                    mm(pss[bkv][:, 0:256], ke_tok[:, n, :], v_g[:, n, :], True, True, [R_ket, R_vg], R_ps[bkv])
                    P.op("dve", lambda g: g.tensor_tensor(Stmp, Sp, pss[bkv][:, 0:256], ALU.add), reads=[R_Sp, R_ps[bkv]], writes=[R_Stmp])
                    P.op("dve", lambda g: g.tensor_scalar(Sp, Stmp, dec[:, n:n + 1], None, ALU.mult), reads=[R_Stmp, R_dec], writes=[R_Sp])
                    P.op("pool", lambda g: g.tensor_scalar(Sbf, Stmp, dec[:, n:n + 1], None, ALU.mult), reads=[R_Stmp, R_dec], writes=[R_Sbf])
            for hh in range(2):
                post_norm_store(l, o_acc[:, hh:hh + 1, :], R_oacc, 1, [sc(50)], sz[:, hh:hh + 1, :], R_sz, 2 * pr + hh, tmps, 128.0, 6)

        if stop == "pC1":
            P.barrier(); return nc
        for h in range(4):
            new_phase("gdn")
            wfm = [ar.alloc([128, 16, 128], BF16, "wfm%d" % i) for i in range(2)]
            xbf, R_xbf = ar.alloc([128, S + 8], BF16, "xbf")
            diag, R_diag = ar.alloc([128, 4, 128], BF16, "diag")
            cs, R_cs = ar.alloc([128, S], F32, "cs")
            knT, R_kn = ar.alloc([128, S], BF16, "knT")
            qnT, R_qn = ar.alloc([128, S], BF16, "qnT")
            cvT, R_cv = ar.alloc([128, S], BF16, "cvT")
            kbT, R_kb = ar.alloc([128, S], BF16, "kbT")
            q_eT, R_qe = ar.alloc([128, S], BF16, "q_eT")
            vb_tok, R_vb = ar.alloc([128, NT, 128], BF16, "vb_tok")
            kbg_tok, R_kbg = ar.alloc([128, NT, 128], BF16, "kbg_tok")
            kt_tok, R_kt = ar.alloc([128, NT, 128], BF16, "kt_tok")
            gc, R_gc = ar.alloc([128, S], F32, "gc")
            beta, R_beta = ar.alloc([128, S], BF16, "beta")
            eg, R_eg = ar.alloc([128, S], F32, "eg")
            tl, R_tl = ar.alloc([128, S], F32, "tl")
            dabT, R_dab = tl[0:8, :], R_tl
            cols, R_cols = ar.alloc([128, 6, 16], F32, "cols")
            nA, R_nA = ar.alloc([128, 4], F32, "nA")
            Sp, R_Sp = ar.alloc([128, 128], F32, "Sp")
            Sbf, R_Sbf = ar.alloc([128, 128], BF16, "Sbf")
            dm = [ar.alloc([128, 128], F32, "dm%d" % i) for i in range(4)]
            mb = [ar.alloc([128, 128], BF16, "mb%d" % i) for i in range(10)]
            u_sb, R_u = ar.alloc([128, 128], F32, "u_sb")
            tmps = [ar.alloc([128, 2, 512], BF16, "sq"), ar.alloc([128, 512], F32, "t1"), ar.alloc([128, 512], F32, "rinv"),
                    ar.alloc([128, 512], F32, "u"), ar.alloc([128, 512], BF16, "ost"), ar.alloc([128, 512], BF16, "ost2")]
            e1, R_e1 = tmps[3]
            o_acc, R_oacc = cs.rearrange("p (a b) -> p a b", a=1), R_cs

            def c_dab(tb, ps, R):
                copy_on("act", dabT[0:8, tb * 512:(tb + 1) * 512], ps, [R], [R_dab])
            proj_fm(l, 21, 8, wfm, c_dab, 0)
            P.op("dve", lambda g: g.memset(xbf[:, 0:3], 0.0), writes=[R_xbf])
            for which in range(3):
                ti = which * 4 + h
                for j in range(4):
                    P.op("dve", lambda g: g.tensor_scalar(diag[:, j, :], ident_f, sc(64 + ti * 4 + j), None, ALU.mult),
                         reads=[R_cmf, R_small], writes=[R_diag])

                def c_x(tb, ps, R):
                    copy_on(alt(), xbf[:, 3 + tb * 512:3 + (tb + 1) * 512], ps, [R], [R_xbf])
                proj_fm(l, 9 + 4 * which + h, 128, wfm, c_x, 1)
                for blk in range(4):
                    sl = slice(blk * 512, (blk + 1) * 512)
                    bk = blk % 2
                    for j in range(4):
                        mm(pss[bk][:], diag[:, j, :], xbf[:, blk * 512 + j: blk * 512 + j + 512], j == 0, j == 3, [R_diag, R_xbf], R_ps[bk])
                    if which == 2:
                        P.op("act", lambda g: g.activation(out=cvT[:, sl], in_=pss[bk][:], func=AF.Silu), reads=[R_ps[bk]], writes=[R_cv])
                    else:
                        P.op("act", lambda g: g.activation(out=cs[:, sl], in_=pss[bk][:], func=AF.Silu), reads=[R_ps[bk]], writes=[R_cs])
                if which < 2:
                    for blk in range(4):
                        sl = slice(blk * 512, (blk + 1) * 512)
                        rinv, R_rinv = rstd_part([(cs[:, sl], R_cs)], 512, 1.0, tmps[0:3], 2 + blk % 2)
                        if which == 0:
                            P.op("dve", lambda g: g.scalar_tensor_tensor(out=qnT[:, sl], in0=cs[:, sl], scalar=128.0 ** -0.5, in1=rinv[:, 0:512],
                                                                         op0=ALU.mult, op1=ALU.mult), reads=[R_cs, R_rinv], writes=[R_qn])
                        else:
                            P.op("dve", lambda g: g.tensor_tensor(knT[:, sl], cs[:, sl], rinv[:, 0:512], ALU.mult), reads=[R_cs, R_rinv], writes=[R_kn])
            if stop == "g1":
                P.barrier(); return nc
            P.op("act", lambda g: g.activation(out=nA, in_=sc(56, 4), func=AF.Exp), reads=[R_small], writes=[R_nA])
            P.op("dve", lambda g: g.tensor_scalar(nA, nA, -1.0, None, ALU.mult), reads=[R_nA], writes=[R_nA])
            for blk in range(4):
                sl = slice(blk * 512, (blk + 1) * 512)
                bk = 4 + blk % 2
                mm(pss[bk][:], selm[0:8, h, :], dabT[0:8, sl], True, True, [R_selm, R_dab], R_ps[bk])
                P.op("act", lambda g: g.activation(out=e1, in_=pss[bk][:], func=AF.Exp, bias=sc(60 + h)), reads=[R_ps[bk], R_small], writes=[R_e1])
                P.op("act", lambda g: g.activation(out=gc[:, sl], in_=e1, func=AF.Ln, bias=one_ap), reads=[R_e1, R_misc], writes=[R_gc])
                bk2 = 6 + blk % 2
                mm(pss[bk2][:], selm[0:8, 4 + h, :], dabT[0:8, sl], True, True, [R_selm, R_dab], R_ps[bk2])
                P.op("act", lambda g: g.activation(out=beta[:, sl], in_=pss[bk2][:], func=AF.Sigmoid), reads=[R_ps[bk2]], writes=[R_beta])
            P.op("dve", lambda g: g.tensor_tensor_scan(out=gc, data0=rmask[:], data1=gc, initial=0.0, op0=ALU.mult, op1=ALU.add),
                 reads=[R_rmask, R_gc], writes=[R_gc])
            P.op("dve", lambda g: g.tensor_scalar(gc, gc, nA[:, h:h + 1], None, ALU.mult), reads=[R_gc, R_nA], writes=[R_gc])
            gcl = cols[:, 0, :]
            cd = cols[:, 1, :]
            P.op("dve", lambda g: g.tensor_copy(gcl, gc[:, 127:S:128]), reads=[R_gc], writes=[R_cols])
            P.op("act", lambda g: g.activation(out=cd, in_=gcl, func=AF.Exp), reads=[R_cols], writes=[R_cols])
            P.op("act", lambda g: g.activation(out=eg, in_=gc, func=AF.Exp), reads=[R_gc], writes=[R_eg])
            P.op("dve", lambda g: g.tensor_tensor(q_eT, qnT, eg, ALU.mult), reads=[R_qn, R_eg], writes=[R_qe])
            P.op("dve", lambda g: g.tensor_tensor(kbT, knT, beta, ALU.mult), reads=[R_kn, R_beta], writes=[R_kb])
            P.op("pool", lambda g: g.tensor_tensor(eg, eg, beta, ALU.mult), reads=[R_eg, R_beta], writes=[R_eg])
            for n in range(NT):
                ch = slice(n * 128, (n + 1) * 128)
                P.op("act", lambda g: g.activation(out=tl[:, ch], in_=gc[:, ch], func=AF.Exp, scale=-1.0, bias=gcl[:, n:n + 1]),
                     reads=[R_gc, R_cols], writes=[R_tl])
            if stop == "g2":
                P.barrier(); return nc
            for qi, (src, R_src) in enumerate(((gc, R_gc), (beta, R_beta), (eg, R_eg), (tl, R_tl))):
                oh = cm_b[:, CI, 0:2] if src is beta else cm_f[:, CI, 0:2]
                for n in range(NT):
                    c0 = (qi * NT + n) * 2
                    mm(pss[3][:, c0:c0 + 2], src[:, n * 128:(n + 1) * 128], oh, True, True, [R_src, R_cmf, R_cmb], R_ps[3])
            P.op("dve", lambda g: g.tensor_copy(cols[:, 2:6, :], pss[3][:, 0:128:2].rearrange("p (a b) -> p a b", b=NT)),
                 reads=[R_ps[3]], writes=[R_cols])
            gc_col, beta_col, bexp_col, tail_col = (cols[:, i, :] for i in (2, 3, 4, 5))
            if stop == "g2b":
                P.barrier(); return nc
            for t in range(NT):
                bk = t % 2
                pT = pss[bk][:].bitcast(BF16)
                ts = slice(t * 128, (t + 1) * 128)
                P.op("pe", lambda g: g.transpose(pT[:, 0:128], knT[:, ts], ident_b), reads=[R_kn, R_cmb], writes=[R_ps[bk]])
                P.op("pe", lambda g: g.transpose(pT[:, 128:256], cvT[:, ts], ident_b), reads=[R_cv, R_cmb], writes=[R_ps[bk]], pe_accum=True)
                P.op("dve", lambda g: g.tensor_scalar(kbg_tok[:, t, :], pT[:, 0:128], bexp_col[:, t:t + 1], None, ALU.mult),
                     reads=[R_ps[bk], R_cols], writes=[R_kbg])
                P.op("dve", lambda g: g.tensor_scalar(kt_tok[:, t, :], pT[:, 0:128], tail_col[:, t:t + 1], None, ALU.mult),
                     reads=[R_ps[bk], R_cols], writes=[R_kt])
                P.op("dve", lambda g: g.tensor_scalar(vb_tok[:, t, :], pT[:, 128:256], beta_col[:, t:t + 1], None, ALU.mult),
                     reads=[R_ps[bk], R_cols], writes=[R_vb])

            if stop == "g3":
                P.barrier(); return nc
            P.barrier(reset=False)
            sz, R_sz = tl.bitcast(BF16)[:, 0:S].rearrange("p (a b) -> p a b", a=1), Reg("sz")
            mb2 = [(xbf[:, i * 128:(i + 1) * 128], Reg("mb2_%d" % i)) for i in range(16)]

            def c_z(tb, ps, R):
                P.op("act", lambda g: g.activation(out=sz[:, 0, tb * 512:(tb + 1) * 512], in_=ps, func=AF.Silu), reads=[R], writes=[R_sz])
            proj_fm(l, 22 + h, 128, wfm, c_z, 1)
            if stop == "g4":
                P.barrier(); return nc
            P.barrier(reset=False)
            G = 4
            eg4 = eg.rearrange("p (t g c) -> p t g c", g=G, c=128)
            (dA4, R_dA), (dB4, R_dB), (dD4, R_dD), (u4, R_u4) = [(eg4[:, i], Reg("f4_%d" % i)) for i in range(4)]
            pool16 = []
            for src in (beta, cvT, wfm[0][0].rearrange("p a b -> p (a b)"), wfm[1][0].rearrange("p a b -> p (a b)"),
                        tl.bitcast(BF16)[:, S:2 * S]):
                v4 = src.rearrange("p (t g c) -> p t g c", g=G, c=128)
                pool16 += [(v4[:, i], Reg("b4_%d" % len(pool16))) for i in range(4)]
            ((Pm4, R_P), (PT4, R_PT), (qk4, R_qk), (Pd4, R_Pd), (PTd4, R_PTd), (Po32, R_Po32), (PTo32, R_PTo32), (Po64, R_Po64),
             Abuf0, ATbuf0, Abuf1, ATbuf1, sq0, sqT0, sq1, sqT1, (U1, R_U1), (T1, R_T1), (wT4, R_wT)) = pool16[0:19]
            vnew, R_vn = mb[0]

            def bc4(blk, f32=True):
                src = cm_f if f32 else cm_b
                return src[:, blk:blk + 1, :].to_broadcast([128, G, 128])

            def flat(t4):
                return t4.rearrange("p g c -> p (g c)")

            def p4(bank):
                return pss[bank][:].rearrange("p (g c) -> p g c", c=128)

            P.op("dve", lambda g: g.memset(Sp, 0.0), writes=[R_Sp])
            P.op("dve", lambda g: g.memset(Sbf, 0.0), writes=[R_Sbf])
            for bb in range(NT // G):
                ns = [bb * G + g_ for g_ in range(G)]
                chs = [slice(n * 128, (n + 1) * 128) for n in ns]
                for g_, n in enumerate(ns):
                    gcc = gc_col[:, n:n + 1]
                    P.op("dve", lambda g: g.tensor_scalar(dA4[:, g_, :], gc[:, chs[g_]], gcc, 0.0, ALU.subtract, ALU.max),
                         reads=[R_gc, R_cols], writes=[R_dA])
                    P.op("dve", lambda g: g.tensor_scalar(dB4[:, g_, :], gc[:, chs[g_]], gcc, 0.0, ALU.subtract, ALU.min),
                         reads=[R_gc, R_cols], writes=[R_dB])
                P.op("act", lambda g: g.activation(out=flat(dA4), in_=flat(dA4), func=AF.Exp, scale=-1.0), reads=[R_dA], writes=[R_dA])
                P.op("act", lambda g: g.activation(out=flat(dB4), in_=flat(dB4), func=AF.Exp), reads=[R_dB], writes=[R_dB])
                P.op("dve", lambda g: g.tensor_tensor(dA4, dA4, bc4(CNSL), ALU.mult), reads=[R_dA, R_cmf], writes=[R_dA])
                P.op("dve", lambda g: g.tensor_tensor(dD4, dB4, bc4(CNSU), ALU.mult), reads=[R_dB, R_cmf], writes=[R_dD])
                P.op("dve", lambda g: g.tensor_tensor(dB4, dB4, bc4(CTU), ALU.mult), reads=[R_dB, R_cmf], writes=[R_dB])
                for g_ in range(G):
                    cs_ = slice(g_ * 128, (g_ + 1) * 128)
                    mm(pss[0][:, cs_], kbT[:, chs[g_]], knT[:, chs[g_]], True, True, [R_kb, R_kn], R_ps[0])
                for g_ in range(G):
                    cs_ = slice(g_ * 128, (g_ + 1) * 128)
                    mm(pss[1][:, cs_], knT[:, chs[g_]], kbT[:, chs[g_]], True, True, [R_kb, R_kn], R_ps[1])
                for g_ in range(G):
                    cs_ = slice(g_ * 128, (g_ + 1) * 128)
                    mm(pss[2][:, cs_], knT[:, chs[g_]], qnT[:, chs[g_]], True, True, [R_kn, R_qn], R_ps[2])
                P.op("dve", lambda g: g.tensor_tensor(Pm4, p4(0), dA4, ALU.mult), reads=[R_ps[0], R_dA], writes=[R_P])
                P.op("dve", lambda g: g.tensor_tensor(PT4, p4(1), dD4, ALU.mult), reads=[R_ps[1], R_dD], writes=[R_PT])
                P.op("dve", lambda g: g.tensor_tensor(qk4, p4(2), dB4, ALU.mult), reads=[R_ps[2], R_dB], writes=[R_qk])
                for dst, R_d, src, R_s, mk in ((Pd4, R_Pd, Pm4, R_P, CB32), (PTd4, R_PTd, PT4, R_PT, CB32), (Po32, R_Po32, Pm4, R_P, CO32),
                                               (PTo32, R_PTo32, PT4, R_PT, CO32), (Po64, R_Po64, Pm4, R_P, CO64)):
                    P.op("dve", lambda g: g.tensor_tensor(dst, src, bc4(mk, False), ALU.mult), reads=[R_s, R_cmb], writes=[R_d])
                Acur, ATcur, Anxt, ATnxt = Abuf0, ATbuf0, Abuf1, ATbuf1
                P.op("dve", lambda g: g.tensor_tensor(Acur[0], Pd4, bc4(CI, False), ALU.add), reads=[R_Pd, R_cmb], writes=[Acur[1]])
                P.op("dve", lambda g: g.tensor_tensor(ATcur[0], PTd4, bc4(CI, False), ALU.add), reads=[R_PTd, R_cmb], writes=[ATcur[1]])

                def mm4(bank, lhs4, R_l, rhs4, R_r, start=True):
                    for g_ in range(G):
                        cs_ = slice(g_ * 128, (g_ + 1) * 128)
                        mm(pss[bank][:, cs_], lhs4[:, g_, :], rhs4[:, g_, :], start, start or g_ == G - 1, [R_l, R_r], R_ps[bank])

                def add_mm4(bank, base, lhs4, R_l, rhs4, R_r):
                    mm(pss[bank][:], ident_b, flat(base[0]), True, False, [R_cmb, base[1]], R_ps[bank])
                    mm4(bank, lhs4, R_l, rhs4, R_r, start=False)

                cur = ((Pd4, R_Pd), (PTd4, R_PTd))
                sqb = [(sq0, sqT0), (sq1, sqT1)]
                for lev in range(4):
                    (cP, R_cP), (cPT, R_cPT) = cur
                    (nP, R_nP), (nPT, R_nPT) = sqb[lev % 2]
                    mm4(3, cPT, R_cPT, cP, R_cP)
                    mm4(4, cP, R_cP, cPT, R_cPT)
                    copy_on("act", flat(nP), pss[3][:], [R_ps[3]], [R_nP])
                    copy_on("dve", flat(nPT), pss[4][:], [R_ps[4]], [R_nPT])
                    add_mm4(5, Acur, nPT, R_nPT, Acur[0], Acur[1])
                    add_mm4(6, ATcur, nP, R_nP, ATcur[0], ATcur[1])
                    copy_on("dve", flat(Anxt[0]), pss[5][:], [R_ps[5]], [Anxt[1]])
                    copy_on("act", flat(ATnxt[0]), pss[6][:], [R_ps[6]], [ATnxt[1]])
                    Acur, Anxt = Anxt, Acur
                    ATcur, ATnxt = ATnxt, ATcur
                    cur = ((nP, R_nP), (nPT, R_nPT))
                mm4(3, PTo32, R_PTo32, Acur[0], Acur[1])
                mm4(4, Po32, R_Po32, ATcur[0], ATcur[1])
                copy_on("act", flat(U1), pss[3][:], [R_ps[3]], [R_U1])
                copy_on("dve", flat(T1), pss[4][:], [R_ps[4]], [R_T1])
                add_mm4(5, Acur, ATcur[0], ATcur[1], U1, R_U1)
                add_mm4(6, ATcur, Acur[0], Acur[1], T1, R_T1)
                copy_on("dve", flat(Anxt[0]), pss[5][:], [R_ps[5]], [Anxt[1]])
                copy_on("act", flat(ATnxt[0]), pss[6][:], [R_ps[6]], [ATnxt[1]])
                Acur, Anxt = Anxt, Acur
                ATcur, ATnxt = ATnxt, ATcur
                mm4(4, Po64, R_Po64, ATcur[0], ATcur[1])
                copy_on("dve", flat(T1), pss[4][:], [R_ps[4]], [R_T1])
                add_mm4(6, ATcur, Acur[0], Acur[1], T1, R_T1)
                copy_on("act", flat(ATnxt[0]), pss[6][:], [R_ps[6]], [ATnxt[1]])
                AT4, R_AT = ATnxt
                for g_, n in enumerate(ns):
                    cs_ = slice(g_ * 128, (g_ + 1) * 128)
                    mm(pss[3][:, cs_], AT4[:, g_, :], vb_tok[:, n, :], True, True, [R_AT, R_vb], R_ps[3])
                for g_, n in enumerate(ns):
                    cs_ = slice(g_ * 128, (g_ + 1) * 128)
                    mm(pss[4][:, cs_], kbg_tok[:, n, :], AT4[:, g_, :], True, True, [R_AT, R_kbg], R_ps[4])
                copy_on("act", flat(u4), pss[3][:], [R_ps[3]], [R_u4])
                copy_on("dve", flat(wT4), pss[4][:], [R_ps[4]], [R_wT])
                for g_, n in enumerate(ns):
                    ch = chs[g_]
                    if n > 0:
                        mm(pss[0][:, 0:128], wT4[:, g_, :], Sbf, True, True, [R_wT, R_Sbf], R_ps[0])
                        P.op("dve", lambda g: g.tensor_tensor(vnew, u4[:, g_, :], pss[0][:, 0:128], ALU.subtract),
                             reads=[R_u4, R_ps[0]], writes=[R_vn])
                    else:
                        copy_on("dve", vnew, u4[:, g_, :], [R_u4], [R_vn])
                    if n > 0:
                        mm(pss[7][:, 0:128], Sbf, q_eT[:, ch], True, False, [R_Sbf, R_qe], R_ps[7])
                    mm(pss[7][:, 0:128], vnew, qk4[:, g_, :], n == 0, True, [R_vn, R_qk], R_ps[7])
                    copy_on("act", o_acc[:, 0, ch], pss[7][:, 0:128], [R_ps[7]], [R_oacc])
                    if n < NT - 1:
                        mm(pss[1][:, 0:128], kt_tok[:, n, :], vnew, True, True, [R_kt, R_vn], R_ps[1])
                        P.op("dve", lambda g: g.scalar_tensor_tensor(out=Sp, in0=Sp, scalar=cd[:, n:n + 1], in1=pss[1][:, 0:128],
                                                                     op0=ALU.mult, op1=ALU.add), reads=[R_Sp, R_cols, R_ps[1]], writes=[R_Sp])
                        copy_on("act", Sbf, Sp, [R_Sp], [R_Sbf])
            post_norm_store(l, o_acc, R_oacc, 1, [sc(51)], sz, R_sz, 4 + h, tmps, 128.0, 6)

        if stop == "pC2":
            P.barrier(); return nc
        for h in range(4):
            new_phase("diff")
            wfm = [ar.alloc([128, 16, 128], BF16, "wfm%d" % i) for i in range(4)]
            wtm = ar.alloc([128, 16, 256], BF16, "wtm")
            qT, R_q = ar.alloc([128, 2, S], BF16, "qT")
            kT, R_k = ar.alloc([128, 2, S], BF16, "kT")
            v_sb, R_v = ar.alloc([128, NT, 256], BF16, "v_sb")
            sz, R_sz = ar.alloc([128, 2, S], BF16, "sz")
            ebuf = [ar.alloc([128, 512], BF16, "e%d" % i) for i in range(3)]
            qn, R_qnb = ar.alloc([128, 512], BF16, "qn")
            ta, R_ta = ar.alloc([128, 512], F32, "ta")
            tb_, R_tb = ar.alloc([128, 512], F32, "tb")
            tO, R_tO = ar.alloc([128, 2, 512], F32, "tO")
            rs, R_rs = ar.alloc([128, 512], F32, "rs")
            sq2, R_sq2 = ar.alloc([128, 1024], BF16, "sq2")
            t1b, R_t1b = ar.alloc([128, 1024], F32, "t1b")
            rinvb, R_rinvb = ar.alloc([128, 1024], F32, "rinvb")
            qnb, R_qnb2 = ar.alloc([128, 1024], BF16, "qnb")
            tab, R_tab = ar.alloc([128, 1024], F32, "tab")
            tbb, R_tbb = ar.alloc([128, 1024], F32, "tbb")
            lamt, R_lam = ar.alloc([128, 8], F32, "lam")
            lprod, R_lprod = ar.alloc([128, 256], F32, "lprod")
            wn2, R_wn2 = ar.alloc([128, 2], F32, "wn2")
            tmps = [ar.alloc([128, 2, 512], BF16, "sq"), ar.alloc([128, 512], F32, "t1"), ar.alloc([128, 512], F32, "rinv"),
                    ar.alloc([128, 512], F32, "u"), ar.alloc([128, 512], BF16, "ost"), ar.alloc([128, 512], BF16, "ost2")]
            P.op("dve", lambda g: g.tensor_tensor(lprod[:, 0:128], sc(112, 128), sc(240, 128), ALU.mult), reads=[R_small], writes=[R_lprod])
            P.op("dve", lambda g: g.tensor_tensor(lprod[:, 128:256], sc(368, 128), sc(496, 128), ALU.mult), reads=[R_small], writes=[R_lprod])
            P.op("dve", lambda g: g.tensor_reduce(lamt[:, 0:2], lprod.rearrange("p (a b) -> p a b", b=128), AX.X, ALU.add),
                 reads=[R_lprod], writes=[R_lam])
            P.op("act", lambda g: g.activation(out=lamt[:, 2:4], in_=lamt[:, 0:2], func=AF.Exp), reads=[R_lam], writes=[R_lam])
            P.op("dve", lambda g: g.scalar_tensor_tensor(out=lamt[:, 4:5], in0=lamt[:, 3:4], scalar=-lam_init, in1=lamt[:, 2:3],
                                                         op0=ALU.add, op1=ALU.subtract), reads=[R_lam], writes=[R_lam])
            nlam = lamt[:, 4:5]
            P.op("dve", lambda g: g.tensor_scalar(wn2, sc(54, 2), 1.0 - lam_init, None, ALU.mult), reads=[R_small], writes=[R_wn2])
            units = [(which, m, half) for which in range(2) for m in range(2) for half in range(2)]
            qk_meta = ((qT, R_q, 26, 52), (kT, R_k, 34, 53))
            ustate = {}

            def qk_issue(ui):
                which, m, half = units[ui]
                g0 = qk_meta[which][2]
                wb, R_wb = qk_w[which * 2 + m]
                banks = [4 + 2 * (ui % 2), 5 + 2 * (ui % 2)]
                for kc in range(16):
                    for i in range(2):
                        tb = half * 2 + i
                        mm(pss[banks[i]][:], wb[:, kc, :], big[:, kc, tb * 512:(tb + 1) * 512], kc == 0, kc == 15,
                           [R_wb, R_big], R_ps[banks[i]])

            def qk_consume(ui):
                which, m, half = units[ui]
                dstT, R_dst, g0, wcol = qk_meta[which]
                b0_ = 4 + 2 * (ui % 2)
                ps2 = psall[:, b0_ * 512:(b0_ + 2) * 512]
                Rb = [R_ps[b0_], R_ps[b0_ + 1]]
                sl2 = slice(half * 1024, (half + 1) * 1024)
                P.op("act", lambda g: g.activation(out=sq2, in_=ps2, func=AF.Square), reads=Rb, writes=[R_sq2])
                for i in range(2):
                    mm(pss[2 + i][:], ones_b, sq2[:, i * 512:(i + 1) * 512], True, True, [R_sq2, R_cmb], R_ps[2 + i])
                P.op("act", lambda g: g.activation(out=t1b, in_=psall[:, 2 * 512:4 * 512], func=AF.Sqrt, scale=1.0 / 128.0, bias=eps_ap),
                     reads=[R_ps[2], R_ps[3], R_misc], writes=[R_t1b])
                P.op("dve", lambda g: g.reciprocal(rinvb, t1b), reads=[R_t1b], writes=[R_rinvb])
                P.op("dve", lambda g: g.scalar_tensor_tensor(out=qnb, in0=ps2, scalar=sc(wcol), in1=rinvb, op0=ALU.mult, op1=ALU.mult),
                     reads=Rb + [R_rinvb, R_small], writes=[R_qnb2])
                for i in range(2):
                    mm(pss[i][:], cm_b[:, CP, :], qnb[:, i * 512:(i + 1) * 512], True, True, [R_cmb, R_qnb2], R_ps[i])
                P.op("pool", lambda g: g.tensor_tensor(tab, qnb, cosT[:, sl2], ALU.mult), reads=[R_qnb2, R_cos], writes=[R_tab])
                P.op("dve", lambda g: g.tensor_tensor(tbb, psall[:, 0:1024], sinT[:, sl2], ALU.mult), reads=[R_ps[0], R_ps[1], R_sin], writes=[R_tbb])
                P.op("pool", lambda g: g.tensor_tensor(dstT[:, m, sl2], tab, tbb, ALU.add), reads=[R_tab, R_tbb], writes=[R_dst])
            qk_w = []
            for which_ in range(2):
                for m_ in range(2):
                    wb_, R_wb_ = wfm[len(qk_w)]
                    P.dma("pool", wb_, winfm_d[l, qk_meta[which_][2] + 2 * h + m_].rearrange("p (a b) -> p a b", b=128), writes=[R_wb_])
                    qk_w.append((wb_, R_wb_))
            qk_issue(0)
            for ui in range(len(units)):
                if ui + 1 < len(units):
                    qk_issue(ui + 1)
                qk_consume(ui)

            def c_v(t, ps, R):
                copy_on(alt(), v_sb[:, t, :], ps, [R], [R_v])
            proj_tm(l, 2 + h, wtm, c_v, [0, 1])
            for j in range(2):
                def c_z(tb, ps, R):
                    P.op("act", lambda g: g.activation(out=sz[:, j, tb * 512:(tb + 1) * 512], in_=ps, func=AF.Silu), reads=[R], writes=[R_sz])
                proj_fm(l, 42 + 2 * h + j, 128, wfm, c_z, j)
            if h == 3:
                for fc in range(16):
                    P.dma("pool", big[:, fc, :], wout_d[l][:, fc * D:(fc + 1) * D], writes=[R_big], nowaw=fc > 0)
            scale = 128.0 ** -0.5
            steps = [(qb, m, kc) for qb in range(4) for m in range(2) for kc in range(4 * (qb + 1))]

            def qk_mm(si):
                qb, m, kc = steps[si]
                col0 = max(kc - 4 * qb, 0) * 128
                bsc = si % 2
                mm(pss[bsc][:, col0:512], kT[:, m, kc * 128:(kc + 1) * 128], qT[:, m, qb * 512 + col0:(qb + 1) * 512], True, True,
                   [R_k, R_q], R_ps[bsc])
            qk_mm(0)
            for si, (qb, m, kc) in enumerate(steps):
                qs = slice(qb * 512, (qb + 1) * 512)
                nk = 4 * (qb + 1)
                bo0, bo1, bs = (2, 3, 4) if m == 0 else (5, 6, 7)
                if si + 1 < len(steps):
                    qk_mm(si + 1)
                bsc = si % 2
                c = kc - 4 * qb
                col0 = max(c, 0) * 128
                e, R_e = ebuf[si % 3]
                P.op("act", lambda g: g.activation(out=e[:, col0:512], in_=pss[bsc][:, col0:512], func=AF.Exp, scale=scale),
                     reads=[R_ps[bsc]], writes=[R_e])
                if c >= 0:
                    P.op("pool", lambda g: g.tensor_tensor(e[:, col0:col0 + 128], e[:, col0:col0 + 128], cm_b[:, CTU, :], ALU.mult),
                         reads=[R_e, R_cmb], writes=[R_e])
                first, last = kc == 0, kc == nk - 1
                mm(pss[bo0][:, col0:512], v_sb[:, kc, 0:128], e[:, col0:512], first, last, [R_v, R_e], R_ps[bo0])
                mm(pss[bo1][:, col0:512], v_sb[:, kc, 128:256], e[:, col0:512], first, last, [R_v, R_e], R_ps[bo1])
                mm(pss[bs][:, col0:512], ones_b, e[:, col0:512], first, last, [R_cmb, R_e], R_ps[bs])
                if not last:
                    continue
                P.op("dve", lambda g: g.reciprocal(rs, pss[bs][:]), reads=[R_ps[bs]], writes=[R_rs])
                for j, bo in enumerate((bo0, bo1)):
                    if m == 0:
                        P.op("dve", lambda g: g.tensor_tensor(tO[:, j, :], pss[bo][:], rs, ALU.mult), reads=[R_ps[bo], R_rs], writes=[R_tO])
                    else:
                        P.op("dve", lambda g: g.scalar_tensor_tensor(out=ta, in0=pss[bo][:], scalar=nlam, in1=rs, op0=ALU.mult, op1=ALU.mult),
                             reads=[R_ps[bo], R_rs, R_lam], writes=[R_ta])
                        P.op("pool", lambda g: g.tensor_tensor(tO[:, j, :], tO[:, j, :], ta, ALU.add), reads=[R_tO, R_ta], writes=[R_tO])
                if m == 0:
                    continue
                (sq, R_sq), (t1, R_t1), (rinv, R_rinv), (u, R_u) = tmps[0:4]
                rstd_part([(tO[:, j, :], R_tO) for j in range(2)], 512, 256.0, tmps[0:3], 4)
                for j in range(2):
                    ost, R_ost = tmps[4 + j]
                    P.op("dve", lambda g: g.scalar_tensor_tensor(out=u, in0=tO[:, j, :], scalar=wn2[:, j:j + 1], in1=rinv, op0=ALU.mult, op1=ALU.mult),
                         reads=[R_tO, R_rinv, R_wn2], writes=[R_u])
                    P.op("pool", lambda g: g.tensor_tensor(ost, u, sz[:, j, qs], ALU.mult), reads=[R_u, R_sz], writes=[R_ost])
                    P.dma("sp", oT_d[8 + 2 * h + j, :, qs], ost, reads=[R_ost], writes=[R_oT], nowaw=True)

        if stop == "pC3":
            P.barrier(); return nc
        new_phase("D")
        wo = big
        xts = [ar.alloc([128, D], F32, "xt%d" % i) for i in range(2)]
        obs = [ar.alloc([128, 16, 512], BF16, "ob%d" % i) for i in range(2)]
        xos = [ar.alloc([128, D], F32, "xo%d" % i) for i in range(2)]
        ytmp, R_yt = ar.alloc([128, 512], F32, "ytmp")
        xsrc = x_d if l == 0 else x1_d
        xdst = out_d if l == L - 1 else x1_d
        R_dst = R_out if l == L - 1 else R_x1
        for tb in range(4):
            ob, R_ob = obs[tb % 2]
            P.dma("sp", ob, oT_d[:, :, tb * 512:(tb + 1) * 512].rearrange("c p t -> p c t"), reads=[R_oT], writes=[R_ob])
            for i in range(4):
                t = tb * 4 + i
                xt, R_xt = xts[t % 2]
                xo, R_xo = xos[t % 2]
                P.dma("sp", xt, xsrc[t * 128:(t + 1) * 128, :], reads=[R_x1] if l > 0 else [], writes=[R_xt])
                for ng in range(4):
                    ns = slice(ng * 512, (ng + 1) * 512)
                    bk = (t * 4 + ng) % 4
                    for fc in range(16):
                        mm(pss[bk][:], ob[:, fc, i * 128:(i + 1) * 128], wo[:, fc, ns], fc == 0, fc == 15, [R_ob, R_big], R_ps[bk])
                    P.op("dve", lambda g: g.tensor_tensor(ytmp, pss[bk][:], gate_bc[:, ns], ALU.mult), reads=[R_ps[bk], R_gate], writes=[R_yt])
                    P.op("pool", lambda g: g.tensor_tensor(xo[:, ns], ytmp, xt[:, ns], ALU.add), reads=[R_yt, R_xt], writes=[R_xo])
                P.dma("sp", xdst[t * 128:(t + 1) * 128, :], xo, reads=[R_xo], writes=[R_dst], nowaw=True)

    P.barrier()
    if scopes and state.get("scope") is not None:
        nc.leave_named_scope(state["scope"][0], state["scope"][1], False)
    return nc


def _col(v):
    return np.ascontiguousarray(np.asarray(v, np.float32).reshape(-1, 128).T)


def _consts():
    p = np.arange(128)[:, None]
    j = np.arange(128)[None, :]
    cm = np.zeros((128, NCM, 128), np.float32)
    cm[:, CI] = (p == j)
    cm[:, CO] = 1.0
    prot = np.zeros((128, 128), np.float32)
    prot[(j[0, :64] + 64), j[0, :64]] = -1.0
    prot[(j[0, 64:] - 64), j[0, 64:]] = 1.0
    cm[:, CP] = prot
    cm[:, CTU] = (p <= j)
    cm[:, CNSU] = -1.0 * (p < j)
    cm[:, CNSL] = -1.0 * (p > j)
    bd32 = (p // 32 == j // 32)
    bd64 = (p // 64 == j // 64)
    cm[:, CB32] = bd32
    cm[:, CO32] = bd64 & ~bd32
    cm[:, CO64] = ~bd64
    selm = np.zeros((8, 8, 128), np.float32)
    for k in range(8):
        selm[k, k, :] = 1.0
    rmask = np.ones((128, S), np.float32)
    rmask[:, 0::128] = 0.0
    half = 64
    inv_freq = (10000.0 ** (-(np.arange(half, dtype=np.float32) / np.float32(half)))).astype(np.float32)
    invf = np.concatenate([inv_freq, inv_freq]).astype(np.float64) / (2.0 * math.pi)
    return cm.reshape(128, NCM * 128), selm.reshape(8, 8 * 128), rmask, invf.astype(np.float32)


FM_GROUPS = ([(0, 128), (128, 128), (256, 128), (384, 128), (1024, 16)] + [(1040 + 128 * i, 128) for i in range(4)]
             + [(1552 + 128 * i, 128) for i in range(12)] + [(3088, 8)] + [(3096 + 128 * i, 128) for i in range(4)]
             + [(3608 + 128 * i, 128) for i in range(8)] + [(4632 + 128 * i, 128) for i in range(8)]
             + [(6680 + 128 * i, 128) for i in range(8)])
TM_GROUPS = [(512, 256), (768, 256)] + [(5656 + 256 * i, 256) for i in range(4)]


def _prep_shared(inp):
    f = lambda k: np.asarray(inp[k], np.float32)
    cm, selm, rmask, invf = _consts()
    w_in = f("w_in")
    winfm = np.zeros((2, 50, 128, 16, 128), np.float32)
    wintm = np.zeros((2, 6, 128, 16, 256), np.float32)
    for l in range(2):
        wl = w_in[l].reshape(16, 128, -1)
        for gi, (c0, n) in enumerate(FM_GROUPS):
            winfm[l, gi, :, :, :n] = wl[:, :, c0:c0 + n].transpose(1, 0, 2)
        for gi, (c0, n) in enumerate(TM_GROUPS):
            wintm[l, gi] = wl[:, :, c0:c0 + n].transpose(1, 0, 2)
    wada = np.ascontiguousarray(f("w_ada").reshape(2, 16, 128, 48, 128).transpose(0, 3, 2, 1, 4)).reshape(2, 48, 128, 16 * 128)
    wout = np.ascontiguousarray(f("w_out").reshape(2, 16, 128, D).transpose(0, 2, 1, 3)).reshape(2, 128, 16 * D)
    bgate = np.ascontiguousarray(f("b_ada")[:, 2 * D:].reshape(2, 1, D))
    sm = np.zeros((128, NS), np.float32)
    sm[:, 16] = invf
    for l in range(2):
        b = SBASE + l * SLW
        sm[:, b:b + 16] = _col(f("norm_w")[l])
        sm[:, b + 16:b + 48] = _col(f("b_ada")[l, :2 * D])
        sm[:, b + 48:b + 50] = _col(f("gla_b_lr")[l])
        sm[:, b + 50] = f("gla_norm_w")[l]
        sm[:, b + 51] = f("gdn_norm_w")[l]
        sm[:, b + 52] = f("diff_q_norm_w")[l]
        sm[:, b + 53] = f("diff_k_norm_w")[l]
        sm[:, b + 54:b + 56] = _col(f("diff_norm_w")[l])
        sm[:, b + 56:b + 60] = f("gdn_a_log")[l][None, :]
        sm[:, b + 60:b + 64] = f("gdn_dt_bias")[l][None, :]
        cw = f("gdn_conv_w")[l]
        sm[:, b + 64:b + 112] = cw.reshape(4, 12, 128).transpose(2, 1, 0).reshape(128, 48)
        sm[:, b + 112:b + 624] = f("diff_lambda")[l].reshape(1, 512)
        sm[0:16, b + 624:b + 880] = f("gla_w_lr")[l]
    shared = {"cmat": cm, "selm": selm, "rmask": rmask, "wada": wada, "bgate": bgate,
              "winfm": winfm.reshape(2, 50, 128, 16 * 128), "wintm": wintm.reshape(2, 6, 128, 16 * 256), "wout": wout}
    return shared, sm


def make_in_maps(inp, cores):
    shared, sm = _prep_shared(inp)
    x = np.asarray(inp["x"], np.float32)
    c = np.asarray(inp["c"], np.float32)
    pos = np.asarray(inp["positions"], np.int32)
    maps = []
    for b in cores:
        s = sm.copy()
        s[:, 0:16] = _col(c[b])
        m = dict(shared)
        m["x"] = np.ascontiguousarray(x[b])
        m["small"] = s
        m["pos"] = np.ascontiguousarray(pos[b:b + 1])
        maps.append(m)
    return maps


_NC_CACHE = {}


def kernel(**inputs):
    if "nc" not in _NC_CACHE:
        _NC_CACHE["nc"] = build(2)
    nc = _NC_CACHE["nc"]
    cores = [i // 2 for i in range(8)]
    maps = make_in_maps(inputs, cores)
    res = run_bass_kernel_spmd(nc, maps, core_ids=list(range(8)))
    out = np.stack([np.asarray(res.results[2 * b]["out"], np.float32) for b in range(4)], axis=0)
    return out
```

```python
import math
import numpy as np
import concourse.bass as bass
import concourse.mybir as mybir
from concourse.bass_utils import run_bass_kernel_spmd

F32 = mybir.dt.float32
BF16 = mybir.dt.bfloat16
I32 = mybir.dt.int32
AF = mybir.ActivationFunctionType
ALU = mybir.AluOpType
AX = mybir.AxisListType

S = 2048
D = 2048
NT = 16
EPS = 1e-6
SLW = 880
SBASE = 32
NS = SBASE + 2 * SLW
CI, CO, CP, CTU, CNSU, CNSL, CB32, CO32, CO64 = range(9)
NCM = 9


class Reg:
    __slots__ = ("name", "lw", "rd", "dsem", "local", "psum")

    def __init__(self, name, local=False, psum=False):
        self.name = name
        self.psum = psum
        self.lw = None
        self.rd = {}
        self.dsem = None
        self.local = local


class Prog:
    def __init__(self, nc):
        self.nc = nc
        self.eng = {"pe": nc.tensor, "act": nc.scalar, "dve": nc.vector, "pool": nc.gpsimd, "sp": nc.sync}
        self.sem, self.cnt, self.semobj = {}, {}, {}
        self.seen = {e: {} for e in self.eng}
        for e in self.eng:
            s = nc.alloc_semaphore(name="s_" + e)
            self.sem[e] = s
            self.semobj[e] = s
            self.cnt[e] = 0
        self.vc = {}
        self.dcnt = {}
        self.local_keys = []
        self.local_next = 0
        self.ninstr = 0
        self.nwait = 0

    def _wait(self, e, key, val):
        if self.seen[e].get(key, 0) >= val:
            return
        self.eng[e].wait_ge(self.semobj[key], val)
        self.nwait += 1
        se = self.seen[e]
        se[key] = val
        clk = self.vc.get((key, val))
        if clk:
            for k, v in clk.items():
                if se.get(k, 0) < v:
                    se[k] = v

    def _deps(self, e, reads, writes, pe_accum=False, nowaw=False):
        need = {}

        def add(k, v):
            if need.get(k, 0) < v:
                need[k] = v
        for r in reads:
            if r.lw is not None:
                add(*r.lw)
            if r.psum:
                for k, v in r.rd.items():
                    if k != e:
                        add(k, v)
        for w in writes:
            if w.lw is not None and not nowaw and not (pe_accum and w.lw[0] == "pe"):
                add(*w.lw)
            for k, v in w.rd.items():
                add(k, v)
        for k, v in sorted(need.items(), key=lambda kv: -kv[1] if isinstance(kv[0], str) else 0):
            self._wait(e, k, v)

    def _record(self, ev, reads, writes, nowaw=False):
        for r in reads:
            if r.rd.get(ev[0], 0) < ev[1]:
                r.rd[ev[0]] = ev[1]
        for w in writes:
            w.lw = ev
            if not nowaw:
                w.rd = {}

    def op(self, e, fn, reads=(), writes=(), pe_accum=False):
        self._deps(e, reads, writes, pe_accum)
        ins = fn(self.eng[e])
        self.cnt[e] += 1
        ins.then_inc(self.sem[e], 1)
        ev = (e, self.cnt[e])
        self.vc[ev] = dict(self.seen[e])
        self._record(ev, reads, writes)
        self.ninstr += 1

    def _newkey(self):
        key = ("d", len(self.dcnt))
        self.semobj[key] = self.nc.alloc_semaphore(name="d_%d" % len(self.dcnt))
        self.dcnt[key] = 0
        return key

    def _dkey(self, w):
        if w.dsem is None:
            if w.local:
                if self.local_next == len(self.local_keys):
                    self.local_keys.append(self._newkey())
                w.dsem = self.local_keys[self.local_next]
                self.local_next += 1
            else:
                w.dsem = self._newkey()
        return w.dsem

    def dma(self, q, out_ap, in_ap, reads=(), writes=(), nowaw=False, **kw):
        w = writes[0]
        self._deps(q, reads, writes, nowaw=nowaw)
        key = self._dkey(w)
        ins = self.eng[q].dma_start(out=out_ap, in_=in_ap, **kw)
        self.dcnt[key] += 16
        ins.then_inc(self.semobj[key], 16)
        ev = (key, self.dcnt[key])
        self.vc[ev] = dict(self.seen[q])
        self._record(ev, reads, writes, nowaw=nowaw)
        self.ninstr += 1

    def collective(self, kind, groups, in_ap, out_ap, reads, writes):
        self._deps("pool", reads, writes)
        key = ("c", len([k for k in self.dcnt if k[0] == "c"]))
        self.semobj[key] = self.nc.alloc_semaphore(name="c_%d" % key[1])
        ins = self.eng["pool"].collective_compute(kind, ALU.bypass, replica_groups=groups, ins=[in_ap], outs=[out_ap])
        ins.then_inc(self.semobj[key])
        self.dcnt[key] = 1
        ev = (key, 1)
        self.vc[ev] = dict(self.seen["pool"])
        self._record(ev, reads, writes)
        self.ninstr += 1

    def barrier(self, reset=True, collectives=False):
        evs = [(e, self.cnt[e]) for e in self.eng if self.cnt[e] > 0]
        evs += [(k, v) for k, v in self.dcnt.items() if v > 0 and (collectives or k[0] != "c")]
        for e in self.eng:
            for k, v in evs:
                if k != e:
                    self._wait(e, k, v)
        if reset:
            self.local_next = 0


class Arena:
    def __init__(self, nc, nwords):
        self.t = nc.alloc_sbuf_tensor("arena", [128, nwords], F32)
        self.n = nwords
        self.off = 0
        self.k = 0

    def reset(self):
        self.off = 0

    def alloc(self, shape, dt, name=None):
        free = int(np.prod(shape[1:]))
        words = free if dt in (F32, I32) else (free + 1) // 2
        words = (words + 7) // 8 * 8
        assert self.off + words <= self.n, ("arena overflow", name, self.off, words, self.n)
        v = self.t[:, self.off:self.off + words]
        self.off += words
        if dt == BF16:
            v = v.bitcast(BF16)[:, 0:free]
        elif dt == I32:
            v = v.bitcast(I32)[:, 0:free]
        else:
            v = v[:, 0:free]
        if len(shape) == 3:
            v = v.rearrange("p (a b) -> p a b", b=shape[2])
        self.k += 1
        if shape[0] < 128:
            v = v[0:shape[0]]
        return v, Reg(name or "a%d" % self.k, local=True)


def build(nlayers=2, debug=False, stop=None, scopes=False, ncores=8):
    nc = bass.Bass("TRN2", target_bir_lowering=False)
    P = Prog(nc)
    L = nlayers

    def din(name, shape, dt=F32):
        return nc.dram_tensor(name, shape, dt, kind="ExternalInput").ap()

    x_d = din("x", [S, D])
    small_d = din("small", [128, NS])
    cmat_d = din("cmat", [128, NCM * 128])
    selm_d = din("selm", [8, 8 * 128])
    rmask_d = din("rmask", [128, S])
    pos_d = din("pos", [1, S], I32)
    wada_d = din("wada", [2, 48, 128, 16 * 128])
    bgate_d = din("bgate", [2, 1, D])
    winfm_d = din("winfm", [2, 26, 128, 16 * 128])
    wintm_d = din("wintm", [2, 3, 128, 16 * 256])
    wout_d = din("wout", [2, 128, 16 * D])
    out_d = nc.dram_tensor("out", [S, D], F32, kind="ExternalOutput").ap()
    x1_d = nc.dram_tensor("x1s", [S, D], F32, kind="Internal").ap()
    oTl_t = [nc.dram_tensor("oTl%d" % i, [4 * 128, S], BF16, kind="Internal") for i in range(2)]
    oTg_t = [nc.dram_tensor("oTg%d" % i, [8 * 128, S], BF16, kind="Internal") for i in range(2)]
    oTl_d = [t.ap() for t in oTl_t]
    oTg_d = [t.ap() for t in oTg_t]
    R_oTl = [Reg("oTl0"), Reg("oTl1")]
    R_oTg = [Reg("oTg0"), Reg("oTg1")]
    PAIRS = [[2 * i, 2 * i + 1] for i in range(ncores // 2)]
    R_out, R_x1 = Reg("out"), Reg("x1")

    def oT_dst(fc, sl):
        return oTl_d[fc // 4][(fc % 4) * 128:(fc % 4 + 1) * 128, sl], R_oTl[fc // 4]

    def sb(name, shape, dt):
        return nc.alloc_sbuf_tensor("sb_" + name, shape, dt), Reg(name)

    big, R_big = sb("big", [128, 16, S], BF16)
    cosT, R_cos = sb("cosT", [128, S], BF16)
    sinT, R_sin = sb("sinT", [128, S], BF16)
    gate_bc, R_gate = sb("gate_bc", [128, D], F32)
    rmask, R_rmask = sb("rmask", [128, S], BF16)
    cm_f, R_cmf = sb("cm_f", [128, NCM, 128], F32)
    cm_b, R_cmb = sb("cm_b", [128, NCM, 128], BF16)
    selm, R_selm = sb("selm", [8, 8, 128], F32)
    small, R_small = sb("small", [128, NS], F32)
    modc, R_modc = sb("modc", [128, 48], F32)
    misc, R_misc = sb("misc", [128, 64], F32)
    wpre, R_wpre = sb("wpre", [128, 16, 128], BF16)
    ar = Arena(nc, (nc.sbuf_bytes_remaining - 2048) // 4)

    psall = nc.alloc_psum_tensor("psall", [128, 8 * 512], F32)
    pss = [psall[:, i * 512:(i + 1) * 512] for i in range(8)]
    R_ps = [Reg("ps%d" % i, psum=True) for i in range(8)]

    state = {"alt": 0}

    def alt():
        state["alt"] ^= 1
        return "act" if state["alt"] else "dve"

    def copy_on(e, out, in_, reads, writes):
        if e == "act":
            P.op("act", lambda g: g.activation(out=out, in_=in_, func=AF.Copy), reads=reads, writes=writes)
        else:
            P.op(e, lambda g: g.tensor_copy(out, in_), reads=reads, writes=writes)

    def mm(out, lhsT, rhs, start, stop, reads, w):
        P.op("pe", lambda g: g.matmul(out, lhsT=lhsT, rhs=rhs, start=start, stop=stop),
             reads=reads, writes=[w], pe_accum=not start)

    def new_phase(name="ph"):
        P.barrier()
        ar.reset()
        if scopes:
            if state.get("scope") is not None:
                nc.leave_named_scope(state["scope"][0], state["scope"][1], False)
            nm = "%s_%d" % (name, state.setdefault("nscope", 0))
            state["nscope"] += 1
            sid, _ = nc.enter_named_scope(nm, False)
            state["scope"] = (nm, sid)

    ident_b = cm_b[:, CI, :]
    ones_b = cm_b[:, CO, :]
    ident_f = cm_f[:, CI, :]

    def rstd_part(srcs, n, denom, tmps, psi):
        (sq, R_sq), (t1, R_t1), (rinv, R_rinv) = tmps
        for i, (sap, sreg) in enumerate(srcs):
            P.op("act", lambda g: g.activation(out=sq[:, i, 0:n], in_=sap, func=AF.Square), reads=[sreg], writes=[R_sq])
        for i in range(len(srcs)):
            mm(pss[psi][:, 0:n], ones_b, sq[:, i, 0:n], i == 0, i == len(srcs) - 1, [R_sq, R_cmb], R_ps[psi])
        P.op("act", lambda g: g.activation(out=t1[:, 0:n], in_=pss[psi][:, 0:n], func=AF.Ln, scale=1.0 / denom, bias=eps_ap),
             reads=[R_ps[psi], R_misc], writes=[R_t1])
        P.op("act", lambda g: g.activation(out=rinv[:, 0:n], in_=t1[:, 0:n], func=AF.Exp, scale=-0.5), reads=[R_t1], writes=[R_rinv])
        return rinv, R_rinv

    P.dma("sp", small[:], small_d, writes=[R_small])
    P.dma("sp", cm_f[:], cmat_d.rearrange("p (a b) -> p a b", b=128), writes=[R_cmf])
    P.dma("sp", selm[:], selm_d.rearrange("p (a b) -> p a b", b=128), writes=[R_selm])
    P.dma("pool", rmask[:], rmask_d, writes=[R_rmask])
    P.op("dve", lambda g: g.tensor_copy(cm_b[:], cm_f[:]), reads=[R_cmf], writes=[R_cmb])
    eps_ap = misc[:, 0:1]
    P.op("dve", lambda g: g.memset(misc[:], 0.0), writes=[R_misc])
    P.op("dve", lambda g: g.memset(misc[:, 0:1], EPS), reads=[], writes=[R_misc])
    P.op("dve", lambda g: g.memset(misc[:, 1:2], 1.0), reads=[], writes=[R_misc])
    one_ap = misc[:, 1:2]

    posi, R_posi = ar.alloc([128, S], I32, "posi")
    y, R_y = ar.alloc([128, S], F32, "y")
    yi, R_yi = ar.alloc([128, S], I32, "yi")
    yf, R_yf = ar.alloc([128, S], F32, "yf")
    fr, R_fr = ar.alloc([128, S], F32, "fr")
    m1, R_m1 = ar.alloc([128, S], F32, "m1")
    P.dma("sp", posi, pos_d.to_broadcast([128, S]), writes=[R_posi])
    P.op("dve", lambda g: g.tensor_copy(y, posi), reads=[R_posi], writes=[R_y])
    P.op("dve", lambda g: g.tensor_scalar(y, y, small[:, 16:17], None, ALU.mult), reads=[R_y, R_small], writes=[R_y])
    P.op("dve", lambda g: g.tensor_copy(yi, y), reads=[R_y], writes=[R_yi])
    P.op("dve", lambda g: g.tensor_copy(yf, yi), reads=[R_yi], writes=[R_yf])
    P.op("dve", lambda g: g.tensor_tensor(fr, y, yf, ALU.subtract), reads=[R_y, R_yf], writes=[R_fr])
    for which, dst, R_dst in ((0, sinT, R_sin), (1, cosT, R_cos)):
        src = fr
        if which == 1:
            P.op("dve", lambda g: g.tensor_scalar(y, fr, 0.25, None, ALU.add), reads=[R_fr], writes=[R_y])
            src = y
        R_src = R_fr if which == 0 else R_y
        P.op("dve", lambda g: g.tensor_scalar(m1, src, 0.5, None, ALU.is_gt), reads=[R_src], writes=[R_m1])
        P.op("dve", lambda g: g.tensor_tensor(yf, src, m1, ALU.subtract), reads=[R_src, R_m1], writes=[R_yf])
        P.op("dve", lambda g: g.tensor_scalar(m1, yf, -0.5, None, ALU.is_lt), reads=[R_yf], writes=[R_m1])
        P.op("dve", lambda g: g.tensor_tensor(yf, yf, m1, ALU.add), reads=[R_yf, R_m1], writes=[R_yf])
        P.op("act", lambda g: g.activation(out=dst[:], in_=yf, func=AF.Sin, scale=2.0 * math.pi), reads=[R_yf], writes=[R_dst])

    if stop == "p0":
        P.barrier(); return nc
    def proj_fm(l, gi, M, wbufs, consume, bankset):
        if state.get("pre") == (l, gi):
            wb, R_wb = wpre, R_wpre
            state["pre"] = None
        else:
            wb, R_wb = wbufs[state.setdefault("wfm_i", 0) % len(wbufs)]
            state["wfm_i"] += 1
            P.dma("pool", wb.rearrange("p a b -> p (a b)"), winfm_d[l, gi], writes=[R_wb])
        banks = [bankset * 4 + i for i in range(4)]
        for kc in range(16):
            for tb in range(4):
                mm(pss[banks[tb]][0:M, :], wb[:, kc, 0:M], big[:, kc, tb * 512:(tb + 1) * 512], kc == 0, kc == 15,
                   [R_wb, R_big], R_ps[banks[tb]])
        for tb in range(4):
            consume(tb, pss[banks[tb]][0:M, :], R_ps[banks[tb]])

    def proj_tm(l, gi, wtb, consume, banks):
        wb, R_wb = wtb
        P.dma("pool", wb.rearrange("p a b -> p (a b)"), wintm_d[l, gi], writes=[R_wb])
        for t in range(NT):
            bk = banks[t % len(banks)]
            for kc in range(16):
                mm(pss[bk][:, 0:256], big[:, kc, t * 128:(t + 1) * 128], wb[:, kc, :], kc == 0, kc == 15,
                   [R_wb, R_big], R_ps[bk])
            consume(t, pss[bk][:, 0:256], R_ps[bk])

    def prefetch(l, gi):
        P.dma("pool", wpre.rearrange("p a b -> p (a b)"), winfm_d[l, gi], writes=[R_wpre])
        state["pre"] = (l, gi)

    def post_norm_store(l, o_acc, R_oacc, nsub, wcols, szs, R_sz, fc0, tmps, denom, psi):
        (sq, R_sq), (t1, R_t1), (rinv, R_rinv), (u, R_u) = tmps[0:4]
        for blk in range(4):
            sl = slice(blk * 512, (blk + 1) * 512)
            rstd_part([(o_acc[:, j, sl], R_oacc) for j in range(nsub)], 512, denom, tmps[0:3], psi)
            for j in range(nsub):
                P.op("dve", lambda g: g.scalar_tensor_tensor(out=u[:, 0:512], in0=o_acc[:, j, sl], scalar=wcols[j], in1=rinv[:, 0:512],
                                                             op0=ALU.mult, op1=ALU.mult),
                     reads=[R_oacc, R_rinv, R_small, R_misc], writes=[R_u])
                ost, R_ost = tmps[4 + state.setdefault("ost_i", 0) % 2]
                state["ost_i"] += 1
                P.op("pool", lambda g: g.tensor_tensor(ost[:, 0:512], u[:, 0:512], szs[:, j, sl], ALU.mult),
                     reads=[R_u, R_sz], writes=[R_ost])
                dst_ap, R_d = oT_dst(fc0 + j, sl)
                P.dma("sp", dst_ap, ost[:, 0:512], reads=[R_ost], writes=[R_d], nowaw=True)

    for l in range(L):
        sb_l = SBASE + l * SLW
        lam_init = 0.8 - 0.6 * math.exp(-0.3 * l)

        def sc(off, n=1):
            return small[:, sb_l + off: sb_l + off + n]

        new_phase("A")
        cact, R_cact = ar.alloc([128, 16], F32, "cact")
        c2, R_c2 = ar.alloc([128, 16, 2], F32, "c2")
        crep, R_crep = ar.alloc([128, 16, 128], F32, "crep")
        bg, R_bg = ar.alloc([128, D], F32, "bg")
        wab = [ar.alloc([128, 16, 128], F32, "wa%d" % i) for i in range(3)]
        P.op("act", lambda g: g.activation(out=cact, in_=small[:, 0:16], func=AF.Silu), reads=[R_small], writes=[R_cact])
        P.op("dve", lambda g: g.tensor_copy(c2, cact.unsqueeze(2).to_broadcast([128, 16, 2])), reads=[R_cact], writes=[R_c2])
        P.op("dve", lambda g: g.tensor_copy(crep, cact.unsqueeze(2).to_broadcast([128, 16, 128])), reads=[R_cact], writes=[R_crep])
        P.dma("sp", bg, bgate_d[l].to_broadcast([128, D]), writes=[R_bg])
        for g_ in range(48):
            wa, R_wa = wab[g_ % 3]
            P.dma("sp", wa.rearrange("p a b -> p (a b)"), wada_d[l, g_], writes=[R_wa])
            if g_ < 32:
                for kc in range(16):
                    mm(pss[0][:, 2 * g_:2 * g_ + 2], wa[:, kc, :], c2[:, kc, :], kc == 0, kc == 15, [R_wa, R_c2], R_ps[0])
                if g_ == 31:
                    P.op("dve", lambda g: g.tensor_tensor(modc[:, 0:32], pss[0][:, 0:64:2], sc(16, 32), ALU.add),
                         reads=[R_ps[0], R_small], writes=[R_modc])
                    P.op("dve", lambda g: g.scalar_tensor_tensor(out=modc[:, 32:48], in0=modc[:, 16:32], scalar=1.0, in1=sc(0, 16),
                                                                 op0=ALU.add, op1=ALU.mult),
                         reads=[R_modc, R_small], writes=[R_modc])
            else:
                gg = g_ - 32
                bk = 1 + (gg // 4) % 2
                c0 = (gg % 4) * 128
                for kc in range(16):
                    mm(pss[bk][:, c0:c0 + 128], crep[:, kc, :], wa[:, kc, :], kc == 0, kc == 15, [R_wa, R_crep], R_ps[bk])
                if gg % 4 == 3:
                    sl = slice((gg // 4) * 512, (gg // 4 + 1) * 512)
                    P.op("dve", lambda g: g.tensor_tensor(gate_bc[:, sl], pss[bk][:], bg[:, sl], ALU.add),
                         reads=[R_ps[bk], R_bg], writes=[R_gate])

        if stop == "pA":
            P.barrier(); return nc
        new_phase("B")
        xsrc = x_d if l == 0 else x1_d
        xts = [ar.alloc([128, D], F32, "xt%d" % i) for i in range(2)]
        xn, R_xn = ar.alloc([128, 4, D], BF16, "xn")
        junk, R_junk = ar.alloc([128, D], BF16, "junk")
        ssq, R_ssq = ar.alloc([128, 8], F32, "ssq")
        hT = big
        for tb in range(4):
            for i in range(4):
                t = tb * 4 + i
                xt, R_xt = xts[t % 2]
                P.dma("sp", xt, xsrc[t * 128:(t + 1) * 128, :], reads=[R_x1] if l > 0 else [], writes=[R_xt])
                P.op("act", lambda g: g.activation(out=junk, in_=xt, func=AF.Square, accum_out=ssq[:, 0:1]),
                     reads=[R_xt], writes=[R_junk, R_ssq])
                P.op("act", lambda g: g.activation(out=ssq[:, 1:2], in_=ssq[:, 0:1], func=AF.Sqrt, scale=1.0 / D, bias=eps_ap),
                     reads=[R_ssq, R_misc], writes=[R_ssq])
                P.op("dve", lambda g: g.reciprocal(ssq[:, 2:3], ssq[:, 1:2]), reads=[R_ssq], writes=[R_ssq])
                P.op("dve", lambda g: g.tensor_scalar(xn[:, i, :], xt, ssq[:, 2:3], None, ALU.mult),
                     reads=[R_xt, R_ssq], writes=[R_xn])
            for fc in range(16):
                bk = fc % 4
                pT = pss[bk][:].bitcast(BF16)
                for i in range(4):
                    P.op("pe", lambda g: g.transpose(pT[:, i * 128:(i + 1) * 128], xn[:, i, fc * 128:(fc + 1) * 128], ident_b),
                         reads=[R_xn, R_cmb], writes=[R_ps[bk]], pe_accum=i > 0)
                dst = hT[:, fc, tb * 512:(tb + 1) * 512]
                if alt() == "act":
                    P.op("act", lambda g: g.activation(out=dst, in_=pT[:, 0:512], func=AF.Identity,
                                                       scale=modc[:, 32 + fc:33 + fc], bias=modc[:, fc:fc + 1]),
                         reads=[R_ps[bk], R_modc], writes=[R_big])
                else:
                    P.op("dve", lambda g: g.tensor_scalar(dst, pT[:, 0:512], modc[:, 32 + fc:33 + fc], modc[:, fc:fc + 1],
                                                          ALU.mult, ALU.add),
                         reads=[R_ps[bk], R_modc], writes=[R_big])

        prefetch(l, 2)
        if stop == "pB":
            P.barrier(); return nc
        for pr in range(1):
            new_phase("gla")
            wfm = [ar.alloc([128, 16, 128], BF16, "wfm%d" % i) for i in range(2)]
            wtm = ar.alloc([128, 16, 256], BF16, "wtm")
            glrT, R_glrT = ar.alloc([16, S], BF16, "glrT")
            wlr, R_wlr = ar.alloc([16, 256], BF16, "wlr")
            bcs, R_bcs = ar.alloc([128, S], F32, "bcs")
            eb, R_eb = ar.alloc([128, S], F32, "eb")
            q_eT, R_qe = ar.alloc([128, S], BF16, "q_eT")
            k_eT, R_ke = ar.alloc([128, S], BF16, "k_eT")
            v_g, R_vg = ar.alloc([128, NT, 256], BF16, "v_g")
            sz, R_sz = ar.alloc([128, 2, S], BF16, "sz")
            ke_tok, R_ket = ar.alloc([128, NT, 128], BF16, "ke_tok")
            o_acc, R_oacc = ar.alloc([128, 2, S], F32, "o_acc")
            e1, R_e1 = ar.alloc([128, 512], F32, "e1")
            dec, R_dec = ar.alloc([128, 16], F32, "dec")
            nb, R_nb = ar.alloc([128, 2], F32, "nb")
            Sp, R_Sp = ar.alloc([128, 256], F32, "Sp")
            Stmp, R_Stmp = ar.alloc([128, 256], F32, "Stmp")
            Sbf, R_Sbf = ar.alloc([128, 256], BF16, "Sbf")
            attm, R_attm = ar.alloc([128, 2, 128], BF16, "attm")
            tmps = [ar.alloc([128, 2, 512], BF16, "sq"), ar.alloc([128, 512], F32, "t1"), ar.alloc([128, 512], F32, "rinv"),
                    ar.alloc([128, 512], F32, "u"), ar.alloc([128, 512], BF16, "ost"), ar.alloc([128, 512], BF16, "ost2")]

            def c_glr(tb, ps, R):
                copy_on("act", glrT[0:16, tb * 512:(tb + 1) * 512], ps, [R], [R_glrT])
            proj_fm(l, 2, 16, wfm, c_glr, 0)
            P.op("dve", lambda g: g.tensor_copy(wlr, sc(624, 256)[0:16, :]), reads=[R_small], writes=[R_wlr])
            P.op("dve", lambda g: g.tensor_scalar(nb, sc(48, 2), -1.0, None, ALU.mult), reads=[R_small], writes=[R_nb])
            for blk in range(4):
                sl = slice(blk * 512, (blk + 1) * 512)
                bk = 4 + blk % 2
                mm(pss[bk][:], wlr[0:16, pr * 128:(pr + 1) * 128], glrT[0:16, sl], True, True, [R_wlr, R_glrT], R_ps[bk])
                P.op("act", lambda g: g.activation(out=e1, in_=pss[bk][:], func=AF.Exp, scale=-1.0, bias=nb[:, pr:pr + 1]),
                     reads=[R_ps[bk], R_nb], writes=[R_e1])
                P.op("act", lambda g: g.activation(out=bcs[:, sl], in_=e1, func=AF.Ln, bias=one_ap), reads=[R_e1, R_misc], writes=[R_bcs])
            P.op("dve", lambda g: g.tensor_tensor_scan(out=bcs, data0=rmask[:], data1=bcs, initial=0.0, op0=ALU.mult, op1=ALU.add),
                 reads=[R_rmask, R_bcs], writes=[R_bcs])
            P.op("act", lambda g: g.activation(out=eb, in_=bcs, func=AF.Exp, scale=-1.0 / 16.0), reads=[R_bcs], writes=[R_eb])
            P.op("dve", lambda g: g.tensor_copy(dec, eb[:, 127:S:128]), reads=[R_eb], writes=[R_dec])
            P.op("act", lambda g: g.activation(out=bcs, in_=bcs, func=AF.Exp, scale=1.0 / 16.0), reads=[R_bcs], writes=[R_bcs])
            enb = bcs

            def c_q(tb, ps, R):
                sl = slice(tb * 512, (tb + 1) * 512)
                P.op("dve", lambda g: g.scalar_tensor_tensor(out=q_eT[:, sl], in0=ps, scalar=0.125, in1=eb[:, sl], op0=ALU.mult, op1=ALU.mult),
                     reads=[R, R_eb], writes=[R_qe])
            proj_fm(l, 0, 128, wfm, c_q, 1)

            def c_k(tb, ps, R):
                sl = slice(tb * 512, (tb + 1) * 512)
                P.op("dve", lambda g: g.tensor_tensor(k_eT[:, sl], ps, enb[:, sl], ALU.mult), reads=[R, R_bcs], writes=[R_ke])
            proj_fm(l, 1, 128, wfm, c_k, 0)

            def c_v(t, ps, R):
                copy_on("act", v_g[:, t, :], ps, [R], [R_vg])
            proj_tm(l, 0, wtm, c_v, [4, 5])
            for hh in range(2):
                def c_z(tb, ps, R):
                    P.op("act", lambda g: g.activation(out=sz[:, hh, tb * 512:(tb + 1) * 512], in_=ps, func=AF.Silu), reads=[R], writes=[R_sz])
                proj_fm(l, 3 + hh, 128, wfm, c_z, hh)
            for t4 in range(4):
                bk = 6 + t4 % 2
                pT = pss[bk][:].bitcast(BF16)
                for i in range(4):
                    t = t4 * 4 + i
                    P.op("pe", lambda g: g.transpose(pT[:, i * 128:(i + 1) * 128], k_eT[:, t * 128:(t + 1) * 128], ident_b),
                         reads=[R_ke, R_cmb], writes=[R_ps[bk]], pe_accum=i > 0)
                P.op("dve", lambda g: g.tensor_copy(ke_tok[:, t4 * 4:(t4 + 1) * 4, :], pT[:, 0:512].rearrange("p (a b) -> p a b", b=128)),
                     reads=[R_ps[bk]], writes=[R_ket])
            P.op("dve", lambda g: g.memset(Sp, 0.0), writes=[R_Sp])
            for n in range(NT):
                ch = slice(n * 128, (n + 1) * 128)
                ba, bo, bkv = n % 2, 2 + n % 2, 4 + n % 2
                for hh in range(2):
                    hp = slice(64 * hh, 64 * hh + 64)
                    mm(pss[ba][:, hh * 128:(hh + 1) * 128], k_eT[hp, ch], q_eT[hp, ch], True, True, [R_ke, R_qe], R_ps[ba])
                P.op("dve", lambda g: g.tensor_tensor(attm, pss[ba][:, 0:256].rearrange("p (a b) -> p a b", b=128),
                                                      cm_f[:, CTU:CTU + 1, :].to_broadcast([128, 2, 128]), ALU.mult),
                     reads=[R_ps[ba], R_cmf], writes=[R_attm])
                for hh in range(2):
                    hp = slice(64 * hh, 64 * hh + 64)
                    vs = slice(hh * 128, (hh + 1) * 128)
                    mm(pss[bo][:, vs], v_g[:, n, vs], attm[:, hh, :], True, n == 0, [R_vg, R_attm], R_ps[bo])
                    if n > 0:
                        mm(pss[bo][:, vs], Sbf[hp, vs], q_eT[hp, ch], False, True, [R_Sbf, R_qe], R_ps[bo])
                P.op("act", lambda g: g.activation(out=o_acc[:, :, ch], in_=pss[bo][:, 0:256].rearrange("p (a b) -> p a b", b=128), func=AF.Copy),
                     reads=[R_ps[bo]], writes=[R_oacc])
                if n < NT - 1:
                    mm(pss[bkv][:, 0:256], ke_tok[:, n, :], v_g[:, n, :], True, True, [R_ket, R_vg], R_ps[bkv])
                    P.op("dve", lambda g: g.tensor_tensor(Stmp, Sp, pss[bkv][:, 0:256], ALU.add), reads=[R_Sp, R_ps[bkv]], writes=[R_Stmp])
                    P.op("dve", lambda g: g.tensor_scalar(Sp, Stmp, dec[:, n:n + 1], None, ALU.mult), reads=[R_Stmp, R_dec], writes=[R_Sp])
                    P.op("pool", lambda g: g.tensor_scalar(Sbf, Stmp, dec[:, n:n + 1], None, ALU.mult), reads=[R_Stmp, R_dec], writes=[R_Sbf])
            for hh in range(2):
                post_norm_store(l, o_acc[:, hh:hh + 1, :], R_oacc, 1, [sc(50)], sz[:, hh:hh + 1, :], R_sz, 2 * pr + hh, tmps, 128.0, 6)
            prefetch(l, 11)

        if stop == "pC1":
            P.barrier(); return nc
        for h in range(2):
            new_phase("gdn")
            wfm = [ar.alloc([128, 16, 128], BF16, "wfm%d" % i) for i in range(2)]
            xbf, R_xbf = ar.alloc([128, S + 8], BF16, "xbf")
            diag, R_diag = ar.alloc([128, 4, 128], BF16, "diag")
            cs, R_cs = ar.alloc([128, S], F32, "cs")
            knT, R_kn = ar.alloc([128, S], BF16, "knT")
            qnT, R_qn = ar.alloc([128, S], BF16, "qnT")
            cvT, R_cv = ar.alloc([128, S], BF16, "cvT")
            kbT, R_kb = ar.alloc([128, S], BF16, "kbT")
            q_eT, R_qe = ar.alloc([128, S], BF16, "q_eT")
            vb_tok, R_vb = ar.alloc([128, NT, 128], BF16, "vb_tok")
            kbg_tok, R_kbg = ar.alloc([128, NT, 128], BF16, "kbg_tok")
            kt_tok, R_kt = ar.alloc([128, NT, 128], BF16, "kt_tok")
            gc, R_gc = ar.alloc([128, S], F32, "gc")
            beta, R_beta = ar.alloc([128, S], BF16, "beta")
            eg, R_eg = ar.alloc([128, S], F32, "eg")
            tl, R_tl = ar.alloc([128, S], F32, "tl")
            dabT, R_dab = tl[0:8, :], R_tl
            cols, R_cols = ar.alloc([128, 6, 16], F32, "cols")
            nA, R_nA = ar.alloc([128, 4], F32, "nA")
            Sp, R_Sp = ar.alloc([128, 128], F32, "Sp")
            Sbf, R_Sbf = ar.alloc([128, 128], BF16, "Sbf")
            dm = [ar.alloc([128, 128], F32, "dm%d" % i) for i in range(4)]
            mb = [ar.alloc([128, 128], BF16, "mb%d" % i) for i in range(10)]
            u_sb, R_u = ar.alloc([128, 128], F32, "u_sb")
            tmps = [ar.alloc([128, 2, 512], BF16, "sq"), ar.alloc([128, 512], F32, "t1"), ar.alloc([128, 512], F32, "rinv"),
                    ar.alloc([128, 512], F32, "u"), ar.alloc([128, 512], BF16, "ost"), ar.alloc([128, 512], BF16, "ost2")]
            e1, R_e1 = tmps[3]
            o_acc, R_oacc = cs.rearrange("p (a b) -> p a b", a=1), R_cs

            def c_dab(tb, ps, R):
                copy_on("act", dabT[0:8, tb * 512:(tb + 1) * 512], ps, [R], [R_dab])
            proj_fm(l, 11, 8, wfm, c_dab, 0)
            P.op("dve", lambda g: g.memset(xbf[:, 0:3], 0.0), writes=[R_xbf])
            for which in range(3):
                ti = which * 4 + h
                for j in range(4):
                    P.op("dve", lambda g: g.tensor_scalar(diag[:, j, :], ident_f, sc(64 + ti * 4 + j), None, ALU.mult),
                         reads=[R_cmf, R_small], writes=[R_diag])

                def c_x(tb, ps, R):
                    copy_on(alt(), xbf[:, 3 + tb * 512:3 + (tb + 1) * 512], ps, [R], [R_xbf])
                proj_fm(l, 5 + 2 * which + h, 128, wfm, c_x, 1)
                for blk in range(4):
                    sl = slice(blk * 512, (blk + 1) * 512)
                    bk = blk % 2
                    for j in range(4):
                        mm(pss[bk][:], diag[:, j, :], xbf[:, blk * 512 + j: blk * 512 + j + 512], j == 0, j == 3, [R_diag, R_xbf], R_ps[bk])
                    if which == 2:
                        P.op("act", lambda g: g.activation(out=cvT[:, sl], in_=pss[bk][:], func=AF.Silu), reads=[R_ps[bk]], writes=[R_cv])
                    else:
                        P.op("act", lambda g: g.activation(out=cs[:, sl], in_=pss[bk][:], func=AF.Silu), reads=[R_ps[bk]], writes=[R_cs])
                if which < 2:
                    for blk in range(4):
                        sl = slice(blk * 512, (blk + 1) * 512)
                        rinv, R_rinv = rstd_part([(cs[:, sl], R_cs)], 512, 1.0, tmps[0:3], 2 + blk % 2)
                        if which == 0:
                            P.op("dve", lambda g: g.scalar_tensor_tensor(out=qnT[:, sl], in0=cs[:, sl], scalar=128.0 ** -0.5, in1=rinv[:, 0:512],
                                                                         op0=ALU.mult, op1=ALU.mult), reads=[R_cs, R_rinv], writes=[R_qn])
                        else:
                            P.op("dve", lambda g: g.tensor_tensor(knT[:, sl], cs[:, sl], rinv[:, 0:512], ALU.mult), reads=[R_cs, R_rinv], writes=[R_kn])
            if stop == "g1":
                P.barrier(); return nc
            P.op("act", lambda g: g.activation(out=nA, in_=sc(56, 4), func=AF.Exp), reads=[R_small], writes=[R_nA])
            P.op("dve", lambda g: g.tensor_scalar(nA, nA, -1.0, None, ALU.mult), reads=[R_nA], writes=[R_nA])
            for blk in range(4):
                sl = slice(blk * 512, (blk + 1) * 512)
                bk = 4 + blk % 2
                mm(pss[bk][:], selm[0:8, h, :], dabT[0:8, sl], True, True, [R_selm, R_dab], R_ps[bk])
                P.op("act", lambda g: g.activation(out=e1, in_=pss[bk][:], func=AF.Exp, bias=sc(60 + h)), reads=[R_ps[bk], R_small], writes=[R_e1])
                P.op("act", lambda g: g.activation(out=gc[:, sl], in_=e1, func=AF.Ln, bias=one_ap), reads=[R_e1, R_misc], writes=[R_gc])
                bk2 = 6 + blk % 2
                mm(pss[bk2][:], selm[0:8, 4 + h, :], dabT[0:8, sl], True, True, [R_selm, R_dab], R_ps[bk2])
                P.op("act", lambda g: g.activation(out=beta[:, sl], in_=pss[bk2][:], func=AF.Sigmoid), reads=[R_ps[bk2]], writes=[R_beta])
            P.op("dve", lambda g: g.tensor_tensor_scan(out=gc, data0=rmask[:], data1=gc, initial=0.0, op0=ALU.mult, op1=ALU.add),
                 reads=[R_rmask, R_gc], writes=[R_gc])
            P.op("dve", lambda g: g.tensor_scalar(gc, gc, nA[:, h:h + 1], None, ALU.mult), reads=[R_gc, R_nA], writes=[R_gc])
            gcl = cols[:, 0, :]
            cd = cols[:, 1, :]
            P.op("dve", lambda g: g.tensor_copy(gcl, gc[:, 127:S:128]), reads=[R_gc], writes=[R_cols])
            P.op("act", lambda g: g.activation(out=cd, in_=gcl, func=AF.Exp), reads=[R_cols], writes=[R_cols])
            P.op("act", lambda g: g.activation(out=eg, in_=gc, func=AF.Exp), reads=[R_gc], writes=[R_eg])
            P.op("dve", lambda g: g.tensor_tensor(q_eT, qnT, eg, ALU.mult), reads=[R_qn, R_eg], writes=[R_qe])
            P.op("dve", lambda g: g.tensor_tensor(kbT, knT, beta, ALU.mult), reads=[R_kn, R_beta], writes=[R_kb])
            P.op("pool", lambda g: g.tensor_tensor(eg, eg, beta, ALU.mult), reads=[R_eg, R_beta], writes=[R_eg])
            for n in range(NT):
                ch = slice(n * 128, (n + 1) * 128)
                P.op("act", lambda g: g.activation(out=tl[:, ch], in_=gc[:, ch], func=AF.Exp, scale=-1.0, bias=gcl[:, n:n + 1]),
                     reads=[R_gc, R_cols], writes=[R_tl])
            if stop == "g2":
                P.barrier(); return nc
            for qi, (src, R_src) in enumerate(((gc, R_gc), (beta, R_beta), (eg, R_eg), (tl, R_tl))):
                oh = cm_b[:, CI, 0:2] if src is beta else cm_f[:, CI, 0:2]
                for n in range(NT):
                    c0 = (qi * NT + n) * 2
                    mm(pss[3][:, c0:c0 + 2], src[:, n * 128:(n + 1) * 128], oh, True, True, [R_src, R_cmf, R_cmb], R_ps[3])
            P.op("dve", lambda g: g.tensor_copy(cols[:, 2:6, :], pss[3][:, 0:128:2].rearrange("p (a b) -> p a b", b=NT)),
                 reads=[R_ps[3]], writes=[R_cols])
            gc_col, beta_col, bexp_col, tail_col = (cols[:, i, :] for i in (2, 3, 4, 5))
            if stop == "g2b":
                P.barrier(); return nc
            for t in range(NT):
                bk = t % 2
                pT = pss[bk][:].bitcast(BF16)
                ts = slice(t * 128, (t + 1) * 128)
                P.op("pe", lambda g: g.transpose(pT[:, 0:128], knT[:, ts], ident_b), reads=[R_kn, R_cmb], writes=[R_ps[bk]])
                P.op("pe", lambda g: g.transpose(pT[:, 128:256], cvT[:, ts], ident_b), reads=[R_cv, R_cmb], writes=[R_ps[bk]], pe_accum=True)
                P.op("dve", lambda g: g.tensor_scalar(kbg_tok[:, t, :], pT[:, 0:128], bexp_col[:, t:t + 1], None, ALU.mult),
                     reads=[R_ps[bk], R_cols], writes=[R_kbg])
                P.op("dve", lambda g: g.tensor_scalar(kt_tok[:, t, :], pT[:, 0:128], tail_col[:, t:t + 1], None, ALU.mult),
                     reads=[R_ps[bk], R_cols], writes=[R_kt])
                P.op("dve", lambda g: g.tensor_scalar(vb_tok[:, t, :], pT[:, 128:256], beta_col[:, t:t + 1], None, ALU.mult),
                     reads=[R_ps[bk], R_cols], writes=[R_vb])

            if stop == "g3":
                P.barrier(); return nc
            P.barrier(reset=False)
            sz, R_sz = tl.bitcast(BF16)[:, 0:S].rearrange("p (a b) -> p a b", a=1), Reg("sz")
            mb2 = [(xbf[:, i * 128:(i + 1) * 128], Reg("mb2_%d" % i)) for i in range(16)]

            def c_z(tb, ps, R):
                P.op("act", lambda g: g.activation(out=sz[:, 0, tb * 512:(tb + 1) * 512], in_=ps, func=AF.Silu), reads=[R], writes=[R_sz])
            proj_fm(l, 12 + h, 128, wfm, c_z, 1)
            if stop == "g4":
                P.barrier(); return nc
            P.barrier(reset=False)
            G = 4
            eg4 = eg.rearrange("p (t g c) -> p t g c", g=G, c=128)
            (dA4, R_dA), (dB4, R_dB), (dD4, R_dD), (u4, R_u4) = [(eg4[:, i], Reg("f4_%d" % i)) for i in range(4)]
            pool16 = []
            for src in (beta, cvT, wfm[0][0].rearrange("p a b -> p (a b)"), wfm[1][0].rearrange("p a b -> p (a b)"),
                        tl.bitcast(BF16)[:, S:2 * S]):
                v4 = src.rearrange("p (t g c) -> p t g c", g=G, c=128)
                pool16 += [(v4[:, i], Reg("b4_%d" % len(pool16))) for i in range(4)]
            ((Pm4, R_P), (PT4, R_PT), (qk4, R_qk), (Pd4, R_Pd), (PTd4, R_PTd), (Po32, R_Po32), (PTo32, R_PTo32), (Po64, R_Po64),
             Abuf0, ATbuf0, Abuf1, ATbuf1, sq0, sqT0, sq1, sqT1, (U1, R_U1), (T1, R_T1), (wT4, R_wT)) = pool16[0:19]
            vnew, R_vn = mb[0]

            def bc4(blk, f32=True):
                src = cm_f if f32 else cm_b
                return src[:, blk:blk + 1, :].to_broadcast([128, G, 128])

            def flat(t4):
                return t4.rearrange("p g c -> p (g c)")

            def p4(bank):
                return pss[bank][:].rearrange("p (g c) -> p g c", c=128)

            P.op("dve", lambda g: g.memset(Sp, 0.0), writes=[R_Sp])
            P.op("dve", lambda g: g.memset(Sbf, 0.0), writes=[R_Sbf])
            for bb in range(NT // G):
                ns = [bb * G + g_ for g_ in range(G)]
                chs = [slice(n * 128, (n + 1) * 128) for n in ns]
                for g_, n in enumerate(ns):
                    gcc = gc_col[:, n:n + 1]
                    P.op("dve", lambda g: g.tensor_scalar(dA4[:, g_, :], gc[:, chs[g_]], gcc, 0.0, ALU.subtract, ALU.max),
                         reads=[R_gc, R_cols], writes=[R_dA])
                    P.op("dve", lambda g: g.tensor_scalar(dB4[:, g_, :], gc[:, chs[g_]], gcc, 0.0, ALU.subtract, ALU.min),
                         reads=[R_gc, R_cols], writes=[R_dB])
                P.op("act", lambda g: g.activation(out=flat(dA4), in_=flat(dA4), func=AF.Exp, scale=-1.0), reads=[R_dA], writes=[R_dA])
                P.op("act", lambda g: g.activation(out=flat(dB4), in_=flat(dB4), func=AF.Exp), reads=[R_dB], writes=[R_dB])
                P.op("dve", lambda g: g.tensor_tensor(dA4, dA4, bc4(CNSL), ALU.mult), reads=[R_dA, R_cmf], writes=[R_dA])
                P.op("dve", lambda g: g.tensor_tensor(dD4, dB4, bc4(CNSU), ALU.mult), reads=[R_dB, R_cmf], writes=[R_dD])
                P.op("dve", lambda g: g.tensor_tensor(dB4, dB4, bc4(CTU), ALU.mult), reads=[R_dB, R_cmf], writes=[R_dB])
                for g_ in range(G):
                    cs_ = slice(g_ * 128, (g_ + 1) * 128)
                    mm(pss[0][:, cs_], kbT[:, chs[g_]], knT[:, chs[g_]], True, True, [R_kb, R_kn], R_ps[0])
                for g_ in range(G):
                    cs_ = slice(g_ * 128, (g_ + 1) * 128)
                    mm(pss[1][:, cs_], knT[:, chs[g_]], kbT[:, chs[g_]], True, True, [R_kb, R_kn], R_ps[1])
                for g_ in range(G):
                    cs_ = slice(g_ * 128, (g_ + 1) * 128)
                    mm(pss[2][:, cs_], knT[:, chs[g_]], qnT[:, chs[g_]], True, True, [R_kn, R_qn], R_ps[2])
                P.op("dve", lambda g: g.tensor_tensor(Pm4, p4(0), dA4, ALU.mult), reads=[R_ps[0], R_dA], writes=[R_P])
                P.op("dve", lambda g: g.tensor_tensor(PT4, p4(1), dD4, ALU.mult), reads=[R_ps[1], R_dD], writes=[R_PT])
                P.op("dve", lambda g: g.tensor_tensor(qk4, p4(2), dB4, ALU.mult), reads=[R_ps[2], R_dB], writes=[R_qk])
                for dst, R_d, src, R_s, mk in ((Pd4, R_Pd, Pm4, R_P, CB32), (PTd4, R_PTd, PT4, R_PT, CB32), (Po32, R_Po32, Pm4, R_P, CO32),
                                               (PTo32, R_PTo32, PT4, R_PT, CO32), (Po64, R_Po64, Pm4, R_P, CO64)):
                    P.op("dve", lambda g: g.tensor_tensor(dst, src, bc4(mk, False), ALU.mult), reads=[R_s, R_cmb], writes=[R_d])
                Acur, ATcur, Anxt, ATnxt = Abuf0, ATbuf0, Abuf1, ATbuf1
                P.op("dve", lambda g: g.tensor_tensor(Acur[0], Pd4, bc4(CI, False), ALU.add), reads=[R_Pd, R_cmb], writes=[Acur[1]])
                P.op("dve", lambda g: g.tensor_tensor(ATcur[0], PTd4, bc4(CI, False), ALU.add), reads=[R_PTd, R_cmb], writes=[ATcur[1]])

                def mm4(bank, lhs4, R_l, rhs4, R_r, start=True):
                    for g_ in range(G):
                        cs_ = slice(g_ * 128, (g_ + 1) * 128)
                        mm(pss[bank][:, cs_], lhs4[:, g_, :], rhs4[:, g_, :], start, start or g_ == G - 1, [R_l, R_r], R_ps[bank])

                def add_mm4(bank, base, lhs4, R_l, rhs4, R_r):
                    mm(pss[bank][:], ident_b, flat(base[0]), True, False, [R_cmb, base[1]], R_ps[bank])
                    mm4(bank, lhs4, R_l, rhs4, R_r, start=False)

                cur = ((Pd4, R_Pd), (PTd4, R_PTd))
                sqb = [(sq0, sqT0), (sq1, sqT1)]
                for lev in range(4):
                    (cP, R_cP), (cPT, R_cPT) = cur
                    (nP, R_nP), (nPT, R_nPT) = sqb[lev % 2]
                    mm4(3, cPT, R_cPT, cP, R_cP)
                    mm4(4, cP, R_cP, cPT, R_cPT)
                    copy_on("act", flat(nP), pss[3][:], [R_ps[3]], [R_nP])
                    copy_on("dve", flat(nPT), pss[4][:], [R_ps[4]], [R_nPT])
                    add_mm4(5, Acur, nPT, R_nPT, Acur[0], Acur[1])
                    add_mm4(6, ATcur, nP, R_nP, ATcur[0], ATcur[1])
                    copy_on("dve", flat(Anxt[0]), pss[5][:], [R_ps[5]], [Anxt[1]])
                    copy_on("act", flat(ATnxt[0]), pss[6][:], [R_ps[6]], [ATnxt[1]])
                    Acur, Anxt = Anxt, Acur
                    ATcur, ATnxt = ATnxt, ATcur
                    cur = ((nP, R_nP), (nPT, R_nPT))
                mm4(3, PTo32, R_PTo32, Acur[0], Acur[1])
                mm4(4, Po32, R_Po32, ATcur[0], ATcur[1])
                copy_on("act", flat(U1), pss[3][:], [R_ps[3]], [R_U1])
                copy_on("dve", flat(T1), pss[4][:], [R_ps[4]], [R_T1])
                add_mm4(5, Acur, ATcur[0], ATcur[1], U1, R_U1)
                add_mm4(6, ATcur, Acur[0], Acur[1], T1, R_T1)
                copy_on("dve", flat(Anxt[0]), pss[5][:], [R_ps[5]], [Anxt[1]])
                copy_on("act", flat(ATnxt[0]), pss[6][:], [R_ps[6]], [ATnxt[1]])
                Acur, Anxt = Anxt, Acur
                ATcur, ATnxt = ATnxt, ATcur
                mm4(4, Po64, R_Po64, ATcur[0], ATcur[1])
                copy_on("dve", flat(T1), pss[4][:], [R_ps[4]], [R_T1])
                add_mm4(6, ATcur, Acur[0], Acur[1], T1, R_T1)
                copy_on("act", flat(ATnxt[0]), pss[6][:], [R_ps[6]], [ATnxt[1]])
                AT4, R_AT = ATnxt
                for g_, n in enumerate(ns):
                    cs_ = slice(g_ * 128, (g_ + 1) * 128)
                    mm(pss[3][:, cs_], AT4[:, g_, :], vb_tok[:, n, :], True, True, [R_AT, R_vb], R_ps[3])
                for g_, n in enumerate(ns):
                    cs_ = slice(g_ * 128, (g_ + 1) * 128)
                    mm(pss[4][:, cs_], kbg_tok[:, n, :], AT4[:, g_, :], True, True, [R_AT, R_kbg], R_ps[4])
                copy_on("act", flat(u4), pss[3][:], [R_ps[3]], [R_u4])
                copy_on("dve", flat(wT4), pss[4][:], [R_ps[4]], [R_wT])
                for g_, n in enumerate(ns):
                    ch = chs[g_]
                    if n > 0:
                        mm(pss[0][:, 0:128], wT4[:, g_, :], Sbf, True, True, [R_wT, R_Sbf], R_ps[0])
                        P.op("dve", lambda g: g.tensor_tensor(vnew, u4[:, g_, :], pss[0][:, 0:128], ALU.subtract),
                             reads=[R_u4, R_ps[0]], writes=[R_vn])
                    else:
                        copy_on("dve", vnew, u4[:, g_, :], [R_u4], [R_vn])
                    if n > 0:
                        mm(pss[7][:, 0:128], Sbf, q_eT[:, ch], True, False, [R_Sbf, R_qe], R_ps[7])
                    mm(pss[7][:, 0:128], vnew, qk4[:, g_, :], n == 0, True, [R_vn, R_qk], R_ps[7])
                    copy_on("act", o_acc[:, 0, ch], pss[7][:, 0:128], [R_ps[7]], [R_oacc])
                    if n < NT - 1:
                        mm(pss[1][:, 0:128], kt_tok[:, n, :], vnew, True, True, [R_kt, R_vn], R_ps[1])
                        P.op("dve", lambda g: g.scalar_tensor_tensor(out=Sp, in0=Sp, scalar=cd[:, n:n + 1], in1=pss[1][:, 0:128],
                                                                     op0=ALU.mult, op1=ALU.add), reads=[R_Sp, R_cols, R_ps[1]], writes=[R_Sp])
                        copy_on("act", Sbf, Sp, [R_Sp], [R_Sbf])
            post_norm_store(l, o_acc, R_oacc, 1, [sc(51)], sz, R_sz, 2 + h, tmps, 128.0, 6)
            prefetch(l, 11 if h == 0 else 14)

        P.collective("AllGather", PAIRS, oTl_t[0].ap().opt(), oTg_t[0].ap().opt(), reads=[R_oTl[0]], writes=[R_oTg[0]])
        if stop == "pC2":
            P.barrier(); return nc
        for h in range(2):
            new_phase("diff")
            wfm = [ar.alloc([128, 16, 128], BF16, "wfm%d" % i) for i in range(3)]
            wtm = ar.alloc([128, 16, 256], BF16, "wtm")
            qT, R_q = ar.alloc([128, 2, S], BF16, "qT")
            kT, R_k = ar.alloc([128, 2, S], BF16, "kT")
            v_sb, R_v = ar.alloc([128, NT, 256], BF16, "v_sb")
            sz, R_sz = ar.alloc([128, 2, S], BF16, "sz")
            ebuf = [ar.alloc([128, 512], BF16, "e%d" % i) for i in range(3)]
            qn, R_qnb = ar.alloc([128, 512], BF16, "qn")
            ta, R_ta = ar.alloc([128, 512], F32, "ta")
            tb_, R_tb = ar.alloc([128, 512], F32, "tb")
            tO, R_tO = ar.alloc([128, 2, 512], F32, "tO")
            rs, R_rs = ar.alloc([128, 512], F32, "rs")
            sq2, R_sq2 = ar.alloc([128, 1024], BF16, "sq2")
            t1b, R_t1b = ar.alloc([128, 1024], F32, "t1b")
            rinvb, R_rinvb = ar.alloc([128, 1024], F32, "rinvb")
            qnb, R_qnb2 = ar.alloc([128, 1024], BF16, "qnb")
            tab, R_tab = ar.alloc([128, 1024], F32, "tab")
            tbb, R_tbb = ar.alloc([128, 1024], F32, "tbb")
            lamt, R_lam = ar.alloc([128, 8], F32, "lam")
            lprod, R_lprod = ar.alloc([128, 256], F32, "lprod")
            wn2, R_wn2 = ar.alloc([128, 2], F32, "wn2")
            tmps = [ar.alloc([128, 2, 512], BF16, "sq"), ar.alloc([128, 512], F32, "t1"), ar.alloc([128, 512], F32, "rinv"),
                    ar.alloc([128, 512], F32, "u"), ar.alloc([128, 512], BF16, "ost"), ar.alloc([128, 512], BF16, "ost2")]
            P.op("dve", lambda g: g.tensor_tensor(lprod[:, 0:128], sc(112, 128), sc(240, 128), ALU.mult), reads=[R_small], writes=[R_lprod])
            P.op("dve", lambda g: g.tensor_tensor(lprod[:, 128:256], sc(368, 128), sc(496, 128), ALU.mult), reads=[R_small], writes=[R_lprod])
            P.op("dve", lambda g: g.tensor_reduce(lamt[:, 0:2], lprod.rearrange("p (a b) -> p a b", b=128), AX.X, ALU.add),
                 reads=[R_lprod], writes=[R_lam])
            P.op("act", lambda g: g.activation(out=lamt[:, 2:4], in_=lamt[:, 0:2], func=AF.Exp), reads=[R_lam], writes=[R_lam])
            P.op("dve", lambda g: g.scalar_tensor_tensor(out=lamt[:, 4:5], in0=lamt[:, 3:4], scalar=-lam_init, in1=lamt[:, 2:3],
                                                         op0=ALU.add, op1=ALU.subtract), reads=[R_lam], writes=[R_lam])
            nlam = lamt[:, 4:5]
            P.op("dve", lambda g: g.tensor_scalar(wn2, sc(54, 2), 1.0 - lam_init, None, ALU.mult), reads=[R_small], writes=[R_wn2])
            units = [(which, m, half) for which in range(2) for m in range(2) for half in range(2)]
            qk_meta = ((qT, R_q, 14, 52), (kT, R_k, 18, 53))
            ustate = {}

            def qk_issue(ui):
                which, m, half = units[ui]
                g0 = qk_meta[which][2]
                wb, R_wb = qk_w[which * 2 + m]
                banks = [4 + 2 * (ui % 2), 5 + 2 * (ui % 2)]
                for kc in range(16):
                    for i in range(2):
                        tb = half * 2 + i
                        mm(pss[banks[i]][:], wb[:, kc, :], big[:, kc, tb * 512:(tb + 1) * 512], kc == 0, kc == 15,
                           [R_wb, R_big], R_ps[banks[i]])

            def qk_consume(ui):
                which, m, half = units[ui]
                dstT, R_dst, g0, wcol = qk_meta[which]
                b0_ = 4 + 2 * (ui % 2)
                ps2 = psall[:, b0_ * 512:(b0_ + 2) * 512]
                Rb = [R_ps[b0_], R_ps[b0_ + 1]]
                sl2 = slice(half * 1024, (half + 1) * 1024)
                P.op("act", lambda g: g.activation(out=sq2, in_=ps2, func=AF.Square), reads=Rb, writes=[R_sq2])
                for i in range(2):
                    mm(pss[2 + i][:], ones_b, sq2[:, i * 512:(i + 1) * 512], True, True, [R_sq2, R_cmb], R_ps[2 + i])
                P.op("act", lambda g: g.activation(out=t1b, in_=psall[:, 2 * 512:4 * 512], func=AF.Ln, scale=1.0 / 128.0, bias=eps_ap),
                     reads=[R_ps[2], R_ps[3], R_misc], writes=[R_t1b])
                P.op("act", lambda g: g.activation(out=rinvb, in_=t1b, func=AF.Exp, scale=-0.5), reads=[R_t1b], writes=[R_rinvb])
                P.op("dve", lambda g: g.scalar_tensor_tensor(out=qnb, in0=ps2, scalar=sc(wcol), in1=rinvb, op0=ALU.mult, op1=ALU.mult),
                     reads=Rb + [R_rinvb, R_small], writes=[R_qnb2])
                for i in range(2):
                    mm(pss[i][:], cm_b[:, CP, :], qnb[:, i * 512:(i + 1) * 512], True, True, [R_cmb, R_qnb2], R_ps[i])
                P.op("pool", lambda g: g.tensor_tensor(tab, qnb, cosT[:, sl2], ALU.mult), reads=[R_qnb2, R_cos], writes=[R_tab])
                P.op("dve", lambda g: g.tensor_tensor(tbb, psall[:, 0:1024], sinT[:, sl2], ALU.mult), reads=[R_ps[0], R_ps[1], R_sin], writes=[R_tbb])
                P.op("pool", lambda g: g.tensor_tensor(dstT[:, m, sl2], tab, tbb, ALU.add), reads=[R_tab, R_tbb], writes=[R_dst])
            qk_w = []
            for which_ in range(2):
                for m_ in range(2):
                    gi_ = qk_meta[which_][2] + 2 * h + m_
                    if state.get("pre") == (l, gi_):
                        qk_w.append((wpre, R_wpre))
                        state["pre"] = None
                        continue
                    wb_, R_wb_ = wfm[sum(1 for w_ in qk_w if w_[0] is not wpre)]
                    P.dma("pool", wb_.rearrange("p a b -> p (a b)"), winfm_d[l, gi_], writes=[R_wb_])
                    qk_w.append((wb_, R_wb_))
            qk_issue(0)
            for ui in range(len(units)):
                if ui + 1 < len(units):
                    qk_issue(ui + 1)
                qk_consume(ui)

            def c_v(t, ps, R):
                copy_on(alt(), v_sb[:, t, :], ps, [R], [R_v])
            proj_tm(l, 1 + h, wtm, c_v, [0, 1])
            for j in range(2):
                def c_z(tb, ps, R):
                    P.op("act", lambda g: g.activation(out=sz[:, j, tb * 512:(tb + 1) * 512], in_=ps, func=AF.Silu), reads=[R], writes=[R_sz])
                proj_fm(l, 22 + 2 * h + j, 128, wfm, c_z, j)
            if h == 1:
                for fc in range(16):
                    P.dma("pool", big[:, fc, :], wout_d[l][:, fc * D:(fc + 1) * D], writes=[R_big], nowaw=fc > 0)
            scale = 128.0 ** -0.5
            steps = [(qb, m, kc) for qb in range(4) for m in range(2) for kc in range(4 * (qb + 1))]

            def qk_mm(si):
                qb, m, kc = steps[si]
                col0 = max(kc - 4 * qb, 0) * 128
                bsc = si % 2
                mm(pss[bsc][:, col0:512], kT[:, m, kc * 128:(kc + 1) * 128], qT[:, m, qb * 512 + col0:(qb + 1) * 512], True, True,
                   [R_k, R_q], R_ps[bsc])
            qk_mm(0)
            for si, (qb, m, kc) in enumerate(steps):
                qs = slice(qb * 512, (qb + 1) * 512)
                nk = 4 * (qb + 1)
                bo0, bo1, bs = (2, 3, 4) if m == 0 else (5, 6, 7)
                if si + 1 < len(steps):
                    qk_mm(si + 1)
                bsc = si % 2
                c = kc - 4 * qb
                col0 = max(c, 0) * 128
                e, R_e = ebuf[si % 3]
                P.op("act", lambda g: g.activation(out=e[:, col0:512], in_=pss[bsc][:, col0:512], func=AF.Exp, scale=scale),
                     reads=[R_ps[bsc]], writes=[R_e])
                if c >= 0:
                    P.op("pool", lambda g: g.tensor_tensor(e[:, col0:col0 + 128], e[:, col0:col0 + 128], cm_b[:, CTU, :], ALU.mult),
                         reads=[R_e, R_cmb], writes=[R_e])
                first, last = kc == 0, kc == nk - 1
                mm(pss[bo0][:, col0:512], v_sb[:, kc, 0:128], e[:, col0:512], first, last, [R_v, R_e], R_ps[bo0])
                mm(pss[bo1][:, col0:512], v_sb[:, kc, 128:256], e[:, col0:512], first, last, [R_v, R_e], R_ps[bo1])
                mm(pss[bs][:, col0:512], ones_b, e[:, col0:512], first, last, [R_cmb, R_e], R_ps[bs])
                if not last:
                    continue
                P.op("dve", lambda g: g.reciprocal(rs, pss[bs][:]), reads=[R_ps[bs]], writes=[R_rs])
                for j, bo in enumerate((bo0, bo1)):
                    if m == 0:
                        P.op("dve", lambda g: g.tensor_tensor(tO[:, j, :], pss[bo][:], rs, ALU.mult), reads=[R_ps[bo], R_rs], writes=[R_tO])
                    else:
                        P.op("dve", lambda g: g.scalar_tensor_tensor(out=ta, in0=pss[bo][:], scalar=nlam, in1=rs, op0=ALU.mult, op1=ALU.mult),
                             reads=[R_ps[bo], R_rs, R_lam], writes=[R_ta])
                        P.op("pool", lambda g: g.tensor_tensor(tO[:, j, :], tO[:, j, :], ta, ALU.add), reads=[R_tO, R_ta], writes=[R_tO])
                if m == 0:
                    continue
                (sq, R_sq), (t1, R_t1), (rinv, R_rinv), (u, R_u) = tmps[0:4]
                rstd_part([(tO[:, j, :], R_tO) for j in range(2)], 512, 256.0, tmps[0:3], 4)
                for j in range(2):
                    ost, R_ost = tmps[4 + j]
                    P.op("dve", lambda g: g.scalar_tensor_tensor(out=u, in0=tO[:, j, :], scalar=wn2[:, j:j + 1], in1=rinv, op0=ALU.mult, op1=ALU.mult),
                         reads=[R_tO, R_rinv, R_wn2], writes=[R_u])
                    P.op("pool", lambda g: g.tensor_tensor(ost, u, sz[:, j, qs], ALU.mult), reads=[R_u, R_sz], writes=[R_ost])
                    dst_ap, R_d = oT_dst(4 + 2 * h + j, qs)
                    P.dma("sp", dst_ap, ost, reads=[R_ost], writes=[R_d], nowaw=True)
            if h == 0:
                prefetch(l, 16)

        P.collective("AllGather", PAIRS, oTl_t[1].ap().opt(), oTg_t[1].ap().opt(), reads=[R_oTl[1]], writes=[R_oTg[1]])
        if stop == "pC3":
            P.barrier(); return nc
        new_phase("D")
        wo = big
        xts = [ar.alloc([128, D], F32, "xt%d" % i) for i in range(2)]
        obs = [ar.alloc([128, 16, 512], BF16, "ob%d" % i) for i in range(2)]
        xos = [ar.alloc([128, D], F32, "xo%d" % i) for i in range(2)]
        ytmp, R_yt = ar.alloc([128, 512], F32, "ytmp")
        xsrc = x_d if l == 0 else x1_d
        xdst = out_d if l == L - 1 else x1_d
        R_dst = R_out if l == L - 1 else R_x1
        for tb in range(4):
            ob, R_ob = obs[tb % 2]
            for i2 in range(2):
                P.dma("sp", ob[:, 8 * i2:8 * i2 + 8, :], oTg_d[i2][:, tb * 512:(tb + 1) * 512].rearrange("(c p) t -> p c t", p=128),
                      reads=[R_oTg[i2]], writes=[R_ob], nowaw=i2 > 0)
            for i in range(4):
                t = tb * 4 + i
                xt, R_xt = xts[t % 2]
                xo, R_xo = xos[t % 2]
                P.dma("sp", xt, xsrc[t * 128:(t + 1) * 128, :], reads=[R_x1] if l > 0 else [], writes=[R_xt])
                for ng in range(4):
                    ns = slice(ng * 512, (ng + 1) * 512)
                    bk = (t * 4 + ng) % 4
                    for fc in range(16):
                        mm(pss[bk][:], ob[:, fc, i * 128:(i + 1) * 128], wo[:, fc, ns], fc == 0, fc == 15, [R_ob, R_big], R_ps[bk])
                    P.op("dve", lambda g: g.tensor_tensor(ytmp, pss[bk][:], gate_bc[:, ns], ALU.mult), reads=[R_ps[bk], R_gate], writes=[R_yt])
                    P.op("pool", lambda g: g.tensor_tensor(xo[:, ns], ytmp, xt[:, ns], ALU.add), reads=[R_yt, R_xt], writes=[R_xo])
                P.dma("sp", xdst[t * 128:(t + 1) * 128, :], xo, reads=[R_xo], writes=[R_dst], nowaw=True)

    P.barrier(collectives=True)
    if scopes and state.get("scope") is not None:
        nc.leave_named_scope(state["scope"][0], state["scope"][1], False)
    return nc


def _col(v):
    return np.ascontiguousarray(np.asarray(v, np.float32).reshape(-1, 128).T)


def _consts():
    p = np.arange(128)[:, None]
    j = np.arange(128)[None, :]
    cm = np.zeros((128, NCM, 128), np.float32)
    cm[:, CI] = (p == j)
    cm[:, CO] = 1.0
    prot = np.zeros((128, 128), np.float32)
    prot[(j[0, :64] + 64), j[0, :64]] = -1.0
    prot[(j[0, 64:] - 64), j[0, 64:]] = 1.0
    cm[:, CP] = prot
    cm[:, CTU] = (p <= j)
    cm[:, CNSU] = -1.0 * (p < j)
    cm[:, CNSL] = -1.0 * (p > j)
    bd32 = (p // 32 == j // 32)
    bd64 = (p // 64 == j // 64)
    cm[:, CB32] = bd32
    cm[:, CO32] = bd64 & ~bd32
    cm[:, CO64] = ~bd64
    selm = np.zeros((8, 8, 128), np.float32)
    for k in range(8):
        selm[k, k, :] = 1.0
    rmask = np.ones((128, S), np.float32)
    rmask[:, 0::128] = 0.0
    half = 64
    inv_freq = (10000.0 ** (-(np.arange(half, dtype=np.float32) / np.float32(half)))).astype(np.float32)
    invf = np.concatenate([inv_freq, inv_freq]).astype(np.float64) / (2.0 * math.pi)
    return cm.reshape(128, NCM * 128), selm.reshape(8, 8 * 128), rmask, invf.astype(np.float32)


def _fm_groups_local(hh):
    g = [(hh * 128, 128), (256 + hh * 128, 128), (1024, 16)]
    for base in (1040, 1552, 2064, 2576):
        g += [(base + 128 * (2 * hh + i), 128) for i in range(2)]
    g.append("dab")
    g += [(3096 + 128 * (2 * hh + i), 128) for i in range(2)]
    for base in (3608, 4632, 6680):
        g += [(base + 256 * (2 * hh + lh) + 128 * m, 128) for lh in range(2) for m in range(2)]
    return g


def _tm_groups_local(hh):
    return [(512 + hh * 256, 256), (5656 + 256 * (2 * hh), 256), (5656 + 256 * (2 * hh + 1), 256)]


OUT_CHUNK_ORDER = [0, 1, 4, 5, 2, 3, 6, 7, 8, 9, 10, 11, 12, 13, 14, 15]


def _prep_shared(inp):
    f = lambda k: np.asarray(inp[k], np.float32)
    cm, selm, rmask, invf = _consts()
    w_in = f("w_in")
    per_half = []
    for hh in range(2):
        winfm = np.zeros((2, 26, 128, 16, 128), np.float32)
        wintm = np.zeros((2, 3, 128, 16, 256), np.float32)
        for l in range(2):
            wl = w_in[l].reshape(16, 128, -1)
            for gi, grp in enumerate(_fm_groups_local(hh)):
                if grp == "dab":
                    for i in range(2):
                        winfm[l, gi, :, :, i] = wl[:, :, 3088 + 2 * hh + i].T
                        winfm[l, gi, :, :, 4 + i] = wl[:, :, 3092 + 2 * hh + i].T
                else:
                    c0, n = grp
                    winfm[l, gi, :, :, :n] = wl[:, :, c0:c0 + n].transpose(1, 0, 2)
            for gi, (c0, n) in enumerate(_tm_groups_local(hh)):
                wintm[l, gi] = wl[:, :, c0:c0 + n].transpose(1, 0, 2)
        sm = np.zeros((128, NS), np.float32)
        sm[:, 16] = invf
        for l in range(2):
            b = SBASE + l * SLW
            sm[:, b:b + 16] = _col(f("norm_w")[l])
            sm[:, b + 16:b + 48] = _col(f("b_ada")[l, :2 * D])
            sm[:, b + 48] = f("gla_b_lr")[l, hh * 128:(hh + 1) * 128]
            sm[:, b + 50] = f("gla_norm_w")[l]
            sm[:, b + 51] = f("gdn_norm_w")[l]
            sm[:, b + 52] = f("diff_q_norm_w")[l]
            sm[:, b + 53] = f("diff_k_norm_w")[l]
            sm[:, b + 54:b + 56] = _col(f("diff_norm_w")[l])
            sm[:, b + 56:b + 58] = f("gdn_a_log")[l][None, 2 * hh:2 * hh + 2]
            sm[:, b + 60:b + 62] = f("gdn_dt_bias")[l][None, 2 * hh:2 * hh + 2]
            cw = f("gdn_conv_w")[l].reshape(4, 12, 128)
            for which in range(3):
                for lh in range(2):
                    ti = which * 4 + lh
                    sm[:, b + 64 + ti * 4:b + 64 + ti * 4 + 4] = cw[:, which * 4 + 2 * hh + lh, :].T
            sm[:, b + 112:b + 624] = f("diff_lambda")[l].reshape(1, 512)
            sm[0:16, b + 624:b + 752] = f("gla_w_lr")[l][:, hh * 128:(hh + 1) * 128]
        per_half.append({"winfm": winfm.reshape(2, 26, 128, 16 * 128), "wintm": wintm.reshape(2, 3, 128, 16 * 256), "small": sm})
    wada = np.ascontiguousarray(f("w_ada").reshape(2, 16, 128, 48, 128).transpose(0, 3, 2, 1, 4)).reshape(2, 48, 128, 16 * 128)
    wo = f("w_out").reshape(2, 16, 128, D)[:, OUT_CHUNK_ORDER]
    wout = np.ascontiguousarray(wo.transpose(0, 2, 1, 3)).reshape(2, 128, 16 * D)
    bgate = np.ascontiguousarray(f("b_ada")[:, 2 * D:].reshape(2, 1, D))
    shared = {"cmat": cm, "selm": selm, "rmask": rmask, "wada": wada, "bgate": bgate, "wout": wout}
    return shared, per_half


def make_in_maps(inp, cores):
    shared, per_half = _prep_shared(inp)
    x = np.asarray(inp["x"], np.float32)
    c = np.asarray(inp["c"], np.float32)
    pos = np.asarray(inp["positions"], np.int32)
    maps = []
    for b, hh in cores:
        s = per_half[hh]["small"].copy()
        s[:, 0:16] = _col(c[b])
        m = dict(shared)
        m["winfm"] = per_half[hh]["winfm"]
        m["wintm"] = per_half[hh]["wintm"]
        m["x"] = np.ascontiguousarray(x[b])
        m["small"] = s
        m["pos"] = np.ascontiguousarray(pos[b:b + 1])
        maps.append(m)
    return maps


_NC_CACHE = {}


def kernel(**inputs):
    if "nc" not in _NC_CACHE:
        _NC_CACHE["nc"] = build(2)
    nc = _NC_CACHE["nc"]
    cores = [(i // 2, i % 2) for i in range(8)]
    maps = make_in_maps(inputs, cores)
    res = run_bass_kernel_spmd(nc, maps, core_ids=list(range(8)))
    out = np.stack([np.asarray(res.results[2 * b]["out"], np.float32) for b in range(4)], axis=0)
    return out
```

```python
import math
import numpy as np
import concourse.bass as bass
import concourse.mybir as mybir
from concourse.bass_utils import run_bass_kernel_spmd

F32 = mybir.dt.float32
BF16 = mybir.dt.bfloat16
I32 = mybir.dt.int32
AF = mybir.ActivationFunctionType
ALU = mybir.AluOpType
AX = mybir.AxisListType

S = 2048
D = 2048
NT = 16
EPS = 1e-6
SLW = 880
SBASE = 32
NS = SBASE + 2 * SLW
CI, CO, CP, CTU, CNSU, CNSL, CB32, CO32, CO64 = range(9)
NCM = 9


class Reg:
    __slots__ = ("name", "lw", "rd", "dsem", "local", "psum")

    def __init__(self, name, local=False, psum=False):
        self.name = name
        self.psum = psum
        self.lw = None
        self.rd = {}
        self.dsem = None
        self.local = local


class Prog:
    def __init__(self, nc):
        self.nc = nc
        self.eng = {"pe": nc.tensor, "act": nc.scalar, "dve": nc.vector, "pool": nc.gpsimd, "sp": nc.sync}
        self.sem, self.cnt, self.semobj = {}, {}, {}
        self.seen = {e: {} for e in self.eng}
        for e in self.eng:
            s = nc.alloc_semaphore(name="s_" + e)
            self.sem[e] = s
            self.semobj[e] = s
            self.cnt[e] = 0
        self.vc = {}
        self.dcnt = {}
        self.local_keys = []
        self.local_next = 0
        self.ninstr = 0
        self.nwait = 0

    def _wait(self, e, key, val):
        if self.seen[e].get(key, 0) >= val:
            return
        self.eng[e].wait_ge(self.semobj[key], val)
        self.nwait += 1
        se = self.seen[e]
        se[key] = val
        clk = self.vc.get((key, val))
        if clk:
            for k, v in clk.items():
                if se.get(k, 0) < v:
                    se[k] = v

    def _deps(self, e, reads, writes, pe_accum=False, nowaw=False):
        need = {}

        def add(k, v):
            if need.get(k, 0) < v:
                need[k] = v
        for r in reads:
            if r.lw is not None:
                add(*r.lw)
            if r.psum:
                for k, v in r.rd.items():
                    if k != e:
                        add(k, v)
        for w in writes:
            if w.lw is not None and not nowaw and not (pe_accum and w.lw[0] == "pe"):
                add(*w.lw)
            for k, v in w.rd.items():
                add(k, v)
        for k, v in sorted(need.items(), key=lambda kv: -kv[1] if isinstance(kv[0], str) else 0):
            self._wait(e, k, v)

    def _record(self, ev, reads, writes, nowaw=False):
        for r in reads:
            if r.rd.get(ev[0], 0) < ev[1]:
                r.rd[ev[0]] = ev[1]
        for w in writes:
            w.lw = ev
            if not nowaw:
                w.rd = {}

    def op(self, e, fn, reads=(), writes=(), pe_accum=False):
        self._deps(e, reads, writes, pe_accum)
        ins = fn(self.eng[e])
        self.cnt[e] += 1
        ins.then_inc(self.sem[e], 1)
        ev = (e, self.cnt[e])
        self.vc[ev] = dict(self.seen[e])
        self._record(ev, reads, writes)
        self.ninstr += 1

    def _newkey(self):
        key = ("d", len(self.dcnt))
        self.semobj[key] = self.nc.alloc_semaphore(name="d_%d" % len(self.dcnt))
        self.dcnt[key] = 0
        return key

    def _dkey(self, w):
        if w.dsem is None:
            if w.local:
                if self.local_next == len(self.local_keys):
                    self.local_keys.append(self._newkey())
                w.dsem = self.local_keys[self.local_next]
                self.local_next += 1
            else:
                w.dsem = self._newkey()
        return w.dsem

    def dma(self, q, out_ap, in_ap, reads=(), writes=(), nowaw=False, **kw):
        w = writes[0]
        self._deps(q, reads, writes, nowaw=nowaw)
        key = self._dkey(w)
        ins = self.eng[q].dma_start(out=out_ap, in_=in_ap, **kw)
        self.dcnt[key] += 16
        ins.then_inc(self.semobj[key], 16)
        ev = (key, self.dcnt[key])
        self.vc[ev] = dict(self.seen[q])
        self._record(ev, reads, writes, nowaw=nowaw)
        self.ninstr += 1

    def collective(self, kind, groups, in_ap, out_ap, reads, writes):
        self._deps("pool", reads, writes)
        key = ("c", len([k for k in self.dcnt if k[0] == "c"]))
        self.semobj[key] = self.nc.alloc_semaphore(name="c_%d" % key[1])
        ins = self.eng["pool"].collective_compute(kind, ALU.bypass, replica_groups=groups, ins=[in_ap], outs=[out_ap])
        ins.then_inc(self.semobj[key])
        self.dcnt[key] = 1
        ev = (key, 1)
        self.vc[ev] = dict(self.seen["pool"])
        self._record(ev, reads, writes)
        self.ninstr += 1

    def barrier(self, reset=True, collectives=False):
        evs = [(e, self.cnt[e]) for e in self.eng if self.cnt[e] > 0]
        evs += [(k, v) for k, v in self.dcnt.items() if v > 0 and (collectives or k[0] != "c")]
        for e in self.eng:
            for k, v in evs:
                if k != e:
                    self._wait(e, k, v)
        if reset:
            self.local_next = 0


class Arena:
    def __init__(self, nc, nwords):
        self.t = nc.alloc_sbuf_tensor("arena", [128, nwords], F32)
        self.n = nwords
        self.off = 0
        self.k = 0

    def reset(self):
        self.off = 0

    def alloc(self, shape, dt, name=None):
        free = int(np.prod(shape[1:]))
        words = free if dt in (F32, I32) else (free + 1) // 2
        words = (words + 7) // 8 * 8
        assert self.off + words <= self.n, ("arena overflow", name, self.off, words, self.n)
        v = self.t[:, self.off:self.off + words]
        self.off += words
        if dt == BF16:
            v = v.bitcast(BF16)[:, 0:free]
        elif dt == I32:
            v = v.bitcast(I32)[:, 0:free]
        else:
            v = v[:, 0:free]
        if len(shape) == 3:
            v = v.rearrange("p (a b) -> p a b", b=shape[2])
        self.k += 1
        if shape[0] < 128:
            v = v[0:shape[0]]
        return v, Reg(name or "a%d" % self.k, local=True)


def build(nlayers=2, debug=False, stop=None, scopes=False, ncores=8):
    nc = bass.Bass("TRN2", target_bir_lowering=False)
    P = Prog(nc)
    L = nlayers

    def din(name, shape, dt=F32):
        return nc.dram_tensor(name, shape, dt, kind="ExternalInput").ap()

    x_d = din("x", [S, D])
    small_d = din("small", [128, NS])
    cmat_d = din("cmat", [128, NCM * 128])
    selm_d = din("selm", [8, 8 * 128])
    rmask_d = din("rmask", [128, S])
    pos_d = din("pos", [1, S], I32)
    wada_d = din("wada", [2, 48, 128, 16 * 128])
    bgate_d = din("bgate", [2, 1, D])
    winfm_d = din("winfm", [2, 26, 128, 16 * 128])
    wintm_d = din("wintm", [2, 3, 128, 16 * 256])
    wout_d = din("wout", [2, 128, 16 * D])
    out_d = nc.dram_tensor("out", [S, D], F32, kind="ExternalOutput").ap()
    x1_d = nc.dram_tensor("x1s", [S, D], F32, kind="Internal").ap()
    OT_ROWS = (4, 2, 2)
    oTl_t = [nc.dram_tensor("oTl%d" % i, [n_ * 128, S], BF16, kind="Internal") for i, n_ in enumerate(OT_ROWS)]
    oTg_t = [nc.dram_tensor("oTg%d" % i, [2 * n_ * 128, S], BF16, kind="Internal") for i, n_ in enumerate(OT_ROWS)]
    oTl_d = [t.ap() for t in oTl_t]
    oTg_d = [t.ap() for t in oTg_t]
    R_oTl = [Reg("oTl%d" % i) for i in range(3)]
    R_oTg = [Reg("oTg%d" % i) for i in range(3)]
    PAIRS = [[2 * i, 2 * i + 1] for i in range(ncores // 2)]
    R_out, R_x1 = Reg("out"), Reg("x1")

    def oT_dst(fc, sl):
        ti, r = (0, fc) if fc < 4 else (1 + (fc - 4) // 2, (fc - 4) % 2)
        return oTl_d[ti][r * 128:(r + 1) * 128, sl], R_oTl[ti]

    def sb(name, shape, dt):
        return nc.alloc_sbuf_tensor("sb_" + name, shape, dt), Reg(name)

    big, R_big = sb("big", [128, 16, S], BF16)
    cosT, R_cos = sb("cosT", [128, S], BF16)
    sinT, R_sin = sb("sinT", [128, S], BF16)
    gate_bc, R_gate = sb("gate_bc", [128, D], F32)
    rmask, R_rmask = sb("rmask", [128, S], BF16)
    cm_f, R_cmf = sb("cm_f", [128, NCM, 128], F32)
    cm_b, R_cmb = sb("cm_b", [128, NCM, 128], BF16)
    selm, R_selm = sb("selm", [8, 8, 128], F32)
    small, R_small = sb("small", [128, NS], F32)
    modc, R_modc = sb("modc", [128, 48], F32)
    misc, R_misc = sb("misc", [128, 64], F32)
    wpre, R_wpre = sb("wpre", [128, 16, 128], BF16)
    ar = Arena(nc, (nc.sbuf_bytes_remaining - 2048) // 4)

    psall = nc.alloc_psum_tensor("psall", [128, 8 * 512], F32)
    pss = [psall[:, i * 512:(i + 1) * 512] for i in range(8)]
    R_ps = [Reg("ps%d" % i, psum=True) for i in range(8)]

    state = {"alt": 0}

    def alt():
        state["alt"] ^= 1
        return "act" if state["alt"] else "dve"

    def copy_on(e, out, in_, reads, writes):
        if e == "act":
            P.op("act", lambda g: g.activation(out=out, in_=in_, func=AF.Copy), reads=reads, writes=writes)
        else:
            P.op(e, lambda g: g.tensor_copy(out, in_), reads=reads, writes=writes)

    def mm(out, lhsT, rhs, start, stop, reads, w):
        P.op("pe", lambda g: g.matmul(out, lhsT=lhsT, rhs=rhs, start=start, stop=stop),
             reads=reads, writes=[w], pe_accum=not start)

    def new_phase(name="ph"):
        P.barrier()
        ar.reset()
        if scopes:
            if state.get("scope") is not None:
                nc.leave_named_scope(state["scope"][0], state["scope"][1], False)
            nm = "%s_%d" % (name, state.setdefault("nscope", 0))
            state["nscope"] += 1
            sid, _ = nc.enter_named_scope(nm, False)
            state["scope"] = (nm, sid)

    ident_b = cm_b[:, CI, :]
    ones_b = cm_b[:, CO, :]
    ident_f = cm_f[:, CI, :]

    def rstd_part(srcs, n, denom, tmps, psi):
        (sq, R_sq), (t1, R_t1), (rinv, R_rinv) = tmps
        for i, (sap, sreg) in enumerate(srcs):
            P.op("act", lambda g: g.activation(out=sq[:, i, 0:n], in_=sap, func=AF.Square), reads=[sreg], writes=[R_sq])
        for i in range(len(srcs)):
            mm(pss[psi][:, 0:n], ones_b, sq[:, i, 0:n], i == 0, i == len(srcs) - 1, [R_sq, R_cmb], R_ps[psi])
        P.op("act", lambda g: g.activation(out=t1[:, 0:n], in_=pss[psi][:, 0:n], func=AF.Ln, scale=1.0 / denom, bias=eps_ap),
             reads=[R_ps[psi], R_misc], writes=[R_t1])
        P.op("act", lambda g: g.activation(out=rinv[:, 0:n], in_=t1[:, 0:n], func=AF.Exp, scale=-0.5), reads=[R_t1], writes=[R_rinv])
        return rinv, R_rinv

    P.dma("sp", small[:], small_d, writes=[R_small])
    P.dma("sp", cm_f[:], cmat_d.rearrange("p (a b) -> p a b", b=128), writes=[R_cmf])
    P.dma("sp", selm[:], selm_d.rearrange("p (a b) -> p a b", b=128), writes=[R_selm])
    P.dma("pool", rmask[:], rmask_d, writes=[R_rmask])
    P.op("dve", lambda g: g.tensor_copy(cm_b[:], cm_f[:]), reads=[R_cmf], writes=[R_cmb])
    eps_ap = misc[:, 0:1]
    P.op("dve", lambda g: g.memset(misc[:], 0.0), writes=[R_misc])
    P.op("dve", lambda g: g.memset(misc[:, 0:1], EPS), reads=[], writes=[R_misc])
    P.op("dve", lambda g: g.memset(misc[:, 1:2], 1.0), reads=[], writes=[R_misc])
    one_ap = misc[:, 1:2]

    posi, R_posi = ar.alloc([128, S], I32, "posi")
    y, R_y = ar.alloc([128, S], F32, "y")
    yi, R_yi = ar.alloc([128, S], I32, "yi")
    yf, R_yf = ar.alloc([128, S], F32, "yf")
    fr, R_fr = ar.alloc([128, S], F32, "fr")
    m1, R_m1 = ar.alloc([128, S], F32, "m1")
    P.dma("sp", posi, pos_d.to_broadcast([128, S]), writes=[R_posi])
    P.op("dve", lambda g: g.tensor_copy(y, posi), reads=[R_posi], writes=[R_y])
    P.op("dve", lambda g: g.tensor_scalar(y, y, small[:, 16:17], None, ALU.mult), reads=[R_y, R_small], writes=[R_y])
    P.op("dve", lambda g: g.tensor_copy(yi, y), reads=[R_y], writes=[R_yi])
    P.op("dve", lambda g: g.tensor_copy(yf, yi), reads=[R_yi], writes=[R_yf])
    P.op("dve", lambda g: g.tensor_tensor(fr, y, yf, ALU.subtract), reads=[R_y, R_yf], writes=[R_fr])
    for which, dst, R_dst in ((0, sinT, R_sin), (1, cosT, R_cos)):
        src = fr
        if which == 1:
            P.op("dve", lambda g: g.tensor_scalar(y, fr, 0.25, None, ALU.add), reads=[R_fr], writes=[R_y])
            src = y
        R_src = R_fr if which == 0 else R_y
        P.op("dve", lambda g: g.tensor_scalar(m1, src, 0.5, None, ALU.is_gt), reads=[R_src], writes=[R_m1])
        P.op("dve", lambda g: g.tensor_tensor(yf, src, m1, ALU.subtract), reads=[R_src, R_m1], writes=[R_yf])
        P.op("dve", lambda g: g.tensor_scalar(m1, yf, -0.5, None, ALU.is_lt), reads=[R_yf], writes=[R_m1])
        P.op("dve", lambda g: g.tensor_tensor(yf, yf, m1, ALU.add), reads=[R_yf, R_m1], writes=[R_yf])
        P.op("act", lambda g: g.activation(out=dst[:], in_=yf, func=AF.Sin, scale=2.0 * math.pi), reads=[R_yf], writes=[R_dst])

    if stop == "p0":
        P.barrier(); return nc
    def proj_fm(l, gi, M, wbufs, consume, bankset):
        if state.get("pre") == (l, gi):
            wb, R_wb = wpre, R_wpre
            state["pre"] = None
        else:
            wb, R_wb = wbufs[state.setdefault("wfm_i", 0) % len(wbufs)]
            state["wfm_i"] += 1
            P.dma("pool", wb.rearrange("p a b -> p (a b)"), winfm_d[l, gi], writes=[R_wb])
        banks = [bankset * 4 + i for i in range(4)]
        for kc in range(16):
            for tb in range(4):
                mm(pss[banks[tb]][0:M, :], wb[:, kc, 0:M], big[:, kc, tb * 512:(tb + 1) * 512], kc == 0, kc == 15,
                   [R_wb, R_big], R_ps[banks[tb]])
        for tb in range(4):
            consume(tb, pss[banks[tb]][0:M, :], R_ps[banks[tb]])

    def proj_tm(l, gi, wtb, consume, banks):
        wb, R_wb = wtb
        P.dma("pool", wb.rearrange("p a b -> p (a b)"), wintm_d[l, gi], writes=[R_wb])
        for t in range(NT):
            bk = banks[t % len(banks)]
            for kc in range(16):
                mm(pss[bk][:, 0:256], big[:, kc, t * 128:(t + 1) * 128], wb[:, kc, :], kc == 0, kc == 15,
                   [R_wb, R_big], R_ps[bk])
            consume(t, pss[bk][:, 0:256], R_ps[bk])

    def prefetch(l, gi):
        P.dma("pool", wpre.rearrange("p a b -> p (a b)"), winfm_d[l, gi], writes=[R_wpre])
        state["pre"] = (l, gi)

    def post_norm_store(l, o_acc, R_oacc, nsub, wcols, szs, R_sz, fc0, tmps, denom, psi):
        (sq, R_sq), (t1, R_t1), (rinv, R_rinv), (u, R_u) = tmps[0:4]
        for blk in range(4):
            sl = slice(blk * 512, (blk + 1) * 512)
            rstd_part([(o_acc[:, j, sl], R_oacc) for j in range(nsub)], 512, denom, tmps[0:3], psi)
            for j in range(nsub):
                P.op("dve", lambda g: g.scalar_tensor_tensor(out=u[:, 0:512], in0=o_acc[:, j, sl], scalar=wcols[j], in1=rinv[:, 0:512],
                                                             op0=ALU.mult, op1=ALU.mult),
                     reads=[R_oacc, R_rinv, R_small, R_misc], writes=[R_u])
                ost, R_ost = tmps[4 + state.setdefault("ost_i", 0) % 2]
                state["ost_i"] += 1
                P.op("pool", lambda g: g.tensor_tensor(ost[:, 0:512], u[:, 0:512], szs[:, j, sl], ALU.mult),
                     reads=[R_u, R_sz], writes=[R_ost])
                dst_ap, R_d = oT_dst(fc0 + j, sl)
                P.dma("sp", dst_ap, ost[:, 0:512], reads=[R_ost], writes=[R_d], nowaw=True)

    for l in range(L):
        sb_l = SBASE + l * SLW
        lam_init = 0.8 - 0.6 * math.exp(-0.3 * l)

        def sc(off, n=1):
            return small[:, sb_l + off: sb_l + off + n]

        new_phase("A")
        cact, R_cact = ar.alloc([128, 16], F32, "cact")
        c2, R_c2 = ar.alloc([128, 16, 2], BF16, "c2")
        crep, R_crep = ar.alloc([128, 16, 128], BF16, "crep")
        bg, R_bg = ar.alloc([128, D], F32, "bg")
        wab = [ar.alloc([128, 16, 128], F32, "wa%d" % i) for i in range(4)]
        wbb = [ar.alloc([128, 16, 128], BF16, "wb%d" % i) for i in range(3)]
        P.op("act", lambda g: g.activation(out=cact, in_=small[:, 0:16], func=AF.Silu), reads=[R_small], writes=[R_cact])
        P.op("dve", lambda g: g.tensor_copy(c2, cact.unsqueeze(2).to_broadcast([128, 16, 2])), reads=[R_cact], writes=[R_c2])
        P.op("dve", lambda g: g.tensor_copy(crep, cact.unsqueeze(2).to_broadcast([128, 16, 128])), reads=[R_cact], writes=[R_crep])
        P.dma("sp", bg, bgate_d[l].to_broadcast([128, D]), writes=[R_bg])
        for g_ in range(48):
            wf, R_wf = wab[g_ % 4]
            P.dma("sp", wf.rearrange("p a b -> p (a b)"), wada_d[l, g_], writes=[R_wf])
            wa, R_wa = wbb[g_ % 3]
            copy_on(("act", "dve", "pool")[g_ % 3], wa.rearrange("p a b -> p (a b)"), wf.rearrange("p a b -> p (a b)"), [R_wf], [R_wa])
            if g_ < 32:
                for kc in range(16):
                    mm(pss[0][:, 2 * g_:2 * g_ + 2], wa[:, kc, :], c2[:, kc, :], kc == 0, kc == 15, [R_wa, R_c2], R_ps[0])
                if g_ == 31:
                    P.op("dve", lambda g: g.tensor_tensor(modc[:, 0:32], pss[0][:, 0:64:2], sc(16, 32), ALU.add),
                         reads=[R_ps[0], R_small], writes=[R_modc])
                    P.op("dve", lambda g: g.scalar_tensor_tensor(out=modc[:, 32:48], in0=modc[:, 16:32], scalar=1.0, in1=sc(0, 16),
                                                                 op0=ALU.add, op1=ALU.mult),
                         reads=[R_modc, R_small], writes=[R_modc])
            else:
                gg = g_ - 32
                bk = 1 + (gg // 4) % 2
                c0 = (gg % 4) * 128
                for kc in range(16):
                    mm(pss[bk][:, c0:c0 + 128], crep[:, kc, :], wa[:, kc, :], kc == 0, kc == 15, [R_wa, R_crep], R_ps[bk])
                if gg % 4 == 3:
                    sl = slice((gg // 4) * 512, (gg // 4 + 1) * 512)
                    P.op("dve", lambda g: g.tensor_tensor(gate_bc[:, sl], pss[bk][:], bg[:, sl], ALU.add),
                         reads=[R_ps[bk], R_bg], writes=[R_gate])

        if stop == "pA":
            P.barrier(); return nc
        new_phase("B")
        xsrc = x_d if l == 0 else x1_d
        xts = [ar.alloc([128, D], F32, "xt%d" % i) for i in range(2)]
        xn, R_xn = ar.alloc([128, 4, D], BF16, "xn")
        junk, R_junk = ar.alloc([128, D], BF16, "junk")
        ssq, R_ssq = ar.alloc([128, 8], F32, "ssq")
        hT = big
        for tb in range(4):
            for i in range(4):
                t = tb * 4 + i
                xt, R_xt = xts[t % 2]
                P.dma("sp", xt, xsrc[t * 128:(t + 1) * 128, :], reads=[R_x1] if l > 0 else [], writes=[R_xt])
                P.op("act", lambda g: g.activation(out=junk, in_=xt, func=AF.Square, accum_out=ssq[:, 0:1]),
                     reads=[R_xt], writes=[R_junk, R_ssq])
                P.op("act", lambda g: g.activation(out=ssq[:, 1:2], in_=ssq[:, 0:1], func=AF.Sqrt, scale=1.0 / D, bias=eps_ap),
                     reads=[R_ssq, R_misc], writes=[R_ssq])
                P.op("dve", lambda g: g.reciprocal(ssq[:, 2:3], ssq[:, 1:2]), reads=[R_ssq], writes=[R_ssq])
                P.op("dve", lambda g: g.tensor_scalar(xn[:, i, :], xt, ssq[:, 2:3], None, ALU.mult),
                     reads=[R_xt, R_ssq], writes=[R_xn])
            for fc in range(16):
                bk = fc % 4
                pT = pss[bk][:].bitcast(BF16)
                for i in range(4):
                    P.op("pe", lambda g: g.transpose(pT[:, i * 128:(i + 1) * 128], xn[:, i, fc * 128:(fc + 1) * 128], ident_b),
                         reads=[R_xn, R_cmb], writes=[R_ps[bk]], pe_accum=i > 0)
                dst = hT[:, fc, tb * 512:(tb + 1) * 512]
                if alt() == "act":
                    P.op("act", lambda g: g.activation(out=dst, in_=pT[:, 0:512], func=AF.Identity,
                                                       scale=modc[:, 32 + fc:33 + fc], bias=modc[:, fc:fc + 1]),
                         reads=[R_ps[bk], R_modc], writes=[R_big])
                else:
                    P.op("dve", lambda g: g.tensor_scalar(dst, pT[:, 0:512], modc[:, 32 + fc:33 + fc], modc[:, fc:fc + 1],
                                                          ALU.mult, ALU.add),
                         reads=[R_ps[bk], R_modc], writes=[R_big])

        prefetch(l, 2)
        if stop == "pB":
            P.barrier(); return nc
        for pr in range(1):
            new_phase("gla")
            wfm = [ar.alloc([128, 16, 128], BF16, "wfm%d" % i) for i in range(2)]
            wtm = ar.alloc([128, 16, 256], BF16, "wtm")
            glrT, R_glrT = ar.alloc([16, S], BF16, "glrT")
            wlr, R_wlr = ar.alloc([16, 256], BF16, "wlr")
            bcs, R_bcs = ar.alloc([128, S], F32, "bcs")
            eb, R_eb = ar.alloc([128, S], F32, "eb")
            q_eT, R_qe = ar.alloc([128, S], BF16, "q_eT")
            k_eT, R_ke = ar.alloc([128, S], BF16, "k_eT")
            v_g, R_vg = ar.alloc([128, NT, 256], BF16, "v_g")
            sz, R_sz = ar.alloc([128, 2, S], BF16, "sz")
            ke_tok, R_ket = ar.alloc([128, NT, 128], BF16, "ke_tok")
            o_acc, R_oacc = ar.alloc([128, 2, S], F32, "o_acc")
            e1, R_e1 = ar.alloc([128, 512], F32, "e1")
            dec, R_dec = ar.alloc([128, 16], F32, "dec")
            nb, R_nb = ar.alloc([128, 2], F32, "nb")
            Sp, R_Sp = ar.alloc([128, 256], F32, "Sp")
            Stmp, R_Stmp = ar.alloc([128, 256], F32, "Stmp")
            Sbf, R_Sbf = ar.alloc([128, 256], BF16, "Sbf")
            attm, R_attm = ar.alloc([128, 2, 128], BF16, "attm")
            tmps = [ar.alloc([128, 2, 512], BF16, "sq"), ar.alloc([128, 512], F32, "t1"), ar.alloc([128, 512], F32, "rinv"),
                    ar.alloc([128, 512], F32, "u"), ar.alloc([128, 512], BF16, "ost"), ar.alloc([128, 512], BF16, "ost2")]

            def c_glr(tb, ps, R):
                copy_on("act", glrT[0:16, tb * 512:(tb + 1) * 512], ps, [R], [R_glrT])
            proj_fm(l, 2, 16, wfm, c_glr, 0)
            P.op("dve", lambda g: g.tensor_copy(wlr, sc(624, 256)[0:16, :]), reads=[R_small], writes=[R_wlr])
            P.op("dve", lambda g: g.tensor_scalar(nb, sc(48, 2), -1.0, None, ALU.mult), reads=[R_small], writes=[R_nb])
            for blk in range(4):
                sl = slice(blk * 512, (blk + 1) * 512)
                bk = 4 + blk % 2
                mm(pss[bk][:], wlr[0:16, pr * 128:(pr + 1) * 128], glrT[0:16, sl], True, True, [R_wlr, R_glrT], R_ps[bk])
                P.op("act", lambda g: g.activation(out=e1, in_=pss[bk][:], func=AF.Exp, scale=-1.0, bias=nb[:, pr:pr + 1]),
                     reads=[R_ps[bk], R_nb], writes=[R_e1])
                P.op("act", lambda g: g.activation(out=bcs[:, sl], in_=e1, func=AF.Ln, bias=one_ap), reads=[R_e1, R_misc], writes=[R_bcs])
            P.op("dve", lambda g: g.tensor_tensor_scan(out=bcs, data0=rmask[:], data1=bcs, initial=0.0, op0=ALU.mult, op1=ALU.add),
                 reads=[R_rmask, R_bcs], writes=[R_bcs])
            P.op("act", lambda g: g.activation(out=eb, in_=bcs, func=AF.Exp, scale=-1.0 / 16.0), reads=[R_bcs], writes=[R_eb])
            P.op("dve", lambda g: g.tensor_copy(dec, eb[:, 127:S:128]), reads=[R_eb], writes=[R_dec])
            P.op("act", lambda g: g.activation(out=bcs, in_=bcs, func=AF.Exp, scale=1.0 / 16.0), reads=[R_bcs], writes=[R_bcs])
            enb = bcs

            def c_q(tb, ps, R):
                sl = slice(tb * 512, (tb + 1) * 512)
                P.op("dve", lambda g: g.scalar_tensor_tensor(out=q_eT[:, sl], in0=ps, scalar=0.125, in1=eb[:, sl], op0=ALU.mult, op1=ALU.mult),
                     reads=[R, R_eb], writes=[R_qe])
            proj_fm(l, 0, 128, wfm, c_q, 1)

            def c_k(tb, ps, R):
                sl = slice(tb * 512, (tb + 1) * 512)
                P.op("dve", lambda g: g.tensor_tensor(k_eT[:, sl], ps, enb[:, sl], ALU.mult), reads=[R, R_bcs], writes=[R_ke])
            proj_fm(l, 1, 128, wfm, c_k, 0)

            def c_v(t, ps, R):
                copy_on("act", v_g[:, t, :], ps, [R], [R_vg])
            proj_tm(l, 0, wtm, c_v, [4, 5])
            for hh in range(2):
                def c_z(tb, ps, R):
                    P.op("act", lambda g: g.activation(out=sz[:, hh, tb * 512:(tb + 1) * 512], in_=ps, func=AF.Silu), reads=[R], writes=[R_sz])
                proj_fm(l, 3 + hh, 128, wfm, c_z, hh)
            for t4 in range(4):
                bk = 6 + t4 % 2
                pT = pss[bk][:].bitcast(BF16)
                for i in range(4):
                    t = t4 * 4 + i
                    P.op("pe", lambda g: g.transpose(pT[:, i * 128:(i + 1) * 128], k_eT[:, t * 128:(t + 1) * 128], ident_b),
                         reads=[R_ke, R_cmb], writes=[R_ps[bk]], pe_accum=i > 0)
                P.op("dve", lambda g: g.tensor_copy(ke_tok[:, t4 * 4:(t4 + 1) * 4, :], pT[:, 0:512].rearrange("p (a b) -> p a b", b=128)),
                     reads=[R_ps[bk]], writes=[R_ket])
            P.op("dve", lambda g: g.memset(Sp, 0.0), writes=[R_Sp])
            for n in range(NT):
                ch = slice(n * 128, (n + 1) * 128)
                ba, bo, bkv = n % 2, 2 + n % 2, 4 + n % 2
                for hh in range(2):
                    hp = slice(64 * hh, 64 * hh + 64)
                    mm(pss[ba][:, hh * 128:(hh + 1) * 128], k_eT[hp, ch], q_eT[hp, ch], True, True, [R_ke, R_qe], R_ps[ba])
                P.op("dve", lambda g: g.tensor_tensor(attm, pss[ba][:, 0:256].rearrange("p (a b) -> p a b", b=128),
                                                      cm_f[:, CTU:CTU + 1, :].to_broadcast([128, 2, 128]), ALU.mult),
                     reads=[R_ps[ba], R_cmf], writes=[R_attm])
                for hh in range(2):
                    hp = slice(64 * hh, 64 * hh + 64)
                    vs = slice(hh * 128, (hh + 1) * 128)
                    mm(pss[bo][:, vs], v_g[:, n, vs], attm[:, hh, :], True, n == 0, [R_vg, R_attm], R_ps[bo])
                    if n > 0:
                        mm(pss[bo][:, vs], Sbf[hp, vs], q_eT[hp, ch], False, True, [R_Sbf, R_qe], R_ps[bo])
                P.op("act", lambda g: g.activation(out=o_acc[:, :, ch], in_=pss[bo][:, 0:256].rearrange("p (a b) -> p a b", b=128), func=AF.Copy),
                     reads=[R_ps[bo]], writes=[R_oacc])
                if n < NT - 1:
                    mm(pss[bkv][:, 0:256], ke_tok[:, n, :], v_g[:, n, :], True, True, [R_ket, R_vg], R_ps[bkv])
                    P.op("dve", lambda g: g.tensor_tensor(Stmp, Sp, pss[bkv][:, 0:256], ALU.add), reads=[R_Sp, R_ps[bkv]], writes=[R_Stmp])
                    P.op("dve", lambda g: g.tensor_scalar(Sp, Stmp, dec[:, n:n + 1], None, ALU.mult), reads=[R_Stmp, R_dec], writes=[R_Sp])
                    P.op("pool", lambda g: g.tensor_scalar(Sbf, Stmp, dec[:, n:n + 1], None, ALU.mult), reads=[R_Stmp, R_dec], writes=[R_Sbf])
            for hh in range(2):
                post_norm_store(l, o_acc[:, hh:hh + 1, :], R_oacc, 1, [sc(50)], sz[:, hh:hh + 1, :], R_sz, 2 * pr + hh, tmps, 128.0, 6)
            prefetch(l, 11)

        if stop == "pC1":
            P.barrier(); return nc
        for h in range(2):
            new_phase("gdn")
            wfm = [ar.alloc([128, 16, 128], BF16, "wfm%d" % i) for i in range(2)]
            xbf, R_xbf = ar.alloc([128, S + 8], BF16, "xbf")
            diag, R_diag = ar.alloc([128, 4, 128], BF16, "diag")
            cs, R_cs = ar.alloc([128, S], F32, "cs")
            knT, R_kn = ar.alloc([128, S], BF16, "knT")
            qnT, R_qn = ar.alloc([128, S], BF16, "qnT")
            cvT, R_cv = ar.alloc([128, S], BF16, "cvT")
            kbT, R_kb = ar.alloc([128, S], BF16, "kbT")
            q_eT, R_qe = ar.alloc([128, S], BF16, "q_eT")
            vb_tok, R_vb = ar.alloc([128, NT, 128], BF16, "vb_tok")
            kbg_tok, R_kbg = ar.alloc([128, NT, 128], BF16, "kbg_tok")
            kt_tok, R_kt = ar.alloc([128, NT, 128], BF16, "kt_tok")
            gc, R_gc = ar.alloc([128, S], F32, "gc")
            beta, R_beta = ar.alloc([128, S], BF16, "beta")
            eg, R_eg = ar.alloc([128, S], F32, "eg")
            tl, R_tl = ar.alloc([128, S], F32, "tl")
            dabT, R_dab = tl[0:8, :], R_tl
            cols, R_cols = ar.alloc([128, 6, 16], F32, "cols")
            nA, R_nA = ar.alloc([128, 4], F32, "nA")
            Sp, R_Sp = ar.alloc([128, 128], F32, "Sp")
            Sbf, R_Sbf = ar.alloc([128, 128], BF16, "Sbf")
            dm = [ar.alloc([128, 128], F32, "dm%d" % i) for i in range(4)]
            mb = [ar.alloc([128, 128], BF16, "mb%d" % i) for i in range(10)]
            u_sb, R_u = ar.alloc([128, 128], F32, "u_sb")
            tmps = [ar.alloc([128, 2, 512], BF16, "sq"), ar.alloc([128, 512], F32, "t1"), ar.alloc([128, 512], F32, "rinv"),
                    ar.alloc([128, 512], F32, "u"), ar.alloc([128, 512], BF16, "ost"), ar.alloc([128, 512], BF16, "ost2")]
            e1, R_e1 = tmps[3]
            o_acc, R_oacc = cs.rearrange("p (a b) -> p a b", a=1), R_cs

            def c_dab(tb, ps, R):
                copy_on("act", dabT[0:8, tb * 512:(tb + 1) * 512], ps, [R], [R_dab])
            proj_fm(l, 11, 8, wfm, c_dab, 0)
            P.op("dve", lambda g: g.memset(xbf[:, 0:3], 0.0), writes=[R_xbf])
            for which in range(3):
                ti = which * 4 + h
                for j in range(4):
                    P.op("dve", lambda g: g.tensor_scalar(diag[:, j, :], ident_f, sc(64 + ti * 4 + j), None, ALU.mult),
                         reads=[R_cmf, R_small], writes=[R_diag])

                def c_x(tb, ps, R):
                    copy_on(alt(), xbf[:, 3 + tb * 512:3 + (tb + 1) * 512], ps, [R], [R_xbf])
                proj_fm(l, 5 + 2 * which + h, 128, wfm, c_x, 1)
                for blk in range(4):
                    sl = slice(blk * 512, (blk + 1) * 512)
                    bk = blk % 2
                    for j in range(4):
                        mm(pss[bk][:], diag[:, j, :], xbf[:, blk * 512 + j: blk * 512 + j + 512], j == 0, j == 3, [R_diag, R_xbf], R_ps[bk])
                    if which == 2:
                        P.op("act", lambda g: g.activation(out=cvT[:, sl], in_=pss[bk][:], func=AF.Silu), reads=[R_ps[bk]], writes=[R_cv])
                    else:
                        P.op("act", lambda g: g.activation(out=cs[:, sl], in_=pss[bk][:], func=AF.Silu), reads=[R_ps[bk]], writes=[R_cs])
                if which < 2:
                    for blk in range(4):
                        sl = slice(blk * 512, (blk + 1) * 512)
                        rinv, R_rinv = rstd_part([(cs[:, sl], R_cs)], 512, 1.0, tmps[0:3], 2 + blk % 2)
                        if which == 0:
                            P.op("dve", lambda g: g.scalar_tensor_tensor(out=qnT[:, sl], in0=cs[:, sl], scalar=128.0 ** -0.5, in1=rinv[:, 0:512],
                                                                         op0=ALU.mult, op1=ALU.mult), reads=[R_cs, R_rinv], writes=[R_qn])
                        else:
                            P.op("dve", lambda g: g.tensor_tensor(knT[:, sl], cs[:, sl], rinv[:, 0:512], ALU.mult), reads=[R_cs, R_rinv], writes=[R_kn])
            if stop == "g1":
                P.barrier(); return nc
            P.op("act", lambda g: g.activation(out=nA, in_=sc(56, 4), func=AF.Exp), reads=[R_small], writes=[R_nA])
            P.op("dve", lambda g: g.tensor_scalar(nA, nA, -1.0, None, ALU.mult), reads=[R_nA], writes=[R_nA])
            for blk in range(4):
                sl = slice(blk * 512, (blk + 1) * 512)
                bk = 4 + blk % 2
                mm(pss[bk][:], selm[0:8, h, :], dabT[0:8, sl], True, True, [R_selm, R_dab], R_ps[bk])
                P.op("act", lambda g: g.activation(out=e1, in_=pss[bk][:], func=AF.Exp, bias=sc(60 + h)), reads=[R_ps[bk], R_small], writes=[R_e1])
                P.op("act", lambda g: g.activation(out=gc[:, sl], in_=e1, func=AF.Ln, bias=one_ap), reads=[R_e1, R_misc], writes=[R_gc])
                bk2 = 6 + blk % 2
                mm(pss[bk2][:], selm[0:8, 4 + h, :], dabT[0:8, sl], True, True, [R_selm, R_dab], R_ps[bk2])
                P.op("act", lambda g: g.activation(out=beta[:, sl], in_=pss[bk2][:], func=AF.Sigmoid), reads=[R_ps[bk2]], writes=[R_beta])
            P.op("dve", lambda g: g.tensor_tensor_scan(out=gc, data0=rmask[:], data1=gc, initial=0.0, op0=ALU.mult, op1=ALU.add),
                 reads=[R_rmask, R_gc], writes=[R_gc])
            P.op("dve", lambda g: g.tensor_scalar(gc, gc, nA[:, h:h + 1], None, ALU.mult), reads=[R_gc, R_nA], writes=[R_gc])
            gcl = cols[:, 0, :]
            cd = cols[:, 1, :]
            P.op("dve", lambda g: g.tensor_copy(gcl, gc[:, 127:S:128]), reads=[R_gc], writes=[R_cols])
            P.op("act", lambda g: g.activation(out=cd, in_=gcl, func=AF.Exp), reads=[R_cols], writes=[R_cols])
            P.op("act", lambda g: g.activation(out=eg, in_=gc, func=AF.Exp), reads=[R_gc], writes=[R_eg])
            P.op("dve", lambda g: g.tensor_tensor(q_eT, qnT, eg, ALU.mult), reads=[R_qn, R_eg], writes=[R_qe])
            P.op("dve", lambda g: g.tensor_tensor(kbT, knT, beta, ALU.mult), reads=[R_kn, R_beta], writes=[R_kb])
            P.op("pool", lambda g: g.tensor_tensor(eg, eg, beta, ALU.mult), reads=[R_eg, R_beta], writes=[R_eg])
            for n in range(NT):
                ch = slice(n * 128, (n + 1) * 128)
                P.op("act", lambda g: g.activation(out=tl[:, ch], in_=gc[:, ch], func=AF.Exp, scale=-1.0, bias=gcl[:, n:n + 1]),
                     reads=[R_gc, R_cols], writes=[R_tl])
            if stop == "g2":
                P.barrier(); return nc
            for qi, (src, R_src) in enumerate(((gc, R_gc), (beta, R_beta), (eg, R_eg), (tl, R_tl))):
                oh = cm_b[:, CI, 0:2] if src is beta else cm_f[:, CI, 0:2]
                for n in range(NT):
                    c0 = (qi * NT + n) * 2
                    mm(pss[3][:, c0:c0 + 2], src[:, n * 128:(n + 1) * 128], oh, True, True, [R_src, R_cmf, R_cmb], R_ps[3])
            P.op("dve", lambda g: g.tensor_copy(cols[:, 2:6, :], pss[3][:, 0:128:2].rearrange("p (a b) -> p a b", b=NT)),
                 reads=[R_ps[3]], writes=[R_cols])
            gc_col, beta_col, bexp_col, tail_col = (cols[:, i, :] for i in (2, 3, 4, 5))
            if stop == "g2b":
                P.barrier(); return nc
            for t in range(NT):
                bk = t % 2
                pT = pss[bk][:].bitcast(BF16)
                ts = slice(t * 128, (t + 1) * 128)
                P.op("pe", lambda g: g.transpose(pT[:, 0:128], knT[:, ts], ident_b), reads=[R_kn, R_cmb], writes=[R_ps[bk]])
                P.op("pe", lambda g: g.transpose(pT[:, 128:256], cvT[:, ts], ident_b), reads=[R_cv, R_cmb], writes=[R_ps[bk]], pe_accum=True)
                P.op("dve", lambda g: g.tensor_scalar(kbg_tok[:, t, :], pT[:, 0:128], bexp_col[:, t:t + 1], None, ALU.mult),
                     reads=[R_ps[bk], R_cols], writes=[R_kbg])
                P.op("dve", lambda g: g.tensor_scalar(kt_tok[:, t, :], pT[:, 0:128], tail_col[:, t:t + 1], None, ALU.mult),
                     reads=[R_ps[bk], R_cols], writes=[R_kt])
                P.op("dve", lambda g: g.tensor_scalar(vb_tok[:, t, :], pT[:, 128:256], beta_col[:, t:t + 1], None, ALU.mult),
                     reads=[R_ps[bk], R_cols], writes=[R_vb])

            if stop == "g3":
                P.barrier(); return nc
            P.barrier(reset=False)
            sz, R_sz = tl.bitcast(BF16)[:, 0:S].rearrange("p (a b) -> p a b", a=1), Reg("sz")
            mb2 = [(xbf[:, i * 128:(i + 1) * 128], Reg("mb2_%d" % i)) for i in range(16)]

            def c_z(tb, ps, R):
                P.op("act", lambda g: g.activation(out=sz[:, 0, tb * 512:(tb + 1) * 512], in_=ps, func=AF.Silu), reads=[R], writes=[R_sz])
            proj_fm(l, 12 + h, 128, wfm, c_z, 1)
            if stop == "g4":
                P.barrier(); return nc
            P.barrier(reset=False)
            G = 4
            eg4 = eg.rearrange("p (t g c) -> p t g c", g=G, c=128)
            (dA4, R_dA), (dB4, R_dB), (dD4, R_dD), (u4, R_u4) = [(eg4[:, i], Reg("f4_%d" % i)) for i in range(4)]
            pool16 = []
            for src in (beta, cvT, wfm[0][0].rearrange("p a b -> p (a b)"), wfm[1][0].rearrange("p a b -> p (a b)"),
                        tl.bitcast(BF16)[:, S:2 * S]):
                v4 = src.rearrange("p (t g c) -> p t g c", g=G, c=128)
                pool16 += [(v4[:, i], Reg("b4_%d" % len(pool16))) for i in range(4)]
            ((Pm4, R_P), (PT4, R_PT), (qk4, R_qk), (Pd4, R_Pd), (PTd4, R_PTd), (Po32, R_Po32), (PTo32, R_PTo32), (Po64, R_Po64),
             Abuf0, ATbuf0, Abuf1, ATbuf1, sq0, sqT0, sq1, sqT1, (U1, R_U1), (T1, R_T1), (wT4, R_wT)) = pool16[0:19]
            vnew, R_vn = mb[0]

            def bc4(blk, f32=True):
                src = cm_f if f32 else cm_b
                return src[:, blk:blk + 1, :].to_broadcast([128, G, 128])

            def flat(t4):
                return t4.rearrange("p g c -> p (g c)")

            def p4(bank):
                return pss[bank][:].rearrange("p (g c) -> p g c", c=128)

            P.op("dve", lambda g: g.memset(Sp, 0.0), writes=[R_Sp])
            P.op("dve", lambda g: g.memset(Sbf, 0.0), writes=[R_Sbf])
            for bb in range(NT // G):
                ns = [bb * G + g_ for g_ in range(G)]
                chs = [slice(n * 128, (n + 1) * 128) for n in ns]
                for g_, n in enumerate(ns):
                    gcc = gc_col[:, n:n + 1]
                    P.op("dve", lambda g: g.tensor_scalar(dA4[:, g_, :], gc[:, chs[g_]], gcc, 0.0, ALU.subtract, ALU.max),
                         reads=[R_gc, R_cols], writes=[R_dA])
                    P.op("dve", lambda g: g.tensor_scalar(dB4[:, g_, :], gc[:, chs[g_]], gcc, 0.0, ALU.subtract, ALU.min),
                         reads=[R_gc, R_cols], writes=[R_dB])
                P.op("act", lambda g: g.activation(out=flat(dA4), in_=flat(dA4), func=AF.Exp, scale=-1.0), reads=[R_dA], writes=[R_dA])
                P.op("act", lambda g: g.activation(out=flat(dB4), in_=flat(dB4), func=AF.Exp), reads=[R_dB], writes=[R_dB])
                P.op("dve", lambda g: g.tensor_tensor(dA4, dA4, bc4(CNSL), ALU.mult), reads=[R_dA, R_cmf], writes=[R_dA])
                P.op("dve", lambda g: g.tensor_tensor(dD4, dB4, bc4(CNSU), ALU.mult), reads=[R_dB, R_cmf], writes=[R_dD])
                P.op("dve", lambda g: g.tensor_tensor(dB4, dB4, bc4(CTU), ALU.mult), reads=[R_dB, R_cmf], writes=[R_dB])
                for g_ in range(G):
                    cs_ = slice(g_ * 128, (g_ + 1) * 128)
                    mm(pss[0][:, cs_], kbT[:, chs[g_]], knT[:, chs[g_]], True, True, [R_kb, R_kn], R_ps[0])
                for g_ in range(G):
                    cs_ = slice(g_ * 128, (g_ + 1) * 128)
                    mm(pss[1][:, cs_], knT[:, chs[g_]], kbT[:, chs[g_]], True, True, [R_kb, R_kn], R_ps[1])
                for g_ in range(G):
                    cs_ = slice(g_ * 128, (g_ + 1) * 128)
                    mm(pss[2][:, cs_], knT[:, chs[g_]], qnT[:, chs[g_]], True, True, [R_kn, R_qn], R_ps[2])
                P.op("dve", lambda g: g.tensor_tensor(Pm4, p4(0), dA4, ALU.mult), reads=[R_ps[0], R_dA], writes=[R_P])
                P.op("dve", lambda g: g.tensor_tensor(PT4, p4(1), dD4, ALU.mult), reads=[R_ps[1], R_dD], writes=[R_PT])
                P.op("dve", lambda g: g.tensor_tensor(qk4, p4(2), dB4, ALU.mult), reads=[R_ps[2], R_dB], writes=[R_qk])
                for dst, R_d, src, R_s, mk in ((Pd4, R_Pd, Pm4, R_P, CB32), (PTd4, R_PTd, PT4, R_PT, CB32), (Po32, R_Po32, Pm4, R_P, CO32),
                                               (PTo32, R_PTo32, PT4, R_PT, CO32), (Po64, R_Po64, Pm4, R_P, CO64)):
                    P.op("dve", lambda g: g.tensor_tensor(dst, src, bc4(mk, False), ALU.mult), reads=[R_s, R_cmb], writes=[R_d])
                Acur, ATcur, Anxt, ATnxt = Abuf0, ATbuf0, Abuf1, ATbuf1
                P.op("dve", lambda g: g.tensor_tensor(Acur[0], Pd4, bc4(CI, False), ALU.add), reads=[R_Pd, R_cmb], writes=[Acur[1]])
                P.op("dve", lambda g: g.tensor_tensor(ATcur[0], PTd4, bc4(CI, False), ALU.add), reads=[R_PTd, R_cmb], writes=[ATcur[1]])

                def mm4(bank, lhs4, R_l, rhs4, R_r, start=True):
                    for g_ in range(G):
                        cs_ = slice(g_ * 128, (g_ + 1) * 128)
                        mm(pss[bank][:, cs_], lhs4[:, g_, :], rhs4[:, g_, :], start, start or g_ == G - 1, [R_l, R_r], R_ps[bank])

                def add_mm4(bank, base, lhs4, R_l, rhs4, R_r):
                    mm(pss[bank][:], ident_b, flat(base[0]), True, False, [R_cmb, base[1]], R_ps[bank])
                    mm4(bank, lhs4, R_l, rhs4, R_r, start=False)

                cur = ((Pd4, R_Pd), (PTd4, R_PTd))
                sqb = [(sq0, sqT0), (sq1, sqT1)]
                for lev in range(4):
                    (cP, R_cP), (cPT, R_cPT) = cur
                    (nP, R_nP), (nPT, R_nPT) = sqb[lev % 2]
                    mm4(3, cPT, R_cPT, cP, R_cP)
                    mm4(4, cP, R_cP, cPT, R_cPT)
                    copy_on("act", flat(nP), pss[3][:], [R_ps[3]], [R_nP])
                    copy_on("dve", flat(nPT), pss[4][:], [R_ps[4]], [R_nPT])
                    add_mm4(5, Acur, nPT, R_nPT, Acur[0], Acur[1])
                    add_mm4(6, ATcur, nP, R_nP, ATcur[0], ATcur[1])
                    copy_on("dve", flat(Anxt[0]), pss[5][:], [R_ps[5]], [Anxt[1]])
                    copy_on("act", flat(ATnxt[0]), pss[6][:], [R_ps[6]], [ATnxt[1]])
                    Acur, Anxt = Anxt, Acur
                    ATcur, ATnxt = ATnxt, ATcur
                    cur = ((nP, R_nP), (nPT, R_nPT))
                mm4(3, PTo32, R_PTo32, Acur[0], Acur[1])
                mm4(4, Po32, R_Po32, ATcur[0], ATcur[1])
                copy_on("act", flat(U1), pss[3][:], [R_ps[3]], [R_U1])
                copy_on("dve", flat(T1), pss[4][:], [R_ps[4]], [R_T1])
                add_mm4(5, Acur, ATcur[0], ATcur[1], U1, R_U1)
                add_mm4(6, ATcur, Acur[0], Acur[1], T1, R_T1)
                copy_on("dve", flat(Anxt[0]), pss[5][:], [R_ps[5]], [Anxt[1]])
                copy_on("act", flat(ATnxt[0]), pss[6][:], [R_ps[6]], [ATnxt[1]])
                Acur, Anxt = Anxt, Acur
                ATcur, ATnxt = ATnxt, ATcur
                mm4(4, Po64, R_Po64, ATcur[0], ATcur[1])
                copy_on("dve", flat(T1), pss[4][:], [R_ps[4]], [R_T1])
                add_mm4(6, ATcur, Acur[0], Acur[1], T1, R_T1)
                copy_on("act", flat(ATnxt[0]), pss[6][:], [R_ps[6]], [ATnxt[1]])
                AT4, R_AT = ATnxt
                for g_, n in enumerate(ns):
                    cs_ = slice(g_ * 128, (g_ + 1) * 128)
                    mm(pss[3][:, cs_], AT4[:, g_, :], vb_tok[:, n, :], True, True, [R_AT, R_vb], R_ps[3])
                for g_, n in enumerate(ns):
                    cs_ = slice(g_ * 128, (g_ + 1) * 128)
                    mm(pss[4][:, cs_], kbg_tok[:, n, :], AT4[:, g_, :], True, True, [R_AT, R_kbg], R_ps[4])
                copy_on("act", flat(u4), pss[3][:], [R_ps[3]], [R_u4])
                copy_on("dve", flat(wT4), pss[4][:], [R_ps[4]], [R_wT])
                for g_, n in enumerate(ns):
                    ch = chs[g_]
                    if n > 0:
                        mm(pss[0][:, 0:128], wT4[:, g_, :], Sbf, True, True, [R_wT, R_Sbf], R_ps[0])
                        P.op("dve", lambda g: g.tensor_tensor(vnew, u4[:, g_, :], pss[0][:, 0:128], ALU.subtract),
                             reads=[R_u4, R_ps[0]], writes=[R_vn])
                    else:
                        copy_on("dve", vnew, u4[:, g_, :], [R_u4], [R_vn])
                    if n > 0:
                        mm(pss[7][:, 0:128], Sbf, q_eT[:, ch], True, False, [R_Sbf, R_qe], R_ps[7])
                    mm(pss[7][:, 0:128], vnew, qk4[:, g_, :], n == 0, True, [R_vn, R_qk], R_ps[7])
                    copy_on("act", o_acc[:, 0, ch], pss[7][:, 0:128], [R_ps[7]], [R_oacc])
                    if n < NT - 1:
                        mm(pss[1][:, 0:128], kt_tok[:, n, :], vnew, True, True, [R_kt, R_vn], R_ps[1])
                        P.op("dve", lambda g: g.scalar_tensor_tensor(out=Sp, in0=Sp, scalar=cd[:, n:n + 1], in1=pss[1][:, 0:128],
                                                                     op0=ALU.mult, op1=ALU.add), reads=[R_Sp, R_cols, R_ps[1]], writes=[R_Sp])
                        copy_on("act", Sbf, Sp, [R_Sp], [R_Sbf])
            post_norm_store(l, o_acc, R_oacc, 1, [sc(51)], sz, R_sz, 2 + h, tmps, 128.0, 6)
            prefetch(l, 11 if h == 0 else 14)

        P.collective("AllGather", PAIRS, oTl_t[0].ap().opt(), oTg_t[0].ap().opt(), reads=[R_oTl[0]], writes=[R_oTg[0]])
        if stop == "pC2":
            P.barrier(); return nc
        for h in range(2):
            new_phase("diff")
            wfm = [ar.alloc([128, 16, 128], BF16, "wfm%d" % i) for i in range(3)]
            wtm = ar.alloc([128, 16, 256], BF16, "wtm")
            qT, R_q = ar.alloc([128, 2, S], BF16, "qT")
            kT, R_k = ar.alloc([128, 2, S], BF16, "kT")
            v_sb, R_v = ar.alloc([128, NT, 256], BF16, "v_sb")
            sz, R_sz = ar.alloc([128, 2, S], BF16, "sz")
            ebuf = [ar.alloc([128, 512], BF16, "e%d" % i) for i in range(3)]
            qn, R_qnb = ar.alloc([128, 512], BF16, "qn")
            ta, R_ta = ar.alloc([128, 512], F32, "ta")
            tb_, R_tb = ar.alloc([128, 512], F32, "tb")
            tO, R_tO = ar.alloc([128, 2, 512], F32, "tO")
            rs, R_rs = ar.alloc([128, 512], F32, "rs")
            sq2, R_sq2 = ar.alloc([128, 1024], BF16, "sq2")
            t1b, R_t1b = ar.alloc([128, 1024], F32, "t1b")
            rinvb, R_rinvb = ar.alloc([128, 1024], F32, "rinvb")
            qnb, R_qnb2 = ar.alloc([128, 1024], BF16, "qnb")
            tab, R_tab = ar.alloc([128, 1024], F32, "tab")
            tbb, R_tbb = ar.alloc([128, 1024], F32, "tbb")
            lamt, R_lam = ar.alloc([128, 8], F32, "lam")
            lprod, R_lprod = ar.alloc([128, 256], F32, "lprod")
            wn2, R_wn2 = ar.alloc([128, 2], F32, "wn2")
            tmps = [ar.alloc([128, 2, 512], BF16, "sq"), ar.alloc([128, 512], F32, "t1"), ar.alloc([128, 512], F32, "rinv"),
                    ar.alloc([128, 512], F32, "u"), ar.alloc([128, 512], BF16, "ost"), ar.alloc([128, 512], BF16, "ost2")]
            P.op("dve", lambda g: g.tensor_tensor(lprod[:, 0:128], sc(112, 128), sc(240, 128), ALU.mult), reads=[R_small], writes=[R_lprod])
            P.op("dve", lambda g: g.tensor_tensor(lprod[:, 128:256], sc(368, 128), sc(496, 128), ALU.mult), reads=[R_small], writes=[R_lprod])
            P.op("dve", lambda g: g.tensor_reduce(lamt[:, 0:2], lprod.rearrange("p (a b) -> p a b", b=128), AX.X, ALU.add),
                 reads=[R_lprod], writes=[R_lam])
            P.op("act", lambda g: g.activation(out=lamt[:, 2:4], in_=lamt[:, 0:2], func=AF.Exp), reads=[R_lam], writes=[R_lam])
            P.op("dve", lambda g: g.scalar_tensor_tensor(out=lamt[:, 4:5], in0=lamt[:, 3:4], scalar=-lam_init, in1=lamt[:, 2:3],
                                                         op0=ALU.add, op1=ALU.subtract), reads=[R_lam], writes=[R_lam])
            nlam = lamt[:, 4:5]
            P.op("dve", lambda g: g.tensor_scalar(wn2, sc(54, 2), 1.0 - lam_init, None, ALU.mult), reads=[R_small], writes=[R_wn2])
            units = [(which, m, half) for which in range(2) for m in range(2) for half in range(2)]
            qk_meta = ((qT, R_q, 14, 52), (kT, R_k, 18, 53))
            ustate = {}

            def qk_issue(ui):
                which, m, half = units[ui]
                g0 = qk_meta[which][2]
                wb, R_wb = qk_w[which * 2 + m]
                banks = [4 + 2 * (ui % 2), 5 + 2 * (ui % 2)]
                for kc in range(16):
                    for i in range(2):
                        tb = half * 2 + i
                        mm(pss[banks[i]][:], wb[:, kc, :], big[:, kc, tb * 512:(tb + 1) * 512], kc == 0, kc == 15,
                           [R_wb, R_big], R_ps[banks[i]])

            def qk_consume(ui):
                which, m, half = units[ui]
                dstT, R_dst, g0, wcol = qk_meta[which]
                b0_ = 4 + 2 * (ui % 2)
                ps2 = psall[:, b0_ * 512:(b0_ + 2) * 512]
                Rb = [R_ps[b0_], R_ps[b0_ + 1]]
                sl2 = slice(half * 1024, (half + 1) * 1024)
                P.op("act", lambda g: g.activation(out=sq2, in_=ps2, func=AF.Square), reads=Rb, writes=[R_sq2])
                for i in range(2):
                    mm(pss[2 + i][:], ones_b, sq2[:, i * 512:(i + 1) * 512], True, True, [R_sq2, R_cmb], R_ps[2 + i])
                P.op("act", lambda g: g.activation(out=t1b, in_=psall[:, 2 * 512:4 * 512], func=AF.Ln, scale=1.0 / 128.0, bias=eps_ap),
                     reads=[R_ps[2], R_ps[3], R_misc], writes=[R_t1b])
                P.op("act", lambda g: g.activation(out=rinvb, in_=t1b, func=AF.Exp, scale=-0.5), reads=[R_t1b], writes=[R_rinvb])
                P.op("dve", lambda g: g.scalar_tensor_tensor(out=qnb, in0=ps2, scalar=sc(wcol), in1=rinvb, op0=ALU.mult, op1=ALU.mult),
                     reads=Rb + [R_rinvb, R_small], writes=[R_qnb2])
                for i in range(2):
                    mm(pss[i][:], cm_b[:, CP, :], qnb[:, i * 512:(i + 1) * 512], True, True, [R_cmb, R_qnb2], R_ps[i])
                P.op("pool", lambda g: g.tensor_tensor(tab, qnb, cosT[:, sl2], ALU.mult), reads=[R_qnb2, R_cos], writes=[R_tab])
                P.op("dve", lambda g: g.tensor_tensor(tbb, psall[:, 0:1024], sinT[:, sl2], ALU.mult), reads=[R_ps[0], R_ps[1], R_sin], writes=[R_tbb])
                P.op("pool", lambda g: g.tensor_tensor(dstT[:, m, sl2], tab, tbb, ALU.add), reads=[R_tab, R_tbb], writes=[R_dst])
            qk_w = []
            for which_ in range(2):
                for m_ in range(2):
                    gi_ = qk_meta[which_][2] + 2 * h + m_
                    if state.get("pre") == (l, gi_):
                        qk_w.append((wpre, R_wpre))
                        state["pre"] = None
                        continue
                    wb_, R_wb_ = wfm[sum(1 for w_ in qk_w if w_[0] is not wpre)]
                    P.dma("pool", wb_.rearrange("p a b -> p (a b)"), winfm_d[l, gi_], writes=[R_wb_])
                    qk_w.append((wb_, R_wb_))
            qk_issue(0)
            for ui in range(len(units)):
                if ui + 1 < len(units):
                    qk_issue(ui + 1)
                qk_consume(ui)

            def c_v(t, ps, R):
                copy_on(alt(), v_sb[:, t, :], ps, [R], [R_v])
            proj_tm(l, 1 + h, wtm, c_v, [0, 1])
            for j in range(2):
                def c_z(tb, ps, R):
                    P.op("act", lambda g: g.activation(out=sz[:, j, tb * 512:(tb + 1) * 512], in_=ps, func=AF.Silu), reads=[R], writes=[R_sz])
                proj_fm(l, 22 + 2 * h + j, 128, wfm, c_z, j)
            if h == 1:
                for fc in range(16):
                    P.dma("pool", big[:, fc, :], wout_d[l][:, fc * D:(fc + 1) * D], writes=[R_big], nowaw=fc > 0)
            scale = 128.0 ** -0.5
            steps = [(qb, m, kc) for qb in range(4) for m in range(2) for kc in range(4 * (qb + 1))]

            def qk_mm(si):
                qb, m, kc = steps[si]
                col0 = max(kc - 4 * qb, 0) * 128
                bsc = si % 2
                mm(pss[bsc][:, col0:512], kT[:, m, kc * 128:(kc + 1) * 128], qT[:, m, qb * 512 + col0:(qb + 1) * 512], True, True,
                   [R_k, R_q], R_ps[bsc])
            qk_mm(0)
            for si, (qb, m, kc) in enumerate(steps):
                qs = slice(qb * 512, (qb + 1) * 512)
                nk = 4 * (qb + 1)
                bo0, bo1, bs = (2, 3, 4) if m == 0 else (5, 6, 7)
                if si + 1 < len(steps):
                    qk_mm(si + 1)
                bsc = si % 2
                c = kc - 4 * qb
                col0 = max(c, 0) * 128
                e, R_e = ebuf[si % 3]
                P.op("act", lambda g: g.activation(out=e[:, col0:512], in_=pss[bsc][:, col0:512], func=AF.Exp, scale=scale),
                     reads=[R_ps[bsc]], writes=[R_e])
                if c >= 0:
                    P.op("pool", lambda g: g.tensor_tensor(e[:, col0:col0 + 128], e[:, col0:col0 + 128], cm_b[:, CTU, :], ALU.mult),
                         reads=[R_e, R_cmb], writes=[R_e])
                first, last = kc == 0, kc == nk - 1
                mm(pss[bo0][:, col0:512], v_sb[:, kc, 0:128], e[:, col0:512], first, last, [R_v, R_e], R_ps[bo0])
                mm(pss[bo1][:, col0:512], v_sb[:, kc, 128:256], e[:, col0:512], first, last, [R_v, R_e], R_ps[bo1])
                mm(pss[bs][:, col0:512], ones_b, e[:, col0:512], first, last, [R_cmb, R_e], R_ps[bs])
                if not last:
                    continue
                P.op("dve", lambda g: g.reciprocal(rs, pss[bs][:]), reads=[R_ps[bs]], writes=[R_rs])
                for j, bo in enumerate((bo0, bo1)):
                    if m == 0:
                        P.op("dve", lambda g: g.tensor_tensor(tO[:, j, :], pss[bo][:], rs, ALU.mult), reads=[R_ps[bo], R_rs], writes=[R_tO])
                    else:
                        P.op("dve", lambda g: g.scalar_tensor_tensor(out=ta, in0=pss[bo][:], scalar=nlam, in1=rs, op0=ALU.mult, op1=ALU.mult),
                             reads=[R_ps[bo], R_rs, R_lam], writes=[R_ta])
                        P.op("pool", lambda g: g.tensor_tensor(tO[:, j, :], tO[:, j, :], ta, ALU.add), reads=[R_tO, R_ta], writes=[R_tO])
                if m == 0:
                    continue
                (sq, R_sq), (t1, R_t1), (rinv, R_rinv), (u, R_u) = tmps[0:4]
                rstd_part([(tO[:, j, :], R_tO) for j in range(2)], 512, 256.0, tmps[0:3], 4)
                for j in range(2):
                    ost, R_ost = tmps[4 + j]
                    P.op("dve", lambda g: g.scalar_tensor_tensor(out=u, in0=tO[:, j, :], scalar=wn2[:, j:j + 1], in1=rinv, op0=ALU.mult, op1=ALU.mult),
                         reads=[R_tO, R_rinv, R_wn2], writes=[R_u])
                    P.op("pool", lambda g: g.tensor_tensor(ost, u, sz[:, j, qs], ALU.mult), reads=[R_u, R_sz], writes=[R_ost])
                    dst_ap, R_d = oT_dst(4 + 2 * h + j, qs)
                    P.dma("sp", dst_ap, ost, reads=[R_ost], writes=[R_d], nowaw=True)
            if h == 0:
                prefetch(l, 16)
            P.collective("AllGather", PAIRS, oTl_t[1 + h].ap().opt(), oTg_t[1 + h].ap().opt(), reads=[R_oTl[1 + h]], writes=[R_oTg[1 + h]])

        if stop == "pC3":
            P.barrier(); return nc
        new_phase("D")
        wo = big
        xts = [ar.alloc([128, D], F32, "xt%d" % i) for i in range(2)]
        obs = [ar.alloc([128, 16, 512], BF16, "ob%d" % i) for i in range(2)]
        xos = [ar.alloc([128, D], F32, "xo%d" % i) for i in range(2)]
        ytmp, R_yt = ar.alloc([128, 512], F32, "ytmp")
        xsrc = x_d if l == 0 else x1_d
        xdst = out_d if l == L - 1 else x1_d
        R_dst = R_out if l == L - 1 else R_x1
        for tb in range(4):
            ob, R_ob = obs[tb % 2]
            for i2, (c0_, n_) in enumerate(((0, 8), (8, 4), (12, 4))):
                P.dma("sp", ob[:, c0_:c0_ + n_, :], oTg_d[i2][:, tb * 512:(tb + 1) * 512].rearrange("(c p) t -> p c t", p=128),
                      reads=[R_oTg[i2]], writes=[R_ob], nowaw=i2 > 0)
            for i in range(4):
                t = tb * 4 + i
                xt, R_xt = xts[t % 2]
                xo, R_xo = xos[t % 2]
                P.dma("sp", xt, xsrc[t * 128:(t + 1) * 128, :], reads=[R_x1] if l > 0 else [], writes=[R_xt])
                for ng in range(4):
                    ns = slice(ng * 512, (ng + 1) * 512)
                    bk = (t * 4 + ng) % 4
                    for fc in range(16):
                        mm(pss[bk][:], ob[:, fc, i * 128:(i + 1) * 128], wo[:, fc, ns], fc == 0, fc == 15, [R_ob, R_big], R_ps[bk])
                    P.op("dve", lambda g: g.tensor_tensor(ytmp, pss[bk][:], gate_bc[:, ns], ALU.mult), reads=[R_ps[bk], R_gate], writes=[R_yt])
                    P.op("pool", lambda g: g.tensor_tensor(xo[:, ns], ytmp, xt[:, ns], ALU.add), reads=[R_yt, R_xt], writes=[R_xo])
                P.dma("sp", xdst[t * 128:(t + 1) * 128, :], xo, reads=[R_xo], writes=[R_dst], nowaw=True)

    P.barrier(collectives=True)
    if scopes and state.get("scope") is not None:
        nc.leave_named_scope(state["scope"][0], state["scope"][1], False)
    return nc


def _col(v):
    return np.ascontiguousarray(np.asarray(v, np.float32).reshape(-1, 128).T)


def _consts():
    p = np.arange(128)[:, None]
    j = np.arange(128)[None, :]
    cm = np.zeros((128, NCM, 128), np.float32)
    cm[:, CI] = (p == j)
    cm[:, CO] = 1.0
    prot = np.zeros((128, 128), np.float32)
    prot[(j[0, :64] + 64), j[0, :64]] = -1.0
    prot[(j[0, 64:] - 64), j[0, 64:]] = 1.0
    cm[:, CP] = prot
    cm[:, CTU] = (p <= j)
    cm[:, CNSU] = -1.0 * (p < j)
    cm[:, CNSL] = -1.0 * (p > j)
    bd32 = (p // 32 == j // 32)
    bd64 = (p // 64 == j // 64)
    cm[:, CB32] = bd32
    cm[:, CO32] = bd64 & ~bd32
    cm[:, CO64] = ~bd64
    selm = np.zeros((8, 8, 128), np.float32)
    for k in range(8):
        selm[k, k, :] = 1.0
    rmask = np.ones((128, S), np.float32)
    rmask[:, 0::128] = 0.0
    half = 64
    inv_freq = (10000.0 ** (-(np.arange(half, dtype=np.float32) / np.float32(half)))).astype(np.float32)
    invf = np.concatenate([inv_freq, inv_freq]).astype(np.float64) / (2.0 * math.pi)
    return cm.reshape(128, NCM * 128), selm.reshape(8, 8 * 128), rmask, invf.astype(np.float32)


def _fm_groups_local(hh):
    g = [(hh * 128, 128), (256 + hh * 128, 128), (1024, 16)]
    for base in (1040, 1552, 2064, 2576):
        g += [(base + 128 * (2 * hh + i), 128) for i in range(2)]
    g.append("dab")
    g += [(3096 + 128 * (2 * hh + i), 128) for i in range(2)]
    for base in (3608, 4632, 6680):
        g += [(base + 256 * (2 * hh + lh) + 128 * m, 128) for lh in range(2) for m in range(2)]
    return g


def _tm_groups_local(hh):
    return [(512 + hh * 256, 256), (5656 + 256 * (2 * hh), 256), (5656 + 256 * (2 * hh + 1), 256)]


OUT_CHUNK_ORDER = [0, 1, 4, 5, 2, 3, 6, 7, 8, 9, 12, 13, 10, 11, 14, 15]


def _prep_shared(inp):
    f = lambda k: np.asarray(inp[k], np.float32)
    cm, selm, rmask, invf = _consts()
    w_in = f("w_in")
    per_half = []
    for hh in range(2):
        winfm = np.zeros((2, 26, 128, 16, 128), np.float32)
        wintm = np.zeros((2, 3, 128, 16, 256), np.float32)
        for l in range(2):
            wl = w_in[l].reshape(16, 128, -1)
            for gi, grp in enumerate(_fm_groups_local(hh)):
                if grp == "dab":
                    for i in range(2):
                        winfm[l, gi, :, :, i] = wl[:, :, 3088 + 2 * hh + i].T
                        winfm[l, gi, :, :, 4 + i] = wl[:, :, 3092 + 2 * hh + i].T
                else:
                    c0, n = grp
                    winfm[l, gi, :, :, :n] = wl[:, :, c0:c0 + n].transpose(1, 0, 2)
            for gi, (c0, n) in enumerate(_tm_groups_local(hh)):
                wintm[l, gi] = wl[:, :, c0:c0 + n].transpose(1, 0, 2)
        sm = np.zeros((128, NS), np.float32)
        sm[:, 16] = invf
        for l in range(2):
            b = SBASE + l * SLW
            sm[:, b:b + 16] = _col(f("norm_w")[l])
            sm[:, b + 16:b + 48] = _col(f("b_ada")[l, :2 * D])
            sm[:, b + 48] = f("gla_b_lr")[l, hh * 128:(hh + 1) * 128]
            sm[:, b + 50] = f("gla_norm_w")[l]
            sm[:, b + 51] = f("gdn_norm_w")[l]
            sm[:, b + 52] = f("diff_q_norm_w")[l]
            sm[:, b + 53] = f("diff_k_norm_w")[l]
            sm[:, b + 54:b + 56] = _col(f("diff_norm_w")[l])
            sm[:, b + 56:b + 58] = f("gdn_a_log")[l][None, 2 * hh:2 * hh + 2]
            sm[:, b + 60:b + 62] = f("gdn_dt_bias")[l][None, 2 * hh:2 * hh + 2]
            cw = f("gdn_conv_w")[l].reshape(4, 12, 128)
            for which in range(3):
                for lh in range(2):
                    ti = which * 4 + lh
                    sm[:, b + 64 + ti * 4:b + 64 + ti * 4 + 4] = cw[:, which * 4 + 2 * hh + lh, :].T
            sm[:, b + 112:b + 624] = f("diff_lambda")[l].reshape(1, 512)
            sm[0:16, b + 624:b + 752] = f("gla_w_lr")[l][:, hh * 128:(hh + 1) * 128]
        per_half.append({"winfm": winfm.reshape(2, 26, 128, 16 * 128), "wintm": wintm.reshape(2, 3, 128, 16 * 256), "small": sm})
    wada = np.ascontiguousarray(f("w_ada").reshape(2, 16, 128, 48, 128).transpose(0, 3, 2, 1, 4)).reshape(2, 48, 128, 16 * 128)
    wo = f("w_out").reshape(2, 16, 128, D)[:, OUT_CHUNK_ORDER]
    wout = np.ascontiguousarray(wo.transpose(0, 2, 1, 3)).reshape(2, 128, 16 * D)
    bgate = np.ascontiguousarray(f("b_ada")[:, 2 * D:].reshape(2, 1, D))
    shared = {"cmat": cm, "selm": selm, "rmask": rmask, "wada": wada, "bgate": bgate, "wout": wout}
    return shared, per_half


def make_in_maps(inp, cores):
    shared, per_half = _prep_shared(inp)
    x = np.asarray(inp["x"], np.float32)
    c = np.asarray(inp["c"], np.float32)
    pos = np.asarray(inp["positions"], np.int32)
    maps = []
    for b, hh in cores:
        s = per_half[hh]["small"].copy()
        s[:, 0:16] = _col(c[b])
        m = dict(shared)
        m["winfm"] = per_half[hh]["winfm"]
        m["wintm"] = per_half[hh]["wintm"]
        m["x"] = np.ascontiguousarray(x[b])
        m["small"] = s
        m["pos"] = np.ascontiguousarray(pos[b:b + 1])
        maps.append(m)
    return maps


_NC_CACHE = {}


def kernel(**inputs):
    if "nc" not in _NC_CACHE:
        _NC_CACHE["nc"] = build(2)
    nc = _NC_CACHE["nc"]
    cores = [(i // 2, i % 2) for i in range(8)]
    maps = make_in_maps(inputs, cores)
    res = run_bass_kernel_spmd(nc, maps, core_ids=list(range(8)))
    out = np.stack([np.asarray(res.results[2 * b]["out"], np.float32) for b in range(4)], axis=0)
    return out
```

```python
import math
import numpy as np
import concourse.bass as bass
import concourse.mybir as mybir
from concourse.bass_utils import run_bass_kernel_spmd

F32 = mybir.dt.float32
BF16 = mybir.dt.bfloat16
I32 = mybir.dt.int32
AF = mybir.ActivationFunctionType
ALU = mybir.AluOpType
AX = mybir.AxisListType

S = 2048
D = 2048
NT = 16
EPS = 1e-6
SLW = 880
SBASE = 32
NS = SBASE + 2 * SLW
CI, CO, CP, CTU, CNSU, CNSL, CB32, CO32, CO64 = range(9)
NCM = 9


class Reg:
    __slots__ = ("name", "lw", "rd", "dsem", "local", "psum")

    def __init__(self, name, local=False, psum=False):
        self.name = name
        self.psum = psum
        self.lw = None
        self.rd = {}
        self.dsem = None
        self.local = local


class Prog:
    def __init__(self, nc):
        self.nc = nc
        self.eng = {"pe": nc.tensor, "act": nc.scalar, "dve": nc.vector, "pool": nc.gpsimd, "sp": nc.sync}
        self.sem, self.cnt, self.semobj = {}, {}, {}
        self.seen = {e: {} for e in self.eng}
        for e in self.eng:
            s = nc.alloc_semaphore(name="s_" + e)
            self.sem[e] = s
            self.semobj[e] = s
            self.cnt[e] = 0
        self.vc = {}
        self.dcnt = {}
        self.local_keys = []
        self.local_next = 0
        self.ninstr = 0
        self.nwait = 0

    def _wait(self, e, key, val):
        if self.seen[e].get(key, 0) >= val:
            return
        self.eng[e].wait_ge(self.semobj[key], val)
        self.nwait += 1
        se = self.seen[e]
        se[key] = val
        clk = self.vc.get((key, val))
        if clk:
            for k, v in clk.items():
                if se.get(k, 0) < v:
                    se[k] = v

    def _deps(self, e, reads, writes, pe_accum=False, nowaw=False):
        need = {}

        def add(k, v):
            if need.get(k, 0) < v:
                need[k] = v
        for r in reads:
            if r.lw is not None:
                add(*r.lw)
            if r.psum:
                for k, v in r.rd.items():
                    if k != e:
                        add(k, v)
        for w in writes:
            if w.lw is not None and not nowaw and not (pe_accum and w.lw[0] == "pe"):
                add(*w.lw)
            for k, v in w.rd.items():
                add(k, v)
        for k, v in sorted(need.items(), key=lambda kv: -kv[1] if isinstance(kv[0], str) else 0):
            self._wait(e, k, v)

    def _record(self, ev, reads, writes, nowaw=False):
        for r in reads:
            if r.rd.get(ev[0], 0) < ev[1]:
                r.rd[ev[0]] = ev[1]
        for w in writes:
            w.lw = ev
            if not nowaw:
                w.rd = {}

    def op(self, e, fn, reads=(), writes=(), pe_accum=False):
        self._deps(e, reads, writes, pe_accum)
        ins = fn(self.eng[e])
        self.cnt[e] += 1
        ins.then_inc(self.sem[e], 1)
        ev = (e, self.cnt[e])
        self.vc[ev] = dict(self.seen[e])
        self._record(ev, reads, writes)
        self.ninstr += 1

    def _newkey(self):
        key = ("d", len(self.dcnt))
        self.semobj[key] = self.nc.alloc_semaphore(name="d_%d" % len(self.dcnt))
        self.dcnt[key] = 0
        return key

    def _dkey(self, w):
        if w.dsem is None:
            if w.local:
                if self.local_next == len(self.local_keys):
                    self.local_keys.append(self._newkey())
                w.dsem = self.local_keys[self.local_next]
                self.local_next += 1
            else:
                w.dsem = self._newkey()
        return w.dsem

    def dma(self, q, out_ap, in_ap, reads=(), writes=(), nowaw=False, **kw):
        w = writes[0]
        self._deps(q, reads, writes, nowaw=nowaw)
        key = self._dkey(w)
        ins = self.eng[q].dma_start(out=out_ap, in_=in_ap, **kw)
        self.dcnt[key] += 16
        ins.then_inc(self.semobj[key], 16)
        ev = (key, self.dcnt[key])
        self.vc[ev] = dict(self.seen[q])
        self._record(ev, reads, writes, nowaw=nowaw)
        self.ninstr += 1

    def collective(self, kind, groups, in_ap, out_ap, reads, writes):
        self._deps("pool", reads, writes)
        key = ("c", len([k for k in self.dcnt if k[0] == "c"]))
        self.semobj[key] = self.nc.alloc_semaphore(name="c_%d" % key[1])
        ins = self.eng["pool"].collective_compute(kind, ALU.bypass, replica_groups=groups, ins=[in_ap], outs=[out_ap])
        ins.then_inc(self.semobj[key])
        self.dcnt[key] = 1
        ev = (key, 1)
        self.vc[ev] = dict(self.seen["pool"])
        self._record(ev, reads, writes)
        self.ninstr += 1

    def barrier(self, reset=True, collectives=False):
        evs = [(e, self.cnt[e]) for e in self.eng if self.cnt[e] > 0]
        evs += [(k, v) for k, v in self.dcnt.items() if v > 0 and (collectives or k[0] != "c")]
        for e in self.eng:
            for k, v in evs:
                if k != e:
                    self._wait(e, k, v)
        if reset:
            self.local_next = 0


class Arena:
    def __init__(self, nc, nwords):
        self.t = nc.alloc_sbuf_tensor("arena", [128, nwords], F32)
        self.n = nwords
        self.off = 0
        self.k = 0

    def reset(self):
        self.off = 0

    def alloc(self, shape, dt, name=None):
        free = int(np.prod(shape[1:]))
        words = free if dt in (F32, I32) else (free + 1) // 2
        words = (words + 7) // 8 * 8
        assert self.off + words <= self.n, ("arena overflow", name, self.off, words, self.n)
        v = self.t[:, self.off:self.off + words]
        self.off += words
        if dt == BF16:
            v = v.bitcast(BF16)[:, 0:free]
        elif dt == I32:
            v = v.bitcast(I32)[:, 0:free]
        else:
            v = v[:, 0:free]
        if len(shape) == 3:
            v = v.rearrange("p (a b) -> p a b", b=shape[2])
        self.k += 1
        if shape[0] < 128:
            v = v[0:shape[0]]
        return v, Reg(name or "a%d" % self.k, local=True)


def build(nlayers=2, debug=False, stop=None, scopes=False, ncores=8):
    nc = bass.Bass("TRN2", target_bir_lowering=False)
    P = Prog(nc)
    L = nlayers

    def din(name, shape, dt=F32):
        return nc.dram_tensor(name, shape, dt, kind="ExternalInput").ap()

    x_d = din("x", [S, D])
    small_d = din("small", [128, NS])
    cmat_d = din("cmat", [128, NCM * 128])
    selm_d = din("selm", [8, 8 * 128])
    rmask_d = din("rmask", [128, S])
    pos_d = din("pos", [1, S], I32)
    wada_d = din("wada", [2, 48, 128, 16 * 128])
    bgate_d = din("bgate", [2, 1, D])
    winfm_d = din("winfm", [2, 26, 128, 16 * 128])
    wintm_d = din("wintm", [2, 3, 128, 16 * 256])
    wout_d = din("wout", [2, 128, 16 * D])
    out_d = nc.dram_tensor("out", [S, D], F32, kind="ExternalOutput").ap()
    x1_d = nc.dram_tensor("x1s", [S, D], F32, kind="Internal").ap()
    OT_ROWS = (4, 2, 2)
    oTl_t = [nc.dram_tensor("oTl%d" % i, [n_ * 128, S], BF16, kind="Internal") for i, n_ in enumerate(OT_ROWS)]
    oTg_t = [nc.dram_tensor("oTg%d" % i, [2 * n_ * 128, S], BF16, kind="Internal") for i, n_ in enumerate(OT_ROWS)]
    oTl_d = [t.ap() for t in oTl_t]
    oTg_d = [t.ap() for t in oTg_t]
    R_oTl = [Reg("oTl%d" % i) for i in range(3)]
    R_oTg = [Reg("oTg%d" % i) for i in range(3)]
    PAIRS = [[2 * i, 2 * i + 1] for i in range(ncores // 2)]
    R_out, R_x1 = Reg("out"), Reg("x1")

    def oT_dst(fc, sl):
        ti, r = (0, fc) if fc < 4 else (1 + (fc - 4) // 2, (fc - 4) % 2)
        return oTl_d[ti][r * 128:(r + 1) * 128, sl], R_oTl[ti]

    def sb(name, shape, dt):
        return nc.alloc_sbuf_tensor("sb_" + name, shape, dt), Reg(name)

    big, R_big = sb("big", [128, 16, S], BF16)
    cosT, R_cos = sb("cosT", [128, S], BF16)
    sinT, R_sin = sb("sinT", [128, S], BF16)
    gate_bc, R_gate = sb("gate_bc", [128, D], F32)
    rmask, R_rmask = sb("rmask", [128, S], BF16)
    cm_f, R_cmf = sb("cm_f", [128, NCM, 128], F32)
    cm_b, R_cmb = sb("cm_b", [128, NCM, 128], BF16)
    selm, R_selm = sb("selm", [8, 8, 128], F32)
    small, R_small = sb("small", [128, NS], F32)
    modc, R_modc = sb("modc", [128, 48], F32)
    misc, R_misc = sb("misc", [128, 64], F32)
    wpre, R_wpre = sb("wpre", [128, 16, 128], BF16)
    ar = Arena(nc, (nc.sbuf_bytes_remaining - 2048) // 4)

    psall = nc.alloc_psum_tensor("psall", [128, 8 * 512], F32)
    pss = [psall[:, i * 512:(i + 1) * 512] for i in range(8)]
    R_ps = [Reg("ps%d" % i, psum=True) for i in range(8)]

    state = {"alt": 0}

    def alt():
        state["alt"] ^= 1
        return "act" if state["alt"] else "dve"

    def copy_on(e, out, in_, reads, writes):
        if e == "act":
            P.op("act", lambda g: g.activation(out=out, in_=in_, func=AF.Copy), reads=reads, writes=writes)
        else:
            P.op(e, lambda g: g.tensor_copy(out, in_), reads=reads, writes=writes)

    def mm(out, lhsT, rhs, start, stop, reads, w):
        P.op("pe", lambda g: g.matmul(out, lhsT=lhsT, rhs=rhs, start=start, stop=stop),
             reads=reads, writes=[w], pe_accum=not start)

    def new_phase(name="ph"):
        P.barrier()
        ar.reset()
        if scopes:
            if state.get("scope") is not None:
                nc.leave_named_scope(state["scope"][0], state["scope"][1], False)
            nm = "%s_%d" % (name, state.setdefault("nscope", 0))
            state["nscope"] += 1
            sid, _ = nc.enter_named_scope(nm, False)
            state["scope"] = (nm, sid)

    ident_b = cm_b[:, CI, :]
    ones_b = cm_b[:, CO, :]
    ident_f = cm_f[:, CI, :]

    def rstd_part(srcs, n, denom, tmps, psi):
        (sq, R_sq), (t1, R_t1), (rinv, R_rinv) = tmps
        for i, (sap, sreg) in enumerate(srcs):
            P.op("act", lambda g: g.activation(out=sq[:, i, 0:n], in_=sap, func=AF.Square), reads=[sreg], writes=[R_sq])
        for i in range(len(srcs)):
            mm(pss[psi][:, 0:n], ones_b, sq[:, i, 0:n], i == 0, i == len(srcs) - 1, [R_sq, R_cmb], R_ps[psi])
        P.op("act", lambda g: g.activation(out=t1[:, 0:n], in_=pss[psi][:, 0:n], func=AF.Ln, scale=1.0 / denom, bias=eps_ap),
             reads=[R_ps[psi], R_misc], writes=[R_t1])
        P.op("act", lambda g: g.activation(out=rinv[:, 0:n], in_=t1[:, 0:n], func=AF.Exp, scale=-0.5), reads=[R_t1], writes=[R_rinv])
        return rinv, R_rinv

    P.dma("sp", small[:], small_d, writes=[R_small])
    P.dma("sp", cm_f[:], cmat_d.rearrange("p (a b) -> p a b", b=128), writes=[R_cmf])
    P.dma("sp", selm[:], selm_d.rearrange("p (a b) -> p a b", b=128), writes=[R_selm])
    P.dma("pool", rmask[:], rmask_d, writes=[R_rmask])
    P.op("dve", lambda g: g.tensor_copy(cm_b[:], cm_f[:]), reads=[R_cmf], writes=[R_cmb])
    eps_ap = misc[:, 0:1]
    P.op("dve", lambda g: g.memset(misc[:], 0.0), writes=[R_misc])
    P.op("dve", lambda g: g.memset(misc[:, 0:1], EPS), reads=[], writes=[R_misc])
    P.op("dve", lambda g: g.memset(misc[:, 1:2], 1.0), reads=[], writes=[R_misc])
    one_ap = misc[:, 1:2]

    posi, R_posi = ar.alloc([128, S], I32, "posi")
    y, R_y = ar.alloc([128, S], F32, "y")
    yi, R_yi = ar.alloc([128, S], I32, "yi")
    yf, R_yf = ar.alloc([128, S], F32, "yf")
    fr, R_fr = ar.alloc([128, S], F32, "fr")
    m1, R_m1 = ar.alloc([128, S], F32, "m1")
    P.dma("sp", posi, pos_d.to_broadcast([128, S]), writes=[R_posi])
    P.op("dve", lambda g: g.tensor_copy(y, posi), reads=[R_posi], writes=[R_y])
    P.op("dve", lambda g: g.tensor_scalar(y, y, small[:, 16:17], None, ALU.mult), reads=[R_y, R_small], writes=[R_y])
    P.op("dve", lambda g: g.tensor_copy(yi, y), reads=[R_y], writes=[R_yi])
    P.op("dve", lambda g: g.tensor_copy(yf, yi), reads=[R_yi], writes=[R_yf])
    P.op("dve", lambda g: g.tensor_tensor(fr, y, yf, ALU.subtract), reads=[R_y, R_yf], writes=[R_fr])
    for which, dst, R_dst in ((0, sinT, R_sin), (1, cosT, R_cos)):
        src = fr
        if which == 1:
            P.op("dve", lambda g: g.tensor_scalar(y, fr, 0.25, None, ALU.add), reads=[R_fr], writes=[R_y])
            src = y
        R_src = R_fr if which == 0 else R_y
        P.op("dve", lambda g: g.tensor_scalar(m1, src, 0.5, None, ALU.is_gt), reads=[R_src], writes=[R_m1])
        P.op("dve", lambda g: g.tensor_tensor(yf, src, m1, ALU.subtract), reads=[R_src, R_m1], writes=[R_yf])
        P.op("dve", lambda g: g.tensor_scalar(m1, yf, -0.5, None, ALU.is_lt), reads=[R_yf], writes=[R_m1])
        P.op("dve", lambda g: g.tensor_tensor(yf, yf, m1, ALU.add), reads=[R_yf, R_m1], writes=[R_yf])
        P.op("act", lambda g: g.activation(out=dst[:], in_=yf, func=AF.Sin, scale=2.0 * math.pi), reads=[R_yf], writes=[R_dst])

    if stop == "p0":
        P.barrier(); return nc
    def proj_fm(l, gi, M, wbufs, consume, bankset):
        if state.get("pre") == (l, gi):
            wb, R_wb = wpre, R_wpre
            state["pre"] = None
        else:
            wb, R_wb = wbufs[state.setdefault("wfm_i", 0) % len(wbufs)]
            state["wfm_i"] += 1
            P.dma("pool", wb.rearrange("p a b -> p (a b)"), winfm_d[l, gi], writes=[R_wb])
        banks = [bankset * 4 + i for i in range(4)]
        for kc in range(16):
            for tb in range(4):
                mm(pss[banks[tb]][0:M, :], wb[:, kc, 0:M], big[:, kc, tb * 512:(tb + 1) * 512], kc == 0, kc == 15,
                   [R_wb, R_big], R_ps[banks[tb]])
        for tb in range(4):
            consume(tb, pss[banks[tb]][0:M, :], R_ps[banks[tb]])

    def proj_tm(l, gi, wtb, consume, banks):
        wb, R_wb = wtb
        P.dma("pool", wb.rearrange("p a b -> p (a b)"), wintm_d[l, gi], writes=[R_wb])
        for t in range(NT):
            bk = banks[t % len(banks)]
            for kc in range(16):
                mm(pss[bk][:, 0:256], big[:, kc, t * 128:(t + 1) * 128], wb[:, kc, :], kc == 0, kc == 15,
                   [R_wb, R_big], R_ps[bk])
            consume(t, pss[bk][:, 0:256], R_ps[bk])

    def prefetch(l, gi):
        P.dma("pool", wpre.rearrange("p a b -> p (a b)"), winfm_d[l, gi], writes=[R_wpre])
        state["pre"] = (l, gi)

    def post_norm_store(l, o_acc, R_oacc, nsub, wcols, szs, R_sz, fc0, tmps, denom, psi):
        (sq, R_sq), (t1, R_t1), (rinv, R_rinv), (u, R_u) = tmps[0:4]
        for blk in range(4):
            sl = slice(blk * 512, (blk + 1) * 512)
            rstd_part([(o_acc[:, j, sl], R_oacc) for j in range(nsub)], 512, denom, tmps[0:3], psi)
            for j in range(nsub):
                P.op("dve", lambda g: g.scalar_tensor_tensor(out=u[:, 0:512], in0=o_acc[:, j, sl], scalar=wcols[j], in1=rinv[:, 0:512],
                                                             op0=ALU.mult, op1=ALU.mult),
                     reads=[R_oacc, R_rinv, R_small, R_misc], writes=[R_u])
                ost, R_ost = tmps[4 + state.setdefault("ost_i", 0) % 2]
                state["ost_i"] += 1
                P.op("pool", lambda g: g.tensor_tensor(ost[:, 0:512], u[:, 0:512], szs[:, j, sl], ALU.mult),
                     reads=[R_u, R_sz], writes=[R_ost])
                dst_ap, R_d = oT_dst(fc0 + j, sl)
                P.dma("sp", dst_ap, ost[:, 0:512], reads=[R_ost], writes=[R_d], nowaw=True)

    for l in range(L):
        sb_l = SBASE + l * SLW
        lam_init = 0.8 - 0.6 * math.exp(-0.3 * l)

        def sc(off, n=1):
            return small[:, sb_l + off: sb_l + off + n]

        new_phase("A")
        cact, R_cact = ar.alloc([128, 16], F32, "cact")
        c2, R_c2 = ar.alloc([128, 16, 2], BF16, "c2")
        crep, R_crep = ar.alloc([128, 16, 128], BF16, "crep")
        bg, R_bg = ar.alloc([128, D], F32, "bg")
        wab = [ar.alloc([128, 16, 128], F32, "wa%d" % i) for i in range(4)]
        wbb = [ar.alloc([128, 16, 128], BF16, "wb%d" % i) for i in range(3)]
        P.op("act", lambda g: g.activation(out=cact, in_=small[:, 0:16], func=AF.Silu), reads=[R_small], writes=[R_cact])
        P.op("dve", lambda g: g.tensor_copy(c2, cact.unsqueeze(2).to_broadcast([128, 16, 2])), reads=[R_cact], writes=[R_c2])
        P.op("dve", lambda g: g.tensor_copy(crep, cact.unsqueeze(2).to_broadcast([128, 16, 128])), reads=[R_cact], writes=[R_crep])
        P.dma("sp", bg, bgate_d[l].to_broadcast([128, D]), writes=[R_bg])
        for g_ in range(48):
            wf, R_wf = wab[g_ % 4]
            P.dma("sp", wf.rearrange("p a b -> p (a b)"), wada_d[l, g_], writes=[R_wf])
            wa, R_wa = wbb[g_ % 3]
            copy_on(("act", "dve")[g_ % 2], wa.rearrange("p a b -> p (a b)"), wf.rearrange("p a b -> p (a b)"), [R_wf], [R_wa])
            if g_ < 32:
                for kc in range(16):
                    mm(pss[0][:, 2 * g_:2 * g_ + 2], wa[:, kc, :], c2[:, kc, :], kc == 0, kc == 15, [R_wa, R_c2], R_ps[0])
                if g_ == 31:
                    P.op("dve", lambda g: g.tensor_tensor(modc[:, 0:32], pss[0][:, 0:64:2], sc(16, 32), ALU.add),
                         reads=[R_ps[0], R_small], writes=[R_modc])
                    P.op("dve", lambda g: g.scalar_tensor_tensor(out=modc[:, 32:48], in0=modc[:, 16:32], scalar=1.0, in1=sc(0, 16),
                                                                 op0=ALU.add, op1=ALU.mult),
                         reads=[R_modc, R_small], writes=[R_modc])
            else:
                gg = g_ - 32
                bk = 1 + (gg // 4) % 2
                c0 = (gg % 4) * 128
                for kc in range(16):
                    mm(pss[bk][:, c0:c0 + 128], crep[:, kc, :], wa[:, kc, :], kc == 0, kc == 15, [R_wa, R_crep], R_ps[bk])
                if gg % 4 == 3:
                    sl = slice((gg // 4) * 512, (gg // 4 + 1) * 512)
                    P.op("dve", lambda g: g.tensor_tensor(gate_bc[:, sl], pss[bk][:], bg[:, sl], ALU.add),
                         reads=[R_ps[bk], R_bg], writes=[R_gate])

        if stop == "pA":
            P.barrier(); return nc
        new_phase("B")
        xsrc = x_d if l == 0 else x1_d
        xts = [ar.alloc([128, D], F32, "xt%d" % i) for i in range(2)]
        xn, R_xn = ar.alloc([128, 4, D], BF16, "xn")
        junk, R_junk = ar.alloc([128, D], BF16, "junk")
        ssq, R_ssq = ar.alloc([128, 8], F32, "ssq")
        hT = big
        for tb in range(4):
            for i in range(4):
                t = tb * 4 + i
                xt, R_xt = xts[t % 2]
                P.dma("sp", xt, xsrc[t * 128:(t + 1) * 128, :], reads=[R_x1] if l > 0 else [], writes=[R_xt])
                P.op("act", lambda g: g.activation(out=junk, in_=xt, func=AF.Square, accum_out=ssq[:, 0:1]),
                     reads=[R_xt], writes=[R_junk, R_ssq])
                P.op("act", lambda g: g.activation(out=ssq[:, 1:2], in_=ssq[:, 0:1], func=AF.Sqrt, scale=1.0 / D, bias=eps_ap),
                     reads=[R_ssq, R_misc], writes=[R_ssq])
                P.op("dve", lambda g: g.reciprocal(ssq[:, 2:3], ssq[:, 1:2]), reads=[R_ssq], writes=[R_ssq])
                P.op("dve", lambda g: g.tensor_scalar(xn[:, i, :], xt, ssq[:, 2:3], None, ALU.mult),
                     reads=[R_xt, R_ssq], writes=[R_xn])
            for fc in range(16):
                bk = fc % 4
                pT = pss[bk][:].bitcast(BF16)
                for i in range(4):
                    P.op("pe", lambda g: g.transpose(pT[:, i * 128:(i + 1) * 128], xn[:, i, fc * 128:(fc + 1) * 128], ident_b),
                         reads=[R_xn, R_cmb], writes=[R_ps[bk]], pe_accum=i > 0)
                dst = hT[:, fc, tb * 512:(tb + 1) * 512]
                if alt() == "act":
                    P.op("act", lambda g: g.activation(out=dst, in_=pT[:, 0:512], func=AF.Identity,
                                                       scale=modc[:, 32 + fc:33 + fc], bias=modc[:, fc:fc + 1]),
                         reads=[R_ps[bk], R_modc], writes=[R_big])
                else:
                    P.op("dve", lambda g: g.tensor_scalar(dst, pT[:, 0:512], modc[:, 32 + fc:33 + fc], modc[:, fc:fc + 1],
                                                          ALU.mult, ALU.add),
                         reads=[R_ps[bk], R_modc], writes=[R_big])

        prefetch(l, 2)
        if stop == "pB":
            P.barrier(); return nc
        for pr in range(1):
            new_phase("gla")
            wfm = [ar.alloc([128, 16, 128], BF16, "wfm%d" % i) for i in range(2)]
            wtm = ar.alloc([128, 16, 256], BF16, "wtm")
            glrT, R_glrT = ar.alloc([16, S], BF16, "glrT")
            wlr, R_wlr = ar.alloc([16, 256], BF16, "wlr")
            bcs, R_bcs = ar.alloc([128, S], F32, "bcs")
            eb, R_eb = ar.alloc([128, S], F32, "eb")
            q_eT, R_qe = ar.alloc([128, S], BF16, "q_eT")
            k_eT, R_ke = ar.alloc([128, S], BF16, "k_eT")
            v_g, R_vg = ar.alloc([128, NT, 256], BF16, "v_g")
            sz, R_sz = ar.alloc([128, 2, S], BF16, "sz")
            ke_tok, R_ket = ar.alloc([128, NT, 128], BF16, "ke_tok")
            o_acc, R_oacc = ar.alloc([128, 2, S], F32, "o_acc")
            e1, R_e1 = ar.alloc([128, 512], F32, "e1")
            dec, R_dec = ar.alloc([128, 16], F32, "dec")
            nb, R_nb = ar.alloc([128, 2], F32, "nb")
            Sp, R_Sp = ar.alloc([128, 256], F32, "Sp")
            Stmp, R_Stmp = ar.alloc([128, 256], F32, "Stmp")
            Sbf, R_Sbf = ar.alloc([128, 256], BF16, "Sbf")
            attm, R_attm = ar.alloc([128, 2, 128], BF16, "attm")
            tmps = [ar.alloc([128, 2, 512], BF16, "sq"), ar.alloc([128, 512], F32, "t1"), ar.alloc([128, 512], F32, "rinv"),
                    ar.alloc([128, 512], F32, "u"), ar.alloc([128, 512], BF16, "ost"), ar.alloc([128, 512], BF16, "ost2")]

            def c_glr(tb, ps, R):
                copy_on("act", glrT[0:16, tb * 512:(tb + 1) * 512], ps, [R], [R_glrT])
            proj_fm(l, 2, 16, wfm, c_glr, 0)
            P.op("dve", lambda g: g.tensor_copy(wlr, sc(624, 256)[0:16, :]), reads=[R_small], writes=[R_wlr])
            P.op("dve", lambda g: g.tensor_scalar(nb, sc(48, 2), -1.0, None, ALU.mult), reads=[R_small], writes=[R_nb])
            for blk in range(4):
                sl = slice(blk * 512, (blk + 1) * 512)
                bk = 4 + blk % 2
                mm(pss[bk][:], wlr[0:16, pr * 128:(pr + 1) * 128], glrT[0:16, sl], True, True, [R_wlr, R_glrT], R_ps[bk])
                P.op("act", lambda g: g.activation(out=e1, in_=pss[bk][:], func=AF.Exp, scale=-1.0, bias=nb[:, pr:pr + 1]),
                     reads=[R_ps[bk], R_nb], writes=[R_e1])
                P.op("act", lambda g: g.activation(out=bcs[:, sl], in_=e1, func=AF.Ln, bias=one_ap), reads=[R_e1, R_misc], writes=[R_bcs])
            P.op("dve", lambda g: g.tensor_tensor_scan(out=bcs, data0=rmask[:], data1=bcs, initial=0.0, op0=ALU.mult, op1=ALU.add),
                 reads=[R_rmask, R_bcs], writes=[R_bcs])
            P.op("act", lambda g: g.activation(out=eb, in_=bcs, func=AF.Exp, scale=-1.0 / 16.0), reads=[R_bcs], writes=[R_eb])
            P.op("dve", lambda g: g.tensor_copy(dec, eb[:, 127:S:128]), reads=[R_eb], writes=[R_dec])
            P.op("act", lambda g: g.activation(out=bcs, in_=bcs, func=AF.Exp, scale=1.0 / 16.0), reads=[R_bcs], writes=[R_bcs])
            enb = bcs

            def c_q(tb, ps, R):
                sl = slice(tb * 512, (tb + 1) * 512)
                P.op("dve", lambda g: g.scalar_tensor_tensor(out=q_eT[:, sl], in0=ps, scalar=0.125, in1=eb[:, sl], op0=ALU.mult, op1=ALU.mult),
                     reads=[R, R_eb], writes=[R_qe])
            proj_fm(l, 0, 128, wfm, c_q, 1)

            def c_k(tb, ps, R):
                sl = slice(tb * 512, (tb + 1) * 512)
                P.op("dve", lambda g: g.tensor_tensor(k_eT[:, sl], ps, enb[:, sl], ALU.mult), reads=[R, R_bcs], writes=[R_ke])
            proj_fm(l, 1, 128, wfm, c_k, 0)

            def c_v(t, ps, R):
                copy_on("act", v_g[:, t, :], ps, [R], [R_vg])
            proj_tm(l, 0, wtm, c_v, [4, 5])
            for hh in range(2):
                def c_z(tb, ps, R):
                    P.op("act", lambda g: g.activation(out=sz[:, hh, tb * 512:(tb + 1) * 512], in_=ps, func=AF.Silu), reads=[R], writes=[R_sz])
                proj_fm(l, 3 + hh, 128, wfm, c_z, hh)
            for t4 in range(4):
                bk = 6 + t4 % 2
                pT = pss[bk][:].bitcast(BF16)
                for i in range(4):
                    t = t4 * 4 + i
                    P.op("pe", lambda g: g.transpose(pT[:, i * 128:(i + 1) * 128], k_eT[:, t * 128:(t + 1) * 128], ident_b),
                         reads=[R_ke, R_cmb], writes=[R_ps[bk]], pe_accum=i > 0)
                P.op("dve", lambda g: g.tensor_copy(ke_tok[:, t4 * 4:(t4 + 1) * 4, :], pT[:, 0:512].rearrange("p (a b) -> p a b", b=128)),
                     reads=[R_ps[bk]], writes=[R_ket])
            P.op("dve", lambda g: g.memset(Sp, 0.0), writes=[R_Sp])
            for n in range(NT):
                ch = slice(n * 128, (n + 1) * 128)
                ba, bo, bkv = n % 2, 2 + n % 2, 4 + n % 2
                for hh in range(2):
                    hp = slice(64 * hh, 64 * hh + 64)
                    mm(pss[ba][:, hh * 128:(hh + 1) * 128], k_eT[hp, ch], q_eT[hp, ch], True, True, [R_ke, R_qe], R_ps[ba])
                P.op("dve", lambda g: g.tensor_tensor(attm, pss[ba][:, 0:256].rearrange("p (a b) -> p a b", b=128),
                                                      cm_f[:, CTU:CTU + 1, :].to_broadcast([128, 2, 128]), ALU.mult),
                     reads=[R_ps[ba], R_cmf], writes=[R_attm])
                for hh in range(2):
                    hp = slice(64 * hh, 64 * hh + 64)
                    vs = slice(hh * 128, (hh + 1) * 128)
                    mm(pss[bo][:, vs], v_g[:, n, vs], attm[:, hh, :], True, n == 0, [R_vg, R_attm], R_ps[bo])
                    if n > 0:
                        mm(pss[bo][:, vs], Sbf[hp, vs], q_eT[hp, ch], False, True, [R_Sbf, R_qe], R_ps[bo])
                P.op("act", lambda g: g.activation(out=o_acc[:, :, ch], in_=pss[bo][:, 0:256].rearrange("p (a b) -> p a b", b=128), func=AF.Copy),
                     reads=[R_ps[bo]], writes=[R_oacc])
                if n < NT - 1:
                    mm(pss[bkv][:, 0:256], ke_tok[:, n, :], v_g[:, n, :], True, True, [R_ket, R_vg], R_ps[bkv])
                    P.op("dve", lambda g: g.tensor_tensor(Stmp, Sp, pss[bkv][:, 0:256], ALU.add), reads=[R_Sp, R_ps[bkv]], writes=[R_Stmp])
                    P.op("dve", lambda g: g.tensor_scalar(Sp, Stmp, dec[:, n:n + 1], None, ALU.mult), reads=[R_Stmp, R_dec], writes=[R_Sp])
                    P.op("act", lambda g: g.activation(out=Sbf, in_=Stmp, func=AF.Copy, scale=dec[:, n:n + 1]),
                         reads=[R_Stmp, R_dec], writes=[R_Sbf])
            for hh in range(2):
                post_norm_store(l, o_acc[:, hh:hh + 1, :], R_oacc, 1, [sc(50)], sz[:, hh:hh + 1, :], R_sz, 2 * pr + hh, tmps, 128.0, 6)
            prefetch(l, 11)

        if stop == "pC1":
            P.barrier(); return nc
        for h in range(2):
            new_phase("gdn")
            wfm = [ar.alloc([128, 16, 128], BF16, "wfm%d" % i) for i in range(2)]
            xbf, R_xbf = ar.alloc([128, S + 8], BF16, "xbf")
            diag, R_diag = ar.alloc([128, 4, 128], BF16, "diag")
            cs, R_cs = ar.alloc([128, S], F32, "cs")
            knT, R_kn = ar.alloc([128, S], BF16, "knT")
            qnT, R_qn = ar.alloc([128, S], BF16, "qnT")
            cvT, R_cv = ar.alloc([128, S], BF16, "cvT")
            kbT, R_kb = ar.alloc([128, S], BF16, "kbT")
            q_eT, R_qe = ar.alloc([128, S], BF16, "q_eT")
            vb_tok, R_vb = ar.alloc([128, NT, 128], BF16, "vb_tok")
            kbg_tok, R_kbg = ar.alloc([128, NT, 128], BF16, "kbg_tok")
            kt_tok, R_kt = ar.alloc([128, NT, 128], BF16, "kt_tok")
            gc, R_gc = ar.alloc([128, S], F32, "gc")
            beta, R_beta = ar.alloc([128, S], BF16, "beta")
            eg, R_eg = ar.alloc([128, S], F32, "eg")
            tl, R_tl = ar.alloc([128, S], F32, "tl")
            dabT, R_dab = tl[0:8, :], R_tl
            cols, R_cols = ar.alloc([128, 6, 16], F32, "cols")
            nA, R_nA = ar.alloc([128, 4], F32, "nA")
            Sp, R_Sp = ar.alloc([128, 128], F32, "Sp")
            Sbf, R_Sbf = ar.alloc([128, 128], BF16, "Sbf")
            dm = [ar.alloc([128, 128], F32, "dm%d" % i) for i in range(4)]
            mb = [ar.alloc([128, 128], BF16, "mb%d" % i) for i in range(10)]
            u_sb, R_u = ar.alloc([128, 128], F32, "u_sb")
            tmps = [ar.alloc([128, 2, 512], BF16, "sq"), ar.alloc([128, 512], F32, "t1"), ar.alloc([128, 512], F32, "rinv"),
                    ar.alloc([128, 512], F32, "u"), ar.alloc([128, 512], BF16, "ost"), ar.alloc([128, 512], BF16, "ost2")]
            e1, R_e1 = tmps[3]
            o_acc, R_oacc = cs.rearrange("p (a b) -> p a b", a=1), R_cs

            def c_dab(tb, ps, R):
                copy_on("act", dabT[0:8, tb * 512:(tb + 1) * 512], ps, [R], [R_dab])
            proj_fm(l, 11, 8, wfm, c_dab, 0)
            P.op("dve", lambda g: g.memset(xbf[:, 0:3], 0.0), writes=[R_xbf])
            for which in range(3):
                ti = which * 4 + h
                for j in range(4):
                    P.op("dve", lambda g: g.tensor_scalar(diag[:, j, :], ident_f, sc(64 + ti * 4 + j), None, ALU.mult),
                         reads=[R_cmf, R_small], writes=[R_diag])

                def c_x(tb, ps, R):
                    copy_on(alt(), xbf[:, 3 + tb * 512:3 + (tb + 1) * 512], ps, [R], [R_xbf])
                proj_fm(l, 5 + 2 * which + h, 128, wfm, c_x, 1)
                for blk in range(4):
                    sl = slice(blk * 512, (blk + 1) * 512)
                    bk = blk % 2
                    for j in range(4):
                        mm(pss[bk][:], diag[:, j, :], xbf[:, blk * 512 + j: blk * 512 + j + 512], j == 0, j == 3, [R_diag, R_xbf], R_ps[bk])
                    if which == 2:
                        P.op("act", lambda g: g.activation(out=cvT[:, sl], in_=pss[bk][:], func=AF.Silu), reads=[R_ps[bk]], writes=[R_cv])
                    else:
                        P.op("act", lambda g: g.activation(out=cs[:, sl], in_=pss[bk][:], func=AF.Silu), reads=[R_ps[bk]], writes=[R_cs])
                if which < 2:
                    for blk in range(4):
                        sl = slice(blk * 512, (blk + 1) * 512)
                        rinv, R_rinv = rstd_part([(cs[:, sl], R_cs)], 512, 1.0, tmps[0:3], 2 + blk % 2)
                        if which == 0:
                            P.op("dve", lambda g: g.scalar_tensor_tensor(out=qnT[:, sl], in0=cs[:, sl], scalar=128.0 ** -0.5, in1=rinv[:, 0:512],
                                                                         op0=ALU.mult, op1=ALU.mult), reads=[R_cs, R_rinv], writes=[R_qn])
                        else:
                            P.op("dve", lambda g: g.tensor_tensor(knT[:, sl], cs[:, sl], rinv[:, 0:512], ALU.mult), reads=[R_cs, R_rinv], writes=[R_kn])
            if stop == "g1":
                P.barrier(); return nc
            P.op("act", lambda g: g.activation(out=nA, in_=sc(56, 4), func=AF.Exp), reads=[R_small], writes=[R_nA])
            P.op("dve", lambda g: g.tensor_scalar(nA, nA, -1.0, None, ALU.mult), reads=[R_nA], writes=[R_nA])
            for blk in range(4):
                sl = slice(blk * 512, (blk + 1) * 512)
                bk = 4 + blk % 2
                mm(pss[bk][:], selm[0:8, h, :], dabT[0:8, sl], True, True, [R_selm, R_dab], R_ps[bk])
                P.op("act", lambda g: g.activation(out=e1, in_=pss[bk][:], func=AF.Exp, bias=sc(60 + h)), reads=[R_ps[bk], R_small], writes=[R_e1])
                P.op("act", lambda g: g.activation(out=gc[:, sl], in_=e1, func=AF.Ln, bias=one_ap), reads=[R_e1, R_misc], writes=[R_gc])
                bk2 = 6 + blk % 2
                mm(pss[bk2][:], selm[0:8, 4 + h, :], dabT[0:8, sl], True, True, [R_selm, R_dab], R_ps[bk2])
                P.op("act", lambda g: g.activation(out=beta[:, sl], in_=pss[bk2][:], func=AF.Sigmoid), reads=[R_ps[bk2]], writes=[R_beta])
            P.op("dve", lambda g: g.tensor_tensor_scan(out=gc, data0=rmask[:], data1=gc, initial=0.0, op0=ALU.mult, op1=ALU.add),
                 reads=[R_rmask, R_gc], writes=[R_gc])
            P.op("dve", lambda g: g.tensor_scalar(gc, gc, nA[:, h:h + 1], None, ALU.mult), reads=[R_gc, R_nA], writes=[R_gc])
            gcl = cols[:, 0, :]
            cd = cols[:, 1, :]
            P.op("dve", lambda g: g.tensor_copy(gcl, gc[:, 127:S:128]), reads=[R_gc], writes=[R_cols])
            P.op("act", lambda g: g.activation(out=cd, in_=gcl, func=AF.Exp), reads=[R_cols], writes=[R_cols])
            P.op("act", lambda g: g.activation(out=eg, in_=gc, func=AF.Exp), reads=[R_gc], writes=[R_eg])
            P.op("dve", lambda g: g.tensor_tensor(q_eT, qnT, eg, ALU.mult), reads=[R_qn, R_eg], writes=[R_qe])
            P.op("dve", lambda g: g.tensor_tensor(kbT, knT, beta, ALU.mult), reads=[R_kn, R_beta], writes=[R_kb])
            P.op("pool", lambda g: g.tensor_tensor(eg, eg, beta, ALU.mult), reads=[R_eg, R_beta], writes=[R_eg])
            for n in range(NT):
                ch = slice(n * 128, (n + 1) * 128)
                P.op("act", lambda g: g.activation(out=tl[:, ch], in_=gc[:, ch], func=AF.Exp, scale=-1.0, bias=gcl[:, n:n + 1]),
                     reads=[R_gc, R_cols], writes=[R_tl])
            if stop == "g2":
                P.barrier(); return nc
            for qi, (src, R_src) in enumerate(((gc, R_gc), (beta, R_beta), (eg, R_eg), (tl, R_tl))):
                oh = cm_b[:, CI, 0:2] if src is beta else cm_f[:, CI, 0:2]
                for n in range(NT):
                    c0 = (qi * NT + n) * 2
                    mm(pss[3][:, c0:c0 + 2], src[:, n * 128:(n + 1) * 128], oh, True, True, [R_src, R_cmf, R_cmb], R_ps[3])
            P.op("dve", lambda g: g.tensor_copy(cols[:, 2:6, :], pss[3][:, 0:128:2].rearrange("p (a b) -> p a b", b=NT)),
                 reads=[R_ps[3]], writes=[R_cols])
            gc_col, beta_col, bexp_col, tail_col = (cols[:, i, :] for i in (2, 3, 4, 5))
            if stop == "g2b":
                P.barrier(); return nc
            for t in range(NT):
                bk = t % 2
                pT = pss[bk][:].bitcast(BF16)
                ts = slice(t * 128, (t + 1) * 128)
                P.op("pe", lambda g: g.transpose(pT[:, 0:128], knT[:, ts], ident_b), reads=[R_kn, R_cmb], writes=[R_ps[bk]])
                P.op("pe", lambda g: g.transpose(pT[:, 128:256], cvT[:, ts], ident_b), reads=[R_cv, R_cmb], writes=[R_ps[bk]], pe_accum=True)
                P.op("dve", lambda g: g.tensor_scalar(kbg_tok[:, t, :], pT[:, 0:128], bexp_col[:, t:t + 1], None, ALU.mult),
                     reads=[R_ps[bk], R_cols], writes=[R_kbg])
                P.op("dve", lambda g: g.tensor_scalar(kt_tok[:, t, :], pT[:, 0:128], tail_col[:, t:t + 1], None, ALU.mult),
                     reads=[R_ps[bk], R_cols], writes=[R_kt])
                P.op("dve", lambda g: g.tensor_scalar(vb_tok[:, t, :], pT[:, 128:256], beta_col[:, t:t + 1], None, ALU.mult),
                     reads=[R_ps[bk], R_cols], writes=[R_vb])

            if stop == "g3":
                P.barrier(); return nc
            P.barrier(reset=False)
            sz, R_sz = tl.bitcast(BF16)[:, 0:S].rearrange("p (a b) -> p a b", a=1), Reg("sz")
            mb2 = [(xbf[:, i * 128:(i + 1) * 128], Reg("mb2_%d" % i)) for i in range(16)]

            def c_z(tb, ps, R):
                P.op("act", lambda g: g.activation(out=sz[:, 0, tb * 512:(tb + 1) * 512], in_=ps, func=AF.Silu), reads=[R], writes=[R_sz])
            proj_fm(l, 12 + h, 128, wfm, c_z, 1)
            if stop == "g4":
                P.barrier(); return nc
            P.barrier(reset=False)
            G = 4
            eg4 = eg.rearrange("p (t g c) -> p t g c", g=G, c=128)
            (dA4, R_dA), (dB4, R_dB), (dD4, R_dD), (u4, R_u4) = [(eg4[:, i], Reg("f4_%d" % i)) for i in range(4)]
            pool16 = []
            for src in (beta, cvT, wfm[0][0].rearrange("p a b -> p (a b)"), wfm[1][0].rearrange("p a b -> p (a b)"),
                        tl.bitcast(BF16)[:, S:2 * S]):
                v4 = src.rearrange("p (t g c) -> p t g c", g=G, c=128)
                pool16 += [(v4[:, i], Reg("b4_%d" % len(pool16))) for i in range(4)]
            ((Pm4, R_P), (PT4, R_PT), (qk4, R_qk), (Pd4, R_Pd), (PTd4, R_PTd), (Po32, R_Po32), (PTo32, R_PTo32), (Po64, R_Po64),
             Abuf0, ATbuf0, Abuf1, ATbuf1, sq0, sqT0, sq1, sqT1, (U1, R_U1), (T1, R_T1), (wT4, R_wT)) = pool16[0:19]
            vnew, R_vn = mb[0]

            def bc4(blk, f32=True):
                src = cm_f if f32 else cm_b
                return src[:, blk:blk + 1, :].to_broadcast([128, G, 128])

            def flat(t4):
                return t4.rearrange("p g c -> p (g c)")

            def p4(bank):
                return pss[bank][:].rearrange("p (g c) -> p g c", c=128)

            P.op("dve", lambda g: g.memset(Sp, 0.0), writes=[R_Sp])
            P.op("dve", lambda g: g.memset(Sbf, 0.0), writes=[R_Sbf])
            for bb in range(NT // G):
                ns = [bb * G + g_ for g_ in range(G)]
                chs = [slice(n * 128, (n + 1) * 128) for n in ns]
                for g_, n in enumerate(ns):
                    gcc = gc_col[:, n:n + 1]
                    P.op("dve", lambda g: g.tensor_scalar(dA4[:, g_, :], gc[:, chs[g_]], gcc, 0.0, ALU.subtract, ALU.max),
                         reads=[R_gc, R_cols], writes=[R_dA])
                    P.op("dve", lambda g: g.tensor_scalar(dB4[:, g_, :], gc[:, chs[g_]], gcc, 0.0, ALU.subtract, ALU.min),
                         reads=[R_gc, R_cols], writes=[R_dB])
                P.op("act", lambda g: g.activation(out=flat(dA4), in_=flat(dA4), func=AF.Exp, scale=-1.0), reads=[R_dA], writes=[R_dA])
                P.op("act", lambda g: g.activation(out=flat(dB4), in_=flat(dB4), func=AF.Exp), reads=[R_dB], writes=[R_dB])
                P.op("dve", lambda g: g.tensor_tensor(dA4, dA4, bc4(CNSL), ALU.mult), reads=[R_dA, R_cmf], writes=[R_dA])
                P.op("dve", lambda g: g.tensor_tensor(dD4, dB4, bc4(CNSU), ALU.mult), reads=[R_dB, R_cmf], writes=[R_dD])
                P.op("dve", lambda g: g.tensor_tensor(dB4, dB4, bc4(CTU), ALU.mult), reads=[R_dB, R_cmf], writes=[R_dB])
                for g_ in range(G):
                    cs_ = slice(g_ * 128, (g_ + 1) * 128)
                    mm(pss[0][:, cs_], kbT[:, chs[g_]], knT[:, chs[g_]], True, True, [R_kb, R_kn], R_ps[0])
                for g_ in range(G):
                    cs_ = slice(g_ * 128, (g_ + 1) * 128)
                    mm(pss[1][:, cs_], knT[:, chs[g_]], kbT[:, chs[g_]], True, True, [R_kb, R_kn], R_ps[1])
                for g_ in range(G):
                    cs_ = slice(g_ * 128, (g_ + 1) * 128)
                    mm(pss[2][:, cs_], knT[:, chs[g_]], qnT[:, chs[g_]], True, True, [R_kn, R_qn], R_ps[2])
                P.op("dve", lambda g: g.tensor_tensor(Pm4, p4(0), dA4, ALU.mult), reads=[R_ps[0], R_dA], writes=[R_P])
                P.op("dve", lambda g: g.tensor_tensor(PT4, p4(1), dD4, ALU.mult), reads=[R_ps[1], R_dD], writes=[R_PT])
                P.op("dve", lambda g: g.tensor_tensor(qk4, p4(2), dB4, ALU.mult), reads=[R_ps[2], R_dB], writes=[R_qk])
                for dst, R_d, src, R_s, mk in ((Pd4, R_Pd, Pm4, R_P, CB32), (PTd4, R_PTd, PT4, R_PT, CB32), (Po32, R_Po32, Pm4, R_P, CO32),
                                               (PTo32, R_PTo32, PT4, R_PT, CO32), (Po64, R_Po64, Pm4, R_P, CO64)):
                    P.op("dve", lambda g: g.tensor_tensor(dst, src, bc4(mk, False), ALU.mult), reads=[R_s, R_cmb], writes=[R_d])
                Acur, ATcur, Anxt, ATnxt = Abuf0, ATbuf0, Abuf1, ATbuf1
                P.op("dve", lambda g: g.tensor_tensor(Acur[0], Pd4, bc4(CI, False), ALU.add), reads=[R_Pd, R_cmb], writes=[Acur[1]])
                P.op("dve", lambda g: g.tensor_tensor(ATcur[0], PTd4, bc4(CI, False), ALU.add), reads=[R_PTd, R_cmb], writes=[ATcur[1]])

                def mm4(bank, lhs4, R_l, rhs4, R_r, start=True):
                    for g_ in range(G):
                        cs_ = slice(g_ * 128, (g_ + 1) * 128)
                        mm(pss[bank][:, cs_], lhs4[:, g_, :], rhs4[:, g_, :], start, start or g_ == G - 1, [R_l, R_r], R_ps[bank])

                def add_mm4(bank, base, lhs4, R_l, rhs4, R_r):
                    mm(pss[bank][:], ident_b, flat(base[0]), True, False, [R_cmb, base[1]], R_ps[bank])
                    mm4(bank, lhs4, R_l, rhs4, R_r, start=False)

                cur = ((Pd4, R_Pd), (PTd4, R_PTd))
                sqb = [(sq0, sqT0), (sq1, sqT1)]
                for lev in range(4):
                    (cP, R_cP), (cPT, R_cPT) = cur
                    (nP, R_nP), (nPT, R_nPT) = sqb[lev % 2]
                    mm4(3, cPT, R_cPT, cP, R_cP)
                    mm4(4, cP, R_cP, cPT, R_cPT)
                    copy_on("act", flat(nP), pss[3][:], [R_ps[3]], [R_nP])
                    copy_on("dve", flat(nPT), pss[4][:], [R_ps[4]], [R_nPT])
                    add_mm4(5, Acur, nPT, R_nPT, Acur[0], Acur[1])
                    add_mm4(6, ATcur, nP, R_nP, ATcur[0], ATcur[1])
                    copy_on("dve", flat(Anxt[0]), pss[5][:], [R_ps[5]], [Anxt[1]])
                    copy_on("act", flat(ATnxt[0]), pss[6][:], [R_ps[6]], [ATnxt[1]])
                    Acur, Anxt = Anxt, Acur
                    ATcur, ATnxt = ATnxt, ATcur
                    cur = ((nP, R_nP), (nPT, R_nPT))
                mm4(3, PTo32, R_PTo32, Acur[0], Acur[1])
                mm4(4, Po32, R_Po32, ATcur[0], ATcur[1])
                copy_on("act", flat(U1), pss[3][:], [R_ps[3]], [R_U1])
                copy_on("dve", flat(T1), pss[4][:], [R_ps[4]], [R_T1])
                add_mm4(5, Acur, ATcur[0], ATcur[1], U1, R_U1)
                add_mm4(6, ATcur, Acur[0], Acur[1], T1, R_T1)
                copy_on("dve", flat(Anxt[0]), pss[5][:], [R_ps[5]], [Anxt[1]])
                copy_on("act", flat(ATnxt[0]), pss[6][:], [R_ps[6]], [ATnxt[1]])
                Acur, Anxt = Anxt, Acur
                ATcur, ATnxt = ATnxt, ATcur
                mm4(4, Po64, R_Po64, ATcur[0], ATcur[1])
                copy_on("dve", flat(T1), pss[4][:], [R_ps[4]], [R_T1])
                add_mm4(6, ATcur, Acur[0], Acur[1], T1, R_T1)
                copy_on("act", flat(ATnxt[0]), pss[6][:], [R_ps[6]], [ATnxt[1]])
                AT4, R_AT = ATnxt
                for g_, n in enumerate(ns):
                    cs_ = slice(g_ * 128, (g_ + 1) * 128)
                    mm(pss[3][:, cs_], AT4[:, g_, :], vb_tok[:, n, :], True, True, [R_AT, R_vb], R_ps[3])
                for g_, n in enumerate(ns):
                    cs_ = slice(g_ * 128, (g_ + 1) * 128)
                    mm(pss[4][:, cs_], kbg_tok[:, n, :], AT4[:, g_, :], True, True, [R_AT, R_kbg], R_ps[4])
                copy_on("act", flat(u4), pss[3][:], [R_ps[3]], [R_u4])
                copy_on("dve", flat(wT4), pss[4][:], [R_ps[4]], [R_wT])
                for g_, n in enumerate(ns):
                    ch = chs[g_]
                    if n > 0:
                        mm(pss[0][:, 0:128], wT4[:, g_, :], Sbf, True, True, [R_wT, R_Sbf], R_ps[0])
                        P.op("dve", lambda g: g.tensor_tensor(vnew, u4[:, g_, :], pss[0][:, 0:128], ALU.subtract),
                             reads=[R_u4, R_ps[0]], writes=[R_vn])
                    else:
                        copy_on("dve", vnew, u4[:, g_, :], [R_u4], [R_vn])
                    if n > 0:
                        mm(pss[7][:, 0:128], Sbf, q_eT[:, ch], True, False, [R_Sbf, R_qe], R_ps[7])
                    mm(pss[7][:, 0:128], vnew, qk4[:, g_, :], n == 0, True, [R_vn, R_qk], R_ps[7])
                    copy_on("act", o_acc[:, 0, ch], pss[7][:, 0:128], [R_ps[7]], [R_oacc])
                    if n < NT - 1:
                        mm(pss[1][:, 0:128], kt_tok[:, n, :], vnew, True, True, [R_kt, R_vn], R_ps[1])
                        P.op("dve", lambda g: g.scalar_tensor_tensor(out=Sp, in0=Sp, scalar=cd[:, n:n + 1], in1=pss[1][:, 0:128],
                                                                     op0=ALU.mult, op1=ALU.add), reads=[R_Sp, R_cols, R_ps[1]], writes=[R_Sp])
                        copy_on("act", Sbf, Sp, [R_Sp], [R_Sbf])
            post_norm_store(l, o_acc, R_oacc, 1, [sc(51)], sz, R_sz, 2 + h, tmps, 128.0, 6)
            prefetch(l, 11 if h == 0 else 14)

        P.collective("AllGather", PAIRS, oTl_t[0].ap().opt(), oTg_t[0].ap().opt(), reads=[R_oTl[0]], writes=[R_oTg[0]])
        if stop == "pC2":
            P.barrier(); return nc
        for h in range(2):
            new_phase("diff")
            wfm = [ar.alloc([128, 16, 128], BF16, "wfm%d" % i) for i in range(3)]
            wtm = ar.alloc([128, 16, 256], BF16, "wtm")
            qT, R_q = ar.alloc([128, 2, S], BF16, "qT")
            kT, R_k = ar.alloc([128, 2, S], BF16, "kT")
            v_sb, R_v = ar.alloc([128, NT, 256], BF16, "v_sb")
            sz, R_sz = ar.alloc([128, 2, S], BF16, "sz")
            ebuf = [ar.alloc([128, 512], BF16, "e%d" % i) for i in range(3)]
            qn, R_qnb = ar.alloc([128, 512], BF16, "qn")
            ta, R_ta = ar.alloc([128, 512], F32, "ta")
            tb_, R_tb = ar.alloc([128, 512], F32, "tb")
            tO, R_tO = ar.alloc([128, 2, 512], F32, "tO")
            rs, R_rs = ar.alloc([128, 512], F32, "rs")
            sq2, R_sq2 = ar.alloc([128, 1024], BF16, "sq2")
            t1b, R_t1b = ar.alloc([128, 1024], F32, "t1b")
            rinvb, R_rinvb = ar.alloc([128, 1024], F32, "rinvb")
            qnb, R_qnb2 = ar.alloc([128, 1024], BF16, "qnb")
            tab, R_tab = ar.alloc([128, 1024], F32, "tab")
            tbb, R_tbb = ar.alloc([128, 1024], F32, "tbb")
            lamt, R_lam = ar.alloc([128, 8], F32, "lam")
            lprod, R_lprod = ar.alloc([128, 256], F32, "lprod")
            wn2, R_wn2 = ar.alloc([128, 2], F32, "wn2")
            tmps = [ar.alloc([128, 2, 512], BF16, "sq"), ar.alloc([128, 512], F32, "t1"), ar.alloc([128, 512], F32, "rinv"),
                    ar.alloc([128, 512], F32, "u"), ar.alloc([128, 512], BF16, "ost"), ar.alloc([128, 512], BF16, "ost2")]
            P.op("dve", lambda g: g.tensor_tensor(lprod[:, 0:128], sc(112, 128), sc(240, 128), ALU.mult), reads=[R_small], writes=[R_lprod])
            P.op("dve", lambda g: g.tensor_tensor(lprod[:, 128:256], sc(368, 128), sc(496, 128), ALU.mult), reads=[R_small], writes=[R_lprod])
            P.op("dve", lambda g: g.tensor_reduce(lamt[:, 0:2], lprod.rearrange("p (a b) -> p a b", b=128), AX.X, ALU.add),
                 reads=[R_lprod], writes=[R_lam])
            P.op("act", lambda g: g.activation(out=lamt[:, 2:4], in_=lamt[:, 0:2], func=AF.Exp), reads=[R_lam], writes=[R_lam])
            P.op("dve", lambda g: g.scalar_tensor_tensor(out=lamt[:, 4:5], in0=lamt[:, 3:4], scalar=-lam_init, in1=lamt[:, 2:3],
                                                         op0=ALU.add, op1=ALU.subtract), reads=[R_lam], writes=[R_lam])
            nlam = lamt[:, 4:5]
            P.op("dve", lambda g: g.tensor_scalar(wn2, sc(54, 2), 1.0 - lam_init, None, ALU.mult), reads=[R_small], writes=[R_wn2])
            units = [(which, m, half) for which in range(2) for m in range(2) for half in range(2)]
            qk_meta = ((qT, R_q, 14, 52), (kT, R_k, 18, 53))
            ustate = {}

            def qk_issue(ui):
                which, m, half = units[ui]
                g0 = qk_meta[which][2]
                wb, R_wb = qk_w[which * 2 + m]
                banks = [4 + 2 * (ui % 2), 5 + 2 * (ui % 2)]
                for kc in range(16):
                    for i in range(2):
                        tb = half * 2 + i
                        mm(pss[banks[i]][:], wb[:, kc, :], big[:, kc, tb * 512:(tb + 1) * 512], kc == 0, kc == 15,
                           [R_wb, R_big], R_ps[banks[i]])

            def qk_consume(ui):
                which, m, half = units[ui]
                dstT, R_dst, g0, wcol = qk_meta[which]
                b0_ = 4 + 2 * (ui % 2)
                ps2 = psall[:, b0_ * 512:(b0_ + 2) * 512]
                Rb = [R_ps[b0_], R_ps[b0_ + 1]]
                sl2 = slice(half * 1024, (half + 1) * 1024)
                P.op("act", lambda g: g.activation(out=sq2, in_=ps2, func=AF.Square), reads=Rb, writes=[R_sq2])
                for i in range(2):
                    mm(pss[2 + i][:], ones_b, sq2[:, i * 512:(i + 1) * 512], True, True, [R_sq2, R_cmb], R_ps[2 + i])
                P.op("act", lambda g: g.activation(out=t1b, in_=psall[:, 2 * 512:4 * 512], func=AF.Ln, scale=1.0 / 128.0, bias=eps_ap),
                     reads=[R_ps[2], R_ps[3], R_misc], writes=[R_t1b])
                P.op("act", lambda g: g.activation(out=rinvb, in_=t1b, func=AF.Exp, scale=-0.5), reads=[R_t1b], writes=[R_rinvb])
                P.op("dve", lambda g: g.scalar_tensor_tensor(out=qnb, in0=ps2, scalar=sc(wcol), in1=rinvb, op0=ALU.mult, op1=ALU.mult),
                     reads=Rb + [R_rinvb, R_small], writes=[R_qnb2])
                for i in range(2):
                    mm(pss[i][:], cm_b[:, CP, :], qnb[:, i * 512:(i + 1) * 512], True, True, [R_cmb, R_qnb2], R_ps[i])
                P.op("pool", lambda g: g.tensor_tensor(tab, qnb, cosT[:, sl2], ALU.mult), reads=[R_qnb2, R_cos], writes=[R_tab])
                P.op("dve", lambda g: g.tensor_tensor(tbb, psall[:, 0:1024], sinT[:, sl2], ALU.mult), reads=[R_ps[0], R_ps[1], R_sin], writes=[R_tbb])
                P.op("pool", lambda g: g.tensor_tensor(dstT[:, m, sl2], tab, tbb, ALU.add), reads=[R_tab, R_tbb], writes=[R_dst])
            qk_w = []
            for which_ in range(2):
                for m_ in range(2):
                    gi_ = qk_meta[which_][2] + 2 * h + m_
                    if state.get("pre") == (l, gi_):
                        qk_w.append((wpre, R_wpre))
                        state["pre"] = None
                        continue
                    wb_, R_wb_ = wfm[sum(1 for w_ in qk_w if w_[0] is not wpre)]
                    P.dma("pool", wb_.rearrange("p a b -> p (a b)"), winfm_d[l, gi_], writes=[R_wb_])
                    qk_w.append((wb_, R_wb_))
            qk_issue(0)
            for ui in range(len(units)):
                if ui + 1 < len(units):
                    qk_issue(ui + 1)
                qk_consume(ui)

            def c_v(t, ps, R):
                copy_on(alt(), v_sb[:, t, :], ps, [R], [R_v])
            proj_tm(l, 1 + h, wtm, c_v, [0, 1])
            for j in range(2):
                def c_z(tb, ps, R):
                    P.op("act", lambda g: g.activation(out=sz[:, j, tb * 512:(tb + 1) * 512], in_=ps, func=AF.Silu), reads=[R], writes=[R_sz])
                proj_fm(l, 22 + 2 * h + j, 128, wfm, c_z, j)
            if h == 1:
                for fc in range(16):
                    P.dma("pool", big[:, fc, :], wout_d[l][:, fc * D:(fc + 1) * D], writes=[R_big], nowaw=fc > 0)
            scale = 128.0 ** -0.5
            steps = [(qb, m, kc) for qb in range(4) for m in range(2) for kc in range(4 * (qb + 1))]

            def qk_mm(si):
                qb, m, kc = steps[si]
                col0 = max(kc - 4 * qb, 0) * 128
                bsc = si % 2
                mm(pss[bsc][:, col0:512], kT[:, m, kc * 128:(kc + 1) * 128], qT[:, m, qb * 512 + col0:(qb + 1) * 512], True, True,
                   [R_k, R_q], R_ps[bsc])
            qk_mm(0)
            for si, (qb, m, kc) in enumerate(steps):
                qs = slice(qb * 512, (qb + 1) * 512)
                nk = 4 * (qb + 1)
                bo0, bo1, bs = (2, 3, 4) if m == 0 else (5, 6, 7)
                if si + 1 < len(steps):
                    qk_mm(si + 1)
                bsc = si % 2
                c = kc - 4 * qb
                col0 = max(c, 0) * 128
                e, R_e = ebuf[si % 3]
                P.op("act", lambda g: g.activation(out=e[:, col0:512], in_=pss[bsc][:, col0:512], func=AF.Exp, scale=scale),
                     reads=[R_ps[bsc]], writes=[R_e])
                if c >= 0:
                    P.op("pool", lambda g: g.tensor_tensor(e[:, col0:col0 + 128], e[:, col0:col0 + 128], cm_b[:, CTU, :], ALU.mult),
                         reads=[R_e, R_cmb], writes=[R_e])
                first, last = kc == 0, kc == nk - 1
                mm(pss[bo0][:, col0:512], v_sb[:, kc, 0:128], e[:, col0:512], first, last, [R_v, R_e], R_ps[bo0])
                mm(pss[bo1][:, col0:512], v_sb[:, kc, 128:256], e[:, col0:512], first, last, [R_v, R_e], R_ps[bo1])
                mm(pss[bs][:, col0:512], ones_b, e[:, col0:512], first, last, [R_cmb, R_e], R_ps[bs])
                if not last:
                    continue
                P.op("dve", lambda g: g.reciprocal(rs, pss[bs][:]), reads=[R_ps[bs]], writes=[R_rs])
                for j, bo in enumerate((bo0, bo1)):
                    if m == 0:
                        P.op("dve", lambda g: g.tensor_tensor(tO[:, j, :], pss[bo][:], rs, ALU.mult), reads=[R_ps[bo], R_rs], writes=[R_tO])
                    else:
                        P.op("dve", lambda g: g.scalar_tensor_tensor(out=ta, in0=pss[bo][:], scalar=nlam, in1=rs, op0=ALU.mult, op1=ALU.mult),
                             reads=[R_ps[bo], R_rs, R_lam], writes=[R_ta])
                        P.op("pool", lambda g: g.tensor_tensor(tO[:, j, :], tO[:, j, :], ta, ALU.add), reads=[R_tO, R_ta], writes=[R_tO])
                if m == 0:
                    continue
                (sq, R_sq), (t1, R_t1), (rinv, R_rinv), (u, R_u) = tmps[0:4]
                rstd_part([(tO[:, j, :], R_tO) for j in range(2)], 512, 256.0, tmps[0:3], 4)
                for j in range(2):
                    ost, R_ost = tmps[4 + j]
                    P.op("dve", lambda g: g.scalar_tensor_tensor(out=u, in0=tO[:, j, :], scalar=wn2[:, j:j + 1], in1=rinv, op0=ALU.mult, op1=ALU.mult),
                         reads=[R_tO, R_rinv, R_wn2], writes=[R_u])
                    P.op("pool", lambda g: g.tensor_tensor(ost, u, sz[:, j, qs], ALU.mult), reads=[R_u, R_sz], writes=[R_ost])
                    dst_ap, R_d = oT_dst(4 + 2 * h + j, qs)
                    P.dma("sp", dst_ap, ost, reads=[R_ost], writes=[R_d], nowaw=True)
            if h == 0:
                prefetch(l, 16)
            P.collective("AllGather", PAIRS, oTl_t[1 + h].ap().opt(), oTg_t[1 + h].ap().opt(), reads=[R_oTl[1 + h]], writes=[R_oTg[1 + h]])

        if stop == "pC3":
            P.barrier(); return nc
        new_phase("D")
        wo = big
        xts = [ar.alloc([128, D], F32, "xt%d" % i) for i in range(2)]
        obs = [ar.alloc([128, 16, 512], BF16, "ob%d" % i) for i in range(2)]
        xos = [ar.alloc([128, D], F32, "xo%d" % i) for i in range(2)]
        ytmp, R_yt = ar.alloc([128, 512], F32, "ytmp")
        xsrc = x_d if l == 0 else x1_d
        xdst = out_d if l == L - 1 else x1_d
        R_dst = R_out if l == L - 1 else R_x1
        for tb in range(4):
            ob, R_ob = obs[tb % 2]
            for i2, (c0_, n_) in enumerate(((0, 8), (8, 4), (12, 4))):
                P.dma("sp", ob[:, c0_:c0_ + n_, :], oTg_d[i2][:, tb * 512:(tb + 1) * 512].rearrange("(c p) t -> p c t", p=128),
                      reads=[R_oTg[i2]], writes=[R_ob], nowaw=i2 > 0)
            for i in range(4):
                t = tb * 4 + i
                xt, R_xt = xts[t % 2]
                xo, R_xo = xos[t % 2]
                P.dma("sp", xt, xsrc[t * 128:(t + 1) * 128, :], reads=[R_x1] if l > 0 else [], writes=[R_xt])
                for ng in range(4):
                    ns = slice(ng * 512, (ng + 1) * 512)
                    bk = (t * 4 + ng) % 4
                    for fc in range(16):
                        mm(pss[bk][:], ob[:, fc, i * 128:(i + 1) * 128], wo[:, fc, ns], fc == 0, fc == 15, [R_ob, R_big], R_ps[bk])
                    P.op("dve", lambda g: g.tensor_tensor(ytmp, pss[bk][:], gate_bc[:, ns], ALU.mult), reads=[R_ps[bk], R_gate], writes=[R_yt])
                    P.op("pool", lambda g: g.tensor_tensor(xo[:, ns], ytmp, xt[:, ns], ALU.add), reads=[R_yt, R_xt], writes=[R_xo])
                P.dma("sp", xdst[t * 128:(t + 1) * 128, :], xo, reads=[R_xo], writes=[R_dst], nowaw=True)

    P.barrier(collectives=True)
    if scopes and state.get("scope") is not None:
        nc.leave_named_scope(state["scope"][0], state["scope"][1], False)
    return nc


def _col(v):
    return np.ascontiguousarray(np.asarray(v, np.float32).reshape(-1, 128).T)


def _consts():
    p = np.arange(128)[:, None]
    j = np.arange(128)[None, :]
    cm = np.zeros((128, NCM, 128), np.float32)
    cm[:, CI] = (p == j)
    cm[:, CO] = 1.0
    prot = np.zeros((128, 128), np.float32)
    prot[(j[0, :64] + 64), j[0, :64]] = -1.0
    prot[(j[0, 64:] - 64), j[0, 64:]] = 1.0
    cm[:, CP] = prot
    cm[:, CTU] = (p <= j)
    cm[:, CNSU] = -1.0 * (p < j)
    cm[:, CNSL] = -1.0 * (p > j)
    bd32 = (p // 32 == j // 32)
    bd64 = (p // 64 == j // 64)
    cm[:, CB32] = bd32
    cm[:, CO32] = bd64 & ~bd32
    cm[:, CO64] = ~bd64
    selm = np.zeros((8, 8, 128), np.float32)
    for k in range(8):
        selm[k, k, :] = 1.0
    rmask = np.ones((128, S), np.float32)
    rmask[:, 0::128] = 0.0
    half = 64
    inv_freq = (10000.0 ** (-(np.arange(half, dtype=np.float32) / np.float32(half)))).astype(np.float32)
    invf = np.concatenate([inv_freq, inv_freq]).astype(np.float64) / (2.0 * math.pi)
    return cm.reshape(128, NCM * 128), selm.reshape(8, 8 * 128), rmask, invf.astype(np.float32)


def _fm_groups_local(hh):
    g = [(hh * 128, 128), (256 + hh * 128, 128), (1024, 16)]
    for base in (1040, 1552, 2064, 2576):
        g += [(base + 128 * (2 * hh + i), 128) for i in range(2)]
    g.append("dab")
    g += [(3096 + 128 * (2 * hh + i), 128) for i in range(2)]
    for base in (3608, 4632, 6680):
        g += [(base + 256 * (2 * hh + lh) + 128 * m, 128) for lh in range(2) for m in range(2)]
    return g


def _tm_groups_local(hh):
    return [(512 + hh * 256, 256), (5656 + 256 * (2 * hh), 256), (5656 + 256 * (2 * hh + 1), 256)]


OUT_CHUNK_ORDER = [0, 1, 4, 5, 2, 3, 6, 7, 8, 9, 12, 13, 10, 11, 14, 15]


def _prep_shared(inp):
    f = lambda k: np.asarray(inp[k], np.float32)
    cm, selm, rmask, invf = _consts()
    w_in = f("w_in")
    per_half = []
    for hh in range(2):
        winfm = np.zeros((2, 26, 128, 16, 128), np.float32)
        wintm = np.zeros((2, 3, 128, 16, 256), np.float32)
        for l in range(2):
            wl = w_in[l].reshape(16, 128, -1)
            for gi, grp in enumerate(_fm_groups_local(hh)):
                if grp == "dab":
                    for i in range(2):
                        winfm[l, gi, :, :, i] = wl[:, :, 3088 + 2 * hh + i].T
                        winfm[l, gi, :, :, 4 + i] = wl[:, :, 3092 + 2 * hh + i].T
                else:
                    c0, n = grp
                    winfm[l, gi, :, :, :n] = wl[:, :, c0:c0 + n].transpose(1, 0, 2)
            for gi, (c0, n) in enumerate(_tm_groups_local(hh)):
                wintm[l, gi] = wl[:, :, c0:c0 + n].transpose(1, 0, 2)
        sm = np.zeros((128, NS), np.float32)
        sm[:, 16] = invf
        for l in range(2):
            b = SBASE + l * SLW
            sm[:, b:b + 16] = _col(f("norm_w")[l])
            sm[:, b + 16:b + 48] = _col(f("b_ada")[l, :2 * D])
            sm[:, b + 48] = f("gla_b_lr")[l, hh * 128:(hh + 1) * 128]
            sm[:, b + 50] = f("gla_norm_w")[l]
            sm[:, b + 51] = f("gdn_norm_w")[l]
            sm[:, b + 52] = f("diff_q_norm_w")[l]
            sm[:, b + 53] = f("diff_k_norm_w")[l]
            sm[:, b + 54:b + 56] = _col(f("diff_norm_w")[l])
            sm[:, b + 56:b + 58] = f("gdn_a_log")[l][None, 2 * hh:2 * hh + 2]
            sm[:, b + 60:b + 62] = f("gdn_dt_bias")[l][None, 2 * hh:2 * hh + 2]
            cw = f("gdn_conv_w")[l].reshape(4, 12, 128)
            for which in range(3):
                for lh in range(2):
                    ti = which * 4 + lh
                    sm[:, b + 64 + ti * 4:b + 64 + ti * 4 + 4] = cw[:, which * 4 + 2 * hh + lh, :].T
            sm[:, b + 112:b + 624] = f("diff_lambda")[l].reshape(1, 512)
            sm[0:16, b + 624:b + 752] = f("gla_w_lr")[l][:, hh * 128:(hh + 1) * 128]
        per_half.append({"winfm": winfm.reshape(2, 26, 128, 16 * 128), "wintm": wintm.reshape(2, 3, 128, 16 * 256), "small": sm})
    wada = np.ascontiguousarray(f("w_ada").reshape(2, 16, 128, 48, 128).transpose(0, 3, 2, 1, 4)).reshape(2, 48, 128, 16 * 128)
    wo = f("w_out").reshape(2, 16, 128, D)[:, OUT_CHUNK_ORDER]
    wout = np.ascontiguousarray(wo.transpose(0, 2, 1, 3)).reshape(2, 128, 16 * D)
    bgate = np.ascontiguousarray(f("b_ada")[:, 2 * D:].reshape(2, 1, D))
    shared = {"cmat": cm, "selm": selm, "rmask": rmask, "wada": wada, "bgate": bgate, "wout": wout}
    return shared, per_half


def make_in_maps(inp, cores):
    shared, per_half = _prep_shared(inp)
    x = np.asarray(inp["x"], np.float32)
    c = np.asarray(inp["c"], np.float32)
    pos = np.asarray(inp["positions"], np.int32)
    maps = []
    for b, hh in cores:
        s = per_half[hh]["small"].copy()
        s[:, 0:16] = _col(c[b])
        m = dict(shared)
        m["winfm"] = per_half[hh]["winfm"]
        m["wintm"] = per_half[hh]["wintm"]
        m["x"] = np.ascontiguousarray(x[b])
        m["small"] = s
        m["pos"] = np.ascontiguousarray(pos[b:b + 1])
        maps.append(m)
    return maps


_NC_CACHE = {}


def kernel(**inputs):
    if "nc" not in _NC_CACHE:
        _NC_CACHE["nc"] = build(2)
    nc = _NC_CACHE["nc"]
    cores = [(i // 2, i % 2) for i in range(8)]
    maps = make_in_maps(inputs, cores)
    res = run_bass_kernel_spmd(nc, maps, core_ids=list(range(8)))
    out = np.stack([np.asarray(res.results[2 * b]["out"], np.float32) for b in range(4)], axis=0)
    return out
```

```python
import math
import numpy as np
import concourse.bass as bass
import concourse.mybir as mybir
from concourse.bass_utils import run_bass_kernel_spmd

F32 = mybir.dt.float32
BF16 = mybir.dt.bfloat16
I32 = mybir.dt.int32
AF = mybir.ActivationFunctionType
ALU = mybir.AluOpType
AX = mybir.AxisListType

S = 2048
D = 2048
NT = 16
EPS = 1e-6
SLW = 880
SBASE = 32
NS = SBASE + 2 * SLW
CI, CO, CP, CTU, CNSU, CNSL, CB32, CO32, CO64 = range(9)
NCM = 9


class Reg:
    __slots__ = ("name", "lw", "rd", "dsem", "local", "psum")

    def __init__(self, name, local=False, psum=False):
        self.name = name
        self.psum = psum
        self.lw = None
        self.rd = {}
        self.dsem = None
        self.local = local


class Prog:
    def __init__(self, nc):
        self.nc = nc
        self.eng = {"pe": nc.tensor, "act": nc.scalar, "dve": nc.vector, "pool": nc.gpsimd, "sp": nc.sync}
        self.sem, self.cnt, self.semobj = {}, {}, {}
        self.seen = {e: {} for e in self.eng}
        for e in self.eng:
            s = nc.alloc_semaphore(name="s_" + e)
            self.sem[e] = s
            self.semobj[e] = s
            self.cnt[e] = 0
        self.vc = {}
        self.dcnt = {}
        self.local_keys = []
        self.local_next = 0
        self.ninstr = 0
        self.nwait = 0

    def _wait(self, e, key, val):
        if self.seen[e].get(key, 0) >= val:
            return
        self.eng[e].wait_ge(self.semobj[key], val)
        self.nwait += 1
        se = self.seen[e]
        se[key] = val
        clk = self.vc.get((key, val))
        if clk:
            for k, v in clk.items():
                if se.get(k, 0) < v:
                    se[k] = v

    def _deps(self, e, reads, writes, pe_accum=False, nowaw=False):
        need = {}

        def add(k, v):
            if need.get(k, 0) < v:
                need[k] = v
        for r in reads:
            if r.lw is not None:
                add(*r.lw)
            if r.psum:
                for k, v in r.rd.items():
                    if k != e:
                        add(k, v)
        for w in writes:
            if w.lw is not None and not nowaw and not (pe_accum and w.lw[0] == "pe"):
                add(*w.lw)
            for k, v in w.rd.items():
                add(k, v)
        for k, v in sorted(need.items(), key=lambda kv: -kv[1] if isinstance(kv[0], str) else 0):
            self._wait(e, k, v)

    def _record(self, ev, reads, writes, nowaw=False):
        for r in reads:
            if r.rd.get(ev[0], 0) < ev[1]:
                r.rd[ev[0]] = ev[1]
        for w in writes:
            w.lw = ev
            if not nowaw:
                w.rd = {}

    def op(self, e, fn, reads=(), writes=(), pe_accum=False):
        self._deps(e, reads, writes, pe_accum)
        ins = fn(self.eng[e])
        self.cnt[e] += 1
        ins.then_inc(self.sem[e], 1)
        ev = (e, self.cnt[e])
        self.vc[ev] = dict(self.seen[e])
        self._record(ev, reads, writes)
        self.ninstr += 1

    def _newkey(self):
        key = ("d", len(self.dcnt))
        self.semobj[key] = self.nc.alloc_semaphore(name="d_%d" % len(self.dcnt))
        self.dcnt[key] = 0
        return key

    def _dkey(self, w):
        if w.dsem is None:
            if w.local:
                if self.local_next == len(self.local_keys):
                    self.local_keys.append(self._newkey())
                w.dsem = self.local_keys[self.local_next]
                self.local_next += 1
            else:
                w.dsem = self._newkey()
        return w.dsem

    def dma(self, q, out_ap, in_ap, reads=(), writes=(), nowaw=False, **kw):
        w = writes[0]
        self._deps(q, reads, writes, nowaw=nowaw)
        key = self._dkey(w)
        ins = self.eng[q].dma_start(out=out_ap, in_=in_ap, **kw)
        self.dcnt[key] += 16
        ins.then_inc(self.semobj[key], 16)
        ev = (key, self.dcnt[key])
        self.vc[ev] = dict(self.seen[q])
        self._record(ev, reads, writes, nowaw=nowaw)
        self.ninstr += 1

    def collective(self, kind, groups, in_ap, out_ap, reads, writes):
        self._deps("pool", reads, writes)
        key = ("c", len([k for k in self.dcnt if k[0] == "c"]))
        self.semobj[key] = self.nc.alloc_semaphore(name="c_%d" % key[1])
        ins = self.eng["pool"].collective_compute(kind, ALU.bypass, replica_groups=groups, ins=[in_ap], outs=[out_ap])
        ins.then_inc(self.semobj[key])
        self.dcnt[key] = 1
        ev = (key, 1)
        self.vc[ev] = dict(self.seen["pool"])
        self._record(ev, reads, writes)
        self.ninstr += 1

    def barrier(self, reset=True, collectives=False):
        evs = [(e, self.cnt[e]) for e in self.eng if self.cnt[e] > 0]
        evs += [(k, v) for k, v in self.dcnt.items() if v > 0 and (collectives or k[0] != "c")]
        for e in self.eng:
            for k, v in evs:
                if k != e:
                    self._wait(e, k, v)
        if reset:
            self.local_next = 0


class Arena:
    def __init__(self, nc, nwords):
        self.t = nc.alloc_sbuf_tensor("arena", [128, nwords], F32)
        self.n = nwords
        self.off = 0
        self.k = 0

    def reset(self):
        self.off = 0

    def alloc(self, shape, dt, name=None):
        free = int(np.prod(shape[1:]))
        words = free if dt in (F32, I32) else (free + 1) // 2
        words = (words + 7) // 8 * 8
        assert self.off + words <= self.n, ("arena overflow", name, self.off, words, self.n)
        v = self.t[:, self.off:self.off + words]
        self.off += words
        if dt == BF16:
            v = v.bitcast(BF16)[:, 0:free]
        elif dt == I32:
            v = v.bitcast(I32)[:, 0:free]
        else:
            v = v[:, 0:free]
        if len(shape) == 3:
            v = v.rearrange("p (a b) -> p a b", b=shape[2])
        self.k += 1
        if shape[0] < 128:
            v = v[0:shape[0]]
        return v, Reg(name or "a%d" % self.k, local=True)


def build(nlayers=2, debug=False, stop=None, scopes=False, ncores=8):
    nc = bass.Bass("TRN2", target_bir_lowering=False)
    P = Prog(nc)
    L = nlayers

    def din(name, shape, dt=F32):
        return nc.dram_tensor(name, shape, dt, kind="ExternalInput").ap()

    x_d = din("x", [S, D])
    small_d = din("small", [128, NS])
    cmat_d = din("cmat", [128, NCM * 128])
    selm_d = din("selm", [8, 8 * 128])
    rmask_d = din("rmask", [128, S])
    pos_d = din("pos", [1, S], I32)
    wada_d = din("wada", [2, 48, 128, 16 * 128])
    bgate_d = din("bgate", [2, 1, D])
    winfm_d = din("winfm", [2, 26, 128, 16 * 128])
    wintm_d = din("wintm", [2, 3, 128, 16 * 256])
    wout_d = din("wout", [2, 128, 16 * D])
    out_d = nc.dram_tensor("out", [S, D], F32, kind="ExternalOutput").ap()
    x1_d = nc.dram_tensor("x1s", [S, D], F32, kind="Internal").ap()
    OT_ROWS = (4, 2, 2)
    oTl_t = [nc.dram_tensor("oTl%d" % i, [n_ * 128, S], BF16, kind="Internal") for i, n_ in enumerate(OT_ROWS)]
    oTg_t = [nc.dram_tensor("oTg%d" % i, [2 * n_ * 128, S], BF16, kind="Internal") for i, n_ in enumerate(OT_ROWS)]
    oTl_d = [t.ap() for t in oTl_t]
    oTg_d = [t.ap() for t in oTg_t]
    R_oTl = [Reg("oTl%d" % i) for i in range(3)]
    R_oTg = [Reg("oTg%d" % i) for i in range(3)]
    PAIRS = [[2 * i, 2 * i + 1] for i in range(ncores // 2)]
    R_out, R_x1 = Reg("out"), Reg("x1")

    def oT_dst(fc, sl):
        ti, r = (0, fc) if fc < 4 else (1 + (fc - 4) // 2, (fc - 4) % 2)
        return oTl_d[ti][r * 128:(r + 1) * 128, sl], R_oTl[ti]

    def sb(name, shape, dt):
        return nc.alloc_sbuf_tensor("sb_" + name, shape, dt), Reg(name)

    big, R_big = sb("big", [128, 16, S], BF16)
    cosT, R_cos = sb("cosT", [128, S], BF16)
    sinT, R_sin = sb("sinT", [128, S], BF16)
    gate_bc, R_gate = sb("gate_bc", [128, D], F32)
    rmask, R_rmask = sb("rmask", [128, S], BF16)
    cm_f, R_cmf = sb("cm_f", [128, NCM, 128], F32)
    cm_b, R_cmb = sb("cm_b", [128, NCM, 128], BF16)
    selm, R_selm = sb("selm", [8, 8, 128], F32)
    small, R_small = sb("small", [128, NS], F32)
    modc, R_modc = sb("modc", [128, 48], F32)
    misc, R_misc = sb("misc", [128, 64], F32)
    wpre, R_wpre = sb("wpre", [128, 16, 128], BF16)
    ar = Arena(nc, (nc.sbuf_bytes_remaining - 2048) // 4)

    psall = nc.alloc_psum_tensor("psall", [128, 8 * 512], F32)
    pss = [psall[:, i * 512:(i + 1) * 512] for i in range(8)]
    R_ps = [Reg("ps%d" % i, psum=True) for i in range(8)]

    state = {"alt": 0}

    def alt():
        state["alt"] ^= 1
        return "act" if state["alt"] else "dve"

    def copy_on(e, out, in_, reads, writes):
        if e == "act":
            P.op("act", lambda g: g.activation(out=out, in_=in_, func=AF.Copy), reads=reads, writes=writes)
        else:
            P.op(e, lambda g: g.tensor_copy(out, in_), reads=reads, writes=writes)

    def mm(out, lhsT, rhs, start, stop, reads, w):
        P.op("pe", lambda g: g.matmul(out, lhsT=lhsT, rhs=rhs, start=start, stop=stop),
             reads=reads, writes=[w], pe_accum=not start)

    def new_phase(name="ph"):
        P.barrier()
        ar.reset()
        if scopes:
            if state.get("scope") is not None:
                nc.leave_named_scope(state["scope"][0], state["scope"][1], False)
            nm = "%s_%d" % (name, state.setdefault("nscope", 0))
            state["nscope"] += 1
            sid, _ = nc.enter_named_scope(nm, False)
            state["scope"] = (nm, sid)

    ident_b = cm_b[:, CI, :]
    ones_b = cm_b[:, CO, :]
    ident_f = cm_f[:, CI, :]

    def rstd_part(srcs, n, denom, tmps, psi):
        (sq, R_sq), (t1, R_t1), (rinv, R_rinv) = tmps
        for i, (sap, sreg) in enumerate(srcs):
            P.op("act", lambda g: g.activation(out=sq[:, i, 0:n], in_=sap, func=AF.Square), reads=[sreg], writes=[R_sq])
        for i in range(len(srcs)):
            mm(pss[psi][:, 0:n], ones_b, sq[:, i, 0:n], i == 0, i == len(srcs) - 1, [R_sq, R_cmb], R_ps[psi])
        P.op("act", lambda g: g.activation(out=t1[:, 0:n], in_=pss[psi][:, 0:n], func=AF.Ln, scale=1.0 / denom, bias=eps_ap),
             reads=[R_ps[psi], R_misc], writes=[R_t1])
        P.op("act", lambda g: g.activation(out=rinv[:, 0:n], in_=t1[:, 0:n], func=AF.Exp, scale=-0.5), reads=[R_t1], writes=[R_rinv])
        return rinv, R_rinv

    P.dma("sp", small[:], small_d, writes=[R_small])
    P.dma("sp", cm_f[:], cmat_d.rearrange("p (a b) -> p a b", b=128), writes=[R_cmf])
    P.dma("sp", selm[:], selm_d.rearrange("p (a b) -> p a b", b=128), writes=[R_selm])
    P.dma("pool", rmask[:], rmask_d, writes=[R_rmask])
    P.op("dve", lambda g: g.tensor_copy(cm_b[:], cm_f[:]), reads=[R_cmf], writes=[R_cmb])
    eps_ap = misc[:, 0:1]
    P.op("dve", lambda g: g.memset(misc[:], 0.0), writes=[R_misc])
    P.op("dve", lambda g: g.memset(misc[:, 0:1], EPS), reads=[], writes=[R_misc])
    P.op("dve", lambda g: g.memset(misc[:, 1:2], 1.0), reads=[], writes=[R_misc])
    one_ap = misc[:, 1:2]

    posi, R_posi = ar.alloc([128, S], I32, "posi")
    y, R_y = ar.alloc([128, S], F32, "y")
    yi, R_yi = ar.alloc([128, S], I32, "yi")
    yf, R_yf = ar.alloc([128, S], F32, "yf")
    fr, R_fr = ar.alloc([128, S], F32, "fr")
    m1, R_m1 = ar.alloc([128, S], F32, "m1")
    P.dma("sp", posi, pos_d.to_broadcast([128, S]), writes=[R_posi])
    P.op("dve", lambda g: g.tensor_copy(y, posi), reads=[R_posi], writes=[R_y])
    P.op("dve", lambda g: g.tensor_scalar(y, y, small[:, 16:17], None, ALU.mult), reads=[R_y, R_small], writes=[R_y])
    P.op("dve", lambda g: g.tensor_copy(yi, y), reads=[R_y], writes=[R_yi])
    P.op("dve", lambda g: g.tensor_copy(yf, yi), reads=[R_yi], writes=[R_yf])
    P.op("dve", lambda g: g.tensor_tensor(fr, y, yf, ALU.subtract), reads=[R_y, R_yf], writes=[R_fr])
    for which, dst, R_dst in ((0, sinT, R_sin), (1, cosT, R_cos)):
        src = fr
        if which == 1:
            P.op("dve", lambda g: g.tensor_scalar(y, fr, 0.25, None, ALU.add), reads=[R_fr], writes=[R_y])
            src = y
        R_src = R_fr if which == 0 else R_y
        P.op("dve", lambda g: g.tensor_scalar(m1, src, 0.5, None, ALU.is_gt), reads=[R_src], writes=[R_m1])
        P.op("dve", lambda g: g.tensor_tensor(yf, src, m1, ALU.subtract), reads=[R_src, R_m1], writes=[R_yf])
        P.op("dve", lambda g: g.tensor_scalar(m1, yf, -0.5, None, ALU.is_lt), reads=[R_yf], writes=[R_m1])
        P.op("dve", lambda g: g.tensor_tensor(yf, yf, m1, ALU.add), reads=[R_yf, R_m1], writes=[R_yf])
        P.op("act", lambda g: g.activation(out=dst[:], in_=yf, func=AF.Sin, scale=2.0 * math.pi), reads=[R_yf], writes=[R_dst])

    if stop == "p0":
        P.barrier(); return nc
    def proj_fm(l, gi, M, wbufs, consume, bankset):
        if state.get("pre") == (l, gi):
            wb, R_wb = wpre, R_wpre
            state["pre"] = None
        else:
            wb, R_wb = wbufs[state.setdefault("wfm_i", 0) % len(wbufs)]
            state["wfm_i"] += 1
            P.dma("pool", wb.rearrange("p a b -> p (a b)"), winfm_d[l, gi], writes=[R_wb])
        banks = [bankset * 4 + i for i in range(4)]
        for kc in range(16):
            for tb in range(4):
                mm(pss[banks[tb]][0:M, :], wb[:, kc, 0:M], big[:, kc, tb * 512:(tb + 1) * 512], kc == 0, kc == 15,
                   [R_wb, R_big], R_ps[banks[tb]])
        for tb in range(4):
            consume(tb, pss[banks[tb]][0:M, :], R_ps[banks[tb]])

    def proj_tm(l, gi, wtb, consume, banks):
        wb, R_wb = wtb
        P.dma("pool", wb.rearrange("p a b -> p (a b)"), wintm_d[l, gi], writes=[R_wb])
        for t in range(NT):
            bk = banks[t % len(banks)]
            for kc in range(16):
                mm(pss[bk][:, 0:256], big[:, kc, t * 128:(t + 1) * 128], wb[:, kc, :], kc == 0, kc == 15,
                   [R_wb, R_big], R_ps[bk])
            consume(t, pss[bk][:, 0:256], R_ps[bk])

    def prefetch(l, gi):
        P.dma("pool", wpre.rearrange("p a b -> p (a b)"), winfm_d[l, gi], writes=[R_wpre])
        state["pre"] = (l, gi)

    def post_norm_store(l, o_acc, R_oacc, nsub, wcols, szs, R_sz, fc0, tmps, denom, psi):
        (sq, R_sq), (t1, R_t1), (rinv, R_rinv), (u, R_u) = tmps[0:4]
        for blk in range(4):
            sl = slice(blk * 512, (blk + 1) * 512)
            rstd_part([(o_acc[:, j, sl], R_oacc) for j in range(nsub)], 512, denom, tmps[0:3], psi)
            for j in range(nsub):
                P.op("dve", lambda g: g.scalar_tensor_tensor(out=u[:, 0:512], in0=o_acc[:, j, sl], scalar=wcols[j], in1=rinv[:, 0:512],
                                                             op0=ALU.mult, op1=ALU.mult),
                     reads=[R_oacc, R_rinv, R_small, R_misc], writes=[R_u])
                ost, R_ost = tmps[4 + state.setdefault("ost_i", 0) % 2]
                state["ost_i"] += 1
                P.op("pool", lambda g: g.tensor_tensor(ost[:, 0:512], u[:, 0:512], szs[:, j, sl], ALU.mult),
                     reads=[R_u, R_sz], writes=[R_ost])
                dst_ap, R_d = oT_dst(fc0 + j, sl)
                P.dma("sp", dst_ap, ost[:, 0:512], reads=[R_ost], writes=[R_d], nowaw=True)

    for l in range(L):
        sb_l = SBASE + l * SLW
        lam_init = 0.8 - 0.6 * math.exp(-0.3 * l)

        def sc(off, n=1):
            return small[:, sb_l + off: sb_l + off + n]

        new_phase("A")
        cact, R_cact = ar.alloc([128, 16], F32, "cact")
        c2, R_c2 = ar.alloc([128, 16, 2], BF16, "c2")
        crep, R_crep = ar.alloc([128, 16, 128], BF16, "crep")
        bg, R_bg = ar.alloc([128, D], F32, "bg")
        wab = [ar.alloc([128, 16, 128], F32, "wa%d" % i) for i in range(4)]
        wbb = [ar.alloc([128, 16, 128], BF16, "wb%d" % i) for i in range(3)]
        P.op("act", lambda g: g.activation(out=cact, in_=small[:, 0:16], func=AF.Silu), reads=[R_small], writes=[R_cact])
        P.op("dve", lambda g: g.tensor_copy(c2, cact.unsqueeze(2).to_broadcast([128, 16, 2])), reads=[R_cact], writes=[R_c2])
        P.op("dve", lambda g: g.tensor_copy(crep, cact.unsqueeze(2).to_broadcast([128, 16, 128])), reads=[R_cact], writes=[R_crep])
        P.dma("sp", bg, bgate_d[l].to_broadcast([128, D]), writes=[R_bg])
        for g_ in range(48):
            wf, R_wf = wab[g_ % 4]
            P.dma("sp", wf.rearrange("p a b -> p (a b)"), wada_d[l, g_], writes=[R_wf])
            wa, R_wa = wbb[g_ % 3]
            copy_on(("act", "dve")[g_ % 2], wa.rearrange("p a b -> p (a b)"), wf.rearrange("p a b -> p (a b)"), [R_wf], [R_wa])
            if g_ < 32:
                for kc in range(16):
                    mm(pss[0][:, 2 * g_:2 * g_ + 2], wa[:, kc, :], c2[:, kc, :], kc == 0, kc == 15, [R_wa, R_c2], R_ps[0])
                if g_ == 31:
                    P.op("dve", lambda g: g.tensor_tensor(modc[:, 0:32], pss[0][:, 0:64:2], sc(16, 32), ALU.add),
                         reads=[R_ps[0], R_small], writes=[R_modc])
                    P.op("dve", lambda g: g.scalar_tensor_tensor(out=modc[:, 32:48], in0=modc[:, 16:32], scalar=1.0, in1=sc(0, 16),
                                                                 op0=ALU.add, op1=ALU.mult),
                         reads=[R_modc, R_small], writes=[R_modc])
            else:
                gg = g_ - 32
                bk = 1 + (gg // 4) % 2
                c0 = (gg % 4) * 128
                for kc in range(16):
                    mm(pss[bk][:, c0:c0 + 128], crep[:, kc, :], wa[:, kc, :], kc == 0, kc == 15, [R_wa, R_crep], R_ps[bk])
                if gg % 4 == 3:
                    sl = slice((gg // 4) * 512, (gg // 4 + 1) * 512)
                    P.op("dve", lambda g: g.tensor_tensor(gate_bc[:, sl], pss[bk][:], bg[:, sl], ALU.add),
                         reads=[R_ps[bk], R_bg], writes=[R_gate])

        if stop == "pA":
            P.barrier(); return nc
        new_phase("B")
        xsrc = x_d if l == 0 else x1_d
        xts = [ar.alloc([128, D], F32, "xt%d" % i) for i in range(2)]
        xn, R_xn = ar.alloc([128, 4, D], BF16, "xn")
        junk, R_junk = ar.alloc([128, D], BF16, "junk")
        ssq, R_ssq = ar.alloc([128, 8], F32, "ssq")
        hT = big
        for tb in range(4):
            for i in range(4):
                t = tb * 4 + i
                xt, R_xt = xts[t % 2]
                P.dma("sp", xt, xsrc[t * 128:(t + 1) * 128, :], reads=[R_x1] if l > 0 else [], writes=[R_xt])
                P.op("act", lambda g: g.activation(out=junk, in_=xt, func=AF.Square, accum_out=ssq[:, 0:1]),
                     reads=[R_xt], writes=[R_junk, R_ssq])
                P.op("act", lambda g: g.activation(out=ssq[:, 1:2], in_=ssq[:, 0:1], func=AF.Sqrt, scale=1.0 / D, bias=eps_ap),
                     reads=[R_ssq, R_misc], writes=[R_ssq])
                P.op("dve", lambda g: g.reciprocal(ssq[:, 2:3], ssq[:, 1:2]), reads=[R_ssq], writes=[R_ssq])
                P.op("dve", lambda g: g.tensor_scalar(xn[:, i, :], xt, ssq[:, 2:3], None, ALU.mult),
                     reads=[R_xt, R_ssq], writes=[R_xn])
            for fc in range(16):
                bk = fc % 4
                pT = pss[bk][:].bitcast(BF16)
                for i in range(4):
                    P.op("pe", lambda g: g.transpose(pT[:, i * 128:(i + 1) * 128], xn[:, i, fc * 128:(fc + 1) * 128], ident_b),
                         reads=[R_xn, R_cmb], writes=[R_ps[bk]], pe_accum=i > 0)
                dst = hT[:, fc, tb * 512:(tb + 1) * 512]
                if alt() == "act":
                    P.op("act", lambda g: g.activation(out=dst, in_=pT[:, 0:512], func=AF.Identity,
                                                       scale=modc[:, 32 + fc:33 + fc], bias=modc[:, fc:fc + 1]),
                         reads=[R_ps[bk], R_modc], writes=[R_big])
                else:
                    P.op("dve", lambda g: g.tensor_scalar(dst, pT[:, 0:512], modc[:, 32 + fc:33 + fc], modc[:, fc:fc + 1],
                                                          ALU.mult, ALU.add),
                         reads=[R_ps[bk], R_modc], writes=[R_big])

        prefetch(l, 2)
        if stop == "pB":
            P.barrier(); return nc
        for pr in range(1):
            new_phase("gla")
            wfm = [ar.alloc([128, 16, 128], BF16, "wfm%d" % i) for i in range(2)]
            wtm = ar.alloc([128, 16, 256], BF16, "wtm")
            glrT, R_glrT = ar.alloc([16, S], BF16, "glrT")
            wlr, R_wlr = ar.alloc([16, 256], BF16, "wlr")
            bcs, R_bcs = ar.alloc([128, S], F32, "bcs")
            eb, R_eb = ar.alloc([128, S], F32, "eb")
            q_eT, R_qe = ar.alloc([128, S], BF16, "q_eT")
            k_eT, R_ke = ar.alloc([128, S], BF16, "k_eT")
            v_g, R_vg = ar.alloc([128, NT, 256], BF16, "v_g")
            sz, R_sz = ar.alloc([128, 2, S], BF16, "sz")
            ke_tok, R_ket = ar.alloc([128, NT, 128], BF16, "ke_tok")
            o_acc, R_oacc = ar.alloc([128, 2, S], F32, "o_acc")
            e1, R_e1 = ar.alloc([128, 512], F32, "e1")
            dec, R_dec = ar.alloc([128, 16], F32, "dec")
            nb, R_nb = ar.alloc([128, 2], F32, "nb")
            Sp, R_Sp = ar.alloc([128, 256], F32, "Sp")
            Stmp, R_Stmp = ar.alloc([128, 256], F32, "Stmp")
            Sbf, R_Sbf = ar.alloc([128, 256], BF16, "Sbf")
            attm, R_attm = ar.alloc([128, 2, 128], BF16, "attm")
            tmps = [ar.alloc([128, 2, 512], BF16, "sq"), ar.alloc([128, 512], F32, "t1"), ar.alloc([128, 512], F32, "rinv"),
                    ar.alloc([128, 512], F32, "u"), ar.alloc([128, 512], BF16, "ost"), ar.alloc([128, 512], BF16, "ost2")]

            def c_glr(tb, ps, R):
                copy_on("act", glrT[0:16, tb * 512:(tb + 1) * 512], ps, [R], [R_glrT])
            proj_fm(l, 2, 16, wfm, c_glr, 0)
            P.op("dve", lambda g: g.tensor_copy(wlr, sc(624, 256)[0:16, :]), reads=[R_small], writes=[R_wlr])
            P.op("dve", lambda g: g.tensor_scalar(nb, sc(48, 2), -1.0, None, ALU.mult), reads=[R_small], writes=[R_nb])
            for blk in range(4):
                sl = slice(blk * 512, (blk + 1) * 512)
                bk = 4 + blk % 2
                mm(pss[bk][:], wlr[0:16, pr * 128:(pr + 1) * 128], glrT[0:16, sl], True, True, [R_wlr, R_glrT], R_ps[bk])
                P.op("act", lambda g: g.activation(out=e1, in_=pss[bk][:], func=AF.Exp, scale=-1.0, bias=nb[:, pr:pr + 1]),
                     reads=[R_ps[bk], R_nb], writes=[R_e1])
                P.op("act", lambda g: g.activation(out=bcs[:, sl], in_=e1, func=AF.Ln, bias=one_ap), reads=[R_e1, R_misc], writes=[R_bcs])
            P.op("dve", lambda g: g.tensor_tensor_scan(out=bcs, data0=rmask[:], data1=bcs, initial=0.0, op0=ALU.mult, op1=ALU.add),
                 reads=[R_rmask, R_bcs], writes=[R_bcs])
            P.op("act", lambda g: g.activation(out=eb, in_=bcs, func=AF.Exp, scale=-1.0 / 16.0), reads=[R_bcs], writes=[R_eb])
            P.op("dve", lambda g: g.tensor_copy(dec, eb[:, 127:S:128]), reads=[R_eb], writes=[R_dec])
            P.op("act", lambda g: g.activation(out=bcs, in_=bcs, func=AF.Exp, scale=1.0 / 16.0), reads=[R_bcs], writes=[R_bcs])
            enb = bcs

            def c_q(tb, ps, R):
                sl = slice(tb * 512, (tb + 1) * 512)
                P.op("dve", lambda g: g.scalar_tensor_tensor(out=q_eT[:, sl], in0=ps, scalar=0.125, in1=eb[:, sl], op0=ALU.mult, op1=ALU.mult),
                     reads=[R, R_eb], writes=[R_qe])
            proj_fm(l, 0, 128, wfm, c_q, 1)

            def c_k(tb, ps, R):
                sl = slice(tb * 512, (tb + 1) * 512)
                P.op("dve", lambda g: g.tensor_tensor(k_eT[:, sl], ps, enb[:, sl], ALU.mult), reads=[R, R_bcs], writes=[R_ke])
            proj_fm(l, 1, 128, wfm, c_k, 0)

            def c_v(t, ps, R):
                copy_on("act", v_g[:, t, :], ps, [R], [R_vg])
            proj_tm(l, 0, wtm, c_v, [4, 5])
            for hh in range(2):
                def c_z(tb, ps, R):
                    P.op("act", lambda g: g.activation(out=sz[:, hh, tb * 512:(tb + 1) * 512], in_=ps, func=AF.Silu), reads=[R], writes=[R_sz])
                proj_fm(l, 3 + hh, 128, wfm, c_z, hh)
            for t4 in range(4):
                bk = 6 + t4 % 2
                pT = pss[bk][:].bitcast(BF16)
                for i in range(4):
                    t = t4 * 4 + i
                    P.op("pe", lambda g: g.transpose(pT[:, i * 128:(i + 1) * 128], k_eT[:, t * 128:(t + 1) * 128], ident_b),
                         reads=[R_ke, R_cmb], writes=[R_ps[bk]], pe_accum=i > 0)
                P.op("dve", lambda g: g.tensor_copy(ke_tok[:, t4 * 4:(t4 + 1) * 4, :], pT[:, 0:512].rearrange("p (a b) -> p a b", b=128)),
                     reads=[R_ps[bk]], writes=[R_ket])
            P.op("dve", lambda g: g.memset(Sp, 0.0), writes=[R_Sp])
            for n in range(NT):
                ch = slice(n * 128, (n + 1) * 128)
                ba, bo, bkv = n % 2, 2 + n % 2, 4 + n % 2
                for hh in range(2):
                    hp = slice(64 * hh, 64 * hh + 64)
                    mm(pss[ba][:, hh * 128:(hh + 1) * 128], k_eT[hp, ch], q_eT[hp, ch], True, True, [R_ke, R_qe], R_ps[ba])
                P.op("dve", lambda g: g.tensor_tensor(attm, pss[ba][:, 0:256].rearrange("p (a b) -> p a b", b=128),
                                                      cm_f[:, CTU:CTU + 1, :].to_broadcast([128, 2, 128]), ALU.mult),
                     reads=[R_ps[ba], R_cmf], writes=[R_attm])
                for hh in range(2):
                    hp = slice(64 * hh, 64 * hh + 64)
                    vs = slice(hh * 128, (hh + 1) * 128)
                    mm(pss[bo][:, vs], v_g[:, n, vs], attm[:, hh, :], True, n == 0, [R_vg, R_attm], R_ps[bo])
                    if n > 0:
                        mm(pss[bo][:, vs], Sbf[hp, vs], q_eT[hp, ch], False, True, [R_Sbf, R_qe], R_ps[bo])
                P.op("act", lambda g: g.activation(out=o_acc[:, :, ch], in_=pss[bo][:, 0:256].rearrange("p (a b) -> p a b", b=128), func=AF.Copy),
                     reads=[R_ps[bo]], writes=[R_oacc])
                if n < NT - 1:
                    mm(pss[bkv][:, 0:256], ke_tok[:, n, :], v_g[:, n, :], True, True, [R_ket, R_vg], R_ps[bkv])
                    P.op("dve", lambda g: g.tensor_tensor(Stmp, Sp, pss[bkv][:, 0:256], ALU.add), reads=[R_Sp, R_ps[bkv]], writes=[R_Stmp])
                    P.op("dve", lambda g: g.tensor_scalar(Sp, Stmp, dec[:, n:n + 1], None, ALU.mult), reads=[R_Stmp, R_dec], writes=[R_Sp])
                    P.op("act", lambda g: g.activation(out=Sbf, in_=Stmp, func=AF.Copy, scale=dec[:, n:n + 1]),
                         reads=[R_Stmp, R_dec], writes=[R_Sbf])
            for hh in range(2):
                post_norm_store(l, o_acc[:, hh:hh + 1, :], R_oacc, 1, [sc(50)], sz[:, hh:hh + 1, :], R_sz, 2 * pr + hh, tmps, 128.0, 6)
            prefetch(l, 11)

        if stop == "pC1":
            P.barrier(); return nc
        for h in range(2):
            new_phase("gdn")
            wfm = [ar.alloc([128, 16, 128], BF16, "wfm%d" % i) for i in range(2)]
            xbf, R_xbf = ar.alloc([128, S + 8], BF16, "xbf")
            diag, R_diag = ar.alloc([128, 4, 128], BF16, "diag")
            cs, R_cs = ar.alloc([128, S], F32, "cs")
            knT, R_kn = ar.alloc([128, S], BF16, "knT")
            qnT, R_qn = ar.alloc([128, S], BF16, "qnT")
            cvT, R_cv = ar.alloc([128, S], BF16, "cvT")
            kbT, R_kb = ar.alloc([128, S], BF16, "kbT")
            q_eT, R_qe = ar.alloc([128, S], BF16, "q_eT")
            vb_tok, R_vb = ar.alloc([128, NT, 128], BF16, "vb_tok")
            kbg_tok, R_kbg = ar.alloc([128, NT, 128], BF16, "kbg_tok")
            kt_tok, R_kt = ar.alloc([128, NT, 128], BF16, "kt_tok")
            gc, R_gc = ar.alloc([128, S], F32, "gc")
            beta, R_beta = ar.alloc([128, S], BF16, "beta")
            eg, R_eg = ar.alloc([128, S], F32, "eg")
            tl, R_tl = ar.alloc([128, S], F32, "tl")
            dabT, R_dab = tl[0:8, :], R_tl
            cols, R_cols = ar.alloc([128, 6, 16], F32, "cols")
            nA, R_nA = ar.alloc([128, 4], F32, "nA")
            Sp, R_Sp = ar.alloc([128, 128], F32, "Sp")
            Sbf, R_Sbf = ar.alloc([128, 128], BF16, "Sbf")
            dm = [ar.alloc([128, 128], F32, "dm%d" % i) for i in range(4)]
            mb = [ar.alloc([128, 128], BF16, "mb%d" % i) for i in range(10)]
            u_sb, R_u = ar.alloc([128, 128], F32, "u_sb")
            tmps = [ar.alloc([128, 2, 512], BF16, "sq"), ar.alloc([128, 512], F32, "t1"), ar.alloc([128, 512], F32, "rinv"),
                    ar.alloc([128, 512], F32, "u"), ar.alloc([128, 512], BF16, "ost"), ar.alloc([128, 512], BF16, "ost2")]
            e1, R_e1 = tmps[3]
            o_acc, R_oacc = cs.rearrange("p (a b) -> p a b", a=1), R_cs

            def c_dab(tb, ps, R):
                copy_on("act", dabT[0:8, tb * 512:(tb + 1) * 512], ps, [R], [R_dab])
            proj_fm(l, 11, 8, wfm, c_dab, 0)
            P.op("dve", lambda g: g.memset(xbf[:, 0:3], 0.0), writes=[R_xbf])
            for which in range(3):
                ti = which * 4 + h
                for j in range(4):
                    P.op("dve", lambda g: g.tensor_scalar(diag[:, j, :], ident_f, sc(64 + ti * 4 + j), None, ALU.mult),
                         reads=[R_cmf, R_small], writes=[R_diag])

                def c_x(tb, ps, R):
                    copy_on(alt(), xbf[:, 3 + tb * 512:3 + (tb + 1) * 512], ps, [R], [R_xbf])
                proj_fm(l, 5 + 2 * which + h, 128, wfm, c_x, 1)
                for blk in range(4):
                    sl = slice(blk * 512, (blk + 1) * 512)
                    bk = blk % 2
                    for j in range(4):
                        mm(pss[bk][:], diag[:, j, :], xbf[:, blk * 512 + j: blk * 512 + j + 512], j == 0, j == 3, [R_diag, R_xbf], R_ps[bk])
                    if which == 2:
                        P.op("act", lambda g: g.activation(out=cvT[:, sl], in_=pss[bk][:], func=AF.Silu), reads=[R_ps[bk]], writes=[R_cv])
                    else:
                        P.op("act", lambda g: g.activation(out=cs[:, sl], in_=pss[bk][:], func=AF.Silu), reads=[R_ps[bk]], writes=[R_cs])
                if which < 2:
                    for blk in range(4):
                        sl = slice(blk * 512, (blk + 1) * 512)
                        rinv, R_rinv = rstd_part([(cs[:, sl], R_cs)], 512, 1.0, tmps[0:3], 2 + blk % 2)
                        if which == 0:
                            P.op("dve", lambda g: g.scalar_tensor_tensor(out=qnT[:, sl], in0=cs[:, sl], scalar=128.0 ** -0.5, in1=rinv[:, 0:512],
                                                                         op0=ALU.mult, op1=ALU.mult), reads=[R_cs, R_rinv], writes=[R_qn])
                        else:
                            P.op("dve", lambda g: g.tensor_tensor(knT[:, sl], cs[:, sl], rinv[:, 0:512], ALU.mult), reads=[R_cs, R_rinv], writes=[R_kn])
            if stop == "g1":
                P.barrier(); return nc
            P.op("act", lambda g: g.activation(out=nA, in_=sc(56, 4), func=AF.Exp), reads=[R_small], writes=[R_nA])
            P.op("dve", lambda g: g.tensor_scalar(nA, nA, -1.0, None, ALU.mult), reads=[R_nA], writes=[R_nA])
            for blk in range(4):
                sl = slice(blk * 512, (blk + 1) * 512)
                bk = 4 + blk % 2
                mm(pss[bk][:], selm[0:8, h, :], dabT[0:8, sl], True, True, [R_selm, R_dab], R_ps[bk])
                P.op("act", lambda g: g.activation(out=e1, in_=pss[bk][:], func=AF.Exp, bias=sc(60 + h)), reads=[R_ps[bk], R_small], writes=[R_e1])
                P.op("act", lambda g: g.activation(out=gc[:, sl], in_=e1, func=AF.Ln, bias=one_ap), reads=[R_e1, R_misc], writes=[R_gc])
                bk2 = 6 + blk % 2
                mm(pss[bk2][:], selm[0:8, 4 + h, :], dabT[0:8, sl], True, True, [R_selm, R_dab], R_ps[bk2])
                P.op("act", lambda g: g.activation(out=beta[:, sl], in_=pss[bk2][:], func=AF.Sigmoid), reads=[R_ps[bk2]], writes=[R_beta])
            P.op("dve", lambda g: g.tensor_tensor_scan(out=gc, data0=rmask[:], data1=gc, initial=0.0, op0=ALU.mult, op1=ALU.add),
                 reads=[R_rmask, R_gc], writes=[R_gc])
            P.op("dve", lambda g: g.tensor_scalar(gc, gc, nA[:, h:h + 1], None, ALU.mult), reads=[R_gc, R_nA], writes=[R_gc])
            gcl = cols[:, 0, :]
            cd = cols[:, 1, :]
            P.op("dve", lambda g: g.tensor_copy(gcl, gc[:, 127:S:128]), reads=[R_gc], writes=[R_cols])
            P.op("act", lambda g: g.activation(out=cd, in_=gcl, func=AF.Exp), reads=[R_cols], writes=[R_cols])
            P.op("act", lambda g: g.activation(out=eg, in_=gc, func=AF.Exp), reads=[R_gc], writes=[R_eg])
            P.op("dve", lambda g: g.tensor_tensor(q_eT, qnT, eg, ALU.mult), reads=[R_qn, R_eg], writes=[R_qe])
            P.op("dve", lambda g: g.tensor_tensor(kbT, knT, beta, ALU.mult), reads=[R_kn, R_beta], writes=[R_kb])
            P.op("pool", lambda g: g.tensor_tensor(eg, eg, beta, ALU.mult), reads=[R_eg, R_beta], writes=[R_eg])
            for n in range(NT):
                ch = slice(n * 128, (n + 1) * 128)
                P.op("act", lambda g: g.activation(out=tl[:, ch], in_=gc[:, ch], func=AF.Exp, scale=-1.0, bias=gcl[:, n:n + 1]),
                     reads=[R_gc, R_cols], writes=[R_tl])
            if stop == "g2":
                P.barrier(); return nc
            for qi, (src, R_src) in enumerate(((gc, R_gc), (beta, R_beta), (eg, R_eg), (tl, R_tl))):
                oh = cm_b[:, CI, 0:2] if src is beta else cm_f[:, CI, 0:2]
                for n in range(NT):
                    c0 = (qi * NT + n) * 2
                    mm(pss[3][:, c0:c0 + 2], src[:, n * 128:(n + 1) * 128], oh, True, True, [R_src, R_cmf, R_cmb], R_ps[3])
            P.op("dve", lambda g: g.tensor_copy(cols[:, 2:6, :], pss[3][:, 0:128:2].rearrange("p (a b) -> p a b", b=NT)),
                 reads=[R_ps[3]], writes=[R_cols])
            gc_col, beta_col, bexp_col, tail_col = (cols[:, i, :] for i in (2, 3, 4, 5))
            if stop == "g2b":
                P.barrier(); return nc
            for t in range(NT):
                bk = t % 2
                pT = pss[bk][:].bitcast(BF16)
                ts = slice(t * 128, (t + 1) * 128)
                P.op("pe", lambda g: g.transpose(pT[:, 0:128], knT[:, ts], ident_b), reads=[R_kn, R_cmb], writes=[R_ps[bk]])
                P.op("pe", lambda g: g.transpose(pT[:, 128:256], cvT[:, ts], ident_b), reads=[R_cv, R_cmb], writes=[R_ps[bk]], pe_accum=True)
                P.op("dve", lambda g: g.tensor_scalar(kbg_tok[:, t, :], pT[:, 0:128], bexp_col[:, t:t + 1], None, ALU.mult),
                     reads=[R_ps[bk], R_cols], writes=[R_kbg])
                P.op("dve", lambda g: g.tensor_scalar(kt_tok[:, t, :], pT[:, 0:128], tail_col[:, t:t + 1], None, ALU.mult),
                     reads=[R_ps[bk], R_cols], writes=[R_kt])
                P.op("dve", lambda g: g.tensor_scalar(vb_tok[:, t, :], pT[:, 128:256], beta_col[:, t:t + 1], None, ALU.mult),
                     reads=[R_ps[bk], R_cols], writes=[R_vb])

            if stop == "g3":
                P.barrier(); return nc
            P.barrier(reset=False)
            sz, R_sz = tl.bitcast(BF16)[:, 0:S].rearrange("p (a b) -> p a b", a=1), Reg("sz")
            mb2 = [(xbf[:, i * 128:(i + 1) * 128], Reg("mb2_%d" % i)) for i in range(16)]

            def c_z(tb, ps, R):
                P.op("act", lambda g: g.activation(out=sz[:, 0, tb * 512:(tb + 1) * 512], in_=ps, func=AF.Silu), reads=[R], writes=[R_sz])
            proj_fm(l, 12 + h, 128, wfm, c_z, 1)
            if stop == "g4":
                P.barrier(); return nc
            P.barrier(reset=False)
            G = 4
            eg4 = eg.rearrange("p (t g c) -> p t g c", g=G, c=128)
            (dA4, R_dA), (dB4, R_dB), (dD4, R_dD), (u4, R_u4) = [(eg4[:, i], Reg("f4_%d" % i)) for i in range(4)]
            pool16 = []
            for src in (beta, cvT, wfm[0][0].rearrange("p a b -> p (a b)"), wfm[1][0].rearrange("p a b -> p (a b)"),
                        tl.bitcast(BF16)[:, S:2 * S]):
                v4 = src.rearrange("p (t g c) -> p t g c", g=G, c=128)
                pool16 += [(v4[:, i], Reg("b4_%d" % len(pool16))) for i in range(4)]
            ((Pm4, R_P), (PT4, R_PT), (qk4, R_qk), (Pd4, R_Pd), (PTd4, R_PTd), (Po32, R_Po32), (PTo32, R_PTo32), (Po64, R_Po64),
             Abuf0, ATbuf0, Abuf1, ATbuf1, sq0, sqT0, sq1, sqT1, (U1, R_U1), (T1, R_T1), (wT4, R_wT)) = pool16[0:19]
            vnew, R_vn = mb[0]

            def bc4(blk, f32=True):
                src = cm_f if f32 else cm_b
                return src[:, blk:blk + 1, :].to_broadcast([128, G, 128])

            def flat(t4):
                return t4.rearrange("p g c -> p (g c)")

            def p4(bank):
                return pss[bank][:].rearrange("p (g c) -> p g c", c=128)

            P.op("dve", lambda g: g.memset(Sp, 0.0), writes=[R_Sp])
            P.op("dve", lambda g: g.memset(Sbf, 0.0), writes=[R_Sbf])
            xq = xbf[:, 0:2048]
            qk4s = [(qk4, R_qk), (xq[:, 0:512].rearrange("p (g c) -> p g c", c=128), Reg("qk4b"))]
            wT4s = [(wT4, R_wT), pool16[19]]
            u4s = [(u4, R_u4), (xq[:, 1024:2048].bitcast(F32).rearrange("p (g c) -> p g c", c=128), Reg("u4b"))]

            def mm4(bank, lhs4, R_l, rhs4, R_r, start=True):
                for g_ in range(G):
                    cs_ = slice(g_ * 128, (g_ + 1) * 128)
                    mm(pss[bank][:, cs_], lhs4[:, g_, :], rhs4[:, g_, :], start, start or g_ == G - 1, [R_l, R_r], R_ps[bank])

            def add_mm4(bank, base, lhs4, R_l, rhs4, R_r):
                mm(pss[bank][:], ident_b, flat(base[0]), True, False, [R_cmb, base[1]], R_ps[bank])
                mm4(bank, lhs4, R_l, rhs4, R_r, start=False)

            def solve_gen(bb):
                par = bb % 2
                qk4_, R_qk_ = qk4s[par]
                wT4_, R_wT_ = wT4s[par]
                u4_, R_u4_ = u4s[par]
                ns = [bb * G + g_ for g_ in range(G)]
                chs = [slice(n * 128, (n + 1) * 128) for n in ns]
                for g_, n in enumerate(ns):
                    gcc = gc_col[:, n:n + 1]
                    P.op("dve", lambda g: g.tensor_scalar(dA4[:, g_, :], gc[:, chs[g_]], gcc, 0.0, ALU.subtract, ALU.max),
                         reads=[R_gc, R_cols], writes=[R_dA])
                    P.op("dve", lambda g: g.tensor_scalar(dB4[:, g_, :], gc[:, chs[g_]], gcc, 0.0, ALU.subtract, ALU.min),
                         reads=[R_gc, R_cols], writes=[R_dB])
                    yield
                P.op("act", lambda g: g.activation(out=flat(dA4), in_=flat(dA4), func=AF.Exp, scale=-1.0), reads=[R_dA], writes=[R_dA])
                P.op("act", lambda g: g.activation(out=flat(dB4), in_=flat(dB4), func=AF.Exp), reads=[R_dB], writes=[R_dB])
                yield
                P.op("dve", lambda g: g.tensor_tensor(dA4, dA4, bc4(CNSL), ALU.mult), reads=[R_dA, R_cmf], writes=[R_dA])
                P.op("dve", lambda g: g.tensor_tensor(dD4, dB4, bc4(CNSU), ALU.mult), reads=[R_dB, R_cmf], writes=[R_dD])
                P.op("dve", lambda g: g.tensor_tensor(dB4, dB4, bc4(CTU), ALU.mult), reads=[R_dB, R_cmf], writes=[R_dB])
                yield
                for bank, (l4, R_l4), (r4, R_r4) in ((3, (kbT, R_kb), (knT, R_kn)), (4, (knT, R_kn), (kbT, R_kb)), (5, (knT, R_kn), (qnT, R_qn))):
                    for g_ in range(G):
                        cs_ = slice(g_ * 128, (g_ + 1) * 128)
                        mm(pss[bank][:, cs_], l4[:, chs[g_]], r4[:, chs[g_]], True, True, [R_l4, R_r4], R_ps[bank])
                yield
                P.op("dve", lambda g: g.tensor_tensor(Pm4, p4(3), dA4, ALU.mult), reads=[R_ps[3], R_dA], writes=[R_P])
                P.op("dve", lambda g: g.tensor_tensor(PT4, p4(4), dD4, ALU.mult), reads=[R_ps[4], R_dD], writes=[R_PT])
                P.op("dve", lambda g: g.tensor_tensor(qk4_, p4(5), dB4, ALU.mult), reads=[R_ps[5], R_dB], writes=[R_qk_])
                yield
                for dst, R_d, src, R_s, mk in ((Pd4, R_Pd, Pm4, R_P, CB32), (PTd4, R_PTd, PT4, R_PT, CB32), (Po32, R_Po32, Pm4, R_P, CO32),
                                               (PTo32, R_PTo32, PT4, R_PT, CO32), (Po64, R_Po64, Pm4, R_P, CO64)):
                    P.op("dve", lambda g: g.tensor_tensor(dst, src, bc4(mk, False), ALU.mult), reads=[R_s, R_cmb], writes=[R_d])
                    yield
                Acur, ATcur, Anxt, ATnxt = Abuf0, ATbuf0, Abuf1, ATbuf1
                P.op("dve", lambda g: g.tensor_tensor(Acur[0], Pd4, bc4(CI, False), ALU.add), reads=[R_Pd, R_cmb], writes=[Acur[1]])
                P.op("dve", lambda g: g.tensor_tensor(ATcur[0], PTd4, bc4(CI, False), ALU.add), reads=[R_PTd, R_cmb], writes=[ATcur[1]])
                yield
                cur = ((Pd4, R_Pd), (PTd4, R_PTd))
                sqb = [(sq0, sqT0), (sq1, sqT1)]
                for lev in range(4):
                    (cP, R_cP), (cPT, R_cPT) = cur
                    (nP, R_nP), (nPT, R_nPT) = sqb[lev % 2]
                    mm4(3, cPT, R_cPT, cP, R_cP)
                    mm4(4, cP, R_cP, cPT, R_cPT)
                    yield
                    copy_on("act", flat(nP), pss[3][:], [R_ps[3]], [R_nP])
                    copy_on("dve", flat(nPT), pss[4][:], [R_ps[4]], [R_nPT])
                    yield
                    add_mm4(5, Acur, nPT, R_nPT, Acur[0], Acur[1])
                    add_mm4(6, ATcur, nP, R_nP, ATcur[0], ATcur[1])
                    yield
                    copy_on("dve", flat(Anxt[0]), pss[5][:], [R_ps[5]], [Anxt[1]])
                    copy_on("act", flat(ATnxt[0]), pss[6][:], [R_ps[6]], [ATnxt[1]])
                    yield
                    Acur, Anxt = Anxt, Acur
                    ATcur, ATnxt = ATnxt, ATcur
                    cur = ((nP, R_nP), (nPT, R_nPT))
                mm4(3, PTo32, R_PTo32, Acur[0], Acur[1])
                mm4(4, Po32, R_Po32, ATcur[0], ATcur[1])
                yield
                copy_on("act", flat(U1), pss[3][:], [R_ps[3]], [R_U1])
                copy_on("dve", flat(T1), pss[4][:], [R_ps[4]], [R_T1])
                yield
                add_mm4(5, Acur, ATcur[0], ATcur[1], U1, R_U1)
                add_mm4(6, ATcur, Acur[0], Acur[1], T1, R_T1)
                yield
                copy_on("dve", flat(Anxt[0]), pss[5][:], [R_ps[5]], [Anxt[1]])
                copy_on("act", flat(ATnxt[0]), pss[6][:], [R_ps[6]], [ATnxt[1]])
                yield
                Acur, Anxt = Anxt, Acur
                ATcur, ATnxt = ATnxt, ATcur
                mm4(4, Po64, R_Po64, ATcur[0], ATcur[1])
                yield
                copy_on("dve", flat(T1), pss[4][:], [R_ps[4]], [R_T1])
                yield
                add_mm4(6, ATcur, Acur[0], Acur[1], T1, R_T1)
                yield
                copy_on("act", flat(ATnxt[0]), pss[6][:], [R_ps[6]], [ATnxt[1]])
                yield
                AT4, R_AT = ATnxt
                for g_, n in enumerate(ns):
                    cs_ = slice(g_ * 128, (g_ + 1) * 128)
                    mm(pss[3][:, cs_], AT4[:, g_, :], vb_tok[:, n, :], True, True, [R_AT, R_vb], R_ps[3])
                for g_, n in enumerate(ns):
                    cs_ = slice(g_ * 128, (g_ + 1) * 128)
                    mm(pss[4][:, cs_], kbg_tok[:, n, :], AT4[:, g_, :], True, True, [R_AT, R_kbg], R_ps[4])
                yield
                copy_on("act", flat(u4_), pss[3][:], [R_ps[3]], [R_u4_])
                copy_on("dve", flat(wT4_), pss[4][:], [R_ps[4]], [R_wT_])
                yield

            def rec_gen(bb):
                par = bb % 2
                qk4_, R_qk_ = qk4s[par]
                wT4_, R_wT_ = wT4s[par]
                u4_, R_u4_ = u4s[par]
                for g_ in range(G):
                    n = bb * G + g_
                    ch = slice(n * 128, (n + 1) * 128)
                    if n > 0:
                        mm(pss[0][:, 0:128], wT4_[:, g_, :], Sbf, True, True, [R_wT_, R_Sbf], R_ps[0])
                        yield
                        P.op("dve", lambda g: g.tensor_tensor(vnew, u4_[:, g_, :], pss[0][:, 0:128], ALU.subtract),
                             reads=[R_u4_, R_ps[0]], writes=[R_vn])
                    else:
                        copy_on("dve", vnew, u4_[:, g_, :], [R_u4_], [R_vn])
                    yield
                    if n > 0:
                        mm(pss[2][:, 0:128], Sbf, q_eT[:, ch], True, False, [R_Sbf, R_qe], R_ps[2])
                    if n < NT - 1:
                        mm(pss[1][:, 0:128], kt_tok[:, n, :], vnew, True, True, [R_kt, R_vn], R_ps[1])
                    mm(pss[2][:, 0:128], vnew, qk4_[:, g_, :], n == 0, True, [R_vn, R_qk_], R_ps[2])
                    yield
                    if n < NT - 1:
                        P.op("dve", lambda g: g.scalar_tensor_tensor(out=Sp, in0=Sp, scalar=cd[:, n:n + 1], in1=pss[1][:, 0:128],
                                                                     op0=ALU.mult, op1=ALU.add), reads=[R_Sp, R_cols, R_ps[1]], writes=[R_Sp])
                    copy_on("act", o_acc[:, 0, ch], pss[2][:, 0:128], [R_ps[2]], [R_oacc])
                    yield
                    if n < NT - 1:
                        copy_on("act", Sbf, Sp, [R_Sp], [R_Sbf])
                        yield

            def run_interleaved(g1, g2):
                live = [g1, g2]
                while live:
                    for gen in list(live):
                        try:
                            next(gen)
                        except StopIteration:
                            live.remove(gen)

            NB = NT // G
            for _ in solve_gen(0):
                pass
            for bb in range(NB):
                run_interleaved(solve_gen(bb + 1) if bb + 1 < NB else iter(()), rec_gen(bb))
            post_norm_store(l, o_acc, R_oacc, 1, [sc(51)], sz, R_sz, 2 + h, tmps, 128.0, 6)
            prefetch(l, 11 if h == 0 else 14)

        P.collective("AllGather", PAIRS, oTl_t[0].ap().opt(), oTg_t[0].ap().opt(), reads=[R_oTl[0]], writes=[R_oTg[0]])
        if stop == "pC2":
            P.barrier(); return nc
        for h in range(2):
            new_phase("diff")
            wfm = [ar.alloc([128, 16, 128], BF16, "wfm%d" % i) for i in range(3)]
            wtm = ar.alloc([128, 16, 256], BF16, "wtm")
            qT, R_q = ar.alloc([128, 2, S], BF16, "qT")
            kT, R_k = ar.alloc([128, 2, S], BF16, "kT")
            v_sb, R_v = ar.alloc([128, NT, 256], BF16, "v_sb")
            sz, R_sz = ar.alloc([128, 2, S], BF16, "sz")
            ebuf = [ar.alloc([128, 512], BF16, "e%d" % i) for i in range(3)]
            qn, R_qnb = ar.alloc([128, 512], BF16, "qn")
            ta, R_ta = ar.alloc([128, 512], F32, "ta")
            tb_, R_tb = ar.alloc([128, 512], F32, "tb")
            tO, R_tO = ar.alloc([128, 2, 512], F32, "tO")
            rs, R_rs = ar.alloc([128, 512], F32, "rs")
            sq2, R_sq2 = ar.alloc([128, 1024], BF16, "sq2")
            t1b, R_t1b = ar.alloc([128, 1024], F32, "t1b")
            rinvb, R_rinvb = ar.alloc([128, 1024], F32, "rinvb")
            qnb, R_qnb2 = ar.alloc([128, 1024], BF16, "qnb")
            tab, R_tab = ar.alloc([128, 1024], F32, "tab")
            tbb, R_tbb = ar.alloc([128, 1024], F32, "tbb")
            lamt, R_lam = ar.alloc([128, 8], F32, "lam")
            lprod, R_lprod = ar.alloc([128, 256], F32, "lprod")
            wn2, R_wn2 = ar.alloc([128, 2], F32, "wn2")
            tmps = [ar.alloc([128, 2, 512], BF16, "sq"), ar.alloc([128, 512], F32, "t1"), ar.alloc([128, 512], F32, "rinv"),
                    ar.alloc([128, 512], F32, "u"), ar.alloc([128, 512], BF16, "ost"), ar.alloc([128, 512], BF16, "ost2")]
            P.op("dve", lambda g: g.tensor_tensor(lprod[:, 0:128], sc(112, 128), sc(240, 128), ALU.mult), reads=[R_small], writes=[R_lprod])
            P.op("dve", lambda g: g.tensor_tensor(lprod[:, 128:256], sc(368, 128), sc(496, 128), ALU.mult), reads=[R_small], writes=[R_lprod])
            P.op("dve", lambda g: g.tensor_reduce(lamt[:, 0:2], lprod.rearrange("p (a b) -> p a b", b=128), AX.X, ALU.add),
                 reads=[R_lprod], writes=[R_lam])
            P.op("act", lambda g: g.activation(out=lamt[:, 2:4], in_=lamt[:, 0:2], func=AF.Exp), reads=[R_lam], writes=[R_lam])
            P.op("dve", lambda g: g.scalar_tensor_tensor(out=lamt[:, 4:5], in0=lamt[:, 3:4], scalar=-lam_init, in1=lamt[:, 2:3],
                                                         op0=ALU.add, op1=ALU.subtract), reads=[R_lam], writes=[R_lam])
            nlam = lamt[:, 4:5]
            P.op("dve", lambda g: g.tensor_scalar(wn2, sc(54, 2), 1.0 - lam_init, None, ALU.mult), reads=[R_small], writes=[R_wn2])
            units = [(which, m, half) for which in range(2) for m in range(2) for half in range(2)]
            qk_meta = ((qT, R_q, 14, 52), (kT, R_k, 18, 53))
            ustate = {}

            def qk_issue(ui):
                which, m, half = units[ui]
                g0 = qk_meta[which][2]
                wb, R_wb = qk_w[which * 2 + m]
                banks = [4 + 2 * (ui % 2), 5 + 2 * (ui % 2)]
                for kc in range(16):
                    for i in range(2):
                        tb = half * 2 + i
                        mm(pss[banks[i]][:], wb[:, kc, :], big[:, kc, tb * 512:(tb + 1) * 512], kc == 0, kc == 15,
                           [R_wb, R_big], R_ps[banks[i]])

            def qk_consume(ui):
                which, m, half = units[ui]
                dstT, R_dst, g0, wcol = qk_meta[which]
                b0_ = 4 + 2 * (ui % 2)
                ps2 = psall[:, b0_ * 512:(b0_ + 2) * 512]
                Rb = [R_ps[b0_], R_ps[b0_ + 1]]
                sl2 = slice(half * 1024, (half + 1) * 1024)
                P.op("act", lambda g: g.activation(out=sq2, in_=ps2, func=AF.Square), reads=Rb, writes=[R_sq2])
                for i in range(2):
                    mm(pss[2 + i][:], ones_b, sq2[:, i * 512:(i + 1) * 512], True, True, [R_sq2, R_cmb], R_ps[2 + i])
                P.op("act", lambda g: g.activation(out=t1b, in_=psall[:, 2 * 512:4 * 512], func=AF.Ln, scale=1.0 / 128.0, bias=eps_ap),
                     reads=[R_ps[2], R_ps[3], R_misc], writes=[R_t1b])
                P.op("act", lambda g: g.activation(out=rinvb, in_=t1b, func=AF.Exp, scale=-0.5), reads=[R_t1b], writes=[R_rinvb])
                P.op("dve", lambda g: g.scalar_tensor_tensor(out=qnb, in0=ps2, scalar=sc(wcol), in1=rinvb, op0=ALU.mult, op1=ALU.mult),
                     reads=Rb + [R_rinvb, R_small], writes=[R_qnb2])
                for i in range(2):
                    mm(pss[i][:], cm_b[:, CP, :], qnb[:, i * 512:(i + 1) * 512], True, True, [R_cmb, R_qnb2], R_ps[i])
                P.op("pool", lambda g: g.tensor_tensor(tab, qnb, cosT[:, sl2], ALU.mult), reads=[R_qnb2, R_cos], writes=[R_tab])
                P.op("dve", lambda g: g.tensor_tensor(tbb, psall[:, 0:1024], sinT[:, sl2], ALU.mult), reads=[R_ps[0], R_ps[1], R_sin], writes=[R_tbb])
                P.op("pool", lambda g: g.tensor_tensor(dstT[:, m, sl2], tab, tbb, ALU.add), reads=[R_tab, R_tbb], writes=[R_dst])
            qk_w = []
            for which_ in range(2):
                for m_ in range(2):
                    gi_ = qk_meta[which_][2] + 2 * h + m_
                    if state.get("pre") == (l, gi_):
                        qk_w.append((wpre, R_wpre))
                        state["pre"] = None
                        continue
                    wb_, R_wb_ = wfm[sum(1 for w_ in qk_w if w_[0] is not wpre)]
                    P.dma("pool", wb_.rearrange("p a b -> p (a b)"), winfm_d[l, gi_], writes=[R_wb_])
                    qk_w.append((wb_, R_wb_))
            qk_issue(0)
            for ui in range(len(units)):
                if ui + 1 < len(units):
                    qk_issue(ui + 1)
                qk_consume(ui)

            def c_v(t, ps, R):
                copy_on(alt(), v_sb[:, t, :], ps, [R], [R_v])
            proj_tm(l, 1 + h, wtm, c_v, [0, 1])
            for j in range(2):
                def c_z(tb, ps, R):
                    P.op("act", lambda g: g.activation(out=sz[:, j, tb * 512:(tb + 1) * 512], in_=ps, func=AF.Silu), reads=[R], writes=[R_sz])
                proj_fm(l, 22 + 2 * h + j, 128, wfm, c_z, j)
            if h == 1:
                for fc in range(16):
                    P.dma("pool", big[:, fc, :], wout_d[l][:, fc * D:(fc + 1) * D], writes=[R_big], nowaw=fc > 0)
            scale = 128.0 ** -0.5
            steps = [(qb, m, kc) for qb in range(4) for m in range(2) for kc in range(4 * (qb + 1))]

            def qk_mm(si):
                qb, m, kc = steps[si]
                col0 = max(kc - 4 * qb, 0) * 128
                bsc = si % 2
                mm(pss[bsc][:, col0:512], kT[:, m, kc * 128:(kc + 1) * 128], qT[:, m, qb * 512 + col0:(qb + 1) * 512], True, True,
                   [R_k, R_q], R_ps[bsc])
            qk_mm(0)
            for si, (qb, m, kc) in enumerate(steps):
                qs = slice(qb * 512, (qb + 1) * 512)
                nk = 4 * (qb + 1)
                bo0, bo1, bs = (2, 3, 4) if m == 0 else (5, 6, 7)
                if si + 1 < len(steps):
                    qk_mm(si + 1)
                bsc = si % 2
                c = kc - 4 * qb
                col0 = max(c, 0) * 128
                e, R_e = ebuf[si % 3]
                P.op("act", lambda g: g.activation(out=e[:, col0:512], in_=pss[bsc][:, col0:512], func=AF.Exp, scale=scale),
                     reads=[R_ps[bsc]], writes=[R_e])
                if c >= 0:
                    P.op("pool", lambda g: g.tensor_tensor(e[:, col0:col0 + 128], e[:, col0:col0 + 128], cm_b[:, CTU, :], ALU.mult),
                         reads=[R_e, R_cmb], writes=[R_e])
                first, last = kc == 0, kc == nk - 1
                mm(pss[bo0][:, col0:512], v_sb[:, kc, 0:128], e[:, col0:512], first, last, [R_v, R_e], R_ps[bo0])
                mm(pss[bo1][:, col0:512], v_sb[:, kc, 128:256], e[:, col0:512], first, last, [R_v, R_e], R_ps[bo1])
                mm(pss[bs][:, col0:512], ones_b, e[:, col0:512], first, last, [R_cmb, R_e], R_ps[bs])
                if not last:
                    continue
                P.op("dve", lambda g: g.reciprocal(rs, pss[bs][:]), reads=[R_ps[bs]], writes=[R_rs])
                for j, bo in enumerate((bo0, bo1)):
                    if m == 0:
                        P.op("dve", lambda g: g.tensor_tensor(tO[:, j, :], pss[bo][:], rs, ALU.mult), reads=[R_ps[bo], R_rs], writes=[R_tO])
                    else:
                        P.op("dve", lambda g: g.scalar_tensor_tensor(out=ta, in0=pss[bo][:], scalar=nlam, in1=rs, op0=ALU.mult, op1=ALU.mult),
                             reads=[R_ps[bo], R_rs, R_lam], writes=[R_ta])
                        P.op("pool", lambda g: g.tensor_tensor(tO[:, j, :], tO[:, j, :], ta, ALU.add), reads=[R_tO, R_ta], writes=[R_tO])
                if m == 0:
                    continue
                (sq, R_sq), (t1, R_t1), (rinv, R_rinv), (u, R_u) = tmps[0:4]
                rstd_part([(tO[:, j, :], R_tO) for j in range(2)], 512, 256.0, tmps[0:3], 4)
                for j in range(2):
                    ost, R_ost = tmps[4 + j]
                    P.op("dve", lambda g: g.scalar_tensor_tensor(out=u, in0=tO[:, j, :], scalar=wn2[:, j:j + 1], in1=rinv, op0=ALU.mult, op1=ALU.mult),
                         reads=[R_tO, R_rinv, R_wn2], writes=[R_u])
                    P.op("pool", lambda g: g.tensor_tensor(ost, u, sz[:, j, qs], ALU.mult), reads=[R_u, R_sz], writes=[R_ost])
                    dst_ap, R_d = oT_dst(4 + 2 * h + j, qs)
                    P.dma("sp", dst_ap, ost, reads=[R_ost], writes=[R_d], nowaw=True)
            if h == 0:
                prefetch(l, 16)
            P.collective("AllGather", PAIRS, oTl_t[1 + h].ap().opt(), oTg_t[1 + h].ap().opt(), reads=[R_oTl[1 + h]], writes=[R_oTg[1 + h]])

        if stop == "pC3":
            P.barrier(); return nc
        new_phase("D")
        wo = big
        xts = [ar.alloc([128, D], F32, "xt%d" % i) for i in range(2)]
        obs = [ar.alloc([128, 16, 512], BF16, "ob%d" % i) for i in range(2)]
        xos = [ar.alloc([128, D], F32, "xo%d" % i) for i in range(2)]
        ytmp, R_yt = ar.alloc([128, 512], F32, "ytmp")
        xsrc = x_d if l == 0 else x1_d
        xdst = out_d if l == L - 1 else x1_d
        R_dst = R_out if l == L - 1 else R_x1
        for tb in range(4):
            ob, R_ob = obs[tb % 2]
            for i2, (c0_, n_) in enumerate(((0, 8), (8, 4), (12, 4))):
                P.dma("sp", ob[:, c0_:c0_ + n_, :], oTg_d[i2][:, tb * 512:(tb + 1) * 512].rearrange("(c p) t -> p c t", p=128),
                      reads=[R_oTg[i2]], writes=[R_ob], nowaw=i2 > 0)
            for i in range(4):
                t = tb * 4 + i
                xt, R_xt = xts[t % 2]
                xo, R_xo = xos[t % 2]
                P.dma("sp", xt, xsrc[t * 128:(t + 1) * 128, :], reads=[R_x1] if l > 0 else [], writes=[R_xt])
                for ng in range(4):
                    ns = slice(ng * 512, (ng + 1) * 512)
                    bk = (t * 4 + ng) % 4
                    for fc in range(16):
                        mm(pss[bk][:], ob[:, fc, i * 128:(i + 1) * 128], wo[:, fc, ns], fc == 0, fc == 15, [R_ob, R_big], R_ps[bk])
                    P.op("dve", lambda g: g.tensor_tensor(ytmp, pss[bk][:], gate_bc[:, ns], ALU.mult), reads=[R_ps[bk], R_gate], writes=[R_yt])
                    P.op("pool", lambda g: g.tensor_tensor(xo[:, ns], ytmp, xt[:, ns], ALU.add), reads=[R_yt, R_xt], writes=[R_xo])
                P.dma("sp", xdst[t * 128:(t + 1) * 128, :], xo, reads=[R_xo], writes=[R_dst], nowaw=True)

    P.barrier(collectives=True)
    if scopes and state.get("scope") is not None:
        nc.leave_named_scope(state["scope"][0], state["scope"][1], False)
    return nc


def _col(v):
    return np.ascontiguousarray(np.asarray(v, np.float32).reshape(-1, 128).T)


def _consts():
    p = np.arange(128)[:, None]
    j = np.arange(128)[None, :]
    cm = np.zeros((128, NCM, 128), np.float32)
    cm[:, CI] = (p == j)
    cm[:, CO] = 1.0
    prot = np.zeros((128, 128), np.float32)
    prot[(j[0, :64] + 64), j[0, :64]] = -1.0
    prot[(j[0, 64:] - 64), j[0, 64:]] = 1.0
    cm[:, CP] = prot
    cm[:, CTU] = (p <= j)
    cm[:, CNSU] = -1.0 * (p < j)
    cm[:, CNSL] = -1.0 * (p > j)
    bd32 = (p // 32 == j // 32)
    bd64 = (p // 64 == j // 64)
    cm[:, CB32] = bd32
    cm[:, CO32] = bd64 & ~bd32
    cm[:, CO64] = ~bd64
    selm = np.zeros((8, 8, 128), np.float32)
    for k in range(8):
        selm[k, k, :] = 1.0
    rmask = np.ones((128, S), np.float32)
    rmask[:, 0::128] = 0.0
    half = 64
    inv_freq = (10000.0 ** (-(np.arange(half, dtype=np.float32) / np.float32(half)))).astype(np.float32)
    invf = np.concatenate([inv_freq, inv_freq]).astype(np.float64) / (2.0 * math.pi)
    return cm.reshape(128, NCM * 128), selm.reshape(8, 8 * 128), rmask, invf.astype(np.float32)


def _fm_groups_local(hh):
    g = [(hh * 128, 128), (256 + hh * 128, 128), (1024, 16)]
    for base in (1040, 1552, 2064, 2576):
        g += [(base + 128 * (2 * hh + i), 128) for i in range(2)]
    g.append("dab")
    g += [(3096 + 128 * (2 * hh + i), 128) for i in range(2)]
    for base in (3608, 4632, 6680):
        g += [(base + 256 * (2 * hh + lh) + 128 * m, 128) for lh in range(2) for m in range(2)]
    return g


def _tm_groups_local(hh):
    return [(512 + hh * 256, 256), (5656 + 256 * (2 * hh), 256), (5656 + 256 * (2 * hh + 1), 256)]


OUT_CHUNK_ORDER = [0, 1, 4, 5, 2, 3, 6, 7, 8, 9, 12, 13, 10, 11, 14, 15]


def _prep_shared(inp):
    f = lambda k: np.asarray(inp[k], np.float32)
    cm, selm, rmask, invf = _consts()
    w_in = f("w_in")
    per_half = []
    for hh in range(2):
        winfm = np.zeros((2, 26, 128, 16, 128), np.float32)
        wintm = np.zeros((2, 3, 128, 16, 256), np.float32)
        for l in range(2):
            wl = w_in[l].reshape(16, 128, -1)
            for gi, grp in enumerate(_fm_groups_local(hh)):
                if grp == "dab":
                    for i in range(2):
                        winfm[l, gi, :, :, i] = wl[:, :, 3088 + 2 * hh + i].T
                        winfm[l, gi, :, :, 4 + i] = wl[:, :, 3092 + 2 * hh + i].T
                else:
                    c0, n = grp
                    winfm[l, gi, :, :, :n] = wl[:, :, c0:c0 + n].transpose(1, 0, 2)
            for gi, (c0, n) in enumerate(_tm_groups_local(hh)):
                wintm[l, gi] = wl[:, :, c0:c0 + n].transpose(1, 0, 2)
        sm = np.zeros((128, NS), np.float32)
        sm[:, 16] = invf
        for l in range(2):
            b = SBASE + l * SLW
            sm[:, b:b + 16] = _col(f("norm_w")[l])
            sm[:, b + 16:b + 48] = _col(f("b_ada")[l, :2 * D])
            sm[:, b + 48] = f("gla_b_lr")[l, hh * 128:(hh + 1) * 128]
            sm[:, b + 50] = f("gla_norm_w")[l]
            sm[:, b + 51] = f("gdn_norm_w")[l]
            sm[:, b + 52] = f("diff_q_norm_w")[l]
            sm[:, b + 53] = f("diff_k_norm_w")[l]
            sm[:, b + 54:b + 56] = _col(f("diff_norm_w")[l])
            sm[:, b + 56:b + 58] = f("gdn_a_log")[l][None, 2 * hh:2 * hh + 2]
            sm[:, b + 60:b + 62] = f("gdn_dt_bias")[l][None, 2 * hh:2 * hh + 2]
            cw = f("gdn_conv_w")[l].reshape(4, 12, 128)
            for which in range(3):
                for lh in range(2):
                    ti = which * 4 + lh
                    sm[:, b + 64 + ti * 4:b + 64 + ti * 4 + 4] = cw[:, which * 4 + 2 * hh + lh, :].T
            sm[:, b + 112:b + 624] = f("diff_lambda")[l].reshape(1, 512)
            sm[0:16, b + 624:b + 752] = f("gla_w_lr")[l][:, hh * 128:(hh + 1) * 128]
        per_half.append({"winfm": winfm.reshape(2, 26, 128, 16 * 128), "wintm": wintm.reshape(2, 3, 128, 16 * 256), "small": sm})
    wada = np.ascontiguousarray(f("w_ada").reshape(2, 16, 128, 48, 128).transpose(0, 3, 2, 1, 4)).reshape(2, 48, 128, 16 * 128)
    wo = f("w_out").reshape(2, 16, 128, D)[:, OUT_CHUNK_ORDER]
    wout = np.ascontiguousarray(wo.transpose(0, 2, 1, 3)).reshape(2, 128, 16 * D)
    bgate = np.ascontiguousarray(f("b_ada")[:, 2 * D:].reshape(2, 1, D))
    shared = {"cmat": cm, "selm": selm, "rmask": rmask, "wada": wada, "bgate": bgate, "wout": wout}
    return shared, per_half


def make_in_maps(inp, cores):
    shared, per_half = _prep_shared(inp)
    x = np.asarray(inp["x"], np.float32)
    c = np.asarray(inp["c"], np.float32)
    pos = np.asarray(inp["positions"], np.int32)
    maps = []
    for b, hh in cores:
        s = per_half[hh]["small"].copy()
        s[:, 0:16] = _col(c[b])
        m = dict(shared)
        m["winfm"] = per_half[hh]["winfm"]
        m["wintm"] = per_half[hh]["wintm"]
        m["x"] = np.ascontiguousarray(x[b])
        m["small"] = s
        m["pos"] = np.ascontiguousarray(pos[b:b + 1])
        maps.append(m)
    return maps


_NC_CACHE = {}


def kernel(**inputs):
    if "nc" not in _NC_CACHE:
        _NC_CACHE["nc"] = build(2)
    nc = _NC_CACHE["nc"]
    cores = [(i // 2, i % 2) for i in range(8)]
    maps = make_in_maps(inputs, cores)
    res = run_bass_kernel_spmd(nc, maps, core_ids=list(range(8)))
    out = np.stack([np.asarray(res.results[2 * b]["out"], np.float32) for b in range(4)], axis=0)
    return out
```

```python
import math
import numpy as np
import concourse.bass as bass
import concourse.mybir as mybir
from concourse.bass_utils import run_bass_kernel_spmd

F32 = mybir.dt.float32
BF16 = mybir.dt.bfloat16
I32 = mybir.dt.int32
AF = mybir.ActivationFunctionType
ALU = mybir.AluOpType
AX = mybir.AxisListType

S = 2048
D = 2048
NT = 16
EPS = 1e-6
SLW = 880
SBASE = 32
NS = SBASE + 2 * SLW
CI, CO, CP, CTU, CNSU, CNSL, CB32, CO32, CO64 = range(9)
NCM = 9


class Reg:
    __slots__ = ("name", "lw", "rd", "dsem", "local", "psum")

    def __init__(self, name, local=False, psum=False):
        self.name = name
        self.psum = psum
        self.lw = None
        self.rd = {}
        self.dsem = None
        self.local = local


class Prog:
    def __init__(self, nc):
        self.nc = nc
        self.eng = {"pe": nc.tensor, "act": nc.scalar, "dve": nc.vector, "pool": nc.gpsimd, "sp": nc.sync}
        self.sem, self.cnt, self.semobj = {}, {}, {}
        self.seen = {e: {} for e in self.eng}
        for e in self.eng:
            s = nc.alloc_semaphore(name="s_" + e)
            self.sem[e] = s
            self.semobj[e] = s
            self.cnt[e] = 0
        self.vc = {}
        self.dcnt = {}
        self.local_keys = []
        self.local_next = 0
        self.ninstr = 0
        self.nwait = 0

    def _wait(self, e, key, val):
        if self.seen[e].get(key, 0) >= val:
            return
        self.eng[e].wait_ge(self.semobj[key], val)
        self.nwait += 1
        se = self.seen[e]
        se[key] = val
        clk = self.vc.get((key, val))
        if clk:
            for k, v in clk.items():
                if se.get(k, 0) < v:
                    se[k] = v

    def _deps(self, e, reads, writes, pe_accum=False, nowaw=False):
        need = {}

        def add(k, v):
            if need.get(k, 0) < v:
                need[k] = v
        for r in reads:
            if r.lw is not None:
                add(*r.lw)
            if r.psum:
                for k, v in r.rd.items():
                    if k != e:
                        add(k, v)
        for w in writes:
            if w.lw is not None and not nowaw and not (pe_accum and w.lw[0] == "pe"):
                add(*w.lw)
            for k, v in w.rd.items():
                add(k, v)
        for k, v in sorted(need.items(), key=lambda kv: -kv[1] if isinstance(kv[0], str) else 0):
            self._wait(e, k, v)

    def _record(self, ev, reads, writes, nowaw=False):
        for r in reads:
            if r.rd.get(ev[0], 0) < ev[1]:
                r.rd[ev[0]] = ev[1]
        for w in writes:
            w.lw = ev
            if not nowaw:
                w.rd = {}

    def op(self, e, fn, reads=(), writes=(), pe_accum=False):
        self._deps(e, reads, writes, pe_accum)
        ins = fn(self.eng[e])
        self.cnt[e] += 1
        ins.then_inc(self.sem[e], 1)
        ev = (e, self.cnt[e])
        self.vc[ev] = dict(self.seen[e])
        self._record(ev, reads, writes)
        self.ninstr += 1

    def _newkey(self):
        key = ("d", len(self.dcnt))
        self.semobj[key] = self.nc.alloc_semaphore(name="d_%d" % len(self.dcnt))
        self.dcnt[key] = 0
        return key

    def _dkey(self, w):
        if w.dsem is None:
            if w.local:
                if self.local_next == len(self.local_keys):
                    self.local_keys.append(self._newkey())
                w.dsem = self.local_keys[self.local_next]
                self.local_next += 1
            else:
                w.dsem = self._newkey()
        return w.dsem

    def dma(self, q, out_ap, in_ap, reads=(), writes=(), nowaw=False, **kw):
        w = writes[0]
        self._deps(q, reads, writes, nowaw=nowaw)
        key = self._dkey(w)
        ins = self.eng[q].dma_start(out=out_ap, in_=in_ap, **kw)
        self.dcnt[key] += 16
        ins.then_inc(self.semobj[key], 16)
        ev = (key, self.dcnt[key])
        self.vc[ev] = dict(self.seen[q])
        self._record(ev, reads, writes, nowaw=nowaw)
        self.ninstr += 1

    def collective(self, kind, groups, in_ap, out_ap, reads, writes):
        self._deps("pool", reads, writes)
        key = ("c", len([k for k in self.dcnt if k[0] == "c"]))
        self.semobj[key] = self.nc.alloc_semaphore(name="c_%d" % key[1])
        ins = self.eng["pool"].collective_compute(kind, ALU.bypass, replica_groups=groups, ins=[in_ap], outs=[out_ap])
        ins.then_inc(self.semobj[key])
        self.dcnt[key] = 1
        ev = (key, 1)
        self.vc[ev] = dict(self.seen["pool"])
        self._record(ev, reads, writes)
        self.ninstr += 1

    def barrier(self, reset=True, collectives=False):
        evs = [(e, self.cnt[e]) for e in self.eng if self.cnt[e] > 0]
        evs += [(k, v) for k, v in self.dcnt.items() if v > 0 and (collectives or k[0] != "c")]
        for e in self.eng:
            for k, v in evs:
                if k != e:
                    self._wait(e, k, v)
        if reset:
            self.local_next = 0


class Arena:
    def __init__(self, nc, nwords):
        self.t = nc.alloc_sbuf_tensor("arena", [128, nwords], F32)
        self.n = nwords
        self.off = 0
        self.k = 0

    def reset(self):
        self.off = 0

    def alloc(self, shape, dt, name=None):
        free = int(np.prod(shape[1:]))
        words = free if dt in (F32, I32) else (free + 1) // 2
        words = (words + 7) // 8 * 8
        assert self.off + words <= self.n, ("arena overflow", name, self.off, words, self.n)
        v = self.t[:, self.off:self.off + words]
        self.off += words
        if dt == BF16:
            v = v.bitcast(BF16)[:, 0:free]
        elif dt == I32:
            v = v.bitcast(I32)[:, 0:free]
        else:
            v = v[:, 0:free]
        if len(shape) == 3:
            v = v.rearrange("p (a b) -> p a b", b=shape[2])
        self.k += 1
        if shape[0] < 128:
            v = v[0:shape[0]]
        return v, Reg(name or "a%d" % self.k, local=True)


def build(nlayers=2, debug=False, stop=None, scopes=False, ncores=8):
    nc = bass.Bass("TRN2", target_bir_lowering=False)
    P = Prog(nc)
    L = nlayers

    def din(name, shape, dt=F32):
        return nc.dram_tensor(name, shape, dt, kind="ExternalInput").ap()

    x_d = din("x", [S, D])
    small_d = din("small", [128, NS])
    cmat_d = din("cmat", [128, NCM * 128])
    selm_d = din("selm", [8, 8 * 128])
    rmask_d = din("rmask", [128, S])
    pos_d = din("pos", [1, S], I32)
    wada_d = din("wada", [2, 48, 128, 16 * 128])
    bgate_d = din("bgate", [2, 1, D])
    winfm_d = din("winfm", [2, 26, 128, 16 * 128])
    wintm_d = din("wintm", [2, 3, 128, 16 * 256])
    wout_d = din("wout", [2, 128, 16 * D])
    out_d = nc.dram_tensor("out", [S, D], F32, kind="ExternalOutput").ap()
    x1_d = nc.dram_tensor("x1s", [S, D], F32, kind="Internal").ap()
    OT_ROWS = (4, 2, 2)
    oTl_t = [nc.dram_tensor("oTl%d" % i, [n_ * 128, S], BF16, kind="Internal") for i, n_ in enumerate(OT_ROWS)]
    oTg_t = [nc.dram_tensor("oTg%d" % i, [2 * n_ * 128, S], BF16, kind="Internal") for i, n_ in enumerate(OT_ROWS)]
    oTl_d = [t.ap() for t in oTl_t]
    oTg_d = [t.ap() for t in oTg_t]
    R_oTl = [Reg("oTl%d" % i) for i in range(3)]
    R_oTg = [Reg("oTg%d" % i) for i in range(3)]
    PAIRS = [[2 * i, 2 * i + 1] for i in range(ncores // 2)]
    R_out, R_x1 = Reg("out"), Reg("x1")

    def oT_dst(fc, sl):
        ti, r = (0, fc) if fc < 4 else (1 + (fc - 4) // 2, (fc - 4) % 2)
        return oTl_d[ti][r * 128:(r + 1) * 128, sl], R_oTl[ti]

    def sb(name, shape, dt):
        return nc.alloc_sbuf_tensor("sb_" + name, shape, dt), Reg(name)

    big, R_big = sb("big", [128, 16, S], BF16)
    cosT, R_cos = sb("cosT", [128, S], BF16)
    sinT, R_sin = sb("sinT", [128, S], BF16)
    gate_bc, R_gate = sb("gate_bc", [128, D], F32)
    rmask, R_rmask = sb("rmask", [128, S], BF16)
    cm_f, R_cmf = sb("cm_f", [128, NCM, 128], F32)
    cm_b, R_cmb = sb("cm_b", [128, NCM, 128], BF16)
    selm, R_selm = sb("selm", [8, 8, 128], F32)
    small, R_small = sb("small", [128, NS], F32)
    modc, R_modc = sb("modc", [128, 48], F32)
    misc, R_misc = sb("misc", [128, 64], F32)
    wpre, R_wpre = sb("wpre", [128, 16, 128], BF16)
    ar = Arena(nc, (nc.sbuf_bytes_remaining - 2048) // 4)

    psall = nc.alloc_psum_tensor("psall", [128, 8 * 512], F32)
    pss = [psall[:, i * 512:(i + 1) * 512] for i in range(8)]
    R_ps = [Reg("ps%d" % i, psum=True) for i in range(8)]

    state = {"alt": 0}

    def alt():
        state["alt"] ^= 1
        return "act" if state["alt"] else "dve"

    def copy_on(e, out, in_, reads, writes):
        if e == "act":
            P.op("act", lambda g: g.activation(out=out, in_=in_, func=AF.Copy), reads=reads, writes=writes)
        else:
            P.op(e, lambda g: g.tensor_copy(out, in_), reads=reads, writes=writes)

    def mm(out, lhsT, rhs, start, stop, reads, w):
        P.op("pe", lambda g: g.matmul(out, lhsT=lhsT, rhs=rhs, start=start, stop=stop),
             reads=reads, writes=[w], pe_accum=not start)

    def new_phase(name="ph"):
        P.barrier()
        ar.reset()
        if scopes:
            if state.get("scope") is not None:
                nc.leave_named_scope(state["scope"][0], state["scope"][1], False)
            nm = "%s_%d" % (name, state.setdefault("nscope", 0))
            state["nscope"] += 1
            sid, _ = nc.enter_named_scope(nm, False)
            state["scope"] = (nm, sid)

    ident_b = cm_b[:, CI, :]
    ones_b = cm_b[:, CO, :]
    ident_f = cm_f[:, CI, :]

    def rstd_part(srcs, n, denom, tmps, psi):
        (sq, R_sq), (t1, R_t1), (rinv, R_rinv) = tmps
        for i, (sap, sreg) in enumerate(srcs):
            P.op("act", lambda g: g.activation(out=sq[:, i, 0:n], in_=sap, func=AF.Square), reads=[sreg], writes=[R_sq])
        for i in range(len(srcs)):
            mm(pss[psi][:, 0:n], ones_b, sq[:, i, 0:n], i == 0, i == len(srcs) - 1, [R_sq, R_cmb], R_ps[psi])
        P.op("act", lambda g: g.activation(out=t1[:, 0:n], in_=pss[psi][:, 0:n], func=AF.Ln, scale=1.0 / denom, bias=eps_ap),
             reads=[R_ps[psi], R_misc], writes=[R_t1])
        P.op("act", lambda g: g.activation(out=rinv[:, 0:n], in_=t1[:, 0:n], func=AF.Exp, scale=-0.5), reads=[R_t1], writes=[R_rinv])
        return rinv, R_rinv

    P.dma("sp", small[:], small_d, writes=[R_small])
    P.dma("sp", cm_f[:], cmat_d.rearrange("p (a b) -> p a b", b=128), writes=[R_cmf])
    P.dma("sp", selm[:], selm_d.rearrange("p (a b) -> p a b", b=128), writes=[R_selm])
    P.dma("pool", rmask[:], rmask_d, writes=[R_rmask])
    P.op("dve", lambda g: g.tensor_copy(cm_b[:], cm_f[:]), reads=[R_cmf], writes=[R_cmb])
    eps_ap = misc[:, 0:1]
    P.op("dve", lambda g: g.memset(misc[:], 0.0), writes=[R_misc])
    P.op("dve", lambda g: g.memset(misc[:, 0:1], EPS), reads=[], writes=[R_misc])
    P.op("dve", lambda g: g.memset(misc[:, 1:2], 1.0), reads=[], writes=[R_misc])
    one_ap = misc[:, 1:2]

    posi, R_posi = ar.alloc([128, S], I32, "posi")
    y, R_y = ar.alloc([128, S], F32, "y")
    yi, R_yi = ar.alloc([128, S], I32, "yi")
    yf, R_yf = ar.alloc([128, S], F32, "yf")
    fr, R_fr = ar.alloc([128, S], F32, "fr")
    m1, R_m1 = ar.alloc([128, S], F32, "m1")
    P.dma("sp", posi, pos_d.to_broadcast([128, S]), writes=[R_posi])
    P.op("dve", lambda g: g.tensor_copy(y, posi), reads=[R_posi], writes=[R_y])
    P.op("dve", lambda g: g.tensor_scalar(y, y, small[:, 16:17], None, ALU.mult), reads=[R_y, R_small], writes=[R_y])
    P.op("dve", lambda g: g.tensor_copy(yi, y), reads=[R_y], writes=[R_yi])
    P.op("dve", lambda g: g.tensor_copy(yf, yi), reads=[R_yi], writes=[R_yf])
    P.op("dve", lambda g: g.tensor_tensor(fr, y, yf, ALU.subtract), reads=[R_y, R_yf], writes=[R_fr])
    for which, dst, R_dst in ((0, sinT, R_sin), (1, cosT, R_cos)):
        src = fr
        if which == 1:
            P.op("dve", lambda g: g.tensor_scalar(y, fr, 0.25, None, ALU.add), reads=[R_fr], writes=[R_y])
            src = y
        R_src = R_fr if which == 0 else R_y
        P.op("dve", lambda g: g.tensor_scalar(m1, src, 0.5, None, ALU.is_gt), reads=[R_src], writes=[R_m1])
        P.op("dve", lambda g: g.tensor_tensor(yf, src, m1, ALU.subtract), reads=[R_src, R_m1], writes=[R_yf])
        P.op("dve", lambda g: g.tensor_scalar(m1, yf, -0.5, None, ALU.is_lt), reads=[R_yf], writes=[R_m1])
        P.op("dve", lambda g: g.tensor_tensor(yf, yf, m1, ALU.add), reads=[R_yf, R_m1], writes=[R_yf])
        P.op("act", lambda g: g.activation(out=dst[:], in_=yf, func=AF.Sin, scale=2.0 * math.pi), reads=[R_yf], writes=[R_dst])

    if stop == "p0":
        P.barrier(); return nc
    def proj_fm(l, gi, M, wbufs, consume, bankset):
        if state.get("pre") == (l, gi):
            wb, R_wb = wpre, R_wpre
            state["pre"] = None
        else:
            wb, R_wb = wbufs[state.setdefault("wfm_i", 0) % len(wbufs)]
            state["wfm_i"] += 1
            P.dma("pool", wb.rearrange("p a b -> p (a b)"), winfm_d[l, gi], writes=[R_wb])
        banks = [bankset * 4 + i for i in range(4)]
        for kc in range(16):
            for tb in range(4):
                mm(pss[banks[tb]][0:M, :], wb[:, kc, 0:M], big[:, kc, tb * 512:(tb + 1) * 512], kc == 0, kc == 15,
                   [R_wb, R_big], R_ps[banks[tb]])
        for tb in range(4):
            consume(tb, pss[banks[tb]][0:M, :], R_ps[banks[tb]])

    def proj_tm(l, gi, wtb, consume, banks):
        wb, R_wb = wtb
        P.dma("pool", wb.rearrange("p a b -> p (a b)"), wintm_d[l, gi], writes=[R_wb])
        for t in range(NT):
            bk = banks[t % len(banks)]
            for kc in range(16):
                mm(pss[bk][:, 0:256], big[:, kc, t * 128:(t + 1) * 128], wb[:, kc, :], kc == 0, kc == 15,
                   [R_wb, R_big], R_ps[bk])
            consume(t, pss[bk][:, 0:256], R_ps[bk])

    def prefetch(l, gi):
        P.dma("pool", wpre.rearrange("p a b -> p (a b)"), winfm_d[l, gi], writes=[R_wpre])
        state["pre"] = (l, gi)

    def post_norm_store(l, o_acc, R_oacc, nsub, wcols, szs, R_sz, fc0, tmps, denom, psi):
        (sq, R_sq), (t1, R_t1), (rinv, R_rinv), (u, R_u) = tmps[0:4]
        for blk in range(4):
            sl = slice(blk * 512, (blk + 1) * 512)
            rstd_part([(o_acc[:, j, sl], R_oacc) for j in range(nsub)], 512, denom, tmps[0:3], psi)
            for j in range(nsub):
                P.op("dve", lambda g: g.scalar_tensor_tensor(out=u[:, 0:512], in0=o_acc[:, j, sl], scalar=wcols[j], in1=rinv[:, 0:512],
                                                             op0=ALU.mult, op1=ALU.mult),
                     reads=[R_oacc, R_rinv, R_small, R_misc], writes=[R_u])
                ost, R_ost = tmps[4 + state.setdefault("ost_i", 0) % 2]
                state["ost_i"] += 1
                P.op("pool", lambda g: g.tensor_tensor(ost[:, 0:512], u[:, 0:512], szs[:, j, sl], ALU.mult),
                     reads=[R_u, R_sz], writes=[R_ost])
                dst_ap, R_d = oT_dst(fc0 + j, sl)
                P.dma("sp", dst_ap, ost[:, 0:512], reads=[R_ost], writes=[R_d], nowaw=True)

    for l in range(L):
        sb_l = SBASE + l * SLW
        lam_init = 0.8 - 0.6 * math.exp(-0.3 * l)

        def sc(off, n=1):
            return small[:, sb_l + off: sb_l + off + n]

        new_phase("A")
        cact, R_cact = ar.alloc([128, 16], F32, "cact")
        c2, R_c2 = ar.alloc([128, 16, 2], BF16, "c2")
        crep, R_crep = ar.alloc([128, 16, 128], BF16, "crep")
        bg, R_bg = ar.alloc([128, D], F32, "bg")
        wab = [ar.alloc([128, 16, 128], F32, "wa%d" % i) for i in range(4)]
        wbb = [ar.alloc([128, 16, 128], BF16, "wb%d" % i) for i in range(3)]
        P.op("act", lambda g: g.activation(out=cact, in_=small[:, 0:16], func=AF.Silu), reads=[R_small], writes=[R_cact])
        P.op("dve", lambda g: g.tensor_copy(c2, cact.unsqueeze(2).to_broadcast([128, 16, 2])), reads=[R_cact], writes=[R_c2])
        P.op("dve", lambda g: g.tensor_copy(crep, cact.unsqueeze(2).to_broadcast([128, 16, 128])), reads=[R_cact], writes=[R_crep])
        P.dma("sp", bg, bgate_d[l].to_broadcast([128, D]), writes=[R_bg])
        for g_ in range(48):
            wf, R_wf = wab[g_ % 4]
            P.dma("sp", wf.rearrange("p a b -> p (a b)"), wada_d[l, g_], writes=[R_wf])
            wa, R_wa = wbb[g_ % 3]
            copy_on(("act", "dve")[g_ % 2], wa.rearrange("p a b -> p (a b)"), wf.rearrange("p a b -> p (a b)"), [R_wf], [R_wa])
            if g_ < 32:
                for kc in range(16):
                    mm(pss[0][:, 2 * g_:2 * g_ + 2], wa[:, kc, :], c2[:, kc, :], kc == 0, kc == 15, [R_wa, R_c2], R_ps[0])
                if g_ == 31:
                    P.op("dve", lambda g: g.tensor_tensor(modc[:, 0:32], pss[0][:, 0:64:2], sc(16, 32), ALU.add),
                         reads=[R_ps[0], R_small], writes=[R_modc])
                    P.op("dve", lambda g: g.scalar_tensor_tensor(out=modc[:, 32:48], in0=modc[:, 16:32], scalar=1.0, in1=sc(0, 16),
                                                                 op0=ALU.add, op1=ALU.mult),
                         reads=[R_modc, R_small], writes=[R_modc])
            else:
                gg = g_ - 32
                bk = 1 + (gg // 4) % 2
                c0 = (gg % 4) * 128
                for kc in range(16):
                    mm(pss[bk][:, c0:c0 + 128], crep[:, kc, :], wa[:, kc, :], kc == 0, kc == 15, [R_wa, R_crep], R_ps[bk])
                if gg % 4 == 3:
                    sl = slice((gg // 4) * 512, (gg // 4 + 1) * 512)
                    P.op("dve", lambda g: g.tensor_tensor(gate_bc[:, sl], pss[bk][:], bg[:, sl], ALU.add),
                         reads=[R_ps[bk], R_bg], writes=[R_gate])

        if stop == "pA":
            P.barrier(); return nc
        new_phase("B")
        xsrc = x_d if l == 0 else x1_d
        xts = [ar.alloc([128, D], F32, "xt%d" % i) for i in range(2)]
        xn, R_xn = ar.alloc([128, 4, D], BF16, "xn")
        junk, R_junk = ar.alloc([128, D], BF16, "junk")
        ssq, R_ssq = ar.alloc([128, 8], F32, "ssq")
        hT = big
        for tb in range(4):
            for i in range(4):
                t = tb * 4 + i
                xt, R_xt = xts[t % 2]
                P.dma("sp", xt, xsrc[t * 128:(t + 1) * 128, :], reads=[R_x1] if l > 0 else [], writes=[R_xt])
                P.op("act", lambda g: g.activation(out=junk, in_=xt, func=AF.Square, accum_out=ssq[:, 0:1]),
                     reads=[R_xt], writes=[R_junk, R_ssq])
                P.op("act", lambda g: g.activation(out=ssq[:, 1:2], in_=ssq[:, 0:1], func=AF.Sqrt, scale=1.0 / D, bias=eps_ap),
                     reads=[R_ssq, R_misc], writes=[R_ssq])
                P.op("dve", lambda g: g.reciprocal(ssq[:, 2:3], ssq[:, 1:2]), reads=[R_ssq], writes=[R_ssq])
                P.op("dve", lambda g: g.tensor_scalar(xn[:, i, :], xt, ssq[:, 2:3], None, ALU.mult),
                     reads=[R_xt, R_ssq], writes=[R_xn])
            for fc in range(16):
                bk = fc % 4
                pT = pss[bk][:].bitcast(BF16)
                for i in range(4):
                    P.op("pe", lambda g: g.transpose(pT[:, i * 128:(i + 1) * 128], xn[:, i, fc * 128:(fc + 1) * 128], ident_b),
                         reads=[R_xn, R_cmb], writes=[R_ps[bk]], pe_accum=i > 0)
                dst = hT[:, fc, tb * 512:(tb + 1) * 512]
                if alt() == "act":
                    P.op("act", lambda g: g.activation(out=dst, in_=pT[:, 0:512], func=AF.Identity,
                                                       scale=modc[:, 32 + fc:33 + fc], bias=modc[:, fc:fc + 1]),
                         reads=[R_ps[bk], R_modc], writes=[R_big])
                else:
                    P.op("dve", lambda g: g.tensor_scalar(dst, pT[:, 0:512], modc[:, 32 + fc:33 + fc], modc[:, fc:fc + 1],
                                                          ALU.mult, ALU.add),
                         reads=[R_ps[bk], R_modc], writes=[R_big])

        prefetch(l, 2)
        if stop == "pB":
            P.barrier(); return nc
        for pr in range(1):
            new_phase("gla")
            wfm = [ar.alloc([128, 16, 128], BF16, "wfm%d" % i) for i in range(2)]
            wtm = ar.alloc([128, 16, 256], BF16, "wtm")
            glrT, R_glrT = ar.alloc([16, S], BF16, "glrT")
            wlr, R_wlr = ar.alloc([16, 256], BF16, "wlr")
            bcs, R_bcs = ar.alloc([128, S], F32, "bcs")
            eb, R_eb = ar.alloc([128, S], F32, "eb")
            q_eT, R_qe = ar.alloc([128, S], BF16, "q_eT")
            k_eT, R_ke = ar.alloc([128, S], BF16, "k_eT")
            v_g, R_vg = ar.alloc([128, NT, 256], BF16, "v_g")
            sz, R_sz = ar.alloc([128, 2, S], BF16, "sz")
            ke_tok, R_ket = ar.alloc([128, NT, 128], BF16, "ke_tok")
            o_acc, R_oacc = ar.alloc([128, 2, S], F32, "o_acc")
            e1, R_e1 = ar.alloc([128, 512], F32, "e1")
            dec, R_dec = ar.alloc([128, 16], F32, "dec")
            nb, R_nb = ar.alloc([128, 2], F32, "nb")
            Sp, R_Sp = ar.alloc([128, 256], F32, "Sp")
            Stmp, R_Stmp = ar.alloc([128, 256], F32, "Stmp")
            Sbf, R_Sbf = ar.alloc([128, 256], BF16, "Sbf")
            attm, R_attm = ar.alloc([128, 2, 128], BF16, "attm")
            tmps = [ar.alloc([128, 2, 512], BF16, "sq"), ar.alloc([128, 512], F32, "t1"), ar.alloc([128, 512], F32, "rinv"),
                    ar.alloc([128, 512], F32, "u"), ar.alloc([128, 512], BF16, "ost"), ar.alloc([128, 512], BF16, "ost2")]

            def c_glr(tb, ps, R):
                copy_on("act", glrT[0:16, tb * 512:(tb + 1) * 512], ps, [R], [R_glrT])
            proj_fm(l, 2, 16, wfm, c_glr, 0)
            P.op("dve", lambda g: g.tensor_copy(wlr, sc(624, 256)[0:16, :]), reads=[R_small], writes=[R_wlr])
            P.op("dve", lambda g: g.tensor_scalar(nb, sc(48, 2), -1.0, None, ALU.mult), reads=[R_small], writes=[R_nb])
            for blk in range(4):
                sl = slice(blk * 512, (blk + 1) * 512)
                bk = 4 + blk % 2
                mm(pss[bk][:], wlr[0:16, pr * 128:(pr + 1) * 128], glrT[0:16, sl], True, True, [R_wlr, R_glrT], R_ps[bk])
                P.op("act", lambda g: g.activation(out=e1, in_=pss[bk][:], func=AF.Exp, scale=-1.0, bias=nb[:, pr:pr + 1]),
                     reads=[R_ps[bk], R_nb], writes=[R_e1])
                P.op("act", lambda g: g.activation(out=bcs[:, sl], in_=e1, func=AF.Ln, bias=one_ap), reads=[R_e1, R_misc], writes=[R_bcs])
            P.op("dve", lambda g: g.tensor_tensor_scan(out=bcs, data0=rmask[:], data1=bcs, initial=0.0, op0=ALU.mult, op1=ALU.add),
                 reads=[R_rmask, R_bcs], writes=[R_bcs])
            P.op("act", lambda g: g.activation(out=eb, in_=bcs, func=AF.Exp, scale=-1.0 / 16.0), reads=[R_bcs], writes=[R_eb])
            P.op("dve", lambda g: g.tensor_copy(dec, eb[:, 127:S:128]), reads=[R_eb], writes=[R_dec])
            P.op("act", lambda g: g.activation(out=bcs, in_=bcs, func=AF.Exp, scale=1.0 / 16.0), reads=[R_bcs], writes=[R_bcs])
            enb = bcs

            def c_q(tb, ps, R):
                sl = slice(tb * 512, (tb + 1) * 512)
                P.op("dve", lambda g: g.scalar_tensor_tensor(out=q_eT[:, sl], in0=ps, scalar=0.125, in1=eb[:, sl], op0=ALU.mult, op1=ALU.mult),
                     reads=[R, R_eb], writes=[R_qe])
            proj_fm(l, 0, 128, wfm, c_q, 1)

            def c_k(tb, ps, R):
                sl = slice(tb * 512, (tb + 1) * 512)
                P.op("dve", lambda g: g.tensor_tensor(k_eT[:, sl], ps, enb[:, sl], ALU.mult), reads=[R, R_bcs], writes=[R_ke])
            proj_fm(l, 1, 128, wfm, c_k, 0)

            def c_v(t, ps, R):
                copy_on("act", v_g[:, t, :], ps, [R], [R_vg])
            proj_tm(l, 0, wtm, c_v, [4, 5])
            for hh in range(2):
                def c_z(tb, ps, R):
                    P.op("act", lambda g: g.activation(out=sz[:, hh, tb * 512:(tb + 1) * 512], in_=ps, func=AF.Silu), reads=[R], writes=[R_sz])
                proj_fm(l, 3 + hh, 128, wfm, c_z, hh)
            for t4 in range(4):
                bk = 6 + t4 % 2
                pT = pss[bk][:].bitcast(BF16)
                for i in range(4):
                    t = t4 * 4 + i
                    P.op("pe", lambda g: g.transpose(pT[:, i * 128:(i + 1) * 128], k_eT[:, t * 128:(t + 1) * 128], ident_b),
                         reads=[R_ke, R_cmb], writes=[R_ps[bk]], pe_accum=i > 0)
                P.op("dve", lambda g: g.tensor_copy(ke_tok[:, t4 * 4:(t4 + 1) * 4, :], pT[:, 0:512].rearrange("p (a b) -> p a b", b=128)),
                     reads=[R_ps[bk]], writes=[R_ket])
            P.op("dve", lambda g: g.memset(Sp, 0.0), writes=[R_Sp])
            attms = [(attm, R_attm), ar.alloc([128, 2, 128], BF16, "attm2")]

            def att_stage(n):
                ch = slice(n * 128, (n + 1) * 128)
                ba = n % 2
                am, R_am = attms[n % 2]
                for hh in range(2):
                    hp = slice(64 * hh, 64 * hh + 64)
                    mm(pss[ba][:, hh * 128:(hh + 1) * 128], k_eT[hp, ch], q_eT[hp, ch], True, True, [R_ke, R_qe], R_ps[ba])
                P.op("dve", lambda g: g.tensor_tensor(am, pss[ba][:, 0:256].rearrange("p (a b) -> p a b", b=128),
                                                      cm_f[:, CTU:CTU + 1, :].to_broadcast([128, 2, 128]), ALU.mult),
                     reads=[R_ps[ba], R_cmf], writes=[R_am])
                if n < NT - 1:
                    bkv = 4 + n % 2
                    mm(pss[bkv][:, 0:256], ke_tok[:, n, :], v_g[:, n, :], True, True, [R_ket, R_vg], R_ps[bkv])

            att_stage(0)
            for n in range(NT):
                if n + 1 < NT:
                    att_stage(n + 1)
                ch = slice(n * 128, (n + 1) * 128)
                bo, bkv = 2 + n % 2, 4 + n % 2
                am, R_am = attms[n % 2]
                for hh in range(2):
                    hp = slice(64 * hh, 64 * hh + 64)
                    vs = slice(hh * 128, (hh + 1) * 128)
                    mm(pss[bo][:, vs], v_g[:, n, vs], am[:, hh, :], True, n == 0, [R_vg, R_am], R_ps[bo])
                    if n > 0:
                        mm(pss[bo][:, vs], Sbf[hp, vs], q_eT[hp, ch], False, True, [R_Sbf, R_qe], R_ps[bo])
                P.op("act", lambda g: g.activation(out=o_acc[:, :, ch], in_=pss[bo][:, 0:256].rearrange("p (a b) -> p a b", b=128), func=AF.Copy),
                     reads=[R_ps[bo]], writes=[R_oacc])
                if n < NT - 1:
                    P.op("dve", lambda g: g.tensor_tensor(Stmp, Sp, pss[bkv][:, 0:256], ALU.add), reads=[R_Sp, R_ps[bkv]], writes=[R_Stmp])
                    P.op("dve", lambda g: g.tensor_scalar(Sp, Stmp, dec[:, n:n + 1], None, ALU.mult), reads=[R_Stmp, R_dec], writes=[R_Sp])
                    P.op("act", lambda g: g.activation(out=Sbf, in_=Stmp, func=AF.Copy, scale=dec[:, n:n + 1]),
                         reads=[R_Stmp, R_dec], writes=[R_Sbf])
            for hh in range(2):
                post_norm_store(l, o_acc[:, hh:hh + 1, :], R_oacc, 1, [sc(50)], sz[:, hh:hh + 1, :], R_sz, 2 * pr + hh, tmps, 128.0, 6)
            prefetch(l, 11)

        if stop == "pC1":
            P.barrier(); return nc
        for h in range(2):
            new_phase("gdn")
            wfm = [ar.alloc([128, 16, 128], BF16, "wfm%d" % i) for i in range(2)]
            xbf, R_xbf = ar.alloc([128, S + 8], BF16, "xbf")
            diag, R_diag = ar.alloc([128, 4, 128], BF16, "diag")
            cs, R_cs = ar.alloc([128, S], F32, "cs")
            knT, R_kn = ar.alloc([128, S], BF16, "knT")
            qnT, R_qn = ar.alloc([128, S], BF16, "qnT")
            cvT, R_cv = ar.alloc([128, S], BF16, "cvT")
            kbT, R_kb = ar.alloc([128, S], BF16, "kbT")
            q_eT, R_qe = ar.alloc([128, S], BF16, "q_eT")
            vb_tok, R_vb = ar.alloc([128, NT, 128], BF16, "vb_tok")
            kbg_tok, R_kbg = ar.alloc([128, NT, 128], BF16, "kbg_tok")
            kt_tok, R_kt = ar.alloc([128, NT, 128], BF16, "kt_tok")
            gc, R_gc = ar.alloc([128, S], F32, "gc")
            beta, R_beta = ar.alloc([128, S], BF16, "beta")
            eg, R_eg = ar.alloc([128, S], F32, "eg")
            tl, R_tl = ar.alloc([128, S], F32, "tl")
            dabT, R_dab = tl[0:8, :], R_tl
            cols, R_cols = ar.alloc([128, 6, 16], F32, "cols")
            nA, R_nA = ar.alloc([128, 4], F32, "nA")
            Sp, R_Sp = ar.alloc([128, 128], F32, "Sp")
            Sbf, R_Sbf = ar.alloc([128, 128], BF16, "Sbf")
            dm = [ar.alloc([128, 128], F32, "dm%d" % i) for i in range(4)]
            mb = [ar.alloc([128, 128], BF16, "mb%d" % i) for i in range(10)]
            u_sb, R_u = ar.alloc([128, 128], F32, "u_sb")
            tmps = [ar.alloc([128, 2, 512], BF16, "sq"), ar.alloc([128, 512], F32, "t1"), ar.alloc([128, 512], F32, "rinv"),
                    ar.alloc([128, 512], F32, "u"), ar.alloc([128, 512], BF16, "ost"), ar.alloc([128, 512], BF16, "ost2")]
            e1, R_e1 = tmps[3]
            o_acc, R_oacc = cs.rearrange("p (a b) -> p a b", a=1), R_cs

            def c_dab(tb, ps, R):
                copy_on("act", dabT[0:8, tb * 512:(tb + 1) * 512], ps, [R], [R_dab])
            proj_fm(l, 11, 8, wfm, c_dab, 0)
            P.op("dve", lambda g: g.memset(xbf[:, 0:3], 0.0), writes=[R_xbf])
            for which in range(3):
                ti = which * 4 + h
                for j in range(4):
                    P.op("dve", lambda g: g.tensor_scalar(diag[:, j, :], ident_f, sc(64 + ti * 4 + j), None, ALU.mult),
                         reads=[R_cmf, R_small], writes=[R_diag])

                def c_x(tb, ps, R):
                    copy_on(alt(), xbf[:, 3 + tb * 512:3 + (tb + 1) * 512], ps, [R], [R_xbf])
                proj_fm(l, 5 + 2 * which + h, 128, wfm, c_x, 1)
                for blk in range(4):
                    sl = slice(blk * 512, (blk + 1) * 512)
                    bk = blk % 2
                    for j in range(4):
                        mm(pss[bk][:], diag[:, j, :], xbf[:, blk * 512 + j: blk * 512 + j + 512], j == 0, j == 3, [R_diag, R_xbf], R_ps[bk])
                    if which == 2:
                        P.op("act", lambda g: g.activation(out=cvT[:, sl], in_=pss[bk][:], func=AF.Silu), reads=[R_ps[bk]], writes=[R_cv])
                    else:
                        P.op("act", lambda g: g.activation(out=cs[:, sl], in_=pss[bk][:], func=AF.Silu), reads=[R_ps[bk]], writes=[R_cs])
                if which < 2:
                    for blk in range(4):
                        sl = slice(blk * 512, (blk + 1) * 512)
                        rinv, R_rinv = rstd_part([(cs[:, sl], R_cs)], 512, 1.0, tmps[0:3], 2 + blk % 2)
                        if which == 0:
                            P.op("dve", lambda g: g.scalar_tensor_tensor(out=qnT[:, sl], in0=cs[:, sl], scalar=128.0 ** -0.5, in1=rinv[:, 0:512],
                                                                         op0=ALU.mult, op1=ALU.mult), reads=[R_cs, R_rinv], writes=[R_qn])
                        else:
                            P.op("dve", lambda g: g.tensor_tensor(knT[:, sl], cs[:, sl], rinv[:, 0:512], ALU.mult), reads=[R_cs, R_rinv], writes=[R_kn])
            if stop == "g1":
                P.barrier(); return nc
            P.op("act", lambda g: g.activation(out=nA, in_=sc(56, 4), func=AF.Exp), reads=[R_small], writes=[R_nA])
            P.op("dve", lambda g: g.tensor_scalar(nA, nA, -1.0, None, ALU.mult), reads=[R_nA], writes=[R_nA])
            for blk in range(4):
                sl = slice(blk * 512, (blk + 1) * 512)
                bk = 4 + blk % 2
                mm(pss[bk][:], selm[0:8, h, :], dabT[0:8, sl], True, True, [R_selm, R_dab], R_ps[bk])
                P.op("act", lambda g: g.activation(out=e1, in_=pss[bk][:], func=AF.Exp, bias=sc(60 + h)), reads=[R_ps[bk], R_small], writes=[R_e1])
                P.op("act", lambda g: g.activation(out=gc[:, sl], in_=e1, func=AF.Ln, bias=one_ap), reads=[R_e1, R_misc], writes=[R_gc])
                bk2 = 6 + blk % 2
                mm(pss[bk2][:], selm[0:8, 4 + h, :], dabT[0:8, sl], True, True, [R_selm, R_dab], R_ps[bk2])
                P.op("act", lambda g: g.activation(out=beta[:, sl], in_=pss[bk2][:], func=AF.Sigmoid), reads=[R_ps[bk2]], writes=[R_beta])
            P.op("dve", lambda g: g.tensor_tensor_scan(out=gc, data0=rmask[:], data1=gc, initial=0.0, op0=ALU.mult, op1=ALU.add),
                 reads=[R_rmask, R_gc], writes=[R_gc])
            P.op("dve", lambda g: g.tensor_scalar(gc, gc, nA[:, h:h + 1], None, ALU.mult), reads=[R_gc, R_nA], writes=[R_gc])
            gcl = cols[:, 0, :]
            cd = cols[:, 1, :]
            P.op("dve", lambda g: g.tensor_copy(gcl, gc[:, 127:S:128]), reads=[R_gc], writes=[R_cols])
            P.op("act", lambda g: g.activation(out=cd, in_=gcl, func=AF.Exp), reads=[R_cols], writes=[R_cols])
            P.op("act", lambda g: g.activation(out=eg, in_=gc, func=AF.Exp), reads=[R_gc], writes=[R_eg])
            P.op("dve", lambda g: g.tensor_tensor(q_eT, qnT, eg, ALU.mult), reads=[R_qn, R_eg], writes=[R_qe])
            P.op("dve", lambda g: g.tensor_tensor(kbT, knT, beta, ALU.mult), reads=[R_kn, R_beta], writes=[R_kb])
            P.op("pool", lambda g: g.tensor_tensor(eg, eg, beta, ALU.mult), reads=[R_eg, R_beta], writes=[R_eg])
            for n in range(NT):
                ch = slice(n * 128, (n + 1) * 128)
                P.op("act", lambda g: g.activation(out=tl[:, ch], in_=gc[:, ch], func=AF.Exp, scale=-1.0, bias=gcl[:, n:n + 1]),
                     reads=[R_gc, R_cols], writes=[R_tl])
            if stop == "g2":
                P.barrier(); return nc
            for qi, (src, R_src) in enumerate(((gc, R_gc), (beta, R_beta), (eg, R_eg), (tl, R_tl))):
                oh = cm_b[:, CI, 0:2] if src is beta else cm_f[:, CI, 0:2]
                for n in range(NT):
                    c0 = (qi * NT + n) * 2
                    mm(pss[3][:, c0:c0 + 2], src[:, n * 128:(n + 1) * 128], oh, True, True, [R_src, R_cmf, R_cmb], R_ps[3])
            P.op("dve", lambda g: g.tensor_copy(cols[:, 2:6, :], pss[3][:, 0:128:2].rearrange("p (a b) -> p a b", b=NT)),
                 reads=[R_ps[3]], writes=[R_cols])
            gc_col, beta_col, bexp_col, tail_col = (cols[:, i, :] for i in (2, 3, 4, 5))
            if stop == "g2b":
                P.barrier(); return nc
            for t in range(NT):
                bk = t % 2
                pT = pss[bk][:].bitcast(BF16)
                ts = slice(t * 128, (t + 1) * 128)
                P.op("pe", lambda g: g.transpose(pT[:, 0:128], knT[:, ts], ident_b), reads=[R_kn, R_cmb], writes=[R_ps[bk]])
                P.op("pe", lambda g: g.transpose(pT[:, 128:256], cvT[:, ts], ident_b), reads=[R_cv, R_cmb], writes=[R_ps[bk]], pe_accum=True)
                P.op("dve", lambda g: g.tensor_scalar(kbg_tok[:, t, :], pT[:, 0:128], bexp_col[:, t:t + 1], None, ALU.mult),
                     reads=[R_ps[bk], R_cols], writes=[R_kbg])
                P.op("dve", lambda g: g.tensor_scalar(kt_tok[:, t, :], pT[:, 0:128], tail_col[:, t:t + 1], None, ALU.mult),
                     reads=[R_ps[bk], R_cols], writes=[R_kt])
                P.op("dve", lambda g: g.tensor_scalar(vb_tok[:, t, :], pT[:, 128:256], beta_col[:, t:t + 1], None, ALU.mult),
                     reads=[R_ps[bk], R_cols], writes=[R_vb])

            if stop == "g3":
                P.barrier(); return nc
            P.barrier(reset=False)
            sz, R_sz = tl.bitcast(BF16)[:, 0:S].rearrange("p (a b) -> p a b", a=1), Reg("sz")
            mb2 = [(xbf[:, i * 128:(i + 1) * 128], Reg("mb2_%d" % i)) for i in range(16)]

            def c_z(tb, ps, R):
                P.op("act", lambda g: g.activation(out=sz[:, 0, tb * 512:(tb + 1) * 512], in_=ps, func=AF.Silu), reads=[R], writes=[R_sz])
            proj_fm(l, 12 + h, 128, wfm, c_z, 1)
            if stop == "g4":
                P.barrier(); return nc
            P.barrier(reset=False)
            G = 4
            eg4 = eg.rearrange("p (t g c) -> p t g c", g=G, c=128)
            (dA4, R_dA), (dB4, R_dB), (dD4, R_dD), (u4, R_u4) = [(eg4[:, i], Reg("f4_%d" % i)) for i in range(4)]
            pool16 = []
            for src in (beta, cvT, wfm[0][0].rearrange("p a b -> p (a b)"), wfm[1][0].rearrange("p a b -> p (a b)"),
                        tl.bitcast(BF16)[:, S:2 * S]):
                v4 = src.rearrange("p (t g c) -> p t g c", g=G, c=128)
                pool16 += [(v4[:, i], Reg("b4_%d" % len(pool16))) for i in range(4)]
            ((Pm4, R_P), (PT4, R_PT), (qk4, R_qk), (Pd4, R_Pd), (PTd4, R_PTd), (Po32, R_Po32), (PTo32, R_PTo32), (Po64, R_Po64),
             Abuf0, ATbuf0, Abuf1, ATbuf1, sq0, sqT0, sq1, sqT1, (U1, R_U1), (T1, R_T1), (wT4, R_wT)) = pool16[0:19]
            vnew, R_vn = mb[0]

            def bc4(blk, f32=True):
                src = cm_f if f32 else cm_b
                return src[:, blk:blk + 1, :].to_broadcast([128, G, 128])

            def flat(t4):
                return t4.rearrange("p g c -> p (g c)")

            def p4(bank):
                return pss[bank][:].rearrange("p (g c) -> p g c", c=128)

            P.op("dve", lambda g: g.memset(Sp, 0.0), writes=[R_Sp])
            P.op("dve", lambda g: g.memset(Sbf, 0.0), writes=[R_Sbf])
            xq = xbf[:, 0:2048]
            qk4s = [(qk4, R_qk), (xq[:, 0:512].rearrange("p (g c) -> p g c", c=128), Reg("qk4b"))]
            wT4s = [(wT4, R_wT), pool16[19]]
            u4s = [(u4, R_u4), (xq[:, 1024:2048].bitcast(F32).rearrange("p (g c) -> p g c", c=128), Reg("u4b"))]

            def mm4(bank, lhs4, R_l, rhs4, R_r, start=True):
                for g_ in range(G):
                    cs_ = slice(g_ * 128, (g_ + 1) * 128)
                    mm(pss[bank][:, cs_], lhs4[:, g_, :], rhs4[:, g_, :], start, start or g_ == G - 1, [R_l, R_r], R_ps[bank])

            def add_mm4(bank, base, lhs4, R_l, rhs4, R_r):
                mm(pss[bank][:], ident_b, flat(base[0]), True, False, [R_cmb, base[1]], R_ps[bank])
                mm4(bank, lhs4, R_l, rhs4, R_r, start=False)

            def solve_gen(bb):
                par = bb % 2
                qk4_, R_qk_ = qk4s[par]
                wT4_, R_wT_ = wT4s[par]
                u4_, R_u4_ = u4s[par]
                ns = [bb * G + g_ for g_ in range(G)]
                chs = [slice(n * 128, (n + 1) * 128) for n in ns]
                for g_, n in enumerate(ns):
                    gcc = gc_col[:, n:n + 1]
                    P.op("dve", lambda g: g.tensor_scalar(dA4[:, g_, :], gc[:, chs[g_]], gcc, 0.0, ALU.subtract, ALU.max),
                         reads=[R_gc, R_cols], writes=[R_dA])
                    P.op("dve", lambda g: g.tensor_scalar(dB4[:, g_, :], gc[:, chs[g_]], gcc, 0.0, ALU.subtract, ALU.min),
                         reads=[R_gc, R_cols], writes=[R_dB])
                    yield
                P.op("act", lambda g: g.activation(out=flat(dA4), in_=flat(dA4), func=AF.Exp, scale=-1.0), reads=[R_dA], writes=[R_dA])
                P.op("act", lambda g: g.activation(out=flat(dB4), in_=flat(dB4), func=AF.Exp), reads=[R_dB], writes=[R_dB])
                yield
                P.op("dve", lambda g: g.tensor_tensor(dA4, dA4, bc4(CNSL), ALU.mult), reads=[R_dA, R_cmf], writes=[R_dA])
                P.op("dve", lambda g: g.tensor_tensor(dD4, dB4, bc4(CNSU), ALU.mult), reads=[R_dB, R_cmf], writes=[R_dD])
                P.op("dve", lambda g: g.tensor_tensor(dB4, dB4, bc4(CTU), ALU.mult), reads=[R_dB, R_cmf], writes=[R_dB])
                yield
                for bank, (l4, R_l4), (r4, R_r4) in ((3, (kbT, R_kb), (knT, R_kn)), (4, (knT, R_kn), (kbT, R_kb)), (5, (knT, R_kn), (qnT, R_qn))):
                    for g_ in range(G):
                        cs_ = slice(g_ * 128, (g_ + 1) * 128)
                        mm(pss[bank][:, cs_], l4[:, chs[g_]], r4[:, chs[g_]], True, True, [R_l4, R_r4], R_ps[bank])
                yield
                P.op("dve", lambda g: g.tensor_tensor(Pm4, p4(3), dA4, ALU.mult), reads=[R_ps[3], R_dA], writes=[R_P])
                P.op("dve", lambda g: g.tensor_tensor(PT4, p4(4), dD4, ALU.mult), reads=[R_ps[4], R_dD], writes=[R_PT])
                P.op("dve", lambda g: g.tensor_tensor(qk4_, p4(5), dB4, ALU.mult), reads=[R_ps[5], R_dB], writes=[R_qk_])
                yield
                for dst, R_d, src, R_s, mk in ((Pd4, R_Pd, Pm4, R_P, CB32), (PTd4, R_PTd, PT4, R_PT, CB32), (Po32, R_Po32, Pm4, R_P, CO32),
                                               (PTo32, R_PTo32, PT4, R_PT, CO32), (Po64, R_Po64, Pm4, R_P, CO64)):
                    P.op("dve", lambda g: g.tensor_tensor(dst, src, bc4(mk, False), ALU.mult), reads=[R_s, R_cmb], writes=[R_d])
                    yield
                Acur, ATcur, Anxt, ATnxt = Abuf0, ATbuf0, Abuf1, ATbuf1
                P.op("dve", lambda g: g.tensor_tensor(Acur[0], Pd4, bc4(CI, False), ALU.add), reads=[R_Pd, R_cmb], writes=[Acur[1]])
                P.op("dve", lambda g: g.tensor_tensor(ATcur[0], PTd4, bc4(CI, False), ALU.add), reads=[R_PTd, R_cmb], writes=[ATcur[1]])
                yield
                cur = ((Pd4, R_Pd), (PTd4, R_PTd))
                sqb = [(sq0, sqT0), (sq1, sqT1)]
                for lev in range(4):
                    (cP, R_cP), (cPT, R_cPT) = cur
                    (nP, R_nP), (nPT, R_nPT) = sqb[lev % 2]
                    mm4(3, cPT, R_cPT, cP, R_cP)
                    mm4(4, cP, R_cP, cPT, R_cPT)
                    yield
                    copy_on("act", flat(nP), pss[3][:], [R_ps[3]], [R_nP])
                    copy_on("dve", flat(nPT), pss[4][:], [R_ps[4]], [R_nPT])
                    yield
                    add_mm4(5, Acur, nPT, R_nPT, Acur[0], Acur[1])
                    add_mm4(6, ATcur, nP, R_nP, ATcur[0], ATcur[1])
                    yield
                    copy_on("dve", flat(Anxt[0]), pss[5][:], [R_ps[5]], [Anxt[1]])
                    copy_on("act", flat(ATnxt[0]), pss[6][:], [R_ps[6]], [ATnxt[1]])
                    yield
                    Acur, Anxt = Anxt, Acur
                    ATcur, ATnxt = ATnxt, ATcur
                    cur = ((nP, R_nP), (nPT, R_nPT))
                mm4(3, PTo32, R_PTo32, Acur[0], Acur[1])
                mm4(4, Po32, R_Po32, ATcur[0], ATcur[1])
                yield
                copy_on("act", flat(U1), pss[3][:], [R_ps[3]], [R_U1])
                copy_on("dve", flat(T1), pss[4][:], [R_ps[4]], [R_T1])
                yield
                add_mm4(5, Acur, ATcur[0], ATcur[1], U1, R_U1)
                add_mm4(6, ATcur, Acur[0], Acur[1], T1, R_T1)
                yield
                copy_on("dve", flat(Anxt[0]), pss[5][:], [R_ps[5]], [Anxt[1]])
                copy_on("act", flat(ATnxt[0]), pss[6][:], [R_ps[6]], [ATnxt[1]])
                yield
                Acur, Anxt = Anxt, Acur
                ATcur, ATnxt = ATnxt, ATcur
                mm4(4, Po64, R_Po64, ATcur[0], ATcur[1])
                yield
                copy_on("dve", flat(T1), pss[4][:], [R_ps[4]], [R_T1])
                yield
                add_mm4(6, ATcur, Acur[0], Acur[1], T1, R_T1)
                yield
                copy_on("act", flat(ATnxt[0]), pss[6][:], [R_ps[6]], [ATnxt[1]])
                yield
                AT4, R_AT = ATnxt
                for g_, n in enumerate(ns):
                    cs_ = slice(g_ * 128, (g_ + 1) * 128)
                    mm(pss[3][:, cs_], AT4[:, g_, :], vb_tok[:, n, :], True, True, [R_AT, R_vb], R_ps[3])
                for g_, n in enumerate(ns):
                    cs_ = slice(g_ * 128, (g_ + 1) * 128)
                    mm(pss[4][:, cs_], kbg_tok[:, n, :], AT4[:, g_, :], True, True, [R_AT, R_kbg], R_ps[4])
                yield
                copy_on("act", flat(u4_), pss[3][:], [R_ps[3]], [R_u4_])
                copy_on("dve", flat(wT4_), pss[4][:], [R_ps[4]], [R_wT_])
                yield

            def rec_gen(bb):
                par = bb % 2
                qk4_, R_qk_ = qk4s[par]
                wT4_, R_wT_ = wT4s[par]
                u4_, R_u4_ = u4s[par]
                for g_ in range(G):
                    n = bb * G + g_
                    ch = slice(n * 128, (n + 1) * 128)
                    if n > 0:
                        mm(pss[0][:, 0:128], wT4_[:, g_, :], Sbf, True, True, [R_wT_, R_Sbf], R_ps[0])
                        yield
                        P.op("dve", lambda g: g.tensor_tensor(vnew, u4_[:, g_, :], pss[0][:, 0:128], ALU.subtract),
                             reads=[R_u4_, R_ps[0]], writes=[R_vn])
                    else:
                        copy_on("dve", vnew, u4_[:, g_, :], [R_u4_], [R_vn])
                    yield
                    if n > 0:
                        mm(pss[2][:, 0:128], Sbf, q_eT[:, ch], True, False, [R_Sbf, R_qe], R_ps[2])
                    if n < NT - 1:
                        mm(pss[1][:, 0:128], kt_tok[:, n, :], vnew, True, True, [R_kt, R_vn], R_ps[1])
                    mm(pss[2][:, 0:128], vnew, qk4_[:, g_, :], n == 0, True, [R_vn, R_qk_], R_ps[2])
                    yield
                    if n < NT - 1:
                        P.op("dve", lambda g: g.scalar_tensor_tensor(out=Sp, in0=Sp, scalar=cd[:, n:n + 1], in1=pss[1][:, 0:128],
                                                                     op0=ALU.mult, op1=ALU.add), reads=[R_Sp, R_cols, R_ps[1]], writes=[R_Sp])
                    copy_on("act", o_acc[:, 0, ch], pss[2][:, 0:128], [R_ps[2]], [R_oacc])
                    yield
                    if n < NT - 1:
                        copy_on("act", Sbf, Sp, [R_Sp], [R_Sbf])
                        yield

            def run_interleaved(g1, g2):
                live = [g1, g2]
                while live:
                    for gen in list(live):
                        try:
                            next(gen)
                        except StopIteration:
                            live.remove(gen)

            NB = NT // G
            for _ in solve_gen(0):
                pass
            for bb in range(NB):
                run_interleaved(solve_gen(bb + 1) if bb + 1 < NB else iter(()), rec_gen(bb))
            post_norm_store(l, o_acc, R_oacc, 1, [sc(51)], sz, R_sz, 2 + h, tmps, 128.0, 6)
            prefetch(l, 11 if h == 0 else 14)

        P.collective("AllGather", PAIRS, oTl_t[0].ap().opt(), oTg_t[0].ap().opt(), reads=[R_oTl[0]], writes=[R_oTg[0]])
        if stop == "pC2":
            P.barrier(); return nc
        for h in range(2):
            new_phase("diff")
            wfm = [ar.alloc([128, 16, 128], BF16, "wfm%d" % i) for i in range(3)]
            wtm = ar.alloc([128, 16, 256], BF16, "wtm")
            qT, R_q = ar.alloc([128, 2, S], BF16, "qT")
            kT, R_k = ar.alloc([128, 2, S], BF16, "kT")
            v_sb, R_v = ar.alloc([128, NT, 256], BF16, "v_sb")
            sz, R_sz = ar.alloc([128, 2, S], BF16, "sz")
            ebuf = [ar.alloc([128, 512], BF16, "e%d" % i) for i in range(3)]
            qn, R_qnb = ar.alloc([128, 512], BF16, "qn")
            ta, R_ta = ar.alloc([128, 512], F32, "ta")
            tb_, R_tb = ar.alloc([128, 512], F32, "tb")
            tO, R_tO = ar.alloc([128, 2, 512], F32, "tO")
            rs, R_rs = ar.alloc([128, 512], F32, "rs")
            sq2, R_sq2 = ar.alloc([128, 1024], BF16, "sq2")
            t1b, R_t1b = ar.alloc([128, 1024], F32, "t1b")
            rinvb, R_rinvb = ar.alloc([128, 1024], F32, "rinvb")
            qnb, R_qnb2 = ar.alloc([128, 1024], BF16, "qnb")
            tab, R_tab = ar.alloc([128, 1024], F32, "tab")
            tbb, R_tbb = ar.alloc([128, 1024], F32, "tbb")
            lamt, R_lam = ar.alloc([128, 8], F32, "lam")
            lprod, R_lprod = ar.alloc([128, 256], F32, "lprod")
            wn2, R_wn2 = ar.alloc([128, 2], F32, "wn2")
            tmps = [ar.alloc([128, 2, 512], BF16, "sq"), ar.alloc([128, 512], F32, "t1"), ar.alloc([128, 512], F32, "rinv"),
                    ar.alloc([128, 512], F32, "u"), ar.alloc([128, 512], BF16, "ost"), ar.alloc([128, 512], BF16, "ost2")]
            P.op("dve", lambda g: g.tensor_tensor(lprod[:, 0:128], sc(112, 128), sc(240, 128), ALU.mult), reads=[R_small], writes=[R_lprod])
            P.op("dve", lambda g: g.tensor_tensor(lprod[:, 128:256], sc(368, 128), sc(496, 128), ALU.mult), reads=[R_small], writes=[R_lprod])
            P.op("dve", lambda g: g.tensor_reduce(lamt[:, 0:2], lprod.rearrange("p (a b) -> p a b", b=128), AX.X, ALU.add),
                 reads=[R_lprod], writes=[R_lam])
            P.op("act", lambda g: g.activation(out=lamt[:, 2:4], in_=lamt[:, 0:2], func=AF.Exp), reads=[R_lam], writes=[R_lam])
            P.op("dve", lambda g: g.scalar_tensor_tensor(out=lamt[:, 4:5], in0=lamt[:, 3:4], scalar=-lam_init, in1=lamt[:, 2:3],
                                                         op0=ALU.add, op1=ALU.subtract), reads=[R_lam], writes=[R_lam])
            nlam = lamt[:, 4:5]
            P.op("dve", lambda g: g.tensor_scalar(wn2, sc(54, 2), 1.0 - lam_init, None, ALU.mult), reads=[R_small], writes=[R_wn2])
            units = [(which, m, half) for which in range(2) for m in range(2) for half in range(2)]
            qk_meta = ((qT, R_q, 14, 52), (kT, R_k, 18, 53))
            ustate = {}

            def qk_issue(ui):
                which, m, half = units[ui]
                g0 = qk_meta[which][2]
                wb, R_wb = qk_w[which * 2 + m]
                banks = [4 + 2 * (ui % 2), 5 + 2 * (ui % 2)]
                for kc in range(16):
                    for i in range(2):
                        tb = half * 2 + i
                        mm(pss[banks[i]][:], wb[:, kc, :], big[:, kc, tb * 512:(tb + 1) * 512], kc == 0, kc == 15,
                           [R_wb, R_big], R_ps[banks[i]])

            def qk_consume(ui):
                which, m, half = units[ui]
                dstT, R_dst, g0, wcol = qk_meta[which]
                b0_ = 4 + 2 * (ui % 2)
                ps2 = psall[:, b0_ * 512:(b0_ + 2) * 512]
                Rb = [R_ps[b0_], R_ps[b0_ + 1]]
                sl2 = slice(half * 1024, (half + 1) * 1024)
                P.op("act", lambda g: g.activation(out=sq2, in_=ps2, func=AF.Square), reads=Rb, writes=[R_sq2])
                for i in range(2):
                    mm(pss[2 + i][:], ones_b, sq2[:, i * 512:(i + 1) * 512], True, True, [R_sq2, R_cmb], R_ps[2 + i])
                P.op("act", lambda g: g.activation(out=t1b, in_=psall[:, 2 * 512:4 * 512], func=AF.Ln, scale=1.0 / 128.0, bias=eps_ap),
                     reads=[R_ps[2], R_ps[3], R_misc], writes=[R_t1b])
                P.op("act", lambda g: g.activation(out=rinvb, in_=t1b, func=AF.Exp, scale=-0.5), reads=[R_t1b], writes=[R_rinvb])
                P.op("dve", lambda g: g.scalar_tensor_tensor(out=qnb, in0=ps2, scalar=sc(wcol), in1=rinvb, op0=ALU.mult, op1=ALU.mult),
                     reads=Rb + [R_rinvb, R_small], writes=[R_qnb2])
                for i in range(2):
                    mm(pss[i][:], cm_b[:, CP, :], qnb[:, i * 512:(i + 1) * 512], True, True, [R_cmb, R_qnb2], R_ps[i])
                P.op("pool", lambda g: g.tensor_tensor(tab, qnb, cosT[:, sl2], ALU.mult), reads=[R_qnb2, R_cos], writes=[R_tab])
                P.op("dve", lambda g: g.tensor_tensor(tbb, psall[:, 0:1024], sinT[:, sl2], ALU.mult), reads=[R_ps[0], R_ps[1], R_sin], writes=[R_tbb])
                P.op("pool", lambda g: g.tensor_tensor(dstT[:, m, sl2], tab, tbb, ALU.add), reads=[R_tab, R_tbb], writes=[R_dst])
            qk_w = []
            for which_ in range(2):
                for m_ in range(2):
                    gi_ = qk_meta[which_][2] + 2 * h + m_
                    if state.get("pre") == (l, gi_):
                        qk_w.append((wpre, R_wpre))
                        state["pre"] = None
                        continue
                    wb_, R_wb_ = wfm[sum(1 for w_ in qk_w if w_[0] is not wpre)]
                    P.dma("pool", wb_.rearrange("p a b -> p (a b)"), winfm_d[l, gi_], writes=[R_wb_])
                    qk_w.append((wb_, R_wb_))
            qk_issue(0)
            for ui in range(len(units)):
                if ui + 1 < len(units):
                    qk_issue(ui + 1)
                qk_consume(ui)

            def c_v(t, ps, R):
                copy_on(alt(), v_sb[:, t, :], ps, [R], [R_v])
            proj_tm(l, 1 + h, wtm, c_v, [0, 1])
            for j in range(2):
                def c_z(tb, ps, R):
                    P.op("act", lambda g: g.activation(out=sz[:, j, tb * 512:(tb + 1) * 512], in_=ps, func=AF.Silu), reads=[R], writes=[R_sz])
                proj_fm(l, 22 + 2 * h + j, 128, wfm, c_z, j)
            if h == 1:
                for fc in range(16):
                    P.dma("pool", big[:, fc, :], wout_d[l][:, fc * D:(fc + 1) * D], writes=[R_big], nowaw=fc > 0)
            scale = 128.0 ** -0.5
            steps = [(qb, m, kc) for qb in range(4) for m in range(2) for kc in range(4 * (qb + 1))]

            def qk_mm(si):
                qb, m, kc = steps[si]
                col0 = max(kc - 4 * qb, 0) * 128
                bsc = si % 2
                mm(pss[bsc][:, col0:512], kT[:, m, kc * 128:(kc + 1) * 128], qT[:, m, qb * 512 + col0:(qb + 1) * 512], True, True,
                   [R_k, R_q], R_ps[bsc])
            qk_mm(0)
            for si, (qb, m, kc) in enumerate(steps):
                qs = slice(qb * 512, (qb + 1) * 512)
                nk = 4 * (qb + 1)
                bo0, bo1, bs = (2, 3, 4) if m == 0 else (5, 6, 7)
                if si + 1 < len(steps):
                    qk_mm(si + 1)
                bsc = si % 2
                c = kc - 4 * qb
                col0 = max(c, 0) * 128
                e, R_e = ebuf[si % 3]
                P.op("act", lambda g: g.activation(out=e[:, col0:512], in_=pss[bsc][:, col0:512], func=AF.Exp, scale=scale),
                     reads=[R_ps[bsc]], writes=[R_e])
                if c >= 0:
                    P.op("pool", lambda g: g.tensor_tensor(e[:, col0:col0 + 128], e[:, col0:col0 + 128], cm_b[:, CTU, :], ALU.mult),
                         reads=[R_e, R_cmb], writes=[R_e])
                first, last = kc == 0, kc == nk - 1
                mm(pss[bo0][:, col0:512], v_sb[:, kc, 0:128], e[:, col0:512], first, last, [R_v, R_e], R_ps[bo0])
                mm(pss[bo1][:, col0:512], v_sb[:, kc, 128:256], e[:, col0:512], first, last, [R_v, R_e], R_ps[bo1])
                mm(pss[bs][:, col0:512], ones_b, e[:, col0:512], first, last, [R_cmb, R_e], R_ps[bs])
                if not last:
                    continue
                P.op("dve", lambda g: g.reciprocal(rs, pss[bs][:]), reads=[R_ps[bs]], writes=[R_rs])
                for j, bo in enumerate((bo0, bo1)):
                    if m == 0:
                        P.op("dve", lambda g: g.tensor_tensor(tO[:, j, :], pss[bo][:], rs, ALU.mult), reads=[R_ps[bo], R_rs], writes=[R_tO])
                    else:
                        P.op("dve", lambda g: g.scalar_tensor_tensor(out=ta, in0=pss[bo][:], scalar=nlam, in1=rs, op0=ALU.mult, op1=ALU.mult),
                             reads=[R_ps[bo], R_rs, R_lam], writes=[R_ta])
                        P.op("pool", lambda g: g.tensor_tensor(tO[:, j, :], tO[:, j, :], ta, ALU.add), reads=[R_tO, R_ta], writes=[R_tO])
                if m == 0:
                    continue
                (sq, R_sq), (t1, R_t1), (rinv, R_rinv), (u, R_u) = tmps[0:4]
                rstd_part([(tO[:, j, :], R_tO) for j in range(2)], 512, 256.0, tmps[0:3], 4)
                for j in range(2):
                    ost, R_ost = tmps[4 + j]
                    P.op("dve", lambda g: g.scalar_tensor_tensor(out=u, in0=tO[:, j, :], scalar=wn2[:, j:j + 1], in1=rinv, op0=ALU.mult, op1=ALU.mult),
                         reads=[R_tO, R_rinv, R_wn2], writes=[R_u])
                    P.op("pool", lambda g: g.tensor_tensor(ost, u, sz[:, j, qs], ALU.mult), reads=[R_u, R_sz], writes=[R_ost])
                    dst_ap, R_d = oT_dst(4 + 2 * h + j, qs)
                    P.dma("sp", dst_ap, ost, reads=[R_ost], writes=[R_d], nowaw=True)
            if h == 0:
                prefetch(l, 16)
            P.collective("AllGather", PAIRS, oTl_t[1 + h].ap().opt(), oTg_t[1 + h].ap().opt(), reads=[R_oTl[1 + h]], writes=[R_oTg[1 + h]])

        if stop == "pC3":
            P.barrier(); return nc
        new_phase("D")
        wo = big
        xts = [ar.alloc([128, D], F32, "xt%d" % i) for i in range(2)]
        obs = [ar.alloc([128, 16, 512], BF16, "ob%d" % i) for i in range(2)]
        xos = [ar.alloc([128, D], F32, "xo%d" % i) for i in range(2)]
        ytmp, R_yt = ar.alloc([128, 512], F32, "ytmp")
        xsrc = x_d if l == 0 else x1_d
        xdst = out_d if l == L - 1 else x1_d
        R_dst = R_out if l == L - 1 else R_x1
        for tb in range(4):
            ob, R_ob = obs[tb % 2]
            for i2, (c0_, n_) in enumerate(((0, 8), (8, 4), (12, 4))):
                P.dma("sp", ob[:, c0_:c0_ + n_, :], oTg_d[i2][:, tb * 512:(tb + 1) * 512].rearrange("(c p) t -> p c t", p=128),
                      reads=[R_oTg[i2]], writes=[R_ob], nowaw=i2 > 0)
            for i in range(4):
                t = tb * 4 + i
                xt, R_xt = xts[t % 2]
                xo, R_xo = xos[t % 2]
                P.dma("sp", xt, xsrc[t * 128:(t + 1) * 128, :], reads=[R_x1] if l > 0 else [], writes=[R_xt])
                for ng in range(4):
                    ns = slice(ng * 512, (ng + 1) * 512)
                    bk = (t * 4 + ng) % 4
                    for fc in range(16):
                        mm(pss[bk][:], ob[:, fc, i * 128:(i + 1) * 128], wo[:, fc, ns], fc == 0, fc == 15, [R_ob, R_big], R_ps[bk])
                    P.op("dve", lambda g: g.tensor_tensor(ytmp, pss[bk][:], gate_bc[:, ns], ALU.mult), reads=[R_ps[bk], R_gate], writes=[R_yt])
                    P.op("pool", lambda g: g.tensor_tensor(xo[:, ns], ytmp, xt[:, ns], ALU.add), reads=[R_yt, R_xt], writes=[R_xo])
                P.dma("sp", xdst[t * 128:(t + 1) * 128, :], xo, reads=[R_xo], writes=[R_dst], nowaw=True)

    P.barrier(collectives=True)
    if scopes and state.get("scope") is not None:
        nc.leave_named_scope(state["scope"][0], state["scope"][1], False)
    return nc


def _col(v):
    return np.ascontiguousarray(np.asarray(v, np.float32).reshape(-1, 128).T)


def _consts():
    p = np.arange(128)[:, None]
    j = np.arange(128)[None, :]
    cm = np.zeros((128, NCM, 128), np.float32)
    cm[:, CI] = (p == j)
    cm[:, CO] = 1.0
    prot = np.zeros((128, 128), np.float32)
    prot[(j[0, :64] + 64), j[0, :64]] = -1.0
    prot[(j[0, 64:] - 64), j[0, 64:]] = 1.0
    cm[:, CP] = prot
    cm[:, CTU] = (p <= j)
    cm[:, CNSU] = -1.0 * (p < j)
    cm[:, CNSL] = -1.0 * (p > j)
    bd32 = (p // 32 == j // 32)
    bd64 = (p // 64 == j // 64)
    cm[:, CB32] = bd32
    cm[:, CO32] = bd64 & ~bd32
    cm[:, CO64] = ~bd64
    selm = np.zeros((8, 8, 128), np.float32)
    for k in range(8):
        selm[k, k, :] = 1.0
    rmask = np.ones((128, S), np.float32)
    rmask[:, 0::128] = 0.0
    half = 64
    inv_freq = (10000.0 ** (-(np.arange(half, dtype=np.float32) / np.float32(half)))).astype(np.float32)
    invf = np.concatenate([inv_freq, inv_freq]).astype(np.float64) / (2.0 * math.pi)
    return cm.reshape(128, NCM * 128), selm.reshape(8, 8 * 128), rmask, invf.astype(np.float32)


def _fm_groups_local(hh):
    g = [(hh * 128, 128), (256 + hh * 128, 128), (1024, 16)]
    for base in (1040, 1552, 2064, 2576):
        g += [(base + 128 * (2 * hh + i), 128) for i in range(2)]
    g.append("dab")
    g += [(3096 + 128 * (2 * hh + i), 128) for i in range(2)]
    for base in (3608, 4632, 6680):
        g += [(base + 256 * (2 * hh + lh) + 128 * m, 128) for lh in range(2) for m in range(2)]
    return g


def _tm_groups_local(hh):
    return [(512 + hh * 256, 256), (5656 + 256 * (2 * hh), 256), (5656 + 256 * (2 * hh + 1), 256)]


OUT_CHUNK_ORDER = [0, 1, 4, 5, 2, 3, 6, 7, 8, 9, 12, 13, 10, 11, 14, 15]


def _prep_shared(inp):
    f = lambda k: np.asarray(inp[k], np.float32)
    cm, selm, rmask, invf = _consts()
    w_in = f("w_in")
    per_half = []
    for hh in range(2):
        winfm = np.zeros((2, 26, 128, 16, 128), np.float32)
        wintm = np.zeros((2, 3, 128, 16, 256), np.float32)
        for l in range(2):
            wl = w_in[l].reshape(16, 128, -1)
            for gi, grp in enumerate(_fm_groups_local(hh)):
                if grp == "dab":
                    for i in range(2):
                        winfm[l, gi, :, :, i] = wl[:, :, 3088 + 2 * hh + i].T
                        winfm[l, gi, :, :, 4 + i] = wl[:, :, 3092 + 2 * hh + i].T
                else:
                    c0, n = grp
                    winfm[l, gi, :, :, :n] = wl[:, :, c0:c0 + n].transpose(1, 0, 2)
            for gi, (c0, n) in enumerate(_tm_groups_local(hh)):
                wintm[l, gi] = wl[:, :, c0:c0 + n].transpose(1, 0, 2)
        sm = np.zeros((128, NS), np.float32)
        sm[:, 16] = invf
        for l in range(2):
            b = SBASE + l * SLW
            sm[:, b:b + 16] = _col(f("norm_w")[l])
            sm[:, b + 16:b + 48] = _col(f("b_ada")[l, :2 * D])
            sm[:, b + 48] = f("gla_b_lr")[l, hh * 128:(hh + 1) * 128]
            sm[:, b + 50] = f("gla_norm_w")[l]
            sm[:, b + 51] = f("gdn_norm_w")[l]
            sm[:, b + 52] = f("diff_q_norm_w")[l]
            sm[:, b + 53] = f("diff_k_norm_w")[l]
            sm[:, b + 54:b + 56] = _col(f("diff_norm_w")[l])
            sm[:, b + 56:b + 58] = f("gdn_a_log")[l][None, 2 * hh:2 * hh + 2]
            sm[:, b + 60:b + 62] = f("gdn_dt_bias")[l][None, 2 * hh:2 * hh + 2]
            cw = f("gdn_conv_w")[l].reshape(4, 12, 128)
            for which in range(3):
                for lh in range(2):
                    ti = which * 4 + lh
                    sm[:, b + 64 + ti * 4:b + 64 + ti * 4 + 4] = cw[:, which * 4 + 2 * hh + lh, :].T
            sm[:, b + 112:b + 624] = f("diff_lambda")[l].reshape(1, 512)
            sm[0:16, b + 624:b + 752] = f("gla_w_lr")[l][:, hh * 128:(hh + 1) * 128]
        per_half.append({"winfm": winfm.reshape(2, 26, 128, 16 * 128), "wintm": wintm.reshape(2, 3, 128, 16 * 256), "small": sm})
    wada = np.ascontiguousarray(f("w_ada").reshape(2, 16, 128, 48, 128).transpose(0, 3, 2, 1, 4)).reshape(2, 48, 128, 16 * 128)
    wo = f("w_out").reshape(2, 16, 128, D)[:, OUT_CHUNK_ORDER]
    wout = np.ascontiguousarray(wo.transpose(0, 2, 1, 3)).reshape(2, 128, 16 * D)
    bgate = np.ascontiguousarray(f("b_ada")[:, 2 * D:].reshape(2, 1, D))
    shared = {"cmat": cm, "selm": selm, "rmask": rmask, "wada": wada, "bgate": bgate, "wout": wout}
    return shared, per_half


def make_in_maps(inp, cores):
    shared, per_half = _prep_shared(inp)
    x = np.asarray(inp["x"], np.float32)
    c = np.asarray(inp["c"], np.float32)
    pos = np.asarray(inp["positions"], np.int32)
    maps = []
    for b, hh in cores:
        s = per_half[hh]["small"].copy()
        s[:, 0:16] = _col(c[b])
        m = dict(shared)
        m["winfm"] = per_half[hh]["winfm"]
        m["wintm"] = per_half[hh]["wintm"]
        m["x"] = np.ascontiguousarray(x[b])
        m["small"] = s
        m["pos"] = np.ascontiguousarray(pos[b:b + 1])
        maps.append(m)
    return maps


_NC_CACHE = {}


def kernel(**inputs):
    if "nc" not in _NC_CACHE:
        _NC_CACHE["nc"] = build(2)
    nc = _NC_CACHE["nc"]
    cores = [(i // 2, i % 2) for i in range(8)]
    maps = make_in_maps(inputs, cores)
    res = run_bass_kernel_spmd(nc, maps, core_ids=list(range(8)))
    out = np.stack([np.asarray(res.results[2 * b]["out"], np.float32) for b in range(4)], axis=0)
    return out
```
